# Optimizing a Trainium2 kernel written in Bass

```python
import jax, jax.numpy as jnp
from jax import lax
import numpy as np

D_MODEL = 1024
BATCH = 2
SEQ = 8192
DEPTH = 1

RWKV_HEAD = 64
RWKV_HEADS = 8
RWKV_WIDTH = RWKV_HEADS * RWKV_HEAD
DECAY_RANK = 64
ICLR_RANK = 64
ATT_HEAD = 64
ATT_Q_HEADS = 8
ATT_KV_HEADS = 2
ATT_GROUP = ATT_Q_HEADS // ATT_KV_HEADS
ATT_WIDTH = ATT_Q_HEADS * ATT_HEAD
ATT_KV_WIDTH = ATT_KV_HEADS * ATT_HEAD
QKV_WIDTH = ATT_WIDTH + 2 * ATT_KV_WIDTH
WINDOW = 128
BLOCK = 128
N_BRANCHES = 2
SHIFT_WIDTH = 3 * RWKV_WIDTH + DECAY_RANK + ICLR_RANK
IN_COLS = SHIFT_WIDTH + RWKV_WIDTH + QKV_WIDTH + ATT_WIDTH + N_BRANCHES * D_MODEL
RMS_EPS = 1e-6
GN_EPS = 64e-5
L2_EPS = 1e-12
NEG_INF = -1e30

kernel_name = 'hybrid_rwkv7_swa_sink_gated'


def _split(t, sizes):
    out, off = [], 0
    for s in sizes:
        out.append(t[..., off:off + s])
        off += s
    return out


def rms_norm(x, g):
    xf = x.astype(jnp.float32)
    y = xf * lax.rsqrt(jnp.mean(xf * xf, axis=-1, keepdims=True) + RMS_EPS)
    return (y * g.astype(jnp.float32)).astype(x.dtype)


def token_shift(p, mu):
    prev = jnp.pad(p[:, :-1], ((0, 0), (1, 0), (0, 0)))
    return p + (prev - p) * mu


def rwkv7_time_mix(r, k, v, w_lo, a_lo, w0, w_decay_up, a0, w_iclr_up, k_k, k_a, r_k, gn_w, gn_b):
    B, T, _ = r.shape
    f32 = jnp.float32
    H, N = RWKV_HEADS, RWKV_HEAD
    w_log = -jax.nn.softplus(-(w0 + jnp.tanh(w_lo) @ w_decay_up)) - 0.5
    decay = jnp.exp(-jnp.exp(w_log.astype(f32)))
    a = jax.nn.sigmoid(a0 + a_lo @ w_iclr_up)
    kk = (k * k_k).reshape(B, T, H, N).astype(f32)
    kk = kk / jnp.maximum(jnp.sqrt(jnp.sum(kk * kk, axis=-1, keepdims=True)), L2_EPS)
    k = k * (1.0 + (a - 1.0) * k_a)
    rh = r.reshape(B, T, H, N).astype(f32)
    kh = k.reshape(B, T, H, N).astype(f32)
    vh = v.reshape(B, T, H, N).astype(f32)
    wh = decay.reshape(B, T, H, N)
    ah = a.reshape(B, T, H, N).astype(f32)
    tm = lambda t: jnp.moveaxis(t, 1, 0)

    def step(S, inp):
        r_t, w_t, k_t, v_t, kk_t, a_t = inp
        sa = jnp.einsum('bhvk,bhk->bhv', S, kk_t)
        S = (S * w_t[:, :, None, :]
             - sa[..., None] * (kk_t * a_t)[:, :, None, :]
             + v_t[..., None] * k_t[:, :, None, :])
        y_t = jnp.einsum('bhvk,bhk->bhv', S, r_t)
        return S, y_t

    S0 = jnp.zeros((B, H, N, N), f32)
    _, y = lax.scan(step, S0, (tm(rh), tm(wh), tm(kh), tm(vh), tm(kk), tm(ah)))
    y = jnp.moveaxis(y, 0, 1)
    mean = jnp.mean(y, axis=-1, keepdims=True)
    var = jnp.mean(jnp.square(y - mean), axis=-1, keepdims=True)
    y = ((y - mean) * lax.rsqrt(var + GN_EPS)).reshape(B, T, RWKV_WIDTH)
    y = y * gn_w.astype(f32) + gn_b.astype(f32)
    bonus = jnp.sum(rh * kh * r_k.reshape(H, N).astype(f32), axis=-1, keepdims=True) * vh
    return (y + bonus.reshape(B, T, RWKV_WIDTH)).astype(r.dtype)


def sliding_window_sink_attention(q, k, v, sinks):
    B, T, _ = q.shape
    nb = T // BLOCK
    f32 = jnp.float32
    KV, G, Dh = ATT_KV_HEADS, ATT_GROUP, ATT_HEAD
    qb = q.reshape(B, nb, BLOCK, KV, G, Dh)
    k = k.reshape(B, T, KV, Dh)
    v = v.reshape(B, T, KV, Dh)

    def banded(t):
        prev = jnp.pad(t, ((0, 0), (BLOCK, 0), (0, 0), (0, 0)))[:, :T]
        return jnp.concatenate([prev.reshape(B, nb, BLOCK, KV, Dh),
                                t.reshape(B, nb, BLOCK, KV, Dh)], axis=2)

    kb, vb = banded(k), banded(v)
    s = jnp.einsum('bnqhgd,bnkhd->bnhgqk', qb, kb).astype(f32) * (Dh ** -0.5)
    qi = jnp.arange(BLOCK)[:, None]
    kj = jnp.arange(2 * BLOCK)[None, :]
    dist = qi + BLOCK - kj
    band = (dist >= 0) & (dist < WINDOW)
    valid = band[None] & ((jnp.arange(nb)[:, None, None] > 0) | (kj[None] >= BLOCK))
    s = jnp.where(valid[None, :, None, None], s, NEG_INF)
    sink = jnp.broadcast_to(sinks.astype(f32).reshape(1, 1, KV, G, 1, 1), s.shape[:-1] + (1,))
    p = jax.nn.softmax(jnp.concatenate([s, sink], axis=-1), axis=-1)[..., :-1]
    o = jnp.einsum('bnhgqk,bnkhd->bnqhgd', p.astype(v.dtype), vb)
    return o.reshape(B, T, ATT_WIDTH)


def hybrid_layer(x, g_pre, w_in, mu_shift, w0, w_decay_up, a0, w_iclr_up, k_k, k_a, r_k,
                 gn_w, gn_b, b_qkv, sinks, w_branch_rwkv, w_branch_att, w_out):
    h = rms_norm(x, g_pre)
    p = jnp.einsum('btd,dc->btc', h, w_in)
    shifted, g_rwkv, qkv, g_att, gates = _split(
        p, (SHIFT_WIDTH, RWKV_WIDTH, QKV_WIDTH, ATT_WIDTH, N_BRANCHES * D_MODEL))
    shifted = token_shift(shifted, mu_shift)
    r, k, v, w_lo, a_lo = _split(shifted, (RWKV_WIDTH, RWKV_WIDTH, RWKV_WIDTH, DECAY_RANK, ICLR_RANK))
    y_rwkv = rwkv7_time_mix(r, k, v, w_lo, a_lo, w0, w_decay_up, a0, w_iclr_up,
                            k_k, k_a, r_k, gn_w, gn_b)
    q, k_att, v_att = _split(qkv + b_qkv, (ATT_WIDTH, ATT_KV_WIDTH, ATT_KV_WIDTH))
    y_att = sliding_window_sink_attention(q, k_att, v_att, sinks)
    br_rwkv = (y_rwkv * jax.nn.silu(g_rwkv)) @ w_branch_rwkv
    br_att = (y_att * jax.nn.silu(g_att)) @ w_branch_att
    gate_rwkv, gate_att = _split(jax.nn.sigmoid(gates), (D_MODEL, D_MODEL))
    merged = gate_rwkv * br_rwkv + gate_att * br_att
    return x + merged @ w_out


def setup_inputs(seed: int = 0) -> dict:
    key = jax.random.key(seed)
    ks = jax.random.split(key, 20)
    L = DEPTH
    f32 = jnp.float32
    nrm = lambda k, shape, scale: jax.random.normal(k, shape, f32) * scale
    return {
        'x': nrm(ks[0], (BATCH, SEQ, D_MODEL), 1.0),
        'g_pre': 1.0 + nrm(ks[1], (L, D_MODEL), 0.05),
        'w_in': nrm(ks[2], (L, D_MODEL, IN_COLS), D_MODEL ** -0.5),
        'mu_shift': jax.random.uniform(ks[3], (L, SHIFT_WIDTH), f32, 0.1, 0.9),
        'w0': jax.random.uniform(ks[4], (L, RWKV_WIDTH), f32, -6.0, 1.0),
        'w_decay_up': nrm(ks[5], (L, DECAY_RANK, RWKV_WIDTH), 0.5 * DECAY_RANK ** -0.5),
        'a0': nrm(ks[6], (L, RWKV_WIDTH), 0.5),
        'w_iclr_up': nrm(ks[7], (L, ICLR_RANK, RWKV_WIDTH), 0.5 * ICLR_RANK ** -0.5),
        'k_k': 0.85 + nrm(ks[8], (L, RWKV_WIDTH), 0.05),
        'k_a': 1.0 + nrm(ks[9], (L, RWKV_WIDTH), 0.05),
        'r_k': nrm(ks[10], (L, RWKV_WIDTH), 0.1),
        'gn_w': 1.0 + nrm(ks[11], (L, RWKV_WIDTH), 0.05),
        'gn_b': nrm(ks[12], (L, RWKV_WIDTH), 0.02),
        'b_qkv': nrm(ks[13], (L, QKV_WIDTH), 0.02),
        'sinks': nrm(ks[14], (L, ATT_Q_HEADS), 1.0),
        'w_branch_rwkv': nrm(ks[15], (L, RWKV_WIDTH, D_MODEL), RWKV_WIDTH ** -0.5),
        'w_branch_att': nrm(ks[16], (L, ATT_WIDTH, D_MODEL), ATT_WIDTH ** -0.5),
        'w_out': nrm(ks[17], (L, D_MODEL, D_MODEL), D_MODEL ** -0.5),
        'g_final': 1.0 + nrm(ks[18], (D_MODEL,), 0.05),
    }


def reference(x, g_pre, w_in, mu_shift, w0, w_decay_up, a0, w_iclr_up, k_k, k_a, r_k,
              gn_w, gn_b, b_qkv, sinks, w_branch_rwkv, w_branch_att, w_out, g_final):
    for l in range(DEPTH):
        x = hybrid_layer(x, g_pre[l], w_in[l], mu_shift[l], w0[l], w_decay_up[l], a0[l],
                         w_iclr_up[l], k_k[l], k_a[l], r_k[l], gn_w[l], gn_b[l], b_qkv[l],
                         sinks[l], w_branch_rwkv[l], w_branch_att[l], w_out[l])
    return rms_norm(x, g_final)
```

```python
import numpy as np
import concourse.bass as bass
import concourse.mybir as mybir
from concourse.bass_utils import run_bass_kernel_spmd

F32 = mybir.dt.float32
BF16 = mybir.dt.bfloat16
AF = mybir.ActivationFunctionType
ALU = mybir.AluOpType
AX = mybir.AxisListType

D = 1024
NCORES = 8
SEQ = 8192
OWN_TOK = 2048
C = 128
SBT = 256
CPS = SBT // C
RMS_EPS = 1e-6
GN_EPS = 64e-5
IN_COLS = 5504
O_SH = 0
O_GR = 1664
O_Q = 2176
O_K = 2688
O_V = 2816
O_GA = 2944
O_GT = 3456

PPI = {}
_n = 0
for _name, _cnt in [("mu", 13), ("w0", 4), ("a0", 4), ("kk", 4), ("ka", 4), ("rk", 4),
                    ("gnw", 4), ("gnb", 4), ("bq", 4), ("bk", 2), ("sink", 8)]:
    PPI[_name] = _n
    _n += _cnt
NPP_IN = _n
for _name, _cnt in [("omu", 13), ("nw0", 4), ("omka", 4), ("na0", 4)]:
    PPI[_name] = _n
    _n += _cnt
NPP = _n


class Buf:
    __slots__ = ("name", "w", "r", "excl")

    def __init__(self, name, excl=False):
        self.name = name
        self.w = None
        self.r = []
        self.excl = excl


class Sched:
    ENG = ("pe", "act", "dve", "pool", "sp")
    NDMA = 24

    def __init__(self, same_sync=True):
        self.ops = {e: [] for e in self.ENG}
        self.cnt = {e: 0 for e in self.ENG}
        self.waited = {e: {} for e in self.ENG}
        self.same_sync = same_sync
        self.dma_val = [0] * self.NDMA
        self.dma_rr = 0
        self.dma_rr2 = 0
        self.out_tokens = []

    def add(self, eng, fn, reads=(), writes=(), dma=False, is_out=False):
        self.total = getattr(self, "total", 0) + 1
        if not dma and self.total > getattr(self, "cut", 10 ** 9):
            return None
        deps = {}

        def need(tk, hard):
            d = deps.get(tk[0])
            if d is None:
                deps[tk[0]] = [tk[1], tk[2], hard]
            else:
                d[0] = max(d[0], tk[1])
                d[2] = d[2] or hard
        for b in reads:
            if b.w is not None:
                need(b.w, True)
            if b.excl:
                for tk in b.r:
                    need(tk, False)
        for b in writes:
            if b.w is not None:
                need(b.w, True)
            for tk in b.r:
                need(tk, False)
        waits = []
        for semkey, (val, src, hard) in deps.items():
            if src == eng and not isinstance(semkey, tuple):
                if eng in ("pe", "sp"):
                    continue
                if not hard or not self.same_sync:
                    continue
            if self.waited[eng].get(semkey, 0) >= val:
                continue
            self.waited[eng][semkey] = val
            waits.append((semkey, val))
        if dma:
            half = self.NDMA // 2
            if eng == "sp":
                j = self.dma_rr
                self.dma_rr = (self.dma_rr + 1) % half
            else:
                j = half + self.dma_rr2
                self.dma_rr2 = (self.dma_rr2 + 1) % half
            semkey = ("dma", j)
            if self.dma_val[j] > 0 and self.waited[eng].get(semkey, 0) < self.dma_val[j]:
                self.waited[eng][semkey] = self.dma_val[j]
                waits.append((semkey, self.dma_val[j]))
            self.dma_val[j] += 16
            tok = (semkey, self.dma_val[j], eng)
            inc = 16
        else:
            self.cnt[eng] += 1
            tok = (eng, self.cnt[eng], eng)
            inc = 1
        for b in reads:
            b.r.append(tok)
        for b in writes:
            b.w = tok
            b.r = []
        if is_out:
            self.out_tokens.append(tok)
        self.ops[eng].append((waits, fn, tok[0], inc))
        return tok

    def emit(self, nc, block, sems):
        engmap = {"pe": block.tensor, "act": block.scalar, "dve": block.vector,
                  "pool": block.gpsimd, "sp": block.sync}
        for e in self.ENG:
            ops = self.ops[e]
            final = list(self.out_tokens) if e == "sp" else ()

            def body(eng, ops=ops, final=final):
                for waits, fn, semkey, inc in ops:
                    for sk, val in waits:
                        eng.wait_ge(sems[sk], val)
                    fn(eng).then_inc(sems[semkey], inc)
                for tk in final:
                    eng.wait_ge(sems[tk[0]], tk[1])
                if final != ():
                    for j in range(self.NDMA):
                        if self.dma_val[j] > 0:
                            eng.wait_ge(sems[("dma", j)], self.dma_val[j])
            engmap[e](body)


def build(nsb=SEQ // SBT, nown=OWN_TOK // SBT, upto=99, dumps=(), same_sync=True, cut=None):
    from contextlib import ExitStack
    nc = bass.Bass("TRN2", target_bir_lowering=False)
    WT = nsb * SBT
    OT = nown * SBT
    NOC = nown * CPS
    S = Sched(same_sync=same_sync)
    if cut is not None:
        S.cut = cut

    def din(name, shape, dt=F32):
        return nc.dram_tensor(name, list(shape), dt, kind="ExternalInput").ap()

    xw = din("xw", [WT, D])
    w_in = din("w_in", [D, IN_COLS])
    w_br = din("w_br", [2, 512, D])
    w_out = din("w_out", [D, D])
    wdi = din("wdi", [128, 2, 512])
    pp_in = din("pp", [128, NPP_IN])
    gpre_d = din("gpre_b", [128, D])
    gfin_d = din("gfin_b", [128, D])
    bv_d = din("bv_b", [128, 128])
    cst_d = din("cst", [128, 9, 128])
    am_d = din("amask", [128, 2, 256])
    out_d = nc.dram_tensor("out", [OT, D], F32, kind="ExternalOutput").ap()
    dump_d = {}

    es = ExitStack()
    with es:
        def sb(name, shape, dt=F32):
            return es.enter_context(nc.sbuf_tensor(name, list(shape), dt))

        def ps(name, shape, dt=F32):
            return es.enter_context(nc.psum_tensor(name, list(shape), dt))

        class T:
            def __init__(self, name, shape, dt=F32, n=1):
                self.t = [sb(f"{name}{i}", shape, dt) for i in range(n)]
                self.b = [Buf(f"{name}{i}") for i in range(n)]
                self.n = n

            def __call__(self, i=0):
                return self.t[i % self.n], self.b[i % self.n]

        def dma(eng, out, in_, reads, writes, is_out=False):
            return S.add(eng, lambda e: e.dma_start(out=out, in_=in_), reads, writes, dma=True, is_out=is_out)

        def dump(name, ap, shape, reads, dt=F32):
            if name not in dumps:
                return
            dd = nc.dram_tensor("dbg_" + name, list(shape), dt, kind="ExternalOutput").ap()
            dump_d[name] = dd
            dma("sp", dd, ap, reads, [], is_out=True)

        xt = T("xt", [128, D], F32, 2)
        cst_b = T("cst_b", [128, 8, 128], BF16)
        cst2 = T("cst2", [128, 1, 128])
        amask = T("amask", [128, 2, 256])
        PP = T("PP", [128, NPP])
        gpre = T("gpre", [128, D])
        gfin = T("gfin", [128, D])
        bvb = T("bvb", [128, 128])
        Wdb = T("Wdb", [128, 3, 512], BF16)
        Wsh = T("Wsh", [128, 8, 1664], BF16)
        Wout = T("Wout", [128, 8, D], BF16)

        stg = xt.t[1][:, :].rearrange("p (a b) -> p a b", a=8)
        dma("sp", stg, cst_d[:, 0:8, :], [], [xt.b[1]])
        dma("sp", cst2.t[0][:], cst_d[:, 8:9, :], [], [cst2.b[0]])
        dma("sp", PP.t[0][:, 0:NPP_IN], pp_in, [], [PP.b[0]])
        dma("sp", gpre.t[0][:], gpre_d, [], [gpre.b[0]])
        stg_w = xt.t[0][:, :].rearrange("p (a b) -> p a b", a=2)
        dma("sp", stg_w, wdi, [], [xt.b[0]])
        S.add("act", lambda e: e.activation(out=Wdb.t[0][:, 0, :], in_=stg_w[:, 0, :], func=AF.Copy), [xt.b[0]], [Wdb.b[0]])
        S.add("act", lambda e: e.activation(out=Wdb.t[0][:, 2, :], in_=stg_w[:, 1, :], func=AF.Copy), [xt.b[0]], [Wdb.b[0]])
        S.add("dve", lambda e: e.tensor_tensor(out=Wdb.t[0][:, 1, :], in0=stg_w[:, 0, :], in1=Wdb.t[0][:, 0, :], op=ALU.subtract), [xt.b[0], Wdb.b[0]], [Wdb.b[0]])
        Wsh_b = [Buf(f"Wsh_k{k}") for k in range(8)]
        for k in range(8):
            S.add("pool", lambda e, k=k: e.dma_start(out=Wsh.t[0][:, k, :], in_=w_in[k * 128:(k + 1) * 128, O_SH:O_SH + 1664]),
                  [], [Wsh_b[k]], dma=True)
        dma("sp", amask.t[0][:], am_d, [], [amask.b[0]])
        dma("sp", gfin.t[0][:], gfin_d, [], [gfin.b[0]])
        dma("sp", bvb.t[0][:], bv_d, [], [bvb.b[0]])
        S.add("dve", lambda e: e.tensor_copy(out=cst_b.t[0][:], in_=stg), [xt.b[1]], [cst_b.b[0]])
        ident_b = cst_b.t[0][:, 0, :]
        mask4 = cst_b.t[0][:, 1:5, :]
        mle2 = cst_b.t[0][:, 5:7, :]
        bones_b = cst_b.t[0][:, 7, :]
        ones_f = cst2.t[0][:, 0, :]
        CB = cst_b.b[0]
        CF = cst2.b[0]
        ppt = PP.t[0]
        PB = PP.b[0]

        def pc(name, i=0):
            j = PPI[name] + i
            return ppt[:, j:j + 1]

        S.add("dve", lambda e: e.tensor_scalar(out=ppt[:, PPI["omu"]:PPI["omu"] + 13], in0=ppt[:, PPI["mu"]:PPI["mu"] + 13],
                                               scalar1=-1.0, scalar2=1.0, op0=ALU.mult, op1=ALU.add), [PB], [PB])
        S.add("dve", lambda e: e.tensor_scalar(out=ppt[:, PPI["nw0"]:PPI["nw0"] + 4], in0=ppt[:, PPI["w0"]:PPI["w0"] + 4],
                                               scalar1=-1.0, scalar2=None, op0=ALU.mult), [PB], [PB])
        S.add("dve", lambda e: e.tensor_scalar(out=ppt[:, PPI["omka"]:PPI["omka"] + 4], in0=ppt[:, PPI["ka"]:PPI["ka"] + 4],
                                               scalar1=-1.0, scalar2=1.0, op0=ALU.mult, op1=ALU.add), [PB], [PB])
        S.add("dve", lambda e: e.tensor_scalar(out=ppt[:, PPI["na0"]:PPI["na0"] + 4], in0=ppt[:, PPI["a0"]:PPI["a0"] + 4],
                                               scalar1=-1.0, scalar2=None, op0=ALU.mult), [PB], [PB])

        psA = [ps(f"psA{i}", [128, 512]) for i in range(2)]
        psA_b = [Buf(f"psA{i}", True) for i in range(2)]
        psT = [ps(f"psT{i}", [128, 1024], BF16) for i in range(2)]
        psT_b = [[Buf(f"psT{i}_{h}", True) for h in range(2)] for i in range(2)]
        psLU = [[ps(f"psL{i}", [128, 512]), ps(f"psU{i}", [128, 512])] for i in range(2)]
        psLU_b = [[[Buf(f"psLU{i}_{lu}_{s}", True) for s in range(4)] for lu in range(2)] for i in range(2)]
        arr = [0]
        prr = [0]
        srr = [0]

        def fullbank():
            i = arr[0]
            arr[0] = (i + 1) % 2
            return psA[i], [psA_b[i]]

        def pair(ns):
            r = prr[0]
            if (r % 4) + ns > 4:
                r = (r // 4 + 1) * 4
            r %= 8
            p, s = r // 4, r % 4
            prr[0] = (r + ns) % 8
            sl = slice(s * 128, (s + ns) * 128)
            return (psLU[p][0][:, sl], psLU_b[p][0][s:s + ns], psLU[p][1][:, sl], psLU_b[p][1][s:s + ns])

        brr = [0]

        def bankx():
            i = brr[0]
            brr[0] = (i + 1) % 4
            p, lu = i // 2, i % 2
            return psLU[p][lu], list(psLU_b[p][lu])

        def single(ns):
            bk, bb = bankx()
            return bk[:, 0:ns * 128], bb

        hb = T("hb", [128, D], BF16, 2)
        hT = T("hT", [128, 8, SBT], BF16, 2)
        st0 = T("st0", [128, 4], F32, 2)
        shwa = T("shwa", [128, SBT])
        shtmp = T("shtmp", [128, SBT], F32, 1)
        shr = T("shr", [128, SBT], F32, 2)
        shk = T("shk", [128, SBT], F32, 2)
        shv = T("shv", [128, SBT], F32, 2)
        tw = T("tw", [128, SBT])
        tw_hi = T("tw_hi", [128, SBT], BF16)
        tw_lo = T("tw_lo", [128, SBT], BF16)
        t_k2b = T("t_k2b", [128, SBT], BF16)
        t_rkb = T("t_rkb", [128, SBT], BF16)
        Hhl = T("Hhl", [128, 2, 64], BF16, 4)
        t_e1 = T("t_e1", [128, SBT])
        t_ew = T("t_ew", [128, SBT])
        t_a = T("t_a", [128, SBT])
        t_cs = T("t_cs", [128, SBT])
        t_csp = T("t_csp", [128, SBT])
        t_en = T("t_en", [128, SBT])
        t_ep = T("t_ep", [128, SBT])
        t_k2 = T("t_k2", [128, SBT])
        t_kkn = T("t_kkn", [128, SBT])
        t_ab = T("t_ab", [128, SBT])
        t_f = T("t_f", [128, SBT])
        gC = T("gC", [128, CPS], F32, 8)
        AR = T("AR", [128, CPS, 2, C], BF16, 4)
        BT = T("BT", [128, SBT], BF16, 4)
        KT = T("KT", [128, SBT], BF16, 4)
        vbf = T("vbf", [128, SBT], BF16, 4)
        bonus = T("bonus", [128, SBT], BF16, 4)
        tm = T("tm", [128, 4, 128], BF16, 4)
        PZ = T("PZ", [128, 3, SBT], BF16, 8)
        Hbfz = T("Hbfz", [128, 64], BF16, 8)
        qTz = T("qTz", [128, SBT], BF16, 8)
        NG = 4
        NMt = T("NM", [128, 2, 2, 128], BF16, 2 * NG)
        Mak = T("Mak", [128, 2, 128], BF16, NG)
        RBK = T("RBK", [128, 2, 2, 128], BF16, NG)
        PAIRS = [(0, 1), (2, 3)]
        Xtile = T("Xt", [128, 2, 2, 64], BF16, 2 * NG)
        ATbd = T("ATbd", [128, 128], BF16, NG)
        Gsb = T("Gsb", [128, 64], F32, NG)
        Ht = T("Hst", [128, 64], F32, 4)
        s1t = T("s1t", [128, 64], F32, NG)
        Wz = T("Wz", [128, 3, 64], BF16, NG)
        Wp = T("Wp", [128, 2, 64], BF16, NG)
        zlo = T("zlo", [128, 64], F32, NG)
        QT = T("QT", [128, 128], BF16, NG)
        prevcol = T("prevcol", [128, 13])
        kc = T("kcols", [128, 8])
        NWS = 4
        ws = T("ws", [128, 8, 128], BF16, NWS)
        ws_b2 = [Buf(f"ws_b2_{i}") for i in range(NWS)]
        wsrr = [0]
        sgr = T("sgr", [128, SBT], BF16, 4)
        sga = T("sga", [128, SBT], BF16, 4)
        NKS = 4
        KTatt = T("KTatt", [128, NKS * 128], BF16, 2)
        NV = 4
        Vpad = T("Vpad", [128, 2, 192], BF16, NV)
        ysqt = T("ysq", [128, 512], F32, 1)
        yn = T("yn", [128, 512], BF16, 1)
        gst = T("gst", [128, 6, 8], F32, 1)
        t1t = T("t1t", [128, 128], F32, 2)
        zr = T("zr", [128, SBT], BF16, 4)
        zatt = T("zatt", [128, SBT], BF16, 4)
        smt = T("smt", [128, 256], F32, 2)
        p32 = T("p32", [128, 256], F32, 2)
        pnt = T("pnt", [128, 256], BF16, 2)
        ptt = T("ptt", [128, 2, 128], BF16, 2)
        ast = T("ast", [128, 8], F32, 2)
        mT = T("mT", [128, 8, SBT], BF16, 1)
        sgt = T("sgt", [128, SBT], F32, 2)
        m12 = T("m12", [128, SBT], F32, 2)
        fst = T("fst", [128, 4], F32, 2)

        S.add("pool", lambda e: e.memset(prevcol.t[0][:], 0.0), [], [prevcol.b[0]])
        kct = kc.t[0]
        KB = kc.b[0]
        for j, val in enumerate([RMS_EPS, 1.0, -0.5, 1e-12, GN_EPS]):
            S.add("pool", lambda e, j=j, val=val: e.memset(kct[:, j:j + 1], val), [], [KB])
        eps_col = kct[:, 0:1]
        one_col = kct[:, 1:2]
        mhalf_col = kct[:, 2:3]
        tiny_col = kct[:, 3:4]
        gneps_col = kct[:, 4:5]
        for i in range(NG):
            S.add("pool", lambda e, i=i: e.memset(ATbd.t[i][:], 0.0), [], [ATbd.b[i]])
            S.add("pool", lambda e, i=i: e.memset(Wz.t[i][:], 0.0), [], [Wz.b[i]])
        for i in range(4):
            S.add("pool", lambda e, i=i: e.memset(Ht.t[i][:], 0.0), [], [Ht.b[i]])
        for i in range(NV):
            S.add("pool", lambda e, i=i: e.memset(Vpad.t[i][:], 0.0), [], [Vpad.b[i]])
        for i in range(8):
            S.add("pool", lambda e, i=i: e.memset(PZ.t[i][:], 0.0), [], [PZ.b[i]])
            S.add("pool", lambda e, i=i: e.memset(Hbfz.t[i][:], 0.0), [], [Hbfz.b[i]])
            S.add("pool", lambda e, i=i: e.memset(qTz.t[i][:], 0.0), [], [qTz.b[i]])
        for i in range(2):
            S.add("pool", lambda e, i=i: e.memset(KTatt.t[i][:], 0.0), [], [KTatt.b[i]])
        Wout_b = [Buf(f"wout{k}") for k in range(8)]
        for k in range(8):
            S.add("pool", lambda e, k=k: e.dma_start(out=Wout.t[0][:, k, :], in_=w_out[k * 128:(k + 1) * 128, :]),
                  [], [Wout_b[k]], dma=True)

        def bcm(ap2, n):
            a = ap2.ap
            return bass.AP(ap2.tensor, ap2.offset, [list(a[0]), [0, n], list(a[1])])

        def bcl(ap2, n):
            a = ap2.ap
            return bass.AP(ap2.tensor, ap2.offset, [list(a[0]), list(a[1]), [0, n]])

        def v3(ap, h=2):
            return ap.rearrange("p (h t) -> p h t", h=h)

        def rms_rstd(in_ap, in_bufs, stt, stb, junk_ap, junk_buf):
            S.add("act", lambda e: e.activation(out=junk_ap, in_=in_ap, func=AF.Square, accum_out=stt[:, 0:1]),
                  in_bufs, [junk_buf, stb])
            S.add("act", lambda e: e.activation(out=stt[:, 1:2], in_=stt[:, 0:1], func=AF.Ln, bias=eps_col, scale=1.0 / D),
                  [stb, KB], [stb])
            S.add("act", lambda e: e.activation(out=stt[:, 2:3], in_=stt[:, 1:2], func=AF.Exp, scale=-0.5), [stb], [stb])

        def ws_load(srcs):
            i = wsrr[0]
            wsrr[0] = (i + 1) % NWS
            t = ws.t[i]
            bufs = [ws.b[i], ws_b2[i]]
            for j, (dfn, dap) in enumerate(srcs):
                S.add("pool", lambda e, dfn=dfn, dap=dap, t=t: e.dma_start(out=dfn(t), in_=dap), [], [bufs[j]], dma=True)
            return t, bufs[:len(srcs)]

        def wcols(c0, n=128):
            return w_in[:, c0:c0 + n].rearrange("(k p) c -> p k c", p=128)

        def proj_fm(hTt, hTb, wt, wbufs, ncols=SBT, col0=0):
            pa, pab = fullbank()
            for k in range(8):
                S.add("pe", lambda e, pa=pa, k=k, wt=wt, hTt=hTt: e.matmul(
                    pa[:, 0:ncols], lhsT=wt[:, k, :], rhs=hTt[:, k, col0:col0 + ncols], start=(k == 0), stop=(k == 7)),
                    wbufs + [hTb], pab)
            return pa, pab

        P_ = [slice(0, 64), slice(64, 128)]
        own0 = nsb - nown

        def stage2_header():
            swt, swb = shwa()
            twt, twb = tw()
            S.add("act", lambda e, twt=twt, swt=swt: e.activation(out=twt[0:64, :], in_=swt[0:64, :], func=AF.Exp, scale=2.0), [swb], [twb])
            S.add("dve", lambda e, twt=twt: e.tensor_scalar(out=twt[0:64, :], in0=twt[0:64, :], scalar1=1.0, scalar2=None, op0=ALU.add), [twb], [twb])
            S.add("dve", lambda e, twt=twt: e.reciprocal(out=twt[0:64, :], in_=twt[0:64, :]), [twb], [twb])
            S.add("dve", lambda e, twt=twt: e.tensor_scalar(out=twt[0:64, :], in0=twt[0:64, :], scalar1=-2.0, scalar2=1.0, op0=ALU.mult, op1=ALU.add), [twb], [twb])
            S.add("act", lambda e, twt=twt, swt=swt: e.activation(out=twt[64:128, :], in_=swt[64:128, :], func=AF.Copy), [swb, twb], [twb])
            twh, twhb = tw_hi()
            twl, twlb = tw_lo()
            S.add("act", lambda e, twh=twh, twt=twt: e.activation(out=twh[:], in_=twt[:], func=AF.Copy), [twb], [twhb])
            S.add("dve", lambda e, twl=twl, twt=twt, twh=twh: e.tensor_tensor(out=twl[:], in0=twt[:], in1=twh[:], op=ALU.subtract), [twb, twhb], [twlb])
            return dict(twt=twt, twh=twh, twl=twl, twhb=twhb, twlb=twlb, twb=twb)
        def prep_hp(hp, sbi, own, twt=None, twh=None, twl=None, twhb=None, twlb=None, twb=None):
            si = sbi * 4 + hp
            rt, rb = shr(si)
            kt_, kb_ = shk(si)
            vt, vb = shv(si)
            pD, pDb = fullbank()
            pAa, pAb = fullbank()
            hsl = slice(hp * 128, (hp + 1) * 128)
            S.add("pe", lambda e, pD=pD, hsl=hsl, twh=twh: e.matmul(pD[:, 0:SBT], lhsT=Wdb.t[0][:, 0, hsl], rhs=twh[:, :], start=True, stop=False),
                  [Wdb.b[0], twhb], pDb)
            S.add("pe", lambda e, pD=pD, hsl=hsl, twl=twl: e.matmul(pD[:, 0:SBT], lhsT=Wdb.t[0][:, 0, hsl], rhs=twl[:, :], start=False, stop=False),
                  [Wdb.b[0], twlb], pDb)
            S.add("pe", lambda e, pD=pD, hsl=hsl, twh=twh: e.matmul(pD[:, 0:SBT], lhsT=Wdb.t[0][:, 1, hsl], rhs=twh[:, :], start=False, stop=True),
                  [Wdb.b[0], twhb], pDb)
            S.add("pe", lambda e, pAa=pAa, hsl=hsl, twh=twh: e.matmul(pAa[:, 0:SBT], lhsT=Wdb.t[0][:, 2, hsl], rhs=twh[:, :], start=True, stop=True),
                  [Wdb.b[0], twhb], pAb)
            e1, e1b = t_e1()
            ew, ewb = t_ew()
            at, ab_ = t_a()
            cs, csb = t_cs()
            csp, cspb = t_csp()
            en, enb = t_en()
            k2, k2b = t_k2()
            kkn, kknb = t_kkn()
            abt, abb = t_ab()
            ft, fb = t_f()
            S.add("act", lambda e, e1=e1, pD=pD, hp=hp: e.activation(out=e1[:], in_=pD[:, 0:SBT], func=AF.Exp, bias=pc("nw0", hp), scale=-1.0),
                  pDb + [PB], [e1b])
            S.add("act", lambda e, e1=e1: e.activation(out=e1[:], in_=e1[:], func=AF.Ln, bias=one_col), [e1b, KB], [e1b])
            S.add("act", lambda e, e1=e1, ew=ew: e.activation(out=ew[:], in_=e1[:], func=AF.Exp, bias=mhalf_col, scale=-1.0), [e1b, KB], [ewb])
            S.add("act", lambda e, at=at, pAa=pAa, hp=hp: e.activation(out=at[:], in_=pAa[:, 0:SBT], func=AF.Exp, bias=pc("na0", hp), scale=-1.0),
                  pAb + [PB], [ab_])
            yield
            S.add("dve", lambda e, at=at: e.tensor_scalar(out=at[:], in0=at[:], scalar1=1.0, scalar2=None, op0=ALU.add), [ab_], [ab_])
            S.add("dve", lambda e, at=at: e.reciprocal(out=at[:], in_=at[:]), [ab_], [ab_])
            for c in range(CPS):
                S.add("dve", lambda e, cs=cs, ew=ew, c=c: e.tensor_tensor_scan(
                    out=cs[:, c * C:(c + 1) * C], data0=ones_f, data1=ew[:, c * C:(c + 1) * C], initial=0.0,
                    op0=ALU.mult, op1=ALU.add), [ewb, CF], [csb])
            S.add("pool", lambda e, csp=csp, cs=cs, ew=ew: e.tensor_tensor(out=csp[:], in0=cs[:], in1=ew[:], op=ALU.subtract), [csb, ewb], [cspb])
            S.add("act", lambda e, en=en, cs=cs: e.activation(out=en[:], in_=cs[:], func=AF.Exp), [csb], [enb])
            S.add("act", lambda e, csp=csp: e.activation(out=csp[:], in_=csp[:], func=AF.Exp, scale=-1.0), [cspb], [cspb])
            gct, gcb = gC(si)
            S.add("act", lambda e, gct=gct, cs=cs: e.activation(
                out=gct[:, 0:CPS], in_=cs[:, :].rearrange("p (c t) -> p c t", t=C)[:, :, C - 1], func=AF.Exp, scale=-1.0), [csb], [gcb])
            yield
            k2h, k2hb = t_k2b()
            S.add("act", lambda e, k2h=k2h, kt_=kt_, hp=hp: e.activation(out=k2h[:], in_=kt_[:], func=AF.Square, scale=pc("kk", hp)), [kb_, PB], [k2hb])
            pS_, pSb = fullbank()
            S.add("pe", lambda e, pS_=pS_, k2h=k2h: e.matmul(pS_[:, 0:SBT], lhsT=bones_b, rhs=k2h[:], start=True, stop=True), [CB, k2hb], pSb)
            S.add("act", lambda e, k2=k2, pS_=pS_: e.activation(out=k2[:], in_=pS_[:, 0:SBT], func=AF.Ln, bias=tiny_col), pSb + [KB], [k2b])
            S.add("act", lambda e, k2=k2: e.activation(out=k2[:], in_=k2[:], func=AF.Exp, scale=-0.5), [k2b], [k2b])
            S.add("dve", lambda e, kkn=kkn, kt_=kt_, k2=k2, hp=hp: e.scalar_tensor_tensor(
                out=kkn[:], in0=kt_[:], scalar=pc("kk", hp), in1=k2[:], op0=ALU.mult, op1=ALU.mult), [kb_, k2b, PB], [kknb])
            yield
            ARt, ARb = AR(si)
            BTt, BTb = BT(si)
            KTt, KTb = KT(si)
            vbt, vbb = vbf(si)
            S.add("dve", lambda e, ARt=ARt, kkn=kkn, csp=csp: e.scalar_tensor_tensor(
                out=ARt[:, :, 0, :], in0=kkn[:, :].rearrange("p (c t) -> p c t", t=C), scalar=-1.0,
                in1=csp[:, :].rearrange("p (c t) -> p c t", t=C), op0=ALU.mult, op1=ALU.mult), [kknb, cspb], [ARb])
            S.add("pool", lambda e, abt=abt, kkn=kkn, at=at: e.tensor_tensor(out=abt[:], in0=kkn[:], in1=at[:], op=ALU.mult), [kknb, ab_], [abb])
            S.add("pool", lambda e, BTt=BTt, abt=abt, en=en: e.tensor_tensor(out=BTt[:], in0=abt[:], in1=en[:], op=ALU.mult), [abb, enb], [BTb])
            yield
            S.add("dve", lambda e, ft=ft, at=at, hp=hp: e.tensor_scalar(out=ft[:], in0=at[:], scalar1=pc("ka", hp), scalar2=pc("omka", hp),
                                                                    op0=ALU.mult, op1=ALU.add), [ab_, PB], [fb])
            S.add("pool", lambda e, ft=ft, kt_=kt_: e.tensor_tensor(out=ft[:], in0=kt_[:], in1=ft[:], op=ALU.mult), [kb_, fb], [fb])
            S.add("pool", lambda e, KTt=KTt, ft=ft, en=en: e.tensor_tensor(out=KTt[:], in0=ft[:], in1=en[:], op=ALU.mult), [fb, enb], [KTb])
            S.add("act", lambda e, vbt=vbt, vt=vt: e.activation(out=vbt[:], in_=vt[:], func=AF.Copy), [vb], [vbb])
            yield
            for hh in range(2):
                zt, zb = PZ(si * 2 + hh)
                S.add("pool", lambda e, zt=zt, ARt=ARt, hh=hh: e.tensor_copy(out=zt[P_[hh], 0, :].rearrange("p (c t) -> p c t", t=C), in_=ARt[P_[hh], :, 0, :]), [ARb], [zb])
                S.add("pool", lambda e, zt=zt, BTt=BTt, hh=hh: e.tensor_copy(out=zt[P_[hh], 1, :], in_=BTt[P_[hh], :]), [BTb], [zb])
                S.add("pool", lambda e, zt=zt, KTt=KTt, hh=hh: e.tensor_copy(out=zt[P_[hh], 2, :], in_=KTt[P_[hh], :]), [KTb], [zb])
            if own:
                ep, epb = t_ep()
                S.add("act", lambda e, ep=ep, cs=cs: e.activation(out=ep[:], in_=cs[:], func=AF.Exp, scale=-1.0), [csb], [epb])
                S.add("dve", lambda e, ARt=ARt, rt=rt, ep=ep: e.tensor_tensor(
                    out=ARt[:, :, 1, :], in0=rt[:, :].rearrange("p (c t) -> p c t", t=C),
                    in1=ep[:, :].rearrange("p (c t) -> p c t", t=C), op=ALU.mult), [rb, epb], [ARb])
                rkb_t, rkb_b = t_rkb()
                S.add("dve", lambda e, rkb_t=rkb_t, rt=rt, ft=ft, hp=hp: e.scalar_tensor_tensor(
                    out=rkb_t[:], in0=rt[:], scalar=pc("rk", hp), in1=ft[:], op0=ALU.mult, op1=ALU.mult), [rb, fb, PB], [rkb_b])
                pB_, pBb = fullbank()
                S.add("pe", lambda e, pB_=pB_, rkb_t=rkb_t: e.matmul(pB_[:, 0:SBT], lhsT=bones_b, rhs=rkb_t[:], start=True, stop=True), [CB, rkb_b], pBb)
                bnt, bnb = bonus(si)
                S.add("dve", lambda e, bnt=bnt, pB_=pB_, vt=vt: e.tensor_tensor(out=bnt[:], in0=pB_[:, 0:SBT], in1=vt[:], op=ALU.mult), pBb + [vb], [bnb])
            yield
        def proj_tile(sbi, ct, hTt, hTb):
            pa, pab = fullbank()
            for k in range(8):
                S.add("pe", lambda e, pa=pa, k=k, ct=ct, hTt=hTt: e.matmul(
                    pa[:, 0:SBT], lhsT=Wsh.t[0][:, k, ct * 128:(ct + 1) * 128], rhs=hTt[:, k, :],
                    start=(k == 0), stop=(k == 7)), [Wsh_b[k], hTb], pab)
            if ct == 12:
                dst, dstb = shwa()
            else:
                hp = ct % 4
                dst, dstb = (shr, shk, shv)[ct // 4](sbi * 4 + hp)
            tmp, tmpb = shtmp(ct)
            S.add("act", lambda e, tmp=tmp, pa=pa, ct=ct: e.activation(
                out=tmp[:], in_=pa[:, 0:SBT], func=AF.Copy, scale=pc("omu", ct)), pab + [PB], [tmpb])
            S.add("dve", lambda e, dst=dst, pa=pa, tmp=tmp, ct=ct: e.scalar_tensor_tensor(
                out=dst[:, 1:SBT], in0=pa[:, 0:SBT - 1], scalar=pc("mu", ct), in1=tmp[:, 1:SBT],
                op0=ALU.mult, op1=ALU.add), pab + [tmpb, PB], [dstb])
            S.add("dve", lambda e, dst=dst, tmp=tmp, ct=ct: e.scalar_tensor_tensor(
                out=dst[:, 0:1], in0=prevcol.t[0][:, ct:ct + 1], scalar=pc("mu", ct), in1=tmp[:, 0:1],
                op0=ALU.mult, op1=ALU.add), [prevcol.b[0], tmpb, PB], [dstb])
            S.add("act", lambda e, pa=pa, ct=ct: e.activation(
                out=prevcol.t[0][:, ct:ct + 1], in_=pa[:, SBT - 1:SBT], func=AF.Copy), pab, [prevcol.b[0]])

        sbst = {}

        def gen_first(sbi):
            own = sbi >= own0
            hTt, hTb = hT(sbi)
            for j in range(CPS):
                gc = sbi * CPS + j
                xtt, xtb = xt(gc)
                hbt, hbb = hb(gc)
                stt, stb = st0(gc)
                dma("sp", xtt[:], xw[gc * C:(gc + 1) * C, :], [], [xtb])
                rms_rstd(xtt[:], [xtb], stt, stb, hbt[:], hbb)
                S.add("dve", lambda e, xtt=xtt, stt=stt, hbt=hbt: e.scalar_tensor_tensor(
                    out=hbt[:], in0=xtt[:], scalar=stt[:, 2:3], in1=gpre.t[0][:], op0=ALU.mult, op1=ALU.mult),
                    [xtb, stb, gpre.b[0]], [hbb])
                yield
                pst = psT[0]
                pstb = psT_b[0]
                for k in range(8):
                    S.add("pe", lambda e, k=k, hbt=hbt, pst=pst: e.transpose(
                        out=pst[:, k * 128:(k + 1) * 128], in_=hbt[:, k * 128:(k + 1) * 128], identity=ident_b),
                        [hbb, CB], pstb)
                S.add("act", lambda e, pst=pst, hTt=hTt, j=j: e.activation(
                    out=hTt[:, :, j * C:(j + 1) * C], in_=pst[:, :].rearrange("p (k t) -> p k t", k=8), func=AF.Copy),
                    pstb, [hTb])
                yield
            proj_tile(sbi, 12, hTt, hTb)
            tw_ctx = stage2_header()
            sbst[sbi] = (tw_ctx, hTt, hTb)
            yield
            for hp in (0, 1):
                for q in range(3):
                    proj_tile(sbi, q * 4 + hp, hTt, hTb)
                    yield
                yield from prep_hp(hp, sbi, own, **tw_ctx)

        def gen_second(sbi):
            own = sbi >= own0
            tw_ctx, hTt, hTb = sbst[sbi]
            for hp in (2, 3):
                for q in range(3):
                    proj_tile(sbi, q * 4 + hp, hTt, hTb)
                    yield
                yield from prep_hp(hp, sbi, own, **tw_ctx)

        def drain(g):
            for _ in g:
                pass

        def mkfill(g, n=1):
            def fill():
                for _ in range(n):
                    try:
                        next(g)
                    except StopIteration:
                        return
            return fill

        drain(gen_first(0))
        for sbi in range(nsb):
            own = sbi >= own0
            halo_sb = (sbi == own0 - 1)
            hTt, hTb = hT(sbi)
            if own:
                drain(gen_second(sbi))
            if own or halo_sb:
                ncols, col0 = (SBT, 0) if own else (C, SBT - C)
                kcol = ((sbi - own0) * CPS + 1) * C if own else 0
                for g in range(2):
                    wt, wb = ws_load([(lambda t: t[:, :, 0:64], wcols(O_K + g * 64, 64)), (lambda t: t[:, :, 64:128], wcols(O_K + g * 64, 64))])
                    pa, pab = proj_fm(hTt, hTb, wt, wb, ncols, col0)
                    for cc in range(ncols // C):
                        ks = ((kcol // C) + cc) % NKS
                        S.add("act", lambda e, pa=pa, g=g, ks=ks, cc=cc: e.activation(
                            out=KTatt.t[g][:, ks * C:(ks + 1) * C], in_=pa[:, cc * C:(cc + 1) * C], func=AF.Identity, bias=pc("bk", g)), pab + [PB], [KTatt.b[g]])
                wt, wb = ws_load([(lambda t: t[:, :, :], wcols(O_V))])
                for c in (range(CPS) if own else [CPS - 1]):
                    lc1 = (sbi - own0) * CPS + c + 1 if own else 0
                    pa, pab = fullbank()
                    for k in range(8):
                        S.add("pe", lambda e, pa=pa, k=k, wt=wt, hTt=hTt, c=c: e.matmul(
                            pa[:, 0:128], lhsT=hTt[:, k, c * C:(c + 1) * C], rhs=wt[:, k, :], start=(k == 0), stop=(k == 7)), wb + [hTb], pab)
                    vp, vpb = Vpad(lc1)
                    S.add("dve", lambda e, vp=vp, pa=pa: e.tensor_tensor(out=vp[:, :, 0:64], in0=v3(pa[:, 0:128]), in1=v3(bvb.t[0][:, :]), op=ALU.add),
                          pab + [bvb.b[0]], [vpb])
                    S.add("pool", lambda e, vp=vp: e.tensor_copy(out=vp[:, :, 128:192], in_=vp[:, :, 0:64]), [vpb], [vpb])
            if own:
                for ct in range(4):
                    si = sbi * 4 + ct
                    wt, wb = ws_load([(lambda t: t[:, :, :], wcols(O_GR + ct * 128))])
                    pa, pab = proj_fm(hTt, hTb, wt, wb)
                    S.add("act", lambda e, pa=pa, si=si: e.activation(out=sgr(si)[0][:], in_=pa[:, 0:SBT], func=AF.Silu), pab, [sgr(si)[1]])
                    wt, wb = ws_load([(lambda t: t[:, :, :], wcols(O_Q + ct * 128))])
                    pa, pab = proj_fm(hTt, hTb, wt, wb)
                    for hh in range(2):
                        qz, qzb = qTz(si * 2 + hh)
                        S.add("act", lambda e, pa=pa, qz=qz, ct=ct, hh=hh: e.activation(
                            out=qz[P_[hh], :], in_=pa[P_[hh], 0:SBT], func=AF.Identity, bias=ppt[P_[hh], PPI["bq"] + ct:PPI["bq"] + ct + 1]),
                            pab + [PB], [qzb])
                    wt, wb = ws_load([(lambda t: t[:, :, :], wcols(O_GA + ct * 128))])
                    pa, pab = proj_fm(hTt, hTb, wt, wb)
                    S.add("act", lambda e, pa=pa, si=si: e.activation(out=sga(si)[0][:], in_=pa[:, 0:SBT], func=AF.Silu), pab, [sga(si)[1]])
            def emit_chunk_pairs(c, pairs, fill, own=own, sbi=sbi):
                gch = sbi * CPS + c
                csl = slice(c * C, (c + 1) * C)
                pYbank, pYbb = psA[0], [psA_b[0]]
                def mkctx(hp):
                    si = sbi * 4 + hp
                    gi = gch * 4 + hp
                    x = dict(hp=hp, si=si, gi=gi)
                    x["AR"], x["ARb"] = AR(si)
                    x["BT"], x["BTb"] = BT(si)
                    x["KT"], x["KTb"] = KT(si)
                    x["vb"], x["vbb"] = vbf(si)
                    x["zts"] = [PZ(si * 2 + hh) for hh in range(2)]
                    x["tm"], x["tmb"] = tm(gi)
                    return x

                def g_transposes(x, c=c, csl=csl):
                    pt_ = psT[1][:, 0:512]
                    ptb = psT_b[1]
                    srcs = [(x["AR"][:, c, 0, :], x["ARb"]), (x["BT"][:, csl], x["BTb"]), (x["KT"][:, csl], x["KTb"]), (x["vb"][:, csl], x["vbb"])]
                    for q, (sap, sbf) in enumerate(srcs):
                        S.add("pe", lambda e, pt_=pt_, q=q, sap=sap: e.transpose(out=pt_[:, q * 128:(q + 1) * 128], in_=sap, identity=ident_b),
                              [sbf, CB], ptb)
                    tmt = x["tm"]
                    S.add("act", lambda e, tmt=tmt, pt_=pt_: e.activation(out=tmt[:], in_=pt_.rearrange("p (q t) -> p q t", q=4), func=AF.Copy),
                          ptb, [x["tmb"]])

                def g_sprod_pe(x, c=c, csl=csl):
                    x["pS1"], x["pS1b"] = single(4)
                    x["pS2"], x["pS2b"] = single(2)
                    ARt, BTt = x["AR"], x["BT"]
                    for hh in range(2):
                        zt, zb = x["zts"][hh]
                        S.add("pe", lambda e, pS=x["pS1"], zt=zt, ARt=ARt, hh=hh: e.matmul(
                            pS[:, hh * 128:(hh + 1) * 128], lhsT=zt[:, 1, csl], rhs=ARt[:, c, 0, :], start=True, stop=True), [zb, x["ARb"]], x["pS1b"])
                    for hh in range(2):
                        zt, zb = x["zts"][hh]
                        S.add("pe", lambda e, pS=x["pS1"], zt=zt, BTt=BTt, hh=hh: e.matmul(
                            pS[:, (2 + hh) * 128:(3 + hh) * 128], lhsT=zt[:, 0, csl], rhs=BTt[:, csl], start=True, stop=True), [zb, x["BTb"]], x["pS1b"])
                    for hh in range(2):
                        zt, zb = x["zts"][hh]
                        S.add("pe", lambda e, pS=x["pS2"], zt=zt, ARt=ARt, hh=hh: e.matmul(
                            pS[:, hh * 128:(hh + 1) * 128], lhsT=zt[:, 2, csl], rhs=ARt[:, c, 0, :], start=True, stop=True), [zb, x["ARb"]], x["pS2b"])

                def g_sprod_evac(x):
                    gi = x["gi"]
                    nm, nmb = NMt(gi * 2)
                    mk, mkb = Mak(gi)
                    S.add("dve", lambda e, nm=nm, pS=x["pS1"]: e.tensor_tensor(out=nm[:, :, :, :].rearrange("p a h t -> p (a h) t"), in0=v3(pS, 4), in1=mask4, op=ALU.mult),
                          x["pS1b"] + [CB], [nmb])
                    S.add("dve", lambda e, mk=mk, pS=x["pS2"]: e.tensor_tensor(out=mk[:], in0=v3(pS, 2), in1=mask4[:, 0:2, :], op=ALU.mult),
                          x["pS2b"] + [CB], [mkb])
                    x["nm"], x["nmb"], x["mk"], x["mkb"] = nm, nmb, mk, mkb

                def g_r_pe(x, c=c, csl=csl):
                    x["pR"], x["pRb"] = single(4)
                    ARt = x["AR"]
                    for a_ in range(2):
                        for hh in range(2):
                            zt, zb = x["zts"][hh]
                            S.add("pe", lambda e, pR=x["pR"], zt=zt, ARt=ARt, hh=hh, a_=a_: e.matmul(
                                pR[:, (a_ * 2 + hh) * 128:(a_ * 2 + hh + 1) * 128], lhsT=zt[:, 1 + a_, csl], rhs=ARt[:, c, 1, :], start=True, stop=True),
                                [zb, x["ARb"]], x["pRb"])

                def g_r_evac(x, c=c, csl=csl):
                    rbk, rbkb = RBK(x["gi"])
                    for a_ in range(2):
                        S.add("dve", lambda e, rbk=rbk, pR=x["pR"], a_=a_: e.tensor_tensor(out=rbk[:, a_, :, :], in0=v3(pR[:, a_ * 256:(a_ + 1) * 256], 2), in1=mle2, op=ALU.mult),
                              x["pRb"] + [CB], [rbkb])
                    x["rbk"], x["rbkb"] = rbk, rbkb

                def g_pv_pe(x, c=c, csl=csl):
                    x["pV"], x["pVb"] = single(1)
                    mk, tmt = x["mk"], x["tm"]
                    for hh in range(2):
                        S.add("pe", lambda e, pV=x["pV"], hh=hh, mk=mk, tmt=tmt: e.matmul(
                            pV[:, hh * 64:(hh + 1) * 64], lhsT=mk[:, hh, :], rhs=tmt[:, 3, hh * 64:(hh + 1) * 64], start=True, stop=True), [x["mkb"], x["tmb"]], x["pVb"])

                def g_x0(x, c=c, csl=csl):
                    Xt, Xb = Xtile(x["gi"] * 2)
                    tmt = x["tm"]
                    S.add("pool", lambda e, Xt=Xt, tmt=tmt: e.tensor_copy(out=Xt[:, :, 0, :], in_=tmt[:, 0, :].rearrange("p (h k) -> p h k", h=2)), [x["tmb"]], [Xb])
                    S.add("act", lambda e, Xt=Xt, pV=x["pV"]: e.activation(out=Xt[:, :, 1, :], in_=pV[:, 0:128].rearrange("p (h k) -> p h k", h=2), func=AF.Copy),
                          x["pVb"], [Xb])
                    x["X"], x["Xb"] = Xt, Xb

                def g_level_pe(x, lv):
                    nm, nmb, Xt, Xb = x["nm"], x["nmb"], x["X"], x["Xb"]
                    x["pX"], x["pXb"] = single(2)
                    for hh in range(2):
                        S.add("pe", lambda e, pX=x["pX"], hh=hh, nm=nm, Xt=Xt: e.matmul(
                            pX[:, hh * 128:(hh + 1) * 128], lhsT=nm[:, 0, hh, :], rhs=Xt[:, hh, :, :].rearrange("p a k -> p (a k)"), start=True, stop=True),
                            [nmb, Xb], x["pXb"])
                    if lv < 6:
                        x["pNM"], x["pNMb"] = single(4)
                        for hh in range(2):
                            S.add("pe", lambda e, pN=x["pNM"], hh=hh, nm=nm: e.matmul(
                                pN[:, hh * 128:(hh + 1) * 128], lhsT=nm[:, 1, hh, :], rhs=nm[:, 0, hh, :], start=True, stop=True), [nmb], x["pNMb"])
                        if lv < 5:
                            for hh in range(2):
                                S.add("pe", lambda e, pN=x["pNM"], hh=hh, nm=nm: e.matmul(
                                    pN[:, (2 + hh) * 128:(3 + hh) * 128], lhsT=nm[:, 0, hh, :], rhs=nm[:, 1, hh, :], start=True, stop=True), [nmb], x["pNMb"])

                def g_level_evac(x, lv):
                    gi = x["gi"]
                    Xt, Xb = x["X"], x["Xb"]
                    Xn, Xnb = Xtile(gi * 2 + lv + 1)
                    S.add("dve", lambda e, Xn=Xn, pX=x["pX"], Xt=Xt: e.tensor_tensor(
                        out=Xn[:, :, :, :].rearrange("p h a k -> p h (a k)"), in0=v3(pX), in1=Xt[:, :, :, :].rearrange("p h a k -> p h (a k)"), op=ALU.add),
                        x["pXb"] + [Xb], [Xnb])
                    x["X"], x["Xb"] = Xn, Xnb
                    if lv < 6:
                        nn, nnb = NMt(gi * 2 + lv + 1)
                        w = 4 if lv < 5 else 2
                        S.add("act", lambda e, nn=nn, pN=x["pNM"], w=w: e.activation(
                            out=nn[:, :, :, :].rearrange("p a h t -> p (a h) t")[:, 0:w, :], in_=v3(pN[:, 0:w * 128], w), func=AF.Copy), x["pNMb"], [nnb])
                        x["nm"], x["nmb"] = nn, nnb

                def g_state(x, c=c, csl=csl, own=own, pYbank=(pYbank if own else None), pYbb=(pYbb if own else None)):
                    gi, hp, si = x["gi"], x["hp"], x["si"]
                    Xt, Xb, tmt, tmb = x["X"], x["Xb"], x["tm"], x["tmb"]
                    ARt, ARb = x["AR"], x["ARb"]
                    wz, wzb = Wz(gi)
                    S.add("pool", lambda e, wz=wz, Xt=Xt: e.tensor_copy(out=wz[:, 0::2, :], in_=Xt[:, :, 0, :]), [Xb], [wzb])
                    wzA = wz[:, 0:2, :].rearrange("p a k -> p (a k)")
                    wzB = wz[:, 1:3, :].rearrange("p a k -> p (a k)")
                    wp, wpb = Wp(gi)
                    S.add("pool", lambda e, wp=wp, Xt=Xt: e.tensor_copy(out=wp[:, :, :], in_=Xt[:, :, 0, :]), [Xb], [wpb])
                    pAT, pATb = single(1)
                    S.add("pe", lambda e, pAT=pAT, wp=wp, tmt=tmt: e.matmul(pAT[:, 0:128], lhsT=wp[:, :, :].rearrange("p a k -> p (a k)"), rhs=tmt[:, 1, :], start=True, stop=True),
                          [wpb, tmb], pATb)
                    atb, atbb = ATbd(gi)
                    for hh in range(2):
                        S.add("act", lambda e, atb=atb, pAT=pAT, hh=hh: e.activation(
                            out=atb[P_[hh], hh * 64:(hh + 1) * 64], in_=pAT[P_[hh], hh * 64:(hh + 1) * 64], func=AF.Copy), pATb, [atbb])
                    pG, pGb = single(1)
                    pG2, pG2b = single(1)
                    S.add("pe", lambda e, pG=pG, tmt=tmt: e.matmul(pG[:, 0:128], lhsT=tmt[:, 2, :], rhs=tmt[:, 3, :], start=True, stop=True),
                          [tmb], pGb)
                    for hh in range(2):
                        S.add("pe", lambda e, pG2=pG2, Xt=Xt, tmt=tmt, hh=hh: e.matmul(
                            pG2[:, hh * 64:(hh + 1) * 64], lhsT=tmt[:, 1, :], rhs=Xt[:, hh, 1, :], start=True, stop=True), [Xb, tmb], pG2b)
                    gs, gsb_ = Gsb(gi)
                    for hh in range(2):
                        S.add("act", lambda e, gs=gs, pG=pG, hh=hh: e.activation(
                            out=gs[P_[hh], :], in_=pG[P_[hh], hh * 64:(hh + 1) * 64], func=AF.Copy), pGb, [gsb_])
                        S.add("dve", lambda e, gs=gs, pG2=pG2, hh=hh: e.tensor_tensor(
                            out=gs[P_[hh], :], in0=pG2[P_[hh], hh * 64:(hh + 1) * 64], in1=gs[P_[hh], :], op=ALU.add), pG2b + [gsb_], [gsb_])
                    Htt, Hb_ = Ht(hp)
                    gct, gcb = gC(si)
                    if own:
                        hbz = [Hbfz(gi * 2 + hh) for hh in range(2)]
                        for hh in range(2):
                            S.add("pool", lambda e, hz=hbz[hh][0], Htt=Htt, hh=hh: e.tensor_copy(out=hz[P_[hh], :], in_=Htt[P_[hh], :]), [Hb_], [hbz[hh][1]])
                    hhl, hhlb = Hhl(gi)
                    S.add("pool", lambda e, hhl=hhl, Htt=Htt: e.tensor_copy(out=hhl[:, 0, :], in_=Htt[:]), [Hb_], [hhlb])
                    S.add("pool", lambda e, hhl=hhl, Htt=Htt: e.tensor_tensor(out=hhl[:, 1, :], in0=Htt[:], in1=hhl[:, 0, :], op=ALU.subtract), [Hb_, hhlb], [hhlb])
                    pZ, pZb = single(1)
                    S.add("pe", lambda e, pZ=pZ, atb=atb, hhl=hhl: e.matmul(pZ[:, 0:128], lhsT=atb[:], rhs=hhl[:, :, :].rearrange("p a v -> p (a v)"), start=True, stop=True),
                          [atbb, hhlb], pZb)
                    s1, s1b = s1t(gi)
                    S.add("pool", lambda e, s1=s1, Htt=Htt, gs=gs: e.tensor_tensor(out=s1[:], in0=Htt[:], in1=gs[:], op=ALU.add), [Hb_, gsb_], [s1b])
                    S.add("pool", lambda e, s1=s1, gct=gct: e.tensor_scalar(out=s1[:], in0=s1[:], scalar1=gct[:, c:c + 1], scalar2=1.0, op0=ALU.mult, op1=ALU.mult),
                          [s1b, gcb], [s1b])
                    S.add("dve", lambda e, pZ=pZ, gct=gct, s1=s1: e.scalar_tensor_tensor(
                        out=s1[:], in0=pZ[:, 0:64], scalar=gct[:, c:c + 1], in1=s1[:], op0=ALU.mult, op1=ALU.add), pZb + [gcb, s1b], [s1b])
                    S.add("dve", lambda e, Htt=Htt, pZ=pZ, gct=gct, s1=s1: e.scalar_tensor_tensor(
                        out=Htt[:], in0=pZ[:, 64:128], scalar=gct[:, c:c + 1], in1=s1[:], op0=ALU.mult, op1=ALU.add), pZb + [gcb, s1b], [Hb_])
                    if own:
                        rbk, rbkb = x["rbk"], x["rbkb"]
                        qt_, qtb = QT(gi)
                        for hh, wzX in enumerate((wzA, wzB)):
                            pQ, pQb = single(1)
                            S.add("pe", lambda e, pQ=pQ, wzX=wzX, rbk=rbk, hh=hh: e.matmul(pQ[:, 0:128], lhsT=wzX, rhs=rbk[:, 0, hh, :], start=True, stop=True),
                                  [wzb, rbkb], pQb)
                            S.add("dve", lambda e, qt_=qt_, pQ=pQ, ARt=ARt, hh=hh: e.tensor_tensor(
                                out=qt_[P_[hh], :], in0=pQ[P_[hh], 0:128], in1=ARt[P_[hh], c, 1, :], op=ALU.add), pQb + [ARb], [qtb])
                        for hh in range(2):
                            pY, pYb = pYbank, pYbb
                            ysl = slice((hp * 2 + hh) * 64, (hp * 2 + hh + 1) * 64)
                            S.add("pe", lambda e, pY=pY, ysl=ysl, hh=hh, rbk=rbk, Xt=Xt: e.matmul(
                                pY[:, ysl], lhsT=rbk[:, 0, hh, :], rhs=Xt[:, hh, 1, :], start=True, stop=False), [rbkb, Xb], pYb)
                            S.add("pe", lambda e, pY=pY, ysl=ysl, hh=hh, rbk=rbk, tmt=tmt: e.matmul(
                                pY[:, ysl], lhsT=rbk[:, 1, hh, :], rhs=tmt[:, 3, hh * 64:(hh + 1) * 64], start=False, stop=False), [rbkb, tmb], pYb)
                            S.add("pe", lambda e, pY=pY, ysl=ysl, qt_=qt_, hz=hbz[hh][0]: e.matmul(
                                pY[:, ysl], lhsT=qt_[:, :], rhs=hz[:, :], start=False, stop=True), [qtb, hbz[hh][1]], pYb)

                for pr in pairs:
                    ctxs = [mkctx(hp) for hp in pr]
                    for x in ctxs:
                        g_transposes(x)
                    for x in ctxs:
                        g_sprod_pe(x)
                    for x in ctxs:
                        g_sprod_evac(x)
                    fill()
                    if own:
                        for x in ctxs:
                            g_r_pe(x)
                        for x in ctxs:
                            g_r_evac(x)
                    for x in ctxs:
                        g_pv_pe(x)
                    for x in ctxs:
                        g_x0(x)
                    fill()
                    for lv in range(7):
                        for x in ctxs:
                            g_level_pe(x, lv)
                        for x in ctxs:
                            g_level_evac(x, lv)
                        fill()
                    for x in ctxs:
                        g_state(x)
            if not own:
                g2 = gen_second(sbi)
                for c in range(CPS):
                    emit_chunk_pairs(c, [PAIRS[0]], mkfill(g2, 2))
                drain(g2)
                g1 = gen_first(sbi + 1) if sbi + 1 < nsb else iter(())
                for c in range(CPS):
                    emit_chunk_pairs(c, [PAIRS[1]], mkfill(g1, 2))
                drain(g1)
                continue
            for c in range(CPS):
                gch = sbi * CPS + c
                csl = slice(c * C, (c + 1) * C)
                pYbank, pYbb = psA[0], [psA_b[0]]
                emit_chunk_pairs(c, PAIRS, lambda: None)
                lc = (sbi - own0) * CPS + c
                g_t, g_b = gst()
                pY, pYb = pYbank, pYbb
                yq, yqb = ysqt(0)
                S.add("dve", lambda e, g_t=g_t, pY=pY: e.tensor_reduce(out=g_t[:, 0, :], in_=v3(pY[:, 0:512], 8), axis=AX.X, op=ALU.add), pYb, [g_b])
                S.add("act", lambda e, yq=yq, pY=pY: e.activation(out=yq[:], in_=pY[:, 0:512], func=AF.Square), pYb, [yqb])
                S.add("dve", lambda e, g_t=g_t, yq=yq: e.tensor_reduce(out=g_t[:, 1, :], in_=v3(yq[:, :], 8), axis=AX.X, op=ALU.add), [yqb], [g_b])
                S.add("dve", lambda e, g_t=g_t: e.tensor_scalar(out=g_t[:, 2, :], in0=g_t[:, 0, :], scalar1=1.0 / 64, scalar2=None, op0=ALU.mult), [g_b], [g_b])
                S.add("dve", lambda e, g_t=g_t: e.tensor_tensor(out=g_t[:, 3, :], in0=g_t[:, 2, :], in1=g_t[:, 2, :], op=ALU.mult), [g_b], [g_b])
                S.add("dve", lambda e, g_t=g_t: e.scalar_tensor_tensor(out=g_t[:, 4, :], in0=g_t[:, 1, :], scalar=1.0 / 64, in1=g_t[:, 3, :],
                                                                       op0=ALU.mult, op1=ALU.subtract), [g_b], [g_b])
                S.add("act", lambda e, g_t=g_t: e.activation(out=g_t[:, 5, :], in_=g_t[:, 4, :], func=AF.Ln, bias=gneps_col), [g_b, KB], [g_b])
                S.add("act", lambda e, g_t=g_t: e.activation(out=g_t[:, 5, :], in_=g_t[:, 5, :], func=AF.Exp, scale=-0.5), [g_b], [g_b])
                ynt, ynb = yn()
                S.add("dve", lambda e, yq=yq, pY=pY, g_t=g_t: e.tensor_tensor(
                    out=v3(yq[:, :], 8), in0=v3(pY[:, 0:512], 8), in1=bcl(g_t[:, 2, :], 64), op=ALU.subtract), pYb + [g_b, yqb], [yqb])
                S.add("pool", lambda e, ynt=ynt, yq=yq, g_t=g_t: e.tensor_tensor(
                    out=v3(ynt[:, :], 8), in0=v3(yq[:, :], 8), in1=bcl(g_t[:, 5, :], 64), op=ALU.mult), [yqb, g_b], [ynb])
                pt_ = psT[1][:, 0:512]
                ptb = psT_b[1]
                for hp in range(4):
                    S.add("pe", lambda e, pt_=pt_, hp=hp, ynt=ynt: e.transpose(out=pt_[:, hp * 128:(hp + 1) * 128], in_=ynt[:, hp * 128:(hp + 1) * 128], identity=ident_b),
                          [ynb, CB], ptb)
                for hp in range(4):
                    si = sbi * 4 + hp
                    t1, t1b = t1t(hp)
                    S.add("dve", lambda e, t1=t1, pt_=pt_, hp=hp: e.tensor_scalar(out=t1[:], in0=pt_[:, hp * 128:(hp + 1) * 128], scalar1=pc("gnw", hp), scalar2=pc("gnb", hp),
                                                                            op0=ALU.mult, op1=ALU.add), ptb + [PB], [t1b])
                    S.add("pool", lambda e, t1=t1, si=si, csl=csl: e.tensor_tensor(out=t1[:], in0=t1[:], in1=bonus(si)[0][:, csl], op=ALU.add), [t1b, bonus(si)[1]], [t1b])
                    S.add("pool", lambda e, t1=t1, si=si, csl=csl: e.tensor_tensor(out=zr(si)[0][:, csl], in0=t1[:], in1=sgr(si)[0][:, csl], op=ALU.mult),
                          [t1b, sgr(si)[1]], [zr(si)[1]])
                if lc == NOC - 1:
                    dump("zr0", zr(sbi * 4)[0][:], [128, SBT], [zr(sbi * 4)[1]], BF16)
                if upto < 4:
                    continue
                am_i = 0 if lc == 0 else 1
                pObank, pObb = psA[1], [psA_b[1]]
                vprev, vprevb = Vpad(lc)
                vcur, vcurb = Vpad(lc + 1)
                for qp in range(4):
                    si = sbi * 4 + qp
                    g = qp // 2
                    pSs = [single(2), single(2)]
                    for hh in range(2):
                        pS, pSb_ = pSs[hh]
                        qz, qzb = qTz(si * 2 + hh)
                        for kk_ in range(2):
                            ks = (lc + kk_) % NKS
                            S.add("pe", lambda e, pS=pS, qz=qz, g=g, ks=ks, kk_=kk_, csl=csl: e.matmul(
                                pS[:, kk_ * 128:(kk_ + 1) * 128], lhsT=qz[:, csl], rhs=KTatt.t[g][:, ks * C:(ks + 1) * C], start=True, stop=True),
                                [qzb, KTatt.b[g]], pSb_)
                    pts = psT[0]
                    PTs = []
                    hs = []
                    for hh in range(2):
                        hd = qp * 2 + hh
                        hs.append(dict(hd=hd, pS=pSs[hh][0], pSb=pSs[hh][1], sm=smt(hd), a=ast(hd), p3=p32(hd), pn=pnt(hd), pt=ptt(hd),
                                       ptsl=pts[:, hh * 256:(hh + 1) * 256]))
                    for h in hs:
                        S.add("dve", lambda e, sm=h["sm"][0], pS=h["pS"], am_i=am_i: e.scalar_tensor_tensor(
                            out=sm[:], in0=pS[:, 0:256], scalar=0.125, in1=amask.t[0][:, am_i, :], op0=ALU.mult, op1=ALU.add), h["pSb"] + [amask.b[0]], [h["sm"][1]])
                    for h in hs:
                        S.add("dve", lambda e, a_t=h["a"][0], sm=h["sm"][0]: e.tensor_reduce(out=a_t[:, 0:1], in_=sm[:], axis=AX.X, op=ALU.max), [h["sm"][1]], [h["a"][1]])
                    for h in hs:
                        S.add("dve", lambda e, a_t=h["a"][0], hd=h["hd"]: e.tensor_scalar(out=a_t[:, 1:2], in0=a_t[:, 0:1], scalar1=pc("sink", hd), scalar2=-1.0, op0=ALU.max, op1=ALU.mult),
                              [h["a"][1], PB], [h["a"][1]])
                    for h in hs:
                        S.add("act", lambda e, pp3=h["p3"][0], sm=h["sm"][0], a_t=h["a"][0]: e.activation(out=pp3[:], in_=sm[:], func=AF.Exp, bias=a_t[:, 1:2], accum_out=a_t[:, 2:3]),
                              [h["sm"][1], h["a"][1]], [h["p3"][1], h["a"][1]])
                    for h in hs:
                        S.add("act", lambda e, a_t=h["a"][0], hd=h["hd"]: e.activation(out=a_t[:, 3:4], in_=pc("sink", hd), func=AF.Exp, bias=a_t[:, 1:2]), [h["a"][1], PB], [h["a"][1]])
                    for h in hs:
                        S.add("dve", lambda e, a_t=h["a"][0]: e.tensor_tensor(out=a_t[:, 4:5], in0=a_t[:, 2:3], in1=a_t[:, 3:4], op=ALU.add), [h["a"][1]], [h["a"][1]])
                    for h in hs:
                        S.add("dve", lambda e, a_t=h["a"][0]: e.reciprocal(out=a_t[:, 5:6], in_=a_t[:, 4:5]), [h["a"][1]], [h["a"][1]])
                    for h in hs:
                        S.add("dve", lambda e, pn=h["pn"][0], pp3=h["p3"][0], a_t=h["a"][0]: e.tensor_scalar(out=pn[:], in0=pp3[:], scalar1=a_t[:, 5:6], scalar2=None, op0=ALU.mult),
                              [h["p3"][1], h["a"][1]], [h["pn"][1]])
                    for h in hs:
                        for kk_ in range(2):
                            S.add("pe", lambda e, ptsl=h["ptsl"], kk_=kk_, pn=h["pn"][0]: e.transpose(out=ptsl[:, kk_ * 128:(kk_ + 1) * 128], in_=pn[:, kk_ * 128:(kk_ + 1) * 128], identity=ident_b),
                                  [h["pn"][1], CB], psT_b[0])
                    for h in hs:
                        S.add("act", lambda e, pt2=h["pt"][0], ptsl=h["ptsl"]: e.activation(out=pt2[:], in_=v3(ptsl), func=AF.Copy), psT_b[0], [h["pt"][1]])
                        PTs.append(h["pt"])
                    pO, pOb = pObank[:, qp * 128:(qp + 1) * 128], pObb
                    n_ = 0
                    for hh in range(2):
                        pt2, pt2b = PTs[hh]
                        for kk_, (vp, vpb) in enumerate([(vprev, vprevb), (vcur, vcurb)]):
                            S.add("pe", lambda e, pO=pO, vp=vp, g=g, hh=hh, pt2=pt2, kk_=kk_, n_=n_: e.matmul(
                                pO[:, 0:128], lhsT=vp[:, g, hh * 64:hh * 64 + 128], rhs=pt2[:, kk_, :], start=(n_ == 0), stop=(n_ == 3)),
                                [vpb, pt2b], pOb)
                            n_ += 1
                    S.add("dve", lambda e, si=si, pO=pO, csl=csl: e.tensor_tensor(out=zatt(si)[0][:, csl], in0=pO[:, 0:128], in1=sga(si)[0][:, csl], op=ALU.mult),
                          pOb + [sga(si)[1]], [zatt(si)[1]])
                if lc == NOC - 1:
                    dump("za0", zatt(sbi * 4)[0][:], [128, SBT], [zatt(sbi * 4)[1]], BF16)
            if not own or upto < 5:
                continue
            mTt, mTb = mT()
            for j in range(8):
                wt, wb = ws_load([(lambda t: t[:, :, :], w_br[:, :, j * 128:(j + 1) * 128].rearrange("b (h p) c -> p (b h) c", p=128))])
                pBr, pBrb = fullbank()
                pBa, pBab = fullbank()
                for hp in range(4):
                    si = sbi * 4 + hp
                    S.add("pe", lambda e, pBr=pBr, wt=wt, hp=hp, si=si: e.matmul(pBr[:, 0:SBT], lhsT=wt[:, hp, :], rhs=zr(si)[0][:], start=(hp == 0), stop=(hp == 3)),
                          wb + [zr(si)[1]], pBrb)
                for hp in range(4):
                    si = sbi * 4 + hp
                    S.add("pe", lambda e, pBa=pBa, wt=wt, hp=hp, si=si: e.matmul(pBa[:, 0:SBT], lhsT=wt[:, 4 + hp, :], rhs=zatt(si)[0][:], start=(hp == 0), stop=(hp == 3)),
                          wb + [zatt(si)[1]], pBab)
                halves = []
                for br in range(2):
                    wt2, wb2 = ws_load([(lambda t: t[:, :, :], wcols(O_GT + br * 1024 + j * 128))])
                    pGt, pGtb = bankx()
                    for k in range(8):
                        S.add("pe", lambda e, pGt=pGt, k=k, wt2=wt2, hTt=hTt: e.matmul(
                            pGt[:, 0:SBT], lhsT=wt2[:, k, :], rhs=hTt[:, k, :], start=(k == 0), stop=(k == 7)), wb2 + [hTb], pGtb)
                    sg_, sgb_ = sgt(br)
                    S.add("act", lambda e, sg_=sg_, pGt=pGt: e.activation(out=sg_[:], in_=pGt[:, 0:SBT], func=AF.Sigmoid), pGtb, [sgb_])
                    halves.append((sg_, sgb_))
                m1, m1b = m12(0)
                m2, m2b = m12(1)
                S.add("dve", lambda e, m1=m1, pBr=pBr, sg_=halves[0][0]: e.tensor_tensor(out=m1[:], in0=pBr[:, 0:SBT], in1=sg_[:], op=ALU.mult), pBrb + [halves[0][1]], [m1b])
                S.add("dve", lambda e, m2=m2, pBa=pBa, sg_=halves[1][0]: e.tensor_tensor(out=m2[:], in0=pBa[:, 0:SBT], in1=sg_[:], op=ALU.mult), pBab + [halves[1][1]], [m2b])
                S.add("pool", lambda e, mTt=mTt, j=j, m1=m1, m2=m2: e.tensor_tensor(out=mTt[:, j, :], in0=m1[:], in1=m2[:], op=ALU.add), [m1b, m2b], [mTb])
            for c in range(CPS):
                gch = sbi * CPS + c
                lc = (sbi - own0) * CPS + c
                xr, xrb = xt(gch)
                dma("sp", xr[:], xw[gch * C:(gch + 1) * C, :], [], [xrb])
                for n in range(2):
                    pa, pab = fullbank()
                    for j in range(8):
                        S.add("pe", lambda e, pa=pa, j=j, n=n, mTt=mTt, c=c: e.matmul(
                            pa[:, 0:512], lhsT=mTt[:, j, c * C:(c + 1) * C], rhs=Wout.t[0][:, j, n * 512:(n + 1) * 512], start=(j == 0), stop=(j == 7)),
                            [mTb, Wout_b[j]], pab)
                    S.add("dve", lambda e, xr=xr, pa=pa, n=n: e.tensor_tensor(out=xr[:, n * 512:(n + 1) * 512], in0=pa[:, 0:512], in1=xr[:, n * 512:(n + 1) * 512], op=ALU.add),
                          pab + [xrb], [xrb])
                ft_, fb_ = fst(gch)
                hbt, hbb = hb(gch)
                rms_rstd(xr[:], [xrb], ft_, fb_, hbt[:], hbb)
                S.add("dve", lambda e, xr=xr, ft_=ft_: e.scalar_tensor_tensor(
                    out=xr[:], in0=xr[:], scalar=ft_[:, 2:3], in1=gfin.t[0][:], op0=ALU.mult, op1=ALU.mult), [xrb, fb_, gfin.b[0]], [xrb])
                dma("sp", out_d[lc * C:(lc + 1) * C, :], xr[:], [xrb], [], is_out=True)
            if sbi + 1 < nsb:
                drain(gen_first(sbi + 1))

        for hp in range(4):
            dump(f"H{hp}", Ht(hp)[0][:], [128, 64], [Ht(hp)[1]])

        semnames = list(Sched.ENG) + [("dma", j) for j in range(Sched.NDMA)]
        sems = {}
        for sk in semnames:
            nm = sk if isinstance(sk, str) else f"dma{sk[1]}"
            sems[sk] = es.enter_context(nc.semaphore("s_" + nm))
        nc._sbuf_left = nc.sbuf_bytes_remaining
        block = es.enter_context(nc.Block())
        S.emit(nc, block, sems)
    nc._dbg_dumps = dump_d
    nc._sched_counts = dict(S.cnt)
    nc._sched_total = S.total
    return nc


def host_consts():
    s = np.arange(128)[:, None]
    t = np.arange(128)[None, :]
    cst = np.zeros((128, 9, 128), np.float32)
    cst[:, 0] = (s == t)
    cst[:, 1] = (s < t)
    cst[:, 2] = (s < t)
    cst[:, 3] = (s > t)
    cst[:, 4] = (s > t)
    cst[:, 5] = (s <= t)
    cst[:, 6] = (s <= t)
    cst[:, 7] = ((s // 64) == (t // 64))
    cst[:, 8] = 1.0
    return cst


def attn_masks(first):
    qi = np.arange(128)[:, None]
    kj = np.arange(256)[None, :]
    dist = qi + 128 - kj
    band = (dist >= 0) & (dist < 128)
    rest = np.where(band, 0.0, -1e30).astype(np.float32)
    fm = np.where(band & (kj >= 128), 0.0, -1e30).astype(np.float32)
    am = np.stack([fm if first else rest, rest], axis=1)
    return np.ascontiguousarray(am)


def pack_params(p):
    pp = np.zeros((128, NPP_IN), np.float32)

    def put(name, vec, n):
        v = np.asarray(vec, np.float32).reshape(n, 128)
        pp[:, PPI[name]:PPI[name] + n] = v.T
    put("mu", p["mu_shift"][0], 13)
    put("w0", p["w0"][0], 4)
    put("a0", p["a0"][0], 4)
    put("kk", p["k_k"][0], 4)
    put("ka", p["k_a"][0], 4)
    put("rk", p["r_k"][0], 4)
    put("gnw", p["gn_w"][0], 4)
    put("gnb", p["gn_b"][0], 4)
    bq = np.asarray(p["b_qkv"][0], np.float32)
    put("bq", bq[0:512], 4)
    bk = bq[512:640]
    pp[:, PPI["bk"] + 0] = np.concatenate([bk[0:64], bk[0:64]])
    pp[:, PPI["bk"] + 1] = np.concatenate([bk[64:128], bk[64:128]])
    sk = np.asarray(p["sinks"][0], np.float32)
    pp[:, PPI["sink"]:PPI["sink"] + 8] = np.broadcast_to(sk[None, :], (128, 8))
    wdi = np.zeros((128, 2, 512), np.float32)
    wdi[0:64, 0] = np.asarray(p["w_decay_up"][0], np.float32)
    wdi[64:128, 1] = np.asarray(p["w_iclr_up"][0], np.float32)
    common = {
        "w_in": np.ascontiguousarray(np.asarray(p["w_in"][0], np.float32)),
        "w_br": np.ascontiguousarray(np.stack([np.asarray(p["w_branch_rwkv"][0], np.float32),
                                               np.asarray(p["w_branch_att"][0], np.float32)])),
        "w_out": np.ascontiguousarray(np.asarray(p["w_out"][0], np.float32)),
        "wdi": np.ascontiguousarray(wdi),
        "pp": pp,
        "gpre_b": np.ascontiguousarray(np.broadcast_to(np.asarray(p["g_pre"][0], np.float32)[None], (128, D))),
        "gfin_b": np.ascontiguousarray(np.broadcast_to(np.asarray(p["g_final"], np.float32)[None], (128, D))),
        "bv_b": np.ascontiguousarray(np.broadcast_to(bq[640:768][None], (128, 128))),
        "cst": host_consts(),
    }
    return common


def kernel(**inputs):
    x = np.asarray(inputs["x"], np.float32)
    common = pack_params(inputs)
    nc = build()
    in_maps = []
    for c in range(NCORES):
        b, q = c // 4, c % 4
        end = (q + 1) * OWN_TOK
        xw = np.zeros((SEQ, D), np.float32)
        xw[SEQ - end:] = x[b, :end]
        m = dict(common)
        m["xw"] = xw
        m["amask"] = attn_masks(q == 0)
        in_maps.append(m)
    res = run_bass_kernel_spmd(nc, in_maps, core_ids=list(range(NCORES)))
    out = np.zeros((2, SEQ, D), np.float32)
    for c in range(NCORES):
        b, q = c // 4, c % 4
        out[b, q * OWN_TOK:(q + 1) * OWN_TOK] = res.results[c]["out"]
    return out
```

```python
import numpy as np
import concourse.bass as bass
import concourse.mybir as mybir
from concourse.bass_utils import run_bass_kernel_spmd

F32 = mybir.dt.float32
BF16 = mybir.dt.bfloat16
AF = mybir.ActivationFunctionType
ALU = mybir.AluOpType
AX = mybir.AxisListType

D = 1024
NCORES = 8
SEQ = 8192
OWN_TOK = 2048
C = 128
SBT = 256
CPS = SBT // C
RMS_EPS = 1e-6
GN_EPS = 64e-5
IN_COLS = 5504
O_SH = 0
O_GR = 1664
O_Q = 2176
O_K = 2688
O_V = 2816
O_GA = 2944
O_GT = 3456

PPI = {}
_n = 0
for _name, _cnt in [("mu", 13), ("w0", 4), ("a0", 4), ("kk", 4), ("ka", 4), ("rk", 4),
                    ("gnw", 4), ("gnb", 4), ("bq", 4), ("bk", 2), ("sink", 8)]:
    PPI[_name] = _n
    _n += _cnt
NPP_IN = _n
for _name, _cnt in [("omu", 13), ("nw0", 4), ("omka", 4), ("na0", 4)]:
    PPI[_name] = _n
    _n += _cnt
NPP = _n


class Buf:
    __slots__ = ("name", "w", "r", "excl")

    def __init__(self, name, excl=False):
        self.name = name
        self.w = None
        self.r = []
        self.excl = excl


class Sched:
    ENG = ("pe", "act", "dve", "pool", "sp")
    NDMA = 24

    def __init__(self, same_sync=True):
        self.ops = {e: [] for e in self.ENG}
        self.cnt = {e: 0 for e in self.ENG}
        self.waited = {e: {} for e in self.ENG}
        self.same_sync = same_sync
        self.dma_val = [0] * self.NDMA
        self.dma_rr = 0
        self.dma_rr2 = 0
        self.out_tokens = []

    def add(self, eng, fn, reads=(), writes=(), dma=False, is_out=False):
        self.total = getattr(self, "total", 0) + 1
        if not dma and self.total > getattr(self, "cut", 10 ** 9):
            return None
        deps = {}

        def need(tk, hard):
            d = deps.get(tk[0])
            if d is None:
                deps[tk[0]] = [tk[1], tk[2], hard]
            else:
                d[0] = max(d[0], tk[1])
                d[2] = d[2] or hard
        for b in reads:
            if b.w is not None:
                need(b.w, True)
            if b.excl:
                for tk in b.r:
                    need(tk, False)
        for b in writes:
            if b.w is not None:
                need(b.w, True)
            for tk in b.r:
                need(tk, False)
        waits = []
        for semkey, (val, src, hard) in deps.items():
            if src == eng and not isinstance(semkey, tuple):
                if eng in ("pe", "sp"):
                    continue
                if not hard or not self.same_sync:
                    continue
            if self.waited[eng].get(semkey, 0) >= val:
                continue
            self.waited[eng][semkey] = val
            waits.append((semkey, val))
        if dma:
            half = self.NDMA // 2
            if eng == "sp":
                j = self.dma_rr
                self.dma_rr = (self.dma_rr + 1) % half
            else:
                j = half + self.dma_rr2
                self.dma_rr2 = (self.dma_rr2 + 1) % half
            semkey = ("dma", j)
            if self.dma_val[j] > 0 and self.waited[eng].get(semkey, 0) < self.dma_val[j]:
                self.waited[eng][semkey] = self.dma_val[j]
                waits.append((semkey, self.dma_val[j]))
            self.dma_val[j] += 16
            tok = (semkey, self.dma_val[j], eng)
            inc = 16
        else:
            self.cnt[eng] += 1
            tok = (eng, self.cnt[eng], eng)
            inc = 1
        for b in reads:
            b.r.append(tok)
        for b in writes:
            b.w = tok
            b.r = []
        if is_out:
            self.out_tokens.append(tok)
        self.ops[eng].append((waits, fn, tok[0], inc))
        return tok

    def emit(self, nc, block, sems):
        engmap = {"pe": block.tensor, "act": block.scalar, "dve": block.vector,
                  "pool": block.gpsimd, "sp": block.sync}
        for e in self.ENG:
            ops = self.ops[e]
            final = list(self.out_tokens) if e == "sp" else ()

            def body(eng, ops=ops, final=final):
                for waits, fn, semkey, inc in ops:
                    for sk, val in waits:
                        eng.wait_ge(sems[sk], val)
                    fn(eng).then_inc(sems[semkey], inc)
                for tk in final:
                    eng.wait_ge(sems[tk[0]], tk[1])
                if final != ():
                    for j in range(self.NDMA):
                        if self.dma_val[j] > 0:
                            eng.wait_ge(sems[("dma", j)], self.dma_val[j])
            engmap[e](body)


def build(nsb=SEQ // SBT, nown=OWN_TOK // SBT, upto=99, dumps=(), same_sync=True, cut=None):
    from contextlib import ExitStack
    nc = bass.Bass("TRN2", target_bir_lowering=False)
    WT = nsb * SBT
    OT = nown * SBT
    NOC = nown * CPS
    S = Sched(same_sync=same_sync)
    if cut is not None:
        S.cut = cut

    def din(name, shape, dt=F32):
        return nc.dram_tensor(name, list(shape), dt, kind="ExternalInput").ap()

    xw = din("xw", [WT, D])
    w_in = din("w_in", [D, IN_COLS])
    w_br = din("w_br", [2, 512, D])
    w_out = din("w_out", [D, D])
    wdi = din("wdi", [128, 2, 512])
    pp_in = din("pp", [128, NPP_IN])
    gpre_d = din("gpre_b", [128, D])
    gfin_d = din("gfin_b", [128, D])
    bv_d = din("bv_b", [128, 128])
    cst_d = din("cst", [128, 9, 128])
    am_d = din("amask", [128, 2, 256])
    out_d = nc.dram_tensor("out", [OT, D], F32, kind="ExternalOutput").ap()
    dump_d = {}

    es = ExitStack()
    with es:
        def sb(name, shape, dt=F32):
            return es.enter_context(nc.sbuf_tensor(name, list(shape), dt))

        def ps(name, shape, dt=F32):
            return es.enter_context(nc.psum_tensor(name, list(shape), dt))

        class T:
            def __init__(self, name, shape, dt=F32, n=1):
                self.t = [sb(f"{name}{i}", shape, dt) for i in range(n)]
                self.b = [Buf(f"{name}{i}") for i in range(n)]
                self.n = n

            def __call__(self, i=0):
                return self.t[i % self.n], self.b[i % self.n]

        def dma(eng, out, in_, reads, writes, is_out=False):
            return S.add(eng, lambda e: e.dma_start(out=out, in_=in_), reads, writes, dma=True, is_out=is_out)

        def dump(name, ap, shape, reads, dt=F32):
            if name not in dumps:
                return
            dd = nc.dram_tensor("dbg_" + name, list(shape), dt, kind="ExternalOutput").ap()
            dump_d[name] = dd
            dma("sp", dd, ap, reads, [], is_out=True)

        xt = T("xt", [128, D], F32, 2)
        cst_b = T("cst_b", [128, 8, 128], BF16)
        cst2 = T("cst2", [128, 1, 128])
        amask = T("amask", [128, 2, 256])
        PP = T("PP", [128, NPP])
        gpre = T("gpre", [128, D])
        gfin = T("gfin", [128, D])
        bvb = T("bvb", [128, 128])
        Wdb = T("Wdb", [128, 3, 512], BF16)
        Wsh = T("Wsh", [128, 8, 1664], BF16)
        Wout = T("Wout", [128, 8, D], BF16)

        stg = xt.t[1][:, :].rearrange("p (a b) -> p a b", a=8)
        dma("sp", stg, cst_d[:, 0:8, :], [], [xt.b[1]])
        dma("sp", cst2.t[0][:], cst_d[:, 8:9, :], [], [cst2.b[0]])
        dma("sp", PP.t[0][:, 0:NPP_IN], pp_in, [], [PP.b[0]])
        dma("sp", gpre.t[0][:], gpre_d, [], [gpre.b[0]])
        stg_w = xt.t[0][:, :].rearrange("p (a b) -> p a b", a=2)
        dma("sp", stg_w, wdi, [], [xt.b[0]])
        S.add("act", lambda e: e.activation(out=Wdb.t[0][:, 0, :], in_=stg_w[:, 0, :], func=AF.Copy), [xt.b[0]], [Wdb.b[0]])
        S.add("act", lambda e: e.activation(out=Wdb.t[0][:, 2, :], in_=stg_w[:, 1, :], func=AF.Copy), [xt.b[0]], [Wdb.b[0]])
        S.add("dve", lambda e: e.tensor_tensor(out=Wdb.t[0][:, 1, :], in0=stg_w[:, 0, :], in1=Wdb.t[0][:, 0, :], op=ALU.subtract), [xt.b[0], Wdb.b[0]], [Wdb.b[0]])
        Wsh_b = [Buf(f"Wsh_k{k}") for k in range(8)]
        for k in range(8):
            S.add("pool", lambda e, k=k: e.dma_start(out=Wsh.t[0][:, k, :], in_=w_in[k * 128:(k + 1) * 128, O_SH:O_SH + 1664]),
                  [], [Wsh_b[k]], dma=True)
        dma("sp", amask.t[0][:], am_d, [], [amask.b[0]])
        dma("sp", gfin.t[0][:], gfin_d, [], [gfin.b[0]])
        dma("sp", bvb.t[0][:], bv_d, [], [bvb.b[0]])
        S.add("dve", lambda e: e.tensor_copy(out=cst_b.t[0][:], in_=stg), [xt.b[1]], [cst_b.b[0]])
        ident_b = cst_b.t[0][:, 0, :]
        mask4 = cst_b.t[0][:, 1:5, :]
        mle2 = cst_b.t[0][:, 5:7, :]
        bones_b = cst_b.t[0][:, 7, :]
        ones_f = cst2.t[0][:, 0, :]
        CB = cst_b.b[0]
        CF = cst2.b[0]
        ppt = PP.t[0]
        PB = PP.b[0]

        def pc(name, i=0):
            j = PPI[name] + i
            return ppt[:, j:j + 1]

        S.add("dve", lambda e: e.tensor_scalar(out=ppt[:, PPI["omu"]:PPI["omu"] + 13], in0=ppt[:, PPI["mu"]:PPI["mu"] + 13],
                                               scalar1=-1.0, scalar2=1.0, op0=ALU.mult, op1=ALU.add), [PB], [PB])
        S.add("dve", lambda e: e.tensor_scalar(out=ppt[:, PPI["nw0"]:PPI["nw0"] + 4], in0=ppt[:, PPI["w0"]:PPI["w0"] + 4],
                                               scalar1=-1.0, scalar2=None, op0=ALU.mult), [PB], [PB])
        S.add("dve", lambda e: e.tensor_scalar(out=ppt[:, PPI["omka"]:PPI["omka"] + 4], in0=ppt[:, PPI["ka"]:PPI["ka"] + 4],
                                               scalar1=-1.0, scalar2=1.0, op0=ALU.mult, op1=ALU.add), [PB], [PB])
        S.add("dve", lambda e: e.tensor_scalar(out=ppt[:, PPI["na0"]:PPI["na0"] + 4], in0=ppt[:, PPI["a0"]:PPI["a0"] + 4],
                                               scalar1=-1.0, scalar2=None, op0=ALU.mult), [PB], [PB])

        psA = [ps(f"psA{i}", [128, 512]) for i in range(2)]
        psA_b = [Buf(f"psA{i}", True) for i in range(2)]
        psT = [ps(f"psT{i}", [128, 1024], BF16) for i in range(2)]
        psT_b = [[Buf(f"psT{i}_{h}", True) for h in range(2)] for i in range(2)]
        psLU = [[ps(f"psL{i}", [128, 512]), ps(f"psU{i}", [128, 512])] for i in range(2)]
        psLU_b = [[[Buf(f"psLU{i}_{lu}_{s}", True) for s in range(4)] for lu in range(2)] for i in range(2)]
        arr = [0]
        prr = [0]
        srr = [0]

        def fullbank():
            i = arr[0]
            arr[0] = (i + 1) % 2
            return psA[i], [psA_b[i]]

        def pair(ns):
            r = prr[0]
            if (r % 4) + ns > 4:
                r = (r // 4 + 1) * 4
            r %= 8
            p, s = r // 4, r % 4
            prr[0] = (r + ns) % 8
            sl = slice(s * 128, (s + ns) * 128)
            return (psLU[p][0][:, sl], psLU_b[p][0][s:s + ns], psLU[p][1][:, sl], psLU_b[p][1][s:s + ns])

        brr = [0]

        def bankx():
            i = brr[0]
            brr[0] = (i + 1) % 4
            p, lu = i // 2, i % 2
            return psLU[p][lu], list(psLU_b[p][lu])

        def single(ns):
            bk, bb = bankx()
            return bk[:, 0:ns * 128], bb

        hb = T("hb", [128, D], BF16, 1)
        hT = T("hT", [128, 8, SBT], BF16, 2)
        st0 = T("st0", [128, 4], F32, 2)
        shwa = T("shwa", [128, SBT])
        shtmp = T("shtmp", [128, SBT], F32, 1)
        shr = T("shr", [128, SBT], F32, 2)
        shk = T("shk", [128, SBT], F32, 2)
        shv = T("shv", [128, SBT], F32, 2)
        tw = T("tw", [128, SBT])
        tw_hi = T("tw_hi", [128, SBT], BF16)
        tw_lo = T("tw_lo", [128, SBT], BF16)
        t_k2b = T("t_k2b", [128, SBT], BF16)
        t_rkb = T("t_rkb", [128, SBT], BF16)
        Hhl = T("Hhl", [128, 2, 64], BF16, 4)
        t_e1 = T("t_e1", [128, SBT])
        t_ew = T("t_ew", [128, SBT])
        t_a = T("t_a", [128, SBT])
        t_cs = T("t_cs", [128, SBT])
        t_csp = T("t_csp", [128, SBT])
        t_en = T("t_en", [128, SBT])
        t_ep = T("t_ep", [128, SBT])
        t_k2 = T("t_k2", [128, SBT])
        t_kkn = T("t_kkn", [128, SBT])
        t_ab = T("t_ab", [128, SBT])
        t_f = T("t_f", [128, SBT])
        gC = T("gC", [128, CPS], F32, 8)
        AR = T("AR", [128, CPS, 2, C], BF16, 4)
        BT = T("BT", [128, SBT], BF16, 4)
        KT = T("KT", [128, SBT], BF16, 4)
        vbf = T("vbf", [128, SBT], BF16, 4)
        bonus = T("bonus", [128, SBT], BF16, 4)
        tm = T("tm", [128, 4, 128], BF16, 4)
        PZ = T("PZ", [128, 3, SBT], BF16, 8)
        Hbfz = T("Hbfz", [128, 64], BF16, 8)
        qTz = T("qTz", [128, SBT], BF16, 8)
        NG = 4
        NMt = T("NM", [128, 2, 2, 128], BF16, 2 * NG)
        Mak = T("Mak", [128, 2, 128], BF16, NG)
        RBK = T("RBK", [128, 2, 2, 128], BF16, NG)
        PAIRS = [(0, 1), (2, 3)]
        Xtile = T("Xt", [128, 2, 2, 64], BF16, 2 * NG)
        ATbd = T("ATbd", [128, 128], BF16, NG)
        Gsb = T("Gsb", [128, 64], F32, NG)
        Ht = T("Hst", [128, 64], F32, 4)
        s1t = T("s1t", [128, 64], F32, NG)
        Wz = T("Wz", [128, 3, 64], BF16, NG)
        Wp = T("Wp", [128, 2, 64], BF16, NG)
        zlo = T("zlo", [128, 64], F32, NG)
        QT = T("QT", [128, 128], BF16, NG)
        prevcol = T("prevcol", [128, 13])
        kc = T("kcols", [128, 8])
        NWS = 5
        ws = T("ws", [128, 8, 128], BF16, NWS)
        ws_b2 = [Buf(f"ws_b2_{i}") for i in range(NWS)]
        wsrr = [0]
        sgr = T("sgr", [128, SBT], BF16, 4)
        sga = T("sga", [128, SBT], BF16, 4)
        NKS = 4
        KTatt = T("KTatt", [128, NKS * 128], BF16, 2)
        NV = 4
        Vpad = T("Vpad", [128, 2, 192], BF16, NV)
        ysqt = T("ysq", [128, 512], F32, 1)
        yn = T("yn", [128, 512], BF16, 1)
        gst = T("gst", [128, 6, 8], F32, 1)
        t1t = T("t1t", [128, 128], F32, 1)
        zr = T("zr", [128, SBT], BF16, 4)
        zatt = T("zatt", [128, SBT], BF16, 4)
        smt = T("smt", [128, 256], F32, 4)
        p32 = T("p32", [128, 256], F32, 4)
        pnt = T("pnt", [128, 256], BF16, 4)
        ptt = T("ptt", [128, 2, 128], BF16, 4)
        ast = T("ast", [128, 8], F32, 4)
        mT = T("mT", [128, 8, SBT], BF16, 1)
        sgt = T("sgt", [128, SBT], F32, 2)
        m12 = T("m12", [128, SBT], F32, 2)
        fst = T("fst", [128, 4], F32, 2)

        S.add("pool", lambda e: e.memset(prevcol.t[0][:], 0.0), [], [prevcol.b[0]])
        kct = kc.t[0]
        KB = kc.b[0]
        for j, val in enumerate([RMS_EPS, 1.0, -0.5, 1e-12, GN_EPS]):
            S.add("pool", lambda e, j=j, val=val: e.memset(kct[:, j:j + 1], val), [], [KB])
        eps_col = kct[:, 0:1]
        one_col = kct[:, 1:2]
        mhalf_col = kct[:, 2:3]
        tiny_col = kct[:, 3:4]
        gneps_col = kct[:, 4:5]
        for i in range(NG):
            S.add("pool", lambda e, i=i: e.memset(ATbd.t[i][:], 0.0), [], [ATbd.b[i]])
            S.add("pool", lambda e, i=i: e.memset(Wz.t[i][:], 0.0), [], [Wz.b[i]])
        for i in range(4):
            S.add("pool", lambda e, i=i: e.memset(Ht.t[i][:], 0.0), [], [Ht.b[i]])
        for i in range(NV):
            S.add("pool", lambda e, i=i: e.memset(Vpad.t[i][:], 0.0), [], [Vpad.b[i]])
        for i in range(8):
            S.add("pool", lambda e, i=i: e.memset(PZ.t[i][:], 0.0), [], [PZ.b[i]])
            S.add("pool", lambda e, i=i: e.memset(Hbfz.t[i][:], 0.0), [], [Hbfz.b[i]])
            S.add("pool", lambda e, i=i: e.memset(qTz.t[i][:], 0.0), [], [qTz.b[i]])
        for i in range(2):
            S.add("pool", lambda e, i=i: e.memset(KTatt.t[i][:], 0.0), [], [KTatt.b[i]])
        Wout_b = [Buf(f"wout{k}") for k in range(8)]
        for k in range(8):
            S.add("pool", lambda e, k=k: e.dma_start(out=Wout.t[0][:, k, :], in_=w_out[k * 128:(k + 1) * 128, :]),
                  [], [Wout_b[k]], dma=True)

        def bcm(ap2, n):
            a = ap2.ap
            return bass.AP(ap2.tensor, ap2.offset, [list(a[0]), [0, n], list(a[1])])

        def bcl(ap2, n):
            a = ap2.ap
            return bass.AP(ap2.tensor, ap2.offset, [list(a[0]), list(a[1]), [0, n]])

        def v3(ap, h=2):
            return ap.rearrange("p (h t) -> p h t", h=h)

        def rms_rstd(in_ap, in_bufs, stt, stb, junk_ap, junk_buf):
            S.add("act", lambda e: e.activation(out=junk_ap, in_=in_ap, func=AF.Square, accum_out=stt[:, 0:1]),
                  in_bufs, [junk_buf, stb])
            S.add("act", lambda e: e.activation(out=stt[:, 1:2], in_=stt[:, 0:1], func=AF.Ln, bias=eps_col, scale=1.0 / D),
                  [stb, KB], [stb])
            S.add("act", lambda e: e.activation(out=stt[:, 2:3], in_=stt[:, 1:2], func=AF.Exp, scale=-0.5), [stb], [stb])

        def ws_load(srcs):
            i = wsrr[0]
            wsrr[0] = (i + 1) % NWS
            t = ws.t[i]
            bufs = [ws.b[i], ws_b2[i]]
            for j, (dfn, dap) in enumerate(srcs):
                S.add("pool", lambda e, dfn=dfn, dap=dap, t=t: e.dma_start(out=dfn(t), in_=dap), [], [bufs[j]], dma=True)
            return t, bufs[:len(srcs)]

        def wcols(c0, n=128):
            return w_in[:, c0:c0 + n].rearrange("(k p) c -> p k c", p=128)

        def proj_fm(hTt, hTb, wt, wbufs, ncols=SBT, col0=0):
            pa, pab = fullbank()
            for k in range(8):
                S.add("pe", lambda e, pa=pa, k=k, wt=wt, hTt=hTt: e.matmul(
                    pa[:, 0:ncols], lhsT=wt[:, k, :], rhs=hTt[:, k, col0:col0 + ncols], start=(k == 0), stop=(k == 7)),
                    wbufs + [hTb], pab)
            return pa, pab

        P_ = [slice(0, 64), slice(64, 128)]
        own0 = nsb - nown

        def stage2_header():
            swt, swb = shwa()
            twt, twb = tw()
            S.add("act", lambda e, twt=twt, swt=swt: e.activation(out=twt[0:64, :], in_=swt[0:64, :], func=AF.Exp, scale=2.0), [swb], [twb])
            S.add("dve", lambda e, twt=twt: e.tensor_scalar(out=twt[0:64, :], in0=twt[0:64, :], scalar1=1.0, scalar2=None, op0=ALU.add), [twb], [twb])
            S.add("dve", lambda e, twt=twt: e.reciprocal(out=twt[0:64, :], in_=twt[0:64, :]), [twb], [twb])
            S.add("dve", lambda e, twt=twt: e.tensor_scalar(out=twt[0:64, :], in0=twt[0:64, :], scalar1=-2.0, scalar2=1.0, op0=ALU.mult, op1=ALU.add), [twb], [twb])
            S.add("act", lambda e, twt=twt, swt=swt: e.activation(out=twt[64:128, :], in_=swt[64:128, :], func=AF.Copy), [swb, twb], [twb])
            twh, twhb = tw_hi()
            twl, twlb = tw_lo()
            S.add("act", lambda e, twh=twh, twt=twt: e.activation(out=twh[:], in_=twt[:], func=AF.Copy), [twb], [twhb])
            S.add("dve", lambda e, twl=twl, twt=twt, twh=twh: e.tensor_tensor(out=twl[:], in0=twt[:], in1=twh[:], op=ALU.subtract), [twb, twhb], [twlb])
            return dict(twt=twt, twh=twh, twl=twl, twhb=twhb, twlb=twlb, twb=twb)
        def prep_hp(hp, sbi, own, twt=None, twh=None, twl=None, twhb=None, twlb=None, twb=None):
            si = sbi * 4 + hp
            rt, rb = shr(si)
            kt_, kb_ = shk(si)
            vt, vb = shv(si)
            pD, pDb = fullbank()
            pAa, pAb = fullbank()
            hsl = slice(hp * 128, (hp + 1) * 128)
            S.add("pe", lambda e, pD=pD, hsl=hsl, twh=twh: e.matmul(pD[:, 0:SBT], lhsT=Wdb.t[0][:, 0, hsl], rhs=twh[:, :], start=True, stop=False),
                  [Wdb.b[0], twhb], pDb)
            S.add("pe", lambda e, pD=pD, hsl=hsl, twl=twl: e.matmul(pD[:, 0:SBT], lhsT=Wdb.t[0][:, 0, hsl], rhs=twl[:, :], start=False, stop=False),
                  [Wdb.b[0], twlb], pDb)
            S.add("pe", lambda e, pD=pD, hsl=hsl, twh=twh: e.matmul(pD[:, 0:SBT], lhsT=Wdb.t[0][:, 1, hsl], rhs=twh[:, :], start=False, stop=True),
                  [Wdb.b[0], twhb], pDb)
            S.add("pe", lambda e, pAa=pAa, hsl=hsl, twh=twh: e.matmul(pAa[:, 0:SBT], lhsT=Wdb.t[0][:, 2, hsl], rhs=twh[:, :], start=True, stop=True),
                  [Wdb.b[0], twhb], pAb)
            e1, e1b = t_e1()
            ew, ewb = t_ew()
            at, ab_ = t_a()
            cs, csb = t_cs()
            csp, cspb = t_csp()
            en, enb = t_en()
            k2, k2b = t_k2()
            kkn, kknb = t_kkn()
            abt, abb = t_ab()
            ft, fb = t_f()
            S.add("act", lambda e, e1=e1, pD=pD, hp=hp: e.activation(out=e1[:], in_=pD[:, 0:SBT], func=AF.Exp, bias=pc("nw0", hp), scale=-1.0),
                  pDb + [PB], [e1b])
            S.add("act", lambda e, e1=e1: e.activation(out=e1[:], in_=e1[:], func=AF.Ln, bias=one_col), [e1b, KB], [e1b])
            S.add("act", lambda e, e1=e1, ew=ew: e.activation(out=ew[:], in_=e1[:], func=AF.Exp, bias=mhalf_col, scale=-1.0), [e1b, KB], [ewb])
            S.add("act", lambda e, at=at, pAa=pAa, hp=hp: e.activation(out=at[:], in_=pAa[:, 0:SBT], func=AF.Exp, bias=pc("na0", hp), scale=-1.0),
                  pAb + [PB], [ab_])
            yield
            S.add("dve", lambda e, at=at: e.tensor_scalar(out=at[:], in0=at[:], scalar1=1.0, scalar2=None, op0=ALU.add), [ab_], [ab_])
            S.add("dve", lambda e, at=at: e.reciprocal(out=at[:], in_=at[:]), [ab_], [ab_])
            for c in range(CPS):
                S.add("dve", lambda e, cs=cs, ew=ew, c=c: e.tensor_tensor_scan(
                    out=cs[:, c * C:(c + 1) * C], data0=ones_f, data1=ew[:, c * C:(c + 1) * C], initial=0.0,
                    op0=ALU.mult, op1=ALU.add), [ewb, CF], [csb])
            S.add("pool", lambda e, csp=csp, cs=cs, ew=ew: e.tensor_tensor(out=csp[:], in0=cs[:], in1=ew[:], op=ALU.subtract), [csb, ewb], [cspb])
            S.add("act", lambda e, en=en, cs=cs: e.activation(out=en[:], in_=cs[:], func=AF.Exp), [csb], [enb])
            S.add("act", lambda e, csp=csp: e.activation(out=csp[:], in_=csp[:], func=AF.Exp, scale=-1.0), [cspb], [cspb])
            gct, gcb = gC(si)
            S.add("act", lambda e, gct=gct, cs=cs: e.activation(
                out=gct[:, 0:CPS], in_=cs[:, :].rearrange("p (c t) -> p c t", t=C)[:, :, C - 1], func=AF.Exp, scale=-1.0), [csb], [gcb])
            yield
            k2h, k2hb = t_k2b()
            S.add("act", lambda e, k2h=k2h, kt_=kt_, hp=hp: e.activation(out=k2h[:], in_=kt_[:], func=AF.Square, scale=pc("kk", hp)), [kb_, PB], [k2hb])
            pS_, pSb = fullbank()
            S.add("pe", lambda e, pS_=pS_, k2h=k2h: e.matmul(pS_[:, 0:SBT], lhsT=bones_b, rhs=k2h[:], start=True, stop=True), [CB, k2hb], pSb)
            S.add("act", lambda e, k2=k2, pS_=pS_: e.activation(out=k2[:], in_=pS_[:, 0:SBT], func=AF.Ln, bias=tiny_col), pSb + [KB], [k2b])
            S.add("act", lambda e, k2=k2: e.activation(out=k2[:], in_=k2[:], func=AF.Exp, scale=-0.5), [k2b], [k2b])
            S.add("dve", lambda e, kkn=kkn, kt_=kt_, k2=k2, hp=hp: e.scalar_tensor_tensor(
                out=kkn[:], in0=kt_[:], scalar=pc("kk", hp), in1=k2[:], op0=ALU.mult, op1=ALU.mult), [kb_, k2b, PB], [kknb])
            yield
            ARt, ARb = AR(si)
            BTt, BTb = BT(si)
            KTt, KTb = KT(si)
            vbt, vbb = vbf(si)
            S.add("dve", lambda e, ARt=ARt, kkn=kkn, csp=csp: e.scalar_tensor_tensor(
                out=ARt[:, :, 0, :], in0=kkn[:, :].rearrange("p (c t) -> p c t", t=C), scalar=-1.0,
                in1=csp[:, :].rearrange("p (c t) -> p c t", t=C), op0=ALU.mult, op1=ALU.mult), [kknb, cspb], [ARb])
            S.add("pool", lambda e, abt=abt, kkn=kkn, at=at: e.tensor_tensor(out=abt[:], in0=kkn[:], in1=at[:], op=ALU.mult), [kknb, ab_], [abb])
            S.add("pool", lambda e, BTt=BTt, abt=abt, en=en: e.tensor_tensor(out=BTt[:], in0=abt[:], in1=en[:], op=ALU.mult), [abb, enb], [BTb])
            yield
            S.add("dve", lambda e, ft=ft, at=at, hp=hp: e.tensor_scalar(out=ft[:], in0=at[:], scalar1=pc("ka", hp), scalar2=pc("omka", hp),
                                                                    op0=ALU.mult, op1=ALU.add), [ab_, PB], [fb])
            S.add("pool", lambda e, ft=ft, kt_=kt_: e.tensor_tensor(out=ft[:], in0=kt_[:], in1=ft[:], op=ALU.mult), [kb_, fb], [fb])
            S.add("pool", lambda e, KTt=KTt, ft=ft, en=en: e.tensor_tensor(out=KTt[:], in0=ft[:], in1=en[:], op=ALU.mult), [fb, enb], [KTb])
            S.add("act", lambda e, vbt=vbt, vt=vt: e.activation(out=vbt[:], in_=vt[:], func=AF.Copy), [vb], [vbb])
            yield
            for hh in range(2):
                zt, zb = PZ(si * 2 + hh)
                S.add("pool", lambda e, zt=zt, ARt=ARt, hh=hh: e.tensor_copy(out=zt[P_[hh], 0, :].rearrange("p (c t) -> p c t", t=C), in_=ARt[P_[hh], :, 0, :]), [ARb], [zb])
                S.add("pool", lambda e, zt=zt, BTt=BTt, hh=hh: e.tensor_copy(out=zt[P_[hh], 1, :], in_=BTt[P_[hh], :]), [BTb], [zb])
                S.add("pool", lambda e, zt=zt, KTt=KTt, hh=hh: e.tensor_copy(out=zt[P_[hh], 2, :], in_=KTt[P_[hh], :]), [KTb], [zb])
            if own:
                ep, epb = t_ep()
                S.add("act", lambda e, ep=ep, cs=cs: e.activation(out=ep[:], in_=cs[:], func=AF.Exp, scale=-1.0), [csb], [epb])
                S.add("dve", lambda e, ARt=ARt, rt=rt, ep=ep: e.tensor_tensor(
                    out=ARt[:, :, 1, :], in0=rt[:, :].rearrange("p (c t) -> p c t", t=C),
                    in1=ep[:, :].rearrange("p (c t) -> p c t", t=C), op=ALU.mult), [rb, epb], [ARb])
                rkb_t, rkb_b = t_rkb()
                S.add("dve", lambda e, rkb_t=rkb_t, rt=rt, ft=ft, hp=hp: e.scalar_tensor_tensor(
                    out=rkb_t[:], in0=rt[:], scalar=pc("rk", hp), in1=ft[:], op0=ALU.mult, op1=ALU.mult), [rb, fb, PB], [rkb_b])
                pB_, pBb = fullbank()
                S.add("pe", lambda e, pB_=pB_, rkb_t=rkb_t: e.matmul(pB_[:, 0:SBT], lhsT=bones_b, rhs=rkb_t[:], start=True, stop=True), [CB, rkb_b], pBb)
                bnt, bnb = bonus(si)
                S.add("dve", lambda e, bnt=bnt, pB_=pB_, vt=vt: e.tensor_tensor(out=bnt[:], in0=pB_[:, 0:SBT], in1=vt[:], op=ALU.mult), pBb + [vb], [bnb])
            yield
        def proj_tile(sbi, ct, hTt, hTb):
            pa, pab = fullbank()
            for k in range(8):
                S.add("pe", lambda e, pa=pa, k=k, ct=ct, hTt=hTt: e.matmul(
                    pa[:, 0:SBT], lhsT=Wsh.t[0][:, k, ct * 128:(ct + 1) * 128], rhs=hTt[:, k, :],
                    start=(k == 0), stop=(k == 7)), [Wsh_b[k], hTb], pab)
            if ct == 12:
                dst, dstb = shwa()
            else:
                hp = ct % 4
                dst, dstb = (shr, shk, shv)[ct // 4](sbi * 4 + hp)
            tmp, tmpb = shtmp(ct)
            S.add("act", lambda e, tmp=tmp, pa=pa, ct=ct: e.activation(
                out=tmp[:], in_=pa[:, 0:SBT], func=AF.Copy, scale=pc("omu", ct)), pab + [PB], [tmpb])
            S.add("dve", lambda e, dst=dst, pa=pa, tmp=tmp, ct=ct: e.scalar_tensor_tensor(
                out=dst[:, 1:SBT], in0=pa[:, 0:SBT - 1], scalar=pc("mu", ct), in1=tmp[:, 1:SBT],
                op0=ALU.mult, op1=ALU.add), pab + [tmpb, PB], [dstb])
            S.add("dve", lambda e, dst=dst, tmp=tmp, ct=ct: e.scalar_tensor_tensor(
                out=dst[:, 0:1], in0=prevcol.t[0][:, ct:ct + 1], scalar=pc("mu", ct), in1=tmp[:, 0:1],
                op0=ALU.mult, op1=ALU.add), [prevcol.b[0], tmpb, PB], [dstb])
            S.add("act", lambda e, pa=pa, ct=ct: e.activation(
                out=prevcol.t[0][:, ct:ct + 1], in_=pa[:, SBT - 1:SBT], func=AF.Copy), pab, [prevcol.b[0]])

        sbst = {}

        def gen_first(sbi):
            own = sbi >= own0
            hTt, hTb = hT(sbi)
            for j in range(CPS):
                gc = sbi * CPS + j
                xtt, xtb = xt(gc)
                hbt, hbb = hb(gc)
                stt, stb = st0(gc)
                dma("sp", xtt[:], xw[gc * C:(gc + 1) * C, :], [], [xtb])
                rms_rstd(xtt[:], [xtb], stt, stb, hbt[:], hbb)
                S.add("dve", lambda e, xtt=xtt, stt=stt, hbt=hbt: e.scalar_tensor_tensor(
                    out=hbt[:], in0=xtt[:], scalar=stt[:, 2:3], in1=gpre.t[0][:], op0=ALU.mult, op1=ALU.mult),
                    [xtb, stb, gpre.b[0]], [hbb])
                yield
                pst = psT[0]
                pstb = psT_b[0]
                for k in range(8):
                    S.add("pe", lambda e, k=k, hbt=hbt, pst=pst: e.transpose(
                        out=pst[:, k * 128:(k + 1) * 128], in_=hbt[:, k * 128:(k + 1) * 128], identity=ident_b),
                        [hbb, CB], pstb)
                S.add("act", lambda e, pst=pst, hTt=hTt, j=j: e.activation(
                    out=hTt[:, :, j * C:(j + 1) * C], in_=pst[:, :].rearrange("p (k t) -> p k t", k=8), func=AF.Copy),
                    pstb, [hTb])
                yield
            proj_tile(sbi, 12, hTt, hTb)
            tw_ctx = stage2_header()
            sbst[sbi] = (tw_ctx, hTt, hTb)
            yield
            for hp in (0, 1):
                for q in range(3):
                    proj_tile(sbi, q * 4 + hp, hTt, hTb)
                    yield
                yield from prep_hp(hp, sbi, own, **tw_ctx)

        def gen_second(sbi):
            own = sbi >= own0
            tw_ctx, hTt, hTb = sbst[sbi]
            for hp in (2, 3):
                for q in range(3):
                    proj_tile(sbi, q * 4 + hp, hTt, hTb)
                    yield
                yield from prep_hp(hp, sbi, own, **tw_ctx)

        def drain(g):
            for _ in g:
                pass

        def mkfill(g, n=1):
            def fill():
                for _ in range(n):
                    try:
                        next(g)
                    except StopIteration:
                        return
            return fill

        drain(gen_first(0))
        for sbi in range(nsb):
            own = sbi >= own0
            halo_sb = (sbi == own0 - 1)
            hTt, hTb = hT(sbi)
            if own:
                drain(gen_second(sbi))
            if own or halo_sb:
                ncols, col0 = (SBT, 0) if own else (C, SBT - C)
                kcol = ((sbi - own0) * CPS + 1) * C if own else 0
                for g in range(2):
                    wt, wb = ws_load([(lambda t: t[:, :, 0:64], wcols(O_K + g * 64, 64)), (lambda t: t[:, :, 64:128], wcols(O_K + g * 64, 64))])
                    pa, pab = proj_fm(hTt, hTb, wt, wb, ncols, col0)
                    for cc in range(ncols // C):
                        ks = ((kcol // C) + cc) % NKS
                        S.add("act", lambda e, pa=pa, g=g, ks=ks, cc=cc: e.activation(
                            out=KTatt.t[g][:, ks * C:(ks + 1) * C], in_=pa[:, cc * C:(cc + 1) * C], func=AF.Identity, bias=pc("bk", g)), pab + [PB], [KTatt.b[g]])
                wt, wb = ws_load([(lambda t: t[:, :, :], wcols(O_V))])
                for c in (range(CPS) if own else [CPS - 1]):
                    lc1 = (sbi - own0) * CPS + c + 1 if own else 0
                    pa, pab = fullbank()
                    for k in range(8):
                        S.add("pe", lambda e, pa=pa, k=k, wt=wt, hTt=hTt, c=c: e.matmul(
                            pa[:, 0:128], lhsT=hTt[:, k, c * C:(c + 1) * C], rhs=wt[:, k, :], start=(k == 0), stop=(k == 7)), wb + [hTb], pab)
                    vp, vpb = Vpad(lc1)
                    S.add("dve", lambda e, vp=vp, pa=pa: e.tensor_tensor(out=vp[:, :, 0:64], in0=v3(pa[:, 0:128]), in1=v3(bvb.t[0][:, :]), op=ALU.add),
                          pab + [bvb.b[0]], [vpb])
                    S.add("pool", lambda e, vp=vp: e.tensor_copy(out=vp[:, :, 128:192], in_=vp[:, :, 0:64]), [vpb], [vpb])
            if own:
                for ct in range(4):
                    si = sbi * 4 + ct
                    wt, wb = ws_load([(lambda t: t[:, :, :], wcols(O_GR + ct * 128))])
                    pa, pab = proj_fm(hTt, hTb, wt, wb)
                    S.add("act", lambda e, pa=pa, si=si: e.activation(out=sgr(si)[0][:], in_=pa[:, 0:SBT], func=AF.Silu), pab, [sgr(si)[1]])
                    wt, wb = ws_load([(lambda t: t[:, :, :], wcols(O_Q + ct * 128))])
                    pa, pab = proj_fm(hTt, hTb, wt, wb)
                    for hh in range(2):
                        qz, qzb = qTz(si * 2 + hh)
                        S.add("act", lambda e, pa=pa, qz=qz, ct=ct, hh=hh: e.activation(
                            out=qz[P_[hh], :], in_=pa[P_[hh], 0:SBT], func=AF.Identity, bias=ppt[P_[hh], PPI["bq"] + ct:PPI["bq"] + ct + 1]),
                            pab + [PB], [qzb])
                    wt, wb = ws_load([(lambda t: t[:, :, :], wcols(O_GA + ct * 128))])
                    pa, pab = proj_fm(hTt, hTb, wt, wb)
                    S.add("act", lambda e, pa=pa, si=si: e.activation(out=sga(si)[0][:], in_=pa[:, 0:SBT], func=AF.Silu), pab, [sga(si)[1]])
            def emit_chunk_pairs(c, pairs, fill, own=own, sbi=sbi):
                gch = sbi * CPS + c
                csl = slice(c * C, (c + 1) * C)
                pYbank, pYbb = psA[0], [psA_b[0]]
                def mkctx(hp):
                    si = sbi * 4 + hp
                    gi = gch * 4 + hp
                    x = dict(hp=hp, si=si, gi=gi)
                    x["AR"], x["ARb"] = AR(si)
                    x["BT"], x["BTb"] = BT(si)
                    x["KT"], x["KTb"] = KT(si)
                    x["vb"], x["vbb"] = vbf(si)
                    x["zts"] = [PZ(si * 2 + hh) for hh in range(2)]
                    x["tm"], x["tmb"] = tm(gi)
                    return x

                def g_transposes(x, c=c, csl=csl):
                    pt_ = psT[1][:, 0:512]
                    ptb = psT_b[1]
                    srcs = [(x["AR"][:, c, 0, :], x["ARb"]), (x["BT"][:, csl], x["BTb"]), (x["KT"][:, csl], x["KTb"]), (x["vb"][:, csl], x["vbb"])]
                    for q, (sap, sbf) in enumerate(srcs):
                        S.add("pe", lambda e, pt_=pt_, q=q, sap=sap: e.transpose(out=pt_[:, q * 128:(q + 1) * 128], in_=sap, identity=ident_b),
                              [sbf, CB], ptb)
                    tmt = x["tm"]
                    S.add("act", lambda e, tmt=tmt, pt_=pt_: e.activation(out=tmt[:], in_=pt_.rearrange("p (q t) -> p q t", q=4), func=AF.Copy),
                          ptb, [x["tmb"]])

                def g_sprod_pe(x, c=c, csl=csl):
                    x["pS1"], x["pS1b"] = single(4)
                    x["pS2"], x["pS2b"] = single(2)
                    ARt, BTt = x["AR"], x["BT"]
                    for hh in range(2):
                        zt, zb = x["zts"][hh]
                        S.add("pe", lambda e, pS=x["pS1"], zt=zt, ARt=ARt, hh=hh: e.matmul(
                            pS[:, hh * 128:(hh + 1) * 128], lhsT=zt[:, 1, csl], rhs=ARt[:, c, 0, :], start=True, stop=True), [zb, x["ARb"]], x["pS1b"])
                    for hh in range(2):
                        zt, zb = x["zts"][hh]
                        S.add("pe", lambda e, pS=x["pS1"], zt=zt, BTt=BTt, hh=hh: e.matmul(
                            pS[:, (2 + hh) * 128:(3 + hh) * 128], lhsT=zt[:, 0, csl], rhs=BTt[:, csl], start=True, stop=True), [zb, x["BTb"]], x["pS1b"])
                    for hh in range(2):
                        zt, zb = x["zts"][hh]
                        S.add("pe", lambda e, pS=x["pS2"], zt=zt, ARt=ARt, hh=hh: e.matmul(
                            pS[:, hh * 128:(hh + 1) * 128], lhsT=zt[:, 2, csl], rhs=ARt[:, c, 0, :], start=True, stop=True), [zb, x["ARb"]], x["pS2b"])

                def g_sprod_evac(x):
                    gi = x["gi"]
                    nm, nmb = NMt(gi * 2)
                    mk, mkb = Mak(gi)
                    S.add("dve", lambda e, nm=nm, pS=x["pS1"]: e.tensor_tensor(out=nm[:, :, :, :].rearrange("p a h t -> p (a h) t"), in0=v3(pS, 4), in1=mask4, op=ALU.mult),
                          x["pS1b"] + [CB], [nmb])
                    S.add("dve", lambda e, mk=mk, pS=x["pS2"]: e.tensor_tensor(out=mk[:], in0=v3(pS, 2), in1=mask4[:, 0:2, :], op=ALU.mult),
                          x["pS2b"] + [CB], [mkb])
                    x["nm"], x["nmb"], x["mk"], x["mkb"] = nm, nmb, mk, mkb

                def g_r_pe(x, c=c, csl=csl):
                    x["pR"], x["pRb"] = single(4)
                    ARt = x["AR"]
                    for a_ in range(2):
                        for hh in range(2):
                            zt, zb = x["zts"][hh]
                            S.add("pe", lambda e, pR=x["pR"], zt=zt, ARt=ARt, hh=hh, a_=a_: e.matmul(
                                pR[:, (a_ * 2 + hh) * 128:(a_ * 2 + hh + 1) * 128], lhsT=zt[:, 1 + a_, csl], rhs=ARt[:, c, 1, :], start=True, stop=True),
                                [zb, x["ARb"]], x["pRb"])

                def g_r_evac(x, c=c, csl=csl):
                    rbk, rbkb = RBK(x["gi"])
                    for a_ in range(2):
                        S.add("dve", lambda e, rbk=rbk, pR=x["pR"], a_=a_: e.tensor_tensor(out=rbk[:, a_, :, :], in0=v3(pR[:, a_ * 256:(a_ + 1) * 256], 2), in1=mle2, op=ALU.mult),
                              x["pRb"] + [CB], [rbkb])
                    x["rbk"], x["rbkb"] = rbk, rbkb

                def g_pv_pe(x, c=c, csl=csl):
                    x["pV"], x["pVb"] = single(1)
                    mk, tmt = x["mk"], x["tm"]
                    for hh in range(2):
                        S.add("pe", lambda e, pV=x["pV"], hh=hh, mk=mk, tmt=tmt: e.matmul(
                            pV[:, hh * 64:(hh + 1) * 64], lhsT=mk[:, hh, :], rhs=tmt[:, 3, hh * 64:(hh + 1) * 64], start=True, stop=True), [x["mkb"], x["tmb"]], x["pVb"])

                def g_x0(x, c=c, csl=csl):
                    Xt, Xb = Xtile(x["gi"] * 2)
                    tmt = x["tm"]
                    S.add("pool", lambda e, Xt=Xt, tmt=tmt: e.tensor_copy(out=Xt[:, :, 0, :], in_=tmt[:, 0, :].rearrange("p (h k) -> p h k", h=2)), [x["tmb"]], [Xb])
                    S.add("act", lambda e, Xt=Xt, pV=x["pV"]: e.activation(out=Xt[:, :, 1, :], in_=pV[:, 0:128].rearrange("p (h k) -> p h k", h=2), func=AF.Copy),
                          x["pVb"], [Xb])
                    x["X"], x["Xb"] = Xt, Xb

                def g_level_pe(x, lv):
                    nm, nmb, Xt, Xb = x["nm"], x["nmb"], x["X"], x["Xb"]
                    x["pX"], x["pXb"] = single(2)
                    for hh in range(2):
                        S.add("pe", lambda e, pX=x["pX"], hh=hh, nm=nm, Xt=Xt: e.matmul(
                            pX[:, hh * 128:(hh + 1) * 128], lhsT=nm[:, 0, hh, :], rhs=Xt[:, hh, :, :].rearrange("p a k -> p (a k)"), start=True, stop=True),
                            [nmb, Xb], x["pXb"])
                    if lv < 6:
                        x["pNM"], x["pNMb"] = single(4)
                        for hh in range(2):
                            S.add("pe", lambda e, pN=x["pNM"], hh=hh, nm=nm: e.matmul(
                                pN[:, hh * 128:(hh + 1) * 128], lhsT=nm[:, 1, hh, :], rhs=nm[:, 0, hh, :], start=True, stop=True), [nmb], x["pNMb"])
                        if lv < 5:
                            for hh in range(2):
                                S.add("pe", lambda e, pN=x["pNM"], hh=hh, nm=nm: e.matmul(
                                    pN[:, (2 + hh) * 128:(3 + hh) * 128], lhsT=nm[:, 0, hh, :], rhs=nm[:, 1, hh, :], start=True, stop=True), [nmb], x["pNMb"])

                def g_level_evac(x, lv):
                    gi = x["gi"]
                    Xt, Xb = x["X"], x["Xb"]
                    Xn, Xnb = Xtile(gi * 2 + lv + 1)
                    S.add("dve", lambda e, Xn=Xn, pX=x["pX"], Xt=Xt: e.tensor_tensor(
                        out=Xn[:, :, :, :].rearrange("p h a k -> p h (a k)"), in0=v3(pX), in1=Xt[:, :, :, :].rearrange("p h a k -> p h (a k)"), op=ALU.add),
                        x["pXb"] + [Xb], [Xnb])
                    x["X"], x["Xb"] = Xn, Xnb
                    if lv < 6:
                        nn, nnb = NMt(gi * 2 + lv + 1)
                        w = 4 if lv < 5 else 2
                        S.add("act", lambda e, nn=nn, pN=x["pNM"], w=w: e.activation(
                            out=nn[:, :, :, :].rearrange("p a h t -> p (a h) t")[:, 0:w, :], in_=v3(pN[:, 0:w * 128], w), func=AF.Copy), x["pNMb"], [nnb])
                        x["nm"], x["nmb"] = nn, nnb

                def g_state(x, c=c, csl=csl, own=own, pYbank=(pYbank if own else None), pYbb=(pYbb if own else None)):
                    gi, hp, si = x["gi"], x["hp"], x["si"]
                    Xt, Xb, tmt, tmb = x["X"], x["Xb"], x["tm"], x["tmb"]
                    ARt, ARb = x["AR"], x["ARb"]
                    wz, wzb = Wz(gi)
                    S.add("pool", lambda e, wz=wz, Xt=Xt: e.tensor_copy(out=wz[:, 0::2, :], in_=Xt[:, :, 0, :]), [Xb], [wzb])
                    wzA = wz[:, 0:2, :].rearrange("p a k -> p (a k)")
                    wzB = wz[:, 1:3, :].rearrange("p a k -> p (a k)")
                    wp, wpb = Wp(gi)
                    S.add("pool", lambda e, wp=wp, Xt=Xt: e.tensor_copy(out=wp[:, :, :], in_=Xt[:, :, 0, :]), [Xb], [wpb])
                    pAT, pATb = single(1)
                    S.add("pe", lambda e, pAT=pAT, wp=wp, tmt=tmt: e.matmul(pAT[:, 0:128], lhsT=wp[:, :, :].rearrange("p a k -> p (a k)"), rhs=tmt[:, 1, :], start=True, stop=True),
                          [wpb, tmb], pATb)
                    atb, atbb = ATbd(gi)
                    for hh in range(2):
                        S.add("act", lambda e, atb=atb, pAT=pAT, hh=hh: e.activation(
                            out=atb[P_[hh], hh * 64:(hh + 1) * 64], in_=pAT[P_[hh], hh * 64:(hh + 1) * 64], func=AF.Copy), pATb, [atbb])
                    pG, pGb = single(1)
                    pG2, pG2b = single(1)
                    S.add("pe", lambda e, pG=pG, tmt=tmt: e.matmul(pG[:, 0:128], lhsT=tmt[:, 2, :], rhs=tmt[:, 3, :], start=True, stop=True),
                          [tmb], pGb)
                    for hh in range(2):
                        S.add("pe", lambda e, pG2=pG2, Xt=Xt, tmt=tmt, hh=hh: e.matmul(
                            pG2[:, hh * 64:(hh + 1) * 64], lhsT=tmt[:, 1, :], rhs=Xt[:, hh, 1, :], start=True, stop=True), [Xb, tmb], pG2b)
                    gs, gsb_ = Gsb(gi)
                    for hh in range(2):
                        S.add("act", lambda e, gs=gs, pG=pG, hh=hh: e.activation(
                            out=gs[P_[hh], :], in_=pG[P_[hh], hh * 64:(hh + 1) * 64], func=AF.Copy), pGb, [gsb_])
                        S.add("dve", lambda e, gs=gs, pG2=pG2, hh=hh: e.tensor_tensor(
                            out=gs[P_[hh], :], in0=pG2[P_[hh], hh * 64:(hh + 1) * 64], in1=gs[P_[hh], :], op=ALU.add), pG2b + [gsb_], [gsb_])
                    Htt, Hb_ = Ht(hp)
                    gct, gcb = gC(si)
                    if own:
                        hbz = [Hbfz(gi * 2 + hh) for hh in range(2)]
                        for hh in range(2):
                            S.add("pool", lambda e, hz=hbz[hh][0], Htt=Htt, hh=hh: e.tensor_copy(out=hz[P_[hh], :], in_=Htt[P_[hh], :]), [Hb_], [hbz[hh][1]])
                    hhl, hhlb = Hhl(gi)
                    S.add("pool", lambda e, hhl=hhl, Htt=Htt: e.tensor_copy(out=hhl[:, 0, :], in_=Htt[:]), [Hb_], [hhlb])
                    S.add("pool", lambda e, hhl=hhl, Htt=Htt: e.tensor_tensor(out=hhl[:, 1, :], in0=Htt[:], in1=hhl[:, 0, :], op=ALU.subtract), [Hb_, hhlb], [hhlb])
                    pZ, pZb = single(1)
                    S.add("pe", lambda e, pZ=pZ, atb=atb, hhl=hhl: e.matmul(pZ[:, 0:128], lhsT=atb[:], rhs=hhl[:, :, :].rearrange("p a v -> p (a v)"), start=True, stop=True),
                          [atbb, hhlb], pZb)
                    s1, s1b = s1t(gi)
                    S.add("pool", lambda e, s1=s1, Htt=Htt, gs=gs: e.tensor_tensor(out=s1[:], in0=Htt[:], in1=gs[:], op=ALU.add), [Hb_, gsb_], [s1b])
                    S.add("pool", lambda e, s1=s1, gct=gct: e.tensor_scalar(out=s1[:], in0=s1[:], scalar1=gct[:, c:c + 1], scalar2=1.0, op0=ALU.mult, op1=ALU.mult),
                          [s1b, gcb], [s1b])
                    S.add("dve", lambda e, pZ=pZ, gct=gct, s1=s1: e.scalar_tensor_tensor(
                        out=s1[:], in0=pZ[:, 0:64], scalar=gct[:, c:c + 1], in1=s1[:], op0=ALU.mult, op1=ALU.add), pZb + [gcb, s1b], [s1b])
                    S.add("dve", lambda e, Htt=Htt, pZ=pZ, gct=gct, s1=s1: e.scalar_tensor_tensor(
                        out=Htt[:], in0=pZ[:, 64:128], scalar=gct[:, c:c + 1], in1=s1[:], op0=ALU.mult, op1=ALU.add), pZb + [gcb, s1b], [Hb_])
                    if own:
                        rbk, rbkb = x["rbk"], x["rbkb"]
                        qt_, qtb = QT(gi)
                        for hh, wzX in enumerate((wzA, wzB)):
                            pQ, pQb = single(1)
                            S.add("pe", lambda e, pQ=pQ, wzX=wzX, rbk=rbk, hh=hh: e.matmul(pQ[:, 0:128], lhsT=wzX, rhs=rbk[:, 0, hh, :], start=True, stop=True),
                                  [wzb, rbkb], pQb)
                            S.add("dve", lambda e, qt_=qt_, pQ=pQ, ARt=ARt, hh=hh: e.tensor_tensor(
                                out=qt_[P_[hh], :], in0=pQ[P_[hh], 0:128], in1=ARt[P_[hh], c, 1, :], op=ALU.add), pQb + [ARb], [qtb])
                        for hh in range(2):
                            pY, pYb = pYbank, pYbb
                            ysl = slice((hp * 2 + hh) * 64, (hp * 2 + hh + 1) * 64)
                            S.add("pe", lambda e, pY=pY, ysl=ysl, hh=hh, rbk=rbk, Xt=Xt: e.matmul(
                                pY[:, ysl], lhsT=rbk[:, 0, hh, :], rhs=Xt[:, hh, 1, :], start=True, stop=False), [rbkb, Xb], pYb)
                            S.add("pe", lambda e, pY=pY, ysl=ysl, hh=hh, rbk=rbk, tmt=tmt: e.matmul(
                                pY[:, ysl], lhsT=rbk[:, 1, hh, :], rhs=tmt[:, 3, hh * 64:(hh + 1) * 64], start=False, stop=False), [rbkb, tmb], pYb)
                            S.add("pe", lambda e, pY=pY, ysl=ysl, qt_=qt_, hz=hbz[hh][0]: e.matmul(
                                pY[:, ysl], lhsT=qt_[:, :], rhs=hz[:, :], start=False, stop=True), [qtb, hbz[hh][1]], pYb)

                for pr in pairs:
                    ctxs = [mkctx(hp) for hp in pr]
                    for x in ctxs:
                        g_transposes(x)
                    for x in ctxs:
                        g_sprod_pe(x)
                    for x in ctxs:
                        g_sprod_evac(x)
                    fill()
                    if own:
                        for x in ctxs:
                            g_r_pe(x)
                        for x in ctxs:
                            g_r_evac(x)
                    for x in ctxs:
                        g_pv_pe(x)
                    for x in ctxs:
                        g_x0(x)
                    fill()
                    for lv in range(7):
                        for x in ctxs:
                            g_level_pe(x, lv)
                        for x in ctxs:
                            g_level_evac(x, lv)
                        fill()
                    for x in ctxs:
                        g_state(x)
            if not own:
                g2 = gen_second(sbi)
                for c in range(CPS):
                    emit_chunk_pairs(c, [PAIRS[0]], mkfill(g2, 2))
                drain(g2)
                g1 = gen_first(sbi + 1) if sbi + 1 < nsb else iter(())
                for c in range(CPS):
                    emit_chunk_pairs(c, [PAIRS[1]], mkfill(g1, 2))
                drain(g1)
                continue
            for c in range(CPS):
                gch = sbi * CPS + c
                csl = slice(c * C, (c + 1) * C)
                pYbank, pYbb = psA[0], [psA_b[0]]
                emit_chunk_pairs(c, PAIRS, lambda: None)
                lc = (sbi - own0) * CPS + c
                g_t, g_b = gst()
                pY, pYb = pYbank, pYbb
                yq, yqb = ysqt(0)
                S.add("dve", lambda e, g_t=g_t, pY=pY: e.tensor_reduce(out=g_t[:, 0, :], in_=v3(pY[:, 0:512], 8), axis=AX.X, op=ALU.add), pYb, [g_b])
                S.add("act", lambda e, yq=yq, pY=pY: e.activation(out=yq[:], in_=pY[:, 0:512], func=AF.Square), pYb, [yqb])
                S.add("dve", lambda e, g_t=g_t, yq=yq: e.tensor_reduce(out=g_t[:, 1, :], in_=v3(yq[:, :], 8), axis=AX.X, op=ALU.add), [yqb], [g_b])
                S.add("dve", lambda e, g_t=g_t: e.tensor_scalar(out=g_t[:, 2, :], in0=g_t[:, 0, :], scalar1=1.0 / 64, scalar2=None, op0=ALU.mult), [g_b], [g_b])
                S.add("dve", lambda e, g_t=g_t: e.tensor_tensor(out=g_t[:, 3, :], in0=g_t[:, 2, :], in1=g_t[:, 2, :], op=ALU.mult), [g_b], [g_b])
                S.add("dve", lambda e, g_t=g_t: e.scalar_tensor_tensor(out=g_t[:, 4, :], in0=g_t[:, 1, :], scalar=1.0 / 64, in1=g_t[:, 3, :],
                                                                       op0=ALU.mult, op1=ALU.subtract), [g_b], [g_b])
                S.add("act", lambda e, g_t=g_t: e.activation(out=g_t[:, 5, :], in_=g_t[:, 4, :], func=AF.Ln, bias=gneps_col), [g_b, KB], [g_b])
                S.add("act", lambda e, g_t=g_t: e.activation(out=g_t[:, 5, :], in_=g_t[:, 5, :], func=AF.Exp, scale=-0.5), [g_b], [g_b])
                ynt, ynb = yn()
                S.add("dve", lambda e, yq=yq, pY=pY, g_t=g_t: e.tensor_tensor(
                    out=v3(yq[:, :], 8), in0=v3(pY[:, 0:512], 8), in1=bcl(g_t[:, 2, :], 64), op=ALU.subtract), pYb + [g_b, yqb], [yqb])
                S.add("pool", lambda e, ynt=ynt, yq=yq, g_t=g_t: e.tensor_tensor(
                    out=v3(ynt[:, :], 8), in0=v3(yq[:, :], 8), in1=bcl(g_t[:, 5, :], 64), op=ALU.mult), [yqb, g_b], [ynb])
                pt_ = psT[1][:, 0:512]
                ptb = psT_b[1]
                for hp in range(4):
                    S.add("pe", lambda e, pt_=pt_, hp=hp, ynt=ynt: e.transpose(out=pt_[:, hp * 128:(hp + 1) * 128], in_=ynt[:, hp * 128:(hp + 1) * 128], identity=ident_b),
                          [ynb, CB], ptb)
                for hp in range(4):
                    si = sbi * 4 + hp
                    t1, t1b = t1t(hp)
                    S.add("dve", lambda e, t1=t1, pt_=pt_, hp=hp: e.tensor_scalar(out=t1[:], in0=pt_[:, hp * 128:(hp + 1) * 128], scalar1=pc("gnw", hp), scalar2=pc("gnb", hp),
                                                                            op0=ALU.mult, op1=ALU.add), ptb + [PB], [t1b])
                    S.add("pool", lambda e, t1=t1, si=si, csl=csl: e.tensor_tensor(out=t1[:], in0=t1[:], in1=bonus(si)[0][:, csl], op=ALU.add), [t1b, bonus(si)[1]], [t1b])
                    S.add("pool", lambda e, t1=t1, si=si, csl=csl: e.tensor_tensor(out=zr(si)[0][:, csl], in0=t1[:], in1=sgr(si)[0][:, csl], op=ALU.mult),
                          [t1b, sgr(si)[1]], [zr(si)[1]])
                if lc == NOC - 1:
                    dump("zr0", zr(sbi * 4)[0][:], [128, SBT], [zr(sbi * 4)[1]], BF16)
                if upto < 4:
                    continue
                am_i = 0 if lc == 0 else 1
                pObank, pObb = psA[1], [psA_b[1]]
                vprev, vprevb = Vpad(lc)
                vcur, vcurb = Vpad(lc + 1)
                for qp0 in (0, 2):
                    pts = psT[0]
                    hs = []
                    for qp in (qp0, qp0 + 1):
                        si = sbi * 4 + qp
                        g = qp // 2
                        for hh in range(2):
                            hd = qp * 2 + hh
                            pS, pSb_ = single(2)
                            qz, qzb = qTz(si * 2 + hh)
                            for kk_ in range(2):
                                ks = (lc + kk_) % NKS
                                S.add("pe", lambda e, pS=pS, qz=qz, g=g, ks=ks, kk_=kk_, csl=csl: e.matmul(
                                    pS[:, kk_ * 128:(kk_ + 1) * 128], lhsT=qz[:, csl], rhs=KTatt.t[g][:, ks * C:(ks + 1) * C], start=True, stop=True),
                                    [qzb, KTatt.b[g]], pSb_)
                            j4 = (qp - qp0) * 2 + hh
                            hs.append(dict(hd=hd, qp=qp, hh=hh, g=g, si=si, pS=pS, pSb=pSb_, sm=smt(j4), a=ast(j4), p3=p32(j4), pn=pnt(j4), pt=ptt(j4),
                                           ptsl=pts[:, j4 * 256:(j4 + 1) * 256]))
                    for h in hs:
                        S.add("dve", lambda e, sm=h["sm"][0], pS=h["pS"], am_i=am_i: e.scalar_tensor_tensor(
                            out=sm[:], in0=pS[:, 0:256], scalar=0.125, in1=amask.t[0][:, am_i, :], op0=ALU.mult, op1=ALU.add), h["pSb"] + [amask.b[0]], [h["sm"][1]])
                    for h in hs:
                        S.add("dve", lambda e, a_t=h["a"][0], sm=h["sm"][0]: e.tensor_reduce(out=a_t[:, 0:1], in_=sm[:], axis=AX.X, op=ALU.max), [h["sm"][1]], [h["a"][1]])
                    for h in hs:
                        S.add("dve", lambda e, a_t=h["a"][0], hd=h["hd"]: e.tensor_scalar(out=a_t[:, 1:2], in0=a_t[:, 0:1], scalar1=pc("sink", hd), scalar2=-1.0, op0=ALU.max, op1=ALU.mult),
                              [h["a"][1], PB], [h["a"][1]])
                    for h in hs:
                        S.add("act", lambda e, pp3=h["p3"][0], sm=h["sm"][0], a_t=h["a"][0]: e.activation(out=pp3[:], in_=sm[:], func=AF.Exp, bias=a_t[:, 1:2], accum_out=a_t[:, 2:3]),
                              [h["sm"][1], h["a"][1]], [h["p3"][1], h["a"][1]])
                    for h in hs:
                        S.add("act", lambda e, a_t=h["a"][0], hd=h["hd"]: e.activation(out=a_t[:, 3:4], in_=pc("sink", hd), func=AF.Exp, bias=a_t[:, 1:2]), [h["a"][1], PB], [h["a"][1]])
                    for h in hs:
                        S.add("dve", lambda e, a_t=h["a"][0]: e.tensor_tensor(out=a_t[:, 4:5], in0=a_t[:, 2:3], in1=a_t[:, 3:4], op=ALU.add), [h["a"][1]], [h["a"][1]])
                    for h in hs:
                        S.add("dve", lambda e, a_t=h["a"][0]: e.reciprocal(out=a_t[:, 5:6], in_=a_t[:, 4:5]), [h["a"][1]], [h["a"][1]])
                    for h in hs:
                        S.add("dve", lambda e, pn=h["pn"][0], pp3=h["p3"][0], a_t=h["a"][0]: e.tensor_scalar(out=pn[:], in0=pp3[:], scalar1=a_t[:, 5:6], scalar2=None, op0=ALU.mult),
                              [h["p3"][1], h["a"][1]], [h["pn"][1]])
                    for h in hs:
                        for kk_ in range(2):
                            S.add("pe", lambda e, ptsl=h["ptsl"], kk_=kk_, pn=h["pn"][0]: e.transpose(out=ptsl[:, kk_ * 128:(kk_ + 1) * 128], in_=pn[:, kk_ * 128:(kk_ + 1) * 128], identity=ident_b),
                                  [h["pn"][1], CB], psT_b[0])
                    for h in hs:
                        S.add("act", lambda e, pt2=h["pt"][0], ptsl=h["ptsl"]: e.activation(out=pt2[:], in_=v3(ptsl), func=AF.Copy), psT_b[0], [h["pt"][1]])
                    for qp in (qp0, qp0 + 1):
                        si = sbi * 4 + qp
                        g = qp // 2
                        pO, pOb = pObank[:, qp * 128:(qp + 1) * 128], pObb
                        n_ = 0
                        for h in [h for h in hs if h["qp"] == qp]:
                            pt2, pt2b = h["pt"]
                            hh = h["hh"]
                            for kk_, (vp, vpb) in enumerate([(vprev, vprevb), (vcur, vcurb)]):
                                S.add("pe", lambda e, pO=pO, vp=vp, g=g, hh=hh, pt2=pt2, kk_=kk_, n_=n_: e.matmul(
                                    pO[:, 0:128], lhsT=vp[:, g, hh * 64:hh * 64 + 128], rhs=pt2[:, kk_, :], start=(n_ == 0), stop=(n_ == 3)),
                                    [vpb, pt2b], pOb)
                                n_ += 1
                    for qp in (qp0, qp0 + 1):
                        si = sbi * 4 + qp
                        pO, pOb = pObank[:, qp * 128:(qp + 1) * 128], pObb
                        S.add("dve", lambda e, si=si, pO=pO, csl=csl: e.tensor_tensor(out=zatt(si)[0][:, csl], in0=pO[:, 0:128], in1=sga(si)[0][:, csl], op=ALU.mult),
                              pOb + [sga(si)[1]], [zatt(si)[1]])
                if lc == NOC - 1:
                    dump("za0", zatt(sbi * 4)[0][:], [128, SBT], [zatt(sbi * 4)[1]], BF16)
            if not own or upto < 5:
                continue
            mTt, mTb = mT()
            def load_j(j):
                return [ws_load([(lambda t: t[:, :, :], w_br[:, :, j * 128:(j + 1) * 128].rearrange("b (h p) c -> p (b h) c", p=128))]),
                        ws_load([(lambda t: t[:, :, :], wcols(O_GT + j * 128))]),
                        ws_load([(lambda t: t[:, :, :], wcols(O_GT + 1024 + j * 128))])]
            for j in range(8):
                cur_w = load_j(j)
                wt, wb = cur_w[0]
                pBr, pBrb = fullbank()
                pBa, pBab = fullbank()
                for hp in range(4):
                    si = sbi * 4 + hp
                    S.add("pe", lambda e, pBr=pBr, wt=wt, hp=hp, si=si: e.matmul(pBr[:, 0:SBT], lhsT=wt[:, hp, :], rhs=zr(si)[0][:], start=(hp == 0), stop=(hp == 3)),
                          wb + [zr(si)[1]], pBrb)
                for hp in range(4):
                    si = sbi * 4 + hp
                    S.add("pe", lambda e, pBa=pBa, wt=wt, hp=hp, si=si: e.matmul(pBa[:, 0:SBT], lhsT=wt[:, 4 + hp, :], rhs=zatt(si)[0][:], start=(hp == 0), stop=(hp == 3)),
                          wb + [zatt(si)[1]], pBab)
                halves = []
                for br in range(2):
                    wt2, wb2 = cur_w[1 + br]
                    pGt, pGtb = bankx()
                    for k in range(8):
                        S.add("pe", lambda e, pGt=pGt, k=k, wt2=wt2, hTt=hTt: e.matmul(
                            pGt[:, 0:SBT], lhsT=wt2[:, k, :], rhs=hTt[:, k, :], start=(k == 0), stop=(k == 7)), wb2 + [hTb], pGtb)
                    sg_, sgb_ = sgt(br)
                    S.add("act", lambda e, sg_=sg_, pGt=pGt: e.activation(out=sg_[:], in_=pGt[:, 0:SBT], func=AF.Sigmoid), pGtb, [sgb_])
                    halves.append((sg_, sgb_))
                m1, m1b = m12(0)
                m2, m2b = m12(1)
                S.add("dve", lambda e, m1=m1, pBr=pBr, sg_=halves[0][0]: e.tensor_tensor(out=m1[:], in0=pBr[:, 0:SBT], in1=sg_[:], op=ALU.mult), pBrb + [halves[0][1]], [m1b])
                S.add("dve", lambda e, m2=m2, pBa=pBa, sg_=halves[1][0]: e.tensor_tensor(out=m2[:], in0=pBa[:, 0:SBT], in1=sg_[:], op=ALU.mult), pBab + [halves[1][1]], [m2b])
                S.add("dve", lambda e, mTt=mTt, j=j, m1=m1, m2=m2: e.tensor_tensor(out=mTt[:, j, :], in0=m1[:], in1=m2[:], op=ALU.add), [m1b, m2b], [mTb])
            for c in range(CPS):
                gch = sbi * CPS + c
                lc = (sbi - own0) * CPS + c
                xr, xrb = xt(gch)
                dma("sp", xr[:], xw[gch * C:(gch + 1) * C, :], [], [xrb])
                for n in range(2):
                    pa, pab = fullbank()
                    for j in range(8):
                        S.add("pe", lambda e, pa=pa, j=j, n=n, mTt=mTt, c=c: e.matmul(
                            pa[:, 0:512], lhsT=mTt[:, j, c * C:(c + 1) * C], rhs=Wout.t[0][:, j, n * 512:(n + 1) * 512], start=(j == 0), stop=(j == 7)),
                            [mTb, Wout_b[j]], pab)
                    S.add("dve", lambda e, xr=xr, pa=pa, n=n: e.tensor_tensor(out=xr[:, n * 512:(n + 1) * 512], in0=pa[:, 0:512], in1=xr[:, n * 512:(n + 1) * 512], op=ALU.add),
                          pab + [xrb], [xrb])
                ft_, fb_ = fst(gch)
                hbt, hbb = hb(gch)
                rms_rstd(xr[:], [xrb], ft_, fb_, hbt[:], hbb)
                S.add("dve", lambda e, xr=xr, ft_=ft_: e.scalar_tensor_tensor(
                    out=xr[:], in0=xr[:], scalar=ft_[:, 2:3], in1=gfin.t[0][:], op0=ALU.mult, op1=ALU.mult), [xrb, fb_, gfin.b[0]], [xrb])
                dma("sp", out_d[lc * C:(lc + 1) * C, :], xr[:], [xrb], [], is_out=True)
            if sbi + 1 < nsb:
                drain(gen_first(sbi + 1))

        for hp in range(4):
            dump(f"H{hp}", Ht(hp)[0][:], [128, 64], [Ht(hp)[1]])

        semnames = list(Sched.ENG) + [("dma", j) for j in range(Sched.NDMA)]
        sems = {}
        for sk in semnames:
            nm = sk if isinstance(sk, str) else f"dma{sk[1]}"
            sems[sk] = es.enter_context(nc.semaphore("s_" + nm))
        nc._sbuf_left = nc.sbuf_bytes_remaining
        block = es.enter_context(nc.Block())
        S.emit(nc, block, sems)
    nc._dbg_dumps = dump_d
    nc._sched_counts = dict(S.cnt)
    nc._sched_total = S.total
    return nc


def host_consts():
    s = np.arange(128)[:, None]
    t = np.arange(128)[None, :]
    cst = np.zeros((128, 9, 128), np.float32)
    cst[:, 0] = (s == t)
    cst[:, 1] = (s < t)
    cst[:, 2] = (s < t)
    cst[:, 3] = (s > t)
    cst[:, 4] = (s > t)
    cst[:, 5] = (s <= t)
    cst[:, 6] = (s <= t)
    cst[:, 7] = ((s // 64) == (t // 64))
    cst[:, 8] = 1.0
    return cst


def attn_masks(first):
    qi = np.arange(128)[:, None]
    kj = np.arange(256)[None, :]
    dist = qi + 128 - kj
    band = (dist >= 0) & (dist < 128)
    rest = np.where(band, 0.0, -1e30).astype(np.float32)
    fm = np.where(band & (kj >= 128), 0.0, -1e30).astype(np.float32)
    am = np.stack([fm if first else rest, rest], axis=1)
    return np.ascontiguousarray(am)


def pack_params(p):
    pp = np.zeros((128, NPP_IN), np.float32)

    def put(name, vec, n):
        v = np.asarray(vec, np.float32).reshape(n, 128)
        pp[:, PPI[name]:PPI[name] + n] = v.T
    put("mu", p["mu_shift"][0], 13)
    put("w0", p["w0"][0], 4)
    put("a0", p["a0"][0], 4)
    put("kk", p["k_k"][0], 4)
    put("ka", p["k_a"][0], 4)
    put("rk", p["r_k"][0], 4)
    put("gnw", p["gn_w"][0], 4)
    put("gnb", p["gn_b"][0], 4)
    bq = np.asarray(p["b_qkv"][0], np.float32)
    put("bq", bq[0:512], 4)
    bk = bq[512:640]
    pp[:, PPI["bk"] + 0] = np.concatenate([bk[0:64], bk[0:64]])
    pp[:, PPI["bk"] + 1] = np.concatenate([bk[64:128], bk[64:128]])
    sk = np.asarray(p["sinks"][0], np.float32)
    pp[:, PPI["sink"]:PPI["sink"] + 8] = np.broadcast_to(sk[None, :], (128, 8))
    wdi = np.zeros((128, 2, 512), np.float32)
    wdi[0:64, 0] = np.asarray(p["w_decay_up"][0], np.float32)
    wdi[64:128, 1] = np.asarray(p["w_iclr_up"][0], np.float32)
    common = {
        "w_in": np.ascontiguousarray(np.asarray(p["w_in"][0], np.float32)),
        "w_br": np.ascontiguousarray(np.stack([np.asarray(p["w_branch_rwkv"][0], np.float32),
                                               np.asarray(p["w_branch_att"][0], np.float32)])),
        "w_out": np.ascontiguousarray(np.asarray(p["w_out"][0], np.float32)),
        "wdi": np.ascontiguousarray(wdi),
        "pp": pp,
        "gpre_b": np.ascontiguousarray(np.broadcast_to(np.asarray(p["g_pre"][0], np.float32)[None], (128, D))),
        "gfin_b": np.ascontiguousarray(np.broadcast_to(np.asarray(p["g_final"], np.float32)[None], (128, D))),
        "bv_b": np.ascontiguousarray(np.broadcast_to(bq[640:768][None], (128, 128))),
        "cst": host_consts(),
    }
    return common


def kernel(**inputs):
    x = np.asarray(inputs["x"], np.float32)
    common = pack_params(inputs)
    nc = build()
    in_maps = []
    for c in range(NCORES):
        b, q = c // 4, c % 4
        end = (q + 1) * OWN_TOK
        xw = np.zeros((SEQ, D), np.float32)
        xw[SEQ - end:] = x[b, :end]
        m = dict(common)
        m["xw"] = xw
        m["amask"] = attn_masks(q == 0)
        in_maps.append(m)
    res = run_bass_kernel_spmd(nc, in_maps, core_ids=list(range(NCORES)))
    out = np.zeros((2, SEQ, D), np.float32)
    for c in range(NCORES):
        b, q = c // 4, c % 4
        out[b, q * OWN_TOK:(q + 1) * OWN_TOK] = res.results[c]["out"]
    return out
```

```python
import numpy as np
import concourse.bass as bass
import concourse.mybir as mybir
from concourse.bass_utils import run_bass_kernel_spmd

F32 = mybir.dt.float32
BF16 = mybir.dt.bfloat16
AF = mybir.ActivationFunctionType
ALU = mybir.AluOpType
AX = mybir.AxisListType

D = 1024
NCORES = 8
SEQ = 8192
OWN_TOK = 2048
C = 128
SBT = 256
CPS = SBT // C
RMS_EPS = 1e-6
GN_EPS = 64e-5
IN_COLS = 5504
O_SH = 0
O_GR = 1664
O_Q = 2176
O_K = 2688
O_V = 2816
O_GA = 2944
O_GT = 3456

PPI = {}
_n = 0
for _name, _cnt in [("mu", 13), ("w0", 4), ("a0", 4), ("kk", 4), ("ka", 4), ("rk", 4),
                    ("gnw", 4), ("gnb", 4), ("bq", 4), ("bk", 2), ("sink", 8)]:
    PPI[_name] = _n
    _n += _cnt
NPP_IN = _n
for _name, _cnt in [("omu", 13), ("nw0", 4), ("omka", 4), ("na0", 4)]:
    PPI[_name] = _n
    _n += _cnt
NPP = _n


class Buf:
    __slots__ = ("name", "w", "r", "excl")

    def __init__(self, name, excl=False):
        self.name = name
        self.w = None
        self.r = []
        self.excl = excl


class Sched:
    ENG = ("pe", "act", "dve", "pool", "sp")
    NDMA = 24

    def __init__(self, same_sync=True):
        self.ops = {e: [] for e in self.ENG}
        self.cnt = {e: 0 for e in self.ENG}
        self.waited = {e: {} for e in self.ENG}
        self.same_sync = same_sync
        self.dma_val = [0] * self.NDMA
        self.dma_rr = 0
        self.dma_rr2 = 0
        self.out_tokens = []

    def add(self, eng, fn, reads=(), writes=(), dma=False, is_out=False):
        self.total = getattr(self, "total", 0) + 1
        if not dma and self.total > getattr(self, "cut", 10 ** 9):
            return None
        deps = {}

        def need(tk, hard):
            d = deps.get(tk[0])
            if d is None:
                deps[tk[0]] = [tk[1], tk[2], hard]
            else:
                d[0] = max(d[0], tk[1])
                d[2] = d[2] or hard
        for b in reads:
            if b.w is not None:
                need(b.w, True)
            if b.excl:
                for tk in b.r:
                    need(tk, False)
        for b in writes:
            if b.w is not None:
                need(b.w, True)
            for tk in b.r:
                need(tk, False)
        waits = []
        for semkey, (val, src, hard) in deps.items():
            if src == eng and not isinstance(semkey, tuple):
                if eng in ("pe", "sp"):
                    continue
                if not hard or not self.same_sync:
                    continue
            if self.waited[eng].get(semkey, 0) >= val:
                continue
            self.waited[eng][semkey] = val
            waits.append((semkey, val))
        if dma:
            half = self.NDMA // 2
            if eng == "sp":
                j = self.dma_rr
                self.dma_rr = (self.dma_rr + 1) % half
            else:
                j = half + self.dma_rr2
                self.dma_rr2 = (self.dma_rr2 + 1) % half
            semkey = ("dma", j)
            if self.dma_val[j] > 0 and self.waited[eng].get(semkey, 0) < self.dma_val[j]:
                self.waited[eng][semkey] = self.dma_val[j]
                waits.append((semkey, self.dma_val[j]))
            self.dma_val[j] += 16
            tok = (semkey, self.dma_val[j], eng)
            inc = 16
        else:
            self.cnt[eng] += 1
            tok = (eng, self.cnt[eng], eng)
            inc = 1
        for b in reads:
            b.r.append(tok)
        for b in writes:
            b.w = tok
            b.r = []
        if is_out:
            self.out_tokens.append(tok)
        self.ops[eng].append((waits, fn, tok[0], inc))
        return tok

    def emit(self, nc, block, sems):
        engmap = {"pe": block.tensor, "act": block.scalar, "dve": block.vector,
                  "pool": block.gpsimd, "sp": block.sync}
        for e in self.ENG:
            ops = self.ops[e]
            final = list(self.out_tokens) if e == "sp" else ()

            def body(eng, ops=ops, final=final):
                for waits, fn, semkey, inc in ops:
                    for sk, val in waits:
                        eng.wait_ge(sems[sk], val)
                    fn(eng).then_inc(sems[semkey], inc)
                for tk in final:
                    eng.wait_ge(sems[tk[0]], tk[1])
                if final != ():
                    for j in range(self.NDMA):
                        if self.dma_val[j] > 0:
                            eng.wait_ge(sems[("dma", j)], self.dma_val[j])
            engmap[e](body)


def build(nsb=SEQ // SBT, nown=OWN_TOK // SBT, upto=99, dumps=(), same_sync=True, cut=None):
    from contextlib import ExitStack
    nc = bass.Bass("TRN2", target_bir_lowering=False)
    WT = nsb * SBT
    OT = nown * SBT
    NOC = nown * CPS
    S = Sched(same_sync=same_sync)
    if cut is not None:
        S.cut = cut

    def din(name, shape, dt=F32):
        return nc.dram_tensor(name, list(shape), dt, kind="ExternalInput").ap()

    xw = din("xw", [WT, D])
    w_in = din("w_in", [D, IN_COLS])
    w_br = din("w_br", [2, 512, D])
    w_out = din("w_out", [D, D])
    wdi = din("wdi", [128, 2, 512])
    pp_in = din("pp", [128, NPP_IN])
    gpre_d = din("gpre_b", [128, D])
    gfin_d = din("gfin_b", [128, D])
    bv_d = din("bv_b", [128, 128])
    cst_d = din("cst", [128, 9, 128])
    am_d = din("amask", [128, 2, 256])
    out_d = nc.dram_tensor("out", [OT, D], F32, kind="ExternalOutput").ap()
    dump_d = {}

    es = ExitStack()
    with es:
        def sb(name, shape, dt=F32):
            return es.enter_context(nc.sbuf_tensor(name, list(shape), dt))

        def ps(name, shape, dt=F32):
            return es.enter_context(nc.psum_tensor(name, list(shape), dt))

        class T:
            def __init__(self, name, shape, dt=F32, n=1):
                self.t = [sb(f"{name}{i}", shape, dt) for i in range(n)]
                self.b = [Buf(f"{name}{i}") for i in range(n)]
                self.n = n

            def __call__(self, i=0):
                return self.t[i % self.n], self.b[i % self.n]

        def dma(eng, out, in_, reads, writes, is_out=False):
            return S.add(eng, lambda e: e.dma_start(out=out, in_=in_), reads, writes, dma=True, is_out=is_out)

        def dump(name, ap, shape, reads, dt=F32):
            if name not in dumps:
                return
            dd = nc.dram_tensor("dbg_" + name, list(shape), dt, kind="ExternalOutput").ap()
            dump_d[name] = dd
            dma("sp", dd, ap, reads, [], is_out=True)

        xt = T("xt", [128, D], F32, 2)
        cst_b = T("cst_b", [128, 8, 128], BF16)
        cst2 = T("cst2", [128, 1, 128])
        amask = T("amask", [128, 2, 256])
        PP = T("PP", [128, NPP])
        gpre = T("gpre", [128, D])
        gfin = T("gfin", [128, D])
        bvb = T("bvb", [128, 128])
        Wdb = T("Wdb", [128, 3, 512], BF16)
        Wsh = T("Wsh", [128, 8, 1664], BF16)
        Wout = T("Wout", [128, 8, D], BF16)

        stg = xt.t[1][:, :].rearrange("p (a b) -> p a b", a=8)
        dma("sp", stg, cst_d[:, 0:8, :], [], [xt.b[1]])
        dma("sp", cst2.t[0][:], cst_d[:, 8:9, :], [], [cst2.b[0]])
        dma("sp", PP.t[0][:, 0:NPP_IN], pp_in, [], [PP.b[0]])
        dma("sp", gpre.t[0][:], gpre_d, [], [gpre.b[0]])
        stg_w = xt.t[0][:, :].rearrange("p (a b) -> p a b", a=2)
        dma("sp", stg_w, wdi, [], [xt.b[0]])
        S.add("act", lambda e: e.activation(out=Wdb.t[0][:, 0, :], in_=stg_w[:, 0, :], func=AF.Copy), [xt.b[0]], [Wdb.b[0]])
        S.add("act", lambda e: e.activation(out=Wdb.t[0][:, 2, :], in_=stg_w[:, 1, :], func=AF.Copy), [xt.b[0]], [Wdb.b[0]])
        S.add("dve", lambda e: e.tensor_tensor(out=Wdb.t[0][:, 1, :], in0=stg_w[:, 0, :], in1=Wdb.t[0][:, 0, :], op=ALU.subtract), [xt.b[0], Wdb.b[0]], [Wdb.b[0]])
        Wsh_b = [Buf(f"Wsh_k{k}") for k in range(8)]
        for k in range(8):
            S.add("pool", lambda e, k=k: e.dma_start(out=Wsh.t[0][:, k, :], in_=w_in[k * 128:(k + 1) * 128, O_SH:O_SH + 1664]),
                  [], [Wsh_b[k]], dma=True)
        dma("sp", amask.t[0][:], am_d, [], [amask.b[0]])
        dma("sp", gfin.t[0][:], gfin_d, [], [gfin.b[0]])
        dma("sp", bvb.t[0][:], bv_d, [], [bvb.b[0]])
        S.add("dve", lambda e: e.tensor_copy(out=cst_b.t[0][:], in_=stg), [xt.b[1]], [cst_b.b[0]])
        ident_b = cst_b.t[0][:, 0, :]
        mask4 = cst_b.t[0][:, 1:5, :]
        mle2 = cst_b.t[0][:, 5:7, :]
        bones_b = cst_b.t[0][:, 7, :]
        ones_f = cst2.t[0][:, 0, :]
        CB = cst_b.b[0]
        CF = cst2.b[0]
        ppt = PP.t[0]
        PB = PP.b[0]

        def pc(name, i=0):
            j = PPI[name] + i
            return ppt[:, j:j + 1]

        S.add("dve", lambda e: e.tensor_scalar(out=ppt[:, PPI["omu"]:PPI["omu"] + 13], in0=ppt[:, PPI["mu"]:PPI["mu"] + 13],
                                               scalar1=-1.0, scalar2=1.0, op0=ALU.mult, op1=ALU.add), [PB], [PB])
        S.add("dve", lambda e: e.tensor_scalar(out=ppt[:, PPI["nw0"]:PPI["nw0"] + 4], in0=ppt[:, PPI["w0"]:PPI["w0"] + 4],
                                               scalar1=-1.0, scalar2=None, op0=ALU.mult), [PB], [PB])
        S.add("dve", lambda e: e.tensor_scalar(out=ppt[:, PPI["omka"]:PPI["omka"] + 4], in0=ppt[:, PPI["ka"]:PPI["ka"] + 4],
                                               scalar1=-1.0, scalar2=1.0, op0=ALU.mult, op1=ALU.add), [PB], [PB])
        S.add("dve", lambda e: e.tensor_scalar(out=ppt[:, PPI["na0"]:PPI["na0"] + 4], in0=ppt[:, PPI["a0"]:PPI["a0"] + 4],
                                               scalar1=-1.0, scalar2=None, op0=ALU.mult), [PB], [PB])

        psA = [ps(f"psA{i}", [128, 512]) for i in range(2)]
        psA_b = [Buf(f"psA{i}", True) for i in range(2)]
        psT = [ps(f"psT{i}", [128, 1024], BF16) for i in range(2)]
        psT_b = [[Buf(f"psT{i}_{h}", True) for h in range(2)] for i in range(2)]
        psLU = [[ps(f"psL{i}", [128, 512]), ps(f"psU{i}", [128, 512])] for i in range(2)]
        psLU_b = [[[Buf(f"psLU{i}_{lu}_{s}", True) for s in range(4)] for lu in range(2)] for i in range(2)]
        arr = [0]
        prr = [0]
        srr = [0]

        fb_mode = [0]

        def fullbank():
            if fb_mode[0]:
                return psA[1], [psA_b[1]]
            i = arr[0]
            arr[0] = (i + 1) % 2
            return psA[i], [psA_b[i]]

        def pair(ns):
            r = prr[0]
            if (r % 4) + ns > 4:
                r = (r // 4 + 1) * 4
            r %= 8
            p, s = r // 4, r % 4
            prr[0] = (r + ns) % 8
            sl = slice(s * 128, (s + ns) * 128)
            return (psLU[p][0][:, sl], psLU_b[p][0][s:s + ns], psLU[p][1][:, sl], psLU_b[p][1][s:s + ns])

        brr = [0]

        def bankx():
            i = brr[0]
            brr[0] = (i + 1) % 4
            p, lu = i // 2, i % 2
            return psLU[p][lu], list(psLU_b[p][lu])

        def single(ns):
            bk, bb = bankx()
            return bk[:, 0:ns * 128], bb

        hb = T("hb", [128, D], BF16, 1)
        hT = T("hT", [128, 8, SBT], BF16, 2)
        st0 = T("st0", [128, 4], F32, 2)
        shwa = T("shwa", [128, SBT])
        shtmp = T("shtmp", [128, SBT], F32, 1)
        shr = T("shr", [128, SBT], F32, 2)
        shk = T("shk", [128, SBT], F32, 2)
        shv = T("shv", [128, SBT], F32, 2)
        tw = T("tw", [128, SBT])
        tw_hi = T("tw_hi", [128, SBT], BF16)
        tw_lo = T("tw_lo", [128, SBT], BF16)
        t_k2b = T("t_k2b", [128, SBT], BF16)
        t_rkb = T("t_rkb", [128, SBT], BF16)
        Hhl = T("Hhl", [128, 2, 64], BF16, 4)
        t_e1 = T("t_e1", [128, SBT])
        t_ew = T("t_ew", [128, SBT])
        t_a = T("t_a", [128, SBT])
        t_cs = T("t_cs", [128, SBT])
        t_csp = T("t_csp", [128, SBT])
        t_en = T("t_en", [128, SBT])
        t_ep = T("t_ep", [128, SBT])
        t_k2 = T("t_k2", [128, SBT])
        t_kkn = T("t_kkn", [128, SBT])
        t_ab = T("t_ab", [128, SBT])
        t_f = T("t_f", [128, SBT])
        gC = T("gC", [128, CPS], F32, 8)
        AR = T("AR", [128, CPS, 2, C], BF16, 4)
        BT = T("BT", [128, SBT], BF16, 4)
        KT = T("KT", [128, SBT], BF16, 4)
        vbf = T("vbf", [128, SBT], BF16, 4)
        bonus = T("bonus", [128, SBT], BF16, 4)
        tm = T("tm", [128, 4, 128], BF16, 4)
        PZ = T("PZ", [128, 3, SBT], BF16, 8)
        Hbfz = T("Hbfz", [128, 64], BF16, 8)
        qTz = T("qTz", [128, SBT], BF16, 8)
        NG = 4
        NMt = T("NM", [128, 2, 2, 128], BF16, 2 * NG)
        Mak = T("Mak", [128, 2, 128], BF16, NG)
        RBK = T("RBK", [128, 2, 2, 128], BF16, NG)
        PAIRS = [(0, 1), (2, 3)]
        Xtile = T("Xt", [128, 2, 2, 64], BF16, 2 * NG)
        ATbd = T("ATbd", [128, 128], BF16, NG)
        Gsb = T("Gsb", [128, 64], F32, NG)
        Ht = T("Hst", [128, 64], F32, 4)
        s1t = T("s1t", [128, 64], F32, NG)
        Wz = T("Wz", [128, 3, 64], BF16, NG)
        Wp = T("Wp", [128, 2, 64], BF16, NG)
        zlo = T("zlo", [128, 64], F32, NG)
        QT = T("QT", [128, 128], BF16, NG)
        prevcol = T("prevcol", [128, 13])
        kc = T("kcols", [128, 8])
        NWS = 5
        ws = T("ws", [128, 8, 128], BF16, NWS)
        ws_b2 = [Buf(f"ws_b2_{i}") for i in range(NWS)]
        wsrr = [0]
        sgr = T("sgr", [128, SBT], BF16, 4)
        sga = T("sga", [128, SBT], BF16, 4)
        NKS = 4
        KTatt = T("KTatt", [128, NKS * 128], BF16, 2)
        NV = 4
        Vpad = T("Vpad", [128, 2, 192], BF16, NV)
        ysqt = T("ysq", [128, 512], F32, 1)
        yn = T("yn", [128, 512], BF16, 1)
        gst = T("gst", [128, 6, 8], F32, 1)
        t1t = T("t1t", [128, 128], F32, 1)
        zr = T("zr", [128, SBT], BF16, 4)
        zatt = T("zatt", [128, SBT], BF16, 4)
        smt = T("smt", [128, 256], F32, 4)
        p32 = T("p32", [128, 256], F32, 4)
        pnt = T("pnt", [128, 256], BF16, 4)
        ptt = T("ptt", [128, 2, 128], BF16, 4)
        ast = T("ast", [128, 8], F32, 4)
        mT = T("mT", [128, 8, SBT], BF16, 1)
        sgt = T("sgt", [128, SBT], F32, 2)
        m12 = T("m12", [128, SBT], F32, 2)
        fst = T("fst", [128, 4], F32, 2)

        S.add("pool", lambda e: e.memset(prevcol.t[0][:], 0.0), [], [prevcol.b[0]])
        kct = kc.t[0]
        KB = kc.b[0]
        for j, val in enumerate([RMS_EPS, 1.0, -0.5, 1e-12, GN_EPS]):
            S.add("pool", lambda e, j=j, val=val: e.memset(kct[:, j:j + 1], val), [], [KB])
        eps_col = kct[:, 0:1]
        one_col = kct[:, 1:2]
        mhalf_col = kct[:, 2:3]
        tiny_col = kct[:, 3:4]
        gneps_col = kct[:, 4:5]
        for i in range(NG):
            S.add("pool", lambda e, i=i: e.memset(ATbd.t[i][:], 0.0), [], [ATbd.b[i]])
            S.add("pool", lambda e, i=i: e.memset(Wz.t[i][:], 0.0), [], [Wz.b[i]])
        for i in range(4):
            S.add("pool", lambda e, i=i: e.memset(Ht.t[i][:], 0.0), [], [Ht.b[i]])
        for i in range(NV):
            S.add("pool", lambda e, i=i: e.memset(Vpad.t[i][:], 0.0), [], [Vpad.b[i]])
        for i in range(8):
            S.add("pool", lambda e, i=i: e.memset(PZ.t[i][:], 0.0), [], [PZ.b[i]])
            S.add("pool", lambda e, i=i: e.memset(Hbfz.t[i][:], 0.0), [], [Hbfz.b[i]])
            S.add("pool", lambda e, i=i: e.memset(qTz.t[i][:], 0.0), [], [qTz.b[i]])
        for i in range(2):
            S.add("pool", lambda e, i=i: e.memset(KTatt.t[i][:], 0.0), [], [KTatt.b[i]])
        Wout_b = [Buf(f"wout{k}") for k in range(8)]
        for k in range(8):
            S.add("pool", lambda e, k=k: e.dma_start(out=Wout.t[0][:, k, :], in_=w_out[k * 128:(k + 1) * 128, :]),
                  [], [Wout_b[k]], dma=True)

        def bcm(ap2, n):
            a = ap2.ap
            return bass.AP(ap2.tensor, ap2.offset, [list(a[0]), [0, n], list(a[1])])

        def bcl(ap2, n):
            a = ap2.ap
            return bass.AP(ap2.tensor, ap2.offset, [list(a[0]), list(a[1]), [0, n]])

        def v3(ap, h=2):
            return ap.rearrange("p (h t) -> p h t", h=h)

        def rms_rstd(in_ap, in_bufs, stt, stb, junk_ap, junk_buf):
            S.add("act", lambda e: e.activation(out=junk_ap, in_=in_ap, func=AF.Square, accum_out=stt[:, 0:1]),
                  in_bufs, [junk_buf, stb])
            S.add("act", lambda e: e.activation(out=stt[:, 1:2], in_=stt[:, 0:1], func=AF.Ln, bias=eps_col, scale=1.0 / D),
                  [stb, KB], [stb])
            S.add("act", lambda e: e.activation(out=stt[:, 2:3], in_=stt[:, 1:2], func=AF.Exp, scale=-0.5), [stb], [stb])

        def ws_load(srcs):
            i = wsrr[0]
            wsrr[0] = (i + 1) % NWS
            t = ws.t[i]
            bufs = [ws.b[i], ws_b2[i]]
            for j, (dfn, dap) in enumerate(srcs):
                S.add("pool", lambda e, dfn=dfn, dap=dap, t=t: e.dma_start(out=dfn(t), in_=dap), [], [bufs[j]], dma=True)
            return t, bufs[:len(srcs)]

        def wcols(c0, n=128):
            return w_in[:, c0:c0 + n].rearrange("(k p) c -> p k c", p=128)

        def proj_fm(hTt, hTb, wt, wbufs, ncols=SBT, col0=0):
            pa, pab = fullbank()
            for k in range(8):
                S.add("pe", lambda e, pa=pa, k=k, wt=wt, hTt=hTt: e.matmul(
                    pa[:, 0:ncols], lhsT=wt[:, k, :], rhs=hTt[:, k, col0:col0 + ncols], start=(k == 0), stop=(k == 7)),
                    wbufs + [hTb], pab)
            return pa, pab

        P_ = [slice(0, 64), slice(64, 128)]
        own0 = nsb - nown

        def stage2_header():
            swt, swb = shwa()
            twt, twb = tw()
            S.add("act", lambda e, twt=twt, swt=swt: e.activation(out=twt[0:64, :], in_=swt[0:64, :], func=AF.Exp, scale=2.0), [swb], [twb])
            S.add("dve", lambda e, twt=twt: e.tensor_scalar(out=twt[0:64, :], in0=twt[0:64, :], scalar1=1.0, scalar2=None, op0=ALU.add), [twb], [twb])
            S.add("dve", lambda e, twt=twt: e.reciprocal(out=twt[0:64, :], in_=twt[0:64, :]), [twb], [twb])
            S.add("dve", lambda e, twt=twt: e.tensor_scalar(out=twt[0:64, :], in0=twt[0:64, :], scalar1=-2.0, scalar2=1.0, op0=ALU.mult, op1=ALU.add), [twb], [twb])
            S.add("act", lambda e, twt=twt, swt=swt: e.activation(out=twt[64:128, :], in_=swt[64:128, :], func=AF.Copy), [swb, twb], [twb])
            twh, twhb = tw_hi()
            twl, twlb = tw_lo()
            S.add("act", lambda e, twh=twh, twt=twt: e.activation(out=twh[:], in_=twt[:], func=AF.Copy), [twb], [twhb])
            S.add("dve", lambda e, twl=twl, twt=twt, twh=twh: e.tensor_tensor(out=twl[:], in0=twt[:], in1=twh[:], op=ALU.subtract), [twb, twhb], [twlb])
            return dict(twt=twt, twh=twh, twl=twl, twhb=twhb, twlb=twlb, twb=twb)
        def prep_hp(hp, sbi, own, twt=None, twh=None, twl=None, twhb=None, twlb=None, twb=None):
            si = sbi * 4 + hp
            rt, rb = shr(si)
            kt_, kb_ = shk(si)
            vt, vb = shv(si)
            pD, pDb = fullbank()
            hsl = slice(hp * 128, (hp + 1) * 128)
            S.add("pe", lambda e, pD=pD, hsl=hsl, twh=twh: e.matmul(pD[:, 0:SBT], lhsT=Wdb.t[0][:, 0, hsl], rhs=twh[:, :], start=True, stop=False),
                  [Wdb.b[0], twhb], pDb)
            S.add("pe", lambda e, pD=pD, hsl=hsl, twl=twl: e.matmul(pD[:, 0:SBT], lhsT=Wdb.t[0][:, 0, hsl], rhs=twl[:, :], start=False, stop=False),
                  [Wdb.b[0], twlb], pDb)
            S.add("pe", lambda e, pD=pD, hsl=hsl, twh=twh: e.matmul(pD[:, 0:SBT], lhsT=Wdb.t[0][:, 1, hsl], rhs=twh[:, :], start=False, stop=True),
                  [Wdb.b[0], twhb], pDb)
            e1, e1b = t_e1()
            ew, ewb = t_ew()
            at, ab_ = t_a()
            cs, csb = t_cs()
            csp, cspb = t_csp()
            en, enb = t_en()
            k2, k2b = t_k2()
            kkn, kknb = t_kkn()
            abt, abb = t_ab()
            ft, fb = t_f()
            S.add("act", lambda e, e1=e1, pD=pD, hp=hp: e.activation(out=e1[:], in_=pD[:, 0:SBT], func=AF.Exp, bias=pc("nw0", hp), scale=-1.0),
                  pDb + [PB], [e1b])
            pAa, pAb = fullbank()
            S.add("pe", lambda e, pAa=pAa, hsl=hsl, twh=twh: e.matmul(pAa[:, 0:SBT], lhsT=Wdb.t[0][:, 2, hsl], rhs=twh[:, :], start=True, stop=True),
                  [Wdb.b[0], twhb], pAb)
            S.add("act", lambda e, e1=e1: e.activation(out=e1[:], in_=e1[:], func=AF.Ln, bias=one_col), [e1b, KB], [e1b])
            S.add("act", lambda e, e1=e1, ew=ew: e.activation(out=ew[:], in_=e1[:], func=AF.Exp, bias=mhalf_col, scale=-1.0), [e1b, KB], [ewb])
            S.add("act", lambda e, at=at, pAa=pAa, hp=hp: e.activation(out=at[:], in_=pAa[:, 0:SBT], func=AF.Exp, bias=pc("na0", hp), scale=-1.0),
                  pAb + [PB], [ab_])
            yield
            S.add("dve", lambda e, at=at: e.tensor_scalar(out=at[:], in0=at[:], scalar1=1.0, scalar2=None, op0=ALU.add), [ab_], [ab_])
            S.add("dve", lambda e, at=at: e.reciprocal(out=at[:], in_=at[:]), [ab_], [ab_])
            for c in range(CPS):
                S.add("dve", lambda e, cs=cs, ew=ew, c=c: e.tensor_tensor_scan(
                    out=cs[:, c * C:(c + 1) * C], data0=ones_f, data1=ew[:, c * C:(c + 1) * C], initial=0.0,
                    op0=ALU.mult, op1=ALU.add), [ewb, CF], [csb])
            S.add("pool", lambda e, csp=csp, cs=cs, ew=ew: e.tensor_tensor(out=csp[:], in0=cs[:], in1=ew[:], op=ALU.subtract), [csb, ewb], [cspb])
            S.add("act", lambda e, en=en, cs=cs: e.activation(out=en[:], in_=cs[:], func=AF.Exp), [csb], [enb])
            S.add("act", lambda e, csp=csp: e.activation(out=csp[:], in_=csp[:], func=AF.Exp, scale=-1.0), [cspb], [cspb])
            gct, gcb = gC(si)
            S.add("act", lambda e, gct=gct, cs=cs: e.activation(
                out=gct[:, 0:CPS], in_=cs[:, :].rearrange("p (c t) -> p c t", t=C)[:, :, C - 1], func=AF.Exp, scale=-1.0), [csb], [gcb])
            yield
            k2h, k2hb = t_k2b()
            S.add("act", lambda e, k2h=k2h, kt_=kt_, hp=hp: e.activation(out=k2h[:], in_=kt_[:], func=AF.Square, scale=pc("kk", hp)), [kb_, PB], [k2hb])
            pS_, pSb = fullbank()
            S.add("pe", lambda e, pS_=pS_, k2h=k2h: e.matmul(pS_[:, 0:SBT], lhsT=bones_b, rhs=k2h[:], start=True, stop=True), [CB, k2hb], pSb)
            S.add("act", lambda e, k2=k2, pS_=pS_: e.activation(out=k2[:], in_=pS_[:, 0:SBT], func=AF.Ln, bias=tiny_col), pSb + [KB], [k2b])
            S.add("act", lambda e, k2=k2: e.activation(out=k2[:], in_=k2[:], func=AF.Exp, scale=-0.5), [k2b], [k2b])
            S.add("dve", lambda e, kkn=kkn, kt_=kt_, k2=k2, hp=hp: e.scalar_tensor_tensor(
                out=kkn[:], in0=kt_[:], scalar=pc("kk", hp), in1=k2[:], op0=ALU.mult, op1=ALU.mult), [kb_, k2b, PB], [kknb])
            yield
            ARt, ARb = AR(si)
            BTt, BTb = BT(si)
            KTt, KTb = KT(si)
            vbt, vbb = vbf(si)
            S.add("dve", lambda e, ARt=ARt, kkn=kkn, csp=csp: e.scalar_tensor_tensor(
                out=ARt[:, :, 0, :], in0=kkn[:, :].rearrange("p (c t) -> p c t", t=C), scalar=-1.0,
                in1=csp[:, :].rearrange("p (c t) -> p c t", t=C), op0=ALU.mult, op1=ALU.mult), [kknb, cspb], [ARb])
            S.add("pool", lambda e, abt=abt, kkn=kkn, at=at: e.tensor_tensor(out=abt[:], in0=kkn[:], in1=at[:], op=ALU.mult), [kknb, ab_], [abb])
            S.add("pool", lambda e, BTt=BTt, abt=abt, en=en: e.tensor_tensor(out=BTt[:], in0=abt[:], in1=en[:], op=ALU.mult), [abb, enb], [BTb])
            yield
            S.add("dve", lambda e, ft=ft, at=at, hp=hp: e.tensor_scalar(out=ft[:], in0=at[:], scalar1=pc("ka", hp), scalar2=pc("omka", hp),
                                                                    op0=ALU.mult, op1=ALU.add), [ab_, PB], [fb])
            S.add("pool", lambda e, ft=ft, kt_=kt_: e.tensor_tensor(out=ft[:], in0=kt_[:], in1=ft[:], op=ALU.mult), [kb_, fb], [fb])
            S.add("pool", lambda e, KTt=KTt, ft=ft, en=en: e.tensor_tensor(out=KTt[:], in0=ft[:], in1=en[:], op=ALU.mult), [fb, enb], [KTb])
            S.add("act", lambda e, vbt=vbt, vt=vt: e.activation(out=vbt[:], in_=vt[:], func=AF.Copy), [vb], [vbb])
            yield
            for hh in range(2):
                zt, zb = PZ(si * 2 + hh)
                S.add("pool", lambda e, zt=zt, ARt=ARt, hh=hh: e.tensor_copy(out=zt[P_[hh], 0, :].rearrange("p (c t) -> p c t", t=C), in_=ARt[P_[hh], :, 0, :]), [ARb], [zb])
                S.add("pool", lambda e, zt=zt, BTt=BTt, hh=hh: e.tensor_copy(out=zt[P_[hh], 1, :], in_=BTt[P_[hh], :]), [BTb], [zb])
                S.add("pool", lambda e, zt=zt, KTt=KTt, hh=hh: e.tensor_copy(out=zt[P_[hh], 2, :], in_=KTt[P_[hh], :]), [KTb], [zb])
            if own:
                ep, epb = t_ep()
                S.add("act", lambda e, ep=ep, cs=cs: e.activation(out=ep[:], in_=cs[:], func=AF.Exp, scale=-1.0), [csb], [epb])
                S.add("dve", lambda e, ARt=ARt, rt=rt, ep=ep: e.tensor_tensor(
                    out=ARt[:, :, 1, :], in0=rt[:, :].rearrange("p (c t) -> p c t", t=C),
                    in1=ep[:, :].rearrange("p (c t) -> p c t", t=C), op=ALU.mult), [rb, epb], [ARb])
                rkb_t, rkb_b = t_rkb()
                S.add("dve", lambda e, rkb_t=rkb_t, rt=rt, ft=ft, hp=hp: e.scalar_tensor_tensor(
                    out=rkb_t[:], in0=rt[:], scalar=pc("rk", hp), in1=ft[:], op0=ALU.mult, op1=ALU.mult), [rb, fb, PB], [rkb_b])
                pB_, pBb = fullbank()
                S.add("pe", lambda e, pB_=pB_, rkb_t=rkb_t: e.matmul(pB_[:, 0:SBT], lhsT=bones_b, rhs=rkb_t[:], start=True, stop=True), [CB, rkb_b], pBb)
                bnt, bnb = bonus(si)
                S.add("dve", lambda e, bnt=bnt, pB_=pB_, vt=vt: e.tensor_tensor(out=bnt[:], in0=pB_[:, 0:SBT], in1=vt[:], op=ALU.mult), pBb + [vb], [bnb])
            yield
        def proj_tile(sbi, ct, hTt, hTb):
            pa, pab = fullbank()
            for k in range(8):
                S.add("pe", lambda e, pa=pa, k=k, ct=ct, hTt=hTt: e.matmul(
                    pa[:, 0:SBT], lhsT=Wsh.t[0][:, k, ct * 128:(ct + 1) * 128], rhs=hTt[:, k, :],
                    start=(k == 0), stop=(k == 7)), [Wsh_b[k], hTb], pab)
            if ct == 12:
                dst, dstb = shwa()
            else:
                hp = ct % 4
                dst, dstb = (shr, shk, shv)[ct // 4](sbi * 4 + hp)
            tmp, tmpb = shtmp(ct)
            S.add("act", lambda e, tmp=tmp, pa=pa, ct=ct: e.activation(
                out=tmp[:], in_=pa[:, 0:SBT], func=AF.Copy, scale=pc("omu", ct)), pab + [PB], [tmpb])
            S.add("dve", lambda e, dst=dst, pa=pa, tmp=tmp, ct=ct: e.scalar_tensor_tensor(
                out=dst[:, 1:SBT], in0=pa[:, 0:SBT - 1], scalar=pc("mu", ct), in1=tmp[:, 1:SBT],
                op0=ALU.mult, op1=ALU.add), pab + [tmpb, PB], [dstb])
            S.add("dve", lambda e, dst=dst, tmp=tmp, ct=ct: e.scalar_tensor_tensor(
                out=dst[:, 0:1], in0=prevcol.t[0][:, ct:ct + 1], scalar=pc("mu", ct), in1=tmp[:, 0:1],
                op0=ALU.mult, op1=ALU.add), [prevcol.b[0], tmpb, PB], [dstb])
            S.add("act", lambda e, pa=pa, ct=ct: e.activation(
                out=prevcol.t[0][:, ct:ct + 1], in_=pa[:, SBT - 1:SBT], func=AF.Copy), pab, [prevcol.b[0]])

        sbst = {}

        def gen_first(sbi):
            own = sbi >= own0
            hTt, hTb = hT(sbi)
            for j in range(CPS):
                gc = sbi * CPS + j
                xtt, xtb = xt(gc)
                hbt, hbb = hb(gc)
                stt, stb = st0(gc)
                dma("sp", xtt[:], xw[gc * C:(gc + 1) * C, :], [], [xtb])
                rms_rstd(xtt[:], [xtb], stt, stb, hbt[:], hbb)
                S.add("dve", lambda e, xtt=xtt, stt=stt, hbt=hbt: e.scalar_tensor_tensor(
                    out=hbt[:], in0=xtt[:], scalar=stt[:, 2:3], in1=gpre.t[0][:], op0=ALU.mult, op1=ALU.mult),
                    [xtb, stb, gpre.b[0]], [hbb])
                yield
                pst = psT[0]
                pstb = psT_b[0]
                for k in range(8):
                    S.add("pe", lambda e, k=k, hbt=hbt, pst=pst: e.transpose(
                        out=pst[:, k * 128:(k + 1) * 128], in_=hbt[:, k * 128:(k + 1) * 128], identity=ident_b),
                        [hbb, CB], pstb)
                S.add("act", lambda e, pst=pst, hTt=hTt, j=j: e.activation(
                    out=hTt[:, :, j * C:(j + 1) * C], in_=pst[:, :].rearrange("p (k t) -> p k t", k=8), func=AF.Copy),
                    pstb, [hTb])
                yield
            proj_tile(sbi, 12, hTt, hTb)
            tw_ctx = stage2_header()
            sbst[sbi] = (tw_ctx, hTt, hTb)
            yield
            for hp in (0, 1):
                for q in range(3):
                    proj_tile(sbi, q * 4 + hp, hTt, hTb)
                    yield

        def gen_first_b(sbi):
            own = sbi >= own0
            tw_ctx, hTt, hTb = sbst[sbi]
            for hp in (0, 1):
                yield from prep_hp(hp, sbi, own, **tw_ctx)

        def gen_first_ab(sbi):
            yield from gen_first(sbi)
            yield from gen_first_b(sbi)

        def gen_second(sbi):
            own = sbi >= own0
            tw_ctx, hTt, hTb = sbst[sbi]
            for hp in (2, 3):
                for q in range(3):
                    proj_tile(sbi, q * 4 + hp, hTt, hTb)
                    yield
                yield from prep_hp(hp, sbi, own, **tw_ctx)

        def drain(g):
            for _ in g:
                pass

        def mkfill(g, n=1):
            def fill():
                for _ in range(n):
                    try:
                        next(g)
                    except StopIteration:
                        return
            return fill

        drain(gen_first_ab(0))
        for sbi in range(nsb):
            own = sbi >= own0
            halo_sb = (sbi == own0 - 1)
            hTt, hTb = hT(sbi)
            def gen_ownproj(sbi=sbi, own=own, halo_sb=halo_sb, hTt=hTt, hTb=hTb):
                if own or halo_sb:
                    ncols, col0 = (SBT, 0) if own else (C, SBT - C)
                    kcol = ((sbi - own0) * CPS + 1) * C if own else 0
                    for g in range(2):
                        wt, wb = ws_load([(lambda t: t[:, :, 0:64], wcols(O_K + g * 64, 64)), (lambda t: t[:, :, 64:128], wcols(O_K + g * 64, 64))])
                        pa, pab = proj_fm(hTt, hTb, wt, wb, ncols, col0)
                        for cc in range(ncols // C):
                            ks = ((kcol // C) + cc) % NKS
                            S.add("act", lambda e, pa=pa, g=g, ks=ks, cc=cc: e.activation(
                                out=KTatt.t[g][:, ks * C:(ks + 1) * C], in_=pa[:, cc * C:(cc + 1) * C], func=AF.Identity, bias=pc("bk", g)), pab + [PB], [KTatt.b[g]])
                        yield
                    wt, wb = ws_load([(lambda t: t[:, :, :], wcols(O_V))])
                    for c in (range(CPS) if own else [CPS - 1]):
                        lc1 = (sbi - own0) * CPS + c + 1 if own else 0
                        pa, pab = fullbank()
                        for k in range(8):
                            S.add("pe", lambda e, pa=pa, k=k, wt=wt, hTt=hTt, c=c: e.matmul(
                                pa[:, 0:128], lhsT=hTt[:, k, c * C:(c + 1) * C], rhs=wt[:, k, :], start=(k == 0), stop=(k == 7)), wb + [hTb], pab)
                        vp, vpb = Vpad(lc1)
                        S.add("dve", lambda e, vp=vp, pa=pa: e.tensor_tensor(out=vp[:, :, 0:64], in0=v3(pa[:, 0:128]), in1=v3(bvb.t[0][:, :]), op=ALU.add),
                              pab + [bvb.b[0]], [vpb])
                        S.add("pool", lambda e, vp=vp: e.tensor_copy(out=vp[:, :, 128:192], in_=vp[:, :, 0:64]), [vpb], [vpb])
                        yield
                if own:
                    for ct in range(4):
                        si = sbi * 4 + ct
                        wt, wb = ws_load([(lambda t: t[:, :, :], wcols(O_GR + ct * 128))])
                        pa, pab = proj_fm(hTt, hTb, wt, wb)
                        S.add("act", lambda e, pa=pa, si=si: e.activation(out=sgr(si)[0][:], in_=pa[:, 0:SBT], func=AF.Silu), pab, [sgr(si)[1]])
                        yield
                        wt, wb = ws_load([(lambda t: t[:, :, :], wcols(O_Q + ct * 128))])
                        pa, pab = proj_fm(hTt, hTb, wt, wb)
                        for hh in range(2):
                            qz, qzb = qTz(si * 2 + hh)
                            S.add("act", lambda e, pa=pa, qz=qz, ct=ct, hh=hh: e.activation(
                                out=qz[P_[hh], :], in_=pa[P_[hh], 0:SBT], func=AF.Identity, bias=ppt[P_[hh], PPI["bq"] + ct:PPI["bq"] + ct + 1]),
                                pab + [PB], [qzb])
                        yield
                        wt, wb = ws_load([(lambda t: t[:, :, :], wcols(O_GA + ct * 128))])
                        pa, pab = proj_fm(hTt, hTb, wt, wb)
                        S.add("act", lambda e, pa=pa, si=si: e.activation(out=sga(si)[0][:], in_=pa[:, 0:SBT], func=AF.Silu), pab, [sga(si)[1]])
                        yield

                yield

            def emit_chunk_pairs(c, pairs, fill, own=own, sbi=sbi):
                gch = sbi * CPS + c
                csl = slice(c * C, (c + 1) * C)
                pYbank, pYbb = psA[0], [psA_b[0]]
                def mkctx(hp):
                    si = sbi * 4 + hp
                    gi = gch * 4 + hp
                    x = dict(hp=hp, si=si, gi=gi)
                    x["AR"], x["ARb"] = AR(si)
                    x["BT"], x["BTb"] = BT(si)
                    x["KT"], x["KTb"] = KT(si)
                    x["vb"], x["vbb"] = vbf(si)
                    x["zts"] = [PZ(si * 2 + hh) for hh in range(2)]
                    x["tm"], x["tmb"] = tm(gi)
                    return x

                def g_transposes(x, c=c, csl=csl):
                    pt_ = psT[1][:, 0:512]
                    ptb = psT_b[1]
                    srcs = [(x["AR"][:, c, 0, :], x["ARb"]), (x["BT"][:, csl], x["BTb"]), (x["KT"][:, csl], x["KTb"]), (x["vb"][:, csl], x["vbb"])]
                    for q, (sap, sbf) in enumerate(srcs):
                        S.add("pe", lambda e, pt_=pt_, q=q, sap=sap: e.transpose(out=pt_[:, q * 128:(q + 1) * 128], in_=sap, identity=ident_b),
                              [sbf, CB], ptb)
                    tmt = x["tm"]
                    S.add("act", lambda e, tmt=tmt, pt_=pt_: e.activation(out=tmt[:], in_=pt_.rearrange("p (q t) -> p q t", q=4), func=AF.Copy),
                          ptb, [x["tmb"]])

                def g_sprod_pe(x, c=c, csl=csl):
                    x["pS1"], x["pS1b"] = single(4)
                    x["pS2"], x["pS2b"] = single(2)
                    ARt, BTt = x["AR"], x["BT"]
                    for hh in range(2):
                        zt, zb = x["zts"][hh]
                        S.add("pe", lambda e, pS=x["pS1"], zt=zt, ARt=ARt, hh=hh: e.matmul(
                            pS[:, hh * 128:(hh + 1) * 128], lhsT=zt[:, 1, csl], rhs=ARt[:, c, 0, :], start=True, stop=True), [zb, x["ARb"]], x["pS1b"])
                    for hh in range(2):
                        zt, zb = x["zts"][hh]
                        S.add("pe", lambda e, pS=x["pS1"], zt=zt, BTt=BTt, hh=hh: e.matmul(
                            pS[:, (2 + hh) * 128:(3 + hh) * 128], lhsT=zt[:, 0, csl], rhs=BTt[:, csl], start=True, stop=True), [zb, x["BTb"]], x["pS1b"])
                    for hh in range(2):
                        zt, zb = x["zts"][hh]
                        S.add("pe", lambda e, pS=x["pS2"], zt=zt, ARt=ARt, hh=hh: e.matmul(
                            pS[:, hh * 128:(hh + 1) * 128], lhsT=zt[:, 2, csl], rhs=ARt[:, c, 0, :], start=True, stop=True), [zb, x["ARb"]], x["pS2b"])

                def g_sprod_evac(x):
                    gi = x["gi"]
                    nm, nmb = NMt(gi * 2)
                    mk, mkb = Mak(gi)
                    S.add("dve", lambda e, nm=nm, pS=x["pS1"]: e.tensor_tensor(out=nm[:, :, :, :].rearrange("p a h t -> p (a h) t"), in0=v3(pS, 4), in1=mask4, op=ALU.mult),
                          x["pS1b"] + [CB], [nmb])
                    S.add("dve", lambda e, mk=mk, pS=x["pS2"]: e.tensor_tensor(out=mk[:], in0=v3(pS, 2), in1=mask4[:, 0:2, :], op=ALU.mult),
                          x["pS2b"] + [CB], [mkb])
                    x["nm"], x["nmb"], x["mk"], x["mkb"] = nm, nmb, mk, mkb

                def g_r_pe(x, c=c, csl=csl):
                    x["pR"], x["pRb"] = single(4)
                    ARt = x["AR"]
                    for a_ in range(2):
                        for hh in range(2):
                            zt, zb = x["zts"][hh]
                            S.add("pe", lambda e, pR=x["pR"], zt=zt, ARt=ARt, hh=hh, a_=a_: e.matmul(
                                pR[:, (a_ * 2 + hh) * 128:(a_ * 2 + hh + 1) * 128], lhsT=zt[:, 1 + a_, csl], rhs=ARt[:, c, 1, :], start=True, stop=True),
                                [zb, x["ARb"]], x["pRb"])

                def g_r_evac(x, c=c, csl=csl):
                    rbk, rbkb = RBK(x["gi"])
                    for a_ in range(2):
                        S.add("dve", lambda e, rbk=rbk, pR=x["pR"], a_=a_: e.tensor_tensor(out=rbk[:, a_, :, :], in0=v3(pR[:, a_ * 256:(a_ + 1) * 256], 2), in1=mle2, op=ALU.mult),
                              x["pRb"] + [CB], [rbkb])
                    x["rbk"], x["rbkb"] = rbk, rbkb

                def g_pv_pe(x, c=c, csl=csl):
                    x["pV"], x["pVb"] = single(1)
                    mk, tmt = x["mk"], x["tm"]
                    for hh in range(2):
                        S.add("pe", lambda e, pV=x["pV"], hh=hh, mk=mk, tmt=tmt: e.matmul(
                            pV[:, hh * 64:(hh + 1) * 64], lhsT=mk[:, hh, :], rhs=tmt[:, 3, hh * 64:(hh + 1) * 64], start=True, stop=True), [x["mkb"], x["tmb"]], x["pVb"])

                def g_x0(x, c=c, csl=csl):
                    Xt, Xb = Xtile(x["gi"] * 2)
                    tmt = x["tm"]
                    S.add("pool", lambda e, Xt=Xt, tmt=tmt: e.tensor_copy(out=Xt[:, :, 0, :], in_=tmt[:, 0, :].rearrange("p (h k) -> p h k", h=2)), [x["tmb"]], [Xb])
                    S.add("act", lambda e, Xt=Xt, pV=x["pV"]: e.activation(out=Xt[:, :, 1, :], in_=pV[:, 0:128].rearrange("p (h k) -> p h k", h=2), func=AF.Copy),
                          x["pVb"], [Xb])
                    x["X"], x["Xb"] = Xt, Xb

                def g_level_pe(x, lv):
                    nm, nmb, Xt, Xb = x["nm"], x["nmb"], x["X"], x["Xb"]
                    x["pX"], x["pXb"] = single(2)
                    for hh in range(2):
                        S.add("pe", lambda e, pX=x["pX"], hh=hh, nm=nm, Xt=Xt: e.matmul(
                            pX[:, hh * 128:(hh + 1) * 128], lhsT=nm[:, 0, hh, :], rhs=Xt[:, hh, :, :].rearrange("p a k -> p (a k)"), start=True, stop=True),
                            [nmb, Xb], x["pXb"])
                    if lv < 6:
                        x["pNM"], x["pNMb"] = single(4)
                        for hh in range(2):
                            S.add("pe", lambda e, pN=x["pNM"], hh=hh, nm=nm: e.matmul(
                                pN[:, hh * 128:(hh + 1) * 128], lhsT=nm[:, 1, hh, :], rhs=nm[:, 0, hh, :], start=True, stop=True), [nmb], x["pNMb"])
                        if lv < 5:
                            for hh in range(2):
                                S.add("pe", lambda e, pN=x["pNM"], hh=hh, nm=nm: e.matmul(
                                    pN[:, (2 + hh) * 128:(3 + hh) * 128], lhsT=nm[:, 0, hh, :], rhs=nm[:, 1, hh, :], start=True, stop=True), [nmb], x["pNMb"])

                def g_level_evac(x, lv):
                    gi = x["gi"]
                    Xt, Xb = x["X"], x["Xb"]
                    Xn, Xnb = Xtile(gi * 2 + lv + 1)
                    S.add("dve", lambda e, Xn=Xn, pX=x["pX"], Xt=Xt: e.tensor_tensor(
                        out=Xn[:, :, :, :].rearrange("p h a k -> p h (a k)"), in0=v3(pX), in1=Xt[:, :, :, :].rearrange("p h a k -> p h (a k)"), op=ALU.add),
                        x["pXb"] + [Xb], [Xnb])
                    x["X"], x["Xb"] = Xn, Xnb
                    if lv < 6:
                        nn, nnb = NMt(gi * 2 + lv + 1)
                        w = 4 if lv < 5 else 2
                        S.add("act", lambda e, nn=nn, pN=x["pNM"], w=w: e.activation(
                            out=nn[:, :, :, :].rearrange("p a h t -> p (a h) t")[:, 0:w, :], in_=v3(pN[:, 0:w * 128], w), func=AF.Copy), x["pNMb"], [nnb])
                        x["nm"], x["nmb"] = nn, nnb

                def g_state(x, c=c, csl=csl, own=own, pYbank=(pYbank if own else None), pYbb=(pYbb if own else None)):
                    gi, hp, si = x["gi"], x["hp"], x["si"]
                    Xt, Xb, tmt, tmb = x["X"], x["Xb"], x["tm"], x["tmb"]
                    ARt, ARb = x["AR"], x["ARb"]
                    wz, wzb = Wz(gi)
                    S.add("pool", lambda e, wz=wz, Xt=Xt: e.tensor_copy(out=wz[:, 0::2, :], in_=Xt[:, :, 0, :]), [Xb], [wzb])
                    wzA = wz[:, 0:2, :].rearrange("p a k -> p (a k)")
                    wzB = wz[:, 1:3, :].rearrange("p a k -> p (a k)")
                    wp, wpb = Wp(gi)
                    S.add("pool", lambda e, wp=wp, Xt=Xt: e.tensor_copy(out=wp[:, :, :], in_=Xt[:, :, 0, :]), [Xb], [wpb])
                    pAT, pATb = single(1)
                    S.add("pe", lambda e, pAT=pAT, wp=wp, tmt=tmt: e.matmul(pAT[:, 0:128], lhsT=wp[:, :, :].rearrange("p a k -> p (a k)"), rhs=tmt[:, 1, :], start=True, stop=True),
                          [wpb, tmb], pATb)
                    atb, atbb = ATbd(gi)
                    for hh in range(2):
                        S.add("act", lambda e, atb=atb, pAT=pAT, hh=hh: e.activation(
                            out=atb[P_[hh], hh * 64:(hh + 1) * 64], in_=pAT[P_[hh], hh * 64:(hh + 1) * 64], func=AF.Copy), pATb, [atbb])
                    pG, pGb = single(1)
                    pG2, pG2b = single(1)
                    S.add("pe", lambda e, pG=pG, tmt=tmt: e.matmul(pG[:, 0:128], lhsT=tmt[:, 2, :], rhs=tmt[:, 3, :], start=True, stop=True),
                          [tmb], pGb)
                    for hh in range(2):
                        S.add("pe", lambda e, pG2=pG2, Xt=Xt, tmt=tmt, hh=hh: e.matmul(
                            pG2[:, hh * 64:(hh + 1) * 64], lhsT=tmt[:, 1, :], rhs=Xt[:, hh, 1, :], start=True, stop=True), [Xb, tmb], pG2b)
                    gs, gsb_ = Gsb(gi)
                    for hh in range(2):
                        S.add("act", lambda e, gs=gs, pG=pG, hh=hh: e.activation(
                            out=gs[P_[hh], :], in_=pG[P_[hh], hh * 64:(hh + 1) * 64], func=AF.Copy), pGb, [gsb_])
                        S.add("dve", lambda e, gs=gs, pG2=pG2, hh=hh: e.tensor_tensor(
                            out=gs[P_[hh], :], in0=pG2[P_[hh], hh * 64:(hh + 1) * 64], in1=gs[P_[hh], :], op=ALU.add), pG2b + [gsb_], [gsb_])
                    Htt, Hb_ = Ht(hp)
                    gct, gcb = gC(si)
                    if own:
                        hbz = [Hbfz(gi * 2 + hh) for hh in range(2)]
                        for hh in range(2):
                            S.add("pool", lambda e, hz=hbz[hh][0], Htt=Htt, hh=hh: e.tensor_copy(out=hz[P_[hh], :], in_=Htt[P_[hh], :]), [Hb_], [hbz[hh][1]])
                    hhl, hhlb = Hhl(gi)
                    S.add("pool", lambda e, hhl=hhl, Htt=Htt: e.tensor_copy(out=hhl[:, 0, :], in_=Htt[:]), [Hb_], [hhlb])
                    S.add("pool", lambda e, hhl=hhl, Htt=Htt: e.tensor_tensor(out=hhl[:, 1, :], in0=Htt[:], in1=hhl[:, 0, :], op=ALU.subtract), [Hb_, hhlb], [hhlb])
                    pZ, pZb = single(1)
                    S.add("pe", lambda e, pZ=pZ, atb=atb, hhl=hhl: e.matmul(pZ[:, 0:128], lhsT=atb[:], rhs=hhl[:, :, :].rearrange("p a v -> p (a v)"), start=True, stop=True),
                          [atbb, hhlb], pZb)
                    s1, s1b = s1t(gi)
                    S.add("pool", lambda e, s1=s1, Htt=Htt, gs=gs: e.tensor_tensor(out=s1[:], in0=Htt[:], in1=gs[:], op=ALU.add), [Hb_, gsb_], [s1b])
                    S.add("pool", lambda e, s1=s1, gct=gct: e.tensor_scalar(out=s1[:], in0=s1[:], scalar1=gct[:, c:c + 1], scalar2=1.0, op0=ALU.mult, op1=ALU.mult),
                          [s1b, gcb], [s1b])
                    S.add("dve", lambda e, pZ=pZ, gct=gct, s1=s1: e.scalar_tensor_tensor(
                        out=s1[:], in0=pZ[:, 0:64], scalar=gct[:, c:c + 1], in1=s1[:], op0=ALU.mult, op1=ALU.add), pZb + [gcb, s1b], [s1b])
                    S.add("dve", lambda e, Htt=Htt, pZ=pZ, gct=gct, s1=s1: e.scalar_tensor_tensor(
                        out=Htt[:], in0=pZ[:, 64:128], scalar=gct[:, c:c + 1], in1=s1[:], op0=ALU.mult, op1=ALU.add), pZb + [gcb, s1b], [Hb_])
                    if own:
                        rbk, rbkb = x["rbk"], x["rbkb"]
                        qt_, qtb = QT(gi)
                        for hh, wzX in enumerate((wzA, wzB)):
                            pQ, pQb = single(1)
                            S.add("pe", lambda e, pQ=pQ, wzX=wzX, rbk=rbk, hh=hh: e.matmul(pQ[:, 0:128], lhsT=wzX, rhs=rbk[:, 0, hh, :], start=True, stop=True),
                                  [wzb, rbkb], pQb)
                            S.add("dve", lambda e, qt_=qt_, pQ=pQ, ARt=ARt, hh=hh: e.tensor_tensor(
                                out=qt_[P_[hh], :], in0=pQ[P_[hh], 0:128], in1=ARt[P_[hh], c, 1, :], op=ALU.add), pQb + [ARb], [qtb])
                        for hh in range(2):
                            pY, pYb = pYbank, pYbb
                            ysl = slice((hp * 2 + hh) * 64, (hp * 2 + hh + 1) * 64)
                            S.add("pe", lambda e, pY=pY, ysl=ysl, hh=hh, rbk=rbk, Xt=Xt: e.matmul(
                                pY[:, ysl], lhsT=rbk[:, 0, hh, :], rhs=Xt[:, hh, 1, :], start=True, stop=False), [rbkb, Xb], pYb)
                            S.add("pe", lambda e, pY=pY, ysl=ysl, hh=hh, rbk=rbk, tmt=tmt: e.matmul(
                                pY[:, ysl], lhsT=rbk[:, 1, hh, :], rhs=tmt[:, 3, hh * 64:(hh + 1) * 64], start=False, stop=False), [rbkb, tmb], pYb)
                            S.add("pe", lambda e, pY=pY, ysl=ysl, qt_=qt_, hz=hbz[hh][0]: e.matmul(
                                pY[:, ysl], lhsT=qt_[:, :], rhs=hz[:, :], start=False, stop=True), [qtb, hbz[hh][1]], pYb)

                for pr in pairs:
                    ctxs = [mkctx(hp) for hp in pr]
                    for x in ctxs:
                        g_transposes(x)
                    for x in ctxs:
                        g_sprod_pe(x)
                    for x in ctxs:
                        g_sprod_evac(x)
                    fill()
                    if own:
                        for x in ctxs:
                            g_r_pe(x)
                        for x in ctxs:
                            g_r_evac(x)
                    for x in ctxs:
                        g_pv_pe(x)
                    for x in ctxs:
                        g_x0(x)
                    fill()
                    for lv in range(7):
                        for x in ctxs:
                            g_level_pe(x, lv)
                        for x in ctxs:
                            g_level_evac(x, lv)
                        fill()
                    for x in ctxs:
                        g_state(x)
            if not own:
                if halo_sb:
                    drain(gen_ownproj())
                g2 = gen_second(sbi)
                for c in range(CPS):
                    emit_chunk_pairs(c, [PAIRS[0]], mkfill(g2, 2))
                drain(g2)
                g1 = gen_first_ab(sbi + 1) if sbi + 1 < nsb else iter(())
                for c in range(CPS):
                    emit_chunk_pairs(c, [PAIRS[1]], mkfill(g1, 2))
                drain(g1)
                continue
            fb_mode[0] = 1
            g1 = iter(())
            for c in range(CPS):
                gch = sbi * CPS + c
                csl = slice(c * C, (c + 1) * C)
                pYbank, pYbb = psA[0], [psA_b[0]]
                if c == 0:
                    g2 = gen_second(sbi)
                    emit_chunk_pairs(c, [PAIRS[0]], mkfill(g2, 2))
                    drain(g2)
                    g3 = gen_ownproj()
                    emit_chunk_pairs(c, [PAIRS[1]], mkfill(g3, 2))
                    drain(g3)
                else:
                    if c == 1 and sbi + 1 < nsb:
                        g1 = gen_first(sbi + 1)
                    emit_chunk_pairs(c, [PAIRS[0]], mkfill(g1, 1))
                    emit_chunk_pairs(c, [PAIRS[1]], mkfill(g1, 1))
                lc = (sbi - own0) * CPS + c
                g_t, g_b = gst()
                pY, pYb = pYbank, pYbb
                yq, yqb = ysqt(0)
                S.add("dve", lambda e, g_t=g_t, pY=pY: e.tensor_reduce(out=g_t[:, 0, :], in_=v3(pY[:, 0:512], 8), axis=AX.X, op=ALU.add), pYb, [g_b])
                S.add("act", lambda e, yq=yq, pY=pY: e.activation(out=yq[:], in_=pY[:, 0:512], func=AF.Square), pYb, [yqb])
                S.add("dve", lambda e, g_t=g_t, yq=yq: e.tensor_reduce(out=g_t[:, 1, :], in_=v3(yq[:, :], 8), axis=AX.X, op=ALU.add), [yqb], [g_b])
                S.add("dve", lambda e, g_t=g_t: e.tensor_scalar(out=g_t[:, 2, :], in0=g_t[:, 0, :], scalar1=1.0 / 64, scalar2=None, op0=ALU.mult), [g_b], [g_b])
                S.add("dve", lambda e, g_t=g_t: e.tensor_tensor(out=g_t[:, 3, :], in0=g_t[:, 2, :], in1=g_t[:, 2, :], op=ALU.mult), [g_b], [g_b])
                S.add("dve", lambda e, g_t=g_t: e.scalar_tensor_tensor(out=g_t[:, 4, :], in0=g_t[:, 1, :], scalar=1.0 / 64, in1=g_t[:, 3, :],
                                                                       op0=ALU.mult, op1=ALU.subtract), [g_b], [g_b])
                S.add("act", lambda e, g_t=g_t: e.activation(out=g_t[:, 5, :], in_=g_t[:, 4, :], func=AF.Ln, bias=gneps_col), [g_b, KB], [g_b])
                S.add("act", lambda e, g_t=g_t: e.activation(out=g_t[:, 5, :], in_=g_t[:, 5, :], func=AF.Exp, scale=-0.5), [g_b], [g_b])
                ynt, ynb = yn()
                S.add("dve", lambda e, yq=yq, pY=pY, g_t=g_t: e.tensor_tensor(
                    out=v3(yq[:, :], 8), in0=v3(pY[:, 0:512], 8), in1=bcl(g_t[:, 2, :], 64), op=ALU.subtract), pYb + [g_b, yqb], [yqb])
                S.add("pool", lambda e, ynt=ynt, yq=yq, g_t=g_t: e.tensor_tensor(
                    out=v3(ynt[:, :], 8), in0=v3(yq[:, :], 8), in1=bcl(g_t[:, 5, :], 64), op=ALU.mult), [yqb, g_b], [ynb])
                pt_ = psT[1][:, 0:512]
                ptb = psT_b[1]
                for hp in range(4):
                    S.add("pe", lambda e, pt_=pt_, hp=hp, ynt=ynt: e.transpose(out=pt_[:, hp * 128:(hp + 1) * 128], in_=ynt[:, hp * 128:(hp + 1) * 128], identity=ident_b),
                          [ynb, CB], ptb)
                for hp in range(4):
                    si = sbi * 4 + hp
                    t1, t1b = t1t(hp)
                    S.add("dve", lambda e, t1=t1, pt_=pt_, hp=hp: e.tensor_scalar(out=t1[:], in0=pt_[:, hp * 128:(hp + 1) * 128], scalar1=pc("gnw", hp), scalar2=pc("gnb", hp),
                                                                            op0=ALU.mult, op1=ALU.add), ptb + [PB], [t1b])
                    S.add("pool", lambda e, t1=t1, si=si, csl=csl: e.tensor_tensor(out=t1[:], in0=t1[:], in1=bonus(si)[0][:, csl], op=ALU.add), [t1b, bonus(si)[1]], [t1b])
                    S.add("pool", lambda e, t1=t1, si=si, csl=csl: e.tensor_tensor(out=zr(si)[0][:, csl], in0=t1[:], in1=sgr(si)[0][:, csl], op=ALU.mult),
                          [t1b, sgr(si)[1]], [zr(si)[1]])
                if lc == NOC - 1:
                    dump("zr0", zr(sbi * 4)[0][:], [128, SBT], [zr(sbi * 4)[1]], BF16)
                if upto < 4:
                    continue
                am_i = 0 if lc == 0 else 1
                pObank, pObb = psA[0], [psA_b[0]]
                vprev, vprevb = Vpad(lc)
                vcur, vcurb = Vpad(lc + 1)
                for qp0 in (0, 2):
                    pts = psT[0]
                    hs = []
                    for qp in (qp0, qp0 + 1):
                        si = sbi * 4 + qp
                        g = qp // 2
                        for hh in range(2):
                            hd = qp * 2 + hh
                            pS, pSb_ = single(2)
                            qz, qzb = qTz(si * 2 + hh)
                            for kk_ in range(2):
                                ks = (lc + kk_) % NKS
                                S.add("pe", lambda e, pS=pS, qz=qz, g=g, ks=ks, kk_=kk_, csl=csl: e.matmul(
                                    pS[:, kk_ * 128:(kk_ + 1) * 128], lhsT=qz[:, csl], rhs=KTatt.t[g][:, ks * C:(ks + 1) * C], start=True, stop=True),
                                    [qzb, KTatt.b[g]], pSb_)
                            j4 = (qp - qp0) * 2 + hh
                            hs.append(dict(hd=hd, qp=qp, hh=hh, g=g, si=si, pS=pS, pSb=pSb_, sm=smt(j4), a=ast(j4), p3=p32(j4), pn=pnt(j4), pt=ptt(j4),
                                           ptsl=pts[:, j4 * 256:(j4 + 1) * 256]))
                    for h in hs:
                        S.add("dve", lambda e, sm=h["sm"][0], pS=h["pS"], am_i=am_i: e.scalar_tensor_tensor(
                            out=sm[:], in0=pS[:, 0:256], scalar=0.125, in1=amask.t[0][:, am_i, :], op0=ALU.mult, op1=ALU.add), h["pSb"] + [amask.b[0]], [h["sm"][1]])
                    for h in hs:
                        S.add("dve", lambda e, a_t=h["a"][0], sm=h["sm"][0]: e.tensor_reduce(out=a_t[:, 0:1], in_=sm[:], axis=AX.X, op=ALU.max), [h["sm"][1]], [h["a"][1]])
                    for h in hs:
                        S.add("dve", lambda e, a_t=h["a"][0], hd=h["hd"]: e.tensor_scalar(out=a_t[:, 1:2], in0=a_t[:, 0:1], scalar1=pc("sink", hd), scalar2=-1.0, op0=ALU.max, op1=ALU.mult),
                              [h["a"][1], PB], [h["a"][1]])
                    for h in hs:
                        S.add("act", lambda e, pp3=h["p3"][0], sm=h["sm"][0], a_t=h["a"][0]: e.activation(out=pp3[:], in_=sm[:], func=AF.Exp, bias=a_t[:, 1:2], accum_out=a_t[:, 2:3]),
                              [h["sm"][1], h["a"][1]], [h["p3"][1], h["a"][1]])
                    for h in hs:
                        S.add("act", lambda e, a_t=h["a"][0], hd=h["hd"]: e.activation(out=a_t[:, 3:4], in_=pc("sink", hd), func=AF.Exp, bias=a_t[:, 1:2]), [h["a"][1], PB], [h["a"][1]])
                    for h in hs:
                        S.add("dve", lambda e, a_t=h["a"][0]: e.tensor_tensor(out=a_t[:, 4:5], in0=a_t[:, 2:3], in1=a_t[:, 3:4], op=ALU.add), [h["a"][1]], [h["a"][1]])
                    for h in hs:
                        S.add("dve", lambda e, a_t=h["a"][0]: e.reciprocal(out=a_t[:, 5:6], in_=a_t[:, 4:5]), [h["a"][1]], [h["a"][1]])
                    for h in hs:
                        S.add("dve", lambda e, pn=h["pn"][0], pp3=h["p3"][0], a_t=h["a"][0]: e.tensor_scalar(out=pn[:], in0=pp3[:], scalar1=a_t[:, 5:6], scalar2=None, op0=ALU.mult),
                              [h["p3"][1], h["a"][1]], [h["pn"][1]])
                    for h in hs:
                        for kk_ in range(2):
                            S.add("pe", lambda e, ptsl=h["ptsl"], kk_=kk_, pn=h["pn"][0]: e.transpose(out=ptsl[:, kk_ * 128:(kk_ + 1) * 128], in_=pn[:, kk_ * 128:(kk_ + 1) * 128], identity=ident_b),
                                  [h["pn"][1], CB], psT_b[0])
                    for h in hs:
                        S.add("act", lambda e, pt2=h["pt"][0], ptsl=h["ptsl"]: e.activation(out=pt2[:], in_=v3(ptsl), func=AF.Copy), psT_b[0], [h["pt"][1]])
                    for qp in (qp0, qp0 + 1):
                        si = sbi * 4 + qp
                        g = qp // 2
                        pO, pOb = pObank[:, qp * 128:(qp + 1) * 128], pObb
                        n_ = 0
                        for h in [h for h in hs if h["qp"] == qp]:
                            pt2, pt2b = h["pt"]
                            hh = h["hh"]
                            for kk_, (vp, vpb) in enumerate([(vprev, vprevb), (vcur, vcurb)]):
                                S.add("pe", lambda e, pO=pO, vp=vp, g=g, hh=hh, pt2=pt2, kk_=kk_, n_=n_: e.matmul(
                                    pO[:, 0:128], lhsT=vp[:, g, hh * 64:hh * 64 + 128], rhs=pt2[:, kk_, :], start=(n_ == 0), stop=(n_ == 3)),
                                    [vpb, pt2b], pOb)
                                n_ += 1
                    for qp in (qp0, qp0 + 1):
                        si = sbi * 4 + qp
                        pO, pOb = pObank[:, qp * 128:(qp + 1) * 128], pObb
                        S.add("dve", lambda e, si=si, pO=pO, csl=csl: e.tensor_tensor(out=zatt(si)[0][:, csl], in0=pO[:, 0:128], in1=sga(si)[0][:, csl], op=ALU.mult),
                              pOb + [sga(si)[1]], [zatt(si)[1]])
                if lc == NOC - 1:
                    dump("za0", zatt(sbi * 4)[0][:], [128, SBT], [zatt(sbi * 4)[1]], BF16)
            if not own or upto < 5:
                continue
            drain(g1)
            fb_mode[0] = 0
            mTt, mTb = mT()
            def load_j(j):
                return [ws_load([(lambda t: t[:, :, :], w_br[:, :, j * 128:(j + 1) * 128].rearrange("b (h p) c -> p (b h) c", p=128))]),
                        ws_load([(lambda t: t[:, :, :], wcols(O_GT + j * 128))]),
                        ws_load([(lambda t: t[:, :, :], wcols(O_GT + 1024 + j * 128))])]
            for j in range(8):
                cur_w = load_j(j)
                wt, wb = cur_w[0]
                pBr, pBrb = fullbank()
                pBa, pBab = fullbank()
                for hp in range(4):
                    si = sbi * 4 + hp
                    S.add("pe", lambda e, pBr=pBr, wt=wt, hp=hp, si=si: e.matmul(pBr[:, 0:SBT], lhsT=wt[:, hp, :], rhs=zr(si)[0][:], start=(hp == 0), stop=(hp == 3)),
                          wb + [zr(si)[1]], pBrb)
                for hp in range(4):
                    si = sbi * 4 + hp
                    S.add("pe", lambda e, pBa=pBa, wt=wt, hp=hp, si=si: e.matmul(pBa[:, 0:SBT], lhsT=wt[:, 4 + hp, :], rhs=zatt(si)[0][:], start=(hp == 0), stop=(hp == 3)),
                          wb + [zatt(si)[1]], pBab)
                halves = []
                for br in range(2):
                    wt2, wb2 = cur_w[1 + br]
                    pGt, pGtb = bankx()
                    for k in range(8):
                        S.add("pe", lambda e, pGt=pGt, k=k, wt2=wt2, hTt=hTt: e.matmul(
                            pGt[:, 0:SBT], lhsT=wt2[:, k, :], rhs=hTt[:, k, :], start=(k == 0), stop=(k == 7)), wb2 + [hTb], pGtb)
                    sg_, sgb_ = sgt(br)
                    S.add("act", lambda e, sg_=sg_, pGt=pGt: e.activation(out=sg_[:], in_=pGt[:, 0:SBT], func=AF.Sigmoid), pGtb, [sgb_])
                    halves.append((sg_, sgb_))
                m1, m1b = m12(0)
                m2, m2b = m12(1)
                S.add("dve", lambda e, m1=m1, pBr=pBr, sg_=halves[0][0]: e.tensor_tensor(out=m1[:], in0=pBr[:, 0:SBT], in1=sg_[:], op=ALU.mult), pBrb + [halves[0][1]], [m1b])
                S.add("dve", lambda e, m2=m2, pBa=pBa, sg_=halves[1][0]: e.tensor_tensor(out=m2[:], in0=pBa[:, 0:SBT], in1=sg_[:], op=ALU.mult), pBab + [halves[1][1]], [m2b])
                S.add("dve", lambda e, mTt=mTt, j=j, m1=m1, m2=m2: e.tensor_tensor(out=mTt[:, j, :], in0=m1[:], in1=m2[:], op=ALU.add), [m1b, m2b], [mTb])
            for c in range(CPS):
                gch = sbi * CPS + c
                lc = (sbi - own0) * CPS + c
                xr, xrb = xt(gch)
                dma("sp", xr[:], xw[gch * C:(gch + 1) * C, :], [], [xrb])
                for n in range(2):
                    pa, pab = fullbank()
                    for j in range(8):
                        S.add("pe", lambda e, pa=pa, j=j, n=n, mTt=mTt, c=c: e.matmul(
                            pa[:, 0:512], lhsT=mTt[:, j, c * C:(c + 1) * C], rhs=Wout.t[0][:, j, n * 512:(n + 1) * 512], start=(j == 0), stop=(j == 7)),
                            [mTb, Wout_b[j]], pab)
                    S.add("dve", lambda e, xr=xr, pa=pa, n=n: e.tensor_tensor(out=xr[:, n * 512:(n + 1) * 512], in0=pa[:, 0:512], in1=xr[:, n * 512:(n + 1) * 512], op=ALU.add),
                          pab + [xrb], [xrb])
                ft_, fb_ = fst(gch)
                hbt, hbb = hb(gch)
                rms_rstd(xr[:], [xrb], ft_, fb_, hbt[:], hbb)
                S.add("dve", lambda e, xr=xr, ft_=ft_: e.scalar_tensor_tensor(
                    out=xr[:], in0=xr[:], scalar=ft_[:, 2:3], in1=gfin.t[0][:], op0=ALU.mult, op1=ALU.mult), [xrb, fb_, gfin.b[0]], [xrb])
                dma("sp", out_d[lc * C:(lc + 1) * C, :], xr[:], [xrb], [], is_out=True)
            if sbi + 1 < nsb:
                drain(gen_first_b(sbi + 1))

        for hp in range(4):
            dump(f"H{hp}", Ht(hp)[0][:], [128, 64], [Ht(hp)[1]])

        semnames = list(Sched.ENG) + [("dma", j) for j in range(Sched.NDMA)]
        sems = {}
        for sk in semnames:
            nm = sk if isinstance(sk, str) else f"dma{sk[1]}"
            sems[sk] = es.enter_context(nc.semaphore("s_" + nm))
        nc._sbuf_left = nc.sbuf_bytes_remaining
        block = es.enter_context(nc.Block())
        S.emit(nc, block, sems)
    nc._dbg_dumps = dump_d
    nc._sched_counts = dict(S.cnt)
    nc._sched_total = S.total
    return nc


def host_consts():
    s = np.arange(128)[:, None]
    t = np.arange(128)[None, :]
    cst = np.zeros((128, 9, 128), np.float32)
    cst[:, 0] = (s == t)
    cst[:, 1] = (s < t)
    cst[:, 2] = (s < t)
    cst[:, 3] = (s > t)
    cst[:, 4] = (s > t)
    cst[:, 5] = (s <= t)
    cst[:, 6] = (s <= t)
    cst[:, 7] = ((s // 64) == (t // 64))
    cst[:, 8] = 1.0
    return cst


def attn_masks(first):
    qi = np.arange(128)[:, None]
    kj = np.arange(256)[None, :]
    dist = qi + 128 - kj
    band = (dist >= 0) & (dist < 128)
    rest = np.where(band, 0.0, -1e30).astype(np.float32)
    fm = np.where(band & (kj >= 128), 0.0, -1e30).astype(np.float32)
    am = np.stack([fm if first else rest, rest], axis=1)
    return np.ascontiguousarray(am)


def pack_params(p):
    pp = np.zeros((128, NPP_IN), np.float32)

    def put(name, vec, n):
        v = np.asarray(vec, np.float32).reshape(n, 128)
        pp[:, PPI[name]:PPI[name] + n] = v.T
    put("mu", p["mu_shift"][0], 13)
    put("w0", p["w0"][0], 4)
    put("a0", p["a0"][0], 4)
    put("kk", p["k_k"][0], 4)
    put("ka", p["k_a"][0], 4)
    put("rk", p["r_k"][0], 4)
    put("gnw", p["gn_w"][0], 4)
    put("gnb", p["gn_b"][0], 4)
    bq = np.asarray(p["b_qkv"][0], np.float32)
    put("bq", bq[0:512], 4)
    bk = bq[512:640]
    pp[:, PPI["bk"] + 0] = np.concatenate([bk[0:64], bk[0:64]])
    pp[:, PPI["bk"] + 1] = np.concatenate([bk[64:128], bk[64:128]])
    sk = np.asarray(p["sinks"][0], np.float32)
    pp[:, PPI["sink"]:PPI["sink"] + 8] = np.broadcast_to(sk[None, :], (128, 8))
    wdi = np.zeros((128, 2, 512), np.float32)
    wdi[0:64, 0] = np.asarray(p["w_decay_up"][0], np.float32)
    wdi[64:128, 1] = np.asarray(p["w_iclr_up"][0], np.float32)
    common = {
        "w_in": np.ascontiguousarray(np.asarray(p["w_in"][0], np.float32)),
        "w_br": np.ascontiguousarray(np.stack([np.asarray(p["w_branch_rwkv"][0], np.float32),
                                               np.asarray(p["w_branch_att"][0], np.float32)])),
        "w_out": np.ascontiguousarray(np.asarray(p["w_out"][0], np.float32)),
        "wdi": np.ascontiguousarray(wdi),
        "pp": pp,
        "gpre_b": np.ascontiguousarray(np.broadcast_to(np.asarray(p["g_pre"][0], np.float32)[None], (128, D))),
        "gfin_b": np.ascontiguousarray(np.broadcast_to(np.asarray(p["g_final"], np.float32)[None], (128, D))),
        "bv_b": np.ascontiguousarray(np.broadcast_to(bq[640:768][None], (128, 128))),
        "cst": host_consts(),
    }
    return common


def kernel(**inputs):
    x = np.asarray(inputs["x"], np.float32)
    common = pack_params(inputs)
    nc = build()
    in_maps = []
    for c in range(NCORES):
        b, q = c // 4, c % 4
        end = (q + 1) * OWN_TOK
        xw = np.zeros((SEQ, D), np.float32)
        xw[SEQ - end:] = x[b, :end]
        m = dict(common)
        m["xw"] = xw
        m["amask"] = attn_masks(q == 0)
        in_maps.append(m)
    res = run_bass_kernel_spmd(nc, in_maps, core_ids=list(range(NCORES)))
    out = np.zeros((2, SEQ, D), np.float32)
    for c in range(NCORES):
        b, q = c // 4, c % 4
        out[b, q * OWN_TOK:(q + 1) * OWN_TOK] = res.results[c]["out"]
    return out
```

```python
import numpy as np
import concourse.bass as bass
import concourse.mybir as mybir
from concourse.bass_utils import run_bass_kernel_spmd

F32 = mybir.dt.float32
BF16 = mybir.dt.bfloat16
AF = mybir.ActivationFunctionType
ALU = mybir.AluOpType
AX = mybir.AxisListType

D = 1024
NCORES = 8
SEQ = 8192
OWN_TOK = 2048
C = 128
SBT = 256
CPS = SBT // C
RMS_EPS = 1e-6
GN_EPS = 64e-5
IN_COLS = 5504
O_SH = 0
O_GR = 1664
O_Q = 2176
O_K = 2688
O_V = 2816
O_GA = 2944
O_GT = 3456

PPI = {}
_n = 0
for _name, _cnt in [("mu", 13), ("w0", 4), ("a0", 4), ("kk", 4), ("ka", 4), ("rk", 4),
                    ("gnw", 4), ("gnb", 4), ("bq", 4), ("bk", 2), ("sink", 8)]:
    PPI[_name] = _n
    _n += _cnt
NPP_IN = _n
for _name, _cnt in [("omu", 13), ("nw0", 4), ("omka", 4), ("na0", 4)]:
    PPI[_name] = _n
    _n += _cnt
NPP = _n


class Buf:
    __slots__ = ("name", "w", "r", "excl")

    def __init__(self, name, excl=False):
        self.name = name
        self.w = None
        self.r = []
        self.excl = excl


class Sched:
    ENG = ("pe", "act", "dve", "pool", "sp")
    NDMA = 24

    def __init__(self, same_sync=True):
        self.ops = {e: [] for e in self.ENG}
        self.cnt = {e: 0 for e in self.ENG}
        self.waited = {e: {} for e in self.ENG}
        self.same_sync = same_sync
        self.dma_val = [0] * self.NDMA
        self.dma_rr = 0
        self.dma_rr2 = 0
        self.out_tokens = []

    def add(self, eng, fn, reads=(), writes=(), dma=False, is_out=False):
        self.total = getattr(self, "total", 0) + 1
        if not dma and self.total > getattr(self, "cut", 10 ** 9):
            return None
        deps = {}

        def need(tk, hard):
            d = deps.get(tk[0])
            if d is None:
                deps[tk[0]] = [tk[1], tk[2], hard]
            else:
                d[0] = max(d[0], tk[1])
                d[2] = d[2] or hard
        for b in reads:
            if b.w is not None:
                need(b.w, True)
            if b.excl:
                for tk in b.r:
                    need(tk, False)
        for b in writes:
            if b.w is not None:
                need(b.w, True)
            for tk in b.r:
                need(tk, False)
        waits = []
        for semkey, (val, src, hard) in deps.items():
            if src == eng and not isinstance(semkey, tuple):
                if eng in ("pe", "sp"):
                    continue
                if not hard or not self.same_sync:
                    continue
            if self.waited[eng].get(semkey, 0) >= val:
                continue
            self.waited[eng][semkey] = val
            waits.append((semkey, val))
        if dma:
            half = self.NDMA // 2
            if eng == "sp":
                j = self.dma_rr
                self.dma_rr = (self.dma_rr + 1) % half
            else:
                j = half + self.dma_rr2
                self.dma_rr2 = (self.dma_rr2 + 1) % half
            semkey = ("dma", j)
            if self.dma_val[j] > 0 and self.waited[eng].get(semkey, 0) < self.dma_val[j]:
                self.waited[eng][semkey] = self.dma_val[j]
                waits.append((semkey, self.dma_val[j]))
            self.dma_val[j] += 16
            tok = (semkey, self.dma_val[j], eng)
            inc = 16
        else:
            self.cnt[eng] += 1
            tok = (eng, self.cnt[eng], eng)
            inc = 1
        for b in reads:
            b.r.append(tok)
        for b in writes:
            b.w = tok
            b.r = []
        if is_out:
            self.out_tokens.append(tok)
        self.ops[eng].append((waits, fn, tok[0], inc))
        return tok

    def emit(self, nc, block, sems):
        engmap = {"pe": block.tensor, "act": block.scalar, "dve": block.vector,
                  "pool": block.gpsimd, "sp": block.sync}
        for e in self.ENG:
            ops = self.ops[e]
            final = list(self.out_tokens) if e == "sp" else ()

            def body(eng, ops=ops, final=final):
                for waits, fn, semkey, inc in ops:
                    for sk, val in waits:
                        eng.wait_ge(sems[sk], val)
                    fn(eng).then_inc(sems[semkey], inc)
                for tk in final:
                    eng.wait_ge(sems[tk[0]], tk[1])
                if final != ():
                    for j in range(self.NDMA):
                        if self.dma_val[j] > 0:
                            eng.wait_ge(sems[("dma", j)], self.dma_val[j])
            engmap[e](body)


def build(nsb=SEQ // SBT, nown=OWN_TOK // SBT, upto=99, dumps=(), same_sync=True, cut=None):
    from contextlib import ExitStack
    nc = bass.Bass("TRN2", target_bir_lowering=False)
    WT = nsb * SBT
    OT = nown * SBT
    NOC = nown * CPS
    S = Sched(same_sync=same_sync)
    if cut is not None:
        S.cut = cut

    def din(name, shape, dt=F32):
        return nc.dram_tensor(name, list(shape), dt, kind="ExternalInput").ap()

    xw = din("xw", [WT, D])
    w_in = din("w_in", [D, IN_COLS])
    w_br = din("w_br", [2, 512, D])
    w_out = din("w_out", [D, D])
    wdi = din("wdi", [128, 2, 512])
    pp_in = din("pp", [128, NPP_IN])
    gpre_d = din("gpre_b", [128, D])
    gfin_d = din("gfin_b", [128, D])
    bv_d = din("bv_b", [128, 128])
    cst_d = din("cst", [128, 9, 128])
    am_d = din("amask", [128, 2, 256])
    out_d = nc.dram_tensor("out", [OT, D], F32, kind="ExternalOutput").ap()
    dump_d = {}

    es = ExitStack()
    with es:
        def sb(name, shape, dt=F32):
            return es.enter_context(nc.sbuf_tensor(name, list(shape), dt))

        def ps(name, shape, dt=F32):
            return es.enter_context(nc.psum_tensor(name, list(shape), dt))

        class T:
            def __init__(self, name, shape, dt=F32, n=1):
                self.t = [sb(f"{name}{i}", shape, dt) for i in range(n)]
                self.b = [Buf(f"{name}{i}") for i in range(n)]
                self.n = n

            def __call__(self, i=0):
                return self.t[i % self.n], self.b[i % self.n]

        def dma(eng, out, in_, reads, writes, is_out=False):
            return S.add(eng, lambda e: e.dma_start(out=out, in_=in_), reads, writes, dma=True, is_out=is_out)

        def dump(name, ap, shape, reads, dt=F32):
            if name not in dumps:
                return
            dd = nc.dram_tensor("dbg_" + name, list(shape), dt, kind="ExternalOutput").ap()
            dump_d[name] = dd
            dma("sp", dd, ap, reads, [], is_out=True)

        xt = T("xt", [128, D], F32, 2)
        cst_b = T("cst_b", [128, 8, 128], BF16)
        cst2 = T("cst2", [128, 1, 128])
        amask = T("amask", [128, 2, 256])
        PP = T("PP", [128, NPP])
        gpre = T("gpre", [128, D])
        gfin = T("gfin", [128, D])
        bvb = T("bvb", [128, 128])
        Wdb = T("Wdb", [128, 3, 512], BF16)
        Wsh = T("Wsh", [128, 8, 1664], BF16)
        Wout = T("Wout", [128, 8, D], BF16)

        stg = xt.t[1][:, :].rearrange("p (a b) -> p a b", a=8)
        dma("sp", stg, cst_d[:, 0:8, :], [], [xt.b[1]])
        dma("sp", cst2.t[0][:], cst_d[:, 8:9, :], [], [cst2.b[0]])
        dma("sp", PP.t[0][:, 0:NPP_IN], pp_in, [], [PP.b[0]])
        dma("sp", gpre.t[0][:], gpre_d, [], [gpre.b[0]])
        stg_w = xt.t[0][:, :].rearrange("p (a b) -> p a b", a=2)
        dma("sp", stg_w, wdi, [], [xt.b[0]])
        S.add("act", lambda e: e.activation(out=Wdb.t[0][:, 0, :], in_=stg_w[:, 0, :], func=AF.Copy), [xt.b[0]], [Wdb.b[0]])
        S.add("act", lambda e: e.activation(out=Wdb.t[0][:, 2, :], in_=stg_w[:, 1, :], func=AF.Copy), [xt.b[0]], [Wdb.b[0]])
        S.add("dve", lambda e: e.tensor_tensor(out=Wdb.t[0][:, 1, :], in0=stg_w[:, 0, :], in1=Wdb.t[0][:, 0, :], op=ALU.subtract), [xt.b[0], Wdb.b[0]], [Wdb.b[0]])
        Wsh_b = [Buf(f"Wsh_k{k}") for k in range(8)]
        for k in range(8):
            S.add("pool", lambda e, k=k: e.dma_start(out=Wsh.t[0][:, k, :], in_=w_in[k * 128:(k + 1) * 128, O_SH:O_SH + 1664]),
                  [], [Wsh_b[k]], dma=True)
        dma("sp", amask.t[0][:], am_d, [], [amask.b[0]])
        dma("sp", gfin.t[0][:], gfin_d, [], [gfin.b[0]])
        dma("sp", bvb.t[0][:], bv_d, [], [bvb.b[0]])
        S.add("dve", lambda e: e.tensor_copy(out=cst_b.t[0][:], in_=stg), [xt.b[1]], [cst_b.b[0]])
        ident_b = cst_b.t[0][:, 0, :]
        mask4 = cst_b.t[0][:, 1:5, :]
        mle2 = cst_b.t[0][:, 5:7, :]
        bones_b = cst_b.t[0][:, 7, :]
        ones_f = cst2.t[0][:, 0, :]
        CB = cst_b.b[0]
        CF = cst2.b[0]
        ppt = PP.t[0]
        PB = PP.b[0]

        def pc(name, i=0):
            j = PPI[name] + i
            return ppt[:, j:j + 1]

        S.add("dve", lambda e: e.tensor_scalar(out=ppt[:, PPI["omu"]:PPI["omu"] + 13], in0=ppt[:, PPI["mu"]:PPI["mu"] + 13],
                                               scalar1=-1.0, scalar2=1.0, op0=ALU.mult, op1=ALU.add), [PB], [PB])
        S.add("dve", lambda e: e.tensor_scalar(out=ppt[:, PPI["nw0"]:PPI["nw0"] + 4], in0=ppt[:, PPI["w0"]:PPI["w0"] + 4],
                                               scalar1=-1.0, scalar2=None, op0=ALU.mult), [PB], [PB])
        S.add("dve", lambda e: e.tensor_scalar(out=ppt[:, PPI["omka"]:PPI["omka"] + 4], in0=ppt[:, PPI["ka"]:PPI["ka"] + 4],
                                               scalar1=-1.0, scalar2=1.0, op0=ALU.mult, op1=ALU.add), [PB], [PB])
        S.add("dve", lambda e: e.tensor_scalar(out=ppt[:, PPI["na0"]:PPI["na0"] + 4], in0=ppt[:, PPI["a0"]:PPI["a0"] + 4],
                                               scalar1=-1.0, scalar2=None, op0=ALU.mult), [PB], [PB])

        psA = [ps(f"psA{i}", [128, 512]) for i in range(2)]
        psA_b = [Buf(f"psA{i}", True) for i in range(2)]
        psT = [ps(f"psT{i}", [128, 1024], BF16) for i in range(2)]
        psT_b = [[Buf(f"psT{i}_{h}", True) for h in range(2)] for i in range(2)]
        psLU = [[ps(f"psL{i}", [128, 512]), ps(f"psU{i}", [128, 512])] for i in range(2)]
        psLU_b = [[[Buf(f"psLU{i}_{lu}_{s}", True) for s in range(4)] for lu in range(2)] for i in range(2)]
        arr = [0]
        prr = [0]
        srr = [0]

        fb_mode = [0]

        def fullbank():
            if fb_mode[0]:
                return psA[1], [psA_b[1]]
            i = arr[0]
            arr[0] = (i + 1) % 2
            return psA[i], [psA_b[i]]

        def pair(ns):
            r = prr[0]
            if (r % 4) + ns > 4:
                r = (r // 4 + 1) * 4
            r %= 8
            p, s = r // 4, r % 4
            prr[0] = (r + ns) % 8
            sl = slice(s * 128, (s + ns) * 128)
            return (psLU[p][0][:, sl], psLU_b[p][0][s:s + ns], psLU[p][1][:, sl], psLU_b[p][1][s:s + ns])

        brr = [0]

        def bankx():
            i = brr[0]
            brr[0] = (i + 1) % 4
            p, lu = i // 2, i % 2
            return psLU[p][lu], list(psLU_b[p][lu])

        def single(ns):
            bk, bb = bankx()
            return bk[:, 0:ns * 128], bb

        hb = T("hb", [128, D], BF16, 1)
        hT = T("hT", [128, 8, SBT], BF16, 2)
        st0 = T("st0", [128, 4], F32, 2)
        shwa = T("shwa", [128, SBT])
        shtmp = T("shtmp", [128, SBT], F32, 1)
        shr = T("shr", [128, SBT], F32, 2)
        shk = T("shk", [128, SBT], F32, 2)
        shv = T("shv", [128, SBT], F32, 2)
        tw = T("tw", [128, SBT])
        tw_hi = T("tw_hi", [128, SBT], BF16)
        tw_lo = T("tw_lo", [128, SBT], BF16)
        t_k2b = T("t_k2b", [128, SBT], BF16)
        t_rkb = T("t_rkb", [128, SBT], BF16)
        Hhl = T("Hhl", [128, 2, 64], BF16, 4)
        t_e1 = T("t_e1", [128, SBT])
        t_ew = T("t_ew", [128, SBT])
        t_a = T("t_a", [128, SBT])
        t_cs = T("t_cs", [128, SBT])
        t_csp = T("t_csp", [128, SBT])
        t_en = T("t_en", [128, SBT])
        t_ep = T("t_ep", [128, SBT])
        t_k2 = T("t_k2", [128, SBT])
        t_kkn = T("t_kkn", [128, SBT])
        t_ab = T("t_ab", [128, SBT])
        t_f = T("t_f", [128, SBT])
        gC = T("gC", [128, CPS], F32, 8)
        AR = T("AR", [128, CPS, 2, C], BF16, 4)
        BT = T("BT", [128, SBT], BF16, 4)
        KT = T("KT", [128, SBT], BF16, 4)
        vbf = T("vbf", [128, SBT], BF16, 4)
        bonus = T("bonus", [128, SBT], BF16, 4)
        tm = T("tm", [128, 4, 128], BF16, 4)
        PZ = T("PZ", [128, 3, SBT], BF16, 8)
        Hbfz = T("Hbfz", [128, 64], BF16, 8)
        qTz = T("qTz", [128, SBT], BF16, 8)
        NG = 4
        NMt = T("NM", [128, 2, 2, 128], BF16, 2 * NG)
        Mak = T("Mak", [128, 2, 128], BF16, NG)
        RBK = T("RBK", [128, 2, 2, 128], BF16, NG)
        PAIRS = [(0, 1), (2, 3)]
        Xtile = T("Xt", [128, 2, 2, 64], BF16, 2 * NG)
        ATbd = T("ATbd", [128, 128], BF16, NG)
        Gsb = T("Gsb", [128, 64], F32, NG)
        Ht = T("Hst", [128, 64], F32, 4)
        s1t = T("s1t", [128, 64], F32, NG)
        Wz = T("Wz", [128, 3, 64], BF16, NG)
        Wp = T("Wp", [128, 2, 64], BF16, NG)
        zlo = T("zlo", [128, 64], F32, NG)
        QT = T("QT", [128, 128], BF16, NG)
        prevcol = T("prevcol", [128, 13])
        kc = T("kcols", [128, 8])
        NWS = 5
        ws = T("ws", [128, 8, 128], BF16, NWS)
        ws_b2 = [Buf(f"ws_b2_{i}") for i in range(NWS)]
        wsrr = [0]
        sgr = T("sgr", [128, SBT], BF16, 4)
        sga = T("sga", [128, SBT], BF16, 4)
        NKS = 4
        KTatt = T("KTatt", [128, NKS * 128], BF16, 2)
        NV = 4
        Vpad = T("Vpad", [128, 2, 192], BF16, NV)
        ysqt = T("ysq", [128, 512], F32, 1)
        yn = T("yn", [128, 512], BF16, 1)
        gst = T("gst", [128, 6, 8], F32, 1)
        t1t = T("t1t", [128, 128], F32, 1)
        zr = T("zr", [128, SBT], BF16, 4)
        zatt = T("zatt", [128, SBT], BF16, 4)
        smt = T("smt", [128, 256], F32, 4)
        p32 = T("p32", [128, 256], F32, 4)
        pnt = T("pnt", [128, 256], BF16, 4)
        ptt = T("ptt", [128, 2, 128], BF16, 4)
        ast = T("ast", [128, 8], F32, 4)
        mT = T("mT", [128, 8, SBT], BF16, 1)
        sgt = T("sgt", [128, SBT], F32, 2)
        m12 = T("m12", [128, SBT], F32, 2)
        fst = T("fst", [128, 4], F32, 2)

        S.add("pool", lambda e: e.memset(prevcol.t[0][:], 0.0), [], [prevcol.b[0]])
        kct = kc.t[0]
        KB = kc.b[0]
        for j, val in enumerate([RMS_EPS, 1.0, -0.5, 1e-12, GN_EPS]):
            S.add("pool", lambda e, j=j, val=val: e.memset(kct[:, j:j + 1], val), [], [KB])
        eps_col = kct[:, 0:1]
        one_col = kct[:, 1:2]
        mhalf_col = kct[:, 2:3]
        tiny_col = kct[:, 3:4]
        gneps_col = kct[:, 4:5]
        for i in range(NG):
            S.add("pool", lambda e, i=i: e.memset(ATbd.t[i][:], 0.0), [], [ATbd.b[i]])
            S.add("pool", lambda e, i=i: e.memset(Wz.t[i][:], 0.0), [], [Wz.b[i]])
        for i in range(4):
            S.add("pool", lambda e, i=i: e.memset(Ht.t[i][:], 0.0), [], [Ht.b[i]])
        for i in range(NV):
            S.add("pool", lambda e, i=i: e.memset(Vpad.t[i][:], 0.0), [], [Vpad.b[i]])
        for i in range(8):
            S.add("pool", lambda e, i=i: e.memset(PZ.t[i][:], 0.0), [], [PZ.b[i]])
            S.add("pool", lambda e, i=i: e.memset(Hbfz.t[i][:], 0.0), [], [Hbfz.b[i]])
            S.add("pool", lambda e, i=i: e.memset(qTz.t[i][:], 0.0), [], [qTz.b[i]])
        for i in range(2):
            S.add("pool", lambda e, i=i: e.memset(KTatt.t[i][:], 0.0), [], [KTatt.b[i]])
        Wout_b = [Buf(f"wout{k}") for k in range(8)]
        for k in range(8):
            S.add("pool", lambda e, k=k: e.dma_start(out=Wout.t[0][:, k, :], in_=w_out[k * 128:(k + 1) * 128, :]),
                  [], [Wout_b[k]], dma=True)

        def bcm(ap2, n):
            a = ap2.ap
            return bass.AP(ap2.tensor, ap2.offset, [list(a[0]), [0, n], list(a[1])])

        def bcl(ap2, n):
            a = ap2.ap
            return bass.AP(ap2.tensor, ap2.offset, [list(a[0]), list(a[1]), [0, n]])

        def v3(ap, h=2):
            return ap.rearrange("p (h t) -> p h t", h=h)

        def rms_rstd(in_ap, in_bufs, stt, stb, junk_ap, junk_buf):
            S.add("act", lambda e: e.activation(out=junk_ap, in_=in_ap, func=AF.Square, accum_out=stt[:, 0:1]),
                  in_bufs, [junk_buf, stb])
            S.add("act", lambda e: e.activation(out=stt[:, 1:2], in_=stt[:, 0:1], func=AF.Ln, bias=eps_col, scale=1.0 / D),
                  [stb, KB], [stb])
            S.add("act", lambda e: e.activation(out=stt[:, 2:3], in_=stt[:, 1:2], func=AF.Exp, scale=-0.5), [stb], [stb])

        def ws_load(srcs):
            i = wsrr[0]
            wsrr[0] = (i + 1) % NWS
            t = ws.t[i]
            bufs = [ws.b[i], ws_b2[i]]
            for j, (dfn, dap) in enumerate(srcs):
                S.add("pool", lambda e, dfn=dfn, dap=dap, t=t: e.dma_start(out=dfn(t), in_=dap), [], [bufs[j]], dma=True)
            return t, bufs[:len(srcs)]

        def wcols(c0, n=128):
            return w_in[:, c0:c0 + n].rearrange("(k p) c -> p k c", p=128)

        def proj_fm(hTt, hTb, wt, wbufs, ncols=SBT, col0=0):
            pa, pab = fullbank()
            for k in range(8):
                S.add("pe", lambda e, pa=pa, k=k, wt=wt, hTt=hTt: e.matmul(
                    pa[:, 0:ncols], lhsT=wt[:, k, :], rhs=hTt[:, k, col0:col0 + ncols], start=(k == 0), stop=(k == 7)),
                    wbufs + [hTb], pab)
            return pa, pab

        P_ = [slice(0, 64), slice(64, 128)]
        own0 = nsb - nown

        def stage2_header():
            swt, swb = shwa()
            twt, twb = tw()
            S.add("act", lambda e, twt=twt, swt=swt: e.activation(out=twt[0:64, :], in_=swt[0:64, :], func=AF.Exp, scale=2.0), [swb], [twb])
            S.add("dve", lambda e, twt=twt: e.tensor_scalar(out=twt[0:64, :], in0=twt[0:64, :], scalar1=1.0, scalar2=None, op0=ALU.add), [twb], [twb])
            S.add("dve", lambda e, twt=twt: e.reciprocal(out=twt[0:64, :], in_=twt[0:64, :]), [twb], [twb])
            S.add("dve", lambda e, twt=twt: e.tensor_scalar(out=twt[0:64, :], in0=twt[0:64, :], scalar1=-2.0, scalar2=1.0, op0=ALU.mult, op1=ALU.add), [twb], [twb])
            S.add("act", lambda e, twt=twt, swt=swt: e.activation(out=twt[64:128, :], in_=swt[64:128, :], func=AF.Copy), [swb, twb], [twb])
            twh, twhb = tw_hi()
            twl, twlb = tw_lo()
            S.add("act", lambda e, twh=twh, twt=twt: e.activation(out=twh[:], in_=twt[:], func=AF.Copy), [twb], [twhb])
            S.add("dve", lambda e, twl=twl, twt=twt, twh=twh: e.tensor_tensor(out=twl[:], in0=twt[:], in1=twh[:], op=ALU.subtract), [twb, twhb], [twlb])
            return dict(twt=twt, twh=twh, twl=twl, twhb=twhb, twlb=twlb, twb=twb)
        def prep_hp(hp, sbi, own, twt=None, twh=None, twl=None, twhb=None, twlb=None, twb=None):
            si = sbi * 4 + hp
            rt, rb = shr(si)
            kt_, kb_ = shk(si)
            vt, vb = shv(si)
            pD, pDb = fullbank()
            hsl = slice(hp * 128, (hp + 1) * 128)
            S.add("pe", lambda e, pD=pD, hsl=hsl, twh=twh: e.matmul(pD[:, 0:SBT], lhsT=Wdb.t[0][:, 0, hsl], rhs=twh[:, :], start=True, stop=False),
                  [Wdb.b[0], twhb], pDb)
            S.add("pe", lambda e, pD=pD, hsl=hsl, twl=twl: e.matmul(pD[:, 0:SBT], lhsT=Wdb.t[0][:, 0, hsl], rhs=twl[:, :], start=False, stop=False),
                  [Wdb.b[0], twlb], pDb)
            S.add("pe", lambda e, pD=pD, hsl=hsl, twh=twh: e.matmul(pD[:, 0:SBT], lhsT=Wdb.t[0][:, 1, hsl], rhs=twh[:, :], start=False, stop=True),
                  [Wdb.b[0], twhb], pDb)
            e1, e1b = t_e1()
            ew, ewb = t_ew()
            at, ab_ = t_a()
            cs, csb = t_cs()
            csp, cspb = t_csp()
            en, enb = t_en()
            k2, k2b = t_k2()
            kkn, kknb = t_kkn()
            abt, abb = t_ab()
            ft, fb = t_f()
            S.add("act", lambda e, e1=e1, pD=pD, hp=hp: e.activation(out=e1[:], in_=pD[:, 0:SBT], func=AF.Exp, bias=pc("nw0", hp), scale=-1.0),
                  pDb + [PB], [e1b])
            pAa, pAb = fullbank()
            S.add("pe", lambda e, pAa=pAa, hsl=hsl, twh=twh: e.matmul(pAa[:, 0:SBT], lhsT=Wdb.t[0][:, 2, hsl], rhs=twh[:, :], start=True, stop=True),
                  [Wdb.b[0], twhb], pAb)
            S.add("act", lambda e, e1=e1: e.activation(out=e1[:], in_=e1[:], func=AF.Ln, bias=one_col), [e1b, KB], [e1b])
            S.add("act", lambda e, e1=e1, ew=ew: e.activation(out=ew[:], in_=e1[:], func=AF.Exp, bias=mhalf_col, scale=-1.0), [e1b, KB], [ewb])
            S.add("act", lambda e, at=at, pAa=pAa, hp=hp: e.activation(out=at[:], in_=pAa[:, 0:SBT], func=AF.Exp, bias=pc("na0", hp), scale=-1.0),
                  pAb + [PB], [ab_])
            yield
            S.add("dve", lambda e, at=at: e.tensor_scalar(out=at[:], in0=at[:], scalar1=1.0, scalar2=None, op0=ALU.add), [ab_], [ab_])
            S.add("dve", lambda e, at=at: e.reciprocal(out=at[:], in_=at[:]), [ab_], [ab_])
            for c in range(CPS):
                S.add("dve", lambda e, cs=cs, ew=ew, c=c: e.tensor_tensor_scan(
                    out=cs[:, c * C:(c + 1) * C], data0=ones_f, data1=ew[:, c * C:(c + 1) * C], initial=0.0,
                    op0=ALU.mult, op1=ALU.add), [ewb, CF], [csb])
            S.add("pool", lambda e, csp=csp, cs=cs, ew=ew: e.tensor_tensor(out=csp[:], in0=cs[:], in1=ew[:], op=ALU.subtract), [csb, ewb], [cspb])
            S.add("act", lambda e, en=en, cs=cs: e.activation(out=en[:], in_=cs[:], func=AF.Exp), [csb], [enb])
            S.add("act", lambda e, csp=csp: e.activation(out=csp[:], in_=csp[:], func=AF.Exp, scale=-1.0), [cspb], [cspb])
            gct, gcb = gC(si)
            S.add("act", lambda e, gct=gct, cs=cs: e.activation(
                out=gct[:, 0:CPS], in_=cs[:, :].rearrange("p (c t) -> p c t", t=C)[:, :, C - 1], func=AF.Exp, scale=-1.0), [csb], [gcb])
            yield
            k2h, k2hb = t_k2b()
            S.add("act", lambda e, k2h=k2h, kt_=kt_, hp=hp: e.activation(out=k2h[:], in_=kt_[:], func=AF.Square, scale=pc("kk", hp)), [kb_, PB], [k2hb])
            pS_, pSb = fullbank()
            S.add("pe", lambda e, pS_=pS_, k2h=k2h: e.matmul(pS_[:, 0:SBT], lhsT=bones_b, rhs=k2h[:], start=True, stop=True), [CB, k2hb], pSb)
            S.add("act", lambda e, k2=k2, pS_=pS_: e.activation(out=k2[:], in_=pS_[:, 0:SBT], func=AF.Ln, bias=tiny_col), pSb + [KB], [k2b])
            S.add("act", lambda e, k2=k2: e.activation(out=k2[:], in_=k2[:], func=AF.Exp, scale=-0.5), [k2b], [k2b])
            S.add("dve", lambda e, kkn=kkn, kt_=kt_, k2=k2, hp=hp: e.scalar_tensor_tensor(
                out=kkn[:], in0=kt_[:], scalar=pc("kk", hp), in1=k2[:], op0=ALU.mult, op1=ALU.mult), [kb_, k2b, PB], [kknb])
            yield
            ARt, ARb = AR(si)
            BTt, BTb = BT(si)
            KTt, KTb = KT(si)
            vbt, vbb = vbf(si)
            S.add("dve", lambda e, ARt=ARt, kkn=kkn, csp=csp: e.scalar_tensor_tensor(
                out=ARt[:, :, 0, :], in0=kkn[:, :].rearrange("p (c t) -> p c t", t=C), scalar=-1.0,
                in1=csp[:, :].rearrange("p (c t) -> p c t", t=C), op0=ALU.mult, op1=ALU.mult), [kknb, cspb], [ARb])
            S.add("pool", lambda e, abt=abt, kkn=kkn, at=at: e.tensor_tensor(out=abt[:], in0=kkn[:], in1=at[:], op=ALU.mult), [kknb, ab_], [abb])
            S.add("pool", lambda e, BTt=BTt, abt=abt, en=en: e.tensor_tensor(out=BTt[:], in0=abt[:], in1=en[:], op=ALU.mult), [abb, enb], [BTb])
            yield
            S.add("dve", lambda e, ft=ft, at=at, hp=hp: e.tensor_scalar(out=ft[:], in0=at[:], scalar1=pc("ka", hp), scalar2=pc("omka", hp),
                                                                    op0=ALU.mult, op1=ALU.add), [ab_, PB], [fb])
            S.add("pool", lambda e, ft=ft, kt_=kt_: e.tensor_tensor(out=ft[:], in0=kt_[:], in1=ft[:], op=ALU.mult), [kb_, fb], [fb])
            S.add("pool", lambda e, KTt=KTt, ft=ft, en=en: e.tensor_tensor(out=KTt[:], in0=ft[:], in1=en[:], op=ALU.mult), [fb, enb], [KTb])
            S.add("act", lambda e, vbt=vbt, vt=vt: e.activation(out=vbt[:], in_=vt[:], func=AF.Copy), [vb], [vbb])
            yield
            for hh in range(2):
                zt, zb = PZ(si * 2 + hh)
                S.add("pool", lambda e, zt=zt, ARt=ARt, hh=hh: e.tensor_copy(out=zt[P_[hh], 0, :].rearrange("p (c t) -> p c t", t=C), in_=ARt[P_[hh], :, 0, :]), [ARb], [zb])
                S.add("pool", lambda e, zt=zt, BTt=BTt, hh=hh: e.tensor_copy(out=zt[P_[hh], 1, :], in_=BTt[P_[hh], :]), [BTb], [zb])
                S.add("pool", lambda e, zt=zt, KTt=KTt, hh=hh: e.tensor_copy(out=zt[P_[hh], 2, :], in_=KTt[P_[hh], :]), [KTb], [zb])
            if own:
                ep, epb = t_ep()
                S.add("act", lambda e, ep=ep, cs=cs: e.activation(out=ep[:], in_=cs[:], func=AF.Exp, scale=-1.0), [csb], [epb])
                S.add("dve", lambda e, ARt=ARt, rt=rt, ep=ep: e.tensor_tensor(
                    out=ARt[:, :, 1, :], in0=rt[:, :].rearrange("p (c t) -> p c t", t=C),
                    in1=ep[:, :].rearrange("p (c t) -> p c t", t=C), op=ALU.mult), [rb, epb], [ARb])
                rkb_t, rkb_b = t_rkb()
                S.add("dve", lambda e, rkb_t=rkb_t, rt=rt, ft=ft, hp=hp: e.scalar_tensor_tensor(
                    out=rkb_t[:], in0=rt[:], scalar=pc("rk", hp), in1=ft[:], op0=ALU.mult, op1=ALU.mult), [rb, fb, PB], [rkb_b])
                pB_, pBb = fullbank()
                S.add("pe", lambda e, pB_=pB_, rkb_t=rkb_t: e.matmul(pB_[:, 0:SBT], lhsT=bones_b, rhs=rkb_t[:], start=True, stop=True), [CB, rkb_b], pBb)
                bnt, bnb = bonus(si)
                S.add("dve", lambda e, bnt=bnt, pB_=pB_, vt=vt: e.tensor_tensor(out=bnt[:], in0=pB_[:, 0:SBT], in1=vt[:], op=ALU.mult), pBb + [vb], [bnb])
            yield
        def proj_tile(sbi, ct, hTt, hTb):
            pa, pab = fullbank()
            for k in range(8):
                S.add("pe", lambda e, pa=pa, k=k, ct=ct, hTt=hTt: e.matmul(
                    pa[:, 0:SBT], lhsT=Wsh.t[0][:, k, ct * 128:(ct + 1) * 128], rhs=hTt[:, k, :],
                    start=(k == 0), stop=(k == 7)), [Wsh_b[k], hTb], pab)
            if ct == 12:
                dst, dstb = shwa()
            else:
                hp = ct % 4
                dst, dstb = (shr, shk, shv)[ct // 4](sbi * 4 + hp)
            tmp, tmpb = shtmp(ct)
            S.add("act", lambda e, tmp=tmp, pa=pa, ct=ct: e.activation(
                out=tmp[:], in_=pa[:, 0:SBT], func=AF.Copy, scale=pc("omu", ct)), pab + [PB], [tmpb])
            S.add("dve", lambda e, dst=dst, pa=pa, tmp=tmp, ct=ct: e.scalar_tensor_tensor(
                out=dst[:, 1:SBT], in0=pa[:, 0:SBT - 1], scalar=pc("mu", ct), in1=tmp[:, 1:SBT],
                op0=ALU.mult, op1=ALU.add), pab + [tmpb, PB], [dstb])
            S.add("dve", lambda e, dst=dst, tmp=tmp, ct=ct: e.scalar_tensor_tensor(
                out=dst[:, 0:1], in0=prevcol.t[0][:, ct:ct + 1], scalar=pc("mu", ct), in1=tmp[:, 0:1],
                op0=ALU.mult, op1=ALU.add), [prevcol.b[0], tmpb, PB], [dstb])
            S.add("act", lambda e, pa=pa, ct=ct: e.activation(
                out=prevcol.t[0][:, ct:ct + 1], in_=pa[:, SBT - 1:SBT], func=AF.Copy), pab, [prevcol.b[0]])

        sbst = {}

        def gen_first(sbi):
            own = sbi >= own0
            hTt, hTb = hT(sbi)
            for j in range(CPS):
                gc = sbi * CPS + j
                xtt, xtb = xt(gc)
                hbt, hbb = hb(gc)
                stt, stb = st0(gc)
                dma("sp", xtt[:], xw[gc * C:(gc + 1) * C, :], [], [xtb])
                rms_rstd(xtt[:], [xtb], stt, stb, hbt[:], hbb)
                S.add("dve", lambda e, xtt=xtt, stt=stt, hbt=hbt: e.scalar_tensor_tensor(
                    out=hbt[:], in0=xtt[:], scalar=stt[:, 2:3], in1=gpre.t[0][:], op0=ALU.mult, op1=ALU.mult),
                    [xtb, stb, gpre.b[0]], [hbb])
                yield
                pst = psT[0]
                pstb = psT_b[0]
                for k in range(8):
                    S.add("pe", lambda e, k=k, hbt=hbt, pst=pst: e.transpose(
                        out=pst[:, k * 128:(k + 1) * 128], in_=hbt[:, k * 128:(k + 1) * 128], identity=ident_b),
                        [hbb, CB], pstb)
                S.add("act", lambda e, pst=pst, hTt=hTt, j=j: e.activation(
                    out=hTt[:, :, j * C:(j + 1) * C], in_=pst[:, :].rearrange("p (k t) -> p k t", k=8), func=AF.Copy),
                    pstb, [hTb])
                yield
            proj_tile(sbi, 12, hTt, hTb)
            tw_ctx = stage2_header()
            sbst[sbi] = (tw_ctx, hTt, hTb)
            yield
            for hp in (0, 1):
                for q in range(3):
                    proj_tile(sbi, q * 4 + hp, hTt, hTb)
                    yield

        def gen_first_b(sbi):
            own = sbi >= own0
            tw_ctx, hTt, hTb = sbst[sbi]
            for hp in (0, 1):
                yield from prep_hp(hp, sbi, own, **tw_ctx)

        def gen_first_ab(sbi):
            yield from gen_first(sbi)
            yield from gen_first_b(sbi)

        def gen_second(sbi):
            own = sbi >= own0
            tw_ctx, hTt, hTb = sbst[sbi]
            for hp in (2, 3):
                for q in range(3):
                    proj_tile(sbi, q * 4 + hp, hTt, hTb)
                    yield
                yield from prep_hp(hp, sbi, own, **tw_ctx)

        def drain(g):
            for _ in g:
                pass

        def mkfill(g, n=1, units=None, slots=None):
            st = [0]

            def fill():
                if units is None:
                    k = n
                else:
                    i = st[0]
                    st[0] += 1
                    k = ((i + 1) * units) // slots - (i * units) // slots
                for _ in range(k):
                    try:
                        next(g)
                    except StopIteration:
                        return
            return fill

        drain(gen_first_ab(0))
        for sbi in range(nsb):
            own = sbi >= own0
            halo_sb = (sbi == own0 - 1)
            hTt, hTb = hT(sbi)
            def gen_ownproj(sbi=sbi, own=own, halo_sb=halo_sb, hTt=hTt, hTb=hTb):
                if own or halo_sb:
                    ncols, col0 = (SBT, 0) if own else (C, SBT - C)
                    kcol = ((sbi - own0) * CPS + 1) * C if own else 0
                    for g in range(2):
                        wt, wb = ws_load([(lambda t: t[:, :, 0:64], wcols(O_K + g * 64, 64)), (lambda t: t[:, :, 64:128], wcols(O_K + g * 64, 64))])
                        pa, pab = proj_fm(hTt, hTb, wt, wb, ncols, col0)
                        for cc in range(ncols // C):
                            ks = ((kcol // C) + cc) % NKS
                            S.add("act", lambda e, pa=pa, g=g, ks=ks, cc=cc: e.activation(
                                out=KTatt.t[g][:, ks * C:(ks + 1) * C], in_=pa[:, cc * C:(cc + 1) * C], func=AF.Identity, bias=pc("bk", g)), pab + [PB], [KTatt.b[g]])
                        yield
                    wt, wb = ws_load([(lambda t: t[:, :, :], wcols(O_V))])
                    for c in (range(CPS) if own else [CPS - 1]):
                        lc1 = (sbi - own0) * CPS + c + 1 if own else 0
                        pa, pab = fullbank()
                        for k in range(8):
                            S.add("pe", lambda e, pa=pa, k=k, wt=wt, hTt=hTt, c=c: e.matmul(
                                pa[:, 0:128], lhsT=hTt[:, k, c * C:(c + 1) * C], rhs=wt[:, k, :], start=(k == 0), stop=(k == 7)), wb + [hTb], pab)
                        vp, vpb = Vpad(lc1)
                        S.add("dve", lambda e, vp=vp, pa=pa: e.tensor_tensor(out=vp[:, :, 0:64], in0=v3(pa[:, 0:128]), in1=v3(bvb.t[0][:, :]), op=ALU.add),
                              pab + [bvb.b[0]], [vpb])
                        S.add("pool", lambda e, vp=vp: e.tensor_copy(out=vp[:, :, 128:192], in_=vp[:, :, 0:64]), [vpb], [vpb])
                        yield
                if own:
                    for ct in range(4):
                        si = sbi * 4 + ct
                        wt, wb = ws_load([(lambda t: t[:, :, :], wcols(O_GR + ct * 128))])
                        pa, pab = proj_fm(hTt, hTb, wt, wb)
                        S.add("act", lambda e, pa=pa, si=si: e.activation(out=sgr(si)[0][:], in_=pa[:, 0:SBT], func=AF.Silu), pab, [sgr(si)[1]])
                        yield
                        wt, wb = ws_load([(lambda t: t[:, :, :], wcols(O_Q + ct * 128))])
                        pa, pab = proj_fm(hTt, hTb, wt, wb)
                        for hh in range(2):
                            qz, qzb = qTz(si * 2 + hh)
                            S.add("act", lambda e, pa=pa, qz=qz, ct=ct, hh=hh: e.activation(
                                out=qz[P_[hh], :], in_=pa[P_[hh], 0:SBT], func=AF.Identity, bias=ppt[P_[hh], PPI["bq"] + ct:PPI["bq"] + ct + 1]),
                                pab + [PB], [qzb])
                        yield
                        wt, wb = ws_load([(lambda t: t[:, :, :], wcols(O_GA + ct * 128))])
                        pa, pab = proj_fm(hTt, hTb, wt, wb)
                        S.add("act", lambda e, pa=pa, si=si: e.activation(out=sga(si)[0][:], in_=pa[:, 0:SBT], func=AF.Silu), pab, [sga(si)[1]])
                        yield

                yield

            def emit_chunk_pairs(c, pairs, fill, own=own, sbi=sbi):
                gch = sbi * CPS + c
                csl = slice(c * C, (c + 1) * C)
                pYbank, pYbb = psA[0], [psA_b[0]]
                def mkctx(hp):
                    si = sbi * 4 + hp
                    gi = gch * 4 + hp
                    x = dict(hp=hp, si=si, gi=gi)
                    x["AR"], x["ARb"] = AR(si)
                    x["BT"], x["BTb"] = BT(si)
                    x["KT"], x["KTb"] = KT(si)
                    x["vb"], x["vbb"] = vbf(si)
                    x["zts"] = [PZ(si * 2 + hh) for hh in range(2)]
                    x["tm"], x["tmb"] = tm(gi)
                    return x

                def g_transposes(x, c=c, csl=csl):
                    pt_ = psT[1][:, 0:512]
                    ptb = psT_b[1]
                    srcs = [(x["AR"][:, c, 0, :], x["ARb"]), (x["BT"][:, csl], x["BTb"]), (x["KT"][:, csl], x["KTb"]), (x["vb"][:, csl], x["vbb"])]
                    for q, (sap, sbf) in enumerate(srcs):
                        S.add("pe", lambda e, pt_=pt_, q=q, sap=sap: e.transpose(out=pt_[:, q * 128:(q + 1) * 128], in_=sap, identity=ident_b),
                              [sbf, CB], ptb)
                    tmt = x["tm"]
                    S.add("act", lambda e, tmt=tmt, pt_=pt_: e.activation(out=tmt[:], in_=pt_.rearrange("p (q t) -> p q t", q=4), func=AF.Copy),
                          ptb, [x["tmb"]])

                def g_sprod_pe(x, c=c, csl=csl):
                    x["pS1"], x["pS1b"] = single(4)
                    x["pS2"], x["pS2b"] = single(2)
                    ARt, BTt = x["AR"], x["BT"]
                    for hh in range(2):
                        zt, zb = x["zts"][hh]
                        S.add("pe", lambda e, pS=x["pS1"], zt=zt, ARt=ARt, hh=hh: e.matmul(
                            pS[:, hh * 128:(hh + 1) * 128], lhsT=zt[:, 1, csl], rhs=ARt[:, c, 0, :], start=True, stop=True), [zb, x["ARb"]], x["pS1b"])
                    for hh in range(2):
                        zt, zb = x["zts"][hh]
                        S.add("pe", lambda e, pS=x["pS1"], zt=zt, BTt=BTt, hh=hh: e.matmul(
                            pS[:, (2 + hh) * 128:(3 + hh) * 128], lhsT=zt[:, 0, csl], rhs=BTt[:, csl], start=True, stop=True), [zb, x["BTb"]], x["pS1b"])
                    for hh in range(2):
                        zt, zb = x["zts"][hh]
                        S.add("pe", lambda e, pS=x["pS2"], zt=zt, ARt=ARt, hh=hh: e.matmul(
                            pS[:, hh * 128:(hh + 1) * 128], lhsT=zt[:, 2, csl], rhs=ARt[:, c, 0, :], start=True, stop=True), [zb, x["ARb"]], x["pS2b"])

                def g_sprod_evac(x):
                    gi = x["gi"]
                    nm, nmb = NMt(gi * 2)
                    mk, mkb = Mak(gi)
                    S.add("dve", lambda e, nm=nm, pS=x["pS1"]: e.tensor_tensor(out=nm[:, :, :, :].rearrange("p a h t -> p (a h) t"), in0=v3(pS, 4), in1=mask4, op=ALU.mult),
                          x["pS1b"] + [CB], [nmb])
                    S.add("dve", lambda e, mk=mk, pS=x["pS2"]: e.tensor_tensor(out=mk[:], in0=v3(pS, 2), in1=mask4[:, 0:2, :], op=ALU.mult),
                          x["pS2b"] + [CB], [mkb])
                    x["nm"], x["nmb"], x["mk"], x["mkb"] = nm, nmb, mk, mkb

                def g_r_pe(x, c=c, csl=csl):
                    x["pR"], x["pRb"] = single(4)
                    ARt = x["AR"]
                    for a_ in range(2):
                        for hh in range(2):
                            zt, zb = x["zts"][hh]
                            S.add("pe", lambda e, pR=x["pR"], zt=zt, ARt=ARt, hh=hh, a_=a_: e.matmul(
                                pR[:, (a_ * 2 + hh) * 128:(a_ * 2 + hh + 1) * 128], lhsT=zt[:, 1 + a_, csl], rhs=ARt[:, c, 1, :], start=True, stop=True),
                                [zb, x["ARb"]], x["pRb"])

                def g_r_evac(x, c=c, csl=csl):
                    rbk, rbkb = RBK(x["gi"])
                    for a_ in range(2):
                        S.add("dve", lambda e, rbk=rbk, pR=x["pR"], a_=a_: e.tensor_tensor(out=rbk[:, a_, :, :], in0=v3(pR[:, a_ * 256:(a_ + 1) * 256], 2), in1=mle2, op=ALU.mult),
                              x["pRb"] + [CB], [rbkb])
                    x["rbk"], x["rbkb"] = rbk, rbkb

                def g_pv_pe(x, c=c, csl=csl):
                    x["pV"], x["pVb"] = single(1)
                    mk, tmt = x["mk"], x["tm"]
                    for hh in range(2):
                        S.add("pe", lambda e, pV=x["pV"], hh=hh, mk=mk, tmt=tmt: e.matmul(
                            pV[:, hh * 64:(hh + 1) * 64], lhsT=mk[:, hh, :], rhs=tmt[:, 3, hh * 64:(hh + 1) * 64], start=True, stop=True), [x["mkb"], x["tmb"]], x["pVb"])

                def g_x0(x, c=c, csl=csl):
                    Xt, Xb = Xtile(x["gi"] * 2)
                    tmt = x["tm"]
                    S.add("pool", lambda e, Xt=Xt, tmt=tmt: e.tensor_copy(out=Xt[:, :, 0, :], in_=tmt[:, 0, :].rearrange("p (h k) -> p h k", h=2)), [x["tmb"]], [Xb])
                    S.add("act", lambda e, Xt=Xt, pV=x["pV"]: e.activation(out=Xt[:, :, 1, :], in_=pV[:, 0:128].rearrange("p (h k) -> p h k", h=2), func=AF.Copy),
                          x["pVb"], [Xb])
                    x["X"], x["Xb"] = Xt, Xb

                def g_level_pe(x, lv):
                    nm, nmb, Xt, Xb = x["nm"], x["nmb"], x["X"], x["Xb"]
                    x["pX"], x["pXb"] = single(2)
                    for hh in range(2):
                        S.add("pe", lambda e, pX=x["pX"], hh=hh, nm=nm, Xt=Xt: e.matmul(
                            pX[:, hh * 128:(hh + 1) * 128], lhsT=nm[:, 0, hh, :], rhs=Xt[:, hh, :, :].rearrange("p a k -> p (a k)"), start=True, stop=True),
                            [nmb, Xb], x["pXb"])
                    if lv < 6:
                        x["pNM"], x["pNMb"] = single(4)
                        for hh in range(2):
                            S.add("pe", lambda e, pN=x["pNM"], hh=hh, nm=nm: e.matmul(
                                pN[:, hh * 128:(hh + 1) * 128], lhsT=nm[:, 1, hh, :], rhs=nm[:, 0, hh, :], start=True, stop=True), [nmb], x["pNMb"])
                        if lv < 5:
                            for hh in range(2):
                                S.add("pe", lambda e, pN=x["pNM"], hh=hh, nm=nm: e.matmul(
                                    pN[:, (2 + hh) * 128:(3 + hh) * 128], lhsT=nm[:, 0, hh, :], rhs=nm[:, 1, hh, :], start=True, stop=True), [nmb], x["pNMb"])

                def g_level_evac(x, lv):
                    gi = x["gi"]
                    Xt, Xb = x["X"], x["Xb"]
                    Xn, Xnb = Xtile(gi * 2 + lv + 1)
                    S.add("dve", lambda e, Xn=Xn, pX=x["pX"], Xt=Xt: e.tensor_tensor(
                        out=Xn[:, :, :, :].rearrange("p h a k -> p h (a k)"), in0=v3(pX), in1=Xt[:, :, :, :].rearrange("p h a k -> p h (a k)"), op=ALU.add),
                        x["pXb"] + [Xb], [Xnb])
                    x["X"], x["Xb"] = Xn, Xnb
                    if lv < 6:
                        nn, nnb = NMt(gi * 2 + lv + 1)
                        w = 4 if lv < 5 else 2
                        S.add("act", lambda e, nn=nn, pN=x["pNM"], w=w: e.activation(
                            out=nn[:, :, :, :].rearrange("p a h t -> p (a h) t")[:, 0:w, :], in_=v3(pN[:, 0:w * 128], w), func=AF.Copy), x["pNMb"], [nnb])
                        x["nm"], x["nmb"] = nn, nnb

                def g_state(x, c=c, csl=csl, own=own, pYbank=(pYbank if own else None), pYbb=(pYbb if own else None)):
                    gi, hp, si = x["gi"], x["hp"], x["si"]
                    Xt, Xb, tmt, tmb = x["X"], x["Xb"], x["tm"], x["tmb"]
                    ARt, ARb = x["AR"], x["ARb"]
                    wz, wzb = Wz(gi)
                    S.add("pool", lambda e, wz=wz, Xt=Xt: e.tensor_copy(out=wz[:, 0::2, :], in_=Xt[:, :, 0, :]), [Xb], [wzb])
                    wzA = wz[:, 0:2, :].rearrange("p a k -> p (a k)")
                    wzB = wz[:, 1:3, :].rearrange("p a k -> p (a k)")
                    wp, wpb = Wp(gi)
                    S.add("pool", lambda e, wp=wp, Xt=Xt: e.tensor_copy(out=wp[:, :, :], in_=Xt[:, :, 0, :]), [Xb], [wpb])
                    pAT, pATb = single(1)
                    S.add("pe", lambda e, pAT=pAT, wp=wp, tmt=tmt: e.matmul(pAT[:, 0:128], lhsT=wp[:, :, :].rearrange("p a k -> p (a k)"), rhs=tmt[:, 1, :], start=True, stop=True),
                          [wpb, tmb], pATb)
                    atb, atbb = ATbd(gi)
                    for hh in range(2):
                        S.add("act", lambda e, atb=atb, pAT=pAT, hh=hh: e.activation(
                            out=atb[P_[hh], hh * 64:(hh + 1) * 64], in_=pAT[P_[hh], hh * 64:(hh + 1) * 64], func=AF.Copy), pATb, [atbb])
                    pG, pGb = single(1)
                    pG2, pG2b = single(1)
                    S.add("pe", lambda e, pG=pG, tmt=tmt: e.matmul(pG[:, 0:128], lhsT=tmt[:, 2, :], rhs=tmt[:, 3, :], start=True, stop=True),
                          [tmb], pGb)
                    for hh in range(2):
                        S.add("pe", lambda e, pG2=pG2, Xt=Xt, tmt=tmt, hh=hh: e.matmul(
                            pG2[:, hh * 64:(hh + 1) * 64], lhsT=tmt[:, 1, :], rhs=Xt[:, hh, 1, :], start=True, stop=True), [Xb, tmb], pG2b)
                    gs, gsb_ = Gsb(gi)
                    for hh in range(2):
                        S.add("act", lambda e, gs=gs, pG=pG, hh=hh: e.activation(
                            out=gs[P_[hh], :], in_=pG[P_[hh], hh * 64:(hh + 1) * 64], func=AF.Copy), pGb, [gsb_])
                        S.add("dve", lambda e, gs=gs, pG2=pG2, hh=hh: e.tensor_tensor(
                            out=gs[P_[hh], :], in0=pG2[P_[hh], hh * 64:(hh + 1) * 64], in1=gs[P_[hh], :], op=ALU.add), pG2b + [gsb_], [gsb_])
                    Htt, Hb_ = Ht(hp)
                    gct, gcb = gC(si)
                    if own:
                        hbz = [Hbfz(gi * 2 + hh) for hh in range(2)]
                        for hh in range(2):
                            S.add("pool", lambda e, hz=hbz[hh][0], Htt=Htt, hh=hh: e.tensor_copy(out=hz[P_[hh], :], in_=Htt[P_[hh], :]), [Hb_], [hbz[hh][1]])
                    hhl, hhlb = Hhl(gi)
                    S.add("pool", lambda e, hhl=hhl, Htt=Htt: e.tensor_copy(out=hhl[:, 0, :], in_=Htt[:]), [Hb_], [hhlb])
                    S.add("pool", lambda e, hhl=hhl, Htt=Htt: e.tensor_tensor(out=hhl[:, 1, :], in0=Htt[:], in1=hhl[:, 0, :], op=ALU.subtract), [Hb_, hhlb], [hhlb])
                    pZ, pZb = single(1)
                    S.add("pe", lambda e, pZ=pZ, atb=atb, hhl=hhl: e.matmul(pZ[:, 0:128], lhsT=atb[:], rhs=hhl[:, :, :].rearrange("p a v -> p (a v)"), start=True, stop=True),
                          [atbb, hhlb], pZb)
                    s1, s1b = s1t(gi)
                    S.add("pool", lambda e, s1=s1, Htt=Htt, gs=gs: e.tensor_tensor(out=s1[:], in0=Htt[:], in1=gs[:], op=ALU.add), [Hb_, gsb_], [s1b])
                    S.add("pool", lambda e, s1=s1, gct=gct: e.tensor_scalar(out=s1[:], in0=s1[:], scalar1=gct[:, c:c + 1], scalar2=1.0, op0=ALU.mult, op1=ALU.mult),
                          [s1b, gcb], [s1b])
                    S.add("dve", lambda e, pZ=pZ, gct=gct, s1=s1: e.scalar_tensor_tensor(
                        out=s1[:], in0=pZ[:, 0:64], scalar=gct[:, c:c + 1], in1=s1[:], op0=ALU.mult, op1=ALU.add), pZb + [gcb, s1b], [s1b])
                    S.add("dve", lambda e, Htt=Htt, pZ=pZ, gct=gct, s1=s1: e.scalar_tensor_tensor(
                        out=Htt[:], in0=pZ[:, 64:128], scalar=gct[:, c:c + 1], in1=s1[:], op0=ALU.mult, op1=ALU.add), pZb + [gcb, s1b], [Hb_])
                    if own:
                        rbk, rbkb = x["rbk"], x["rbkb"]
                        qt_, qtb = QT(gi)
                        for hh, wzX in enumerate((wzA, wzB)):
                            pQ, pQb = single(1)
                            S.add("pe", lambda e, pQ=pQ, wzX=wzX, rbk=rbk, hh=hh: e.matmul(pQ[:, 0:128], lhsT=wzX, rhs=rbk[:, 0, hh, :], start=True, stop=True),
                                  [wzb, rbkb], pQb)
                            S.add("dve", lambda e, qt_=qt_, pQ=pQ, ARt=ARt, hh=hh: e.tensor_tensor(
                                out=qt_[P_[hh], :], in0=pQ[P_[hh], 0:128], in1=ARt[P_[hh], c, 1, :], op=ALU.add), pQb + [ARb], [qtb])
                        for hh in range(2):
                            pY, pYb = pYbank, pYbb
                            ysl = slice((hp * 2 + hh) * 64, (hp * 2 + hh + 1) * 64)
                            S.add("pe", lambda e, pY=pY, ysl=ysl, hh=hh, rbk=rbk, Xt=Xt: e.matmul(
                                pY[:, ysl], lhsT=rbk[:, 0, hh, :], rhs=Xt[:, hh, 1, :], start=True, stop=False), [rbkb, Xb], pYb)
                            S.add("pe", lambda e, pY=pY, ysl=ysl, hh=hh, rbk=rbk, tmt=tmt: e.matmul(
                                pY[:, ysl], lhsT=rbk[:, 1, hh, :], rhs=tmt[:, 3, hh * 64:(hh + 1) * 64], start=False, stop=False), [rbkb, tmb], pYb)
                            S.add("pe", lambda e, pY=pY, ysl=ysl, qt_=qt_, hz=hbz[hh][0]: e.matmul(
                                pY[:, ysl], lhsT=qt_[:, :], rhs=hz[:, :], start=False, stop=True), [qtb, hbz[hh][1]], pYb)

                for pr in pairs:
                    ctxs = [mkctx(hp) for hp in pr]
                    for x in ctxs:
                        g_transposes(x)
                    for x in ctxs:
                        g_sprod_pe(x)
                    for x in ctxs:
                        g_sprod_evac(x)
                    fill()
                    if own:
                        for x in ctxs:
                            g_r_pe(x)
                        for x in ctxs:
                            g_r_evac(x)
                    for x in ctxs:
                        g_pv_pe(x)
                    for x in ctxs:
                        g_x0(x)
                    fill()
                    for lv in range(7):
                        for x in ctxs:
                            g_level_pe(x, lv)
                        for x in ctxs:
                            g_level_evac(x, lv)
                        fill()
                    for x in ctxs:
                        g_state(x)
            if not own:
                if halo_sb:
                    drain(gen_ownproj())
                g2 = gen_second(sbi)
                f2 = mkfill(g2, units=23, slots=17)
                for c in range(CPS):
                    emit_chunk_pairs(c, [PAIRS[0]], f2)
                drain(g2)
                g1 = gen_first_ab(sbi + 1) if sbi + 1 < nsb else iter(())
                f1 = mkfill(g1, units=28, slots=17)
                for c in range(CPS):
                    emit_chunk_pairs(c, [PAIRS[1]], f1)
                drain(g1)
                continue
            fb_mode[0] = 1
            g1 = iter(())
            for c in range(CPS):
                gch = sbi * CPS + c
                csl = slice(c * C, (c + 1) * C)
                pYbank, pYbb = psA[0], [psA_b[0]]
                if c == 0:
                    g2 = gen_second(sbi)
                    emit_chunk_pairs(c, [PAIRS[0]], mkfill(g2, 2))
                    drain(g2)
                    g3 = gen_ownproj()
                    emit_chunk_pairs(c, [PAIRS[1]], mkfill(g3, 2))
                    drain(g3)
                else:
                    if c == 1 and sbi + 1 < nsb:
                        g1 = gen_first(sbi + 1)
                    emit_chunk_pairs(c, [PAIRS[0]], mkfill(g1, 1))
                    emit_chunk_pairs(c, [PAIRS[1]], mkfill(g1, 1))
                lc = (sbi - own0) * CPS + c
                g_t, g_b = gst()
                pY, pYb = pYbank, pYbb
                yq, yqb = ysqt(0)
                S.add("dve", lambda e, g_t=g_t, pY=pY: e.tensor_reduce(out=g_t[:, 0, :], in_=v3(pY[:, 0:512], 8), axis=AX.X, op=ALU.add), pYb, [g_b])
                S.add("act", lambda e, yq=yq, pY=pY: e.activation(out=yq[:], in_=pY[:, 0:512], func=AF.Square), pYb, [yqb])
                S.add("dve", lambda e, g_t=g_t, yq=yq: e.tensor_reduce(out=g_t[:, 1, :], in_=v3(yq[:, :], 8), axis=AX.X, op=ALU.add), [yqb], [g_b])
                S.add("dve", lambda e, g_t=g_t: e.tensor_scalar(out=g_t[:, 2, :], in0=g_t[:, 0, :], scalar1=1.0 / 64, scalar2=None, op0=ALU.mult), [g_b], [g_b])
                S.add("dve", lambda e, g_t=g_t: e.tensor_tensor(out=g_t[:, 3, :], in0=g_t[:, 2, :], in1=g_t[:, 2, :], op=ALU.mult), [g_b], [g_b])
                S.add("dve", lambda e, g_t=g_t: e.scalar_tensor_tensor(out=g_t[:, 4, :], in0=g_t[:, 1, :], scalar=1.0 / 64, in1=g_t[:, 3, :],
                                                                       op0=ALU.mult, op1=ALU.subtract), [g_b], [g_b])
                S.add("act", lambda e, g_t=g_t: e.activation(out=g_t[:, 5, :], in_=g_t[:, 4, :], func=AF.Ln, bias=gneps_col), [g_b, KB], [g_b])
                S.add("act", lambda e, g_t=g_t: e.activation(out=g_t[:, 5, :], in_=g_t[:, 5, :], func=AF.Exp, scale=-0.5), [g_b], [g_b])
                ynt, ynb = yn()
                S.add("dve", lambda e, yq=yq, pY=pY, g_t=g_t: e.tensor_tensor(
                    out=v3(yq[:, :], 8), in0=v3(pY[:, 0:512], 8), in1=bcl(g_t[:, 2, :], 64), op=ALU.subtract), pYb + [g_b, yqb], [yqb])
                S.add("pool", lambda e, ynt=ynt, yq=yq, g_t=g_t: e.tensor_tensor(
                    out=v3(ynt[:, :], 8), in0=v3(yq[:, :], 8), in1=bcl(g_t[:, 5, :], 64), op=ALU.mult), [yqb, g_b], [ynb])
                pt_ = psT[1][:, 0:512]
                ptb = psT_b[1]
                for hp in range(4):
                    S.add("pe", lambda e, pt_=pt_, hp=hp, ynt=ynt: e.transpose(out=pt_[:, hp * 128:(hp + 1) * 128], in_=ynt[:, hp * 128:(hp + 1) * 128], identity=ident_b),
                          [ynb, CB], ptb)
                for hp in range(4):
                    si = sbi * 4 + hp
                    t1, t1b = t1t(hp)
                    S.add("dve", lambda e, t1=t1, pt_=pt_, hp=hp: e.tensor_scalar(out=t1[:], in0=pt_[:, hp * 128:(hp + 1) * 128], scalar1=pc("gnw", hp), scalar2=pc("gnb", hp),
                                                                            op0=ALU.mult, op1=ALU.add), ptb + [PB], [t1b])
                    S.add("pool", lambda e, t1=t1, si=si, csl=csl: e.tensor_tensor(out=t1[:], in0=t1[:], in1=bonus(si)[0][:, csl], op=ALU.add), [t1b, bonus(si)[1]], [t1b])
                    S.add("pool", lambda e, t1=t1, si=si, csl=csl: e.tensor_tensor(out=zr(si)[0][:, csl], in0=t1[:], in1=sgr(si)[0][:, csl], op=ALU.mult),
                          [t1b, sgr(si)[1]], [zr(si)[1]])
                if lc == NOC - 1:
                    dump("zr0", zr(sbi * 4)[0][:], [128, SBT], [zr(sbi * 4)[1]], BF16)
                if upto < 4:
                    continue
                am_i = 0 if lc == 0 else 1
                pObank, pObb = psA[0], [psA_b[0]]
                vprev, vprevb = Vpad(lc)
                vcur, vcurb = Vpad(lc + 1)
                for qp0 in (0, 2):
                    pts = psT[0]
                    hs = []
                    for qp in (qp0, qp0 + 1):
                        si = sbi * 4 + qp
                        g = qp // 2
                        for hh in range(2):
                            hd = qp * 2 + hh
                            pS, pSb_ = single(2)
                            qz, qzb = qTz(si * 2 + hh)
                            for kk_ in range(2):
                                ks = (lc + kk_) % NKS
                                S.add("pe", lambda e, pS=pS, qz=qz, g=g, ks=ks, kk_=kk_, csl=csl: e.matmul(
                                    pS[:, kk_ * 128:(kk_ + 1) * 128], lhsT=qz[:, csl], rhs=KTatt.t[g][:, ks * C:(ks + 1) * C], start=True, stop=True),
                                    [qzb, KTatt.b[g]], pSb_)
                            j4 = (qp - qp0) * 2 + hh
                            hs.append(dict(hd=hd, qp=qp, hh=hh, g=g, si=si, pS=pS, pSb=pSb_, sm=smt(j4), a=ast(j4), p3=p32(j4), pn=pnt(j4), pt=ptt(j4),
                                           ptsl=pts[:, j4 * 256:(j4 + 1) * 256]))
                    for h in hs:
                        S.add("dve", lambda e, sm=h["sm"][0], pS=h["pS"], am_i=am_i: e.scalar_tensor_tensor(
                            out=sm[:], in0=pS[:, 0:256], scalar=0.125, in1=amask.t[0][:, am_i, :], op0=ALU.mult, op1=ALU.add), h["pSb"] + [amask.b[0]], [h["sm"][1]])
                    for h in hs:
                        S.add("dve", lambda e, a_t=h["a"][0], sm=h["sm"][0]: e.tensor_reduce(out=a_t[:, 0:1], in_=sm[:], axis=AX.X, op=ALU.max), [h["sm"][1]], [h["a"][1]])
                    for h in hs:
                        S.add("dve", lambda e, a_t=h["a"][0], hd=h["hd"]: e.tensor_scalar(out=a_t[:, 1:2], in0=a_t[:, 0:1], scalar1=pc("sink", hd), scalar2=-1.0, op0=ALU.max, op1=ALU.mult),
                              [h["a"][1], PB], [h["a"][1]])
                    for h in hs:
                        S.add("act", lambda e, pp3=h["p3"][0], sm=h["sm"][0], a_t=h["a"][0]: e.activation(out=pp3[:], in_=sm[:], func=AF.Exp, bias=a_t[:, 1:2], accum_out=a_t[:, 2:3]),
                              [h["sm"][1], h["a"][1]], [h["p3"][1], h["a"][1]])
                    for h in hs:
                        S.add("act", lambda e, a_t=h["a"][0], hd=h["hd"]: e.activation(out=a_t[:, 3:4], in_=pc("sink", hd), func=AF.Exp, bias=a_t[:, 1:2]), [h["a"][1], PB], [h["a"][1]])
                    for h in hs:
                        S.add("dve", lambda e, a_t=h["a"][0]: e.tensor_tensor(out=a_t[:, 4:5], in0=a_t[:, 2:3], in1=a_t[:, 3:4], op=ALU.add), [h["a"][1]], [h["a"][1]])
                    for h in hs:
                        S.add("dve", lambda e, a_t=h["a"][0]: e.reciprocal(out=a_t[:, 5:6], in_=a_t[:, 4:5]), [h["a"][1]], [h["a"][1]])
                    for h in hs:
                        S.add("dve", lambda e, pn=h["pn"][0], pp3=h["p3"][0], a_t=h["a"][0]: e.tensor_scalar(out=pn[:], in0=pp3[:], scalar1=a_t[:, 5:6], scalar2=None, op0=ALU.mult),
                              [h["p3"][1], h["a"][1]], [h["pn"][1]])
                    for h in hs:
                        for kk_ in range(2):
                            S.add("pe", lambda e, ptsl=h["ptsl"], kk_=kk_, pn=h["pn"][0]: e.transpose(out=ptsl[:, kk_ * 128:(kk_ + 1) * 128], in_=pn[:, kk_ * 128:(kk_ + 1) * 128], identity=ident_b),
                                  [h["pn"][1], CB], psT_b[0])
                    for h in hs:
                        S.add("act", lambda e, pt2=h["pt"][0], ptsl=h["ptsl"]: e.activation(out=pt2[:], in_=v3(ptsl), func=AF.Copy), psT_b[0], [h["pt"][1]])
                    for qp in (qp0, qp0 + 1):
                        si = sbi * 4 + qp
                        g = qp // 2
                        pO, pOb = pObank[:, qp * 128:(qp + 1) * 128], pObb
                        n_ = 0
                        for h in [h for h in hs if h["qp"] == qp]:
                            pt2, pt2b = h["pt"]
                            hh = h["hh"]
                            for kk_, (vp, vpb) in enumerate([(vprev, vprevb), (vcur, vcurb)]):
                                S.add("pe", lambda e, pO=pO, vp=vp, g=g, hh=hh, pt2=pt2, kk_=kk_, n_=n_: e.matmul(
                                    pO[:, 0:128], lhsT=vp[:, g, hh * 64:hh * 64 + 128], rhs=pt2[:, kk_, :], start=(n_ == 0), stop=(n_ == 3)),
                                    [vpb, pt2b], pOb)
                                n_ += 1
                    for qp in (qp0, qp0 + 1):
                        si = sbi * 4 + qp
                        pO, pOb = pObank[:, qp * 128:(qp + 1) * 128], pObb
                        S.add("dve", lambda e, si=si, pO=pO, csl=csl: e.tensor_tensor(out=zatt(si)[0][:, csl], in0=pO[:, 0:128], in1=sga(si)[0][:, csl], op=ALU.mult),
                              pOb + [sga(si)[1]], [zatt(si)[1]])
                if lc == NOC - 1:
                    dump("za0", zatt(sbi * 4)[0][:], [128, SBT], [zatt(sbi * 4)[1]], BF16)
            if not own or upto < 5:
                continue
            drain(g1)
            fb_mode[0] = 0
            gfb = gen_first_b(sbi + 1) if sbi + 1 < nsb else iter(())
            ffb = mkfill(gfb, 0)
            mTt, mTb = mT()
            def load_j(j):
                return [ws_load([(lambda t: t[:, :, :], w_br[:, :, j * 128:(j + 1) * 128].rearrange("b (h p) c -> p (b h) c", p=128))]),
                        ws_load([(lambda t: t[:, :, :], wcols(O_GT + j * 128))]),
                        ws_load([(lambda t: t[:, :, :], wcols(O_GT + 1024 + j * 128))])]
            for j in range(8):
                cur_w = load_j(j)
                wt, wb = cur_w[0]
                pBr, pBrb = fullbank()
                pBa, pBab = fullbank()
                for hp in range(4):
                    si = sbi * 4 + hp
                    S.add("pe", lambda e, pBr=pBr, wt=wt, hp=hp, si=si: e.matmul(pBr[:, 0:SBT], lhsT=wt[:, hp, :], rhs=zr(si)[0][:], start=(hp == 0), stop=(hp == 3)),
                          wb + [zr(si)[1]], pBrb)
                for hp in range(4):
                    si = sbi * 4 + hp
                    S.add("pe", lambda e, pBa=pBa, wt=wt, hp=hp, si=si: e.matmul(pBa[:, 0:SBT], lhsT=wt[:, 4 + hp, :], rhs=zatt(si)[0][:], start=(hp == 0), stop=(hp == 3)),
                          wb + [zatt(si)[1]], pBab)
                halves = []
                for br in range(2):
                    wt2, wb2 = cur_w[1 + br]
                    pGt, pGtb = bankx()
                    for k in range(8):
                        S.add("pe", lambda e, pGt=pGt, k=k, wt2=wt2, hTt=hTt: e.matmul(
                            pGt[:, 0:SBT], lhsT=wt2[:, k, :], rhs=hTt[:, k, :], start=(k == 0), stop=(k == 7)), wb2 + [hTb], pGtb)
                    sg_, sgb_ = sgt(br)
                    S.add("act", lambda e, sg_=sg_, pGt=pGt: e.activation(out=sg_[:], in_=pGt[:, 0:SBT], func=AF.Sigmoid), pGtb, [sgb_])
                    halves.append((sg_, sgb_))
                m1, m1b = m12(0)
                m2, m2b = m12(1)
                S.add("dve", lambda e, m1=m1, pBr=pBr, sg_=halves[0][0]: e.tensor_tensor(out=m1[:], in0=pBr[:, 0:SBT], in1=sg_[:], op=ALU.mult), pBrb + [halves[0][1]], [m1b])
                S.add("dve", lambda e, m2=m2, pBa=pBa, sg_=halves[1][0]: e.tensor_tensor(out=m2[:], in0=pBa[:, 0:SBT], in1=sg_[:], op=ALU.mult), pBab + [halves[1][1]], [m2b])
                S.add("dve", lambda e, mTt=mTt, j=j, m1=m1, m2=m2: e.tensor_tensor(out=mTt[:, j, :], in0=m1[:], in1=m2[:], op=ALU.add), [m1b, m2b], [mTb])
                ffb()
            for c in range(CPS):
                gch = sbi * CPS + c
                lc = (sbi - own0) * CPS + c
                xr, xrb = xt(gch)
                dma("sp", xr[:], xw[gch * C:(gch + 1) * C, :], [], [xrb])
                for n in range(2):
                    pa, pab = fullbank()
                    for j in range(8):
                        S.add("pe", lambda e, pa=pa, j=j, n=n, mTt=mTt, c=c: e.matmul(
                            pa[:, 0:512], lhsT=mTt[:, j, c * C:(c + 1) * C], rhs=Wout.t[0][:, j, n * 512:(n + 1) * 512], start=(j == 0), stop=(j == 7)),
                            [mTb, Wout_b[j]], pab)
                    S.add("dve", lambda e, xr=xr, pa=pa, n=n: e.tensor_tensor(out=xr[:, n * 512:(n + 1) * 512], in0=pa[:, 0:512], in1=xr[:, n * 512:(n + 1) * 512], op=ALU.add),
                          pab + [xrb], [xrb])
                ft_, fb_ = fst(gch)
                hbt, hbb = hb(gch)
                rms_rstd(xr[:], [xrb], ft_, fb_, hbt[:], hbb)
                S.add("dve", lambda e, xr=xr, ft_=ft_: e.scalar_tensor_tensor(
                    out=xr[:], in0=xr[:], scalar=ft_[:, 2:3], in1=gfin.t[0][:], op0=ALU.mult, op1=ALU.mult), [xrb, fb_, gfin.b[0]], [xrb])
                dma("sp", out_d[lc * C:(lc + 1) * C, :], xr[:], [xrb], [], is_out=True)
            drain(gfb)

        for hp in range(4):
            dump(f"H{hp}", Ht(hp)[0][:], [128, 64], [Ht(hp)[1]])

        semnames = list(Sched.ENG) + [("dma", j) for j in range(Sched.NDMA)]
        sems = {}
        for sk in semnames:
            nm = sk if isinstance(sk, str) else f"dma{sk[1]}"
            sems[sk] = es.enter_context(nc.semaphore("s_" + nm))
        nc._sbuf_left = nc.sbuf_bytes_remaining
        block = es.enter_context(nc.Block())
        S.emit(nc, block, sems)
    nc._dbg_dumps = dump_d
    nc._sched_counts = dict(S.cnt)
    nc._sched_total = S.total
    return nc


def host_consts():
    s = np.arange(128)[:, None]
    t = np.arange(128)[None, :]
    cst = np.zeros((128, 9, 128), np.float32)
    cst[:, 0] = (s == t)
    cst[:, 1] = (s < t)
    cst[:, 2] = (s < t)
    cst[:, 3] = (s > t)
    cst[:, 4] = (s > t)
    cst[:, 5] = (s <= t)
    cst[:, 6] = (s <= t)
    cst[:, 7] = ((s // 64) == (t // 64))
    cst[:, 8] = 1.0
    return cst


def attn_masks(first):
    qi = np.arange(128)[:, None]
    kj = np.arange(256)[None, :]
    dist = qi + 128 - kj
    band = (dist >= 0) & (dist < 128)
    rest = np.where(band, 0.0, -1e30).astype(np.float32)
    fm = np.where(band & (kj >= 128), 0.0, -1e30).astype(np.float32)
    am = np.stack([fm if first else rest, rest], axis=1)
    return np.ascontiguousarray(am)


def pack_params(p):
    pp = np.zeros((128, NPP_IN), np.float32)

    def put(name, vec, n):
        v = np.asarray(vec, np.float32).reshape(n, 128)
        pp[:, PPI[name]:PPI[name] + n] = v.T
    put("mu", p["mu_shift"][0], 13)
    put("w0", p["w0"][0], 4)
    put("a0", p["a0"][0], 4)
    put("kk", p["k_k"][0], 4)
    put("ka", p["k_a"][0], 4)
    put("rk", p["r_k"][0], 4)
    put("gnw", p["gn_w"][0], 4)
    put("gnb", p["gn_b"][0], 4)
    bq = np.asarray(p["b_qkv"][0], np.float32)
    put("bq", bq[0:512], 4)
    bk = bq[512:640]
    pp[:, PPI["bk"] + 0] = np.concatenate([bk[0:64], bk[0:64]])
    pp[:, PPI["bk"] + 1] = np.concatenate([bk[64:128], bk[64:128]])
    sk = np.asarray(p["sinks"][0], np.float32)
    pp[:, PPI["sink"]:PPI["sink"] + 8] = np.broadcast_to(sk[None, :], (128, 8))
    wdi = np.zeros((128, 2, 512), np.float32)
    wdi[0:64, 0] = np.asarray(p["w_decay_up"][0], np.float32)
    wdi[64:128, 1] = np.asarray(p["w_iclr_up"][0], np.float32)
    common = {
        "w_in": np.ascontiguousarray(np.asarray(p["w_in"][0], np.float32)),
        "w_br": np.ascontiguousarray(np.stack([np.asarray(p["w_branch_rwkv"][0], np.float32),
                                               np.asarray(p["w_branch_att"][0], np.float32)])),
        "w_out": np.ascontiguousarray(np.asarray(p["w_out"][0], np.float32)),
        "wdi": np.ascontiguousarray(wdi),
        "pp": pp,
        "gpre_b": np.ascontiguousarray(np.broadcast_to(np.asarray(p["g_pre"][0], np.float32)[None], (128, D))),
        "gfin_b": np.ascontiguousarray(np.broadcast_to(np.asarray(p["g_final"], np.float32)[None], (128, D))),
        "bv_b": np.ascontiguousarray(np.broadcast_to(bq[640:768][None], (128, 128))),
        "cst": host_consts(),
    }
    return common


def kernel(**inputs):
    x = np.asarray(inputs["x"], np.float32)
    common = pack_params(inputs)
    nc = build()
    in_maps = []
    for c in range(NCORES):
        b, q = c // 4, c % 4
        end = (q + 1) * OWN_TOK
        xw = np.zeros((SEQ, D), np.float32)
        xw[SEQ - end:] = x[b, :end]
        m = dict(common)
        m["xw"] = xw
        m["amask"] = attn_masks(q == 0)
        in_maps.append(m)
    res = run_bass_kernel_spmd(nc, in_maps, core_ids=list(range(NCORES)))
    out = np.zeros((2, SEQ, D), np.float32)
    for c in range(NCORES):
        b, q = c // 4, c % 4
        out[b, q * OWN_TOK:(q + 1) * OWN_TOK] = res.results[c]["out"]
    return out
```

```python
import numpy as np
import concourse.bass as bass
import concourse.mybir as mybir
from concourse.bass_utils import run_bass_kernel_spmd

F32 = mybir.dt.float32
BF16 = mybir.dt.bfloat16
AF = mybir.ActivationFunctionType
ALU = mybir.AluOpType
AX = mybir.AxisListType

D = 1024
NCORES = 8
SEQ = 8192
OWN_TOK = 2048
C = 128
SBT = 256
CPS = SBT // C
RMS_EPS = 1e-6
GN_EPS = 64e-5
IN_COLS = 5504
O_SH = 0
O_GR = 1664
O_Q = 2176
O_K = 2688
O_V = 2816
O_GA = 2944
O_GT = 3456

PPI = {}
_n = 0
for _name, _cnt in [("mu", 13), ("w0", 4), ("a0", 4), ("kk", 4), ("ka", 4), ("rk", 4),
                    ("gnw", 4), ("gnb", 4), ("bq", 4), ("bk", 2), ("sink", 8)]:
    PPI[_name] = _n
    _n += _cnt
NPP_IN = _n
for _name, _cnt in [("omu", 13), ("nw0", 4), ("omka", 4), ("na0", 4)]:
    PPI[_name] = _n
    _n += _cnt
NPP = _n


class Buf:
    __slots__ = ("name", "w", "r", "excl")

    def __init__(self, name, excl=False):
        self.name = name
        self.w = None
        self.r = []
        self.excl = excl


class Sched:
    ENG = ("pe", "act", "dve", "pool", "sp")
    NDMA = 24

    def __init__(self, same_sync=True):
        self.ops = {e: [] for e in self.ENG}
        self.cnt = {e: 0 for e in self.ENG}
        self.waited = {e: {} for e in self.ENG}
        self.same_sync = same_sync
        self.dma_val = [0] * self.NDMA
        self.dma_rr = 0
        self.dma_rr2 = 0
        self.out_tokens = []

    def add(self, eng, fn, reads=(), writes=(), dma=False, is_out=False):
        self.total = getattr(self, "total", 0) + 1
        if not dma and self.total > getattr(self, "cut", 10 ** 9):
            return None
        deps = {}

        def need(tk, hard):
            d = deps.get(tk[0])
            if d is None:
                deps[tk[0]] = [tk[1], tk[2], hard]
            else:
                d[0] = max(d[0], tk[1])
                d[2] = d[2] or hard
        for b in reads:
            if b.w is not None:
                need(b.w, True)
            if b.excl:
                for tk in b.r:
                    need(tk, False)
        for b in writes:
            if b.w is not None:
                need(b.w, True)
            for tk in b.r:
                need(tk, False)
        waits = []
        for semkey, (val, src, hard) in deps.items():
            if src == eng and not isinstance(semkey, tuple):
                if eng in ("pe", "sp"):
                    continue
                if not hard or not self.same_sync:
                    continue
            if self.waited[eng].get(semkey, 0) >= val:
                continue
            self.waited[eng][semkey] = val
            waits.append((semkey, val))
        if dma:
            half = self.NDMA // 2
            if eng == "sp":
                j = self.dma_rr
                self.dma_rr = (self.dma_rr + 1) % half
            else:
                j = half + self.dma_rr2
                self.dma_rr2 = (self.dma_rr2 + 1) % half
            semkey = ("dma", j)
            if self.dma_val[j] > 0 and self.waited[eng].get(semkey, 0) < self.dma_val[j]:
                self.waited[eng][semkey] = self.dma_val[j]
                waits.append((semkey, self.dma_val[j]))
            self.dma_val[j] += 16
            tok = (semkey, self.dma_val[j], eng)
            inc = 16
        else:
            self.cnt[eng] += 1
            tok = (eng, self.cnt[eng], eng)
            inc = 1
        for b in reads:
            b.r.append(tok)
        for b in writes:
            b.w = tok
            b.r = []
        if is_out:
            self.out_tokens.append(tok)
        self.ops[eng].append((waits, fn, tok[0], inc))
        return tok

    def emit(self, nc, block, sems):
        engmap = {"pe": block.tensor, "act": block.scalar, "dve": block.vector,
                  "pool": block.gpsimd, "sp": block.sync}
        for e in self.ENG:
            ops = self.ops[e]
            final = list(self.out_tokens) if e == "sp" else ()

            def body(eng, ops=ops, final=final):
                for waits, fn, semkey, inc in ops:
                    for sk, val in waits:
                        eng.wait_ge(sems[sk], val)
                    fn(eng).then_inc(sems[semkey], inc)
                for tk in final:
                    eng.wait_ge(sems[tk[0]], tk[1])
                if final != ():
                    for j in range(self.NDMA):
                        if self.dma_val[j] > 0:
                            eng.wait_ge(sems[("dma", j)], self.dma_val[j])
            engmap[e](body)


def build(nsb=SEQ // SBT, nown=OWN_TOK // SBT, upto=99, dumps=(), same_sync=True, cut=None):
    from contextlib import ExitStack
    nc = bass.Bass("TRN2", target_bir_lowering=False)
    WT = nsb * SBT
    OT = nown * SBT
    NOC = nown * CPS
    S = Sched(same_sync=same_sync)
    if cut is not None:
        S.cut = cut

    def din(name, shape, dt=F32):
        return nc.dram_tensor(name, list(shape), dt, kind="ExternalInput").ap()

    xw = din("xw", [WT, D])
    w_in = din("w_in", [D, IN_COLS])
    w_br = din("w_br", [2, 512, D])
    w_out = din("w_out", [D, D])
    wdi = din("wdi", [128, 2, 512])
    pp_in = din("pp", [128, NPP_IN])
    gpre_d = din("gpre_b", [128, D])
    gfin_d = din("gfin_b", [128, D])
    bv_d = din("bv_b", [128, 128])
    cst_d = din("cst", [128, 9, 128])
    am_d = din("amask", [128, 2, 256])
    out_d = nc.dram_tensor("out", [OT, D], F32, kind="ExternalOutput").ap()
    wib = nc.dram_tensor("wib_scratch", [D, IN_COLS - O_GR], BF16).ap()
    wbb = nc.dram_tensor("wbb_scratch", [2 * 512, D], BF16).ap()
    dump_d = {}

    es = ExitStack()
    with es:
        def sb(name, shape, dt=F32):
            return es.enter_context(nc.sbuf_tensor(name, list(shape), dt))

        def ps(name, shape, dt=F32):
            return es.enter_context(nc.psum_tensor(name, list(shape), dt))

        class T:
            def __init__(self, name, shape, dt=F32, n=1):
                self.t = [sb(f"{name}{i}", shape, dt) for i in range(n)]
                self.b = [Buf(f"{name}{i}") for i in range(n)]
                self.n = n

            def __call__(self, i=0):
                return self.t[i % self.n], self.b[i % self.n]

        def dma(eng, out, in_, reads, writes, is_out=False):
            return S.add(eng, lambda e: e.dma_start(out=out, in_=in_), reads, writes, dma=True, is_out=is_out)

        def dump(name, ap, shape, reads, dt=F32):
            if name not in dumps:
                return
            dd = nc.dram_tensor("dbg_" + name, list(shape), dt, kind="ExternalOutput").ap()
            dump_d[name] = dd
            dma("sp", dd, ap, reads, [], is_out=True)

        xt = T("xt", [128, D], F32, 2)
        cst_b = T("cst_b", [128, 8, 128], BF16)
        cst2 = T("cst2", [128, 1, 128])
        amask = T("amask", [128, 2, 256])
        PP = T("PP", [128, NPP])
        gpre = T("gpre", [128, D])
        gfin = T("gfin", [128, D])
        bvb = T("bvb", [128, 128])
        Wdb = T("Wdb", [128, 3, 512], BF16)
        Wsh = T("Wsh", [128, 8, 1664], BF16)
        Wout = T("Wout", [128, 8, D], BF16)

        stg = xt.t[1][:, :].rearrange("p (a b) -> p a b", a=8)
        dma("sp", stg, cst_d[:, 0:8, :], [], [xt.b[1]])
        dma("sp", cst2.t[0][:], cst_d[:, 8:9, :], [], [cst2.b[0]])
        dma("sp", PP.t[0][:, 0:NPP_IN], pp_in, [], [PP.b[0]])
        dma("sp", gpre.t[0][:], gpre_d, [], [gpre.b[0]])
        stg_w = xt.t[0][:, :].rearrange("p (a b) -> p a b", a=2)
        dma("sp", stg_w, wdi, [], [xt.b[0]])
        S.add("act", lambda e: e.activation(out=Wdb.t[0][:, 0, :], in_=stg_w[:, 0, :], func=AF.Copy), [xt.b[0]], [Wdb.b[0]])
        S.add("act", lambda e: e.activation(out=Wdb.t[0][:, 2, :], in_=stg_w[:, 1, :], func=AF.Copy), [xt.b[0]], [Wdb.b[0]])
        S.add("dve", lambda e: e.tensor_tensor(out=Wdb.t[0][:, 1, :], in0=stg_w[:, 0, :], in1=Wdb.t[0][:, 0, :], op=ALU.subtract), [xt.b[0], Wdb.b[0]], [Wdb.b[0]])
        Wsh_b = [Buf(f"Wsh_k{k}") for k in range(8)]
        for k in range(8):
            S.add("pool", lambda e, k=k: e.dma_start(out=Wsh.t[0][:, k, :], in_=w_in[k * 128:(k + 1) * 128, O_SH:O_SH + 1664]),
                  [], [Wsh_b[k]], dma=True)
        dma("sp", amask.t[0][:], am_d, [], [amask.b[0]])
        dma("sp", gfin.t[0][:], gfin_d, [], [gfin.b[0]])
        dma("sp", bvb.t[0][:], bv_d, [], [bvb.b[0]])
        S.add("dve", lambda e: e.tensor_copy(out=cst_b.t[0][:], in_=stg), [xt.b[1]], [cst_b.b[0]])
        ident_b = cst_b.t[0][:, 0, :]
        mask4 = cst_b.t[0][:, 1:5, :]
        mle2 = cst_b.t[0][:, 5:7, :]
        bones_b = cst_b.t[0][:, 7, :]
        ones_f = cst2.t[0][:, 0, :]
        CB = cst_b.b[0]
        CF = cst2.b[0]
        ppt = PP.t[0]
        PB = PP.b[0]

        def pc(name, i=0):
            j = PPI[name] + i
            return ppt[:, j:j + 1]

        S.add("dve", lambda e: e.tensor_scalar(out=ppt[:, PPI["omu"]:PPI["omu"] + 13], in0=ppt[:, PPI["mu"]:PPI["mu"] + 13],
                                               scalar1=-1.0, scalar2=1.0, op0=ALU.mult, op1=ALU.add), [PB], [PB])
        S.add("dve", lambda e: e.tensor_scalar(out=ppt[:, PPI["nw0"]:PPI["nw0"] + 4], in0=ppt[:, PPI["w0"]:PPI["w0"] + 4],
                                               scalar1=-1.0, scalar2=None, op0=ALU.mult), [PB], [PB])
        S.add("dve", lambda e: e.tensor_scalar(out=ppt[:, PPI["omka"]:PPI["omka"] + 4], in0=ppt[:, PPI["ka"]:PPI["ka"] + 4],
                                               scalar1=-1.0, scalar2=1.0, op0=ALU.mult, op1=ALU.add), [PB], [PB])
        S.add("dve", lambda e: e.tensor_scalar(out=ppt[:, PPI["na0"]:PPI["na0"] + 4], in0=ppt[:, PPI["a0"]:PPI["a0"] + 4],
                                               scalar1=-1.0, scalar2=None, op0=ALU.mult), [PB], [PB])

        psA = [ps(f"psA{i}", [128, 512]) for i in range(2)]
        psA_b = [Buf(f"psA{i}", True) for i in range(2)]
        psT = [ps(f"psT{i}", [128, 1024], BF16) for i in range(2)]
        psT_b = [[Buf(f"psT{i}_{h}", True) for h in range(2)] for i in range(2)]
        psLU = [[ps(f"psL{i}", [128, 512]), ps(f"psU{i}", [128, 512])] for i in range(2)]
        psLU_b = [[[Buf(f"psLU{i}_{lu}_{s}", True) for s in range(4)] for lu in range(2)] for i in range(2)]
        arr = [0]
        prr = [0]
        srr = [0]

        fb_mode = [0]

        def fullbank():
            if fb_mode[0]:
                return psA[1], [psA_b[1]]
            i = arr[0]
            arr[0] = (i + 1) % 2
            return psA[i], [psA_b[i]]

        def pair(ns):
            r = prr[0]
            if (r % 4) + ns > 4:
                r = (r // 4 + 1) * 4
            r %= 8
            p, s = r // 4, r % 4
            prr[0] = (r + ns) % 8
            sl = slice(s * 128, (s + ns) * 128)
            return (psLU[p][0][:, sl], psLU_b[p][0][s:s + ns], psLU[p][1][:, sl], psLU_b[p][1][s:s + ns])

        brr = [0]

        def bankx():
            i = brr[0]
            brr[0] = (i + 1) % 4
            p, lu = i // 2, i % 2
            return psLU[p][lu], list(psLU_b[p][lu])

        def single(ns):
            bk, bb = bankx()
            return bk[:, 0:ns * 128], bb

        hb = T("hb", [128, D], BF16, 1)
        hT = T("hT", [128, 8, SBT], BF16, 2)
        st0 = T("st0", [128, 4], F32, 2)
        shwa = T("shwa", [128, SBT])
        shtmp = T("shtmp", [128, SBT], F32, 1)
        shr = T("shr", [128, SBT], F32, 2)
        shk = T("shk", [128, SBT], F32, 2)
        shv = T("shv", [128, SBT], F32, 2)
        tw = T("tw", [128, SBT])
        tw_hi = T("tw_hi", [128, SBT], BF16)
        tw_lo = T("tw_lo", [128, SBT], BF16)
        t_k2b = T("t_k2b", [128, SBT], BF16)
        t_rkb = T("t_rkb", [128, SBT], BF16)
        Hhl = T("Hhl", [128, 2, 64], BF16, 4)
        t_e1 = T("t_e1", [128, SBT])
        t_ew = T("t_ew", [128, SBT])
        t_a = T("t_a", [128, SBT])
        t_cs = T("t_cs", [128, SBT])
        t_csp = T("t_csp", [128, SBT])
        t_en = T("t_en", [128, SBT])
        t_ep = T("t_ep", [128, SBT])
        t_k2 = T("t_k2", [128, SBT])
        t_kkn = T("t_kkn", [128, SBT])
        t_ab = T("t_ab", [128, SBT])
        t_f = T("t_f", [128, SBT])
        gC = T("gC", [128, CPS], F32, 8)
        AR = T("AR", [128, CPS, 2, C], BF16, 4)
        BT = T("BT", [128, SBT], BF16, 4)
        KT = T("KT", [128, SBT], BF16, 4)
        vbf = T("vbf", [128, SBT], BF16, 4)
        bonus = T("bonus", [128, SBT], BF16, 4)
        tm = T("tm", [128, 4, 128], BF16, 4)
        PZ = T("PZ", [128, 3, SBT], BF16, 8)
        Hbfz = T("Hbfz", [128, 64], BF16, 8)
        qTz = T("qTz", [128, SBT], BF16, 8)
        NG = 4
        NMt = T("NM", [128, 2, 2, 128], BF16, 2 * NG)
        Mak = T("Mak", [128, 2, 128], BF16, NG)
        RBK = T("RBK", [128, 2, 2, 128], BF16, NG)
        PAIRS = [(0, 1), (2, 3)]
        Xtile = T("Xt", [128, 2, 2, 64], BF16, 2 * NG)
        ATbd = T("ATbd", [128, 128], BF16, NG)
        Gsb = T("Gsb", [128, 64], F32, NG)
        Ht = T("Hst", [128, 64], F32, 4)
        s1t = T("s1t", [128, 64], F32, NG)
        Wz = T("Wz", [128, 3, 64], BF16, NG)
        Wp = T("Wp", [128, 2, 64], BF16, NG)
        zlo = T("zlo", [128, 64], F32, NG)
        QT = T("QT", [128, 128], BF16, NG)
        prevcol = T("prevcol", [128, 13])
        kc = T("kcols", [128, 8])
        NWS = 5
        ws = T("ws", [128, 8, 128], BF16, NWS)
        ws_b2 = [Buf(f"ws_b2_{i}") for i in range(NWS)]
        wsrr = [0]
        sgr = T("sgr", [128, SBT], BF16, 4)
        sga = T("sga", [128, SBT], BF16, 4)
        NKS = 4
        KTatt = T("KTatt", [128, NKS * 128], BF16, 2)
        NV = 4
        Vpad = T("Vpad", [128, 2, 192], BF16, NV)
        ysqt = T("ysq", [128, 512], F32, 1)
        yn = T("yn", [128, 512], BF16, 1)
        gst = T("gst", [128, 6, 8], F32, 1)
        t1t = T("t1t", [128, 128], F32, 1)
        zr = T("zr", [128, SBT], BF16, 4)
        zatt = T("zatt", [128, SBT], BF16, 4)
        smt = T("smt", [128, 256], F32, 4)
        p32 = T("p32", [128, 256], F32, 4)
        pnt = T("pnt", [128, 256], BF16, 4)
        ptt = T("ptt", [128, 2, 128], BF16, 4)
        ast = T("ast", [128, 8], F32, 4)
        mT = T("mT", [128, 8, SBT], BF16, 1)
        sgt = T("sgt", [128, SBT], F32, 2)
        m12 = T("m12", [128, SBT], F32, 2)
        fst = T("fst", [128, 4], F32, 2)

        S.add("pool", lambda e: e.memset(prevcol.t[0][:], 0.0), [], [prevcol.b[0]])
        kct = kc.t[0]
        KB = kc.b[0]
        for j, val in enumerate([RMS_EPS, 1.0, -0.5, 1e-12, GN_EPS]):
            S.add("pool", lambda e, j=j, val=val: e.memset(kct[:, j:j + 1], val), [], [KB])
        eps_col = kct[:, 0:1]
        one_col = kct[:, 1:2]
        mhalf_col = kct[:, 2:3]
        tiny_col = kct[:, 3:4]
        gneps_col = kct[:, 4:5]
        for i in range(NG):
            S.add("pool", lambda e, i=i: e.memset(ATbd.t[i][:], 0.0), [], [ATbd.b[i]])
            S.add("pool", lambda e, i=i: e.memset(Wz.t[i][:], 0.0), [], [Wz.b[i]])
        for i in range(4):
            S.add("pool", lambda e, i=i: e.memset(Ht.t[i][:], 0.0), [], [Ht.b[i]])
        for i in range(NV):
            S.add("pool", lambda e, i=i: e.memset(Vpad.t[i][:], 0.0), [], [Vpad.b[i]])
        for i in range(8):
            S.add("pool", lambda e, i=i: e.memset(PZ.t[i][:], 0.0), [], [PZ.b[i]])
            S.add("pool", lambda e, i=i: e.memset(Hbfz.t[i][:], 0.0), [], [Hbfz.b[i]])
            S.add("pool", lambda e, i=i: e.memset(qTz.t[i][:], 0.0), [], [qTz.b[i]])
        for i in range(2):
            S.add("pool", lambda e, i=i: e.memset(KTatt.t[i][:], 0.0), [], [KTatt.b[i]])
        Wout_b = [Buf(f"wout{k}") for k in range(8)]
        for k in range(8):
            S.add("pool", lambda e, k=k: e.dma_start(out=Wout.t[0][:, k, :], in_=w_out[k * 128:(k + 1) * 128, :]),
                  [], [Wout_b[k]], dma=True)

        wib_b = [Buf(f"wib{k}") for k in range(8)]
        wbb_b = [Buf(f"wbb{k}") for k in range(8)]
        w_br_flat = w_br.rearrange("b r c -> (b r) c")
        for k in range(8):
            S.add("pool", lambda e, k=k: e.dma_start(out=wib[k * 128:(k + 1) * 128, :], in_=w_in[k * 128:(k + 1) * 128, O_GR:IN_COLS]),
                  [], [wib_b[k]], dma=True)
        for k in range(8):
            S.add("pool", lambda e, k=k: e.dma_start(out=wbb[k * 128:(k + 1) * 128, :], in_=w_br_flat[k * 128:(k + 1) * 128, :]),
                  [], [wbb_b[k]], dma=True)

        def bcm(ap2, n):
            a = ap2.ap
            return bass.AP(ap2.tensor, ap2.offset, [list(a[0]), [0, n], list(a[1])])

        def bcl(ap2, n):
            a = ap2.ap
            return bass.AP(ap2.tensor, ap2.offset, [list(a[0]), list(a[1]), [0, n]])

        def v3(ap, h=2):
            return ap.rearrange("p (h t) -> p h t", h=h)

        def rms_rstd(in_ap, in_bufs, stt, stb, junk_ap, junk_buf):
            S.add("act", lambda e: e.activation(out=junk_ap, in_=in_ap, func=AF.Square, accum_out=stt[:, 0:1]),
                  in_bufs, [junk_buf, stb])
            S.add("act", lambda e: e.activation(out=stt[:, 1:2], in_=stt[:, 0:1], func=AF.Ln, bias=eps_col, scale=1.0 / D),
                  [stb, KB], [stb])
            S.add("act", lambda e: e.activation(out=stt[:, 2:3], in_=stt[:, 1:2], func=AF.Exp, scale=-0.5), [stb], [stb])

        def ws_load(srcs):
            i = wsrr[0]
            wsrr[0] = (i + 1) % NWS
            t = ws.t[i]
            bufs = [ws.b[i], ws_b2[i]]
            for j, (dfn, dap) in enumerate(srcs):
                S.add("sp", lambda e, dfn=dfn, dap=dap, t=t: e.dma_start(out=dfn(t), in_=dap), wib_b + wbb_b, [bufs[j]], dma=True)
            return t, bufs[:len(srcs)]

        def wcols(c0, n=128):
            return wib[:, c0 - O_GR:c0 - O_GR + n].rearrange("(k p) c -> p k c", p=128)

        def proj_fm(hTt, hTb, wt, wbufs, ncols=SBT, col0=0):
            pa, pab = fullbank()
            for k in range(8):
                S.add("pe", lambda e, pa=pa, k=k, wt=wt, hTt=hTt: e.matmul(
                    pa[:, 0:ncols], lhsT=wt[:, k, :], rhs=hTt[:, k, col0:col0 + ncols], start=(k == 0), stop=(k == 7)),
                    wbufs + [hTb], pab)
            return pa, pab

        P_ = [slice(0, 64), slice(64, 128)]
        own0 = nsb - nown

        def stage2_header():
            swt, swb = shwa()
            twt, twb = tw()
            S.add("act", lambda e, twt=twt, swt=swt: e.activation(out=twt[0:64, :], in_=swt[0:64, :], func=AF.Exp, scale=2.0), [swb], [twb])
            S.add("dve", lambda e, twt=twt: e.tensor_scalar(out=twt[0:64, :], in0=twt[0:64, :], scalar1=1.0, scalar2=None, op0=ALU.add), [twb], [twb])
            S.add("dve", lambda e, twt=twt: e.reciprocal(out=twt[0:64, :], in_=twt[0:64, :]), [twb], [twb])
            S.add("dve", lambda e, twt=twt: e.tensor_scalar(out=twt[0:64, :], in0=twt[0:64, :], scalar1=-2.0, scalar2=1.0, op0=ALU.mult, op1=ALU.add), [twb], [twb])
            S.add("act", lambda e, twt=twt, swt=swt: e.activation(out=twt[64:128, :], in_=swt[64:128, :], func=AF.Copy), [swb, twb], [twb])
            twh, twhb = tw_hi()
            twl, twlb = tw_lo()
            S.add("act", lambda e, twh=twh, twt=twt: e.activation(out=twh[:], in_=twt[:], func=AF.Copy), [twb], [twhb])
            S.add("dve", lambda e, twl=twl, twt=twt, twh=twh: e.tensor_tensor(out=twl[:], in0=twt[:], in1=twh[:], op=ALU.subtract), [twb, twhb], [twlb])
            return dict(twt=twt, twh=twh, twl=twl, twhb=twhb, twlb=twlb, twb=twb)
        def prep_hp(hp, sbi, own, twt=None, twh=None, twl=None, twhb=None, twlb=None, twb=None):
            si = sbi * 4 + hp
            rt, rb = shr(si)
            kt_, kb_ = shk(si)
            vt, vb = shv(si)
            pD, pDb = fullbank()
            hsl = slice(hp * 128, (hp + 1) * 128)
            S.add("pe", lambda e, pD=pD, hsl=hsl, twh=twh: e.matmul(pD[:, 0:SBT], lhsT=Wdb.t[0][:, 0, hsl], rhs=twh[:, :], start=True, stop=False),
                  [Wdb.b[0], twhb], pDb)
            S.add("pe", lambda e, pD=pD, hsl=hsl, twl=twl: e.matmul(pD[:, 0:SBT], lhsT=Wdb.t[0][:, 0, hsl], rhs=twl[:, :], start=False, stop=False),
                  [Wdb.b[0], twlb], pDb)
            S.add("pe", lambda e, pD=pD, hsl=hsl, twh=twh: e.matmul(pD[:, 0:SBT], lhsT=Wdb.t[0][:, 1, hsl], rhs=twh[:, :], start=False, stop=True),
                  [Wdb.b[0], twhb], pDb)
            e1, e1b = t_e1()
            ew, ewb = t_ew()
            at, ab_ = t_a()
            cs, csb = t_cs()
            csp, cspb = t_csp()
            en, enb = t_en()
            k2, k2b = t_k2()
            kkn, kknb = t_kkn()
            abt, abb = t_ab()
            ft, fb = t_f()
            S.add("act", lambda e, e1=e1, pD=pD, hp=hp: e.activation(out=e1[:], in_=pD[:, 0:SBT], func=AF.Exp, bias=pc("nw0", hp), scale=-1.0),
                  pDb + [PB], [e1b])
            pAa, pAb = fullbank()
            S.add("pe", lambda e, pAa=pAa, hsl=hsl, twh=twh: e.matmul(pAa[:, 0:SBT], lhsT=Wdb.t[0][:, 2, hsl], rhs=twh[:, :], start=True, stop=True),
                  [Wdb.b[0], twhb], pAb)
            S.add("act", lambda e, e1=e1: e.activation(out=e1[:], in_=e1[:], func=AF.Ln, bias=one_col), [e1b, KB], [e1b])
            S.add("act", lambda e, e1=e1, ew=ew: e.activation(out=ew[:], in_=e1[:], func=AF.Exp, bias=mhalf_col, scale=-1.0), [e1b, KB], [ewb])
            S.add("act", lambda e, at=at, pAa=pAa, hp=hp: e.activation(out=at[:], in_=pAa[:, 0:SBT], func=AF.Exp, bias=pc("na0", hp), scale=-1.0),
                  pAb + [PB], [ab_])
            yield
            S.add("dve", lambda e, at=at: e.tensor_scalar(out=at[:], in0=at[:], scalar1=1.0, scalar2=None, op0=ALU.add), [ab_], [ab_])
            S.add("dve", lambda e, at=at: e.reciprocal(out=at[:], in_=at[:]), [ab_], [ab_])
            for c in range(CPS):
                S.add("dve", lambda e, cs=cs, ew=ew, c=c: e.tensor_tensor_scan(
                    out=cs[:, c * C:(c + 1) * C], data0=ones_f, data1=ew[:, c * C:(c + 1) * C], initial=0.0,
                    op0=ALU.mult, op1=ALU.add), [ewb, CF], [csb])
            S.add("pool", lambda e, csp=csp, cs=cs, ew=ew: e.tensor_tensor(out=csp[:], in0=cs[:], in1=ew[:], op=ALU.subtract), [csb, ewb], [cspb])
            S.add("act", lambda e, en=en, cs=cs: e.activation(out=en[:], in_=cs[:], func=AF.Exp), [csb], [enb])
            S.add("act", lambda e, csp=csp: e.activation(out=csp[:], in_=csp[:], func=AF.Exp, scale=-1.0), [cspb], [cspb])
            gct, gcb = gC(si)
            S.add("act", lambda e, gct=gct, cs=cs: e.activation(
                out=gct[:, 0:CPS], in_=cs[:, :].rearrange("p (c t) -> p c t", t=C)[:, :, C - 1], func=AF.Exp, scale=-1.0), [csb], [gcb])
            yield
            k2h, k2hb = t_k2b()
            S.add("act", lambda e, k2h=k2h, kt_=kt_, hp=hp: e.activation(out=k2h[:], in_=kt_[:], func=AF.Square, scale=pc("kk", hp)), [kb_, PB], [k2hb])
            pS_, pSb = fullbank()
            S.add("pe", lambda e, pS_=pS_, k2h=k2h: e.matmul(pS_[:, 0:SBT], lhsT=bones_b, rhs=k2h[:], start=True, stop=True), [CB, k2hb], pSb)
            S.add("act", lambda e, k2=k2, pS_=pS_: e.activation(out=k2[:], in_=pS_[:, 0:SBT], func=AF.Ln, bias=tiny_col), pSb + [KB], [k2b])
            S.add("act", lambda e, k2=k2: e.activation(out=k2[:], in_=k2[:], func=AF.Exp, scale=-0.5), [k2b], [k2b])
            S.add("dve", lambda e, kkn=kkn, kt_=kt_, k2=k2, hp=hp: e.scalar_tensor_tensor(
                out=kkn[:], in0=kt_[:], scalar=pc("kk", hp), in1=k2[:], op0=ALU.mult, op1=ALU.mult), [kb_, k2b, PB], [kknb])
            yield
            ARt, ARb = AR(si)
            BTt, BTb = BT(si)
            KTt, KTb = KT(si)
            vbt, vbb = vbf(si)
            S.add("dve", lambda e, ARt=ARt, kkn=kkn, csp=csp: e.scalar_tensor_tensor(
                out=ARt[:, :, 0, :], in0=kkn[:, :].rearrange("p (c t) -> p c t", t=C), scalar=-1.0,
                in1=csp[:, :].rearrange("p (c t) -> p c t", t=C), op0=ALU.mult, op1=ALU.mult), [kknb, cspb], [ARb])
            S.add("pool", lambda e, abt=abt, kkn=kkn, at=at: e.tensor_tensor(out=abt[:], in0=kkn[:], in1=at[:], op=ALU.mult), [kknb, ab_], [abb])
            S.add("pool", lambda e, BTt=BTt, abt=abt, en=en: e.tensor_tensor(out=BTt[:], in0=abt[:], in1=en[:], op=ALU.mult), [abb, enb], [BTb])
            yield
            S.add("dve", lambda e, ft=ft, at=at, hp=hp: e.tensor_scalar(out=ft[:], in0=at[:], scalar1=pc("ka", hp), scalar2=pc("omka", hp),
                                                                    op0=ALU.mult, op1=ALU.add), [ab_, PB], [fb])
            S.add("pool", lambda e, ft=ft, kt_=kt_: e.tensor_tensor(out=ft[:], in0=kt_[:], in1=ft[:], op=ALU.mult), [kb_, fb], [fb])
            S.add("pool", lambda e, KTt=KTt, ft=ft, en=en: e.tensor_tensor(out=KTt[:], in0=ft[:], in1=en[:], op=ALU.mult), [fb, enb], [KTb])
            S.add("act", lambda e, vbt=vbt, vt=vt: e.activation(out=vbt[:], in_=vt[:], func=AF.Copy), [vb], [vbb])
            yield
            for hh in range(2):
                zt, zb = PZ(si * 2 + hh)
                S.add("pool", lambda e, zt=zt, ARt=ARt, hh=hh: e.tensor_copy(out=zt[P_[hh], 0, :].rearrange("p (c t) -> p c t", t=C), in_=ARt[P_[hh], :, 0, :]), [ARb], [zb])
                S.add("pool", lambda e, zt=zt, BTt=BTt, hh=hh: e.tensor_copy(out=zt[P_[hh], 1, :], in_=BTt[P_[hh], :]), [BTb], [zb])
                S.add("pool", lambda e, zt=zt, KTt=KTt, hh=hh: e.tensor_copy(out=zt[P_[hh], 2, :], in_=KTt[P_[hh], :]), [KTb], [zb])
            if own:
                ep, epb = t_ep()
                S.add("act", lambda e, ep=ep, cs=cs: e.activation(out=ep[:], in_=cs[:], func=AF.Exp, scale=-1.0), [csb], [epb])
                S.add("dve", lambda e, ARt=ARt, rt=rt, ep=ep: e.tensor_tensor(
                    out=ARt[:, :, 1, :], in0=rt[:, :].rearrange("p (c t) -> p c t", t=C),
                    in1=ep[:, :].rearrange("p (c t) -> p c t", t=C), op=ALU.mult), [rb, epb], [ARb])
                rkb_t, rkb_b = t_rkb()
                S.add("dve", lambda e, rkb_t=rkb_t, rt=rt, ft=ft, hp=hp: e.scalar_tensor_tensor(
                    out=rkb_t[:], in0=rt[:], scalar=pc("rk", hp), in1=ft[:], op0=ALU.mult, op1=ALU.mult), [rb, fb, PB], [rkb_b])
                pB_, pBb = fullbank()
                S.add("pe", lambda e, pB_=pB_, rkb_t=rkb_t: e.matmul(pB_[:, 0:SBT], lhsT=bones_b, rhs=rkb_t[:], start=True, stop=True), [CB, rkb_b], pBb)
                bnt, bnb = bonus(si)
                S.add("dve", lambda e, bnt=bnt, pB_=pB_, vt=vt: e.tensor_tensor(out=bnt[:], in0=pB_[:, 0:SBT], in1=vt[:], op=ALU.mult), pBb + [vb], [bnb])
            yield
        def proj_tile(sbi, ct, hTt, hTb):
            pa, pab = fullbank()
            for k in range(8):
                S.add("pe", lambda e, pa=pa, k=k, ct=ct, hTt=hTt: e.matmul(
                    pa[:, 0:SBT], lhsT=Wsh.t[0][:, k, ct * 128:(ct + 1) * 128], rhs=hTt[:, k, :],
                    start=(k == 0), stop=(k == 7)), [Wsh_b[k], hTb], pab)
            if ct == 12:
                dst, dstb = shwa()
            else:
                hp = ct % 4
                dst, dstb = (shr, shk, shv)[ct // 4](sbi * 4 + hp)
            tmp, tmpb = shtmp(ct)
            S.add("act", lambda e, tmp=tmp, pa=pa, ct=ct: e.activation(
                out=tmp[:], in_=pa[:, 0:SBT], func=AF.Copy, scale=pc("omu", ct)), pab + [PB], [tmpb])
            S.add("dve", lambda e, dst=dst, pa=pa, tmp=tmp, ct=ct: e.scalar_tensor_tensor(
                out=dst[:, 1:SBT], in0=pa[:, 0:SBT - 1], scalar=pc("mu", ct), in1=tmp[:, 1:SBT],
                op0=ALU.mult, op1=ALU.add), pab + [tmpb, PB], [dstb])
            S.add("dve", lambda e, dst=dst, tmp=tmp, ct=ct: e.scalar_tensor_tensor(
                out=dst[:, 0:1], in0=prevcol.t[0][:, ct:ct + 1], scalar=pc("mu", ct), in1=tmp[:, 0:1],
                op0=ALU.mult, op1=ALU.add), [prevcol.b[0], tmpb, PB], [dstb])
            S.add("act", lambda e, pa=pa, ct=ct: e.activation(
                out=prevcol.t[0][:, ct:ct + 1], in_=pa[:, SBT - 1:SBT], func=AF.Copy), pab, [prevcol.b[0]])

        sbst = {}

        def gen_first(sbi):
            own = sbi >= own0
            hTt, hTb = hT(sbi)
            for j in range(CPS):
                gc = sbi * CPS + j
                xtt, xtb = xt(gc)
                hbt, hbb = hb(gc)
                stt, stb = st0(gc)
                dma("sp", xtt[:], xw[gc * C:(gc + 1) * C, :], [], [xtb])
                rms_rstd(xtt[:], [xtb], stt, stb, hbt[:], hbb)
                S.add("dve", lambda e, xtt=xtt, stt=stt, hbt=hbt: e.scalar_tensor_tensor(
                    out=hbt[:], in0=xtt[:], scalar=stt[:, 2:3], in1=gpre.t[0][:], op0=ALU.mult, op1=ALU.mult),
                    [xtb, stb, gpre.b[0]], [hbb])
                yield
                pst = psT[0]
                pstb = psT_b[0]
                for k in range(8):
                    S.add("pe", lambda e, k=k, hbt=hbt, pst=pst: e.transpose(
                        out=pst[:, k * 128:(k + 1) * 128], in_=hbt[:, k * 128:(k + 1) * 128], identity=ident_b),
                        [hbb, CB], pstb)
                S.add("act", lambda e, pst=pst, hTt=hTt, j=j: e.activation(
                    out=hTt[:, :, j * C:(j + 1) * C], in_=pst[:, :].rearrange("p (k t) -> p k t", k=8), func=AF.Copy),
                    pstb, [hTb])
                yield
            proj_tile(sbi, 12, hTt, hTb)
            tw_ctx = stage2_header()
            sbst[sbi] = (tw_ctx, hTt, hTb)
            yield
            for hp in (0, 1):
                for q in range(3):
                    proj_tile(sbi, q * 4 + hp, hTt, hTb)
                    yield

        def gen_first_b(sbi):
            own = sbi >= own0
            tw_ctx, hTt, hTb = sbst[sbi]
            for hp in (0, 1):
                yield from prep_hp(hp, sbi, own, **tw_ctx)

        def gen_first_ab(sbi):
            yield from gen_first(sbi)
            yield from gen_first_b(sbi)

        def gen_second(sbi):
            own = sbi >= own0
            tw_ctx, hTt, hTb = sbst[sbi]
            for hp in (2, 3):
                for q in range(3):
                    proj_tile(sbi, q * 4 + hp, hTt, hTb)
                    yield
                yield from prep_hp(hp, sbi, own, **tw_ctx)

        def drain(g):
            for _ in g:
                pass

        def mkfill(g, n=1, units=None, slots=None):
            st = [0]

            def fill():
                if units is None:
                    k = n
                else:
                    i = st[0]
                    st[0] += 1
                    k = ((i + 1) * units) // slots - (i * units) // slots
                for _ in range(k):
                    try:
                        next(g)
                    except StopIteration:
                        return
            return fill

        drain(gen_first_ab(0))
        for sbi in range(nsb):
            own = sbi >= own0
            halo_sb = (sbi == own0 - 1)
            hTt, hTb = hT(sbi)
            def gen_ownproj(sbi=sbi, own=own, halo_sb=halo_sb, hTt=hTt, hTb=hTb):
                if own or halo_sb:
                    ncols, col0 = (SBT, 0) if own else (C, SBT - C)
                    kcol = ((sbi - own0) * CPS + 1) * C if own else 0
                    for g in range(2):
                        wt, wb = ws_load([(lambda t: t[:, :, 0:64], wcols(O_K + g * 64, 64)), (lambda t: t[:, :, 64:128], wcols(O_K + g * 64, 64))])
                        pa, pab = proj_fm(hTt, hTb, wt, wb, ncols, col0)
                        for cc in range(ncols // C):
                            ks = ((kcol // C) + cc) % NKS
                            S.add("act", lambda e, pa=pa, g=g, ks=ks, cc=cc: e.activation(
                                out=KTatt.t[g][:, ks * C:(ks + 1) * C], in_=pa[:, cc * C:(cc + 1) * C], func=AF.Identity, bias=pc("bk", g)), pab + [PB], [KTatt.b[g]])
                        yield
                    wt, wb = ws_load([(lambda t: t[:, :, :], wcols(O_V))])
                    for c in (range(CPS) if own else [CPS - 1]):
                        lc1 = (sbi - own0) * CPS + c + 1 if own else 0
                        pa, pab = fullbank()
                        for k in range(8):
                            S.add("pe", lambda e, pa=pa, k=k, wt=wt, hTt=hTt, c=c: e.matmul(
                                pa[:, 0:128], lhsT=hTt[:, k, c * C:(c + 1) * C], rhs=wt[:, k, :], start=(k == 0), stop=(k == 7)), wb + [hTb], pab)
                        vp, vpb = Vpad(lc1)
                        S.add("dve", lambda e, vp=vp, pa=pa: e.tensor_tensor(out=vp[:, :, 0:64], in0=v3(pa[:, 0:128]), in1=v3(bvb.t[0][:, :]), op=ALU.add),
                              pab + [bvb.b[0]], [vpb])
                        S.add("pool", lambda e, vp=vp: e.tensor_copy(out=vp[:, :, 128:192], in_=vp[:, :, 0:64]), [vpb], [vpb])
                        yield
                if own:
                    for ct in range(4):
                        si = sbi * 4 + ct
                        wt, wb = ws_load([(lambda t: t[:, :, :], wcols(O_GR + ct * 128))])
                        pa, pab = proj_fm(hTt, hTb, wt, wb)
                        S.add("act", lambda e, pa=pa, si=si: e.activation(out=sgr(si)[0][:], in_=pa[:, 0:SBT], func=AF.Silu), pab, [sgr(si)[1]])
                        yield
                        wt, wb = ws_load([(lambda t: t[:, :, :], wcols(O_Q + ct * 128))])
                        pa, pab = proj_fm(hTt, hTb, wt, wb)
                        for hh in range(2):
                            qz, qzb = qTz(si * 2 + hh)
                            S.add("act", lambda e, pa=pa, qz=qz, ct=ct, hh=hh: e.activation(
                                out=qz[P_[hh], :], in_=pa[P_[hh], 0:SBT], func=AF.Identity, bias=ppt[P_[hh], PPI["bq"] + ct:PPI["bq"] + ct + 1]),
                                pab + [PB], [qzb])
                        yield
                        wt, wb = ws_load([(lambda t: t[:, :, :], wcols(O_GA + ct * 128))])
                        pa, pab = proj_fm(hTt, hTb, wt, wb)
                        S.add("act", lambda e, pa=pa, si=si: e.activation(out=sga(si)[0][:], in_=pa[:, 0:SBT], func=AF.Silu), pab, [sga(si)[1]])
                        yield

                yield

            def emit_chunk_pairs(c, pairs, fill, own=own, sbi=sbi):
                gch = sbi * CPS + c
                csl = slice(c * C, (c + 1) * C)
                pYbank, pYbb = psA[0], [psA_b[0]]
                def mkctx(hp):
                    si = sbi * 4 + hp
                    gi = gch * 4 + hp
                    x = dict(hp=hp, si=si, gi=gi)
                    x["AR"], x["ARb"] = AR(si)
                    x["BT"], x["BTb"] = BT(si)
                    x["KT"], x["KTb"] = KT(si)
                    x["vb"], x["vbb"] = vbf(si)
                    x["zts"] = [PZ(si * 2 + hh) for hh in range(2)]
                    x["tm"], x["tmb"] = tm(gi)
                    return x

                def g_transposes(x, c=c, csl=csl):
                    pt_ = psT[1][:, 0:512]
                    ptb = psT_b[1]
                    srcs = [(x["AR"][:, c, 0, :], x["ARb"]), (x["BT"][:, csl], x["BTb"]), (x["KT"][:, csl], x["KTb"]), (x["vb"][:, csl], x["vbb"])]
                    for q, (sap, sbf) in enumerate(srcs):
                        S.add("pe", lambda e, pt_=pt_, q=q, sap=sap: e.transpose(out=pt_[:, q * 128:(q + 1) * 128], in_=sap, identity=ident_b),
                              [sbf, CB], ptb)
                    tmt = x["tm"]
                    S.add("act", lambda e, tmt=tmt, pt_=pt_: e.activation(out=tmt[:], in_=pt_.rearrange("p (q t) -> p q t", q=4), func=AF.Copy),
                          ptb, [x["tmb"]])

                def g_sprod_pe(x, c=c, csl=csl):
                    x["pS1"], x["pS1b"] = single(4)
                    x["pS2"], x["pS2b"] = single(2)
                    ARt, BTt = x["AR"], x["BT"]
                    for hh in range(2):
                        zt, zb = x["zts"][hh]
                        S.add("pe", lambda e, pS=x["pS1"], zt=zt, ARt=ARt, hh=hh: e.matmul(
                            pS[:, hh * 128:(hh + 1) * 128], lhsT=zt[:, 1, csl], rhs=ARt[:, c, 0, :], start=True, stop=True), [zb, x["ARb"]], x["pS1b"])
                    for hh in range(2):
                        zt, zb = x["zts"][hh]
                        S.add("pe", lambda e, pS=x["pS1"], zt=zt, BTt=BTt, hh=hh: e.matmul(
                            pS[:, (2 + hh) * 128:(3 + hh) * 128], lhsT=zt[:, 0, csl], rhs=BTt[:, csl], start=True, stop=True), [zb, x["BTb"]], x["pS1b"])
                    for hh in range(2):
                        zt, zb = x["zts"][hh]
                        S.add("pe", lambda e, pS=x["pS2"], zt=zt, ARt=ARt, hh=hh: e.matmul(
                            pS[:, hh * 128:(hh + 1) * 128], lhsT=zt[:, 2, csl], rhs=ARt[:, c, 0, :], start=True, stop=True), [zb, x["ARb"]], x["pS2b"])

                def g_sprod_evac(x):
                    gi = x["gi"]
                    nm, nmb = NMt(gi * 2)
                    mk, mkb = Mak(gi)
                    S.add("dve", lambda e, nm=nm, pS=x["pS1"]: e.tensor_tensor(out=nm[:, :, :, :].rearrange("p a h t -> p (a h) t"), in0=v3(pS, 4), in1=mask4, op=ALU.mult),
                          x["pS1b"] + [CB], [nmb])
                    S.add("dve", lambda e, mk=mk, pS=x["pS2"]: e.tensor_tensor(out=mk[:], in0=v3(pS, 2), in1=mask4[:, 0:2, :], op=ALU.mult),
                          x["pS2b"] + [CB], [mkb])
                    x["nm"], x["nmb"], x["mk"], x["mkb"] = nm, nmb, mk, mkb

                def g_r_pe(x, c=c, csl=csl):
                    x["pR"], x["pRb"] = single(4)
                    ARt = x["AR"]
                    for a_ in range(2):
                        for hh in range(2):
                            zt, zb = x["zts"][hh]
                            S.add("pe", lambda e, pR=x["pR"], zt=zt, ARt=ARt, hh=hh, a_=a_: e.matmul(
                                pR[:, (a_ * 2 + hh) * 128:(a_ * 2 + hh + 1) * 128], lhsT=zt[:, 1 + a_, csl], rhs=ARt[:, c, 1, :], start=True, stop=True),
                                [zb, x["ARb"]], x["pRb"])

                def g_r_evac(x, c=c, csl=csl):
                    rbk, rbkb = RBK(x["gi"])
                    for a_ in range(2):
                        S.add("dve", lambda e, rbk=rbk, pR=x["pR"], a_=a_: e.tensor_tensor(out=rbk[:, a_, :, :], in0=v3(pR[:, a_ * 256:(a_ + 1) * 256], 2), in1=mle2, op=ALU.mult),
                              x["pRb"] + [CB], [rbkb])
                    x["rbk"], x["rbkb"] = rbk, rbkb

                def g_pv_pe(x, c=c, csl=csl):
                    x["pV"], x["pVb"] = single(1)
                    mk, tmt = x["mk"], x["tm"]
                    for hh in range(2):
                        S.add("pe", lambda e, pV=x["pV"], hh=hh, mk=mk, tmt=tmt: e.matmul(
                            pV[:, hh * 64:(hh + 1) * 64], lhsT=mk[:, hh, :], rhs=tmt[:, 3, hh * 64:(hh + 1) * 64], start=True, stop=True), [x["mkb"], x["tmb"]], x["pVb"])

                def g_x0(x, c=c, csl=csl):
                    Xt, Xb = Xtile(x["gi"] * 2)
                    tmt = x["tm"]
                    S.add("pool", lambda e, Xt=Xt, tmt=tmt: e.tensor_copy(out=Xt[:, :, 0, :], in_=tmt[:, 0, :].rearrange("p (h k) -> p h k", h=2)), [x["tmb"]], [Xb])
                    S.add("act", lambda e, Xt=Xt, pV=x["pV"]: e.activation(out=Xt[:, :, 1, :], in_=pV[:, 0:128].rearrange("p (h k) -> p h k", h=2), func=AF.Copy),
                          x["pVb"], [Xb])
                    x["X"], x["Xb"] = Xt, Xb

                def g_level_pe(x, lv):
                    nm, nmb, Xt, Xb = x["nm"], x["nmb"], x["X"], x["Xb"]
                    x["pX"], x["pXb"] = single(2)
                    for hh in range(2):
                        S.add("pe", lambda e, pX=x["pX"], hh=hh, nm=nm, Xt=Xt: e.matmul(
                            pX[:, hh * 128:(hh + 1) * 128], lhsT=nm[:, 0, hh, :], rhs=Xt[:, hh, :, :].rearrange("p a k -> p (a k)"), start=True, stop=True),
                            [nmb, Xb], x["pXb"])
                    if lv < 6:
                        x["pNM"], x["pNMb"] = single(4)
                        for hh in range(2):
                            S.add("pe", lambda e, pN=x["pNM"], hh=hh, nm=nm: e.matmul(
                                pN[:, hh * 128:(hh + 1) * 128], lhsT=nm[:, 1, hh, :], rhs=nm[:, 0, hh, :], start=True, stop=True), [nmb], x["pNMb"])
                        if lv < 5:
                            for hh in range(2):
                                S.add("pe", lambda e, pN=x["pNM"], hh=hh, nm=nm: e.matmul(
                                    pN[:, (2 + hh) * 128:(3 + hh) * 128], lhsT=nm[:, 0, hh, :], rhs=nm[:, 1, hh, :], start=True, stop=True), [nmb], x["pNMb"])

                def g_level_evac(x, lv):
                    gi = x["gi"]
                    Xt, Xb = x["X"], x["Xb"]
                    Xn, Xnb = Xtile(gi * 2 + lv + 1)
                    S.add("dve", lambda e, Xn=Xn, pX=x["pX"], Xt=Xt: e.tensor_tensor(
                        out=Xn[:, :, :, :].rearrange("p h a k -> p h (a k)"), in0=v3(pX), in1=Xt[:, :, :, :].rearrange("p h a k -> p h (a k)"), op=ALU.add),
                        x["pXb"] + [Xb], [Xnb])
                    x["X"], x["Xb"] = Xn, Xnb
                    if lv < 6:
                        nn, nnb = NMt(gi * 2 + lv + 1)
                        w = 4 if lv < 5 else 2
                        S.add("act", lambda e, nn=nn, pN=x["pNM"], w=w: e.activation(
                            out=nn[:, :, :, :].rearrange("p a h t -> p (a h) t")[:, 0:w, :], in_=v3(pN[:, 0:w * 128], w), func=AF.Copy), x["pNMb"], [nnb])
                        x["nm"], x["nmb"] = nn, nnb

                def g_state(x, c=c, csl=csl, own=own, pYbank=(pYbank if own else None), pYbb=(pYbb if own else None)):
                    gi, hp, si = x["gi"], x["hp"], x["si"]
                    Xt, Xb, tmt, tmb = x["X"], x["Xb"], x["tm"], x["tmb"]
                    ARt, ARb = x["AR"], x["ARb"]
                    wz, wzb = Wz(gi)
                    S.add("pool", lambda e, wz=wz, Xt=Xt: e.tensor_copy(out=wz[:, 0::2, :], in_=Xt[:, :, 0, :]), [Xb], [wzb])
                    wzA = wz[:, 0:2, :].rearrange("p a k -> p (a k)")
                    wzB = wz[:, 1:3, :].rearrange("p a k -> p (a k)")
                    wp, wpb = Wp(gi)
                    S.add("pool", lambda e, wp=wp, Xt=Xt: e.tensor_copy(out=wp[:, :, :], in_=Xt[:, :, 0, :]), [Xb], [wpb])
                    pAT, pATb = single(1)
                    S.add("pe", lambda e, pAT=pAT, wp=wp, tmt=tmt: e.matmul(pAT[:, 0:128], lhsT=wp[:, :, :].rearrange("p a k -> p (a k)"), rhs=tmt[:, 1, :], start=True, stop=True),
                          [wpb, tmb], pATb)
                    atb, atbb = ATbd(gi)
                    for hh in range(2):
                        S.add("act", lambda e, atb=atb, pAT=pAT, hh=hh: e.activation(
                            out=atb[P_[hh], hh * 64:(hh + 1) * 64], in_=pAT[P_[hh], hh * 64:(hh + 1) * 64], func=AF.Copy), pATb, [atbb])
                    pG, pGb = single(1)
                    pG2, pG2b = single(1)
                    S.add("pe", lambda e, pG=pG, tmt=tmt: e.matmul(pG[:, 0:128], lhsT=tmt[:, 2, :], rhs=tmt[:, 3, :], start=True, stop=True),
                          [tmb], pGb)
                    for hh in range(2):
                        S.add("pe", lambda e, pG2=pG2, Xt=Xt, tmt=tmt, hh=hh: e.matmul(
                            pG2[:, hh * 64:(hh + 1) * 64], lhsT=tmt[:, 1, :], rhs=Xt[:, hh, 1, :], start=True, stop=True), [Xb, tmb], pG2b)
                    gs, gsb_ = Gsb(gi)
                    for hh in range(2):
                        S.add("act", lambda e, gs=gs, pG=pG, hh=hh: e.activation(
                            out=gs[P_[hh], :], in_=pG[P_[hh], hh * 64:(hh + 1) * 64], func=AF.Copy), pGb, [gsb_])
                        S.add("dve", lambda e, gs=gs, pG2=pG2, hh=hh: e.tensor_tensor(
                            out=gs[P_[hh], :], in0=pG2[P_[hh], hh * 64:(hh + 1) * 64], in1=gs[P_[hh], :], op=ALU.add), pG2b + [gsb_], [gsb_])
                    Htt, Hb_ = Ht(hp)
                    gct, gcb = gC(si)
                    if own:
                        hbz = [Hbfz(gi * 2 + hh) for hh in range(2)]
                        for hh in range(2):
                            S.add("pool", lambda e, hz=hbz[hh][0], Htt=Htt, hh=hh: e.tensor_copy(out=hz[P_[hh], :], in_=Htt[P_[hh], :]), [Hb_], [hbz[hh][1]])
                    hhl, hhlb = Hhl(gi)
                    S.add("pool", lambda e, hhl=hhl, Htt=Htt: e.tensor_copy(out=hhl[:, 0, :], in_=Htt[:]), [Hb_], [hhlb])
                    S.add("pool", lambda e, hhl=hhl, Htt=Htt: e.tensor_tensor(out=hhl[:, 1, :], in0=Htt[:], in1=hhl[:, 0, :], op=ALU.subtract), [Hb_, hhlb], [hhlb])
                    pZ, pZb = single(1)
                    S.add("pe", lambda e, pZ=pZ, atb=atb, hhl=hhl: e.matmul(pZ[:, 0:128], lhsT=atb[:], rhs=hhl[:, :, :].rearrange("p a v -> p (a v)"), start=True, stop=True),
                          [atbb, hhlb], pZb)
                    s1, s1b = s1t(gi)
                    S.add("pool", lambda e, s1=s1, Htt=Htt, gs=gs: e.tensor_tensor(out=s1[:], in0=Htt[:], in1=gs[:], op=ALU.add), [Hb_, gsb_], [s1b])
                    S.add("pool", lambda e, s1=s1, gct=gct: e.tensor_scalar(out=s1[:], in0=s1[:], scalar1=gct[:, c:c + 1], scalar2=1.0, op0=ALU.mult, op1=ALU.mult),
                          [s1b, gcb], [s1b])
                    S.add("dve", lambda e, pZ=pZ, gct=gct, s1=s1: e.scalar_tensor_tensor(
                        out=s1[:], in0=pZ[:, 0:64], scalar=gct[:, c:c + 1], in1=s1[:], op0=ALU.mult, op1=ALU.add), pZb + [gcb, s1b], [s1b])
                    S.add("dve", lambda e, Htt=Htt, pZ=pZ, gct=gct, s1=s1: e.scalar_tensor_tensor(
                        out=Htt[:], in0=pZ[:, 64:128], scalar=gct[:, c:c + 1], in1=s1[:], op0=ALU.mult, op1=ALU.add), pZb + [gcb, s1b], [Hb_])
                    if own:
                        rbk, rbkb = x["rbk"], x["rbkb"]
                        qt_, qtb = QT(gi)
                        for hh, wzX in enumerate((wzA, wzB)):
                            pQ, pQb = single(1)
                            S.add("pe", lambda e, pQ=pQ, wzX=wzX, rbk=rbk, hh=hh: e.matmul(pQ[:, 0:128], lhsT=wzX, rhs=rbk[:, 0, hh, :], start=True, stop=True),
                                  [wzb, rbkb], pQb)
                            S.add("dve", lambda e, qt_=qt_, pQ=pQ, ARt=ARt, hh=hh: e.tensor_tensor(
                                out=qt_[P_[hh], :], in0=pQ[P_[hh], 0:128], in1=ARt[P_[hh], c, 1, :], op=ALU.add), pQb + [ARb], [qtb])
                        for hh in range(2):
                            pY, pYb = pYbank, pYbb
                            ysl = slice((hp * 2 + hh) * 64, (hp * 2 + hh + 1) * 64)
                            S.add("pe", lambda e, pY=pY, ysl=ysl, hh=hh, rbk=rbk, Xt=Xt: e.matmul(
                                pY[:, ysl], lhsT=rbk[:, 0, hh, :], rhs=Xt[:, hh, 1, :], start=True, stop=False), [rbkb, Xb], pYb)
                            S.add("pe", lambda e, pY=pY, ysl=ysl, hh=hh, rbk=rbk, tmt=tmt: e.matmul(
                                pY[:, ysl], lhsT=rbk[:, 1, hh, :], rhs=tmt[:, 3, hh * 64:(hh + 1) * 64], start=False, stop=False), [rbkb, tmb], pYb)
                            S.add("pe", lambda e, pY=pY, ysl=ysl, qt_=qt_, hz=hbz[hh][0]: e.matmul(
                                pY[:, ysl], lhsT=qt_[:, :], rhs=hz[:, :], start=False, stop=True), [qtb, hbz[hh][1]], pYb)

                for pr in pairs:
                    ctxs = [mkctx(hp) for hp in pr]
                    for x in ctxs:
                        g_transposes(x)
                    for x in ctxs:
                        g_sprod_pe(x)
                    for x in ctxs:
                        g_sprod_evac(x)
                    fill()
                    if own:
                        for x in ctxs:
                            g_r_pe(x)
                        for x in ctxs:
                            g_r_evac(x)
                    for x in ctxs:
                        g_pv_pe(x)
                    for x in ctxs:
                        g_x0(x)
                    fill()
                    for lv in range(7):
                        for x in ctxs:
                            g_level_pe(x, lv)
                        for x in ctxs:
                            g_level_evac(x, lv)
                        fill()
                    for x in ctxs:
                        g_state(x)
            if not own:
                if halo_sb:
                    drain(gen_ownproj())
                g2 = gen_second(sbi)
                f2 = mkfill(g2, units=23, slots=17)
                for c in range(CPS):
                    emit_chunk_pairs(c, [PAIRS[0]], f2)
                drain(g2)
                g1 = gen_first_ab(sbi + 1) if sbi + 1 < nsb else iter(())
                f1 = mkfill(g1, units=28, slots=17)
                for c in range(CPS):
                    emit_chunk_pairs(c, [PAIRS[1]], f1)
                drain(g1)
                continue
            fb_mode[0] = 1
            g1 = iter(())
            for c in range(CPS):
                gch = sbi * CPS + c
                csl = slice(c * C, (c + 1) * C)
                pYbank, pYbb = psA[0], [psA_b[0]]
                if c == 0:
                    g2 = gen_second(sbi)
                    emit_chunk_pairs(c, [PAIRS[0]], mkfill(g2, 2))
                    drain(g2)
                    g3 = gen_ownproj()
                    emit_chunk_pairs(c, [PAIRS[1]], mkfill(g3, 2))
                    drain(g3)
                else:
                    if c == 1 and sbi + 1 < nsb:
                        g1 = gen_first(sbi + 1)
                    emit_chunk_pairs(c, [PAIRS[0]], mkfill(g1, 1))
                    emit_chunk_pairs(c, [PAIRS[1]], mkfill(g1, 1))
                lc = (sbi - own0) * CPS + c
                g_t, g_b = gst()
                pY, pYb = pYbank, pYbb
                yq, yqb = ysqt(0)
                S.add("dve", lambda e, g_t=g_t, pY=pY: e.tensor_reduce(out=g_t[:, 0, :], in_=v3(pY[:, 0:512], 8), axis=AX.X, op=ALU.add), pYb, [g_b])
                S.add("act", lambda e, yq=yq, pY=pY: e.activation(out=yq[:], in_=pY[:, 0:512], func=AF.Square), pYb, [yqb])
                S.add("dve", lambda e, g_t=g_t, yq=yq: e.tensor_reduce(out=g_t[:, 1, :], in_=v3(yq[:, :], 8), axis=AX.X, op=ALU.add), [yqb], [g_b])
                S.add("dve", lambda e, g_t=g_t: e.tensor_scalar(out=g_t[:, 2, :], in0=g_t[:, 0, :], scalar1=1.0 / 64, scalar2=None, op0=ALU.mult), [g_b], [g_b])
                S.add("dve", lambda e, g_t=g_t: e.tensor_tensor(out=g_t[:, 3, :], in0=g_t[:, 2, :], in1=g_t[:, 2, :], op=ALU.mult), [g_b], [g_b])
                S.add("dve", lambda e, g_t=g_t: e.scalar_tensor_tensor(out=g_t[:, 4, :], in0=g_t[:, 1, :], scalar=1.0 / 64, in1=g_t[:, 3, :],
                                                                       op0=ALU.mult, op1=ALU.subtract), [g_b], [g_b])
                S.add("act", lambda e, g_t=g_t: e.activation(out=g_t[:, 5, :], in_=g_t[:, 4, :], func=AF.Ln, bias=gneps_col), [g_b, KB], [g_b])
                S.add("act", lambda e, g_t=g_t: e.activation(out=g_t[:, 5, :], in_=g_t[:, 5, :], func=AF.Exp, scale=-0.5), [g_b], [g_b])
                ynt, ynb = yn()
                S.add("dve", lambda e, yq=yq, pY=pY, g_t=g_t: e.tensor_tensor(
                    out=v3(yq[:, :], 8), in0=v3(pY[:, 0:512], 8), in1=bcl(g_t[:, 2, :], 64), op=ALU.subtract), pYb + [g_b, yqb], [yqb])
                S.add("pool", lambda e, ynt=ynt, yq=yq, g_t=g_t: e.tensor_tensor(
                    out=v3(ynt[:, :], 8), in0=v3(yq[:, :], 8), in1=bcl(g_t[:, 5, :], 64), op=ALU.mult), [yqb, g_b], [ynb])
                pt_ = psT[1][:, 0:512]
                ptb = psT_b[1]
                for hp in range(4):
                    S.add("pe", lambda e, pt_=pt_, hp=hp, ynt=ynt: e.transpose(out=pt_[:, hp * 128:(hp + 1) * 128], in_=ynt[:, hp * 128:(hp + 1) * 128], identity=ident_b),
                          [ynb, CB], ptb)
                for hp in range(4):
                    si = sbi * 4 + hp
                    t1, t1b = t1t(hp)
                    S.add("dve", lambda e, t1=t1, pt_=pt_, hp=hp: e.tensor_scalar(out=t1[:], in0=pt_[:, hp * 128:(hp + 1) * 128], scalar1=pc("gnw", hp), scalar2=pc("gnb", hp),
                                                                            op0=ALU.mult, op1=ALU.add), ptb + [PB], [t1b])
                    S.add("pool", lambda e, t1=t1, si=si, csl=csl: e.tensor_tensor(out=t1[:], in0=t1[:], in1=bonus(si)[0][:, csl], op=ALU.add), [t1b, bonus(si)[1]], [t1b])
                    S.add("pool", lambda e, t1=t1, si=si, csl=csl: e.tensor_tensor(out=zr(si)[0][:, csl], in0=t1[:], in1=sgr(si)[0][:, csl], op=ALU.mult),
                          [t1b, sgr(si)[1]], [zr(si)[1]])
                if lc == NOC - 1:
                    dump("zr0", zr(sbi * 4)[0][:], [128, SBT], [zr(sbi * 4)[1]], BF16)
                if upto < 4:
                    continue
                am_i = 0 if lc == 0 else 1
                pObank, pObb = psA[0], [psA_b[0]]
                vprev, vprevb = Vpad(lc)
                vcur, vcurb = Vpad(lc + 1)
                for qp0 in (0, 2):
                    pts = psT[0]
                    hs = []
                    for qp in (qp0, qp0 + 1):
                        si = sbi * 4 + qp
                        g = qp // 2
                        for hh in range(2):
                            hd = qp * 2 + hh
                            pS, pSb_ = single(2)
                            qz, qzb = qTz(si * 2 + hh)
                            for kk_ in range(2):
                                ks = (lc + kk_) % NKS
                                S.add("pe", lambda e, pS=pS, qz=qz, g=g, ks=ks, kk_=kk_, csl=csl: e.matmul(
                                    pS[:, kk_ * 128:(kk_ + 1) * 128], lhsT=qz[:, csl], rhs=KTatt.t[g][:, ks * C:(ks + 1) * C], start=True, stop=True),
                                    [qzb, KTatt.b[g]], pSb_)
                            j4 = (qp - qp0) * 2 + hh
                            hs.append(dict(hd=hd, qp=qp, hh=hh, g=g, si=si, pS=pS, pSb=pSb_, sm=smt(j4), a=ast(j4), p3=p32(j4), pn=pnt(j4), pt=ptt(j4),
                                           ptsl=pts[:, j4 * 256:(j4 + 1) * 256]))
                    for h in hs:
                        S.add("dve", lambda e, sm=h["sm"][0], pS=h["pS"], am_i=am_i: e.scalar_tensor_tensor(
                            out=sm[:], in0=pS[:, 0:256], scalar=0.125, in1=amask.t[0][:, am_i, :], op0=ALU.mult, op1=ALU.add), h["pSb"] + [amask.b[0]], [h["sm"][1]])
                    for h in hs:
                        S.add("dve", lambda e, a_t=h["a"][0], sm=h["sm"][0]: e.tensor_reduce(out=a_t[:, 0:1], in_=sm[:], axis=AX.X, op=ALU.max), [h["sm"][1]], [h["a"][1]])
                    for h in hs:
                        S.add("dve", lambda e, a_t=h["a"][0], hd=h["hd"]: e.tensor_scalar(out=a_t[:, 1:2], in0=a_t[:, 0:1], scalar1=pc("sink", hd), scalar2=-1.0, op0=ALU.max, op1=ALU.mult),
                              [h["a"][1], PB], [h["a"][1]])
                    for h in hs:
                        S.add("act", lambda e, pp3=h["p3"][0], sm=h["sm"][0], a_t=h["a"][0]: e.activation(out=pp3[:], in_=sm[:], func=AF.Exp, bias=a_t[:, 1:2], accum_out=a_t[:, 2:3]),
                              [h["sm"][1], h["a"][1]], [h["p3"][1], h["a"][1]])
                    for h in hs:
                        S.add("act", lambda e, a_t=h["a"][0], hd=h["hd"]: e.activation(out=a_t[:, 3:4], in_=pc("sink", hd), func=AF.Exp, bias=a_t[:, 1:2]), [h["a"][1], PB], [h["a"][1]])
                    for h in hs:
                        S.add("dve", lambda e, a_t=h["a"][0]: e.tensor_tensor(out=a_t[:, 4:5], in0=a_t[:, 2:3], in1=a_t[:, 3:4], op=ALU.add), [h["a"][1]], [h["a"][1]])
                    for h in hs:
                        S.add("dve", lambda e, a_t=h["a"][0]: e.reciprocal(out=a_t[:, 5:6], in_=a_t[:, 4:5]), [h["a"][1]], [h["a"][1]])
                    for h in hs:
                        S.add("dve", lambda e, pn=h["pn"][0], pp3=h["p3"][0], a_t=h["a"][0]: e.tensor_scalar(out=pn[:], in0=pp3[:], scalar1=a_t[:, 5:6], scalar2=None, op0=ALU.mult),
                              [h["p3"][1], h["a"][1]], [h["pn"][1]])
                    for h in hs:
                        for kk_ in range(2):
                            S.add("pe", lambda e, ptsl=h["ptsl"], kk_=kk_, pn=h["pn"][0]: e.transpose(out=ptsl[:, kk_ * 128:(kk_ + 1) * 128], in_=pn[:, kk_ * 128:(kk_ + 1) * 128], identity=ident_b),
                                  [h["pn"][1], CB], psT_b[0])
                    for h in hs:
                        S.add("act", lambda e, pt2=h["pt"][0], ptsl=h["ptsl"]: e.activation(out=pt2[:], in_=v3(ptsl), func=AF.Copy), psT_b[0], [h["pt"][1]])
                    for qp in (qp0, qp0 + 1):
                        si = sbi * 4 + qp
                        g = qp // 2
                        pO, pOb = pObank[:, qp * 128:(qp + 1) * 128], pObb
                        n_ = 0
                        for h in [h for h in hs if h["qp"] == qp]:
                            pt2, pt2b = h["pt"]
                            hh = h["hh"]
                            for kk_, (vp, vpb) in enumerate([(vprev, vprevb), (vcur, vcurb)]):
                                S.add("pe", lambda e, pO=pO, vp=vp, g=g, hh=hh, pt2=pt2, kk_=kk_, n_=n_: e.matmul(
                                    pO[:, 0:128], lhsT=vp[:, g, hh * 64:hh * 64 + 128], rhs=pt2[:, kk_, :], start=(n_ == 0), stop=(n_ == 3)),
                                    [vpb, pt2b], pOb)
                                n_ += 1
                    for qp in (qp0, qp0 + 1):
                        si = sbi * 4 + qp
                        pO, pOb = pObank[:, qp * 128:(qp + 1) * 128], pObb
                        S.add("dve", lambda e, si=si, pO=pO, csl=csl: e.tensor_tensor(out=zatt(si)[0][:, csl], in0=pO[:, 0:128], in1=sga(si)[0][:, csl], op=ALU.mult),
                              pOb + [sga(si)[1]], [zatt(si)[1]])
                if lc == NOC - 1:
                    dump("za0", zatt(sbi * 4)[0][:], [128, SBT], [zatt(sbi * 4)[1]], BF16)
            if not own or upto < 5:
                continue
            drain(g1)
            fb_mode[0] = 0
            gfb = gen_first_b(sbi + 1) if sbi + 1 < nsb else iter(())
            ffb = mkfill(gfb, 0)
            mTt, mTb = mT()
            def load_j(j):
                return [ws_load([(lambda t: t[:, :, :], wbb[:, j * 128:(j + 1) * 128].rearrange("(bh p) c -> p bh c", p=128))]),
                        ws_load([(lambda t: t[:, :, :], wcols(O_GT + j * 128))]),
                        ws_load([(lambda t: t[:, :, :], wcols(O_GT + 1024 + j * 128))])]
            for j in range(8):
                cur_w = load_j(j)
                wt, wb = cur_w[0]
                pBr, pBrb = fullbank()
                pBa, pBab = fullbank()
                for hp in range(4):
                    si = sbi * 4 + hp
                    S.add("pe", lambda e, pBr=pBr, wt=wt, hp=hp, si=si: e.matmul(pBr[:, 0:SBT], lhsT=wt[:, hp, :], rhs=zr(si)[0][:], start=(hp == 0), stop=(hp == 3)),
                          wb + [zr(si)[1]], pBrb)
                for hp in range(4):
                    si = sbi * 4 + hp
                    S.add("pe", lambda e, pBa=pBa, wt=wt, hp=hp, si=si: e.matmul(pBa[:, 0:SBT], lhsT=wt[:, 4 + hp, :], rhs=zatt(si)[0][:], start=(hp == 0), stop=(hp == 3)),
                          wb + [zatt(si)[1]], pBab)
                halves = []
                for br in range(2):
                    wt2, wb2 = cur_w[1 + br]
                    pGt, pGtb = bankx()
                    for k in range(8):
                        S.add("pe", lambda e, pGt=pGt, k=k, wt2=wt2, hTt=hTt: e.matmul(
                            pGt[:, 0:SBT], lhsT=wt2[:, k, :], rhs=hTt[:, k, :], start=(k == 0), stop=(k == 7)), wb2 + [hTb], pGtb)
                    sg_, sgb_ = sgt(br)
                    S.add("act", lambda e, sg_=sg_, pGt=pGt: e.activation(out=sg_[:], in_=pGt[:, 0:SBT], func=AF.Sigmoid), pGtb, [sgb_])
                    halves.append((sg_, sgb_))
                m1, m1b = m12(0)
                m2, m2b = m12(1)
                S.add("dve", lambda e, m1=m1, pBr=pBr, sg_=halves[0][0]: e.tensor_tensor(out=m1[:], in0=pBr[:, 0:SBT], in1=sg_[:], op=ALU.mult), pBrb + [halves[0][1]], [m1b])
                S.add("dve", lambda e, m2=m2, pBa=pBa, sg_=halves[1][0]: e.tensor_tensor(out=m2[:], in0=pBa[:, 0:SBT], in1=sg_[:], op=ALU.mult), pBab + [halves[1][1]], [m2b])
                S.add("dve", lambda e, mTt=mTt, j=j, m1=m1, m2=m2: e.tensor_tensor(out=mTt[:, j, :], in0=m1[:], in1=m2[:], op=ALU.add), [m1b, m2b], [mTb])
                ffb()
            for c in range(CPS):
                gch = sbi * CPS + c
                lc = (sbi - own0) * CPS + c
                xr, xrb = xt(gch)
                dma("sp", xr[:], xw[gch * C:(gch + 1) * C, :], [], [xrb])
                for n in range(2):
                    pa, pab = fullbank()
                    for j in range(8):
                        S.add("pe", lambda e, pa=pa, j=j, n=n, mTt=mTt, c=c: e.matmul(
                            pa[:, 0:512], lhsT=mTt[:, j, c * C:(c + 1) * C], rhs=Wout.t[0][:, j, n * 512:(n + 1) * 512], start=(j == 0), stop=(j == 7)),
                            [mTb, Wout_b[j]], pab)
                    S.add("dve", lambda e, xr=xr, pa=pa, n=n: e.tensor_tensor(out=xr[:, n * 512:(n + 1) * 512], in0=pa[:, 0:512], in1=xr[:, n * 512:(n + 1) * 512], op=ALU.add),
                          pab + [xrb], [xrb])
                ft_, fb_ = fst(gch)
                hbt, hbb = hb(gch)
                rms_rstd(xr[:], [xrb], ft_, fb_, hbt[:], hbb)
                S.add("dve", lambda e, xr=xr, ft_=ft_: e.scalar_tensor_tensor(
                    out=xr[:], in0=xr[:], scalar=ft_[:, 2:3], in1=gfin.t[0][:], op0=ALU.mult, op1=ALU.mult), [xrb, fb_, gfin.b[0]], [xrb])
                dma("sp", out_d[lc * C:(lc + 1) * C, :], xr[:], [xrb], [], is_out=True)
            drain(gfb)

        for hp in range(4):
            dump(f"H{hp}", Ht(hp)[0][:], [128, 64], [Ht(hp)[1]])

        semnames = list(Sched.ENG) + [("dma", j) for j in range(Sched.NDMA)]
        sems = {}
        for sk in semnames:
            nm = sk if isinstance(sk, str) else f"dma{sk[1]}"
            sems[sk] = es.enter_context(nc.semaphore("s_" + nm))
        nc._sbuf_left = nc.sbuf_bytes_remaining
        block = es.enter_context(nc.Block())
        S.emit(nc, block, sems)
    nc._dbg_dumps = dump_d
    nc._sched_counts = dict(S.cnt)
    nc._sched_total = S.total
    return nc


def host_consts():
    s = np.arange(128)[:, None]
    t = np.arange(128)[None, :]
    cst = np.zeros((128, 9, 128), np.float32)
    cst[:, 0] = (s == t)
    cst[:, 1] = (s < t)
    cst[:, 2] = (s < t)
    cst[:, 3] = (s > t)
    cst[:, 4] = (s > t)
    cst[:, 5] = (s <= t)
    cst[:, 6] = (s <= t)
    cst[:, 7] = ((s // 64) == (t // 64))
    cst[:, 8] = 1.0
    return cst


def attn_masks(first):
    qi = np.arange(128)[:, None]
    kj = np.arange(256)[None, :]
    dist = qi + 128 - kj
    band = (dist >= 0) & (dist < 128)
    rest = np.where(band, 0.0, -1e30).astype(np.float32)
    fm = np.where(band & (kj >= 128), 0.0, -1e30).astype(np.float32)
    am = np.stack([fm if first else rest, rest], axis=1)
    return np.ascontiguousarray(am)


def pack_params(p):
    pp = np.zeros((128, NPP_IN), np.float32)

    def put(name, vec, n):
        v = np.asarray(vec, np.float32).reshape(n, 128)
        pp[:, PPI[name]:PPI[name] + n] = v.T
    put("mu", p["mu_shift"][0], 13)
    put("w0", p["w0"][0], 4)
    put("a0", p["a0"][0], 4)
    put("kk", p["k_k"][0], 4)
    put("ka", p["k_a"][0], 4)
    put("rk", p["r_k"][0], 4)
    put("gnw", p["gn_w"][0], 4)
    put("gnb", p["gn_b"][0], 4)
    bq = np.asarray(p["b_qkv"][0], np.float32)
    put("bq", bq[0:512], 4)
    bk = bq[512:640]
    pp[:, PPI["bk"] + 0] = np.concatenate([bk[0:64], bk[0:64]])
    pp[:, PPI["bk"] + 1] = np.concatenate([bk[64:128], bk[64:128]])
    sk = np.asarray(p["sinks"][0], np.float32)
    pp[:, PPI["sink"]:PPI["sink"] + 8] = np.broadcast_to(sk[None, :], (128, 8))
    wdi = np.zeros((128, 2, 512), np.float32)
    wdi[0:64, 0] = np.asarray(p["w_decay_up"][0], np.float32)
    wdi[64:128, 1] = np.asarray(p["w_iclr_up"][0], np.float32)
    common = {
        "w_in": np.ascontiguousarray(np.asarray(p["w_in"][0], np.float32)),
        "w_br": np.ascontiguousarray(np.stack([np.asarray(p["w_branch_rwkv"][0], np.float32),
                                               np.asarray(p["w_branch_att"][0], np.float32)])),
        "w_out": np.ascontiguousarray(np.asarray(p["w_out"][0], np.float32)),
        "wdi": np.ascontiguousarray(wdi),
        "pp": pp,
        "gpre_b": np.ascontiguousarray(np.broadcast_to(np.asarray(p["g_pre"][0], np.float32)[None], (128, D))),
        "gfin_b": np.ascontiguousarray(np.broadcast_to(np.asarray(p["g_final"], np.float32)[None], (128, D))),
        "bv_b": np.ascontiguousarray(np.broadcast_to(bq[640:768][None], (128, 128))),
        "cst": host_consts(),
    }
    return common


def kernel(**inputs):
    x = np.asarray(inputs["x"], np.float32)
    common = pack_params(inputs)
    nc = build()
    in_maps = []
    for c in range(NCORES):
        b, q = c // 4, c % 4
        end = (q + 1) * OWN_TOK
        xw = np.zeros((SEQ, D), np.float32)
        xw[SEQ - end:] = x[b, :end]
        m = dict(common)
        m["xw"] = xw
        m["amask"] = attn_masks(q == 0)
        in_maps.append(m)
    res = run_bass_kernel_spmd(nc, in_maps, core_ids=list(range(NCORES)))
    out = np.zeros((2, SEQ, D), np.float32)
    for c in range(NCORES):
        b, q = c // 4, c % 4
        out[b, q * OWN_TOK:(q + 1) * OWN_TOK] = res.results[c]["out"]
    return out
```

```python
import numpy as np
import concourse.bass as bass
import concourse.mybir as mybir
from concourse.bass_utils import run_bass_kernel_spmd

F32 = mybir.dt.float32
BF16 = mybir.dt.bfloat16
AF = mybir.ActivationFunctionType
ALU = mybir.AluOpType
AX = mybir.AxisListType

D = 1024
NCORES = 8
SEQ = 8192
OWN_TOK = 2048
C = 128
SBT = 256
CPS = SBT // C
RMS_EPS = 1e-6
GN_EPS = 64e-5
IN_COLS = 5504
O_SH = 0
O_GR = 1664
O_Q = 2176
O_K = 2688
O_V = 2816
O_GA = 2944
O_GT = 3456

PPI = {}
_n = 0
for _name, _cnt in [("mu", 13), ("w0", 4), ("a0", 4), ("kk", 4), ("ka", 4), ("rk", 4),
                    ("gnw", 4), ("gnb", 4), ("bq", 4), ("bk", 2), ("sink", 8)]:
    PPI[_name] = _n
    _n += _cnt
NPP_IN = _n
for _name, _cnt in [("omu", 13), ("nw0", 4), ("omka", 4), ("na0", 4)]:
    PPI[_name] = _n
    _n += _cnt
NPP = _n


class Buf:
    __slots__ = ("name", "w", "r", "excl")

    def __init__(self, name, excl=False):
        self.name = name
        self.w = None
        self.r = []
        self.excl = excl


class Sched:
    ENG = ("pe", "act", "dve", "pool", "sp")
    NDMA = 24

    def __init__(self, same_sync=True):
        self.ops = {e: [] for e in self.ENG}
        self.cnt = {e: 0 for e in self.ENG}
        self.waited = {e: {} for e in self.ENG}
        self.same_sync = same_sync
        self.dma_val = [0] * self.NDMA
        self.dma_rr = 0
        self.dma_rr2 = 0
        self.out_tokens = []

    def add(self, eng, fn, reads=(), writes=(), dma=False, is_out=False):
        self.total = getattr(self, "total", 0) + 1
        if not dma and self.total > getattr(self, "cut", 10 ** 9):
            return None
        deps = {}

        def need(tk, hard):
            d = deps.get(tk[0])
            if d is None:
                deps[tk[0]] = [tk[1], tk[2], hard]
            else:
                d[0] = max(d[0], tk[1])
                d[2] = d[2] or hard
        for b in reads:
            if b.w is not None:
                need(b.w, True)
            if b.excl:
                for tk in b.r:
                    need(tk, False)
        for b in writes:
            if b.w is not None:
                need(b.w, True)
            for tk in b.r:
                need(tk, False)
        waits = []
        for semkey, (val, src, hard) in deps.items():
            if src == eng and not isinstance(semkey, tuple):
                if eng in ("pe", "sp"):
                    continue
                if not hard or not self.same_sync:
                    continue
            if self.waited[eng].get(semkey, 0) >= val:
                continue
            self.waited[eng][semkey] = val
            waits.append((semkey, val))
        if dma:
            half = self.NDMA // 2
            if eng == "sp":
                j = self.dma_rr
                self.dma_rr = (self.dma_rr + 1) % half
            else:
                j = half + self.dma_rr2
                self.dma_rr2 = (self.dma_rr2 + 1) % half
            semkey = ("dma", j)
            if self.dma_val[j] > 0 and self.waited[eng].get(semkey, 0) < self.dma_val[j]:
                self.waited[eng][semkey] = self.dma_val[j]
                waits.append((semkey, self.dma_val[j]))
            self.dma_val[j] += 16
            tok = (semkey, self.dma_val[j], eng)
            inc = 16
        else:
            self.cnt[eng] += 1
            tok = (eng, self.cnt[eng], eng)
            inc = 1
        for b in reads:
            b.r.append(tok)
        for b in writes:
            b.w = tok
            b.r = []
        if is_out:
            self.out_tokens.append(tok)
        self.ops[eng].append((waits, fn, tok[0], inc))
        return tok

    def emit(self, nc, block, sems):
        engmap = {"pe": block.tensor, "act": block.scalar, "dve": block.vector,
                  "pool": block.gpsimd, "sp": block.sync}
        for e in self.ENG:
            ops = self.ops[e]
            final = list(self.out_tokens) if e == "sp" else ()

            def body(eng, ops=ops, final=final):
                for waits, fn, semkey, inc in ops:
                    for sk, val in waits:
                        eng.wait_ge(sems[sk], val)
                    fn(eng).then_inc(sems[semkey], inc)
                for tk in final:
                    eng.wait_ge(sems[tk[0]], tk[1])
                if final != ():
                    for j in range(self.NDMA):
                        if self.dma_val[j] > 0:
                            eng.wait_ge(sems[("dma", j)], self.dma_val[j])
            engmap[e](body)


def build(nsb=SEQ // SBT, nown=OWN_TOK // SBT, upto=99, dumps=(), same_sync=True, cut=None):
    from contextlib import ExitStack
    nc = bass.Bass("TRN2", target_bir_lowering=False)
    WT = nsb * SBT
    OT = nown * SBT
    NOC = nown * CPS
    S = Sched(same_sync=same_sync)
    if cut is not None:
        S.cut = cut

    def din(name, shape, dt=F32):
        return nc.dram_tensor(name, list(shape), dt, kind="ExternalInput").ap()

    xw = din("xw", [WT, D])
    w_in = din("w_in", [D, IN_COLS])
    w_br = din("w_br", [2, 512, D])
    w_out = din("w_out", [D, D])
    wdi = din("wdi", [128, 2, 512])
    pp_in = din("pp", [128, NPP_IN])
    gpre_d = din("gpre_b", [128, D])
    gfin_d = din("gfin_b", [128, D])
    bv_d = din("bv_b", [128, 128])
    cst_d = din("cst", [128, 9, 128])
    am_d = din("amask", [128, 2, 256])
    out_d = nc.dram_tensor("out", [OT, D], F32, kind="ExternalOutput").ap()
    wib = nc.dram_tensor("wib_scratch", [D, IN_COLS - O_GR], BF16).ap()
    wbb = nc.dram_tensor("wbb_scratch", [2 * 512, D], BF16).ap()
    dump_d = {}

    es = ExitStack()
    with es:
        def sb(name, shape, dt=F32):
            return es.enter_context(nc.sbuf_tensor(name, list(shape), dt))

        def ps(name, shape, dt=F32):
            return es.enter_context(nc.psum_tensor(name, list(shape), dt))

        class T:
            def __init__(self, name, shape, dt=F32, n=1):
                self.t = [sb(f"{name}{i}", shape, dt) for i in range(n)]
                self.b = [Buf(f"{name}{i}") for i in range(n)]
                self.n = n

            def __call__(self, i=0):
                return self.t[i % self.n], self.b[i % self.n]

        def dma(eng, out, in_, reads, writes, is_out=False):
            return S.add(eng, lambda e: e.dma_start(out=out, in_=in_), reads, writes, dma=True, is_out=is_out)

        def dump(name, ap, shape, reads, dt=F32):
            if name not in dumps:
                return
            dd = nc.dram_tensor("dbg_" + name, list(shape), dt, kind="ExternalOutput").ap()
            dump_d[name] = dd
            dma("sp", dd, ap, reads, [], is_out=True)

        xt = T("xt", [128, D], F32, 2)
        cst_b = T("cst_b", [128, 8, 128], BF16)
        cst2 = T("cst2", [128, 1, 128])
        amask = T("amask", [128, 2, 256])
        PP = T("PP", [128, NPP])
        gpre = T("gpre", [128, D])
        gfin = T("gfin", [128, D])
        bvb = T("bvb", [128, 128])
        Wdb = T("Wdb", [128, 3, 512], BF16)
        Wsh = T("Wsh", [128, 8, 1664], BF16)
        Wout = T("Wout", [128, 8, D], BF16)

        stg = xt.t[1][:, :].rearrange("p (a b) -> p a b", a=8)
        dma("sp", stg, cst_d[:, 0:8, :], [], [xt.b[1]])
        dma("sp", cst2.t[0][:], cst_d[:, 8:9, :], [], [cst2.b[0]])
        dma("sp", PP.t[0][:, 0:NPP_IN], pp_in, [], [PP.b[0]])
        dma("sp", gpre.t[0][:], gpre_d, [], [gpre.b[0]])
        stg_w = xt.t[0][:, :].rearrange("p (a b) -> p a b", a=2)
        dma("sp", stg_w, wdi, [], [xt.b[0]])
        S.add("act", lambda e: e.activation(out=Wdb.t[0][:, 0, :], in_=stg_w[:, 0, :], func=AF.Copy), [xt.b[0]], [Wdb.b[0]])
        S.add("act", lambda e: e.activation(out=Wdb.t[0][:, 2, :], in_=stg_w[:, 1, :], func=AF.Copy), [xt.b[0]], [Wdb.b[0]])
        S.add("dve", lambda e: e.tensor_tensor(out=Wdb.t[0][:, 1, :], in0=stg_w[:, 0, :], in1=Wdb.t[0][:, 0, :], op=ALU.subtract), [xt.b[0], Wdb.b[0]], [Wdb.b[0]])
        Wsh_b = [Buf(f"Wsh_k{k}") for k in range(8)]
        for k in range(8):
            S.add("pool", lambda e, k=k: e.dma_start(out=Wsh.t[0][:, k, :], in_=w_in[k * 128:(k + 1) * 128, O_SH:O_SH + 1664]),
                  [], [Wsh_b[k]], dma=True)
        dma("sp", amask.t[0][:], am_d, [], [amask.b[0]])
        dma("sp", gfin.t[0][:], gfin_d, [], [gfin.b[0]])
        dma("sp", bvb.t[0][:], bv_d, [], [bvb.b[0]])
        S.add("dve", lambda e: e.tensor_copy(out=cst_b.t[0][:], in_=stg), [xt.b[1]], [cst_b.b[0]])
        ident_b = cst_b.t[0][:, 0, :]
        mask4 = cst_b.t[0][:, 1:5, :]
        mle2 = cst_b.t[0][:, 5:7, :]
        bones_b = cst_b.t[0][:, 7, :]
        ones_f = cst2.t[0][:, 0, :]
        CB = cst_b.b[0]
        CF = cst2.b[0]
        ppt = PP.t[0]
        PB = PP.b[0]

        def pc(name, i=0):
            j = PPI[name] + i
            return ppt[:, j:j + 1]

        S.add("dve", lambda e: e.tensor_scalar(out=ppt[:, PPI["omu"]:PPI["omu"] + 13], in0=ppt[:, PPI["mu"]:PPI["mu"] + 13],
                                               scalar1=-1.0, scalar2=1.0, op0=ALU.mult, op1=ALU.add), [PB], [PB])
        S.add("dve", lambda e: e.tensor_scalar(out=ppt[:, PPI["nw0"]:PPI["nw0"] + 4], in0=ppt[:, PPI["w0"]:PPI["w0"] + 4],
                                               scalar1=-1.0, scalar2=None, op0=ALU.mult), [PB], [PB])
        S.add("dve", lambda e: e.tensor_scalar(out=ppt[:, PPI["omka"]:PPI["omka"] + 4], in0=ppt[:, PPI["ka"]:PPI["ka"] + 4],
                                               scalar1=-1.0, scalar2=1.0, op0=ALU.mult, op1=ALU.add), [PB], [PB])
        S.add("dve", lambda e: e.tensor_scalar(out=ppt[:, PPI["na0"]:PPI["na0"] + 4], in0=ppt[:, PPI["a0"]:PPI["a0"] + 4],
                                               scalar1=-1.0, scalar2=None, op0=ALU.mult), [PB], [PB])

        psA = [ps(f"psA{i}", [128, 512]) for i in range(2)]
        psA_b = [Buf(f"psA{i}", True) for i in range(2)]
        psT = [ps(f"psT{i}", [128, 1024], BF16) for i in range(2)]
        psT_b = [[Buf(f"psT{i}_{h}", True) for h in range(2)] for i in range(2)]
        psLU = [[ps(f"psL{i}", [128, 512]), ps(f"psU{i}", [128, 512])] for i in range(2)]
        psLU_b = [[[Buf(f"psLU{i}_{lu}_{s}", True) for s in range(4)] for lu in range(2)] for i in range(2)]
        arr = [0]
        prr = [0]
        srr = [0]

        fb_mode = [0]

        def fullbank():
            if fb_mode[0]:
                return psA[1], [psA_b[1]]
            i = arr[0]
            arr[0] = (i + 1) % 2
            return psA[i], [psA_b[i]]

        def pair(ns):
            r = prr[0]
            if (r % 4) + ns > 4:
                r = (r // 4 + 1) * 4
            r %= 8
            p, s = r // 4, r % 4
            prr[0] = (r + ns) % 8
            sl = slice(s * 128, (s + ns) * 128)
            return (psLU[p][0][:, sl], psLU_b[p][0][s:s + ns], psLU[p][1][:, sl], psLU_b[p][1][s:s + ns])

        brr = [0]

        def bankx():
            i = brr[0]
            brr[0] = (i + 1) % 4
            p, lu = i // 2, i % 2
            return psLU[p][lu], list(psLU_b[p][lu])

        def single(ns):
            bk, bb = bankx()
            return bk[:, 0:ns * 128], bb

        hb = T("hb", [128, D], BF16, 1)
        hT = T("hT", [128, 8, SBT], BF16, 2)
        st0 = T("st0", [128, 4], F32, 2)
        shwa = T("shwa", [128, SBT])
        shtmp = T("shtmp", [128, SBT], F32, 1)
        shr = T("shr", [128, SBT], F32, 2)
        shk = T("shk", [128, SBT], F32, 2)
        shv = T("shv", [128, SBT], F32, 2)
        tw = T("tw", [128, SBT])
        tw_hi = T("tw_hi", [128, SBT], BF16)
        tw_lo = T("tw_lo", [128, SBT], BF16)
        t_k2b = T("t_k2b", [128, SBT], BF16)
        t_rkb = T("t_rkb", [128, SBT], BF16)
        Hhl = T("Hhl", [128, 2, 64], BF16, 4)
        t_e1 = T("t_e1", [128, SBT])
        t_ew = T("t_ew", [128, SBT])
        t_a = T("t_a", [128, SBT])
        t_cs = T("t_cs", [128, SBT])
        t_csp = T("t_csp", [128, SBT])
        t_en = T("t_en", [128, SBT])
        t_ep = T("t_ep", [128, SBT])
        t_k2 = T("t_k2", [128, SBT])
        t_kkn = T("t_kkn", [128, SBT])
        t_ab = T("t_ab", [128, SBT])
        t_f = T("t_f", [128, SBT])
        gC = T("gC", [128, CPS], F32, 8)
        AR = T("AR", [128, CPS, 2, C], BF16, 4)
        BT = T("BT", [128, SBT], BF16, 4)
        KT = T("KT", [128, SBT], BF16, 4)
        vbf = T("vbf", [128, SBT], BF16, 4)
        bonus = T("bonus", [128, SBT], BF16, 4)
        tm = T("tm", [128, 4, 128], BF16, 4)
        PZ = T("PZ", [128, 3, SBT], BF16, 8)
        Hbfz = T("Hbfz", [128, 64], BF16, 8)
        qTz = T("qTz", [128, SBT], BF16, 8)
        NG = 4
        NMt = T("NM", [128, 2, 2, 128], BF16, 2 * NG)
        Mak = T("Mak", [128, 2, 128], BF16, NG)
        RBK = T("RBK", [128, 2, 2, 128], BF16, NG)
        PAIRS = [(0, 1), (2, 3)]
        Xtile = T("Xt", [128, 2, 2, 64], BF16, 2 * NG)
        ATbd = T("ATbd", [128, 128], BF16, NG)
        Gsb = T("Gsb", [128, 64], F32, NG)
        Ht = T("Hst", [128, 64], F32, 4)
        s1t = T("s1t", [128, 64], F32, NG)
        Wz = T("Wz", [128, 3, 64], BF16, NG)
        Wp = T("Wp", [128, 2, 64], BF16, NG)
        zlo = T("zlo", [128, 64], F32, NG)
        QT = T("QT", [128, 128], BF16, NG)
        prevcol = T("prevcol", [128, 13])
        kc = T("kcols", [128, 8])
        NWS = 5
        ws = T("ws", [128, 8, 128], BF16, NWS)
        ws_b2 = [Buf(f"ws_b2_{i}") for i in range(NWS)]
        wsrr = [0]
        sgr = T("sgr", [128, SBT], BF16, 4)
        sga = T("sga", [128, SBT], BF16, 4)
        NKS = 4
        KTatt = T("KTatt", [128, NKS * 128], BF16, 2)
        NV = 4
        Vpad = T("Vpad", [128, 2, 192], BF16, NV)
        ysqt = T("ysq", [128, 512], F32, 1)
        yn = T("yn", [128, 512], BF16, 1)
        gst = T("gst", [128, 6, 8], F32, 1)
        t1t = T("t1t", [128, 128], F32, 1)
        zr = T("zr", [128, SBT], BF16, 4)
        zatt = T("zatt", [128, SBT], BF16, 4)
        smt = T("smt", [128, 256], F32, 4)
        p32 = T("p32", [128, 256], F32, 4)
        pnt = T("pnt", [128, 256], BF16, 4)
        ptt = T("ptt", [128, 2, 128], BF16, 4)
        ast = T("ast", [128, 8], F32, 4)
        mT = T("mT", [128, 8, SBT], BF16, 1)
        sgt = T("sgt", [128, SBT], F32, 2)
        m12 = T("m12", [128, SBT], F32, 2)
        fst = T("fst", [128, 4], F32, 2)

        S.add("pool", lambda e: e.memset(prevcol.t[0][:], 0.0), [], [prevcol.b[0]])
        kct = kc.t[0]
        KB = kc.b[0]
        for j, val in enumerate([RMS_EPS, 1.0, -0.5, 1e-12, GN_EPS]):
            S.add("pool", lambda e, j=j, val=val: e.memset(kct[:, j:j + 1], val), [], [KB])
        eps_col = kct[:, 0:1]
        one_col = kct[:, 1:2]
        mhalf_col = kct[:, 2:3]
        tiny_col = kct[:, 3:4]
        gneps_col = kct[:, 4:5]
        for i in range(NG):
            S.add("pool", lambda e, i=i: e.memset(ATbd.t[i][:], 0.0), [], [ATbd.b[i]])
            S.add("pool", lambda e, i=i: e.memset(Wz.t[i][:], 0.0), [], [Wz.b[i]])
        for i in range(4):
            S.add("pool", lambda e, i=i: e.memset(Ht.t[i][:], 0.0), [], [Ht.b[i]])
        for i in range(NV):
            S.add("pool", lambda e, i=i: e.memset(Vpad.t[i][:], 0.0), [], [Vpad.b[i]])
        for i in range(8):
            S.add("pool", lambda e, i=i: e.memset(PZ.t[i][:], 0.0), [], [PZ.b[i]])
            S.add("pool", lambda e, i=i: e.memset(Hbfz.t[i][:], 0.0), [], [Hbfz.b[i]])
            S.add("pool", lambda e, i=i: e.memset(qTz.t[i][:], 0.0), [], [qTz.b[i]])
        for i in range(2):
            S.add("pool", lambda e, i=i: e.memset(KTatt.t[i][:], 0.0), [], [KTatt.b[i]])
        Wout_b = [Buf(f"wout{k}") for k in range(8)]
        for k in range(8):
            S.add("pool", lambda e, k=k: e.dma_start(out=Wout.t[0][:, k, :], in_=w_out[k * 128:(k + 1) * 128, :]),
                  [], [Wout_b[k]], dma=True)

        wib_b = [Buf(f"wib{k}") for k in range(8)]
        wbb_b = [Buf(f"wbb{k}") for k in range(8)]
        w_br_flat = w_br.rearrange("b r c -> (b r) c")
        for k in range(8):
            S.add("pool", lambda e, k=k: e.dma_start(out=wib[k * 128:(k + 1) * 128, :], in_=w_in[k * 128:(k + 1) * 128, O_GR:IN_COLS]),
                  [], [wib_b[k]], dma=True)
        for k in range(8):
            S.add("pool", lambda e, k=k: e.dma_start(out=wbb[k * 128:(k + 1) * 128, :], in_=w_br_flat[k * 128:(k + 1) * 128, :]),
                  [], [wbb_b[k]], dma=True)

        def bcm(ap2, n):
            a = ap2.ap
            return bass.AP(ap2.tensor, ap2.offset, [list(a[0]), [0, n], list(a[1])])

        def bcl(ap2, n):
            a = ap2.ap
            return bass.AP(ap2.tensor, ap2.offset, [list(a[0]), list(a[1]), [0, n]])

        def v3(ap, h=2):
            return ap.rearrange("p (h t) -> p h t", h=h)

        def rms_rstd(in_ap, in_bufs, stt, stb, junk_ap, junk_buf):
            S.add("act", lambda e: e.activation(out=junk_ap, in_=in_ap, func=AF.Square, accum_out=stt[:, 0:1]),
                  in_bufs, [junk_buf, stb])
            S.add("act", lambda e: e.activation(out=stt[:, 1:2], in_=stt[:, 0:1], func=AF.Ln, bias=eps_col, scale=1.0 / D),
                  [stb, KB], [stb])
            S.add("act", lambda e: e.activation(out=stt[:, 2:3], in_=stt[:, 1:2], func=AF.Exp, scale=-0.5), [stb], [stb])

        def ws_load(srcs):
            i = wsrr[0]
            wsrr[0] = (i + 1) % NWS
            t = ws.t[i]
            bufs = [ws.b[i], ws_b2[i]]
            for j, (dfn, dap) in enumerate(srcs):
                S.add("sp", lambda e, dfn=dfn, dap=dap, t=t: e.dma_start(out=dfn(t), in_=dap), wib_b + wbb_b, [bufs[j]], dma=True)
            return t, bufs[:len(srcs)]

        def wcols(c0, n=128):
            return wib[:, c0 - O_GR:c0 - O_GR + n].rearrange("(k p) c -> p k c", p=128)

        def proj_fm(hTt, hTb, wt, wbufs, ncols=SBT, col0=0):
            pa, pab = fullbank()
            for k in range(8):
                S.add("pe", lambda e, pa=pa, k=k, wt=wt, hTt=hTt: e.matmul(
                    pa[:, 0:ncols], lhsT=wt[:, k, :], rhs=hTt[:, k, col0:col0 + ncols], start=(k == 0), stop=(k == 7)),
                    wbufs + [hTb], pab)
            return pa, pab

        P_ = [slice(0, 64), slice(64, 128)]
        own0 = nsb - nown

        def stage2_header():
            swt, swb = shwa()
            twt, twb = tw()
            S.add("act", lambda e, twt=twt, swt=swt: e.activation(out=twt[0:64, :], in_=swt[0:64, :], func=AF.Exp, scale=2.0), [swb], [twb])
            S.add("dve", lambda e, twt=twt: e.tensor_scalar(out=twt[0:64, :], in0=twt[0:64, :], scalar1=1.0, scalar2=None, op0=ALU.add), [twb], [twb])
            S.add("dve", lambda e, twt=twt: e.reciprocal(out=twt[0:64, :], in_=twt[0:64, :]), [twb], [twb])
            S.add("dve", lambda e, twt=twt: e.tensor_scalar(out=twt[0:64, :], in0=twt[0:64, :], scalar1=-2.0, scalar2=1.0, op0=ALU.mult, op1=ALU.add), [twb], [twb])
            S.add("act", lambda e, twt=twt, swt=swt: e.activation(out=twt[64:128, :], in_=swt[64:128, :], func=AF.Copy), [swb, twb], [twb])
            twh, twhb = tw_hi()
            twl, twlb = tw_lo()
            S.add("act", lambda e, twh=twh, twt=twt: e.activation(out=twh[:], in_=twt[:], func=AF.Copy), [twb], [twhb])
            S.add("dve", lambda e, twl=twl, twt=twt, twh=twh: e.tensor_tensor(out=twl[:], in0=twt[:], in1=twh[:], op=ALU.subtract), [twb, twhb], [twlb])
            return dict(twt=twt, twh=twh, twl=twl, twhb=twhb, twlb=twlb, twb=twb)
        def prep_hp(hp, sbi, own, twt=None, twh=None, twl=None, twhb=None, twlb=None, twb=None):
            si = sbi * 4 + hp
            rt, rb = shr(si)
            kt_, kb_ = shk(si)
            vt, vb = shv(si)
            pD, pDb = fullbank()
            hsl = slice(hp * 128, (hp + 1) * 128)
            S.add("pe", lambda e, pD=pD, hsl=hsl, twh=twh: e.matmul(pD[:, 0:SBT], lhsT=Wdb.t[0][:, 0, hsl], rhs=twh[:, :], start=True, stop=False),
                  [Wdb.b[0], twhb], pDb)
            S.add("pe", lambda e, pD=pD, hsl=hsl, twl=twl: e.matmul(pD[:, 0:SBT], lhsT=Wdb.t[0][:, 0, hsl], rhs=twl[:, :], start=False, stop=False),
                  [Wdb.b[0], twlb], pDb)
            S.add("pe", lambda e, pD=pD, hsl=hsl, twh=twh: e.matmul(pD[:, 0:SBT], lhsT=Wdb.t[0][:, 1, hsl], rhs=twh[:, :], start=False, stop=True),
                  [Wdb.b[0], twhb], pDb)
            e1, e1b = t_e1()
            ew, ewb = t_ew()
            at, ab_ = t_a()
            cs, csb = t_cs()
            csp, cspb = t_csp()
            en, enb = t_en()
            k2, k2b = t_k2()
            kkn, kknb = t_kkn()
            abt, abb = t_ab()
            ft, fb = t_f()
            S.add("act", lambda e, e1=e1, pD=pD, hp=hp: e.activation(out=e1[:], in_=pD[:, 0:SBT], func=AF.Exp, bias=pc("nw0", hp), scale=-1.0),
                  pDb + [PB], [e1b])
            pAa, pAb = fullbank()
            S.add("pe", lambda e, pAa=pAa, hsl=hsl, twh=twh: e.matmul(pAa[:, 0:SBT], lhsT=Wdb.t[0][:, 2, hsl], rhs=twh[:, :], start=True, stop=True),
                  [Wdb.b[0], twhb], pAb)
            S.add("act", lambda e, e1=e1: e.activation(out=e1[:], in_=e1[:], func=AF.Ln, bias=one_col), [e1b, KB], [e1b])
            S.add("act", lambda e, e1=e1, ew=ew: e.activation(out=ew[:], in_=e1[:], func=AF.Exp, bias=mhalf_col, scale=-1.0), [e1b, KB], [ewb])
            S.add("act", lambda e, at=at, pAa=pAa, hp=hp: e.activation(out=at[:], in_=pAa[:, 0:SBT], func=AF.Exp, bias=pc("na0", hp), scale=-1.0),
                  pAb + [PB], [ab_])
            yield
            S.add("act", lambda e, at=at: e.activation(out=at[:], in_=at[:], func=AF.Ln, bias=one_col), [ab_, KB], [ab_])
            S.add("act", lambda e, at=at: e.activation(out=at[:], in_=at[:], func=AF.Exp, scale=-1.0), [ab_], [ab_])
            for c in range(CPS):
                S.add("dve", lambda e, cs=cs, ew=ew, c=c: e.tensor_tensor_scan(
                    out=cs[:, c * C:(c + 1) * C], data0=ones_f, data1=ew[:, c * C:(c + 1) * C], initial=0.0,
                    op0=ALU.mult, op1=ALU.add), [ewb, CF], [csb])
            S.add("pool", lambda e, csp=csp, cs=cs, ew=ew: e.tensor_tensor(out=csp[:], in0=cs[:], in1=ew[:], op=ALU.subtract), [csb, ewb], [cspb])
            S.add("act", lambda e, en=en, cs=cs: e.activation(out=en[:], in_=cs[:], func=AF.Exp), [csb], [enb])
            S.add("act", lambda e, csp=csp: e.activation(out=csp[:], in_=csp[:], func=AF.Exp, scale=-1.0), [cspb], [cspb])
            gct, gcb = gC(si)
            S.add("act", lambda e, gct=gct, cs=cs: e.activation(
                out=gct[:, 0:CPS], in_=cs[:, :].rearrange("p (c t) -> p c t", t=C)[:, :, C - 1], func=AF.Exp, scale=-1.0), [csb], [gcb])
            yield
            k2h, k2hb = t_k2b()
            S.add("act", lambda e, k2h=k2h, kt_=kt_, hp=hp: e.activation(out=k2h[:], in_=kt_[:], func=AF.Square, scale=pc("kk", hp)), [kb_, PB], [k2hb])
            pS_, pSb = fullbank()
            S.add("pe", lambda e, pS_=pS_, k2h=k2h: e.matmul(pS_[:, 0:SBT], lhsT=bones_b, rhs=k2h[:], start=True, stop=True), [CB, k2hb], pSb)
            S.add("act", lambda e, k2=k2, pS_=pS_: e.activation(out=k2[:], in_=pS_[:, 0:SBT], func=AF.Ln, bias=tiny_col), pSb + [KB], [k2b])
            S.add("act", lambda e, k2=k2: e.activation(out=k2[:], in_=k2[:], func=AF.Exp, scale=-0.5), [k2b], [k2b])
            S.add("dve", lambda e, kkn=kkn, kt_=kt_, k2=k2, hp=hp: e.scalar_tensor_tensor(
                out=kkn[:], in0=kt_[:], scalar=pc("kk", hp), in1=k2[:], op0=ALU.mult, op1=ALU.mult), [kb_, k2b, PB], [kknb])
            yield
            ARt, ARb = AR(si)
            BTt, BTb = BT(si)
            KTt, KTb = KT(si)
            vbt, vbb = vbf(si)
            S.add("dve", lambda e, ARt=ARt, kkn=kkn, csp=csp: e.scalar_tensor_tensor(
                out=ARt[:, :, 0, :], in0=kkn[:, :].rearrange("p (c t) -> p c t", t=C), scalar=-1.0,
                in1=csp[:, :].rearrange("p (c t) -> p c t", t=C), op0=ALU.mult, op1=ALU.mult), [kknb, cspb], [ARb])
            S.add("pool", lambda e, abt=abt, kkn=kkn, at=at: e.tensor_tensor(out=abt[:], in0=kkn[:], in1=at[:], op=ALU.mult), [kknb, ab_], [abb])
            S.add("pool", lambda e, BTt=BTt, abt=abt, en=en: e.tensor_tensor(out=BTt[:], in0=abt[:], in1=en[:], op=ALU.mult), [abb, enb], [BTb])
            yield
            S.add("dve", lambda e, ft=ft, at=at, hp=hp: e.tensor_scalar(out=ft[:], in0=at[:], scalar1=pc("ka", hp), scalar2=pc("omka", hp),
                                                                    op0=ALU.mult, op1=ALU.add), [ab_, PB], [fb])
            S.add("pool", lambda e, ft=ft, kt_=kt_: e.tensor_tensor(out=ft[:], in0=kt_[:], in1=ft[:], op=ALU.mult), [kb_, fb], [fb])
            S.add("pool", lambda e, KTt=KTt, ft=ft, en=en: e.tensor_tensor(out=KTt[:], in0=ft[:], in1=en[:], op=ALU.mult), [fb, enb], [KTb])
            S.add("act", lambda e, vbt=vbt, vt=vt: e.activation(out=vbt[:], in_=vt[:], func=AF.Copy), [vb], [vbb])
            yield
            for hh in range(2):
                zt, zb = PZ(si * 2 + hh)
                S.add("pool", lambda e, zt=zt, ARt=ARt, hh=hh: e.tensor_copy(out=zt[P_[hh], 0, :].rearrange("p (c t) -> p c t", t=C), in_=ARt[P_[hh], :, 0, :]), [ARb], [zb])
                S.add("pool", lambda e, zt=zt, BTt=BTt, hh=hh: e.tensor_copy(out=zt[P_[hh], 1, :], in_=BTt[P_[hh], :]), [BTb], [zb])
                S.add("pool", lambda e, zt=zt, KTt=KTt, hh=hh: e.tensor_copy(out=zt[P_[hh], 2, :], in_=KTt[P_[hh], :]), [KTb], [zb])
            if own:
                ep, epb = t_ep()
                S.add("act", lambda e, ep=ep, cs=cs: e.activation(out=ep[:], in_=cs[:], func=AF.Exp, scale=-1.0), [csb], [epb])
                S.add("dve", lambda e, ARt=ARt, rt=rt, ep=ep: e.tensor_tensor(
                    out=ARt[:, :, 1, :], in0=rt[:, :].rearrange("p (c t) -> p c t", t=C),
                    in1=ep[:, :].rearrange("p (c t) -> p c t", t=C), op=ALU.mult), [rb, epb], [ARb])
                rkb_t, rkb_b = t_rkb()
                S.add("dve", lambda e, rkb_t=rkb_t, rt=rt, ft=ft, hp=hp: e.scalar_tensor_tensor(
                    out=rkb_t[:], in0=rt[:], scalar=pc("rk", hp), in1=ft[:], op0=ALU.mult, op1=ALU.mult), [rb, fb, PB], [rkb_b])
                pB_, pBb = fullbank()
                S.add("pe", lambda e, pB_=pB_, rkb_t=rkb_t: e.matmul(pB_[:, 0:SBT], lhsT=bones_b, rhs=rkb_t[:], start=True, stop=True), [CB, rkb_b], pBb)
                bnt, bnb = bonus(si)
                S.add("dve", lambda e, bnt=bnt, pB_=pB_, vt=vt: e.tensor_tensor(out=bnt[:], in0=pB_[:, 0:SBT], in1=vt[:], op=ALU.mult), pBb + [vb], [bnb])
            yield
        def proj_tile(sbi, ct, hTt, hTb):
            pa, pab = fullbank()
            for k in range(8):
                S.add("pe", lambda e, pa=pa, k=k, ct=ct, hTt=hTt: e.matmul(
                    pa[:, 0:SBT], lhsT=Wsh.t[0][:, k, ct * 128:(ct + 1) * 128], rhs=hTt[:, k, :],
                    start=(k == 0), stop=(k == 7)), [Wsh_b[k], hTb], pab)
            if ct == 12:
                dst, dstb = shwa()
            else:
                hp = ct % 4
                dst, dstb = (shr, shk, shv)[ct // 4](sbi * 4 + hp)
            tmp, tmpb = shtmp(ct)
            S.add("act", lambda e, tmp=tmp, pa=pa, ct=ct: e.activation(
                out=tmp[:], in_=pa[:, 0:SBT], func=AF.Copy, scale=pc("omu", ct)), pab + [PB], [tmpb])
            S.add("dve", lambda e, dst=dst, pa=pa, tmp=tmp, ct=ct: e.scalar_tensor_tensor(
                out=dst[:, 1:SBT], in0=pa[:, 0:SBT - 1], scalar=pc("mu", ct), in1=tmp[:, 1:SBT],
                op0=ALU.mult, op1=ALU.add), pab + [tmpb, PB], [dstb])
            S.add("dve", lambda e, dst=dst, tmp=tmp, ct=ct: e.scalar_tensor_tensor(
                out=dst[:, 0:1], in0=prevcol.t[0][:, ct:ct + 1], scalar=pc("mu", ct), in1=tmp[:, 0:1],
                op0=ALU.mult, op1=ALU.add), [prevcol.b[0], tmpb, PB], [dstb])
            S.add("act", lambda e, pa=pa, ct=ct: e.activation(
                out=prevcol.t[0][:, ct:ct + 1], in_=pa[:, SBT - 1:SBT], func=AF.Copy), pab, [prevcol.b[0]])

        sbst = {}

        def gen_first(sbi):
            own = sbi >= own0
            hTt, hTb = hT(sbi)
            for j in range(CPS):
                gc = sbi * CPS + j
                xtt, xtb = xt(gc)
                hbt, hbb = hb(gc)
                stt, stb = st0(gc)
                dma("sp", xtt[:], xw[gc * C:(gc + 1) * C, :], [], [xtb])
                rms_rstd(xtt[:], [xtb], stt, stb, hbt[:], hbb)
                S.add("dve", lambda e, xtt=xtt, stt=stt, hbt=hbt: e.scalar_tensor_tensor(
                    out=hbt[:], in0=xtt[:], scalar=stt[:, 2:3], in1=gpre.t[0][:], op0=ALU.mult, op1=ALU.mult),
                    [xtb, stb, gpre.b[0]], [hbb])
                yield
                pst = psT[0]
                pstb = psT_b[0]
                for k in range(8):
                    S.add("pe", lambda e, k=k, hbt=hbt, pst=pst: e.transpose(
                        out=pst[:, k * 128:(k + 1) * 128], in_=hbt[:, k * 128:(k + 1) * 128], identity=ident_b),
                        [hbb, CB], pstb)
                S.add("act", lambda e, pst=pst, hTt=hTt, j=j: e.activation(
                    out=hTt[:, :, j * C:(j + 1) * C], in_=pst[:, :].rearrange("p (k t) -> p k t", k=8), func=AF.Copy),
                    pstb, [hTb])
                yield
            proj_tile(sbi, 12, hTt, hTb)
            tw_ctx = stage2_header()
            sbst[sbi] = (tw_ctx, hTt, hTb)
            yield
            for hp in (0, 1):
                for q in range(3):
                    proj_tile(sbi, q * 4 + hp, hTt, hTb)
                    yield

        def gen_first_b(sbi):
            own = sbi >= own0
            tw_ctx, hTt, hTb = sbst[sbi]
            for hp in (0, 1):
                yield from prep_hp(hp, sbi, own, **tw_ctx)

        def gen_first_ab(sbi):
            yield from gen_first(sbi)
            yield from gen_first_b(sbi)

        def gen_second(sbi):
            own = sbi >= own0
            tw_ctx, hTt, hTb = sbst[sbi]
            for hp in (2, 3):
                for q in range(3):
                    proj_tile(sbi, q * 4 + hp, hTt, hTb)
                    yield
                yield from prep_hp(hp, sbi, own, **tw_ctx)

        def drain(g):
            for _ in g:
                pass

        def mkfill(g, n=1, units=None, slots=None):
            st = [0]

            def fill():
                if units is None:
                    k = n
                else:
                    i = st[0]
                    st[0] += 1
                    k = ((i + 1) * units) // slots - (i * units) // slots
                for _ in range(k):
                    try:
                        next(g)
                    except StopIteration:
                        return
            return fill

        drain(gen_first_ab(0))
        for sbi in range(nsb):
            own = sbi >= own0
            halo_sb = (sbi == own0 - 1)
            hTt, hTb = hT(sbi)
            def gen_ownproj(sbi=sbi, own=own, halo_sb=halo_sb, hTt=hTt, hTb=hTb):
                if own or halo_sb:
                    ncols, col0 = (SBT, 0) if own else (C, SBT - C)
                    kcol = ((sbi - own0) * CPS + 1) * C if own else 0
                    for g in range(2):
                        wt, wb = ws_load([(lambda t: t[:, :, 0:64], wcols(O_K + g * 64, 64)), (lambda t: t[:, :, 64:128], wcols(O_K + g * 64, 64))])
                        pa, pab = proj_fm(hTt, hTb, wt, wb, ncols, col0)
                        for cc in range(ncols // C):
                            ks = ((kcol // C) + cc) % NKS
                            S.add("act", lambda e, pa=pa, g=g, ks=ks, cc=cc: e.activation(
                                out=KTatt.t[g][:, ks * C:(ks + 1) * C], in_=pa[:, cc * C:(cc + 1) * C], func=AF.Identity, bias=pc("bk", g)), pab + [PB], [KTatt.b[g]])
                        yield
                    wt, wb = ws_load([(lambda t: t[:, :, :], wcols(O_V))])
                    for c in (range(CPS) if own else [CPS - 1]):
                        lc1 = (sbi - own0) * CPS + c + 1 if own else 0
                        pa, pab = fullbank()
                        for k in range(8):
                            S.add("pe", lambda e, pa=pa, k=k, wt=wt, hTt=hTt, c=c: e.matmul(
                                pa[:, 0:128], lhsT=hTt[:, k, c * C:(c + 1) * C], rhs=wt[:, k, :], start=(k == 0), stop=(k == 7)), wb + [hTb], pab)
                        vp, vpb = Vpad(lc1)
                        S.add("dve", lambda e, vp=vp, pa=pa: e.tensor_tensor(out=vp[:, :, 0:64], in0=v3(pa[:, 0:128]), in1=v3(bvb.t[0][:, :]), op=ALU.add),
                              pab + [bvb.b[0]], [vpb])
                        S.add("pool", lambda e, vp=vp: e.tensor_copy(out=vp[:, :, 128:192], in_=vp[:, :, 0:64]), [vpb], [vpb])
                        yield
                if own:
                    for ct in range(4):
                        si = sbi * 4 + ct
                        wt, wb = ws_load([(lambda t: t[:, :, :], wcols(O_GR + ct * 128))])
                        pa, pab = proj_fm(hTt, hTb, wt, wb)
                        S.add("act", lambda e, pa=pa, si=si: e.activation(out=sgr(si)[0][:], in_=pa[:, 0:SBT], func=AF.Silu), pab, [sgr(si)[1]])
                        yield
                        wt, wb = ws_load([(lambda t: t[:, :, :], wcols(O_Q + ct * 128))])
                        pa, pab = proj_fm(hTt, hTb, wt, wb)
                        for hh in range(2):
                            qz, qzb = qTz(si * 2 + hh)
                            S.add("act", lambda e, pa=pa, qz=qz, ct=ct, hh=hh: e.activation(
                                out=qz[P_[hh], :], in_=pa[P_[hh], 0:SBT], func=AF.Identity, bias=ppt[P_[hh], PPI["bq"] + ct:PPI["bq"] + ct + 1]),
                                pab + [PB], [qzb])
                        yield
                        wt, wb = ws_load([(lambda t: t[:, :, :], wcols(O_GA + ct * 128))])
                        pa, pab = proj_fm(hTt, hTb, wt, wb)
                        S.add("act", lambda e, pa=pa, si=si: e.activation(out=sga(si)[0][:], in_=pa[:, 0:SBT], func=AF.Silu), pab, [sga(si)[1]])
                        yield

                yield

            def emit_chunk_pairs(c, pairs, fill, own=own, sbi=sbi):
                gch = sbi * CPS + c
                csl = slice(c * C, (c + 1) * C)
                pYbank, pYbb = psA[0], [psA_b[0]]
                def mkctx(hp):
                    si = sbi * 4 + hp
                    gi = gch * 4 + hp
                    x = dict(hp=hp, si=si, gi=gi)
                    x["AR"], x["ARb"] = AR(si)
                    x["BT"], x["BTb"] = BT(si)
                    x["KT"], x["KTb"] = KT(si)
                    x["vb"], x["vbb"] = vbf(si)
                    x["zts"] = [PZ(si * 2 + hh) for hh in range(2)]
                    x["tm"], x["tmb"] = tm(gi)
                    return x

                def g_transposes(x, c=c, csl=csl):
                    pt_ = psT[1][:, 0:512]
                    ptb = psT_b[1]
                    srcs = [(x["AR"][:, c, 0, :], x["ARb"]), (x["BT"][:, csl], x["BTb"]), (x["KT"][:, csl], x["KTb"]), (x["vb"][:, csl], x["vbb"])]
                    for q, (sap, sbf) in enumerate(srcs):
                        S.add("pe", lambda e, pt_=pt_, q=q, sap=sap: e.transpose(out=pt_[:, q * 128:(q + 1) * 128], in_=sap, identity=ident_b),
                              [sbf, CB], ptb)
                    tmt = x["tm"]
                    S.add("act", lambda e, tmt=tmt, pt_=pt_: e.activation(out=tmt[:], in_=pt_.rearrange("p (q t) -> p q t", q=4), func=AF.Copy),
                          ptb, [x["tmb"]])

                def g_sprod_pe(x, c=c, csl=csl):
                    x["pS1"], x["pS1b"] = single(4)
                    x["pS2"], x["pS2b"] = single(2)
                    ARt, BTt = x["AR"], x["BT"]
                    for hh in range(2):
                        zt, zb = x["zts"][hh]
                        S.add("pe", lambda e, pS=x["pS1"], zt=zt, ARt=ARt, hh=hh: e.matmul(
                            pS[:, hh * 128:(hh + 1) * 128], lhsT=zt[:, 1, csl], rhs=ARt[:, c, 0, :], start=True, stop=True), [zb, x["ARb"]], x["pS1b"])
                    for hh in range(2):
                        zt, zb = x["zts"][hh]
                        S.add("pe", lambda e, pS=x["pS1"], zt=zt, BTt=BTt, hh=hh: e.matmul(
                            pS[:, (2 + hh) * 128:(3 + hh) * 128], lhsT=zt[:, 0, csl], rhs=BTt[:, csl], start=True, stop=True), [zb, x["BTb"]], x["pS1b"])
                    for hh in range(2):
                        zt, zb = x["zts"][hh]
                        S.add("pe", lambda e, pS=x["pS2"], zt=zt, ARt=ARt, hh=hh: e.matmul(
                            pS[:, hh * 128:(hh + 1) * 128], lhsT=zt[:, 2, csl], rhs=ARt[:, c, 0, :], start=True, stop=True), [zb, x["ARb"]], x["pS2b"])

                def g_sprod_evac(x):
                    gi = x["gi"]
                    nm, nmb = NMt(gi * 2)
                    mk, mkb = Mak(gi)
                    S.add("dve", lambda e, nm=nm, pS=x["pS1"]: e.tensor_tensor(out=nm[:, :, :, :].rearrange("p a h t -> p (a h) t"), in0=v3(pS, 4), in1=mask4, op=ALU.mult),
                          x["pS1b"] + [CB], [nmb])
                    S.add("dve", lambda e, mk=mk, pS=x["pS2"]: e.tensor_tensor(out=mk[:], in0=v3(pS, 2), in1=mask4[:, 0:2, :], op=ALU.mult),
                          x["pS2b"] + [CB], [mkb])
                    x["nm"], x["nmb"], x["mk"], x["mkb"] = nm, nmb, mk, mkb

                def g_r_pe(x, c=c, csl=csl):
                    x["pR"], x["pRb"] = single(4)
                    ARt = x["AR"]
                    for a_ in range(2):
                        for hh in range(2):
                            zt, zb = x["zts"][hh]
                            S.add("pe", lambda e, pR=x["pR"], zt=zt, ARt=ARt, hh=hh, a_=a_: e.matmul(
                                pR[:, (a_ * 2 + hh) * 128:(a_ * 2 + hh + 1) * 128], lhsT=zt[:, 1 + a_, csl], rhs=ARt[:, c, 1, :], start=True, stop=True),
                                [zb, x["ARb"]], x["pRb"])

                def g_r_evac(x, c=c, csl=csl):
                    rbk, rbkb = RBK(x["gi"])
                    for a_ in range(2):
                        S.add("dve", lambda e, rbk=rbk, pR=x["pR"], a_=a_: e.tensor_tensor(out=rbk[:, a_, :, :], in0=v3(pR[:, a_ * 256:(a_ + 1) * 256], 2), in1=mle2, op=ALU.mult),
                              x["pRb"] + [CB], [rbkb])
                    x["rbk"], x["rbkb"] = rbk, rbkb

                def g_pv_pe(x, c=c, csl=csl):
                    x["pV"], x["pVb"] = single(1)
                    mk, tmt = x["mk"], x["tm"]
                    for hh in range(2):
                        S.add("pe", lambda e, pV=x["pV"], hh=hh, mk=mk, tmt=tmt: e.matmul(
                            pV[:, hh * 64:(hh + 1) * 64], lhsT=mk[:, hh, :], rhs=tmt[:, 3, hh * 64:(hh + 1) * 64], start=True, stop=True), [x["mkb"], x["tmb"]], x["pVb"])

                def g_x0(x, c=c, csl=csl):
                    Xt, Xb = Xtile(x["gi"] * 2)
                    tmt = x["tm"]
                    S.add("pool", lambda e, Xt=Xt, tmt=tmt: e.tensor_copy(out=Xt[:, :, 0, :], in_=tmt[:, 0, :].rearrange("p (h k) -> p h k", h=2)), [x["tmb"]], [Xb])
                    S.add("act", lambda e, Xt=Xt, pV=x["pV"]: e.activation(out=Xt[:, :, 1, :], in_=pV[:, 0:128].rearrange("p (h k) -> p h k", h=2), func=AF.Copy),
                          x["pVb"], [Xb])
                    x["X"], x["Xb"] = Xt, Xb

                def g_level_pe(x, lv):
                    nm, nmb, Xt, Xb = x["nm"], x["nmb"], x["X"], x["Xb"]
                    x["pX"], x["pXb"] = single(2)
                    for hh in range(2):
                        S.add("pe", lambda e, pX=x["pX"], hh=hh, nm=nm, Xt=Xt: e.matmul(
                            pX[:, hh * 128:(hh + 1) * 128], lhsT=nm[:, 0, hh, :], rhs=Xt[:, hh, :, :].rearrange("p a k -> p (a k)"), start=True, stop=True),
                            [nmb, Xb], x["pXb"])
                    if lv < 6:
                        x["pNM"], x["pNMb"] = single(4)
                        for hh in range(2):
                            S.add("pe", lambda e, pN=x["pNM"], hh=hh, nm=nm: e.matmul(
                                pN[:, hh * 128:(hh + 1) * 128], lhsT=nm[:, 1, hh, :], rhs=nm[:, 0, hh, :], start=True, stop=True), [nmb], x["pNMb"])
                        if lv < 5:
                            for hh in range(2):
                                S.add("pe", lambda e, pN=x["pNM"], hh=hh, nm=nm: e.matmul(
                                    pN[:, (2 + hh) * 128:(3 + hh) * 128], lhsT=nm[:, 0, hh, :], rhs=nm[:, 1, hh, :], start=True, stop=True), [nmb], x["pNMb"])

                def g_level_evac(x, lv):
                    gi = x["gi"]
                    Xt, Xb = x["X"], x["Xb"]
                    Xn, Xnb = Xtile(gi * 2 + lv + 1)
                    S.add("dve", lambda e, Xn=Xn, pX=x["pX"], Xt=Xt: e.tensor_tensor(
                        out=Xn[:, :, :, :].rearrange("p h a k -> p h (a k)"), in0=v3(pX), in1=Xt[:, :, :, :].rearrange("p h a k -> p h (a k)"), op=ALU.add),
                        x["pXb"] + [Xb], [Xnb])
                    x["X"], x["Xb"] = Xn, Xnb
                    if lv < 6:
                        nn, nnb = NMt(gi * 2 + lv + 1)
                        w = 4 if lv < 5 else 2
                        S.add("act", lambda e, nn=nn, pN=x["pNM"], w=w: e.activation(
                            out=nn[:, :, :, :].rearrange("p a h t -> p (a h) t")[:, 0:w, :], in_=v3(pN[:, 0:w * 128], w), func=AF.Copy), x["pNMb"], [nnb])
                        x["nm"], x["nmb"] = nn, nnb

                def g_state(x, c=c, csl=csl, own=own, pYbank=(pYbank if own else None), pYbb=(pYbb if own else None)):
                    gi, hp, si = x["gi"], x["hp"], x["si"]
                    Xt, Xb, tmt, tmb = x["X"], x["Xb"], x["tm"], x["tmb"]
                    ARt, ARb = x["AR"], x["ARb"]
                    wz, wzb = Wz(gi)
                    S.add("pool", lambda e, wz=wz, Xt=Xt: e.tensor_copy(out=wz[:, 0::2, :], in_=Xt[:, :, 0, :]), [Xb], [wzb])
                    wzA = wz[:, 0:2, :].rearrange("p a k -> p (a k)")
                    wzB = wz[:, 1:3, :].rearrange("p a k -> p (a k)")
                    wp, wpb = Wp(gi)
                    S.add("pool", lambda e, wp=wp, Xt=Xt: e.tensor_copy(out=wp[:, :, :], in_=Xt[:, :, 0, :]), [Xb], [wpb])
                    pAT, pATb = single(1)
                    S.add("pe", lambda e, pAT=pAT, wp=wp, tmt=tmt: e.matmul(pAT[:, 0:128], lhsT=wp[:, :, :].rearrange("p a k -> p (a k)"), rhs=tmt[:, 1, :], start=True, stop=True),
                          [wpb, tmb], pATb)
                    atb, atbb = ATbd(gi)
                    for hh in range(2):
                        S.add("act", lambda e, atb=atb, pAT=pAT, hh=hh: e.activation(
                            out=atb[P_[hh], hh * 64:(hh + 1) * 64], in_=pAT[P_[hh], hh * 64:(hh + 1) * 64], func=AF.Copy), pATb, [atbb])
                    pG, pGb = single(1)
                    pG2, pG2b = single(1)
                    S.add("pe", lambda e, pG=pG, tmt=tmt: e.matmul(pG[:, 0:128], lhsT=tmt[:, 2, :], rhs=tmt[:, 3, :], start=True, stop=True),
                          [tmb], pGb)
                    for hh in range(2):
                        S.add("pe", lambda e, pG2=pG2, Xt=Xt, tmt=tmt, hh=hh: e.matmul(
                            pG2[:, hh * 64:(hh + 1) * 64], lhsT=tmt[:, 1, :], rhs=Xt[:, hh, 1, :], start=True, stop=True), [Xb, tmb], pG2b)
                    gs, gsb_ = Gsb(gi)
                    for hh in range(2):
                        S.add("act", lambda e, gs=gs, pG=pG, hh=hh: e.activation(
                            out=gs[P_[hh], :], in_=pG[P_[hh], hh * 64:(hh + 1) * 64], func=AF.Copy), pGb, [gsb_])
                        S.add("dve", lambda e, gs=gs, pG2=pG2, hh=hh: e.tensor_tensor(
                            out=gs[P_[hh], :], in0=pG2[P_[hh], hh * 64:(hh + 1) * 64], in1=gs[P_[hh], :], op=ALU.add), pG2b + [gsb_], [gsb_])
                    Htt, Hb_ = Ht(hp)
                    gct, gcb = gC(si)
                    if own:
                        hbz = [Hbfz(gi * 2 + hh) for hh in range(2)]
                        for hh in range(2):
                            S.add("pool", lambda e, hz=hbz[hh][0], Htt=Htt, hh=hh: e.tensor_copy(out=hz[P_[hh], :], in_=Htt[P_[hh], :]), [Hb_], [hbz[hh][1]])
                    hhl, hhlb = Hhl(gi)
                    S.add("pool", lambda e, hhl=hhl, Htt=Htt: e.tensor_copy(out=hhl[:, 0, :], in_=Htt[:]), [Hb_], [hhlb])
                    S.add("pool", lambda e, hhl=hhl, Htt=Htt: e.tensor_tensor(out=hhl[:, 1, :], in0=Htt[:], in1=hhl[:, 0, :], op=ALU.subtract), [Hb_, hhlb], [hhlb])
                    pZ, pZb = single(1)
                    S.add("pe", lambda e, pZ=pZ, atb=atb, hhl=hhl: e.matmul(pZ[:, 0:128], lhsT=atb[:], rhs=hhl[:, :, :].rearrange("p a v -> p (a v)"), start=True, stop=True),
                          [atbb, hhlb], pZb)
                    s1, s1b = s1t(gi)
                    S.add("pool", lambda e, s1=s1, Htt=Htt, gs=gs: e.tensor_tensor(out=s1[:], in0=Htt[:], in1=gs[:], op=ALU.add), [Hb_, gsb_], [s1b])
                    S.add("pool", lambda e, s1=s1, gct=gct: e.tensor_scalar(out=s1[:], in0=s1[:], scalar1=gct[:, c:c + 1], scalar2=1.0, op0=ALU.mult, op1=ALU.mult),
                          [s1b, gcb], [s1b])
                    S.add("dve", lambda e, pZ=pZ, gct=gct, s1=s1: e.scalar_tensor_tensor(
                        out=s1[:], in0=pZ[:, 0:64], scalar=gct[:, c:c + 1], in1=s1[:], op0=ALU.mult, op1=ALU.add), pZb + [gcb, s1b], [s1b])
                    S.add("dve", lambda e, Htt=Htt, pZ=pZ, gct=gct, s1=s1: e.scalar_tensor_tensor(
                        out=Htt[:], in0=pZ[:, 64:128], scalar=gct[:, c:c + 1], in1=s1[:], op0=ALU.mult, op1=ALU.add), pZb + [gcb, s1b], [Hb_])
                    if own:
                        rbk, rbkb = x["rbk"], x["rbkb"]
                        qt_, qtb = QT(gi)
                        for hh, wzX in enumerate((wzA, wzB)):
                            pQ, pQb = single(1)
                            S.add("pe", lambda e, pQ=pQ, wzX=wzX, rbk=rbk, hh=hh: e.matmul(pQ[:, 0:128], lhsT=wzX, rhs=rbk[:, 0, hh, :], start=True, stop=True),
                                  [wzb, rbkb], pQb)
                            S.add("dve", lambda e, qt_=qt_, pQ=pQ, ARt=ARt, hh=hh: e.tensor_tensor(
                                out=qt_[P_[hh], :], in0=pQ[P_[hh], 0:128], in1=ARt[P_[hh], c, 1, :], op=ALU.add), pQb + [ARb], [qtb])
                        for hh in range(2):
                            pY, pYb = pYbank, pYbb
                            ysl = slice((hp * 2 + hh) * 64, (hp * 2 + hh + 1) * 64)
                            S.add("pe", lambda e, pY=pY, ysl=ysl, hh=hh, rbk=rbk, Xt=Xt: e.matmul(
                                pY[:, ysl], lhsT=rbk[:, 0, hh, :], rhs=Xt[:, hh, 1, :], start=True, stop=False), [rbkb, Xb], pYb)
                            S.add("pe", lambda e, pY=pY, ysl=ysl, hh=hh, rbk=rbk, tmt=tmt: e.matmul(
                                pY[:, ysl], lhsT=rbk[:, 1, hh, :], rhs=tmt[:, 3, hh * 64:(hh + 1) * 64], start=False, stop=False), [rbkb, tmb], pYb)
                            S.add("pe", lambda e, pY=pY, ysl=ysl, qt_=qt_, hz=hbz[hh][0]: e.matmul(
                                pY[:, ysl], lhsT=qt_[:, :], rhs=hz[:, :], start=False, stop=True), [qtb, hbz[hh][1]], pYb)

                for pr in pairs:
                    ctxs = [mkctx(hp) for hp in pr]
                    for x in ctxs:
                        g_transposes(x)
                    for x in ctxs:
                        g_sprod_pe(x)
                    for x in ctxs:
                        g_sprod_evac(x)
                    fill()
                    if own:
                        for x in ctxs:
                            g_r_pe(x)
                        for x in ctxs:
                            g_r_evac(x)
                    for x in ctxs:
                        g_pv_pe(x)
                    for x in ctxs:
                        g_x0(x)
                    fill()
                    for lv in range(7):
                        for x in ctxs:
                            g_level_pe(x, lv)
                        for x in ctxs:
                            g_level_evac(x, lv)
                        fill()
                    for x in ctxs:
                        g_state(x)
            if not own:
                if halo_sb:
                    drain(gen_ownproj())
                g2 = gen_second(sbi)
                f2 = mkfill(g2, units=23, slots=17)
                for c in range(CPS):
                    emit_chunk_pairs(c, [PAIRS[0]], f2)
                drain(g2)
                g1 = gen_first_ab(sbi + 1) if sbi + 1 < nsb else iter(())
                f1 = mkfill(g1, units=28, slots=17)
                for c in range(CPS):
                    emit_chunk_pairs(c, [PAIRS[1]], f1)
                drain(g1)
                continue
            fb_mode[0] = 1
            g1 = iter(())
            for c in range(CPS):
                gch = sbi * CPS + c
                csl = slice(c * C, (c + 1) * C)
                pYbank, pYbb = psA[0], [psA_b[0]]
                if c == 0:
                    g2 = gen_second(sbi)
                    emit_chunk_pairs(c, [PAIRS[0]], mkfill(g2, 2))
                    drain(g2)
                    g3 = gen_ownproj()
                    emit_chunk_pairs(c, [PAIRS[1]], mkfill(g3, 2))
                    drain(g3)
                else:
                    if c == 1 and sbi + 1 < nsb:
                        g1 = gen_first(sbi + 1)
                    emit_chunk_pairs(c, [PAIRS[0]], mkfill(g1, 1))
                    emit_chunk_pairs(c, [PAIRS[1]], mkfill(g1, 1))
                lc = (sbi - own0) * CPS + c
                g_t, g_b = gst()
                pY, pYb = pYbank, pYbb
                yq, yqb = ysqt(0)
                S.add("dve", lambda e, g_t=g_t, pY=pY: e.tensor_reduce(out=g_t[:, 0, :], in_=v3(pY[:, 0:512], 8), axis=AX.X, op=ALU.add), pYb, [g_b])
                S.add("act", lambda e, yq=yq, pY=pY: e.activation(out=yq[:], in_=pY[:, 0:512], func=AF.Square), pYb, [yqb])
                S.add("dve", lambda e, g_t=g_t, yq=yq: e.tensor_reduce(out=g_t[:, 1, :], in_=v3(yq[:, :], 8), axis=AX.X, op=ALU.add), [yqb], [g_b])
                S.add("dve", lambda e, g_t=g_t: e.tensor_scalar(out=g_t[:, 2, :], in0=g_t[:, 0, :], scalar1=1.0 / 64, scalar2=None, op0=ALU.mult), [g_b], [g_b])
                S.add("dve", lambda e, g_t=g_t: e.tensor_tensor(out=g_t[:, 3, :], in0=g_t[:, 2, :], in1=g_t[:, 2, :], op=ALU.mult), [g_b], [g_b])
                S.add("dve", lambda e, g_t=g_t: e.scalar_tensor_tensor(out=g_t[:, 4, :], in0=g_t[:, 1, :], scalar=1.0 / 64, in1=g_t[:, 3, :],
                                                                       op0=ALU.mult, op1=ALU.subtract), [g_b], [g_b])
                S.add("act", lambda e, g_t=g_t: e.activation(out=g_t[:, 5, :], in_=g_t[:, 4, :], func=AF.Ln, bias=gneps_col), [g_b, KB], [g_b])
                S.add("act", lambda e, g_t=g_t: e.activation(out=g_t[:, 5, :], in_=g_t[:, 5, :], func=AF.Exp, scale=-0.5), [g_b], [g_b])
                ynt, ynb = yn()
                S.add("dve", lambda e, yq=yq, pY=pY, g_t=g_t: e.tensor_tensor(
                    out=v3(yq[:, :], 8), in0=v3(pY[:, 0:512], 8), in1=bcl(g_t[:, 2, :], 64), op=ALU.subtract), pYb + [g_b, yqb], [yqb])
                S.add("pool", lambda e, ynt=ynt, yq=yq, g_t=g_t: e.tensor_tensor(
                    out=v3(ynt[:, :], 8), in0=v3(yq[:, :], 8), in1=bcl(g_t[:, 5, :], 64), op=ALU.mult), [yqb, g_b], [ynb])
                pt_ = psT[1][:, 0:512]
                ptb = psT_b[1]
                for hp in range(4):
                    S.add("pe", lambda e, pt_=pt_, hp=hp, ynt=ynt: e.transpose(out=pt_[:, hp * 128:(hp + 1) * 128], in_=ynt[:, hp * 128:(hp + 1) * 128], identity=ident_b),
                          [ynb, CB], ptb)
                for hp in range(4):
                    si = sbi * 4 + hp
                    t1, t1b = t1t(hp)
                    S.add("dve", lambda e, t1=t1, pt_=pt_, hp=hp: e.tensor_scalar(out=t1[:], in0=pt_[:, hp * 128:(hp + 1) * 128], scalar1=pc("gnw", hp), scalar2=pc("gnb", hp),
                                                                            op0=ALU.mult, op1=ALU.add), ptb + [PB], [t1b])
                    S.add("pool", lambda e, t1=t1, si=si, csl=csl: e.tensor_tensor(out=t1[:], in0=t1[:], in1=bonus(si)[0][:, csl], op=ALU.add), [t1b, bonus(si)[1]], [t1b])
                    S.add("pool", lambda e, t1=t1, si=si, csl=csl: e.tensor_tensor(out=zr(si)[0][:, csl], in0=t1[:], in1=sgr(si)[0][:, csl], op=ALU.mult),
                          [t1b, sgr(si)[1]], [zr(si)[1]])
                if lc == NOC - 1:
                    dump("zr0", zr(sbi * 4)[0][:], [128, SBT], [zr(sbi * 4)[1]], BF16)
                if upto < 4:
                    continue
                am_i = 0 if lc == 0 else 1
                pObank, pObb = psA[0], [psA_b[0]]
                vprev, vprevb = Vpad(lc)
                vcur, vcurb = Vpad(lc + 1)
                for qp0 in (0, 2):
                    pts = psT[0]
                    hs = []
                    for qp in (qp0, qp0 + 1):
                        si = sbi * 4 + qp
                        g = qp // 2
                        for hh in range(2):
                            hd = qp * 2 + hh
                            pS, pSb_ = single(2)
                            qz, qzb = qTz(si * 2 + hh)
                            for kk_ in range(2):
                                ks = (lc + kk_) % NKS
                                S.add("pe", lambda e, pS=pS, qz=qz, g=g, ks=ks, kk_=kk_, csl=csl: e.matmul(
                                    pS[:, kk_ * 128:(kk_ + 1) * 128], lhsT=qz[:, csl], rhs=KTatt.t[g][:, ks * C:(ks + 1) * C], start=True, stop=True),
                                    [qzb, KTatt.b[g]], pSb_)
                            j4 = (qp - qp0) * 2 + hh
                            hs.append(dict(hd=hd, qp=qp, hh=hh, g=g, si=si, pS=pS, pSb=pSb_, sm=smt(j4), a=ast(j4), p3=p32(j4), pn=pnt(j4), pt=ptt(j4),
                                           ptsl=pts[:, j4 * 256:(j4 + 1) * 256]))
                    for h in hs:
                        S.add("dve", lambda e, sm=h["sm"][0], pS=h["pS"], am_i=am_i: e.scalar_tensor_tensor(
                            out=sm[:], in0=pS[:, 0:256], scalar=0.125, in1=amask.t[0][:, am_i, :], op0=ALU.mult, op1=ALU.add), h["pSb"] + [amask.b[0]], [h["sm"][1]])
                    for h in hs:
                        S.add("dve", lambda e, a_t=h["a"][0], sm=h["sm"][0]: e.tensor_reduce(out=a_t[:, 0:1], in_=sm[:], axis=AX.X, op=ALU.max), [h["sm"][1]], [h["a"][1]])
                    for h in hs:
                        S.add("dve", lambda e, a_t=h["a"][0], hd=h["hd"]: e.tensor_scalar(out=a_t[:, 1:2], in0=a_t[:, 0:1], scalar1=pc("sink", hd), scalar2=-1.0, op0=ALU.max, op1=ALU.mult),
                              [h["a"][1], PB], [h["a"][1]])
                    for h in hs:
                        S.add("act", lambda e, pp3=h["p3"][0], sm=h["sm"][0], a_t=h["a"][0]: e.activation(out=pp3[:], in_=sm[:], func=AF.Exp, bias=a_t[:, 1:2], accum_out=a_t[:, 2:3]),
                              [h["sm"][1], h["a"][1]], [h["p3"][1], h["a"][1]])
                    for h in hs:
                        S.add("act", lambda e, a_t=h["a"][0], hd=h["hd"]: e.activation(out=a_t[:, 3:4], in_=pc("sink", hd), func=AF.Exp, bias=a_t[:, 1:2]), [h["a"][1], PB], [h["a"][1]])
                    for h in hs:
                        S.add("dve", lambda e, a_t=h["a"][0]: e.tensor_tensor(out=a_t[:, 4:5], in0=a_t[:, 2:3], in1=a_t[:, 3:4], op=ALU.add), [h["a"][1]], [h["a"][1]])
                    for h in hs:
                        S.add("dve", lambda e, a_t=h["a"][0]: e.reciprocal(out=a_t[:, 5:6], in_=a_t[:, 4:5]), [h["a"][1]], [h["a"][1]])
                    for h in hs:
                        S.add("dve", lambda e, pn=h["pn"][0], pp3=h["p3"][0], a_t=h["a"][0]: e.tensor_scalar(out=pn[:], in0=pp3[:], scalar1=a_t[:, 5:6], scalar2=None, op0=ALU.mult),
                              [h["p3"][1], h["a"][1]], [h["pn"][1]])
                    for h in hs:
                        for kk_ in range(2):
                            S.add("pe", lambda e, ptsl=h["ptsl"], kk_=kk_, pn=h["pn"][0]: e.transpose(out=ptsl[:, kk_ * 128:(kk_ + 1) * 128], in_=pn[:, kk_ * 128:(kk_ + 1) * 128], identity=ident_b),
                                  [h["pn"][1], CB], psT_b[0])
                    for h in hs:
                        S.add("act", lambda e, pt2=h["pt"][0], ptsl=h["ptsl"]: e.activation(out=pt2[:], in_=v3(ptsl), func=AF.Copy), psT_b[0], [h["pt"][1]])
                    for qp in (qp0, qp0 + 1):
                        si = sbi * 4 + qp
                        g = qp // 2
                        pO, pOb = pObank[:, qp * 128:(qp + 1) * 128], pObb
                        n_ = 0
                        for h in [h for h in hs if h["qp"] == qp]:
                            pt2, pt2b = h["pt"]
                            hh = h["hh"]
                            for kk_, (vp, vpb) in enumerate([(vprev, vprevb), (vcur, vcurb)]):
                                S.add("pe", lambda e, pO=pO, vp=vp, g=g, hh=hh, pt2=pt2, kk_=kk_, n_=n_: e.matmul(
                                    pO[:, 0:128], lhsT=vp[:, g, hh * 64:hh * 64 + 128], rhs=pt2[:, kk_, :], start=(n_ == 0), stop=(n_ == 3)),
                                    [vpb, pt2b], pOb)
                                n_ += 1
                    for qp in (qp0, qp0 + 1):
                        si = sbi * 4 + qp
                        pO, pOb = pObank[:, qp * 128:(qp + 1) * 128], pObb
                        S.add("dve", lambda e, si=si, pO=pO, csl=csl: e.tensor_tensor(out=zatt(si)[0][:, csl], in0=pO[:, 0:128], in1=sga(si)[0][:, csl], op=ALU.mult),
                              pOb + [sga(si)[1]], [zatt(si)[1]])
                if lc == NOC - 1:
                    dump("za0", zatt(sbi * 4)[0][:], [128, SBT], [zatt(sbi * 4)[1]], BF16)
            if not own or upto < 5:
                continue
            drain(g1)
            fb_mode[0] = 0
            gfb = gen_first_b(sbi + 1) if sbi + 1 < nsb else iter(())
            ffb = mkfill(gfb, 0)
            mTt, mTb = mT()
            def load_j(j):
                return [ws_load([(lambda t: t[:, :, :], wbb[:, j * 128:(j + 1) * 128].rearrange("(bh p) c -> p bh c", p=128))]),
                        ws_load([(lambda t: t[:, :, :], wcols(O_GT + j * 128))]),
                        ws_load([(lambda t: t[:, :, :], wcols(O_GT + 1024 + j * 128))])]
            for j in range(8):
                cur_w = load_j(j)
                wt, wb = cur_w[0]
                pBr, pBrb = fullbank()
                pBa, pBab = fullbank()
                for hp in range(4):
                    si = sbi * 4 + hp
                    S.add("pe", lambda e, pBr=pBr, wt=wt, hp=hp, si=si: e.matmul(pBr[:, 0:SBT], lhsT=wt[:, hp, :], rhs=zr(si)[0][:], start=(hp == 0), stop=(hp == 3)),
                          wb + [zr(si)[1]], pBrb)
                for hp in range(4):
                    si = sbi * 4 + hp
                    S.add("pe", lambda e, pBa=pBa, wt=wt, hp=hp, si=si: e.matmul(pBa[:, 0:SBT], lhsT=wt[:, 4 + hp, :], rhs=zatt(si)[0][:], start=(hp == 0), stop=(hp == 3)),
                          wb + [zatt(si)[1]], pBab)
                halves = []
                for br in range(2):
                    wt2, wb2 = cur_w[1 + br]
                    pGt, pGtb = bankx()
                    for k in range(8):
                        S.add("pe", lambda e, pGt=pGt, k=k, wt2=wt2, hTt=hTt: e.matmul(
                            pGt[:, 0:SBT], lhsT=wt2[:, k, :], rhs=hTt[:, k, :], start=(k == 0), stop=(k == 7)), wb2 + [hTb], pGtb)
                    sg_, sgb_ = sgt(br)
                    S.add("act", lambda e, sg_=sg_, pGt=pGt: e.activation(out=sg_[:], in_=pGt[:, 0:SBT], func=AF.Sigmoid), pGtb, [sgb_])
                    halves.append((sg_, sgb_))
                m1, m1b = m12(0)
                m2, m2b = m12(1)
                S.add("dve", lambda e, m1=m1, pBr=pBr, sg_=halves[0][0]: e.tensor_tensor(out=m1[:], in0=pBr[:, 0:SBT], in1=sg_[:], op=ALU.mult), pBrb + [halves[0][1]], [m1b])
                S.add("dve", lambda e, m2=m2, pBa=pBa, sg_=halves[1][0]: e.tensor_tensor(out=m2[:], in0=pBa[:, 0:SBT], in1=sg_[:], op=ALU.mult), pBab + [halves[1][1]], [m2b])
                S.add("dve", lambda e, mTt=mTt, j=j, m1=m1, m2=m2: e.tensor_tensor(out=mTt[:, j, :], in0=m1[:], in1=m2[:], op=ALU.add), [m1b, m2b], [mTb])
                ffb()
            for c in range(CPS):
                gch = sbi * CPS + c
                lc = (sbi - own0) * CPS + c
                xr, xrb = xt(gch)
                dma("sp", xr[:], xw[gch * C:(gch + 1) * C, :], [], [xrb])
                for n in range(2):
                    pa, pab = fullbank()
                    for j in range(8):
                        S.add("pe", lambda e, pa=pa, j=j, n=n, mTt=mTt, c=c: e.matmul(
                            pa[:, 0:512], lhsT=mTt[:, j, c * C:(c + 1) * C], rhs=Wout.t[0][:, j, n * 512:(n + 1) * 512], start=(j == 0), stop=(j == 7)),
                            [mTb, Wout_b[j]], pab)
                    S.add("dve", lambda e, xr=xr, pa=pa, n=n: e.tensor_tensor(out=xr[:, n * 512:(n + 1) * 512], in0=pa[:, 0:512], in1=xr[:, n * 512:(n + 1) * 512], op=ALU.add),
                          pab + [xrb], [xrb])
                ft_, fb_ = fst(gch)
                hbt, hbb = hb(gch)
                rms_rstd(xr[:], [xrb], ft_, fb_, hbt[:], hbb)
                S.add("dve", lambda e, xr=xr, ft_=ft_: e.scalar_tensor_tensor(
                    out=xr[:], in0=xr[:], scalar=ft_[:, 2:3], in1=gfin.t[0][:], op0=ALU.mult, op1=ALU.mult), [xrb, fb_, gfin.b[0]], [xrb])
                dma("sp", out_d[lc * C:(lc + 1) * C, :], xr[:], [xrb], [], is_out=True)
            drain(gfb)

        for hp in range(4):
            dump(f"H{hp}", Ht(hp)[0][:], [128, 64], [Ht(hp)[1]])

        semnames = list(Sched.ENG) + [("dma", j) for j in range(Sched.NDMA)]
        sems = {}
        for sk in semnames:
            nm = sk if isinstance(sk, str) else f"dma{sk[1]}"
            sems[sk] = es.enter_context(nc.semaphore("s_" + nm))
        nc._sbuf_left = nc.sbuf_bytes_remaining
        block = es.enter_context(nc.Block())
        S.emit(nc, block, sems)
    nc._dbg_dumps = dump_d
    nc._sched_counts = dict(S.cnt)
    nc._sched_total = S.total
    return nc


def host_consts():
    s = np.arange(128)[:, None]
    t = np.arange(128)[None, :]
    cst = np.zeros((128, 9, 128), np.float32)
    cst[:, 0] = (s == t)
    cst[:, 1] = (s < t)
    cst[:, 2] = (s < t)
    cst[:, 3] = (s > t)
    cst[:, 4] = (s > t)
    cst[:, 5] = (s <= t)
    cst[:, 6] = (s <= t)
    cst[:, 7] = ((s // 64) == (t // 64))
    cst[:, 8] = 1.0
    return cst


def attn_masks(first):
    qi = np.arange(128)[:, None]
    kj = np.arange(256)[None, :]
    dist = qi + 128 - kj
    band = (dist >= 0) & (dist < 128)
    rest = np.where(band, 0.0, -1e30).astype(np.float32)
    fm = np.where(band & (kj >= 128), 0.0, -1e30).astype(np.float32)
    am = np.stack([fm if first else rest, rest], axis=1)
    return np.ascontiguousarray(am)


def pack_params(p):
    pp = np.zeros((128, NPP_IN), np.float32)

    def put(name, vec, n):
        v = np.asarray(vec, np.float32).reshape(n, 128)
        pp[:, PPI[name]:PPI[name] + n] = v.T
    put("mu", p["mu_shift"][0], 13)
    put("w0", p["w0"][0], 4)
    put("a0", p["a0"][0], 4)
    put("kk", p["k_k"][0], 4)
    put("ka", p["k_a"][0], 4)
    put("rk", p["r_k"][0], 4)
    put("gnw", p["gn_w"][0], 4)
    put("gnb", p["gn_b"][0], 4)
    bq = np.asarray(p["b_qkv"][0], np.float32)
    put("bq", bq[0:512], 4)
    bk = bq[512:640]
    pp[:, PPI["bk"] + 0] = np.concatenate([bk[0:64], bk[0:64]])
    pp[:, PPI["bk"] + 1] = np.concatenate([bk[64:128], bk[64:128]])
    sk = np.asarray(p["sinks"][0], np.float32)
    pp[:, PPI["sink"]:PPI["sink"] + 8] = np.broadcast_to(sk[None, :], (128, 8))
    wdi = np.zeros((128, 2, 512), np.float32)
    wdi[0:64, 0] = np.asarray(p["w_decay_up"][0], np.float32)
    wdi[64:128, 1] = np.asarray(p["w_iclr_up"][0], np.float32)
    common = {
        "w_in": np.ascontiguousarray(np.asarray(p["w_in"][0], np.float32)),
        "w_br": np.ascontiguousarray(np.stack([np.asarray(p["w_branch_rwkv"][0], np.float32),
                                               np.asarray(p["w_branch_att"][0], np.float32)])),
        "w_out": np.ascontiguousarray(np.asarray(p["w_out"][0], np.float32)),
        "wdi": np.ascontiguousarray(wdi),
        "pp": pp,
        "gpre_b": np.ascontiguousarray(np.broadcast_to(np.asarray(p["g_pre"][0], np.float32)[None], (128, D))),
        "gfin_b": np.ascontiguousarray(np.broadcast_to(np.asarray(p["g_final"], np.float32)[None], (128, D))),
        "bv_b": np.ascontiguousarray(np.broadcast_to(bq[640:768][None], (128, 128))),
        "cst": host_consts(),
    }
    return common


def kernel(**inputs):
    x = np.asarray(inputs["x"], np.float32)
    common = pack_params(inputs)
    nc = build()
    in_maps = []
    for c in range(NCORES):
        b, q = c // 4, c % 4
        end = (q + 1) * OWN_TOK
        xw = np.zeros((SEQ, D), np.float32)
        xw[SEQ - end:] = x[b, :end]
        m = dict(common)
        m["xw"] = xw
        m["amask"] = attn_masks(q == 0)
        in_maps.append(m)
    res = run_bass_kernel_spmd(nc, in_maps, core_ids=list(range(NCORES)))
    out = np.zeros((2, SEQ, D), np.float32)
    for c in range(NCORES):
        b, q = c // 4, c % 4
        out[b, q * OWN_TOK:(q + 1) * OWN_TOK] = res.results[c]["out"]
    return out
```

```python
import numpy as np
import concourse.bass as bass
import concourse.mybir as mybir
from concourse.bass_utils import run_bass_kernel_spmd

F32 = mybir.dt.float32
BF16 = mybir.dt.bfloat16
AF = mybir.ActivationFunctionType
ALU = mybir.AluOpType
AX = mybir.AxisListType

D = 1024
NCORES = 8
SEQ = 8192
OWN_TOK = 2048
C = 128
SBT = 256
CPS = SBT // C
RMS_EPS = 1e-6
GN_EPS = 64e-5
IN_COLS = 5504
O_SH = 0
O_GR = 1664
O_Q = 2176
O_K = 2688
O_V = 2816
O_GA = 2944
O_GT = 3456

PPI = {}
_n = 0
for _name, _cnt in [("mu", 13), ("w0", 4), ("a0", 4), ("kk", 4), ("ka", 4), ("rk", 4),
                    ("gnw", 4), ("gnb", 4), ("bq", 4), ("bk", 2), ("sink", 8)]:
    PPI[_name] = _n
    _n += _cnt
NPP_IN = _n
for _name, _cnt in [("omu", 13), ("nw0", 4), ("omka", 4), ("na0", 4)]:
    PPI[_name] = _n
    _n += _cnt
NPP = _n


class Buf:
    __slots__ = ("name", "w", "r", "excl")

    def __init__(self, name, excl=False):
        self.name = name
        self.w = None
        self.r = []
        self.excl = excl


class Sched:
    ENG = ("pe", "act", "dve", "pool", "sp")
    NDMA = 24

    def __init__(self, same_sync=True):
        self.ops = {e: [] for e in self.ENG}
        self.cnt = {e: 0 for e in self.ENG}
        self.waited = {e: {} for e in self.ENG}
        self.same_sync = same_sync
        self.dma_val = [0] * self.NDMA
        self.dma_rr = 0
        self.dma_rr2 = 0
        self.out_tokens = []

    def add(self, eng, fn, reads=(), writes=(), dma=False, is_out=False):
        self.total = getattr(self, "total", 0) + 1
        if not dma and self.total > getattr(self, "cut", 10 ** 9):
            return None
        deps = {}

        def need(tk, hard):
            d = deps.get(tk[0])
            if d is None:
                deps[tk[0]] = [tk[1], tk[2], hard]
            else:
                d[0] = max(d[0], tk[1])
                d[2] = d[2] or hard
        for b in reads:
            if b.w is not None:
                need(b.w, True)
            if b.excl:
                for tk in b.r:
                    need(tk, False)
        for b in writes:
            if b.w is not None:
                need(b.w, True)
            for tk in b.r:
                need(tk, False)
        waits = []
        for semkey, (val, src, hard) in deps.items():
            if src == eng and not isinstance(semkey, tuple):
                if eng in ("pe", "sp"):
                    continue
                if not hard or not self.same_sync:
                    continue
            if self.waited[eng].get(semkey, 0) >= val:
                continue
            self.waited[eng][semkey] = val
            waits.append((semkey, val))
        if dma:
            half = self.NDMA // 2
            if eng == "sp":
                j = self.dma_rr
                self.dma_rr = (self.dma_rr + 1) % half
            else:
                j = half + self.dma_rr2
                self.dma_rr2 = (self.dma_rr2 + 1) % half
            semkey = ("dma", j)
            if self.dma_val[j] > 0 and self.waited[eng].get(semkey, 0) < self.dma_val[j]:
                self.waited[eng][semkey] = self.dma_val[j]
                waits.append((semkey, self.dma_val[j]))
            self.dma_val[j] += 16
            tok = (semkey, self.dma_val[j], eng)
            inc = 16
        else:
            self.cnt[eng] += 1
            tok = (eng, self.cnt[eng], eng)
            inc = 1
        for b in reads:
            b.r.append(tok)
        for b in writes:
            b.w = tok
            b.r = []
        if is_out:
            self.out_tokens.append(tok)
        self.ops[eng].append((waits, fn, tok[0], inc))
        return tok

    def emit(self, nc, block, sems):
        engmap = {"pe": block.tensor, "act": block.scalar, "dve": block.vector,
                  "pool": block.gpsimd, "sp": block.sync}
        for e in self.ENG:
            ops = self.ops[e]
            final = list(self.out_tokens) if e == "sp" else ()

            def body(eng, ops=ops, final=final):
                for waits, fn, semkey, inc in ops:
                    for sk, val in waits:
                        eng.wait_ge(sems[sk], val)
                    fn(eng).then_inc(sems[semkey], inc)
                for tk in final:
                    eng.wait_ge(sems[tk[0]], tk[1])
                if final != ():
                    for j in range(self.NDMA):
                        if self.dma_val[j] > 0:
                            eng.wait_ge(sems[("dma", j)], self.dma_val[j])
            engmap[e](body)


def build(nsb=SEQ // SBT, nown=OWN_TOK // SBT, upto=99, dumps=(), same_sync=True, cut=None):
    from contextlib import ExitStack
    nc = bass.Bass("TRN2", target_bir_lowering=False)
    WT = nsb * SBT
    OT = nown * SBT
    NOC = nown * CPS
    S = Sched(same_sync=same_sync)
    if cut is not None:
        S.cut = cut

    def din(name, shape, dt=F32):
        return nc.dram_tensor(name, list(shape), dt, kind="ExternalInput").ap()

    xw = din("xw", [WT, D])
    w_in = din("w_in", [D, IN_COLS])
    w_br = din("w_br", [2, 512, D])
    w_out = din("w_out", [D, D])
    wdi = din("wdi", [128, 2, 512])
    pp_in = din("pp", [128, NPP_IN])
    gpre_d = din("gpre_b", [128, D])
    gfin_d = din("gfin_b", [128, D])
    bv_d = din("bv_b", [128, 128])
    cst_d = din("cst", [128, 9, 128])
    am_d = din("amask", [128, 2, 256])
    out_d = nc.dram_tensor("out", [OT, D], F32, kind="ExternalOutput").ap()
    wib = nc.dram_tensor("wib_scratch", [D, IN_COLS - O_GR], BF16).ap()
    wbb = nc.dram_tensor("wbb_scratch", [2 * 512, D], BF16).ap()
    dump_d = {}

    es = ExitStack()
    with es:
        def sb(name, shape, dt=F32):
            return es.enter_context(nc.sbuf_tensor(name, list(shape), dt))

        def ps(name, shape, dt=F32):
            return es.enter_context(nc.psum_tensor(name, list(shape), dt))

        class T:
            def __init__(self, name, shape, dt=F32, n=1):
                self.t = [sb(f"{name}{i}", shape, dt) for i in range(n)]
                self.b = [Buf(f"{name}{i}") for i in range(n)]
                self.n = n

            def __call__(self, i=0):
                return self.t[i % self.n], self.b[i % self.n]

        def dma(eng, out, in_, reads, writes, is_out=False):
            return S.add(eng, lambda e: e.dma_start(out=out, in_=in_), reads, writes, dma=True, is_out=is_out)

        def dump(name, ap, shape, reads, dt=F32):
            if name not in dumps:
                return
            dd = nc.dram_tensor("dbg_" + name, list(shape), dt, kind="ExternalOutput").ap()
            dump_d[name] = dd
            dma("sp", dd, ap, reads, [], is_out=True)

        xt = T("xt", [128, D], F32, 2)
        cst_b = T("cst_b", [128, 8, 128], BF16)
        cst2 = T("cst2", [128, 1, 128])
        amask = T("amask", [128, 2, 256])
        PP = T("PP", [128, NPP])
        gpre = T("gpre", [128, D])
        gfin = T("gfin", [128, D])
        bvb = T("bvb", [128, 128])
        Wdb = T("Wdb", [128, 3, 512], BF16)
        Wsh = T("Wsh", [128, 8, 1664], BF16)
        Wout = T("Wout", [128, 8, D], BF16)

        stg = xt.t[1][:, :].rearrange("p (a b) -> p a b", a=8)
        dma("sp", stg, cst_d[:, 0:8, :], [], [xt.b[1]])
        dma("sp", cst2.t[0][:], cst_d[:, 8:9, :], [], [cst2.b[0]])
        dma("sp", PP.t[0][:, 0:NPP_IN], pp_in, [], [PP.b[0]])
        dma("sp", gpre.t[0][:], gpre_d, [], [gpre.b[0]])
        stg_w = xt.t[0][:, :].rearrange("p (a b) -> p a b", a=2)
        dma("sp", stg_w, wdi, [], [xt.b[0]])
        S.add("act", lambda e: e.activation(out=Wdb.t[0][:, 0, :], in_=stg_w[:, 0, :], func=AF.Copy), [xt.b[0]], [Wdb.b[0]])
        S.add("act", lambda e: e.activation(out=Wdb.t[0][:, 2, :], in_=stg_w[:, 1, :], func=AF.Copy), [xt.b[0]], [Wdb.b[0]])
        S.add("dve", lambda e: e.tensor_tensor(out=Wdb.t[0][:, 1, :], in0=stg_w[:, 0, :], in1=Wdb.t[0][:, 0, :], op=ALU.subtract), [xt.b[0], Wdb.b[0]], [Wdb.b[0]])
        Wsh_b = [Buf(f"Wsh_k{k}") for k in range(8)]
        for k in range(8):
            S.add("pool", lambda e, k=k: e.dma_start(out=Wsh.t[0][:, k, :], in_=w_in[k * 128:(k + 1) * 128, O_SH:O_SH + 1664]),
                  [], [Wsh_b[k]], dma=True)
        dma("sp", amask.t[0][:], am_d, [], [amask.b[0]])
        dma("sp", gfin.t[0][:], gfin_d, [], [gfin.b[0]])
        dma("sp", bvb.t[0][:], bv_d, [], [bvb.b[0]])
        S.add("dve", lambda e: e.tensor_copy(out=cst_b.t[0][:], in_=stg), [xt.b[1]], [cst_b.b[0]])
        ident_b = cst_b.t[0][:, 0, :]
        mask4 = cst_b.t[0][:, 1:5, :]
        mle2 = cst_b.t[0][:, 5:7, :]
        bones_b = cst_b.t[0][:, 7, :]
        ones_f = cst2.t[0][:, 0, :]
        CB = cst_b.b[0]
        CF = cst2.b[0]
        ppt = PP.t[0]
        PB = PP.b[0]

        def pc(name, i=0):
            j = PPI[name] + i
            return ppt[:, j:j + 1]

        S.add("dve", lambda e: e.tensor_scalar(out=ppt[:, PPI["omu"]:PPI["omu"] + 13], in0=ppt[:, PPI["mu"]:PPI["mu"] + 13],
                                               scalar1=-1.0, scalar2=1.0, op0=ALU.mult, op1=ALU.add), [PB], [PB])
        S.add("dve", lambda e: e.tensor_scalar(out=ppt[:, PPI["nw0"]:PPI["nw0"] + 4], in0=ppt[:, PPI["w0"]:PPI["w0"] + 4],
                                               scalar1=-1.0, scalar2=None, op0=ALU.mult), [PB], [PB])
        S.add("dve", lambda e: e.tensor_scalar(out=ppt[:, PPI["omka"]:PPI["omka"] + 4], in0=ppt[:, PPI["ka"]:PPI["ka"] + 4],
                                               scalar1=-1.0, scalar2=1.0, op0=ALU.mult, op1=ALU.add), [PB], [PB])
        S.add("dve", lambda e: e.tensor_scalar(out=ppt[:, PPI["na0"]:PPI["na0"] + 4], in0=ppt[:, PPI["a0"]:PPI["a0"] + 4],
                                               scalar1=-1.0, scalar2=None, op0=ALU.mult), [PB], [PB])

        psA = [ps(f"psA{i}", [128, 512]) for i in range(2)]
        psA_b = [Buf(f"psA{i}", True) for i in range(2)]
        psT = [ps(f"psT{i}", [128, 1024], BF16) for i in range(2)]
        psT_b = [[Buf(f"psT{i}_{h}", True) for h in range(2)] for i in range(2)]
        psLU = [[ps(f"psL{i}", [128, 512]), ps(f"psU{i}", [128, 512])] for i in range(2)]
        psLU_b = [[[Buf(f"psLU{i}_{lu}_{s}", True) for s in range(4)] for lu in range(2)] for i in range(2)]
        arr = [0]
        prr = [0]
        srr = [0]

        fb_mode = [0]

        def fullbank():
            if fb_mode[0]:
                return psA[1], [psA_b[1]]
            i = arr[0]
            arr[0] = (i + 1) % 2
            return psA[i], [psA_b[i]]

        def pair(ns):
            r = prr[0]
            if (r % 4) + ns > 4:
                r = (r // 4 + 1) * 4
            r %= 8
            p, s = r // 4, r % 4
            prr[0] = (r + ns) % 8
            sl = slice(s * 128, (s + ns) * 128)
            return (psLU[p][0][:, sl], psLU_b[p][0][s:s + ns], psLU[p][1][:, sl], psLU_b[p][1][s:s + ns])

        brr = [0]

        def bankx():
            i = brr[0]
            brr[0] = (i + 1) % 4
            p, lu = i // 2, i % 2
            return psLU[p][lu], list(psLU_b[p][lu])

        def single(ns):
            bk, bb = bankx()
            return bk[:, 0:ns * 128], bb

        hb = T("hb", [128, D], BF16, 1)
        hT = T("hT", [128, 8, SBT], BF16, 2)
        st0 = T("st0", [128, 4], F32, 2)
        shwa = T("shwa", [128, SBT])
        shtmp = T("shtmp", [128, SBT], F32, 1)
        shr = T("shr", [128, SBT], F32, 2)
        shk = T("shk", [128, SBT], F32, 2)
        shv = T("shv", [128, SBT], F32, 2)
        tw = T("tw", [128, SBT])
        tw_hi = T("tw_hi", [128, SBT], BF16)
        tw_lo = T("tw_lo", [128, SBT], BF16)
        t_k2b = T("t_k2b", [128, SBT], BF16)
        t_rkb = T("t_rkb", [128, SBT], BF16)
        Hhl = T("Hhl", [128, 2, 64], BF16, 4)
        t_e1 = T("t_e1", [128, SBT])
        t_ew = T("t_ew", [128, SBT])
        t_a = T("t_a", [128, SBT])
        t_cs = T("t_cs", [128, SBT])
        t_csp = T("t_csp", [128, SBT])
        t_en = T("t_en", [128, SBT])
        t_ep = T("t_ep", [128, SBT])
        t_k2 = T("t_k2", [128, SBT])
        t_kkn = T("t_kkn", [128, SBT])
        t_ab = T("t_ab", [128, SBT])
        t_f = T("t_f", [128, SBT])
        gC = T("gC", [128, CPS], F32, 8)
        AR = T("AR", [128, CPS, 2, C], BF16, 4)
        BT = T("BT", [128, SBT], BF16, 4)
        KT = T("KT", [128, SBT], BF16, 4)
        vbf = T("vbf", [128, SBT], BF16, 4)
        bonus = T("bonus", [128, SBT], BF16, 4)
        tm = T("tm", [128, 4, 128], BF16, 4)
        PZ = T("PZ", [128, 3, SBT], BF16, 8)
        Hbfz = T("Hbfz", [128, 64], BF16, 8)
        qTz = T("qTz", [128, SBT], BF16, 8)
        NG = 4
        NMt = T("NM", [128, 2, 2, 128], BF16, 2 * NG)
        Mak = T("Mak", [128, 2, 128], BF16, NG)
        RBK = T("RBK", [128, 2, 2, 128], BF16, NG)
        PAIRS = [(0, 1), (2, 3)]
        Xtile = T("Xt", [128, 2, 2, 64], BF16, 2 * NG)
        ATbd = T("ATbd", [128, 128], BF16, NG)
        Gsb = T("Gsb", [128, 64], F32, NG)
        Ht = T("Hst", [128, 64], F32, 4)
        s1t = T("s1t", [128, 64], F32, NG)
        Wz = T("Wz", [128, 3, 64], BF16, NG)
        Wp = T("Wp", [128, 2, 64], BF16, NG)
        zlo = T("zlo", [128, 64], F32, NG)
        QT = T("QT", [128, 128], BF16, NG)
        prevcol = T("prevcol", [128, 13])
        kc = T("kcols", [128, 8])
        NWS = 5
        ws = T("ws", [128, 8, 128], BF16, NWS)
        ws_b2 = [Buf(f"ws_b2_{i}") for i in range(NWS)]
        wsrr = [0]
        sgr = T("sgr", [128, SBT], BF16, 4)
        sga = T("sga", [128, SBT], BF16, 4)
        NKS = 4
        KTatt = T("KTatt", [128, NKS * 128], BF16, 2)
        NV = 4
        Vpad = T("Vpad", [128, 2, 192], BF16, NV)
        ysqt = T("ysq", [128, 512], F32, 1)
        yn = T("yn", [128, 512], BF16, 1)
        gst = T("gst", [128, 6, 8], F32, 1)
        t1t = T("t1t", [128, 128], F32, 1)
        zr = T("zr", [128, SBT], BF16, 4)
        zatt = T("zatt", [128, SBT], BF16, 4)
        smt = T("smt", [128, 256], F32, 4)
        p32 = T("p32", [128, 256], F32, 4)
        pnt = T("pnt", [128, 256], BF16, 4)
        ptt = T("ptt", [128, 2, 128], BF16, 4)
        ast = T("ast", [128, 8], F32, 4)
        mT = T("mT", [128, 8, SBT], BF16, 1)
        sgt = T("sgt", [128, SBT], F32, 2)
        m12 = T("m12", [128, SBT], F32, 2)
        fst = T("fst", [128, 4], F32, 2)

        S.add("pool", lambda e: e.memset(prevcol.t[0][:], 0.0), [], [prevcol.b[0]])
        kct = kc.t[0]
        KB = kc.b[0]
        for j, val in enumerate([RMS_EPS, 1.0, -0.5, 1e-12, GN_EPS]):
            S.add("pool", lambda e, j=j, val=val: e.memset(kct[:, j:j + 1], val), [], [KB])
        eps_col = kct[:, 0:1]
        one_col = kct[:, 1:2]
        mhalf_col = kct[:, 2:3]
        tiny_col = kct[:, 3:4]
        gneps_col = kct[:, 4:5]
        for i in range(NG):
            S.add("pool", lambda e, i=i: e.memset(ATbd.t[i][:], 0.0), [], [ATbd.b[i]])
            S.add("pool", lambda e, i=i: e.memset(Wz.t[i][:], 0.0), [], [Wz.b[i]])
        for i in range(4):
            S.add("pool", lambda e, i=i: e.memset(Ht.t[i][:], 0.0), [], [Ht.b[i]])
        for i in range(NV):
            S.add("pool", lambda e, i=i: e.memset(Vpad.t[i][:], 0.0), [], [Vpad.b[i]])
        for i in range(8):
            S.add("pool", lambda e, i=i: e.memset(PZ.t[i][:], 0.0), [], [PZ.b[i]])
            S.add("pool", lambda e, i=i: e.memset(Hbfz.t[i][:], 0.0), [], [Hbfz.b[i]])
            S.add("pool", lambda e, i=i: e.memset(qTz.t[i][:], 0.0), [], [qTz.b[i]])
        for i in range(2):
            S.add("pool", lambda e, i=i: e.memset(KTatt.t[i][:], 0.0), [], [KTatt.b[i]])
        Wout_b = [Buf(f"wout{k}") for k in range(8)]
        for k in range(8):
            S.add("pool", lambda e, k=k: e.dma_start(out=Wout.t[0][:, k, :], in_=w_out[k * 128:(k + 1) * 128, :]),
                  [], [Wout_b[k]], dma=True)

        wib_b = [Buf(f"wib{k}") for k in range(8)]
        wbb_b = [Buf(f"wbb{k}") for k in range(8)]
        w_br_flat = w_br.rearrange("b r c -> (b r) c")
        for k in range(8):
            S.add("pool", lambda e, k=k: e.dma_start(out=wib[k * 128:(k + 1) * 128, :], in_=w_in[k * 128:(k + 1) * 128, O_GR:IN_COLS]),
                  [], [wib_b[k]], dma=True)
        for k in range(8):
            S.add("pool", lambda e, k=k: e.dma_start(out=wbb[k * 128:(k + 1) * 128, :], in_=w_br_flat[k * 128:(k + 1) * 128, :]),
                  [], [wbb_b[k]], dma=True)

        def bcm(ap2, n):
            a = ap2.ap
            return bass.AP(ap2.tensor, ap2.offset, [list(a[0]), [0, n], list(a[1])])

        def bcl(ap2, n):
            a = ap2.ap
            return bass.AP(ap2.tensor, ap2.offset, [list(a[0]), list(a[1]), [0, n]])

        def v3(ap, h=2):
            return ap.rearrange("p (h t) -> p h t", h=h)

        def rms_rstd(in_ap, in_bufs, stt, stb, junk_ap, junk_buf):
            S.add("act", lambda e: e.activation(out=junk_ap, in_=in_ap, func=AF.Square, accum_out=stt[:, 0:1]),
                  in_bufs, [junk_buf, stb])
            S.add("act", lambda e: e.activation(out=stt[:, 1:2], in_=stt[:, 0:1], func=AF.Ln, bias=eps_col, scale=1.0 / D),
                  [stb, KB], [stb])
            S.add("act", lambda e: e.activation(out=stt[:, 2:3], in_=stt[:, 1:2], func=AF.Exp, scale=-0.5), [stb], [stb])

        def ws_load(srcs):
            i = wsrr[0]
            wsrr[0] = (i + 1) % NWS
            t = ws.t[i]
            bufs = [ws.b[i], ws_b2[i]]
            for j, (dfn, dap) in enumerate(srcs):
                S.add("sp", lambda e, dfn=dfn, dap=dap, t=t: e.dma_start(out=dfn(t), in_=dap), wib_b + wbb_b, [bufs[j]], dma=True)
            return t, bufs[:len(srcs)]

        def wcols(c0, n=128):
            return wib[:, c0 - O_GR:c0 - O_GR + n].rearrange("(k p) c -> p k c", p=128)

        def proj_fm(hTt, hTb, wt, wbufs, ncols=SBT, col0=0):
            pa, pab = fullbank()
            for k in range(8):
                S.add("pe", lambda e, pa=pa, k=k, wt=wt, hTt=hTt: e.matmul(
                    pa[:, 0:ncols], lhsT=wt[:, k, :], rhs=hTt[:, k, col0:col0 + ncols], start=(k == 0), stop=(k == 7)),
                    wbufs + [hTb], pab)
            return pa, pab

        P_ = [slice(0, 64), slice(64, 128)]
        own0 = nsb - nown

        def stage2_header():
            swt, swb = shwa()
            twt, twb = tw()
            S.add("act", lambda e, twt=twt, swt=swt: e.activation(out=twt[0:64, :], in_=swt[0:64, :], func=AF.Exp, scale=2.0), [swb], [twb])
            S.add("act", lambda e, twt=twt: e.activation(out=twt[0:64, :], in_=twt[0:64, :], func=AF.Ln, bias=kct[0:64, 1:2]), [twb, KB], [twb])
            S.add("act", lambda e, twt=twt: e.activation(out=twt[0:64, :], in_=twt[0:64, :], func=AF.Exp, scale=-1.0), [twb], [twb])
            S.add("dve", lambda e, twt=twt: e.tensor_scalar(out=twt[0:64, :], in0=twt[0:64, :], scalar1=-2.0, scalar2=1.0, op0=ALU.mult, op1=ALU.add), [twb], [twb])
            S.add("act", lambda e, twt=twt, swt=swt: e.activation(out=twt[64:128, :], in_=swt[64:128, :], func=AF.Copy), [swb, twb], [twb])
            twh, twhb = tw_hi()
            twl, twlb = tw_lo()
            S.add("act", lambda e, twh=twh, twt=twt: e.activation(out=twh[:], in_=twt[:], func=AF.Copy), [twb], [twhb])
            S.add("dve", lambda e, twl=twl, twt=twt, twh=twh: e.tensor_tensor(out=twl[:], in0=twt[:], in1=twh[:], op=ALU.subtract), [twb, twhb], [twlb])
            return dict(twt=twt, twh=twh, twl=twl, twhb=twhb, twlb=twlb, twb=twb)
        def prep_hp(hp, sbi, own, twt=None, twh=None, twl=None, twhb=None, twlb=None, twb=None):
            si = sbi * 4 + hp
            rt, rb = shr(si)
            kt_, kb_ = shk(si)
            vt, vb = shv(si)
            pD, pDb = fullbank()
            hsl = slice(hp * 128, (hp + 1) * 128)
            S.add("pe", lambda e, pD=pD, hsl=hsl, twh=twh: e.matmul(pD[:, 0:SBT], lhsT=Wdb.t[0][:, 0, hsl], rhs=twh[:, :], start=True, stop=False),
                  [Wdb.b[0], twhb], pDb)
            S.add("pe", lambda e, pD=pD, hsl=hsl, twl=twl: e.matmul(pD[:, 0:SBT], lhsT=Wdb.t[0][:, 0, hsl], rhs=twl[:, :], start=False, stop=False),
                  [Wdb.b[0], twlb], pDb)
            S.add("pe", lambda e, pD=pD, hsl=hsl, twh=twh: e.matmul(pD[:, 0:SBT], lhsT=Wdb.t[0][:, 1, hsl], rhs=twh[:, :], start=False, stop=True),
                  [Wdb.b[0], twhb], pDb)
            e1, e1b = t_e1()
            ew, ewb = t_ew()
            at, ab_ = t_a()
            cs, csb = t_cs()
            csp, cspb = t_csp()
            en, enb = t_en()
            k2, k2b = t_k2()
            kkn, kknb = t_kkn()
            abt, abb = t_ab()
            ft, fb = t_f()
            S.add("act", lambda e, e1=e1, pD=pD, hp=hp: e.activation(out=e1[:], in_=pD[:, 0:SBT], func=AF.Exp, bias=pc("nw0", hp), scale=-1.0),
                  pDb + [PB], [e1b])
            pAa, pAb = fullbank()
            S.add("pe", lambda e, pAa=pAa, hsl=hsl, twh=twh: e.matmul(pAa[:, 0:SBT], lhsT=Wdb.t[0][:, 2, hsl], rhs=twh[:, :], start=True, stop=True),
                  [Wdb.b[0], twhb], pAb)
            S.add("act", lambda e, e1=e1: e.activation(out=e1[:], in_=e1[:], func=AF.Ln, bias=one_col), [e1b, KB], [e1b])
            S.add("act", lambda e, e1=e1, ew=ew: e.activation(out=ew[:], in_=e1[:], func=AF.Exp, bias=mhalf_col, scale=-1.0), [e1b, KB], [ewb])
            S.add("act", lambda e, at=at, pAa=pAa, hp=hp: e.activation(out=at[:], in_=pAa[:, 0:SBT], func=AF.Exp, bias=pc("na0", hp), scale=-1.0),
                  pAb + [PB], [ab_])
            yield
            S.add("act", lambda e, at=at: e.activation(out=at[:], in_=at[:], func=AF.Ln, bias=one_col), [ab_, KB], [ab_])
            S.add("act", lambda e, at=at: e.activation(out=at[:], in_=at[:], func=AF.Exp, scale=-1.0), [ab_], [ab_])
            for c in range(CPS):
                S.add("dve", lambda e, cs=cs, ew=ew, c=c: e.tensor_tensor_scan(
                    out=cs[:, c * C:(c + 1) * C], data0=ones_f, data1=ew[:, c * C:(c + 1) * C], initial=0.0,
                    op0=ALU.mult, op1=ALU.add), [ewb, CF], [csb])
            S.add("pool", lambda e, csp=csp, cs=cs, ew=ew: e.tensor_tensor(out=csp[:], in0=cs[:], in1=ew[:], op=ALU.subtract), [csb, ewb], [cspb])
            S.add("act", lambda e, en=en, cs=cs: e.activation(out=en[:], in_=cs[:], func=AF.Exp), [csb], [enb])
            S.add("act", lambda e, csp=csp: e.activation(out=csp[:], in_=csp[:], func=AF.Exp, scale=-1.0), [cspb], [cspb])
            gct, gcb = gC(si)
            S.add("act", lambda e, gct=gct, cs=cs: e.activation(
                out=gct[:, 0:CPS], in_=cs[:, :].rearrange("p (c t) -> p c t", t=C)[:, :, C - 1], func=AF.Exp, scale=-1.0), [csb], [gcb])
            yield
            k2h, k2hb = t_k2b()
            S.add("act", lambda e, k2h=k2h, kt_=kt_, hp=hp: e.activation(out=k2h[:], in_=kt_[:], func=AF.Square, scale=pc("kk", hp)), [kb_, PB], [k2hb])
            pS_, pSb = fullbank()
            S.add("pe", lambda e, pS_=pS_, k2h=k2h: e.matmul(pS_[:, 0:SBT], lhsT=bones_b, rhs=k2h[:], start=True, stop=True), [CB, k2hb], pSb)
            S.add("act", lambda e, k2=k2, pS_=pS_: e.activation(out=k2[:], in_=pS_[:, 0:SBT], func=AF.Ln, bias=tiny_col), pSb + [KB], [k2b])
            S.add("act", lambda e, k2=k2: e.activation(out=k2[:], in_=k2[:], func=AF.Exp, scale=-0.5), [k2b], [k2b])
            S.add("dve", lambda e, kkn=kkn, kt_=kt_, k2=k2, hp=hp: e.scalar_tensor_tensor(
                out=kkn[:], in0=kt_[:], scalar=pc("kk", hp), in1=k2[:], op0=ALU.mult, op1=ALU.mult), [kb_, k2b, PB], [kknb])
            yield
            ARt, ARb = AR(si)
            BTt, BTb = BT(si)
            KTt, KTb = KT(si)
            vbt, vbb = vbf(si)
            S.add("dve", lambda e, ARt=ARt, kkn=kkn, csp=csp: e.scalar_tensor_tensor(
                out=ARt[:, :, 0, :], in0=kkn[:, :].rearrange("p (c t) -> p c t", t=C), scalar=-1.0,
                in1=csp[:, :].rearrange("p (c t) -> p c t", t=C), op0=ALU.mult, op1=ALU.mult), [kknb, cspb], [ARb])
            S.add("pool", lambda e, abt=abt, kkn=kkn, at=at: e.tensor_tensor(out=abt[:], in0=kkn[:], in1=at[:], op=ALU.mult), [kknb, ab_], [abb])
            S.add("pool", lambda e, BTt=BTt, abt=abt, en=en: e.tensor_tensor(out=BTt[:], in0=abt[:], in1=en[:], op=ALU.mult), [abb, enb], [BTb])
            yield
            S.add("dve", lambda e, ft=ft, at=at, hp=hp: e.tensor_scalar(out=ft[:], in0=at[:], scalar1=pc("ka", hp), scalar2=pc("omka", hp),
                                                                    op0=ALU.mult, op1=ALU.add), [ab_, PB], [fb])
            S.add("pool", lambda e, ft=ft, kt_=kt_: e.tensor_tensor(out=ft[:], in0=kt_[:], in1=ft[:], op=ALU.mult), [kb_, fb], [fb])
            S.add("pool", lambda e, KTt=KTt, ft=ft, en=en: e.tensor_tensor(out=KTt[:], in0=ft[:], in1=en[:], op=ALU.mult), [fb, enb], [KTb])
            S.add("act", lambda e, vbt=vbt, vt=vt: e.activation(out=vbt[:], in_=vt[:], func=AF.Copy), [vb], [vbb])
            yield
            for hh in range(2):
                zt, zb = PZ(si * 2 + hh)
                S.add("pool", lambda e, zt=zt, ARt=ARt, hh=hh: e.tensor_copy(out=zt[P_[hh], 0, :].rearrange("p (c t) -> p c t", t=C), in_=ARt[P_[hh], :, 0, :]), [ARb], [zb])
                S.add("pool", lambda e, zt=zt, BTt=BTt, hh=hh: e.tensor_copy(out=zt[P_[hh], 1, :], in_=BTt[P_[hh], :]), [BTb], [zb])
                S.add("pool", lambda e, zt=zt, KTt=KTt, hh=hh: e.tensor_copy(out=zt[P_[hh], 2, :], in_=KTt[P_[hh], :]), [KTb], [zb])
            if own:
                ep, epb = t_ep()
                S.add("act", lambda e, ep=ep, cs=cs: e.activation(out=ep[:], in_=cs[:], func=AF.Exp, scale=-1.0), [csb], [epb])
                S.add("dve", lambda e, ARt=ARt, rt=rt, ep=ep: e.tensor_tensor(
                    out=ARt[:, :, 1, :], in0=rt[:, :].rearrange("p (c t) -> p c t", t=C),
                    in1=ep[:, :].rearrange("p (c t) -> p c t", t=C), op=ALU.mult), [rb, epb], [ARb])
                rkb_t, rkb_b = t_rkb()
                S.add("dve", lambda e, rkb_t=rkb_t, rt=rt, ft=ft, hp=hp: e.scalar_tensor_tensor(
                    out=rkb_t[:], in0=rt[:], scalar=pc("rk", hp), in1=ft[:], op0=ALU.mult, op1=ALU.mult), [rb, fb, PB], [rkb_b])
                pB_, pBb = fullbank()
                S.add("pe", lambda e, pB_=pB_, rkb_t=rkb_t: e.matmul(pB_[:, 0:SBT], lhsT=bones_b, rhs=rkb_t[:], start=True, stop=True), [CB, rkb_b], pBb)
                bnt, bnb = bonus(si)
                S.add("dve", lambda e, bnt=bnt, pB_=pB_, vt=vt: e.tensor_tensor(out=bnt[:], in0=pB_[:, 0:SBT], in1=vt[:], op=ALU.mult), pBb + [vb], [bnb])
            yield
        def proj_tile(sbi, ct, hTt, hTb):
            pa, pab = fullbank()
            for k in range(8):
                S.add("pe", lambda e, pa=pa, k=k, ct=ct, hTt=hTt: e.matmul(
                    pa[:, 0:SBT], lhsT=Wsh.t[0][:, k, ct * 128:(ct + 1) * 128], rhs=hTt[:, k, :],
                    start=(k == 0), stop=(k == 7)), [Wsh_b[k], hTb], pab)
            if ct == 12:
                dst, dstb = shwa()
            else:
                hp = ct % 4
                dst, dstb = (shr, shk, shv)[ct // 4](sbi * 4 + hp)
            tmp, tmpb = shtmp(ct)
            S.add("act", lambda e, tmp=tmp, pa=pa, ct=ct: e.activation(
                out=tmp[:], in_=pa[:, 0:SBT], func=AF.Copy, scale=pc("omu", ct)), pab + [PB], [tmpb])
            S.add("dve", lambda e, dst=dst, pa=pa, tmp=tmp, ct=ct: e.scalar_tensor_tensor(
                out=dst[:, 1:SBT], in0=pa[:, 0:SBT - 1], scalar=pc("mu", ct), in1=tmp[:, 1:SBT],
                op0=ALU.mult, op1=ALU.add), pab + [tmpb, PB], [dstb])
            S.add("dve", lambda e, dst=dst, tmp=tmp, ct=ct: e.scalar_tensor_tensor(
                out=dst[:, 0:1], in0=prevcol.t[0][:, ct:ct + 1], scalar=pc("mu", ct), in1=tmp[:, 0:1],
                op0=ALU.mult, op1=ALU.add), [prevcol.b[0], tmpb, PB], [dstb])
            S.add("act", lambda e, pa=pa, ct=ct: e.activation(
                out=prevcol.t[0][:, ct:ct + 1], in_=pa[:, SBT - 1:SBT], func=AF.Copy), pab, [prevcol.b[0]])

        sbst = {}

        def gen_first(sbi):
            own = sbi >= own0
            hTt, hTb = hT(sbi)
            for j in range(CPS):
                gc = sbi * CPS + j
                xtt, xtb = xt(gc)
                hbt, hbb = hb(gc)
                stt, stb = st0(gc)
                dma("sp", xtt[:], xw[gc * C:(gc + 1) * C, :], [], [xtb])
                rms_rstd(xtt[:], [xtb], stt, stb, hbt[:], hbb)
                S.add("dve", lambda e, xtt=xtt, stt=stt, hbt=hbt: e.scalar_tensor_tensor(
                    out=hbt[:], in0=xtt[:], scalar=stt[:, 2:3], in1=gpre.t[0][:], op0=ALU.mult, op1=ALU.mult),
                    [xtb, stb, gpre.b[0]], [hbb])
                yield
                pst = psT[0]
                pstb = psT_b[0]
                for k in range(8):
                    S.add("pe", lambda e, k=k, hbt=hbt, pst=pst: e.transpose(
                        out=pst[:, k * 128:(k + 1) * 128], in_=hbt[:, k * 128:(k + 1) * 128], identity=ident_b),
                        [hbb, CB], pstb)
                S.add("act", lambda e, pst=pst, hTt=hTt, j=j: e.activation(
                    out=hTt[:, :, j * C:(j + 1) * C], in_=pst[:, :].rearrange("p (k t) -> p k t", k=8), func=AF.Copy),
                    pstb, [hTb])
                yield
            proj_tile(sbi, 12, hTt, hTb)
            tw_ctx = stage2_header()
            sbst[sbi] = (tw_ctx, hTt, hTb)
            yield
            for hp in (0, 1):
                for q in range(3):
                    proj_tile(sbi, q * 4 + hp, hTt, hTb)
                    yield

        def gen_first_b(sbi):
            own = sbi >= own0
            tw_ctx, hTt, hTb = sbst[sbi]
            for hp in (0, 1):
                yield from prep_hp(hp, sbi, own, **tw_ctx)

        def gen_first_ab(sbi):
            yield from gen_first(sbi)
            yield from gen_first_b(sbi)

        def gen_second(sbi):
            own = sbi >= own0
            tw_ctx, hTt, hTb = sbst[sbi]
            for hp in (2, 3):
                for q in range(3):
                    proj_tile(sbi, q * 4 + hp, hTt, hTb)
                    yield
                yield from prep_hp(hp, sbi, own, **tw_ctx)

        def drain(g):
            for _ in g:
                pass

        def mkfill(g, n=1, units=None, slots=None):
            st = [0]

            def fill():
                if units is None:
                    k = n
                else:
                    i = st[0]
                    st[0] += 1
                    k = ((i + 1) * units) // slots - (i * units) // slots
                for _ in range(k):
                    try:
                        next(g)
                    except StopIteration:
                        return
            return fill

        drain(gen_first_ab(0))
        for sbi in range(nsb):
            own = sbi >= own0
            halo_sb = (sbi == own0 - 1)
            hTt, hTb = hT(sbi)
            def gen_ownproj(sbi=sbi, own=own, halo_sb=halo_sb, hTt=hTt, hTb=hTb):
                if own or halo_sb:
                    ncols, col0 = (SBT, 0) if own else (C, SBT - C)
                    kcol = ((sbi - own0) * CPS + 1) * C if own else 0
                    for g in range(2):
                        wt, wb = ws_load([(lambda t: t[:, :, 0:64], wcols(O_K + g * 64, 64)), (lambda t: t[:, :, 64:128], wcols(O_K + g * 64, 64))])
                        pa, pab = proj_fm(hTt, hTb, wt, wb, ncols, col0)
                        for cc in range(ncols // C):
                            ks = ((kcol // C) + cc) % NKS
                            S.add("act", lambda e, pa=pa, g=g, ks=ks, cc=cc: e.activation(
                                out=KTatt.t[g][:, ks * C:(ks + 1) * C], in_=pa[:, cc * C:(cc + 1) * C], func=AF.Identity, bias=pc("bk", g)), pab + [PB], [KTatt.b[g]])
                        yield
                    wt, wb = ws_load([(lambda t: t[:, :, :], wcols(O_V))])
                    for c in (range(CPS) if own else [CPS - 1]):
                        lc1 = (sbi - own0) * CPS + c + 1 if own else 0
                        pa, pab = fullbank()
                        for k in range(8):
                            S.add("pe", lambda e, pa=pa, k=k, wt=wt, hTt=hTt, c=c: e.matmul(
                                pa[:, 0:128], lhsT=hTt[:, k, c * C:(c + 1) * C], rhs=wt[:, k, :], start=(k == 0), stop=(k == 7)), wb + [hTb], pab)
                        vp, vpb = Vpad(lc1)
                        S.add("dve", lambda e, vp=vp, pa=pa: e.tensor_tensor(out=vp[:, :, 0:64], in0=v3(pa[:, 0:128]), in1=v3(bvb.t[0][:, :]), op=ALU.add),
                              pab + [bvb.b[0]], [vpb])
                        S.add("pool", lambda e, vp=vp: e.tensor_copy(out=vp[:, :, 128:192], in_=vp[:, :, 0:64]), [vpb], [vpb])
                        yield
                if own:
                    for ct in range(4):
                        si = sbi * 4 + ct
                        wt, wb = ws_load([(lambda t: t[:, :, :], wcols(O_GR + ct * 128))])
                        pa, pab = proj_fm(hTt, hTb, wt, wb)
                        S.add("act", lambda e, pa=pa, si=si: e.activation(out=sgr(si)[0][:], in_=pa[:, 0:SBT], func=AF.Silu), pab, [sgr(si)[1]])
                        yield
                        wt, wb = ws_load([(lambda t: t[:, :, :], wcols(O_Q + ct * 128))])
                        pa, pab = proj_fm(hTt, hTb, wt, wb)
                        for hh in range(2):
                            qz, qzb = qTz(si * 2 + hh)
                            S.add("act", lambda e, pa=pa, qz=qz, ct=ct, hh=hh: e.activation(
                                out=qz[P_[hh], :], in_=pa[P_[hh], 0:SBT], func=AF.Identity, bias=ppt[P_[hh], PPI["bq"] + ct:PPI["bq"] + ct + 1]),
                                pab + [PB], [qzb])
                        yield
                        wt, wb = ws_load([(lambda t: t[:, :, :], wcols(O_GA + ct * 128))])
                        pa, pab = proj_fm(hTt, hTb, wt, wb)
                        S.add("act", lambda e, pa=pa, si=si: e.activation(out=sga(si)[0][:], in_=pa[:, 0:SBT], func=AF.Silu), pab, [sga(si)[1]])
                        yield

                yield

            def emit_chunk_pairs(c, pairs, fill, own=own, sbi=sbi):
                gch = sbi * CPS + c
                csl = slice(c * C, (c + 1) * C)
                pYbank, pYbb = psA[0], [psA_b[0]]
                def mkctx(hp):
                    si = sbi * 4 + hp
                    gi = gch * 4 + hp
                    x = dict(hp=hp, si=si, gi=gi)
                    x["AR"], x["ARb"] = AR(si)
                    x["BT"], x["BTb"] = BT(si)
                    x["KT"], x["KTb"] = KT(si)
                    x["vb"], x["vbb"] = vbf(si)
                    x["zts"] = [PZ(si * 2 + hh) for hh in range(2)]
                    x["tm"], x["tmb"] = tm(gi)
                    return x

                def g_transposes(x, c=c, csl=csl):
                    pt_ = psT[1][:, 0:512]
                    ptb = psT_b[1]
                    srcs = [(x["AR"][:, c, 0, :], x["ARb"]), (x["BT"][:, csl], x["BTb"]), (x["KT"][:, csl], x["KTb"]), (x["vb"][:, csl], x["vbb"])]
                    for q, (sap, sbf) in enumerate(srcs):
                        S.add("pe", lambda e, pt_=pt_, q=q, sap=sap: e.transpose(out=pt_[:, q * 128:(q + 1) * 128], in_=sap, identity=ident_b),
                              [sbf, CB], ptb)
                    tmt = x["tm"]
                    S.add("act", lambda e, tmt=tmt, pt_=pt_: e.activation(out=tmt[:], in_=pt_.rearrange("p (q t) -> p q t", q=4), func=AF.Copy),
                          ptb, [x["tmb"]])

                def g_sprod_pe(x, c=c, csl=csl):
                    x["pS1"], x["pS1b"] = single(4)
                    x["pS2"], x["pS2b"] = single(2)
                    ARt, BTt = x["AR"], x["BT"]
                    for hh in range(2):
                        zt, zb = x["zts"][hh]
                        S.add("pe", lambda e, pS=x["pS1"], zt=zt, ARt=ARt, hh=hh: e.matmul(
                            pS[:, hh * 128:(hh + 1) * 128], lhsT=zt[:, 1, csl], rhs=ARt[:, c, 0, :], start=True, stop=True), [zb, x["ARb"]], x["pS1b"])
                    for hh in range(2):
                        zt, zb = x["zts"][hh]
                        S.add("pe", lambda e, pS=x["pS1"], zt=zt, BTt=BTt, hh=hh: e.matmul(
                            pS[:, (2 + hh) * 128:(3 + hh) * 128], lhsT=zt[:, 0, csl], rhs=BTt[:, csl], start=True, stop=True), [zb, x["BTb"]], x["pS1b"])
                    for hh in range(2):
                        zt, zb = x["zts"][hh]
                        S.add("pe", lambda e, pS=x["pS2"], zt=zt, ARt=ARt, hh=hh: e.matmul(
                            pS[:, hh * 128:(hh + 1) * 128], lhsT=zt[:, 2, csl], rhs=ARt[:, c, 0, :], start=True, stop=True), [zb, x["ARb"]], x["pS2b"])

                def g_sprod_evac(x):
                    gi = x["gi"]
                    nm, nmb = NMt(gi * 2)
                    mk, mkb = Mak(gi)
                    S.add("dve", lambda e, nm=nm, pS=x["pS1"]: e.tensor_tensor(out=nm[:, :, :, :].rearrange("p a h t -> p (a h) t"), in0=v3(pS, 4), in1=mask4, op=ALU.mult),
                          x["pS1b"] + [CB], [nmb])
                    S.add("dve", lambda e, mk=mk, pS=x["pS2"]: e.tensor_tensor(out=mk[:], in0=v3(pS, 2), in1=mask4[:, 0:2, :], op=ALU.mult),
                          x["pS2b"] + [CB], [mkb])
                    x["nm"], x["nmb"], x["mk"], x["mkb"] = nm, nmb, mk, mkb

                def g_r_pe(x, c=c, csl=csl):
                    x["pR"], x["pRb"] = single(4)
                    ARt = x["AR"]
                    for a_ in range(2):
                        for hh in range(2):
                            zt, zb = x["zts"][hh]
                            S.add("pe", lambda e, pR=x["pR"], zt=zt, ARt=ARt, hh=hh, a_=a_: e.matmul(
                                pR[:, (a_ * 2 + hh) * 128:(a_ * 2 + hh + 1) * 128], lhsT=zt[:, 1 + a_, csl], rhs=ARt[:, c, 1, :], start=True, stop=True),
                                [zb, x["ARb"]], x["pRb"])

                def g_r_evac(x, c=c, csl=csl):
                    rbk, rbkb = RBK(x["gi"])
                    for a_ in range(2):
                        S.add("dve", lambda e, rbk=rbk, pR=x["pR"], a_=a_: e.tensor_tensor(out=rbk[:, a_, :, :], in0=v3(pR[:, a_ * 256:(a_ + 1) * 256], 2), in1=mle2, op=ALU.mult),
                              x["pRb"] + [CB], [rbkb])
                    x["rbk"], x["rbkb"] = rbk, rbkb

                def g_pv_pe(x, c=c, csl=csl):
                    x["pV"], x["pVb"] = single(1)
                    mk, tmt = x["mk"], x["tm"]
                    for hh in range(2):
                        S.add("pe", lambda e, pV=x["pV"], hh=hh, mk=mk, tmt=tmt: e.matmul(
                            pV[:, hh * 64:(hh + 1) * 64], lhsT=mk[:, hh, :], rhs=tmt[:, 3, hh * 64:(hh + 1) * 64], start=True, stop=True), [x["mkb"], x["tmb"]], x["pVb"])

                def g_x0(x, c=c, csl=csl):
                    Xt, Xb = Xtile(x["gi"] * 2)
                    tmt = x["tm"]
                    S.add("pool", lambda e, Xt=Xt, tmt=tmt: e.tensor_copy(out=Xt[:, :, 0, :], in_=tmt[:, 0, :].rearrange("p (h k) -> p h k", h=2)), [x["tmb"]], [Xb])
                    S.add("act", lambda e, Xt=Xt, pV=x["pV"]: e.activation(out=Xt[:, :, 1, :], in_=pV[:, 0:128].rearrange("p (h k) -> p h k", h=2), func=AF.Copy),
                          x["pVb"], [Xb])
                    x["X"], x["Xb"] = Xt, Xb

                def g_level_pe(x, lv):
                    nm, nmb, Xt, Xb = x["nm"], x["nmb"], x["X"], x["Xb"]
                    x["pX"], x["pXb"] = single(2)
                    for hh in range(2):
                        S.add("pe", lambda e, pX=x["pX"], hh=hh, nm=nm, Xt=Xt: e.matmul(
                            pX[:, hh * 128:(hh + 1) * 128], lhsT=nm[:, 0, hh, :], rhs=Xt[:, hh, :, :].rearrange("p a k -> p (a k)"), start=True, stop=True),
                            [nmb, Xb], x["pXb"])
                    if lv < 6:
                        x["pNM"], x["pNMb"] = single(4)
                        for hh in range(2):
                            S.add("pe", lambda e, pN=x["pNM"], hh=hh, nm=nm: e.matmul(
                                pN[:, hh * 128:(hh + 1) * 128], lhsT=nm[:, 1, hh, :], rhs=nm[:, 0, hh, :], start=True, stop=True), [nmb], x["pNMb"])
                        if lv < 5:
                            for hh in range(2):
                                S.add("pe", lambda e, pN=x["pNM"], hh=hh, nm=nm: e.matmul(
                                    pN[:, (2 + hh) * 128:(3 + hh) * 128], lhsT=nm[:, 0, hh, :], rhs=nm[:, 1, hh, :], start=True, stop=True), [nmb], x["pNMb"])

                def g_level_evac(x, lv):
                    gi = x["gi"]
                    Xt, Xb = x["X"], x["Xb"]
                    Xn, Xnb = Xtile(gi * 2 + lv + 1)
                    S.add("dve", lambda e, Xn=Xn, pX=x["pX"], Xt=Xt: e.tensor_tensor(
                        out=Xn[:, :, :, :].rearrange("p h a k -> p h (a k)"), in0=v3(pX), in1=Xt[:, :, :, :].rearrange("p h a k -> p h (a k)"), op=ALU.add),
                        x["pXb"] + [Xb], [Xnb])
                    x["X"], x["Xb"] = Xn, Xnb
                    if lv < 6:
                        nn, nnb = NMt(gi * 2 + lv + 1)
                        w = 4 if lv < 5 else 2
                        S.add("act", lambda e, nn=nn, pN=x["pNM"], w=w: e.activation(
                            out=nn[:, :, :, :].rearrange("p a h t -> p (a h) t")[:, 0:w, :], in_=v3(pN[:, 0:w * 128], w), func=AF.Copy), x["pNMb"], [nnb])
                        x["nm"], x["nmb"] = nn, nnb

                def g_state(x, c=c, csl=csl, own=own, pYbank=(pYbank if own else None), pYbb=(pYbb if own else None)):
                    gi, hp, si = x["gi"], x["hp"], x["si"]
                    Xt, Xb, tmt, tmb = x["X"], x["Xb"], x["tm"], x["tmb"]
                    ARt, ARb = x["AR"], x["ARb"]
                    wz, wzb = Wz(gi)
                    S.add("pool", lambda e, wz=wz, Xt=Xt: e.tensor_copy(out=wz[:, 0::2, :], in_=Xt[:, :, 0, :]), [Xb], [wzb])
                    wzA = wz[:, 0:2, :].rearrange("p a k -> p (a k)")
                    wzB = wz[:, 1:3, :].rearrange("p a k -> p (a k)")
                    wp, wpb = Wp(gi)
                    S.add("pool", lambda e, wp=wp, Xt=Xt: e.tensor_copy(out=wp[:, :, :], in_=Xt[:, :, 0, :]), [Xb], [wpb])
                    pAT, pATb = single(1)
                    S.add("pe", lambda e, pAT=pAT, wp=wp, tmt=tmt: e.matmul(pAT[:, 0:128], lhsT=wp[:, :, :].rearrange("p a k -> p (a k)"), rhs=tmt[:, 1, :], start=True, stop=True),
                          [wpb, tmb], pATb)
                    atb, atbb = ATbd(gi)
                    for hh in range(2):
                        S.add("dve", lambda e, atb=atb, pAT=pAT, hh=hh: e.tensor_copy(
                            out=atb[P_[hh], hh * 64:(hh + 1) * 64], in_=pAT[P_[hh], hh * 64:(hh + 1) * 64]), pATb, [atbb])
                    pG, pGb = single(1)
                    pG2, pG2b = single(1)
                    S.add("pe", lambda e, pG=pG, tmt=tmt: e.matmul(pG[:, 0:128], lhsT=tmt[:, 2, :], rhs=tmt[:, 3, :], start=True, stop=True),
                          [tmb], pGb)
                    for hh in range(2):
                        S.add("pe", lambda e, pG2=pG2, Xt=Xt, tmt=tmt, hh=hh: e.matmul(
                            pG2[:, hh * 64:(hh + 1) * 64], lhsT=tmt[:, 1, :], rhs=Xt[:, hh, 1, :], start=True, stop=True), [Xb, tmb], pG2b)
                    gs, gsb_ = Gsb(gi)
                    for hh in range(2):
                        S.add("act", lambda e, gs=gs, pG=pG, hh=hh: e.activation(
                            out=gs[P_[hh], :], in_=pG[P_[hh], hh * 64:(hh + 1) * 64], func=AF.Copy), pGb, [gsb_])
                        S.add("dve", lambda e, gs=gs, pG2=pG2, hh=hh: e.tensor_tensor(
                            out=gs[P_[hh], :], in0=pG2[P_[hh], hh * 64:(hh + 1) * 64], in1=gs[P_[hh], :], op=ALU.add), pG2b + [gsb_], [gsb_])
                    Htt, Hb_ = Ht(hp)
                    gct, gcb = gC(si)
                    if own:
                        hbz = [Hbfz(gi * 2 + hh) for hh in range(2)]
                        for hh in range(2):
                            S.add("pool", lambda e, hz=hbz[hh][0], Htt=Htt, hh=hh: e.tensor_copy(out=hz[P_[hh], :], in_=Htt[P_[hh], :]), [Hb_], [hbz[hh][1]])
                    hhl, hhlb = Hhl(gi)
                    S.add("pool", lambda e, hhl=hhl, Htt=Htt: e.tensor_copy(out=hhl[:, 0, :], in_=Htt[:]), [Hb_], [hhlb])
                    S.add("pool", lambda e, hhl=hhl, Htt=Htt: e.tensor_tensor(out=hhl[:, 1, :], in0=Htt[:], in1=hhl[:, 0, :], op=ALU.subtract), [Hb_, hhlb], [hhlb])
                    pZ, pZb = single(1)
                    S.add("pe", lambda e, pZ=pZ, atb=atb, hhl=hhl: e.matmul(pZ[:, 0:128], lhsT=atb[:], rhs=hhl[:, :, :].rearrange("p a v -> p (a v)"), start=True, stop=True),
                          [atbb, hhlb], pZb)
                    s1, s1b = s1t(gi)
                    S.add("pool", lambda e, s1=s1, Htt=Htt, gs=gs: e.tensor_tensor(out=s1[:], in0=Htt[:], in1=gs[:], op=ALU.add), [Hb_, gsb_], [s1b])
                    S.add("pool", lambda e, s1=s1, gct=gct: e.tensor_scalar(out=s1[:], in0=s1[:], scalar1=gct[:, c:c + 1], scalar2=1.0, op0=ALU.mult, op1=ALU.mult),
                          [s1b, gcb], [s1b])
                    S.add("dve", lambda e, pZ=pZ, gct=gct, s1=s1: e.scalar_tensor_tensor(
                        out=s1[:], in0=pZ[:, 0:64], scalar=gct[:, c:c + 1], in1=s1[:], op0=ALU.mult, op1=ALU.add), pZb + [gcb, s1b], [s1b])
                    S.add("dve", lambda e, Htt=Htt, pZ=pZ, gct=gct, s1=s1: e.scalar_tensor_tensor(
                        out=Htt[:], in0=pZ[:, 64:128], scalar=gct[:, c:c + 1], in1=s1[:], op0=ALU.mult, op1=ALU.add), pZb + [gcb, s1b], [Hb_])
                    if own:
                        rbk, rbkb = x["rbk"], x["rbkb"]
                        qt_, qtb = QT(gi)
                        for hh, wzX in enumerate((wzA, wzB)):
                            pQ, pQb = single(1)
                            S.add("pe", lambda e, pQ=pQ, wzX=wzX, rbk=rbk, hh=hh: e.matmul(pQ[:, 0:128], lhsT=wzX, rhs=rbk[:, 0, hh, :], start=True, stop=True),
                                  [wzb, rbkb], pQb)
                            S.add("dve", lambda e, qt_=qt_, pQ=pQ, ARt=ARt, hh=hh: e.tensor_tensor(
                                out=qt_[P_[hh], :], in0=pQ[P_[hh], 0:128], in1=ARt[P_[hh], c, 1, :], op=ALU.add), pQb + [ARb], [qtb])
                        for hh in range(2):
                            pY, pYb = pYbank, pYbb
                            ysl = slice((hp * 2 + hh) * 64, (hp * 2 + hh + 1) * 64)
                            S.add("pe", lambda e, pY=pY, ysl=ysl, hh=hh, rbk=rbk, Xt=Xt: e.matmul(
                                pY[:, ysl], lhsT=rbk[:, 0, hh, :], rhs=Xt[:, hh, 1, :], start=True, stop=False), [rbkb, Xb], pYb)
                            S.add("pe", lambda e, pY=pY, ysl=ysl, hh=hh, rbk=rbk, tmt=tmt: e.matmul(
                                pY[:, ysl], lhsT=rbk[:, 1, hh, :], rhs=tmt[:, 3, hh * 64:(hh + 1) * 64], start=False, stop=False), [rbkb, tmb], pYb)
                            S.add("pe", lambda e, pY=pY, ysl=ysl, qt_=qt_, hz=hbz[hh][0]: e.matmul(
                                pY[:, ysl], lhsT=qt_[:, :], rhs=hz[:, :], start=False, stop=True), [qtb, hbz[hh][1]], pYb)

                for pr in pairs:
                    ctxs = [mkctx(hp) for hp in pr]
                    for x in ctxs:
                        g_transposes(x)
                    for x in ctxs:
                        g_sprod_pe(x)
                    for x in ctxs:
                        g_sprod_evac(x)
                    fill()
                    if own:
                        for x in ctxs:
                            g_r_pe(x)
                        for x in ctxs:
                            g_r_evac(x)
                    for x in ctxs:
                        g_pv_pe(x)
                    for x in ctxs:
                        g_x0(x)
                    fill()
                    for lv in range(7):
                        for x in ctxs:
                            g_level_pe(x, lv)
                        for x in ctxs:
                            g_level_evac(x, lv)
                        fill()
                    for x in ctxs:
                        g_state(x)
            if not own:
                if halo_sb:
                    drain(gen_ownproj())
                g2 = gen_second(sbi)
                f2 = mkfill(g2, units=23, slots=17)
                for c in range(CPS):
                    emit_chunk_pairs(c, [PAIRS[0]], f2)
                drain(g2)
                g1 = gen_first_ab(sbi + 1) if sbi + 1 < nsb else iter(())
                f1 = mkfill(g1, units=28, slots=17)
                for c in range(CPS):
                    emit_chunk_pairs(c, [PAIRS[1]], f1)
                drain(g1)
                continue
            fb_mode[0] = 1
            g1 = iter(())
            for c in range(CPS):
                gch = sbi * CPS + c
                csl = slice(c * C, (c + 1) * C)
                pYbank, pYbb = psA[0], [psA_b[0]]
                if c == 0:
                    g2 = gen_second(sbi)
                    emit_chunk_pairs(c, [PAIRS[0]], mkfill(g2, 2))
                    drain(g2)
                    g3 = gen_ownproj()
                    emit_chunk_pairs(c, [PAIRS[1]], mkfill(g3, 2))
                    drain(g3)
                else:
                    if c == 1 and sbi + 1 < nsb:
                        g1 = gen_first(sbi + 1)
                    emit_chunk_pairs(c, [PAIRS[0]], mkfill(g1, 1))
                    emit_chunk_pairs(c, [PAIRS[1]], mkfill(g1, 1))
                lc = (sbi - own0) * CPS + c
                g_t, g_b = gst()
                pY, pYb = pYbank, pYbb
                yq, yqb = ysqt(0)
                S.add("dve", lambda e, g_t=g_t, pY=pY: e.tensor_reduce(out=g_t[:, 0, :], in_=v3(pY[:, 0:512], 8), axis=AX.X, op=ALU.add), pYb, [g_b])
                S.add("act", lambda e, yq=yq, pY=pY: e.activation(out=yq[:], in_=pY[:, 0:512], func=AF.Square), pYb, [yqb])
                S.add("dve", lambda e, g_t=g_t, yq=yq: e.tensor_reduce(out=g_t[:, 1, :], in_=v3(yq[:, :], 8), axis=AX.X, op=ALU.add), [yqb], [g_b])
                S.add("dve", lambda e, g_t=g_t: e.tensor_scalar(out=g_t[:, 2, :], in0=g_t[:, 0, :], scalar1=1.0 / 64, scalar2=None, op0=ALU.mult), [g_b], [g_b])
                S.add("dve", lambda e, g_t=g_t: e.tensor_tensor(out=g_t[:, 3, :], in0=g_t[:, 2, :], in1=g_t[:, 2, :], op=ALU.mult), [g_b], [g_b])
                S.add("dve", lambda e, g_t=g_t: e.scalar_tensor_tensor(out=g_t[:, 4, :], in0=g_t[:, 1, :], scalar=1.0 / 64, in1=g_t[:, 3, :],
                                                                       op0=ALU.mult, op1=ALU.subtract), [g_b], [g_b])
                S.add("act", lambda e, g_t=g_t: e.activation(out=g_t[:, 5, :], in_=g_t[:, 4, :], func=AF.Ln, bias=gneps_col), [g_b, KB], [g_b])
                S.add("act", lambda e, g_t=g_t: e.activation(out=g_t[:, 5, :], in_=g_t[:, 5, :], func=AF.Exp, scale=-0.5), [g_b], [g_b])
                ynt, ynb = yn()
                S.add("dve", lambda e, yq=yq, pY=pY, g_t=g_t: e.tensor_tensor(
                    out=v3(yq[:, :], 8), in0=v3(pY[:, 0:512], 8), in1=bcl(g_t[:, 2, :], 64), op=ALU.subtract), pYb + [g_b, yqb], [yqb])
                S.add("pool", lambda e, ynt=ynt, yq=yq, g_t=g_t: e.tensor_tensor(
                    out=v3(ynt[:, :], 8), in0=v3(yq[:, :], 8), in1=bcl(g_t[:, 5, :], 64), op=ALU.mult), [yqb, g_b], [ynb])
                pt_ = psT[1][:, 0:512]
                ptb = psT_b[1]
                for hp in range(4):
                    S.add("pe", lambda e, pt_=pt_, hp=hp, ynt=ynt: e.transpose(out=pt_[:, hp * 128:(hp + 1) * 128], in_=ynt[:, hp * 128:(hp + 1) * 128], identity=ident_b),
                          [ynb, CB], ptb)
                for hp in range(4):
                    si = sbi * 4 + hp
                    t1, t1b = t1t(hp)
                    S.add("dve", lambda e, t1=t1, pt_=pt_, hp=hp: e.tensor_scalar(out=t1[:], in0=pt_[:, hp * 128:(hp + 1) * 128], scalar1=pc("gnw", hp), scalar2=pc("gnb", hp),
                                                                            op0=ALU.mult, op1=ALU.add), ptb + [PB], [t1b])
                    S.add("pool", lambda e, t1=t1, si=si, csl=csl: e.tensor_tensor(out=t1[:], in0=t1[:], in1=bonus(si)[0][:, csl], op=ALU.add), [t1b, bonus(si)[1]], [t1b])
                    S.add("pool", lambda e, t1=t1, si=si, csl=csl: e.tensor_tensor(out=zr(si)[0][:, csl], in0=t1[:], in1=sgr(si)[0][:, csl], op=ALU.mult),
                          [t1b, sgr(si)[1]], [zr(si)[1]])
                if lc == NOC - 1:
                    dump("zr0", zr(sbi * 4)[0][:], [128, SBT], [zr(sbi * 4)[1]], BF16)
                if upto < 4:
                    continue
                am_i = 0 if lc == 0 else 1
                pObank, pObb = psA[0], [psA_b[0]]
                vprev, vprevb = Vpad(lc)
                vcur, vcurb = Vpad(lc + 1)
                for qp0 in (0, 2):
                    pts = psT[0]
                    hs = []
                    for qp in (qp0, qp0 + 1):
                        si = sbi * 4 + qp
                        g = qp // 2
                        for hh in range(2):
                            hd = qp * 2 + hh
                            pS, pSb_ = single(2)
                            qz, qzb = qTz(si * 2 + hh)
                            for kk_ in range(2):
                                ks = (lc + kk_) % NKS
                                S.add("pe", lambda e, pS=pS, qz=qz, g=g, ks=ks, kk_=kk_, csl=csl: e.matmul(
                                    pS[:, kk_ * 128:(kk_ + 1) * 128], lhsT=qz[:, csl], rhs=KTatt.t[g][:, ks * C:(ks + 1) * C], start=True, stop=True),
                                    [qzb, KTatt.b[g]], pSb_)
                            j4 = (qp - qp0) * 2 + hh
                            hs.append(dict(hd=hd, qp=qp, hh=hh, g=g, si=si, pS=pS, pSb=pSb_, sm=smt(j4), a=ast(j4), p3=p32(j4), pn=pnt(j4), pt=ptt(j4),
                                           ptsl=pts[:, j4 * 256:(j4 + 1) * 256]))
                    for h in hs:
                        S.add("dve", lambda e, sm=h["sm"][0], pS=h["pS"], am_i=am_i: e.scalar_tensor_tensor(
                            out=sm[:], in0=pS[:, 0:256], scalar=0.125, in1=amask.t[0][:, am_i, :], op0=ALU.mult, op1=ALU.add), h["pSb"] + [amask.b[0]], [h["sm"][1]])
                    for h in hs:
                        S.add("dve", lambda e, a_t=h["a"][0], sm=h["sm"][0]: e.tensor_reduce(out=a_t[:, 0:1], in_=sm[:], axis=AX.X, op=ALU.max), [h["sm"][1]], [h["a"][1]])
                    for h in hs:
                        S.add("dve", lambda e, a_t=h["a"][0], hd=h["hd"]: e.tensor_scalar(out=a_t[:, 1:2], in0=a_t[:, 0:1], scalar1=pc("sink", hd), scalar2=-1.0, op0=ALU.max, op1=ALU.mult),
                              [h["a"][1], PB], [h["a"][1]])
                    for h in hs:
                        S.add("act", lambda e, pp3=h["p3"][0], sm=h["sm"][0], a_t=h["a"][0]: e.activation(out=pp3[:], in_=sm[:], func=AF.Exp, bias=a_t[:, 1:2], accum_out=a_t[:, 2:3]),
                              [h["sm"][1], h["a"][1]], [h["p3"][1], h["a"][1]])
                    for h in hs:
                        S.add("act", lambda e, a_t=h["a"][0], hd=h["hd"]: e.activation(out=a_t[:, 3:4], in_=pc("sink", hd), func=AF.Exp, bias=a_t[:, 1:2]), [h["a"][1], PB], [h["a"][1]])
                    for h in hs:
                        S.add("dve", lambda e, a_t=h["a"][0]: e.tensor_tensor(out=a_t[:, 4:5], in0=a_t[:, 2:3], in1=a_t[:, 3:4], op=ALU.add), [h["a"][1]], [h["a"][1]])
                    for h in hs:
                        S.add("dve", lambda e, a_t=h["a"][0]: e.reciprocal(out=a_t[:, 5:6], in_=a_t[:, 4:5]), [h["a"][1]], [h["a"][1]])
                    for h in hs:
                        S.add("dve", lambda e, pn=h["pn"][0], pp3=h["p3"][0], a_t=h["a"][0]: e.tensor_scalar(out=pn[:], in0=pp3[:], scalar1=a_t[:, 5:6], scalar2=None, op0=ALU.mult),
                              [h["p3"][1], h["a"][1]], [h["pn"][1]])
                    for h in hs:
                        for kk_ in range(2):
                            S.add("pe", lambda e, ptsl=h["ptsl"], kk_=kk_, pn=h["pn"][0]: e.transpose(out=ptsl[:, kk_ * 128:(kk_ + 1) * 128], in_=pn[:, kk_ * 128:(kk_ + 1) * 128], identity=ident_b),
                                  [h["pn"][1], CB], psT_b[0])
                    for h in hs:
                        S.add("act", lambda e, pt2=h["pt"][0], ptsl=h["ptsl"]: e.activation(out=pt2[:], in_=v3(ptsl), func=AF.Copy), psT_b[0], [h["pt"][1]])
                    for qp in (qp0, qp0 + 1):
                        si = sbi * 4 + qp
                        g = qp // 2
                        pO, pOb = pObank[:, qp * 128:(qp + 1) * 128], pObb
                        n_ = 0
                        for h in [h for h in hs if h["qp"] == qp]:
                            pt2, pt2b = h["pt"]
                            hh = h["hh"]
                            for kk_, (vp, vpb) in enumerate([(vprev, vprevb), (vcur, vcurb)]):
                                S.add("pe", lambda e, pO=pO, vp=vp, g=g, hh=hh, pt2=pt2, kk_=kk_, n_=n_: e.matmul(
                                    pO[:, 0:128], lhsT=vp[:, g, hh * 64:hh * 64 + 128], rhs=pt2[:, kk_, :], start=(n_ == 0), stop=(n_ == 3)),
                                    [vpb, pt2b], pOb)
                                n_ += 1
                    for qp in (qp0, qp0 + 1):
                        si = sbi * 4 + qp
                        pO, pOb = pObank[:, qp * 128:(qp + 1) * 128], pObb
                        S.add("dve", lambda e, si=si, pO=pO, csl=csl: e.tensor_tensor(out=zatt(si)[0][:, csl], in0=pO[:, 0:128], in1=sga(si)[0][:, csl], op=ALU.mult),
                              pOb + [sga(si)[1]], [zatt(si)[1]])
                if lc == NOC - 1:
                    dump("za0", zatt(sbi * 4)[0][:], [128, SBT], [zatt(sbi * 4)[1]], BF16)
            if not own or upto < 5:
                continue
            drain(g1)
            fb_mode[0] = 0
            gfb = gen_first_b(sbi + 1) if sbi + 1 < nsb else iter(())
            ffb = mkfill(gfb, 0)
            mTt, mTb = mT()
            def load_j(j):
                return [ws_load([(lambda t: t[:, :, :], wbb[:, j * 128:(j + 1) * 128].rearrange("(bh p) c -> p bh c", p=128))]),
                        ws_load([(lambda t: t[:, :, :], wcols(O_GT + j * 128))]),
                        ws_load([(lambda t: t[:, :, :], wcols(O_GT + 1024 + j * 128))])]
            for j in range(8):
                cur_w = load_j(j)
                wt, wb = cur_w[0]
                pBr, pBrb = fullbank()
                pBa, pBab = fullbank()
                for hp in range(4):
                    si = sbi * 4 + hp
                    S.add("pe", lambda e, pBr=pBr, wt=wt, hp=hp, si=si: e.matmul(pBr[:, 0:SBT], lhsT=wt[:, hp, :], rhs=zr(si)[0][:], start=(hp == 0), stop=(hp == 3)),
                          wb + [zr(si)[1]], pBrb)
                for hp in range(4):
                    si = sbi * 4 + hp
                    S.add("pe", lambda e, pBa=pBa, wt=wt, hp=hp, si=si: e.matmul(pBa[:, 0:SBT], lhsT=wt[:, 4 + hp, :], rhs=zatt(si)[0][:], start=(hp == 0), stop=(hp == 3)),
                          wb + [zatt(si)[1]], pBab)
                halves = []
                for br in range(2):
                    wt2, wb2 = cur_w[1 + br]
                    pGt, pGtb = bankx()
                    for k in range(8):
                        S.add("pe", lambda e, pGt=pGt, k=k, wt2=wt2, hTt=hTt: e.matmul(
                            pGt[:, 0:SBT], lhsT=wt2[:, k, :], rhs=hTt[:, k, :], start=(k == 0), stop=(k == 7)), wb2 + [hTb], pGtb)
                    sg_, sgb_ = sgt(br)
                    S.add("act", lambda e, sg_=sg_, pGt=pGt: e.activation(out=sg_[:], in_=pGt[:, 0:SBT], func=AF.Sigmoid), pGtb, [sgb_])
                    halves.append((sg_, sgb_))
                m1, m1b = m12(0)
                m2, m2b = m12(1)
                S.add("dve", lambda e, m1=m1, pBr=pBr, sg_=halves[0][0]: e.tensor_tensor(out=m1[:], in0=pBr[:, 0:SBT], in1=sg_[:], op=ALU.mult), pBrb + [halves[0][1]], [m1b])
                S.add("dve", lambda e, m2=m2, pBa=pBa, sg_=halves[1][0]: e.tensor_tensor(out=m2[:], in0=pBa[:, 0:SBT], in1=sg_[:], op=ALU.mult), pBab + [halves[1][1]], [m2b])
                S.add("dve", lambda e, mTt=mTt, j=j, m1=m1, m2=m2: e.tensor_tensor(out=mTt[:, j, :], in0=m1[:], in1=m2[:], op=ALU.add), [m1b, m2b], [mTb])
                ffb()
            for c in range(CPS):
                gch = sbi * CPS + c
                lc = (sbi - own0) * CPS + c
                xr, xrb = xt(gch)
                dma("sp", xr[:], xw[gch * C:(gch + 1) * C, :], [], [xrb])
                for n in range(2):
                    pa, pab = fullbank()
                    for j in range(8):
                        S.add("pe", lambda e, pa=pa, j=j, n=n, mTt=mTt, c=c: e.matmul(
                            pa[:, 0:512], lhsT=mTt[:, j, c * C:(c + 1) * C], rhs=Wout.t[0][:, j, n * 512:(n + 1) * 512], start=(j == 0), stop=(j == 7)),
                            [mTb, Wout_b[j]], pab)
                    S.add("dve", lambda e, xr=xr, pa=pa, n=n: e.tensor_tensor(out=xr[:, n * 512:(n + 1) * 512], in0=pa[:, 0:512], in1=xr[:, n * 512:(n + 1) * 512], op=ALU.add),
                          pab + [xrb], [xrb])
                ft_, fb_ = fst(gch)
                hbt, hbb = hb(gch)
                rms_rstd(xr[:], [xrb], ft_, fb_, hbt[:], hbb)
                S.add("dve", lambda e, xr=xr, ft_=ft_: e.scalar_tensor_tensor(
                    out=xr[:], in0=xr[:], scalar=ft_[:, 2:3], in1=gfin.t[0][:], op0=ALU.mult, op1=ALU.mult), [xrb, fb_, gfin.b[0]], [xrb])
                dma("sp", out_d[lc * C:(lc + 1) * C, :], xr[:], [xrb], [], is_out=True)
            drain(gfb)

        for hp in range(4):
            dump(f"H{hp}", Ht(hp)[0][:], [128, 64], [Ht(hp)[1]])

        semnames = list(Sched.ENG) + [("dma", j) for j in range(Sched.NDMA)]
        sems = {}
        for sk in semnames:
            nm = sk if isinstance(sk, str) else f"dma{sk[1]}"
            sems[sk] = es.enter_context(nc.semaphore("s_" + nm))
        nc._sbuf_left = nc.sbuf_bytes_remaining
        block = es.enter_context(nc.Block())
        S.emit(nc, block, sems)
    nc._dbg_dumps = dump_d
    nc._sched_counts = dict(S.cnt)
    nc._sched_total = S.total
    return nc


def host_consts():
    s = np.arange(128)[:, None]
    t = np.arange(128)[None, :]
    cst = np.zeros((128, 9, 128), np.float32)
    cst[:, 0] = (s == t)
    cst[:, 1] = (s < t)
    cst[:, 2] = (s < t)
    cst[:, 3] = (s > t)
    cst[:, 4] = (s > t)
    cst[:, 5] = (s <= t)
    cst[:, 6] = (s <= t)
    cst[:, 7] = ((s // 64) == (t // 64))
    cst[:, 8] = 1.0
    return cst


def attn_masks(first):
    qi = np.arange(128)[:, None]
    kj = np.arange(256)[None, :]
    dist = qi + 128 - kj
    band = (dist >= 0) & (dist < 128)
    rest = np.where(band, 0.0, -1e30).astype(np.float32)
    fm = np.where(band & (kj >= 128), 0.0, -1e30).astype(np.float32)
    am = np.stack([fm if first else rest, rest], axis=1)
    return np.ascontiguousarray(am)


def pack_params(p):
    pp = np.zeros((128, NPP_IN), np.float32)

    def put(name, vec, n):
        v = np.asarray(vec, np.float32).reshape(n, 128)
        pp[:, PPI[name]:PPI[name] + n] = v.T
    put("mu", p["mu_shift"][0], 13)
    put("w0", p["w0"][0], 4)
    put("a0", p["a0"][0], 4)
    put("kk", p["k_k"][0], 4)
    put("ka", p["k_a"][0], 4)
    put("rk", p["r_k"][0], 4)
    put("gnw", p["gn_w"][0], 4)
    put("gnb", p["gn_b"][0], 4)
    bq = np.asarray(p["b_qkv"][0], np.float32)
    put("bq", bq[0:512], 4)
    bk = bq[512:640]
    pp[:, PPI["bk"] + 0] = np.concatenate([bk[0:64], bk[0:64]])
    pp[:, PPI["bk"] + 1] = np.concatenate([bk[64:128], bk[64:128]])
    sk = np.asarray(p["sinks"][0], np.float32)
    pp[:, PPI["sink"]:PPI["sink"] + 8] = np.broadcast_to(sk[None, :], (128, 8))
    wdi = np.zeros((128, 2, 512), np.float32)
    wdi[0:64, 0] = np.asarray(p["w_decay_up"][0], np.float32)
    wdi[64:128, 1] = np.asarray(p["w_iclr_up"][0], np.float32)
    common = {
        "w_in": np.ascontiguousarray(np.asarray(p["w_in"][0], np.float32)),
        "w_br": np.ascontiguousarray(np.stack([np.asarray(p["w_branch_rwkv"][0], np.float32),
                                               np.asarray(p["w_branch_att"][0], np.float32)])),
        "w_out": np.ascontiguousarray(np.asarray(p["w_out"][0], np.float32)),
        "wdi": np.ascontiguousarray(wdi),
        "pp": pp,
        "gpre_b": np.ascontiguousarray(np.broadcast_to(np.asarray(p["g_pre"][0], np.float32)[None], (128, D))),
        "gfin_b": np.ascontiguousarray(np.broadcast_to(np.asarray(p["g_final"], np.float32)[None], (128, D))),
        "bv_b": np.ascontiguousarray(np.broadcast_to(bq[640:768][None], (128, 128))),
        "cst": host_consts(),
    }
    return common


def kernel(**inputs):
    x = np.asarray(inputs["x"], np.float32)
    common = pack_params(inputs)
    nc = build()
    in_maps = []
    for c in range(NCORES):
        b, q = c // 4, c % 4
        end = (q + 1) * OWN_TOK
        xw = np.zeros((SEQ, D), np.float32)
        xw[SEQ - end:] = x[b, :end]
        m = dict(common)
        m["xw"] = xw
        m["amask"] = attn_masks(q == 0)
        in_maps.append(m)
    res = run_bass_kernel_spmd(nc, in_maps, core_ids=list(range(NCORES)))
    out = np.zeros((2, SEQ, D), np.float32)
    for c in range(NCORES):
        b, q = c // 4, c % 4
        out[b, q * OWN_TOK:(q + 1) * OWN_TOK] = res.results[c]["out"]
    return out
```

```python
import numpy as np
import concourse.bass as bass
import concourse.mybir as mybir
from concourse.bass_utils import run_bass_kernel_spmd

F32 = mybir.dt.float32
BF16 = mybir.dt.bfloat16
AF = mybir.ActivationFunctionType
ALU = mybir.AluOpType
AX = mybir.AxisListType

D = 1024
NCORES = 8
SEQ = 8192
OWN_TOK = 2048
C = 128
SBT = 256
CPS = SBT // C
RMS_EPS = 1e-6
GN_EPS = 64e-5
IN_COLS = 5504
O_SH = 0
O_GR = 1664
O_Q = 2176
O_K = 2688
O_V = 2816
O_GA = 2944
O_GT = 3456

PPI = {}
_n = 0
for _name, _cnt in [("mu", 13), ("w0", 4), ("a0", 4), ("kk", 4), ("ka", 4), ("rk", 4),
                    ("gnw", 4), ("gnb", 4), ("bq", 4), ("bk", 2), ("sink", 8)]:
    PPI[_name] = _n
    _n += _cnt
NPP_IN = _n
for _name, _cnt in [("omu", 13), ("nw0", 4), ("omka", 4), ("na0", 4)]:
    PPI[_name] = _n
    _n += _cnt
NPP = _n


class Buf:
    __slots__ = ("name", "w", "r", "excl")

    def __init__(self, name, excl=False):
        self.name = name
        self.w = None
        self.r = []
        self.excl = excl


class Sched:
    ENG = ("pe", "act", "dve", "pool", "sp")
    NDMA = 24

    def __init__(self, same_sync=True):
        self.ops = {e: [] for e in self.ENG}
        self.cnt = {e: 0 for e in self.ENG}
        self.waited = {e: {} for e in self.ENG}
        self.same_sync = same_sync
        self.dma_val = [0] * self.NDMA
        self.dma_rr = 0
        self.dma_rr2 = 0
        self.out_tokens = []

    def add(self, eng, fn, reads=(), writes=(), dma=False, is_out=False):
        self.total = getattr(self, "total", 0) + 1
        if not dma and self.total > getattr(self, "cut", 10 ** 9):
            return None
        deps = {}

        def need(tk, hard):
            d = deps.get(tk[0])
            if d is None:
                deps[tk[0]] = [tk[1], tk[2], hard]
            else:
                d[0] = max(d[0], tk[1])
                d[2] = d[2] or hard
        for b in reads:
            if b.w is not None:
                need(b.w, True)
            if b.excl:
                for tk in b.r:
                    need(tk, False)
        for b in writes:
            if b.w is not None:
                need(b.w, True)
            for tk in b.r:
                need(tk, False)
        waits = []
        for semkey, (val, src, hard) in deps.items():
            if src == eng and not isinstance(semkey, tuple):
                if eng in ("pe", "sp"):
                    continue
                if not hard or not self.same_sync:
                    continue
            if self.waited[eng].get(semkey, 0) >= val:
                continue
            self.waited[eng][semkey] = val
            waits.append((semkey, val))
        if dma:
            half = self.NDMA // 2
            if eng == "sp":
                j = self.dma_rr
                self.dma_rr = (self.dma_rr + 1) % half
            else:
                j = half + self.dma_rr2
                self.dma_rr2 = (self.dma_rr2 + 1) % half
            semkey = ("dma", j)
            if self.dma_val[j] > 0 and self.waited[eng].get(semkey, 0) < self.dma_val[j]:
                self.waited[eng][semkey] = self.dma_val[j]
                waits.append((semkey, self.dma_val[j]))
            self.dma_val[j] += 16
            tok = (semkey, self.dma_val[j], eng)
            inc = 16
        else:
            self.cnt[eng] += 1
            tok = (eng, self.cnt[eng], eng)
            inc = 1
        for b in reads:
            b.r.append(tok)
        for b in writes:
            b.w = tok
            b.r = []
        if is_out:
            self.out_tokens.append(tok)
        self.ops[eng].append((waits, fn, tok[0], inc))
        return tok

    def emit(self, nc, block, sems):
        engmap = {"pe": block.tensor, "act": block.scalar, "dve": block.vector,
                  "pool": block.gpsimd, "sp": block.sync}
        for e in self.ENG:
            ops = self.ops[e]
            final = list(self.out_tokens) if e == "sp" else ()

            def body(eng, ops=ops, final=final):
                for waits, fn, semkey, inc in ops:
                    for sk, val in waits:
                        eng.wait_ge(sems[sk], val)
                    fn(eng).then_inc(sems[semkey], inc)
                for tk in final:
                    eng.wait_ge(sems[tk[0]], tk[1])
                if final != ():
                    for j in range(self.NDMA):
                        if self.dma_val[j] > 0:
                            eng.wait_ge(sems[("dma", j)], self.dma_val[j])
            engmap[e](body)


def build(nsb=SEQ // SBT, nown=OWN_TOK // SBT, upto=99, dumps=(), same_sync=True, cut=None):
    from contextlib import ExitStack
    nc = bass.Bass("TRN2", target_bir_lowering=False)
    WT = nsb * SBT
    OT = nown * SBT
    NOC = nown * CPS
    S = Sched(same_sync=same_sync)
    if cut is not None:
        S.cut = cut

    def din(name, shape, dt=F32):
        return nc.dram_tensor(name, list(shape), dt, kind="ExternalInput").ap()

    xw = din("xw", [WT, D])
    w_in = din("w_in", [D, IN_COLS])
    w_br = din("w_br", [2, 512, D])
    w_out = din("w_out", [D, D])
    wdi = din("wdi", [128, 2, 512])
    pp_in = din("pp", [128, NPP_IN])
    gpre_d = din("gpre_b", [128, D])
    gfin_d = din("gfin_b", [128, D])
    bv_d = din("bv_b", [128, 128])
    cst_d = din("cst", [128, 9, 128])
    am_d = din("amask", [128, 2, 256])
    out_d = nc.dram_tensor("out", [OT, D], F32, kind="ExternalOutput").ap()
    wib = nc.dram_tensor("wib_scratch", [D, IN_COLS - O_GR], BF16).ap()
    wbb = nc.dram_tensor("wbb_scratch", [2 * 512, D], BF16).ap()
    dump_d = {}

    es = ExitStack()
    with es:
        def sb(name, shape, dt=F32):
            return es.enter_context(nc.sbuf_tensor(name, list(shape), dt))

        def ps(name, shape, dt=F32):
            return es.enter_context(nc.psum_tensor(name, list(shape), dt))

        class T:
            def __init__(self, name, shape, dt=F32, n=1):
                self.t = [sb(f"{name}{i}", shape, dt) for i in range(n)]
                self.b = [Buf(f"{name}{i}") for i in range(n)]
                self.n = n

            def __call__(self, i=0):
                return self.t[i % self.n], self.b[i % self.n]

        def dma(eng, out, in_, reads, writes, is_out=False):
            return S.add(eng, lambda e: e.dma_start(out=out, in_=in_), reads, writes, dma=True, is_out=is_out)

        def dump(name, ap, shape, reads, dt=F32):
            if name not in dumps:
                return
            dd = nc.dram_tensor("dbg_" + name, list(shape), dt, kind="ExternalOutput").ap()
            dump_d[name] = dd
            dma("sp", dd, ap, reads, [], is_out=True)

        xt = T("xt", [128, D], F32, 2)
        cst_b = T("cst_b", [128, 8, 128], BF16)
        cst2 = T("cst2", [128, 1, 128])
        amask = T("amask", [128, 2, 256])
        PP = T("PP", [128, NPP])
        gpre = T("gpre", [128, D])
        gfin = T("gfin", [128, D])
        bvb = T("bvb", [128, 128])
        Wdb = T("Wdb", [128, 3, 512], BF16)
        Wsh = T("Wsh", [128, 8, 1664], BF16)
        Wout = T("Wout", [128, 8, D], BF16)

        stg = xt.t[1][:, :].rearrange("p (a b) -> p a b", a=8)
        dma("sp", stg, cst_d[:, 0:8, :], [], [xt.b[1]])
        dma("sp", cst2.t[0][:], cst_d[:, 8:9, :], [], [cst2.b[0]])
        dma("sp", PP.t[0][:, 0:NPP_IN], pp_in, [], [PP.b[0]])
        dma("sp", gpre.t[0][:], gpre_d, [], [gpre.b[0]])
        stg_w = xt.t[0][:, :].rearrange("p (a b) -> p a b", a=2)
        dma("sp", stg_w, wdi, [], [xt.b[0]])
        S.add("act", lambda e: e.activation(out=Wdb.t[0][:, 0, :], in_=stg_w[:, 0, :], func=AF.Copy), [xt.b[0]], [Wdb.b[0]])
        S.add("act", lambda e: e.activation(out=Wdb.t[0][:, 2, :], in_=stg_w[:, 1, :], func=AF.Copy), [xt.b[0]], [Wdb.b[0]])
        S.add("dve", lambda e: e.tensor_tensor(out=Wdb.t[0][:, 1, :], in0=stg_w[:, 0, :], in1=Wdb.t[0][:, 0, :], op=ALU.subtract), [xt.b[0], Wdb.b[0]], [Wdb.b[0]])
        Wsh_b = [Buf(f"Wsh_k{k}") for k in range(8)]
        for k in range(8):
            S.add("pool", lambda e, k=k: e.dma_start(out=Wsh.t[0][:, k, :], in_=w_in[k * 128:(k + 1) * 128, O_SH:O_SH + 1664]),
                  [], [Wsh_b[k]], dma=True)
        dma("sp", amask.t[0][:], am_d, [], [amask.b[0]])
        dma("sp", gfin.t[0][:], gfin_d, [], [gfin.b[0]])
        dma("sp", bvb.t[0][:], bv_d, [], [bvb.b[0]])
        S.add("dve", lambda e: e.tensor_copy(out=cst_b.t[0][:], in_=stg), [xt.b[1]], [cst_b.b[0]])
        ident_b = cst_b.t[0][:, 0, :]
        mask4 = cst_b.t[0][:, 1:5, :]
        mle2 = cst_b.t[0][:, 5:7, :]
        bones_b = cst_b.t[0][:, 7, :]
        ones_f = cst2.t[0][:, 0, :]
        CB = cst_b.b[0]
        CF = cst2.b[0]
        ppt = PP.t[0]
        PB = PP.b[0]

        def pc(name, i=0):
            j = PPI[name] + i
            return ppt[:, j:j + 1]

        S.add("dve", lambda e: e.tensor_scalar(out=ppt[:, PPI["omu"]:PPI["omu"] + 13], in0=ppt[:, PPI["mu"]:PPI["mu"] + 13],
                                               scalar1=-1.0, scalar2=1.0, op0=ALU.mult, op1=ALU.add), [PB], [PB])
        S.add("dve", lambda e: e.tensor_scalar(out=ppt[:, PPI["nw0"]:PPI["nw0"] + 4], in0=ppt[:, PPI["w0"]:PPI["w0"] + 4],
                                               scalar1=-1.0, scalar2=None, op0=ALU.mult), [PB], [PB])
        S.add("dve", lambda e: e.tensor_scalar(out=ppt[:, PPI["omka"]:PPI["omka"] + 4], in0=ppt[:, PPI["ka"]:PPI["ka"] + 4],
                                               scalar1=-1.0, scalar2=1.0, op0=ALU.mult, op1=ALU.add), [PB], [PB])
        S.add("dve", lambda e: e.tensor_scalar(out=ppt[:, PPI["na0"]:PPI["na0"] + 4], in0=ppt[:, PPI["a0"]:PPI["a0"] + 4],
                                               scalar1=-1.0, scalar2=None, op0=ALU.mult), [PB], [PB])

        psA = [ps(f"psA{i}", [128, 512]) for i in range(2)]
        psA_b = [Buf(f"psA{i}", True) for i in range(2)]
        psT = [ps(f"psT{i}", [128, 1024], BF16) for i in range(2)]
        psT_b = [[Buf(f"psT{i}_{h}", True) for h in range(2)] for i in range(2)]
        psLU = [[ps(f"psL{i}", [128, 512]), ps(f"psU{i}", [128, 512])] for i in range(2)]
        psLU_b = [[[Buf(f"psLU{i}_{lu}_{s}", True) for s in range(4)] for lu in range(2)] for i in range(2)]
        arr = [0]
        prr = [0]
        srr = [0]

        fb_mode = [0]

        def fullbank():
            if fb_mode[0]:
                return psA[1], [psA_b[1]]
            i = arr[0]
            arr[0] = (i + 1) % 2
            return psA[i], [psA_b[i]]

        def pair(ns):
            r = prr[0]
            if (r % 4) + ns > 4:
                r = (r // 4 + 1) * 4
            r %= 8
            p, s = r // 4, r % 4
            prr[0] = (r + ns) % 8
            sl = slice(s * 128, (s + ns) * 128)
            return (psLU[p][0][:, sl], psLU_b[p][0][s:s + ns], psLU[p][1][:, sl], psLU_b[p][1][s:s + ns])

        brr = [0]

        def bankx():
            i = brr[0]
            brr[0] = (i + 1) % 4
            p, lu = i // 2, i % 2
            return psLU[p][lu], list(psLU_b[p][lu])

        def single(ns):
            bk, bb = bankx()
            return bk[:, 0:ns * 128], bb

        hb = T("hb", [128, D], BF16, 1)
        hT = T("hT", [128, 8, SBT], BF16, 2)
        st0 = T("st0", [128, 4], F32, 2)
        shwa = T("shwa", [128, SBT])
        shtmp = T("shtmp", [128, SBT], F32, 1)
        shr = T("shr", [128, SBT], F32, 2)
        shk = T("shk", [128, SBT], F32, 2)
        shv = T("shv", [128, SBT], F32, 2)
        tw = T("tw", [128, SBT])
        tw_hi = T("tw_hi", [128, SBT], BF16)
        tw_lo = T("tw_lo", [128, SBT], BF16)
        t_k2b = T("t_k2b", [128, SBT], BF16)
        t_rkb = T("t_rkb", [128, SBT], BF16)
        Hhl = T("Hhl", [128, 2, 64], BF16, 4)
        t_e1 = T("t_e1", [128, SBT])
        t_ew = T("t_ew", [128, SBT])
        t_a = T("t_a", [128, SBT])
        t_cs = T("t_cs", [128, SBT])
        t_csp = T("t_csp", [128, SBT])
        t_en = T("t_en", [128, SBT])
        t_ep = T("t_ep", [128, SBT])
        t_k2 = T("t_k2", [128, SBT])
        t_kkn = T("t_kkn", [128, SBT])
        t_ab = T("t_ab", [128, SBT])
        t_f = T("t_f", [128, SBT])
        gC = T("gC", [128, CPS], F32, 8)
        AR = T("AR", [128, CPS, 2, C], BF16, 4)
        BT = T("BT", [128, SBT], BF16, 4)
        KT = T("KT", [128, SBT], BF16, 4)
        vbf = T("vbf", [128, SBT], BF16, 4)
        bonus = T("bonus", [128, SBT], BF16, 4)
        tm = T("tm", [128, 4, 128], BF16, 4)
        PZ = T("PZ", [128, 3, SBT], BF16, 8)
        Hbfz = T("Hbfz", [128, 64], BF16, 8)
        qTz = T("qTz", [128, SBT], BF16, 8)
        NG = 4
        NMt = T("NM", [128, 2, 2, 128], BF16, 2 * NG)
        Mak = T("Mak", [128, 2, 128], BF16, NG)
        RBK = T("RBK", [128, 2, 2, 128], BF16, NG)
        PAIRS = [(0, 1), (2, 3)]
        Xtile = T("Xt", [128, 2, 2, 64], BF16, 2 * NG)
        ATbd = T("ATbd", [128, 128], BF16, NG)
        Gsb = T("Gsb", [128, 64], F32, NG)
        Ht = T("Hst", [128, 64], F32, 4)
        s1t = T("s1t", [128, 64], F32, NG)
        Wz = T("Wz", [128, 3, 64], BF16, NG)
        Wp = T("Wp", [128, 2, 64], BF16, NG)
        zlo = T("zlo", [128, 64], F32, NG)
        QT = T("QT", [128, 128], BF16, NG)
        prevcol = T("prevcol", [128, 13])
        kc = T("kcols", [128, 8])
        NWS = 5
        ws = T("ws", [128, 8, 128], BF16, NWS)
        ws_b2 = [Buf(f"ws_b2_{i}") for i in range(NWS)]
        wsrr = [0]
        sgr = T("sgr", [128, SBT], BF16, 4)
        sga = T("sga", [128, SBT], BF16, 4)
        NKS = 4
        KTatt = T("KTatt", [128, NKS * 128], BF16, 2)
        NV = 4
        Vpad = T("Vpad", [128, 2, 192], BF16, NV)
        ysqt = T("ysq", [128, 512], F32, 1)
        yn = T("yn", [128, 512], BF16, 1)
        gst = T("gst", [128, 6, 8], F32, 1)
        t1t = T("t1t", [128, 128], F32, 1)
        zr = T("zr", [128, SBT], BF16, 4)
        zatt = T("zatt", [128, SBT], BF16, 4)
        smt = T("smt", [128, 256], F32, 4)
        p32 = T("p32", [128, 256], F32, 4)
        pnt = T("pnt", [128, 256], BF16, 4)
        ptt = T("ptt", [128, 2, 128], BF16, 4)
        ast = T("ast", [128, 8], F32, 4)
        mT = T("mT", [128, 8, SBT], BF16, 1)
        sgt = T("sgt", [128, SBT], F32, 2)
        m12 = T("m12", [128, SBT], F32, 2)
        fst = T("fst", [128, 4], F32, 2)

        S.add("pool", lambda e: e.memset(prevcol.t[0][:], 0.0), [], [prevcol.b[0]])
        kct = kc.t[0]
        KB = kc.b[0]
        for j, val in enumerate([RMS_EPS, 1.0, -0.5, 1e-12, GN_EPS]):
            S.add("pool", lambda e, j=j, val=val: e.memset(kct[:, j:j + 1], val), [], [KB])
        eps_col = kct[:, 0:1]
        one_col = kct[:, 1:2]
        mhalf_col = kct[:, 2:3]
        tiny_col = kct[:, 3:4]
        gneps_col = kct[:, 4:5]
        for i in range(NG):
            S.add("pool", lambda e, i=i: e.memset(ATbd.t[i][:], 0.0), [], [ATbd.b[i]])
            S.add("pool", lambda e, i=i: e.memset(Wz.t[i][:], 0.0), [], [Wz.b[i]])
        for i in range(4):
            S.add("pool", lambda e, i=i: e.memset(Ht.t[i][:], 0.0), [], [Ht.b[i]])
        for i in range(NV):
            S.add("pool", lambda e, i=i: e.memset(Vpad.t[i][:], 0.0), [], [Vpad.b[i]])
        for i in range(8):
            S.add("pool", lambda e, i=i: e.memset(PZ.t[i][:], 0.0), [], [PZ.b[i]])
            S.add("pool", lambda e, i=i: e.memset(Hbfz.t[i][:], 0.0), [], [Hbfz.b[i]])
            S.add("pool", lambda e, i=i: e.memset(qTz.t[i][:], 0.0), [], [qTz.b[i]])
        for i in range(2):
            S.add("pool", lambda e, i=i: e.memset(KTatt.t[i][:], 0.0), [], [KTatt.b[i]])
        Wout_b = [Buf(f"wout{k}") for k in range(8)]
        for k in range(8):
            S.add("pool", lambda e, k=k: e.dma_start(out=Wout.t[0][:, k, :], in_=w_out[k * 128:(k + 1) * 128, :]),
                  [], [Wout_b[k]], dma=True)

        wib_b = [Buf(f"wib{k}") for k in range(8)]
        wbb_b = [Buf(f"wbb{k}") for k in range(8)]
        w_br_flat = w_br.rearrange("b r c -> (b r) c")
        for k in range(8):
            S.add("pool", lambda e, k=k: e.dma_start(out=wib[k * 128:(k + 1) * 128, :], in_=w_in[k * 128:(k + 1) * 128, O_GR:IN_COLS]),
                  [], [wib_b[k]], dma=True)
        for k in range(8):
            S.add("pool", lambda e, k=k: e.dma_start(out=wbb[k * 128:(k + 1) * 128, :], in_=w_br_flat[k * 128:(k + 1) * 128, :]),
                  [], [wbb_b[k]], dma=True)

        def bcm(ap2, n):
            a = ap2.ap
            return bass.AP(ap2.tensor, ap2.offset, [list(a[0]), [0, n], list(a[1])])

        def bcl(ap2, n):
            a = ap2.ap
            return bass.AP(ap2.tensor, ap2.offset, [list(a[0]), list(a[1]), [0, n]])

        def v3(ap, h=2):
            return ap.rearrange("p (h t) -> p h t", h=h)

        def rms_rstd(in_ap, in_bufs, stt, stb, junk_ap, junk_buf):
            S.add("act", lambda e: e.activation(out=junk_ap, in_=in_ap, func=AF.Square, accum_out=stt[:, 0:1]),
                  in_bufs, [junk_buf, stb])
            S.add("act", lambda e: e.activation(out=stt[:, 1:2], in_=stt[:, 0:1], func=AF.Ln, bias=eps_col, scale=1.0 / D),
                  [stb, KB], [stb])
            S.add("act", lambda e: e.activation(out=stt[:, 2:3], in_=stt[:, 1:2], func=AF.Exp, scale=-0.5), [stb], [stb])

        def ws_load(srcs):
            i = wsrr[0]
            wsrr[0] = (i + 1) % NWS
            t = ws.t[i]
            bufs = [ws.b[i], ws_b2[i]]
            for j, (dfn, dap) in enumerate(srcs):
                S.add("sp", lambda e, dfn=dfn, dap=dap, t=t: e.dma_start(out=dfn(t), in_=dap), wib_b + wbb_b, [bufs[j]], dma=True)
            return t, bufs[:len(srcs)]

        def wcols(c0, n=128):
            return wib[:, c0 - O_GR:c0 - O_GR + n].rearrange("(k p) c -> p k c", p=128)

        def proj_fm(hTt, hTb, wt, wbufs, ncols=SBT, col0=0):
            pa, pab = fullbank()
            for k in range(8):
                S.add("pe", lambda e, pa=pa, k=k, wt=wt, hTt=hTt: e.matmul(
                    pa[:, 0:ncols], lhsT=wt[:, k, :], rhs=hTt[:, k, col0:col0 + ncols], start=(k == 0), stop=(k == 7)),
                    wbufs + [hTb], pab)
            return pa, pab

        P_ = [slice(0, 64), slice(64, 128)]
        own0 = nsb - nown

        def stage2_header():
            swt, swb = shwa()
            twt, twb = tw()
            S.add("act", lambda e, twt=twt, swt=swt: e.activation(out=twt[0:64, :], in_=swt[0:64, :], func=AF.Exp, scale=2.0), [swb], [twb])
            S.add("act", lambda e, twt=twt: e.activation(out=twt[0:64, :], in_=twt[0:64, :], func=AF.Ln, bias=kct[0:64, 1:2]), [twb, KB], [twb])
            S.add("act", lambda e, twt=twt: e.activation(out=twt[0:64, :], in_=twt[0:64, :], func=AF.Exp, scale=-1.0), [twb], [twb])
            S.add("dve", lambda e, twt=twt: e.tensor_scalar(out=twt[0:64, :], in0=twt[0:64, :], scalar1=-2.0, scalar2=1.0, op0=ALU.mult, op1=ALU.add), [twb], [twb])
            S.add("act", lambda e, twt=twt, swt=swt: e.activation(out=twt[64:128, :], in_=swt[64:128, :], func=AF.Copy), [swb, twb], [twb])
            twh, twhb = tw_hi()
            twl, twlb = tw_lo()
            S.add("act", lambda e, twh=twh, twt=twt: e.activation(out=twh[:], in_=twt[:], func=AF.Copy), [twb], [twhb])
            S.add("dve", lambda e, twl=twl, twt=twt, twh=twh: e.tensor_tensor(out=twl[:], in0=twt[:], in1=twh[:], op=ALU.subtract), [twb, twhb], [twlb])
            return dict(twt=twt, twh=twh, twl=twl, twhb=twhb, twlb=twlb, twb=twb)
        def prep_hp(hp, sbi, own, twt=None, twh=None, twl=None, twhb=None, twlb=None, twb=None):
            si = sbi * 4 + hp
            rt, rb = shr(si)
            kt_, kb_ = shk(si)
            vt, vb = shv(si)
            pD, pDb = fullbank()
            hsl = slice(hp * 128, (hp + 1) * 128)
            S.add("pe", lambda e, pD=pD, hsl=hsl, twh=twh: e.matmul(pD[:, 0:SBT], lhsT=Wdb.t[0][:, 0, hsl], rhs=twh[:, :], start=True, stop=False),
                  [Wdb.b[0], twhb], pDb)
            S.add("pe", lambda e, pD=pD, hsl=hsl, twl=twl: e.matmul(pD[:, 0:SBT], lhsT=Wdb.t[0][:, 0, hsl], rhs=twl[:, :], start=False, stop=False),
                  [Wdb.b[0], twlb], pDb)
            S.add("pe", lambda e, pD=pD, hsl=hsl, twh=twh: e.matmul(pD[:, 0:SBT], lhsT=Wdb.t[0][:, 1, hsl], rhs=twh[:, :], start=False, stop=True),
                  [Wdb.b[0], twhb], pDb)
            e1, e1b = t_e1()
            ew, ewb = t_ew()
            at, ab_ = t_a()
            cs, csb = t_cs()
            csp, cspb = t_csp()
            en, enb = t_en()
            k2, k2b = t_k2()
            kkn, kknb = t_kkn()
            abt, abb = t_ab()
            ft, fb = t_f()
            S.add("act", lambda e, e1=e1, pD=pD, hp=hp: e.activation(out=e1[:], in_=pD[:, 0:SBT], func=AF.Exp, bias=pc("nw0", hp), scale=-1.0),
                  pDb + [PB], [e1b])
            pAa, pAb = fullbank()
            S.add("pe", lambda e, pAa=pAa, hsl=hsl, twh=twh: e.matmul(pAa[:, 0:SBT], lhsT=Wdb.t[0][:, 2, hsl], rhs=twh[:, :], start=True, stop=True),
                  [Wdb.b[0], twhb], pAb)
            S.add("act", lambda e, e1=e1: e.activation(out=e1[:], in_=e1[:], func=AF.Ln, bias=one_col), [e1b, KB], [e1b])
            S.add("act", lambda e, e1=e1, ew=ew: e.activation(out=ew[:], in_=e1[:], func=AF.Exp, bias=mhalf_col, scale=-1.0), [e1b, KB], [ewb])
            S.add("act", lambda e, at=at, pAa=pAa, hp=hp: e.activation(out=at[:], in_=pAa[:, 0:SBT], func=AF.Exp, bias=pc("na0", hp), scale=-1.0),
                  pAb + [PB], [ab_])
            yield
            S.add("act", lambda e, at=at: e.activation(out=at[:], in_=at[:], func=AF.Ln, bias=one_col), [ab_, KB], [ab_])
            S.add("act", lambda e, at=at: e.activation(out=at[:], in_=at[:], func=AF.Exp, scale=-1.0), [ab_], [ab_])
            for c in range(CPS):
                S.add("dve", lambda e, cs=cs, ew=ew, c=c: e.tensor_tensor_scan(
                    out=cs[:, c * C:(c + 1) * C], data0=ones_f, data1=ew[:, c * C:(c + 1) * C], initial=0.0,
                    op0=ALU.mult, op1=ALU.add), [ewb, CF], [csb])
            S.add("pool", lambda e, csp=csp, cs=cs, ew=ew: e.tensor_tensor(out=csp[:], in0=cs[:], in1=ew[:], op=ALU.subtract), [csb, ewb], [cspb])
            S.add("act", lambda e, en=en, cs=cs: e.activation(out=en[:], in_=cs[:], func=AF.Exp), [csb], [enb])
            S.add("act", lambda e, csp=csp: e.activation(out=csp[:], in_=csp[:], func=AF.Exp, scale=-1.0), [cspb], [cspb])
            gct, gcb = gC(si)
            S.add("act", lambda e, gct=gct, cs=cs: e.activation(
                out=gct[:, 0:CPS], in_=cs[:, :].rearrange("p (c t) -> p c t", t=C)[:, :, C - 1], func=AF.Exp, scale=-1.0), [csb], [gcb])
            yield
            k2h, k2hb = t_k2b()
            S.add("act", lambda e, k2h=k2h, kt_=kt_, hp=hp: e.activation(out=k2h[:], in_=kt_[:], func=AF.Square, scale=pc("kk", hp)), [kb_, PB], [k2hb])
            pS_, pSb = fullbank()
            S.add("pe", lambda e, pS_=pS_, k2h=k2h: e.matmul(pS_[:, 0:SBT], lhsT=bones_b, rhs=k2h[:], start=True, stop=True), [CB, k2hb], pSb)
            S.add("act", lambda e, k2=k2, pS_=pS_: e.activation(out=k2[:], in_=pS_[:, 0:SBT], func=AF.Ln, bias=tiny_col), pSb + [KB], [k2b])
            S.add("act", lambda e, k2=k2: e.activation(out=k2[:], in_=k2[:], func=AF.Exp, scale=-0.5), [k2b], [k2b])
            S.add("dve", lambda e, kkn=kkn, kt_=kt_, k2=k2, hp=hp: e.scalar_tensor_tensor(
                out=kkn[:], in0=kt_[:], scalar=pc("kk", hp), in1=k2[:], op0=ALU.mult, op1=ALU.mult), [kb_, k2b, PB], [kknb])
            yield
            ARt, ARb = AR(si)
            BTt, BTb = BT(si)
            KTt, KTb = KT(si)
            vbt, vbb = vbf(si)
            S.add("dve", lambda e, ARt=ARt, kkn=kkn, csp=csp: e.scalar_tensor_tensor(
                out=ARt[:, :, 0, :], in0=kkn[:, :].rearrange("p (c t) -> p c t", t=C), scalar=-1.0,
                in1=csp[:, :].rearrange("p (c t) -> p c t", t=C), op0=ALU.mult, op1=ALU.mult), [kknb, cspb], [ARb])
            S.add("pool", lambda e, abt=abt, kkn=kkn, at=at: e.tensor_tensor(out=abt[:], in0=kkn[:], in1=at[:], op=ALU.mult), [kknb, ab_], [abb])
            S.add("pool", lambda e, BTt=BTt, abt=abt, en=en: e.tensor_tensor(out=BTt[:], in0=abt[:], in1=en[:], op=ALU.mult), [abb, enb], [BTb])
            yield
            S.add("dve", lambda e, ft=ft, at=at, hp=hp: e.tensor_scalar(out=ft[:], in0=at[:], scalar1=pc("ka", hp), scalar2=pc("omka", hp),
                                                                    op0=ALU.mult, op1=ALU.add), [ab_, PB], [fb])
            S.add("pool", lambda e, ft=ft, kt_=kt_: e.tensor_tensor(out=ft[:], in0=kt_[:], in1=ft[:], op=ALU.mult), [kb_, fb], [fb])
            S.add("pool", lambda e, KTt=KTt, ft=ft, en=en: e.tensor_tensor(out=KTt[:], in0=ft[:], in1=en[:], op=ALU.mult), [fb, enb], [KTb])
            S.add("act", lambda e, vbt=vbt, vt=vt: e.activation(out=vbt[:], in_=vt[:], func=AF.Copy), [vb], [vbb])
            yield
            for hh in range(2):
                zt, zb = PZ(si * 2 + hh)
                S.add("pool", lambda e, zt=zt, ARt=ARt, hh=hh: e.tensor_copy(out=zt[P_[hh], 0, :].rearrange("p (c t) -> p c t", t=C), in_=ARt[P_[hh], :, 0, :]), [ARb], [zb])
                S.add("pool", lambda e, zt=zt, BTt=BTt, hh=hh: e.tensor_copy(out=zt[P_[hh], 1, :], in_=BTt[P_[hh], :]), [BTb], [zb])
                S.add("pool", lambda e, zt=zt, KTt=KTt, hh=hh: e.tensor_copy(out=zt[P_[hh], 2, :], in_=KTt[P_[hh], :]), [KTb], [zb])
            if own:
                ep, epb = t_ep()
                S.add("act", lambda e, ep=ep, cs=cs: e.activation(out=ep[:], in_=cs[:], func=AF.Exp, scale=-1.0), [csb], [epb])
                S.add("dve", lambda e, ARt=ARt, rt=rt, ep=ep: e.tensor_tensor(
                    out=ARt[:, :, 1, :], in0=rt[:, :].rearrange("p (c t) -> p c t", t=C),
                    in1=ep[:, :].rearrange("p (c t) -> p c t", t=C), op=ALU.mult), [rb, epb], [ARb])
                rkb_t, rkb_b = t_rkb()
                S.add("dve", lambda e, rkb_t=rkb_t, rt=rt, ft=ft, hp=hp: e.scalar_tensor_tensor(
                    out=rkb_t[:], in0=rt[:], scalar=pc("rk", hp), in1=ft[:], op0=ALU.mult, op1=ALU.mult), [rb, fb, PB], [rkb_b])
                pB_, pBb = fullbank()
                S.add("pe", lambda e, pB_=pB_, rkb_t=rkb_t: e.matmul(pB_[:, 0:SBT], lhsT=bones_b, rhs=rkb_t[:], start=True, stop=True), [CB, rkb_b], pBb)
                bnt, bnb = bonus(si)
                S.add("dve", lambda e, bnt=bnt, pB_=pB_, vt=vt: e.tensor_tensor(out=bnt[:], in0=pB_[:, 0:SBT], in1=vt[:], op=ALU.mult), pBb + [vb], [bnb])
            yield
        def proj_tile(sbi, ct, hTt, hTb):
            pa, pab = fullbank()
            for k in range(8):
                S.add("pe", lambda e, pa=pa, k=k, ct=ct, hTt=hTt: e.matmul(
                    pa[:, 0:SBT], lhsT=Wsh.t[0][:, k, ct * 128:(ct + 1) * 128], rhs=hTt[:, k, :],
                    start=(k == 0), stop=(k == 7)), [Wsh_b[k], hTb], pab)
            if ct == 12:
                dst, dstb = shwa()
            else:
                hp = ct % 4
                dst, dstb = (shr, shk, shv)[ct // 4](sbi * 4 + hp)
            tmp, tmpb = shtmp(ct)
            S.add("act", lambda e, tmp=tmp, pa=pa, ct=ct: e.activation(
                out=tmp[:], in_=pa[:, 0:SBT], func=AF.Copy, scale=pc("omu", ct)), pab + [PB], [tmpb])
            S.add("dve", lambda e, dst=dst, pa=pa, tmp=tmp, ct=ct: e.scalar_tensor_tensor(
                out=dst[:, 1:SBT], in0=pa[:, 0:SBT - 1], scalar=pc("mu", ct), in1=tmp[:, 1:SBT],
                op0=ALU.mult, op1=ALU.add), pab + [tmpb, PB], [dstb])
            S.add("dve", lambda e, dst=dst, tmp=tmp, ct=ct: e.scalar_tensor_tensor(
                out=dst[:, 0:1], in0=prevcol.t[0][:, ct:ct + 1], scalar=pc("mu", ct), in1=tmp[:, 0:1],
                op0=ALU.mult, op1=ALU.add), [prevcol.b[0], tmpb, PB], [dstb])
            S.add("act", lambda e, pa=pa, ct=ct: e.activation(
                out=prevcol.t[0][:, ct:ct + 1], in_=pa[:, SBT - 1:SBT], func=AF.Copy), pab, [prevcol.b[0]])

        sbst = {}

        def gen_first(sbi):
            own = sbi >= own0
            hTt, hTb = hT(sbi)
            for j in range(CPS):
                gc = sbi * CPS + j
                xtt, xtb = xt(gc)
                hbt, hbb = hb(gc)
                stt, stb = st0(gc)
                dma("sp", xtt[:], xw[gc * C:(gc + 1) * C, :], [], [xtb])
                rms_rstd(xtt[:], [xtb], stt, stb, hbt[:], hbb)
                S.add("dve", lambda e, xtt=xtt, stt=stt, hbt=hbt: e.scalar_tensor_tensor(
                    out=hbt[:], in0=xtt[:], scalar=stt[:, 2:3], in1=gpre.t[0][:], op0=ALU.mult, op1=ALU.mult),
                    [xtb, stb, gpre.b[0]], [hbb])
                yield
                pst = psT[0]
                pstb = psT_b[0]
                for k in range(8):
                    S.add("pe", lambda e, k=k, hbt=hbt, pst=pst: e.transpose(
                        out=pst[:, k * 128:(k + 1) * 128], in_=hbt[:, k * 128:(k + 1) * 128], identity=ident_b),
                        [hbb, CB], pstb)
                S.add("act", lambda e, pst=pst, hTt=hTt, j=j: e.activation(
                    out=hTt[:, :, j * C:(j + 1) * C], in_=pst[:, :].rearrange("p (k t) -> p k t", k=8), func=AF.Copy),
                    pstb, [hTb])
                yield
            proj_tile(sbi, 12, hTt, hTb)
            tw_ctx = stage2_header()
            sbst[sbi] = (tw_ctx, hTt, hTb)
            yield
            for hp in (0, 1):
                for q in range(3):
                    proj_tile(sbi, q * 4 + hp, hTt, hTb)
                    yield

        def gen_first_b(sbi):
            own = sbi >= own0
            tw_ctx, hTt, hTb = sbst[sbi]
            for hp in (0, 1):
                yield from prep_hp(hp, sbi, own, **tw_ctx)

        def gen_first_ab(sbi):
            yield from gen_first(sbi)
            yield from gen_first_b(sbi)

        def gen_second(sbi):
            own = sbi >= own0
            tw_ctx, hTt, hTb = sbst[sbi]
            for hp in (2, 3):
                for q in range(3):
                    proj_tile(sbi, q * 4 + hp, hTt, hTb)
                    yield
                yield from prep_hp(hp, sbi, own, **tw_ctx)

        def drain(g):
            for _ in g:
                pass

        def mkfill(g, n=1, units=None, slots=None):
            st = [0]

            def fill():
                if units is None:
                    k = n
                else:
                    i = st[0]
                    st[0] += 1
                    k = ((i + 1) * units) // slots - (i * units) // slots
                for _ in range(k):
                    try:
                        next(g)
                    except StopIteration:
                        return
            return fill

        drain(gen_first_ab(0))
        for sbi in range(nsb):
            own = sbi >= own0
            halo_sb = (sbi == own0 - 1)
            hTt, hTb = hT(sbi)
            def gen_ownproj(sbi=sbi, own=own, halo_sb=halo_sb, hTt=hTt, hTb=hTb):
                if own or halo_sb:
                    ncols, col0 = (SBT, 0) if own else (C, SBT - C)
                    kcol = ((sbi - own0) * CPS + 1) * C if own else 0
                    for g in range(2):
                        wt, wb = ws_load([(lambda t: t[:, :, 0:64], wcols(O_K + g * 64, 64)), (lambda t: t[:, :, 64:128], wcols(O_K + g * 64, 64))])
                        pa, pab = proj_fm(hTt, hTb, wt, wb, ncols, col0)
                        for cc in range(ncols // C):
                            ks = ((kcol // C) + cc) % NKS
                            S.add("act", lambda e, pa=pa, g=g, ks=ks, cc=cc: e.activation(
                                out=KTatt.t[g][:, ks * C:(ks + 1) * C], in_=pa[:, cc * C:(cc + 1) * C], func=AF.Identity, bias=pc("bk", g)), pab + [PB], [KTatt.b[g]])
                        yield
                    wt, wb = ws_load([(lambda t: t[:, :, :], wcols(O_V))])
                    for c in (range(CPS) if own else [CPS - 1]):
                        lc1 = (sbi - own0) * CPS + c + 1 if own else 0
                        pa, pab = fullbank()
                        for k in range(8):
                            S.add("pe", lambda e, pa=pa, k=k, wt=wt, hTt=hTt, c=c: e.matmul(
                                pa[:, 0:128], lhsT=hTt[:, k, c * C:(c + 1) * C], rhs=wt[:, k, :], start=(k == 0), stop=(k == 7)), wb + [hTb], pab)
                        vp, vpb = Vpad(lc1)
                        S.add("dve", lambda e, vp=vp, pa=pa: e.tensor_tensor(out=vp[:, :, 0:64], in0=v3(pa[:, 0:128]), in1=v3(bvb.t[0][:, :]), op=ALU.add),
                              pab + [bvb.b[0]], [vpb])
                        S.add("pool", lambda e, vp=vp: e.tensor_copy(out=vp[:, :, 128:192], in_=vp[:, :, 0:64]), [vpb], [vpb])
                        yield
                if own:
                    for ct in range(4):
                        si = sbi * 4 + ct
                        wt, wb = ws_load([(lambda t: t[:, :, :], wcols(O_GR + ct * 128))])
                        pa, pab = proj_fm(hTt, hTb, wt, wb)
                        S.add("act", lambda e, pa=pa, si=si: e.activation(out=sgr(si)[0][:], in_=pa[:, 0:SBT], func=AF.Silu), pab, [sgr(si)[1]])
                        yield
                        wt, wb = ws_load([(lambda t: t[:, :, :], wcols(O_Q + ct * 128))])
                        pa, pab = proj_fm(hTt, hTb, wt, wb)
                        for hh in range(2):
                            qz, qzb = qTz(si * 2 + hh)
                            S.add("act", lambda e, pa=pa, qz=qz, ct=ct, hh=hh: e.activation(
                                out=qz[P_[hh], :], in_=pa[P_[hh], 0:SBT], func=AF.Identity, bias=ppt[P_[hh], PPI["bq"] + ct:PPI["bq"] + ct + 1]),
                                pab + [PB], [qzb])
                        yield
                        wt, wb = ws_load([(lambda t: t[:, :, :], wcols(O_GA + ct * 128))])
                        pa, pab = proj_fm(hTt, hTb, wt, wb)
                        S.add("act", lambda e, pa=pa, si=si: e.activation(out=sga(si)[0][:], in_=pa[:, 0:SBT], func=AF.Silu), pab, [sga(si)[1]])
                        yield

                yield

            def emit_chunk_pairs(c, pairs, fill, own=own, sbi=sbi):
                gch = sbi * CPS + c
                csl = slice(c * C, (c + 1) * C)
                pYbank, pYbb = psA[0], [psA_b[0]]
                def mkctx(hp):
                    si = sbi * 4 + hp
                    gi = gch * 4 + hp
                    x = dict(hp=hp, si=si, gi=gi)
                    x["AR"], x["ARb"] = AR(si)
                    x["BT"], x["BTb"] = BT(si)
                    x["KT"], x["KTb"] = KT(si)
                    x["vb"], x["vbb"] = vbf(si)
                    x["zts"] = [PZ(si * 2 + hh) for hh in range(2)]
                    x["tm"], x["tmb"] = tm(gi)
                    return x

                def g_transposes(x, c=c, csl=csl):
                    pt_ = psT[1][:, 0:512]
                    ptb = psT_b[1]
                    srcs = [(x["AR"][:, c, 0, :], x["ARb"]), (x["BT"][:, csl], x["BTb"]), (x["KT"][:, csl], x["KTb"]), (x["vb"][:, csl], x["vbb"])]
                    for q, (sap, sbf) in enumerate(srcs):
                        S.add("pe", lambda e, pt_=pt_, q=q, sap=sap: e.transpose(out=pt_[:, q * 128:(q + 1) * 128], in_=sap, identity=ident_b),
                              [sbf, CB], ptb)
                    tmt = x["tm"]
                    S.add("act", lambda e, tmt=tmt, pt_=pt_: e.activation(out=tmt[:], in_=pt_.rearrange("p (q t) -> p q t", q=4), func=AF.Copy),
                          ptb, [x["tmb"]])

                def g_sprod_pe(x, c=c, csl=csl):
                    x["pS1"], x["pS1b"] = single(4)
                    x["pS2"], x["pS2b"] = single(2)
                    ARt, BTt = x["AR"], x["BT"]
                    for hh in range(2):
                        zt, zb = x["zts"][hh]
                        S.add("pe", lambda e, pS=x["pS1"], zt=zt, ARt=ARt, hh=hh: e.matmul(
                            pS[:, hh * 128:(hh + 1) * 128], lhsT=zt[:, 1, csl], rhs=ARt[:, c, 0, :], start=True, stop=True), [zb, x["ARb"]], x["pS1b"])
                    for hh in range(2):
                        zt, zb = x["zts"][hh]
                        S.add("pe", lambda e, pS=x["pS1"], zt=zt, BTt=BTt, hh=hh: e.matmul(
                            pS[:, (2 + hh) * 128:(3 + hh) * 128], lhsT=zt[:, 0, csl], rhs=BTt[:, csl], start=True, stop=True), [zb, x["BTb"]], x["pS1b"])
                    for hh in range(2):
                        zt, zb = x["zts"][hh]
                        S.add("pe", lambda e, pS=x["pS2"], zt=zt, ARt=ARt, hh=hh: e.matmul(
                            pS[:, hh * 128:(hh + 1) * 128], lhsT=zt[:, 2, csl], rhs=ARt[:, c, 0, :], start=True, stop=True), [zb, x["ARb"]], x["pS2b"])

                def g_sprod_evac(x):
                    gi = x["gi"]
                    nm, nmb = NMt(gi * 2)
                    mk, mkb = Mak(gi)
                    S.add("dve", lambda e, nm=nm, pS=x["pS1"]: e.tensor_tensor(out=nm[:, :, :, :].rearrange("p a h t -> p (a h) t"), in0=v3(pS, 4), in1=mask4, op=ALU.mult),
                          x["pS1b"] + [CB], [nmb])
                    S.add("dve", lambda e, mk=mk, pS=x["pS2"]: e.tensor_tensor(out=mk[:], in0=v3(pS, 2), in1=mask4[:, 0:2, :], op=ALU.mult),
                          x["pS2b"] + [CB], [mkb])
                    x["nm"], x["nmb"], x["mk"], x["mkb"] = nm, nmb, mk, mkb

                def g_r_pe(x, c=c, csl=csl):
                    x["pR"], x["pRb"] = single(4)
                    ARt = x["AR"]
                    for a_ in range(2):
                        for hh in range(2):
                            zt, zb = x["zts"][hh]
                            S.add("pe", lambda e, pR=x["pR"], zt=zt, ARt=ARt, hh=hh, a_=a_: e.matmul(
                                pR[:, (a_ * 2 + hh) * 128:(a_ * 2 + hh + 1) * 128], lhsT=zt[:, 1 + a_, csl], rhs=ARt[:, c, 1, :], start=True, stop=True),
                                [zb, x["ARb"]], x["pRb"])

                def g_r_evac(x, c=c, csl=csl):
                    rbk, rbkb = RBK(x["gi"])
                    for a_ in range(2):
                        S.add("dve", lambda e, rbk=rbk, pR=x["pR"], a_=a_: e.tensor_tensor(out=rbk[:, a_, :, :], in0=v3(pR[:, a_ * 256:(a_ + 1) * 256], 2), in1=mle2, op=ALU.mult),
                              x["pRb"] + [CB], [rbkb])
                    x["rbk"], x["rbkb"] = rbk, rbkb

                def g_pv_pe(x, c=c, csl=csl):
                    x["pV"], x["pVb"] = single(1)
                    mk, tmt = x["mk"], x["tm"]
                    for hh in range(2):
                        S.add("pe", lambda e, pV=x["pV"], hh=hh, mk=mk, tmt=tmt: e.matmul(
                            pV[:, hh * 64:(hh + 1) * 64], lhsT=mk[:, hh, :], rhs=tmt[:, 3, hh * 64:(hh + 1) * 64], start=True, stop=True), [x["mkb"], x["tmb"]], x["pVb"])

                def g_x0(x, c=c, csl=csl):
                    Xt, Xb = Xtile(x["gi"] * 2)
                    tmt = x["tm"]
                    S.add("pool", lambda e, Xt=Xt, tmt=tmt: e.tensor_copy(out=Xt[:, :, 0, :], in_=tmt[:, 0, :].rearrange("p (h k) -> p h k", h=2)), [x["tmb"]], [Xb])
                    S.add("act", lambda e, Xt=Xt, pV=x["pV"]: e.activation(out=Xt[:, :, 1, :], in_=pV[:, 0:128].rearrange("p (h k) -> p h k", h=2), func=AF.Copy),
                          x["pVb"], [Xb])
                    x["X"], x["Xb"] = Xt, Xb

                def g_level_pe(x, lv):
                    nm, nmb, Xt, Xb = x["nm"], x["nmb"], x["X"], x["Xb"]
                    x["pX"], x["pXb"] = single(2)
                    for hh in range(2):
                        S.add("pe", lambda e, pX=x["pX"], hh=hh, nm=nm, Xt=Xt: e.matmul(
                            pX[:, hh * 128:(hh + 1) * 128], lhsT=nm[:, 0, hh, :], rhs=Xt[:, hh, :, :].rearrange("p a k -> p (a k)"), start=True, stop=True),
                            [nmb, Xb], x["pXb"])
                    if lv < 6:
                        x["pNM"], x["pNMb"] = single(4)
                        for hh in range(2):
                            S.add("pe", lambda e, pN=x["pNM"], hh=hh, nm=nm: e.matmul(
                                pN[:, hh * 128:(hh + 1) * 128], lhsT=nm[:, 1, hh, :], rhs=nm[:, 0, hh, :], start=True, stop=True), [nmb], x["pNMb"])
                        if lv < 5:
                            for hh in range(2):
                                S.add("pe", lambda e, pN=x["pNM"], hh=hh, nm=nm: e.matmul(
                                    pN[:, (2 + hh) * 128:(3 + hh) * 128], lhsT=nm[:, 0, hh, :], rhs=nm[:, 1, hh, :], start=True, stop=True), [nmb], x["pNMb"])

                def g_level_evac(x, lv):
                    gi = x["gi"]
                    Xt, Xb = x["X"], x["Xb"]
                    Xn, Xnb = Xtile(gi * 2 + lv + 1)
                    S.add("dve", lambda e, Xn=Xn, pX=x["pX"], Xt=Xt: e.tensor_tensor(
                        out=Xn[:, :, :, :].rearrange("p h a k -> p h (a k)"), in0=v3(pX), in1=Xt[:, :, :, :].rearrange("p h a k -> p h (a k)"), op=ALU.add),
                        x["pXb"] + [Xb], [Xnb])
                    x["X"], x["Xb"] = Xn, Xnb
                    if lv < 6:
                        nn, nnb = NMt(gi * 2 + lv + 1)
                        w = 4 if lv < 5 else 2
                        S.add("act", lambda e, nn=nn, pN=x["pNM"], w=w: e.activation(
                            out=nn[:, :, :, :].rearrange("p a h t -> p (a h) t")[:, 0:w, :], in_=v3(pN[:, 0:w * 128], w), func=AF.Copy), x["pNMb"], [nnb])
                        x["nm"], x["nmb"] = nn, nnb

                def g_state(x, c=c, csl=csl, own=own, pYbank=(pYbank if own else None), pYbb=(pYbb if own else None)):
                    gi, hp, si = x["gi"], x["hp"], x["si"]
                    Xt, Xb, tmt, tmb = x["X"], x["Xb"], x["tm"], x["tmb"]
                    ARt, ARb = x["AR"], x["ARb"]
                    wz, wzb = Wz(gi)
                    S.add("pool", lambda e, wz=wz, Xt=Xt: e.tensor_copy(out=wz[:, 0::2, :], in_=Xt[:, :, 0, :]), [Xb], [wzb])
                    wzA = wz[:, 0:2, :].rearrange("p a k -> p (a k)")
                    wzB = wz[:, 1:3, :].rearrange("p a k -> p (a k)")
                    wp, wpb = Wp(gi)
                    S.add("pool", lambda e, wp=wp, Xt=Xt: e.tensor_copy(out=wp[:, :, :], in_=Xt[:, :, 0, :]), [Xb], [wpb])
                    pAT, pATb = single(1)
                    S.add("pe", lambda e, pAT=pAT, wp=wp, tmt=tmt: e.matmul(pAT[:, 0:128], lhsT=wp[:, :, :].rearrange("p a k -> p (a k)"), rhs=tmt[:, 1, :], start=True, stop=True),
                          [wpb, tmb], pATb)
                    atb, atbb = ATbd(gi)
                    for hh in range(2):
                        S.add("dve", lambda e, atb=atb, pAT=pAT, hh=hh: e.tensor_copy(
                            out=atb[P_[hh], hh * 64:(hh + 1) * 64], in_=pAT[P_[hh], hh * 64:(hh + 1) * 64]), pATb, [atbb])
                    pG, pGb = single(1)
                    pG2, pG2b = single(1)
                    S.add("pe", lambda e, pG=pG, tmt=tmt: e.matmul(pG[:, 0:128], lhsT=tmt[:, 2, :], rhs=tmt[:, 3, :], start=True, stop=True),
                          [tmb], pGb)
                    for hh in range(2):
                        S.add("pe", lambda e, pG2=pG2, Xt=Xt, tmt=tmt, hh=hh: e.matmul(
                            pG2[:, hh * 64:(hh + 1) * 64], lhsT=tmt[:, 1, :], rhs=Xt[:, hh, 1, :], start=True, stop=True), [Xb, tmb], pG2b)
                    gs, gsb_ = Gsb(gi)
                    for hh in range(2):
                        S.add("act", lambda e, gs=gs, pG=pG, hh=hh: e.activation(
                            out=gs[P_[hh], :], in_=pG[P_[hh], hh * 64:(hh + 1) * 64], func=AF.Copy), pGb, [gsb_])
                        S.add("dve", lambda e, gs=gs, pG2=pG2, hh=hh: e.tensor_tensor(
                            out=gs[P_[hh], :], in0=pG2[P_[hh], hh * 64:(hh + 1) * 64], in1=gs[P_[hh], :], op=ALU.add), pG2b + [gsb_], [gsb_])
                    Htt, Hb_ = Ht(hp)
                    gct, gcb = gC(si)
                    if own:
                        hbz = [Hbfz(gi * 2 + hh) for hh in range(2)]
                        for hh in range(2):
                            S.add("pool", lambda e, hz=hbz[hh][0], Htt=Htt, hh=hh: e.tensor_copy(out=hz[P_[hh], :], in_=Htt[P_[hh], :]), [Hb_], [hbz[hh][1]])
                    hhl, hhlb = Hhl(gi)
                    S.add("pool", lambda e, hhl=hhl, Htt=Htt: e.tensor_copy(out=hhl[:, 0, :], in_=Htt[:]), [Hb_], [hhlb])
                    S.add("pool", lambda e, hhl=hhl, Htt=Htt: e.tensor_tensor(out=hhl[:, 1, :], in0=Htt[:], in1=hhl[:, 0, :], op=ALU.subtract), [Hb_, hhlb], [hhlb])
                    pZ, pZb = single(1)
                    S.add("pe", lambda e, pZ=pZ, atb=atb, hhl=hhl: e.matmul(pZ[:, 0:128], lhsT=atb[:], rhs=hhl[:, :, :].rearrange("p a v -> p (a v)"), start=True, stop=True),
                          [atbb, hhlb], pZb)
                    s1, s1b = s1t(gi)
                    S.add("pool", lambda e, s1=s1, Htt=Htt, gs=gs: e.tensor_tensor(out=s1[:], in0=Htt[:], in1=gs[:], op=ALU.add), [Hb_, gsb_], [s1b])
                    S.add("pool", lambda e, s1=s1, gct=gct: e.tensor_scalar(out=s1[:], in0=s1[:], scalar1=gct[:, c:c + 1], scalar2=1.0, op0=ALU.mult, op1=ALU.mult),
                          [s1b, gcb], [s1b])
                    S.add("dve", lambda e, pZ=pZ, gct=gct, s1=s1: e.scalar_tensor_tensor(
                        out=s1[:], in0=pZ[:, 0:64], scalar=gct[:, c:c + 1], in1=s1[:], op0=ALU.mult, op1=ALU.add), pZb + [gcb, s1b], [s1b])
                    S.add("dve", lambda e, Htt=Htt, pZ=pZ, gct=gct, s1=s1: e.scalar_tensor_tensor(
                        out=Htt[:], in0=pZ[:, 64:128], scalar=gct[:, c:c + 1], in1=s1[:], op0=ALU.mult, op1=ALU.add), pZb + [gcb, s1b], [Hb_])
                    if own:
                        rbk, rbkb = x["rbk"], x["rbkb"]
                        qt_, qtb = QT(gi)
                        for hh, wzX in enumerate((wzA, wzB)):
                            pQ, pQb = single(1)
                            S.add("pe", lambda e, pQ=pQ, wzX=wzX, rbk=rbk, hh=hh: e.matmul(pQ[:, 0:128], lhsT=wzX, rhs=rbk[:, 0, hh, :], start=True, stop=True),
                                  [wzb, rbkb], pQb)
                            S.add("dve", lambda e, qt_=qt_, pQ=pQ, ARt=ARt, hh=hh: e.tensor_tensor(
                                out=qt_[P_[hh], :], in0=pQ[P_[hh], 0:128], in1=ARt[P_[hh], c, 1, :], op=ALU.add), pQb + [ARb], [qtb])
                        for hh in range(2):
                            pY, pYb = pYbank, pYbb
                            ysl = slice((hp * 2 + hh) * 64, (hp * 2 + hh + 1) * 64)
                            S.add("pe", lambda e, pY=pY, ysl=ysl, hh=hh, rbk=rbk, Xt=Xt: e.matmul(
                                pY[:, ysl], lhsT=rbk[:, 0, hh, :], rhs=Xt[:, hh, 1, :], start=True, stop=False), [rbkb, Xb], pYb)
                            S.add("pe", lambda e, pY=pY, ysl=ysl, hh=hh, rbk=rbk, tmt=tmt: e.matmul(
                                pY[:, ysl], lhsT=rbk[:, 1, hh, :], rhs=tmt[:, 3, hh * 64:(hh + 1) * 64], start=False, stop=False), [rbkb, tmb], pYb)
                            S.add("pe", lambda e, pY=pY, ysl=ysl, qt_=qt_, hz=hbz[hh][0]: e.matmul(
                                pY[:, ysl], lhsT=qt_[:, :], rhs=hz[:, :], start=False, stop=True), [qtb, hbz[hh][1]], pYb)

                for pr in pairs:
                    ctxs = [mkctx(hp) for hp in pr]
                    for x in ctxs:
                        g_transposes(x)
                    for x in ctxs:
                        g_sprod_pe(x)
                    for x in ctxs:
                        g_sprod_evac(x)
                    fill()
                    if own:
                        for x in ctxs:
                            g_r_pe(x)
                        for x in ctxs:
                            g_r_evac(x)
                    for x in ctxs:
                        g_pv_pe(x)
                    for x in ctxs:
                        g_x0(x)
                    fill()
                    for lv in range(7):
                        for x in ctxs:
                            g_level_pe(x, lv)
                        for x in ctxs:
                            g_level_evac(x, lv)
                        fill()
                    for x in ctxs:
                        g_state(x)
            if not own:
                if halo_sb:
                    drain(gen_ownproj())
                g2 = gen_second(sbi)
                f2 = mkfill(g2, units=23, slots=17)
                for c in range(CPS):
                    emit_chunk_pairs(c, [PAIRS[0]], f2)
                drain(g2)
                g1 = gen_first_ab(sbi + 1) if sbi + 1 < nsb else iter(())
                f1 = mkfill(g1, units=28, slots=17)
                for c in range(CPS):
                    emit_chunk_pairs(c, [PAIRS[1]], f1)
                drain(g1)
                continue
            fb_mode[0] = 1
            g1 = iter(())
            for c in range(CPS):
                gch = sbi * CPS + c
                csl = slice(c * C, (c + 1) * C)
                pYbank, pYbb = psA[0], [psA_b[0]]
                if c == 0:
                    g2 = gen_second(sbi)
                    emit_chunk_pairs(c, [PAIRS[0]], mkfill(g2, 2))
                    drain(g2)
                    g3 = gen_ownproj()
                    emit_chunk_pairs(c, [PAIRS[1]], mkfill(g3, 2))
                    drain(g3)
                else:
                    if c == 1 and sbi + 1 < nsb:
                        g1 = gen_first(sbi + 1)
                    emit_chunk_pairs(c, [PAIRS[0]], mkfill(g1, 1))
                    emit_chunk_pairs(c, [PAIRS[1]], mkfill(g1, 1))
                afill = lambda: None
                if c == CPS - 1 and sbi + 1 < nsb:
                    drain(g1)
                    gfb = gen_first_b(sbi + 1)
                    afill = mkfill(gfb, units=13, slots=10)
                lc = (sbi - own0) * CPS + c
                g_t, g_b = gst()
                pY, pYb = pYbank, pYbb
                yq, yqb = ysqt(0)
                S.add("dve", lambda e, g_t=g_t, pY=pY: e.tensor_reduce(out=g_t[:, 0, :], in_=v3(pY[:, 0:512], 8), axis=AX.X, op=ALU.add), pYb, [g_b])
                S.add("act", lambda e, yq=yq, pY=pY: e.activation(out=yq[:], in_=pY[:, 0:512], func=AF.Square), pYb, [yqb])
                S.add("dve", lambda e, g_t=g_t, yq=yq: e.tensor_reduce(out=g_t[:, 1, :], in_=v3(yq[:, :], 8), axis=AX.X, op=ALU.add), [yqb], [g_b])
                S.add("dve", lambda e, g_t=g_t: e.tensor_scalar(out=g_t[:, 2, :], in0=g_t[:, 0, :], scalar1=1.0 / 64, scalar2=None, op0=ALU.mult), [g_b], [g_b])
                S.add("dve", lambda e, g_t=g_t: e.tensor_tensor(out=g_t[:, 3, :], in0=g_t[:, 2, :], in1=g_t[:, 2, :], op=ALU.mult), [g_b], [g_b])
                S.add("dve", lambda e, g_t=g_t: e.scalar_tensor_tensor(out=g_t[:, 4, :], in0=g_t[:, 1, :], scalar=1.0 / 64, in1=g_t[:, 3, :],
                                                                       op0=ALU.mult, op1=ALU.subtract), [g_b], [g_b])
                S.add("act", lambda e, g_t=g_t: e.activation(out=g_t[:, 5, :], in_=g_t[:, 4, :], func=AF.Ln, bias=gneps_col), [g_b, KB], [g_b])
                S.add("act", lambda e, g_t=g_t: e.activation(out=g_t[:, 5, :], in_=g_t[:, 5, :], func=AF.Exp, scale=-0.5), [g_b], [g_b])
                ynt, ynb = yn()
                S.add("dve", lambda e, yq=yq, pY=pY, g_t=g_t: e.tensor_tensor(
                    out=v3(yq[:, :], 8), in0=v3(pY[:, 0:512], 8), in1=bcl(g_t[:, 2, :], 64), op=ALU.subtract), pYb + [g_b, yqb], [yqb])
                S.add("pool", lambda e, ynt=ynt, yq=yq, g_t=g_t: e.tensor_tensor(
                    out=v3(ynt[:, :], 8), in0=v3(yq[:, :], 8), in1=bcl(g_t[:, 5, :], 64), op=ALU.mult), [yqb, g_b], [ynb])
                pt_ = psT[1][:, 0:512]
                ptb = psT_b[1]
                for hp in range(4):
                    S.add("pe", lambda e, pt_=pt_, hp=hp, ynt=ynt: e.transpose(out=pt_[:, hp * 128:(hp + 1) * 128], in_=ynt[:, hp * 128:(hp + 1) * 128], identity=ident_b),
                          [ynb, CB], ptb)
                for hp in range(4):
                    si = sbi * 4 + hp
                    t1, t1b = t1t(hp)
                    S.add("dve", lambda e, t1=t1, pt_=pt_, hp=hp: e.tensor_scalar(out=t1[:], in0=pt_[:, hp * 128:(hp + 1) * 128], scalar1=pc("gnw", hp), scalar2=pc("gnb", hp),
                                                                            op0=ALU.mult, op1=ALU.add), ptb + [PB], [t1b])
                    S.add("pool", lambda e, t1=t1, si=si, csl=csl: e.tensor_tensor(out=t1[:], in0=t1[:], in1=bonus(si)[0][:, csl], op=ALU.add), [t1b, bonus(si)[1]], [t1b])
                    S.add("pool", lambda e, t1=t1, si=si, csl=csl: e.tensor_tensor(out=zr(si)[0][:, csl], in0=t1[:], in1=sgr(si)[0][:, csl], op=ALU.mult),
                          [t1b, sgr(si)[1]], [zr(si)[1]])
                if lc == NOC - 1:
                    dump("zr0", zr(sbi * 4)[0][:], [128, SBT], [zr(sbi * 4)[1]], BF16)
                if upto < 4:
                    continue
                am_i = 0 if lc == 0 else 1
                pObank, pObb = psA[0], [psA_b[0]]
                vprev, vprevb = Vpad(lc)
                vcur, vcurb = Vpad(lc + 1)
                for qp0 in (0, 2):
                    pts = psT[0]
                    hs = []
                    for qp in (qp0, qp0 + 1):
                        si = sbi * 4 + qp
                        g = qp // 2
                        for hh in range(2):
                            hd = qp * 2 + hh
                            pS, pSb_ = single(2)
                            qz, qzb = qTz(si * 2 + hh)
                            for kk_ in range(2):
                                ks = (lc + kk_) % NKS
                                S.add("pe", lambda e, pS=pS, qz=qz, g=g, ks=ks, kk_=kk_, csl=csl: e.matmul(
                                    pS[:, kk_ * 128:(kk_ + 1) * 128], lhsT=qz[:, csl], rhs=KTatt.t[g][:, ks * C:(ks + 1) * C], start=True, stop=True),
                                    [qzb, KTatt.b[g]], pSb_)
                            j4 = (qp - qp0) * 2 + hh
                            hs.append(dict(hd=hd, qp=qp, hh=hh, g=g, si=si, pS=pS, pSb=pSb_, sm=smt(j4), a=ast(j4), p3=p32(j4), pn=pnt(j4), pt=ptt(j4),
                                           ptsl=pts[:, j4 * 256:(j4 + 1) * 256]))
                    for h in hs:
                        S.add("dve", lambda e, sm=h["sm"][0], pS=h["pS"], am_i=am_i: e.scalar_tensor_tensor(
                            out=sm[:], in0=pS[:, 0:256], scalar=0.125, in1=amask.t[0][:, am_i, :], op0=ALU.mult, op1=ALU.add), h["pSb"] + [amask.b[0]], [h["sm"][1]])
                    afill()
                    for h in hs:
                        S.add("dve", lambda e, a_t=h["a"][0], sm=h["sm"][0]: e.tensor_reduce(out=a_t[:, 0:1], in_=sm[:], axis=AX.X, op=ALU.max), [h["sm"][1]], [h["a"][1]])
                    for h in hs:
                        S.add("dve", lambda e, a_t=h["a"][0], hd=h["hd"]: e.tensor_scalar(out=a_t[:, 1:2], in0=a_t[:, 0:1], scalar1=pc("sink", hd), scalar2=-1.0, op0=ALU.max, op1=ALU.mult),
                              [h["a"][1], PB], [h["a"][1]])
                    afill()
                    for h in hs:
                        S.add("act", lambda e, pp3=h["p3"][0], sm=h["sm"][0], a_t=h["a"][0]: e.activation(out=pp3[:], in_=sm[:], func=AF.Exp, bias=a_t[:, 1:2], accum_out=a_t[:, 2:3]),
                              [h["sm"][1], h["a"][1]], [h["p3"][1], h["a"][1]])
                    for h in hs:
                        S.add("act", lambda e, a_t=h["a"][0], hd=h["hd"]: e.activation(out=a_t[:, 3:4], in_=pc("sink", hd), func=AF.Exp, bias=a_t[:, 1:2]), [h["a"][1], PB], [h["a"][1]])
                    for h in hs:
                        S.add("dve", lambda e, a_t=h["a"][0]: e.tensor_tensor(out=a_t[:, 4:5], in0=a_t[:, 2:3], in1=a_t[:, 3:4], op=ALU.add), [h["a"][1]], [h["a"][1]])
                    for h in hs:
                        S.add("dve", lambda e, a_t=h["a"][0]: e.reciprocal(out=a_t[:, 5:6], in_=a_t[:, 4:5]), [h["a"][1]], [h["a"][1]])
                    afill()
                    for h in hs:
                        S.add("dve", lambda e, pn=h["pn"][0], pp3=h["p3"][0], a_t=h["a"][0]: e.tensor_scalar(out=pn[:], in0=pp3[:], scalar1=a_t[:, 5:6], scalar2=None, op0=ALU.mult),
                              [h["p3"][1], h["a"][1]], [h["pn"][1]])
                    for h in hs:
                        for kk_ in range(2):
                            S.add("pe", lambda e, ptsl=h["ptsl"], kk_=kk_, pn=h["pn"][0]: e.transpose(out=ptsl[:, kk_ * 128:(kk_ + 1) * 128], in_=pn[:, kk_ * 128:(kk_ + 1) * 128], identity=ident_b),
                                  [h["pn"][1], CB], psT_b[0])
                    afill()
                    for h in hs:
                        S.add("act", lambda e, pt2=h["pt"][0], ptsl=h["ptsl"]: e.activation(out=pt2[:], in_=v3(ptsl), func=AF.Copy), psT_b[0], [h["pt"][1]])
                    for qp in (qp0, qp0 + 1):
                        si = sbi * 4 + qp
                        g = qp // 2
                        pO, pOb = pObank[:, qp * 128:(qp + 1) * 128], pObb
                        n_ = 0
                        for h in [h for h in hs if h["qp"] == qp]:
                            pt2, pt2b = h["pt"]
                            hh = h["hh"]
                            for kk_, (vp, vpb) in enumerate([(vprev, vprevb), (vcur, vcurb)]):
                                S.add("pe", lambda e, pO=pO, vp=vp, g=g, hh=hh, pt2=pt2, kk_=kk_, n_=n_: e.matmul(
                                    pO[:, 0:128], lhsT=vp[:, g, hh * 64:hh * 64 + 128], rhs=pt2[:, kk_, :], start=(n_ == 0), stop=(n_ == 3)),
                                    [vpb, pt2b], pOb)
                                n_ += 1
                    afill()
                    for qp in (qp0, qp0 + 1):
                        si = sbi * 4 + qp
                        pO, pOb = pObank[:, qp * 128:(qp + 1) * 128], pObb
                        S.add("dve", lambda e, si=si, pO=pO, csl=csl: e.tensor_tensor(out=zatt(si)[0][:, csl], in0=pO[:, 0:128], in1=sga(si)[0][:, csl], op=ALU.mult),
                              pOb + [sga(si)[1]], [zatt(si)[1]])
                if lc == NOC - 1:
                    dump("za0", zatt(sbi * 4)[0][:], [128, SBT], [zatt(sbi * 4)[1]], BF16)
            if not own or upto < 5:
                continue
            drain(g1)
            fb_mode[0] = 0
            if sbi + 1 >= nsb:
                gfb = iter(())
            ffb = lambda: None
            mTt, mTb = mT()
            def load_j(j):
                return [ws_load([(lambda t: t[:, :, :], wbb[:, j * 128:(j + 1) * 128].rearrange("(bh p) c -> p bh c", p=128))]),
                        ws_load([(lambda t: t[:, :, :], wcols(O_GT + j * 128))]),
                        ws_load([(lambda t: t[:, :, :], wcols(O_GT + 1024 + j * 128))])]
            for j in range(8):
                cur_w = load_j(j)
                wt, wb = cur_w[0]
                pBr, pBrb = fullbank()
                pBa, pBab = fullbank()
                for hp in range(4):
                    si = sbi * 4 + hp
                    S.add("pe", lambda e, pBr=pBr, wt=wt, hp=hp, si=si: e.matmul(pBr[:, 0:SBT], lhsT=wt[:, hp, :], rhs=zr(si)[0][:], start=(hp == 0), stop=(hp == 3)),
                          wb + [zr(si)[1]], pBrb)
                for hp in range(4):
                    si = sbi * 4 + hp
                    S.add("pe", lambda e, pBa=pBa, wt=wt, hp=hp, si=si: e.matmul(pBa[:, 0:SBT], lhsT=wt[:, 4 + hp, :], rhs=zatt(si)[0][:], start=(hp == 0), stop=(hp == 3)),
                          wb + [zatt(si)[1]], pBab)
                halves = []
                for br in range(2):
                    wt2, wb2 = cur_w[1 + br]
                    pGt, pGtb = bankx()
                    for k in range(8):
                        S.add("pe", lambda e, pGt=pGt, k=k, wt2=wt2, hTt=hTt: e.matmul(
                            pGt[:, 0:SBT], lhsT=wt2[:, k, :], rhs=hTt[:, k, :], start=(k == 0), stop=(k == 7)), wb2 + [hTb], pGtb)
                    sg_, sgb_ = sgt(br)
                    S.add("act", lambda e, sg_=sg_, pGt=pGt: e.activation(out=sg_[:], in_=pGt[:, 0:SBT], func=AF.Sigmoid), pGtb, [sgb_])
                    halves.append((sg_, sgb_))
                m1, m1b = m12(0)
                m2, m2b = m12(1)
                S.add("dve", lambda e, m1=m1, pBr=pBr, sg_=halves[0][0]: e.tensor_tensor(out=m1[:], in0=pBr[:, 0:SBT], in1=sg_[:], op=ALU.mult), pBrb + [halves[0][1]], [m1b])
                S.add("dve", lambda e, m2=m2, pBa=pBa, sg_=halves[1][0]: e.tensor_tensor(out=m2[:], in0=pBa[:, 0:SBT], in1=sg_[:], op=ALU.mult), pBab + [halves[1][1]], [m2b])
                S.add("dve", lambda e, mTt=mTt, j=j, m1=m1, m2=m2: e.tensor_tensor(out=mTt[:, j, :], in0=m1[:], in1=m2[:], op=ALU.add), [m1b, m2b], [mTb])
                ffb()
            for c in range(CPS):
                gch = sbi * CPS + c
                lc = (sbi - own0) * CPS + c
                xr, xrb = xt(gch)
                dma("sp", xr[:], xw[gch * C:(gch + 1) * C, :], [], [xrb])
                for n in range(2):
                    pa, pab = fullbank()
                    for j in range(8):
                        S.add("pe", lambda e, pa=pa, j=j, n=n, mTt=mTt, c=c: e.matmul(
                            pa[:, 0:512], lhsT=mTt[:, j, c * C:(c + 1) * C], rhs=Wout.t[0][:, j, n * 512:(n + 1) * 512], start=(j == 0), stop=(j == 7)),
                            [mTb, Wout_b[j]], pab)
                    S.add("dve", lambda e, xr=xr, pa=pa, n=n: e.tensor_tensor(out=xr[:, n * 512:(n + 1) * 512], in0=pa[:, 0:512], in1=xr[:, n * 512:(n + 1) * 512], op=ALU.add),
                          pab + [xrb], [xrb])
                ft_, fb_ = fst(gch)
                hbt, hbb = hb(gch)
                rms_rstd(xr[:], [xrb], ft_, fb_, hbt[:], hbb)
                S.add("dve", lambda e, xr=xr, ft_=ft_: e.scalar_tensor_tensor(
                    out=xr[:], in0=xr[:], scalar=ft_[:, 2:3], in1=gfin.t[0][:], op0=ALU.mult, op1=ALU.mult), [xrb, fb_, gfin.b[0]], [xrb])
                dma("sp", out_d[lc * C:(lc + 1) * C, :], xr[:], [xrb], [], is_out=True)
            drain(gfb)

        for hp in range(4):
            dump(f"H{hp}", Ht(hp)[0][:], [128, 64], [Ht(hp)[1]])

        semnames = list(Sched.ENG) + [("dma", j) for j in range(Sched.NDMA)]
        sems = {}
        for sk in semnames:
            nm = sk if isinstance(sk, str) else f"dma{sk[1]}"
            sems[sk] = es.enter_context(nc.semaphore("s_" + nm))
        nc._sbuf_left = nc.sbuf_bytes_remaining
        block = es.enter_context(nc.Block())
        S.emit(nc, block, sems)
    nc._dbg_dumps = dump_d
    nc._sched_counts = dict(S.cnt)
    nc._sched_total = S.total
    return nc


def host_consts():
    s = np.arange(128)[:, None]
    t = np.arange(128)[None, :]
    cst = np.zeros((128, 9, 128), np.float32)
    cst[:, 0] = (s == t)
    cst[:, 1] = (s < t)
    cst[:, 2] = (s < t)
    cst[:, 3] = (s > t)
    cst[:, 4] = (s > t)
    cst[:, 5] = (s <= t)
    cst[:, 6] = (s <= t)
    cst[:, 7] = ((s // 64) == (t // 64))
    cst[:, 8] = 1.0
    return cst


def attn_masks(first):
    qi = np.arange(128)[:, None]
    kj = np.arange(256)[None, :]
    dist = qi + 128 - kj
    band = (dist >= 0) & (dist < 128)
    rest = np.where(band, 0.0, -1e30).astype(np.float32)
    fm = np.where(band & (kj >= 128), 0.0, -1e30).astype(np.float32)
    am = np.stack([fm if first else rest, rest], axis=1)
    return np.ascontiguousarray(am)


def pack_params(p):
    pp = np.zeros((128, NPP_IN), np.float32)

    def put(name, vec, n):
        v = np.asarray(vec, np.float32).reshape(n, 128)
        pp[:, PPI[name]:PPI[name] + n] = v.T
    put("mu", p["mu_shift"][0], 13)
    put("w0", p["w0"][0], 4)
    put("a0", p["a0"][0], 4)
    put("kk", p["k_k"][0], 4)
    put("ka", p["k_a"][0], 4)
    put("rk", p["r_k"][0], 4)
    put("gnw", p["gn_w"][0], 4)
    put("gnb", p["gn_b"][0], 4)
    bq = np.asarray(p["b_qkv"][0], np.float32)
    put("bq", bq[0:512], 4)
    bk = bq[512:640]
    pp[:, PPI["bk"] + 0] = np.concatenate([bk[0:64], bk[0:64]])
    pp[:, PPI["bk"] + 1] = np.concatenate([bk[64:128], bk[64:128]])
    sk = np.asarray(p["sinks"][0], np.float32)
    pp[:, PPI["sink"]:PPI["sink"] + 8] = np.broadcast_to(sk[None, :], (128, 8))
    wdi = np.zeros((128, 2, 512), np.float32)
    wdi[0:64, 0] = np.asarray(p["w_decay_up"][0], np.float32)
    wdi[64:128, 1] = np.asarray(p["w_iclr_up"][0], np.float32)
    common = {
        "w_in": np.ascontiguousarray(np.asarray(p["w_in"][0], np.float32)),
        "w_br": np.ascontiguousarray(np.stack([np.asarray(p["w_branch_rwkv"][0], np.float32),
                                               np.asarray(p["w_branch_att"][0], np.float32)])),
        "w_out": np.ascontiguousarray(np.asarray(p["w_out"][0], np.float32)),
        "wdi": np.ascontiguousarray(wdi),
        "pp": pp,
        "gpre_b": np.ascontiguousarray(np.broadcast_to(np.asarray(p["g_pre"][0], np.float32)[None], (128, D))),
        "gfin_b": np.ascontiguousarray(np.broadcast_to(np.asarray(p["g_final"], np.float32)[None], (128, D))),
        "bv_b": np.ascontiguousarray(np.broadcast_to(bq[640:768][None], (128, 128))),
        "cst": host_consts(),
    }
    return common


def kernel(**inputs):
    x = np.asarray(inputs["x"], np.float32)
    common = pack_params(inputs)
    nc = build()
    in_maps = []
    for c in range(NCORES):
        b, q = c // 4, c % 4
        end = (q + 1) * OWN_TOK
        xw = np.zeros((SEQ, D), np.float32)
        xw[SEQ - end:] = x[b, :end]
        m = dict(common)
        m["xw"] = xw
        m["amask"] = attn_masks(q == 0)
        in_maps.append(m)
    res = run_bass_kernel_spmd(nc, in_maps, core_ids=list(range(NCORES)))
    out = np.zeros((2, SEQ, D), np.float32)
    for c in range(NCORES):
        b, q = c // 4, c % 4
        out[b, q * OWN_TOK:(q + 1) * OWN_TOK] = res.results[c]["out"]
    return out
```

```python
import numpy as np
import concourse.bass as bass
import concourse.mybir as mybir
from concourse.bass_utils import run_bass_kernel_spmd

F32 = mybir.dt.float32
BF16 = mybir.dt.bfloat16
AF = mybir.ActivationFunctionType
ALU = mybir.AluOpType
AX = mybir.AxisListType

D = 1024
NCORES = 8
SEQ = 8192
OWN_TOK = 2048
C = 128
SBT = 256
CPS = SBT // C
RMS_EPS = 1e-6
GN_EPS = 64e-5
IN_COLS = 5504
O_SH = 0
O_GR = 1664
O_Q = 2176
O_K = 2688
O_V = 2816
O_GA = 2944
O_GT = 3456

PPI = {}
_n = 0
for _name, _cnt in [("mu", 13), ("w0", 4), ("a0", 4), ("kk", 4), ("ka", 4), ("rk", 4),
                    ("gnw", 4), ("gnb", 4), ("bq", 4), ("bk", 2), ("sink", 8)]:
    PPI[_name] = _n
    _n += _cnt
NPP_IN = _n
for _name, _cnt in [("omu", 13), ("nw0", 4), ("omka", 4), ("na0", 4)]:
    PPI[_name] = _n
    _n += _cnt
NPP = _n


class Buf:
    __slots__ = ("name", "w", "r", "excl")

    def __init__(self, name, excl=False):
        self.name = name
        self.w = None
        self.r = []
        self.excl = excl


class Sched:
    ENG = ("pe", "act", "dve", "pool", "sp")
    NDMA = 24

    def __init__(self, same_sync=True):
        self.ops = {e: [] for e in self.ENG}
        self.cnt = {e: 0 for e in self.ENG}
        self.waited = {e: {} for e in self.ENG}
        self.same_sync = same_sync
        self.dma_val = [0] * self.NDMA
        self.dma_rr = 0
        self.dma_rr2 = 0
        self.out_tokens = []

    def add(self, eng, fn, reads=(), writes=(), dma=False, is_out=False):
        self.total = getattr(self, "total", 0) + 1
        if not dma and self.total > getattr(self, "cut", 10 ** 9):
            return None
        deps = {}

        def need(tk, hard):
            d = deps.get(tk[0])
            if d is None:
                deps[tk[0]] = [tk[1], tk[2], hard]
            else:
                d[0] = max(d[0], tk[1])
                d[2] = d[2] or hard
        for b in reads:
            if b.w is not None:
                need(b.w, True)
            if b.excl:
                for tk in b.r:
                    need(tk, False)
        for b in writes:
            if b.w is not None:
                need(b.w, True)
            for tk in b.r:
                need(tk, False)
        waits = []
        for semkey, (val, src, hard) in deps.items():
            if src == eng and not isinstance(semkey, tuple):
                if eng in ("pe", "sp"):
                    continue
                if not hard or not self.same_sync:
                    continue
            if self.waited[eng].get(semkey, 0) >= val:
                continue
            self.waited[eng][semkey] = val
            waits.append((semkey, val))
        if dma:
            half = self.NDMA // 2
            if eng == "sp":
                j = self.dma_rr
                self.dma_rr = (self.dma_rr + 1) % half
            else:
                j = half + self.dma_rr2
                self.dma_rr2 = (self.dma_rr2 + 1) % half
            semkey = ("dma", j)
            if self.dma_val[j] > 0 and self.waited[eng].get(semkey, 0) < self.dma_val[j]:
                self.waited[eng][semkey] = self.dma_val[j]
                waits.append((semkey, self.dma_val[j]))
            self.dma_val[j] += 16
            tok = (semkey, self.dma_val[j], eng)
            inc = 16
        else:
            self.cnt[eng] += 1
            tok = (eng, self.cnt[eng], eng)
            inc = 1
        for b in reads:
            b.r.append(tok)
        for b in writes:
            b.w = tok
            b.r = []
        if is_out:
            self.out_tokens.append(tok)
        self.ops[eng].append((waits, fn, tok[0], inc))
        return tok

    def emit(self, nc, block, sems):
        engmap = {"pe": block.tensor, "act": block.scalar, "dve": block.vector,
                  "pool": block.gpsimd, "sp": block.sync}
        for e in self.ENG:
            ops = self.ops[e]
            final = list(self.out_tokens) if e == "sp" else ()

            def body(eng, ops=ops, final=final):
                for waits, fn, semkey, inc in ops:
                    for sk, val in waits:
                        eng.wait_ge(sems[sk], val)
                    fn(eng).then_inc(sems[semkey], inc)
                for tk in final:
                    eng.wait_ge(sems[tk[0]], tk[1])
                if final != ():
                    for j in range(self.NDMA):
                        if self.dma_val[j] > 0:
                            eng.wait_ge(sems[("dma", j)], self.dma_val[j])
            engmap[e](body)


def build(nsb=SEQ // SBT, nown=OWN_TOK // SBT, upto=99, dumps=(), same_sync=True, cut=None):
    from contextlib import ExitStack
    nc = bass.Bass("TRN2", target_bir_lowering=False)
    WT = nsb * SBT
    OT = nown * SBT
    NOC = nown * CPS
    S = Sched(same_sync=same_sync)
    if cut is not None:
        S.cut = cut

    def din(name, shape, dt=F32):
        return nc.dram_tensor(name, list(shape), dt, kind="ExternalInput").ap()

    xw = din("xw", [WT, D])
    w_in = din("w_in", [D, IN_COLS])
    w_br = din("w_br", [2, 512, D])
    w_out = din("w_out", [D, D])
    wdi = din("wdi", [128, 2, 512])
    pp_in = din("pp", [128, NPP_IN])
    gpre_d = din("gpre_b", [128, D])
    gfin_d = din("gfin_b", [128, D])
    bv_d = din("bv_b", [128, 128])
    cst_d = din("cst", [128, 9, 128])
    am_d = din("amask", [128, 2, 256])
    out_d = nc.dram_tensor("out", [OT, D], F32, kind="ExternalOutput").ap()
    wib = nc.dram_tensor("wib_scratch", [D, IN_COLS - O_GR], BF16).ap()
    wbb = nc.dram_tensor("wbb_scratch", [2 * 512, D], BF16).ap()
    dump_d = {}

    es = ExitStack()
    with es:
        def sb(name, shape, dt=F32):
            return es.enter_context(nc.sbuf_tensor(name, list(shape), dt))

        def ps(name, shape, dt=F32):
            return es.enter_context(nc.psum_tensor(name, list(shape), dt))

        class T:
            def __init__(self, name, shape, dt=F32, n=1):
                self.t = [sb(f"{name}{i}", shape, dt) for i in range(n)]
                self.b = [Buf(f"{name}{i}") for i in range(n)]
                self.n = n

            def __call__(self, i=0):
                return self.t[i % self.n], self.b[i % self.n]

        def dma(eng, out, in_, reads, writes, is_out=False):
            return S.add(eng, lambda e: e.dma_start(out=out, in_=in_), reads, writes, dma=True, is_out=is_out)

        def dump(name, ap, shape, reads, dt=F32):
            if name not in dumps:
                return
            dd = nc.dram_tensor("dbg_" + name, list(shape), dt, kind="ExternalOutput").ap()
            dump_d[name] = dd
            dma("sp", dd, ap, reads, [], is_out=True)

        xt = T("xt", [128, D], F32, 2)
        cst_b = T("cst_b", [128, 8, 128], BF16)
        cst2 = T("cst2", [128, 1, 128])
        amask = T("amask", [128, 2, 256])
        PP = T("PP", [128, NPP])
        gpre = T("gpre", [128, D])
        gfin = T("gfin", [128, D])
        bvb = T("bvb", [128, 128])
        Wdb = T("Wdb", [128, 3, 512], BF16)
        Wsh = T("Wsh", [128, 8, 1664], BF16)
        Wout = T("Wout", [128, 8, D], BF16)

        stg = xt.t[1][:, :].rearrange("p (a b) -> p a b", a=8)
        dma("sp", stg, cst_d[:, 0:8, :], [], [xt.b[1]])
        dma("sp", cst2.t[0][:], cst_d[:, 8:9, :], [], [cst2.b[0]])
        dma("sp", PP.t[0][:, 0:NPP_IN], pp_in, [], [PP.b[0]])
        dma("sp", gpre.t[0][:], gpre_d, [], [gpre.b[0]])
        stg_w = xt.t[0][:, :].rearrange("p (a b) -> p a b", a=2)
        dma("sp", stg_w, wdi, [], [xt.b[0]])
        S.add("act", lambda e: e.activation(out=Wdb.t[0][:, 0, :], in_=stg_w[:, 0, :], func=AF.Copy), [xt.b[0]], [Wdb.b[0]])
        S.add("act", lambda e: e.activation(out=Wdb.t[0][:, 2, :], in_=stg_w[:, 1, :], func=AF.Copy), [xt.b[0]], [Wdb.b[0]])
        S.add("dve", lambda e: e.tensor_tensor(out=Wdb.t[0][:, 1, :], in0=stg_w[:, 0, :], in1=Wdb.t[0][:, 0, :], op=ALU.subtract), [xt.b[0], Wdb.b[0]], [Wdb.b[0]])
        Wsh_b = [Buf(f"Wsh_k{k}") for k in range(8)]
        for k in range(8):
            S.add("pool", lambda e, k=k: e.dma_start(out=Wsh.t[0][:, k, :], in_=w_in[k * 128:(k + 1) * 128, O_SH:O_SH + 1664]),
                  [], [Wsh_b[k]], dma=True)
        dma("sp", amask.t[0][:], am_d, [], [amask.b[0]])
        dma("sp", gfin.t[0][:], gfin_d, [], [gfin.b[0]])
        dma("sp", bvb.t[0][:], bv_d, [], [bvb.b[0]])
        S.add("dve", lambda e: e.tensor_copy(out=cst_b.t[0][:], in_=stg), [xt.b[1]], [cst_b.b[0]])
        ident_b = cst_b.t[0][:, 0, :]
        mask4 = cst_b.t[0][:, 1:5, :]
        mle2 = cst_b.t[0][:, 5:7, :]
        bones_b = cst_b.t[0][:, 7, :]
        ones_f = cst2.t[0][:, 0, :]
        CB = cst_b.b[0]
        CF = cst2.b[0]
        ppt = PP.t[0]
        PB = PP.b[0]

        def pc(name, i=0):
            j = PPI[name] + i
            return ppt[:, j:j + 1]

        S.add("dve", lambda e: e.tensor_scalar(out=ppt[:, PPI["omu"]:PPI["omu"] + 13], in0=ppt[:, PPI["mu"]:PPI["mu"] + 13],
                                               scalar1=-1.0, scalar2=1.0, op0=ALU.mult, op1=ALU.add), [PB], [PB])
        S.add("dve", lambda e: e.tensor_scalar(out=ppt[:, PPI["nw0"]:PPI["nw0"] + 4], in0=ppt[:, PPI["w0"]:PPI["w0"] + 4],
                                               scalar1=-1.0, scalar2=None, op0=ALU.mult), [PB], [PB])
        S.add("dve", lambda e: e.tensor_scalar(out=ppt[:, PPI["omka"]:PPI["omka"] + 4], in0=ppt[:, PPI["ka"]:PPI["ka"] + 4],
                                               scalar1=-1.0, scalar2=1.0, op0=ALU.mult, op1=ALU.add), [PB], [PB])
        S.add("dve", lambda e: e.tensor_scalar(out=ppt[:, PPI["na0"]:PPI["na0"] + 4], in0=ppt[:, PPI["a0"]:PPI["a0"] + 4],
                                               scalar1=-1.0, scalar2=None, op0=ALU.mult), [PB], [PB])

        psA = [ps(f"psA{i}", [128, 512]) for i in range(2)]
        psA_b = [Buf(f"psA{i}", True) for i in range(2)]
        psT = [ps(f"psT{i}", [128, 1024], BF16) for i in range(2)]
        psT_b = [[Buf(f"psT{i}_{h}", True) for h in range(2)] for i in range(2)]
        psLU = [[ps(f"psL{i}", [128, 512]), ps(f"psU{i}", [128, 512])] for i in range(2)]
        psLU_b = [[[Buf(f"psLU{i}_{lu}_{s}", True) for s in range(4)] for lu in range(2)] for i in range(2)]
        arr = [0]
        prr = [0]
        srr = [0]

        fb_mode = [0]

        def fullbank():
            if fb_mode[0]:
                return psA[1], [psA_b[1]]
            i = arr[0]
            arr[0] = (i + 1) % 2
            return psA[i], [psA_b[i]]

        def pair(ns):
            r = prr[0]
            if (r % 4) + ns > 4:
                r = (r // 4 + 1) * 4
            r %= 8
            p, s = r // 4, r % 4
            prr[0] = (r + ns) % 8
            sl = slice(s * 128, (s + ns) * 128)
            return (psLU[p][0][:, sl], psLU_b[p][0][s:s + ns], psLU[p][1][:, sl], psLU_b[p][1][s:s + ns])

        brr = [0]

        def bankx():
            i = brr[0]
            brr[0] = (i + 1) % 4
            p, lu = i // 2, i % 2
            return psLU[p][lu], list(psLU_b[p][lu])

        def single(ns):
            bk, bb = bankx()
            return bk[:, 0:ns * 128], bb

        hb = T("hb", [128, D], BF16, 1)
        hT = T("hT", [128, 8, SBT], BF16, 2)
        st0 = T("st0", [128, 4], F32, 2)
        shwa = T("shwa", [128, SBT])
        shtmp = T("shtmp", [128, SBT], F32, 1)
        shr = T("shr", [128, SBT], F32, 2)
        shk = T("shk", [128, SBT], F32, 2)
        shv = T("shv", [128, SBT], F32, 2)
        tw = T("tw", [128, SBT])
        tw_hi = T("tw_hi", [128, SBT], BF16)
        tw_lo = T("tw_lo", [128, SBT], BF16)
        t_k2b = T("t_k2b", [128, SBT], BF16)
        t_rkb = T("t_rkb", [128, SBT], BF16)
        Hhl = T("Hhl", [128, 2, 64], BF16, 4)
        t_e1 = T("t_e1", [128, SBT])
        t_ew = T("t_ew", [128, SBT])
        t_a = T("t_a", [128, SBT])
        t_cs = T("t_cs", [128, SBT])
        t_csp = T("t_csp", [128, SBT])
        t_en = T("t_en", [128, SBT])
        t_ep = T("t_ep", [128, SBT])
        t_k2 = T("t_k2", [128, SBT])
        t_kkn = T("t_kkn", [128, SBT])
        t_ab = T("t_ab", [128, SBT])
        t_f = T("t_f", [128, SBT])
        gC = T("gC", [128, CPS], F32, 8)
        AR = T("AR", [128, CPS, 2, C], BF16, 4)
        BT = T("BT", [128, SBT], BF16, 4)
        KT = T("KT", [128, SBT], BF16, 4)
        vbf = T("vbf", [128, SBT], BF16, 4)
        bonus = T("bonus", [128, SBT], BF16, 4)
        tm = T("tm", [128, 4, 128], BF16, 4)
        PZ = T("PZ", [128, 3, SBT], BF16, 8)
        Hbfz = T("Hbfz", [128, 64], BF16, 8)
        qTz = T("qTz", [128, SBT], BF16, 8)
        NG = 4
        NMt = T("NM", [128, 2, 2, 128], BF16, 2 * NG)
        Mak = T("Mak", [128, 2, 128], BF16, NG)
        RBK = T("RBK", [128, 2, 2, 128], BF16, NG)
        PAIRS = [(0, 1), (2, 3)]
        Xtile = T("Xt", [128, 2, 2, 64], BF16, 2 * NG)
        ATbd = T("ATbd", [128, 128], BF16, NG)
        Gsb = T("Gsb", [128, 64], F32, NG)
        Ht = T("Hst", [128, 64], F32, 4)
        s1t = T("s1t", [128, 64], F32, NG)
        Wz = T("Wz", [128, 3, 64], BF16, NG)
        Wp = T("Wp", [128, 2, 64], BF16, NG)
        zlo = T("zlo", [128, 64], F32, NG)
        QT = T("QT", [128, 128], BF16, NG)
        prevcol = T("prevcol", [128, 13], F32, 2)
        kc = T("kcols", [128, 8])
        NWS = 5
        ws = T("ws", [128, 8, 128], BF16, NWS)
        ws_b2 = [Buf(f"ws_b2_{i}") for i in range(NWS)]
        wsrr = [0]
        sgr = T("sgr", [128, SBT], BF16, 4)
        sga = T("sga", [128, SBT], BF16, 4)
        NKS = 4
        KTatt = T("KTatt", [128, NKS * 128], BF16, 2)
        NV = 4
        Vpad = T("Vpad", [128, 2, 192], BF16, NV)
        ysqt = T("ysq", [128, 512], F32, 1)
        yn = T("yn", [128, 512], BF16, 1)
        gst = T("gst", [128, 6, 8], F32, 1)
        t1t = T("t1t", [128, 128], F32, 1)
        zr = T("zr", [128, SBT], BF16, 4)
        zatt = T("zatt", [128, SBT], BF16, 4)
        smt = T("smt", [128, 256], F32, 4)
        p32 = T("p32", [128, 256], F32, 4)
        pnt = T("pnt", [128, 256], BF16, 4)
        ptt = T("ptt", [128, 2, 128], BF16, 4)
        ast = T("ast", [128, 8], F32, 4)
        mT = T("mT", [128, 8, SBT], BF16, 1)
        sgt = T("sgt", [128, SBT], F32, 2)
        m12 = T("m12", [128, SBT], F32, 2)
        fst = T("fst", [128, 4], F32, 2)

        S.add("pool", lambda e: e.memset(prevcol.t[0][:], 0.0), [], [prevcol.b[0]])
        S.add("pool", lambda e: e.memset(prevcol.t[1][:], 0.0), [], [prevcol.b[1]])
        kct = kc.t[0]
        KB = kc.b[0]
        for j, val in enumerate([RMS_EPS, 1.0, -0.5, 1e-12, GN_EPS]):
            S.add("pool", lambda e, j=j, val=val: e.memset(kct[:, j:j + 1], val), [], [KB])
        eps_col = kct[:, 0:1]
        one_col = kct[:, 1:2]
        mhalf_col = kct[:, 2:3]
        tiny_col = kct[:, 3:4]
        gneps_col = kct[:, 4:5]
        for i in range(NG):
            S.add("pool", lambda e, i=i: e.memset(ATbd.t[i][:], 0.0), [], [ATbd.b[i]])
            S.add("pool", lambda e, i=i: e.memset(Wz.t[i][:], 0.0), [], [Wz.b[i]])
        for i in range(4):
            S.add("pool", lambda e, i=i: e.memset(Ht.t[i][:], 0.0), [], [Ht.b[i]])
        for i in range(NV):
            S.add("pool", lambda e, i=i: e.memset(Vpad.t[i][:], 0.0), [], [Vpad.b[i]])
        for i in range(8):
            S.add("pool", lambda e, i=i: e.memset(PZ.t[i][:], 0.0), [], [PZ.b[i]])
            S.add("pool", lambda e, i=i: e.memset(Hbfz.t[i][:], 0.0), [], [Hbfz.b[i]])
            S.add("pool", lambda e, i=i: e.memset(qTz.t[i][:], 0.0), [], [qTz.b[i]])
        for i in range(2):
            S.add("pool", lambda e, i=i: e.memset(KTatt.t[i][:], 0.0), [], [KTatt.b[i]])
        Wout_b = [Buf(f"wout{k}") for k in range(8)]
        for k in range(8):
            S.add("pool", lambda e, k=k: e.dma_start(out=Wout.t[0][:, k, :], in_=w_out[k * 128:(k + 1) * 128, :]),
                  [], [Wout_b[k]], dma=True)

        wib_b = [Buf(f"wib{k}") for k in range(8)]
        wbb_b = [Buf(f"wbb{k}") for k in range(8)]
        w_br_flat = w_br.rearrange("b r c -> (b r) c")
        for k in range(8):
            S.add("pool", lambda e, k=k: e.dma_start(out=wib[k * 128:(k + 1) * 128, :], in_=w_in[k * 128:(k + 1) * 128, O_GR:IN_COLS]),
                  [], [wib_b[k]], dma=True)
        for k in range(8):
            S.add("pool", lambda e, k=k: e.dma_start(out=wbb[k * 128:(k + 1) * 128, :], in_=w_br_flat[k * 128:(k + 1) * 128, :]),
                  [], [wbb_b[k]], dma=True)

        def bcm(ap2, n):
            a = ap2.ap
            return bass.AP(ap2.tensor, ap2.offset, [list(a[0]), [0, n], list(a[1])])

        def bcl(ap2, n):
            a = ap2.ap
            return bass.AP(ap2.tensor, ap2.offset, [list(a[0]), list(a[1]), [0, n]])

        def v3(ap, h=2):
            return ap.rearrange("p (h t) -> p h t", h=h)

        def rms_rstd(in_ap, in_bufs, stt, stb, junk_ap, junk_buf):
            S.add("act", lambda e: e.activation(out=junk_ap, in_=in_ap, func=AF.Square, accum_out=stt[:, 0:1]),
                  in_bufs, [junk_buf, stb])
            S.add("act", lambda e: e.activation(out=stt[:, 1:2], in_=stt[:, 0:1], func=AF.Ln, bias=eps_col, scale=1.0 / D),
                  [stb, KB], [stb])
            S.add("act", lambda e: e.activation(out=stt[:, 2:3], in_=stt[:, 1:2], func=AF.Exp, scale=-0.5), [stb], [stb])

        def ws_load(srcs):
            i = wsrr[0]
            wsrr[0] = (i + 1) % NWS
            t = ws.t[i]
            bufs = [ws.b[i], ws_b2[i]]
            for j, (dfn, dap) in enumerate(srcs):
                S.add("sp", lambda e, dfn=dfn, dap=dap, t=t: e.dma_start(out=dfn(t), in_=dap), wib_b + wbb_b, [bufs[j]], dma=True)
            return t, bufs[:len(srcs)]

        def wcols(c0, n=128):
            return wib[:, c0 - O_GR:c0 - O_GR + n].rearrange("(k p) c -> p k c", p=128)

        def proj_fm(hTt, hTb, wt, wbufs, ncols=SBT, col0=0):
            pa, pab = fullbank()
            for k in range(8):
                S.add("pe", lambda e, pa=pa, k=k, wt=wt, hTt=hTt: e.matmul(
                    pa[:, 0:ncols], lhsT=wt[:, k, :], rhs=hTt[:, k, col0:col0 + ncols], start=(k == 0), stop=(k == 7)),
                    wbufs + [hTb], pab)
            return pa, pab

        P_ = [slice(0, 64), slice(64, 128)]
        own0 = nsb - nown

        def stage2_header():
            swt, swb = shwa()
            twt, twb = tw()
            S.add("act", lambda e, twt=twt, swt=swt: e.activation(out=twt[0:64, :], in_=swt[0:64, :], func=AF.Exp, scale=2.0), [swb], [twb])
            S.add("act", lambda e, twt=twt: e.activation(out=twt[0:64, :], in_=twt[0:64, :], func=AF.Ln, bias=kct[0:64, 1:2]), [twb, KB], [twb])
            S.add("act", lambda e, twt=twt: e.activation(out=twt[0:64, :], in_=twt[0:64, :], func=AF.Exp, scale=-1.0), [twb], [twb])
            S.add("dve", lambda e, twt=twt: e.tensor_scalar(out=twt[0:64, :], in0=twt[0:64, :], scalar1=-2.0, scalar2=1.0, op0=ALU.mult, op1=ALU.add), [twb], [twb])
            S.add("act", lambda e, twt=twt, swt=swt: e.activation(out=twt[64:128, :], in_=swt[64:128, :], func=AF.Copy), [swb, twb], [twb])
            twh, twhb = tw_hi()
            twl, twlb = tw_lo()
            S.add("act", lambda e, twh=twh, twt=twt: e.activation(out=twh[:], in_=twt[:], func=AF.Copy), [twb], [twhb])
            S.add("dve", lambda e, twl=twl, twt=twt, twh=twh: e.tensor_tensor(out=twl[:], in0=twt[:], in1=twh[:], op=ALU.subtract), [twb, twhb], [twlb])
            return dict(twt=twt, twh=twh, twl=twl, twhb=twhb, twlb=twlb, twb=twb)
        def prep_hp(hp, sbi, own, twt=None, twh=None, twl=None, twhb=None, twlb=None, twb=None):
            si = sbi * 4 + hp
            rt, rb = shr(si)
            kt_, kb_ = shk(si)
            vt, vb = shv(si)
            pD, pDb = fullbank()
            hsl = slice(hp * 128, (hp + 1) * 128)
            S.add("pe", lambda e, pD=pD, hsl=hsl, twh=twh: e.matmul(pD[:, 0:SBT], lhsT=Wdb.t[0][:, 0, hsl], rhs=twh[:, :], start=True, stop=False),
                  [Wdb.b[0], twhb], pDb)
            S.add("pe", lambda e, pD=pD, hsl=hsl, twl=twl: e.matmul(pD[:, 0:SBT], lhsT=Wdb.t[0][:, 0, hsl], rhs=twl[:, :], start=False, stop=False),
                  [Wdb.b[0], twlb], pDb)
            S.add("pe", lambda e, pD=pD, hsl=hsl, twh=twh: e.matmul(pD[:, 0:SBT], lhsT=Wdb.t[0][:, 1, hsl], rhs=twh[:, :], start=False, stop=True),
                  [Wdb.b[0], twhb], pDb)
            e1, e1b = t_e1()
            ew, ewb = t_ew()
            at, ab_ = t_a()
            cs, csb = t_cs()
            csp, cspb = t_csp()
            en, enb = t_en()
            k2, k2b = t_k2()
            kkn, kknb = t_kkn()
            abt, abb = t_ab()
            ft, fb = t_f()
            S.add("act", lambda e, e1=e1, pD=pD, hp=hp: e.activation(out=e1[:], in_=pD[:, 0:SBT], func=AF.Exp, bias=pc("nw0", hp), scale=-1.0),
                  pDb + [PB], [e1b])
            pAa, pAb = fullbank()
            S.add("pe", lambda e, pAa=pAa, hsl=hsl, twh=twh: e.matmul(pAa[:, 0:SBT], lhsT=Wdb.t[0][:, 2, hsl], rhs=twh[:, :], start=True, stop=True),
                  [Wdb.b[0], twhb], pAb)
            S.add("act", lambda e, e1=e1: e.activation(out=e1[:], in_=e1[:], func=AF.Ln, bias=one_col), [e1b, KB], [e1b])
            S.add("act", lambda e, e1=e1, ew=ew: e.activation(out=ew[:], in_=e1[:], func=AF.Exp, bias=mhalf_col, scale=-1.0), [e1b, KB], [ewb])
            S.add("act", lambda e, at=at, pAa=pAa, hp=hp: e.activation(out=at[:], in_=pAa[:, 0:SBT], func=AF.Exp, bias=pc("na0", hp), scale=-1.0),
                  pAb + [PB], [ab_])
            yield
            S.add("act", lambda e, at=at: e.activation(out=at[:], in_=at[:], func=AF.Ln, bias=one_col), [ab_, KB], [ab_])
            S.add("act", lambda e, at=at: e.activation(out=at[:], in_=at[:], func=AF.Exp, scale=-1.0), [ab_], [ab_])
            for c in range(CPS):
                S.add("dve", lambda e, cs=cs, ew=ew, c=c: e.tensor_tensor_scan(
                    out=cs[:, c * C:(c + 1) * C], data0=ones_f, data1=ew[:, c * C:(c + 1) * C], initial=0.0,
                    op0=ALU.mult, op1=ALU.add), [ewb, CF], [csb])
            S.add("pool", lambda e, csp=csp, cs=cs, ew=ew: e.tensor_tensor(out=csp[:], in0=cs[:], in1=ew[:], op=ALU.subtract), [csb, ewb], [cspb])
            S.add("act", lambda e, en=en, cs=cs: e.activation(out=en[:], in_=cs[:], func=AF.Exp), [csb], [enb])
            S.add("act", lambda e, csp=csp: e.activation(out=csp[:], in_=csp[:], func=AF.Exp, scale=-1.0), [cspb], [cspb])
            gct, gcb = gC(si)
            S.add("act", lambda e, gct=gct, cs=cs: e.activation(
                out=gct[:, 0:CPS], in_=cs[:, :].rearrange("p (c t) -> p c t", t=C)[:, :, C - 1], func=AF.Exp, scale=-1.0), [csb], [gcb])
            yield
            k2h, k2hb = t_k2b()
            S.add("act", lambda e, k2h=k2h, kt_=kt_, hp=hp: e.activation(out=k2h[:], in_=kt_[:], func=AF.Square, scale=pc("kk", hp)), [kb_, PB], [k2hb])
            pS_, pSb = fullbank()
            S.add("pe", lambda e, pS_=pS_, k2h=k2h: e.matmul(pS_[:, 0:SBT], lhsT=bones_b, rhs=k2h[:], start=True, stop=True), [CB, k2hb], pSb)
            S.add("act", lambda e, k2=k2, pS_=pS_: e.activation(out=k2[:], in_=pS_[:, 0:SBT], func=AF.Ln, bias=tiny_col), pSb + [KB], [k2b])
            S.add("act", lambda e, k2=k2: e.activation(out=k2[:], in_=k2[:], func=AF.Exp, scale=-0.5), [k2b], [k2b])
            S.add("dve", lambda e, kkn=kkn, kt_=kt_, k2=k2, hp=hp: e.scalar_tensor_tensor(
                out=kkn[:], in0=kt_[:], scalar=pc("kk", hp), in1=k2[:], op0=ALU.mult, op1=ALU.mult), [kb_, k2b, PB], [kknb])
            yield
            ARt, ARb = AR(si)
            BTt, BTb = BT(si)
            KTt, KTb = KT(si)
            vbt, vbb = vbf(si)
            S.add("dve", lambda e, ARt=ARt, kkn=kkn, csp=csp: e.scalar_tensor_tensor(
                out=ARt[:, :, 0, :], in0=kkn[:, :].rearrange("p (c t) -> p c t", t=C), scalar=-1.0,
                in1=csp[:, :].rearrange("p (c t) -> p c t", t=C), op0=ALU.mult, op1=ALU.mult), [kknb, cspb], [ARb])
            S.add("pool", lambda e, abt=abt, kkn=kkn, at=at: e.tensor_tensor(out=abt[:], in0=kkn[:], in1=at[:], op=ALU.mult), [kknb, ab_], [abb])
            S.add("pool", lambda e, BTt=BTt, abt=abt, en=en: e.tensor_tensor(out=BTt[:], in0=abt[:], in1=en[:], op=ALU.mult), [abb, enb], [BTb])
            yield
            S.add("dve", lambda e, ft=ft, at=at, hp=hp: e.tensor_scalar(out=ft[:], in0=at[:], scalar1=pc("ka", hp), scalar2=pc("omka", hp),
                                                                    op0=ALU.mult, op1=ALU.add), [ab_, PB], [fb])
            S.add("pool", lambda e, ft=ft, kt_=kt_: e.tensor_tensor(out=ft[:], in0=kt_[:], in1=ft[:], op=ALU.mult), [kb_, fb], [fb])
            S.add("pool", lambda e, KTt=KTt, ft=ft, en=en: e.tensor_tensor(out=KTt[:], in0=ft[:], in1=en[:], op=ALU.mult), [fb, enb], [KTb])
            S.add("act", lambda e, vbt=vbt, vt=vt: e.activation(out=vbt[:], in_=vt[:], func=AF.Copy), [vb], [vbb])
            yield
            for hh in range(2):
                zt, zb = PZ(si * 2 + hh)
                S.add("pool", lambda e, zt=zt, ARt=ARt, hh=hh: e.tensor_copy(out=zt[P_[hh], 0, :].rearrange("p (c t) -> p c t", t=C), in_=ARt[P_[hh], :, 0, :]), [ARb], [zb])
                S.add("pool", lambda e, zt=zt, BTt=BTt, hh=hh: e.tensor_copy(out=zt[P_[hh], 1, :], in_=BTt[P_[hh], :]), [BTb], [zb])
                S.add("pool", lambda e, zt=zt, KTt=KTt, hh=hh: e.tensor_copy(out=zt[P_[hh], 2, :], in_=KTt[P_[hh], :]), [KTb], [zb])
            if own:
                ep, epb = t_ep()
                S.add("act", lambda e, ep=ep, cs=cs: e.activation(out=ep[:], in_=cs[:], func=AF.Exp, scale=-1.0), [csb], [epb])
                S.add("dve", lambda e, ARt=ARt, rt=rt, ep=ep: e.tensor_tensor(
                    out=ARt[:, :, 1, :], in0=rt[:, :].rearrange("p (c t) -> p c t", t=C),
                    in1=ep[:, :].rearrange("p (c t) -> p c t", t=C), op=ALU.mult), [rb, epb], [ARb])
                rkb_t, rkb_b = t_rkb()
                S.add("dve", lambda e, rkb_t=rkb_t, rt=rt, ft=ft, hp=hp: e.scalar_tensor_tensor(
                    out=rkb_t[:], in0=rt[:], scalar=pc("rk", hp), in1=ft[:], op0=ALU.mult, op1=ALU.mult), [rb, fb, PB], [rkb_b])
                pB_, pBb = fullbank()
                S.add("pe", lambda e, pB_=pB_, rkb_t=rkb_t: e.matmul(pB_[:, 0:SBT], lhsT=bones_b, rhs=rkb_t[:], start=True, stop=True), [CB, rkb_b], pBb)
                bnt, bnb = bonus(si)
                S.add("dve", lambda e, bnt=bnt, pB_=pB_, vt=vt: e.tensor_tensor(out=bnt[:], in0=pB_[:, 0:SBT], in1=vt[:], op=ALU.mult), pBb + [vb], [bnb])
            yield
        def proj_tile(sbi, ct, hTt, hTb):
            pa, pab = fullbank()
            for k in range(8):
                S.add("pe", lambda e, pa=pa, k=k, ct=ct, hTt=hTt: e.matmul(
                    pa[:, 0:SBT], lhsT=Wsh.t[0][:, k, ct * 128:(ct + 1) * 128], rhs=hTt[:, k, :],
                    start=(k == 0), stop=(k == 7)), [Wsh_b[k], hTb], pab)
            if ct == 12:
                dst, dstb = shwa()
            else:
                hp = ct % 4
                dst, dstb = (shr, shk, shv)[ct // 4](sbi * 4 + hp)
            tmp, tmpb = shtmp(ct)
            pcur, pcurb = prevcol(sbi)
            pnxt, pnxtb = prevcol(sbi + 1)
            S.add("act", lambda e, tmp=tmp, pa=pa, ct=ct: e.activation(
                out=tmp[:], in_=pa[:, 0:SBT], func=AF.Copy, scale=pc("omu", ct)), pab + [PB], [tmpb])
            S.add("act", lambda e, pa=pa, ct=ct, pnxt=pnxt: e.activation(
                out=pnxt[:, ct:ct + 1], in_=pa[:, SBT - 1:SBT], func=AF.Copy), pab, [pnxtb])
            S.add("dve", lambda e, dst=dst, pa=pa, tmp=tmp, ct=ct: e.scalar_tensor_tensor(
                out=dst[:, 1:SBT], in0=pa[:, 0:SBT - 1], scalar=pc("mu", ct), in1=tmp[:, 1:SBT],
                op0=ALU.mult, op1=ALU.add), pab + [tmpb, PB], [dstb])
            S.add("dve", lambda e, dst=dst, tmp=tmp, ct=ct, pcur=pcur: e.scalar_tensor_tensor(
                out=dst[:, 0:1], in0=pcur[:, ct:ct + 1], scalar=pc("mu", ct), in1=tmp[:, 0:1],
                op0=ALU.mult, op1=ALU.add), [pcurb, tmpb, PB], [dstb])

        sbst = {}

        def gen_first(sbi):
            own = sbi >= own0
            hTt, hTb = hT(sbi)
            for j in range(CPS):
                gc = sbi * CPS + j
                xtt, xtb = xt(gc)
                hbt, hbb = hb(gc)
                stt, stb = st0(gc)
                dma("sp", xtt[:], xw[gc * C:(gc + 1) * C, :], [], [xtb])
                rms_rstd(xtt[:], [xtb], stt, stb, hbt[:], hbb)
                S.add("dve", lambda e, xtt=xtt, stt=stt, hbt=hbt: e.scalar_tensor_tensor(
                    out=hbt[:], in0=xtt[:], scalar=stt[:, 2:3], in1=gpre.t[0][:], op0=ALU.mult, op1=ALU.mult),
                    [xtb, stb, gpre.b[0]], [hbb])
                yield
                pst = psT[0]
                pstb = psT_b[0]
                for k in range(8):
                    S.add("pe", lambda e, k=k, hbt=hbt, pst=pst: e.transpose(
                        out=pst[:, k * 128:(k + 1) * 128], in_=hbt[:, k * 128:(k + 1) * 128], identity=ident_b),
                        [hbb, CB], pstb)
                S.add("act", lambda e, pst=pst, hTt=hTt, j=j: e.activation(
                    out=hTt[:, :, j * C:(j + 1) * C], in_=pst[:, :].rearrange("p (k t) -> p k t", k=8), func=AF.Copy),
                    pstb, [hTb])
                yield
            proj_tile(sbi, 12, hTt, hTb)
            tw_ctx = stage2_header()
            sbst[sbi] = (tw_ctx, hTt, hTb)
            yield
            for hp in (0, 1):
                for q in range(3):
                    proj_tile(sbi, q * 4 + hp, hTt, hTb)
                    yield

        def gen_first_b(sbi):
            own = sbi >= own0
            tw_ctx, hTt, hTb = sbst[sbi]
            for hp in (0, 1):
                yield from prep_hp(hp, sbi, own, **tw_ctx)

        def gen_first_ab(sbi):
            yield from gen_first(sbi)
            yield from gen_first_b(sbi)

        def gen_second(sbi):
            own = sbi >= own0
            tw_ctx, hTt, hTb = sbst[sbi]
            for hp in (2, 3):
                for q in range(3):
                    proj_tile(sbi, q * 4 + hp, hTt, hTb)
                    yield
                yield from prep_hp(hp, sbi, own, **tw_ctx)

        def drain(g):
            for _ in g:
                pass

        def mkfill(g, n=1, units=None, slots=None):
            st = [0]

            def fill():
                if units is None:
                    k = n
                else:
                    i = st[0]
                    st[0] += 1
                    k = ((i + 1) * units) // slots - (i * units) // slots
                for _ in range(k):
                    try:
                        next(g)
                    except StopIteration:
                        return
            return fill

        drain(gen_first_ab(0))
        for sbi in range(nsb):
            own = sbi >= own0
            halo_sb = (sbi == own0 - 1)
            hTt, hTb = hT(sbi)
            def gen_ownproj(sbi=sbi, own=own, halo_sb=halo_sb, hTt=hTt, hTb=hTb):
                if own or halo_sb:
                    ncols, col0 = (SBT, 0) if own else (C, SBT - C)
                    kcol = ((sbi - own0) * CPS + 1) * C if own else 0
                    for g in range(2):
                        wt, wb = ws_load([(lambda t: t[:, :, 0:64], wcols(O_K + g * 64, 64)), (lambda t: t[:, :, 64:128], wcols(O_K + g * 64, 64))])
                        pa, pab = proj_fm(hTt, hTb, wt, wb, ncols, col0)
                        for cc in range(ncols // C):
                            ks = ((kcol // C) + cc) % NKS
                            S.add("act", lambda e, pa=pa, g=g, ks=ks, cc=cc: e.activation(
                                out=KTatt.t[g][:, ks * C:(ks + 1) * C], in_=pa[:, cc * C:(cc + 1) * C], func=AF.Identity, bias=pc("bk", g)), pab + [PB], [KTatt.b[g]])
                        yield
                    wt, wb = ws_load([(lambda t: t[:, :, :], wcols(O_V))])
                    for c in (range(CPS) if own else [CPS - 1]):
                        lc1 = (sbi - own0) * CPS + c + 1 if own else 0
                        pa, pab = fullbank()
                        for k in range(8):
                            S.add("pe", lambda e, pa=pa, k=k, wt=wt, hTt=hTt, c=c: e.matmul(
                                pa[:, 0:128], lhsT=hTt[:, k, c * C:(c + 1) * C], rhs=wt[:, k, :], start=(k == 0), stop=(k == 7)), wb + [hTb], pab)
                        vp, vpb = Vpad(lc1)
                        S.add("dve", lambda e, vp=vp, pa=pa: e.tensor_tensor(out=vp[:, :, 0:64], in0=v3(pa[:, 0:128]), in1=v3(bvb.t[0][:, :]), op=ALU.add),
                              pab + [bvb.b[0]], [vpb])
                        S.add("pool", lambda e, vp=vp: e.tensor_copy(out=vp[:, :, 128:192], in_=vp[:, :, 0:64]), [vpb], [vpb])
                        yield
                if own:
                    for ct in range(4):
                        si = sbi * 4 + ct
                        wt, wb = ws_load([(lambda t: t[:, :, :], wcols(O_GR + ct * 128))])
                        pa, pab = proj_fm(hTt, hTb, wt, wb)
                        S.add("act", lambda e, pa=pa, si=si: e.activation(out=sgr(si)[0][:], in_=pa[:, 0:SBT], func=AF.Silu), pab, [sgr(si)[1]])
                        yield
                        wt, wb = ws_load([(lambda t: t[:, :, :], wcols(O_Q + ct * 128))])
                        pa, pab = proj_fm(hTt, hTb, wt, wb)
                        for hh in range(2):
                            qz, qzb = qTz(si * 2 + hh)
                            S.add("act", lambda e, pa=pa, qz=qz, ct=ct, hh=hh: e.activation(
                                out=qz[P_[hh], :], in_=pa[P_[hh], 0:SBT], func=AF.Identity, bias=ppt[P_[hh], PPI["bq"] + ct:PPI["bq"] + ct + 1]),
                                pab + [PB], [qzb])
                        yield
                        wt, wb = ws_load([(lambda t: t[:, :, :], wcols(O_GA + ct * 128))])
                        pa, pab = proj_fm(hTt, hTb, wt, wb)
                        S.add("act", lambda e, pa=pa, si=si: e.activation(out=sga(si)[0][:], in_=pa[:, 0:SBT], func=AF.Silu), pab, [sga(si)[1]])
                        yield

                yield

            def emit_chunk_pairs(c, pairs, fill, own=own, sbi=sbi):
                gch = sbi * CPS + c
                csl = slice(c * C, (c + 1) * C)
                pYbank, pYbb = psA[0], [psA_b[0]]
                def mkctx(hp):
                    si = sbi * 4 + hp
                    gi = gch * 4 + hp
                    x = dict(hp=hp, si=si, gi=gi)
                    x["AR"], x["ARb"] = AR(si)
                    x["BT"], x["BTb"] = BT(si)
                    x["KT"], x["KTb"] = KT(si)
                    x["vb"], x["vbb"] = vbf(si)
                    x["zts"] = [PZ(si * 2 + hh) for hh in range(2)]
                    x["tm"], x["tmb"] = tm(gi)
                    return x

                def g_transposes(x, c=c, csl=csl):
                    pt_ = psT[1][:, 0:512]
                    ptb = psT_b[1]
                    srcs = [(x["AR"][:, c, 0, :], x["ARb"]), (x["BT"][:, csl], x["BTb"]), (x["KT"][:, csl], x["KTb"]), (x["vb"][:, csl], x["vbb"])]
                    for q, (sap, sbf) in enumerate(srcs):
                        S.add("pe", lambda e, pt_=pt_, q=q, sap=sap: e.transpose(out=pt_[:, q * 128:(q + 1) * 128], in_=sap, identity=ident_b),
                              [sbf, CB], ptb)
                    tmt = x["tm"]
                    S.add("act", lambda e, tmt=tmt, pt_=pt_: e.activation(out=tmt[:], in_=pt_.rearrange("p (q t) -> p q t", q=4), func=AF.Copy),
                          ptb, [x["tmb"]])

                def g_sprod_pe(x, c=c, csl=csl):
                    x["pS1"], x["pS1b"] = single(4)
                    x["pS2"], x["pS2b"] = single(2)
                    ARt, BTt = x["AR"], x["BT"]
                    for hh in range(2):
                        zt, zb = x["zts"][hh]
                        S.add("pe", lambda e, pS=x["pS1"], zt=zt, ARt=ARt, hh=hh: e.matmul(
                            pS[:, hh * 128:(hh + 1) * 128], lhsT=zt[:, 1, csl], rhs=ARt[:, c, 0, :], start=True, stop=True), [zb, x["ARb"]], x["pS1b"])
                    for hh in range(2):
                        zt, zb = x["zts"][hh]
                        S.add("pe", lambda e, pS=x["pS1"], zt=zt, BTt=BTt, hh=hh: e.matmul(
                            pS[:, (2 + hh) * 128:(3 + hh) * 128], lhsT=zt[:, 0, csl], rhs=BTt[:, csl], start=True, stop=True), [zb, x["BTb"]], x["pS1b"])
                    for hh in range(2):
                        zt, zb = x["zts"][hh]
                        S.add("pe", lambda e, pS=x["pS2"], zt=zt, ARt=ARt, hh=hh: e.matmul(
                            pS[:, hh * 128:(hh + 1) * 128], lhsT=zt[:, 2, csl], rhs=ARt[:, c, 0, :], start=True, stop=True), [zb, x["ARb"]], x["pS2b"])

                def g_sprod_evac(x):
                    gi = x["gi"]
                    nm, nmb = NMt(gi * 2)
                    mk, mkb = Mak(gi)
                    S.add("dve", lambda e, nm=nm, pS=x["pS1"]: e.tensor_tensor(out=nm[:, :, :, :].rearrange("p a h t -> p (a h) t"), in0=v3(pS, 4), in1=mask4, op=ALU.mult),
                          x["pS1b"] + [CB], [nmb])
                    S.add("dve", lambda e, mk=mk, pS=x["pS2"]: e.tensor_tensor(out=mk[:], in0=v3(pS, 2), in1=mask4[:, 0:2, :], op=ALU.mult),
                          x["pS2b"] + [CB], [mkb])
                    x["nm"], x["nmb"], x["mk"], x["mkb"] = nm, nmb, mk, mkb

                def g_r_pe(x, c=c, csl=csl):
                    x["pR"], x["pRb"] = single(4)
                    ARt = x["AR"]
                    for a_ in range(2):
                        for hh in range(2):
                            zt, zb = x["zts"][hh]
                            S.add("pe", lambda e, pR=x["pR"], zt=zt, ARt=ARt, hh=hh, a_=a_: e.matmul(
                                pR[:, (a_ * 2 + hh) * 128:(a_ * 2 + hh + 1) * 128], lhsT=zt[:, 1 + a_, csl], rhs=ARt[:, c, 1, :], start=True, stop=True),
                                [zb, x["ARb"]], x["pRb"])

                def g_r_evac(x, c=c, csl=csl):
                    rbk, rbkb = RBK(x["gi"])
                    for a_ in range(2):
                        S.add("dve", lambda e, rbk=rbk, pR=x["pR"], a_=a_: e.tensor_tensor(out=rbk[:, a_, :, :], in0=v3(pR[:, a_ * 256:(a_ + 1) * 256], 2), in1=mle2, op=ALU.mult),
                              x["pRb"] + [CB], [rbkb])
                    x["rbk"], x["rbkb"] = rbk, rbkb

                def g_pv_pe(x, c=c, csl=csl):
                    x["pV"], x["pVb"] = single(1)
                    mk, tmt = x["mk"], x["tm"]
                    for hh in range(2):
                        S.add("pe", lambda e, pV=x["pV"], hh=hh, mk=mk, tmt=tmt: e.matmul(
                            pV[:, hh * 64:(hh + 1) * 64], lhsT=mk[:, hh, :], rhs=tmt[:, 3, hh * 64:(hh + 1) * 64], start=True, stop=True), [x["mkb"], x["tmb"]], x["pVb"])

                def g_x0(x, c=c, csl=csl):
                    Xt, Xb = Xtile(x["gi"] * 2)
                    tmt = x["tm"]
                    S.add("pool", lambda e, Xt=Xt, tmt=tmt: e.tensor_copy(out=Xt[:, :, 0, :], in_=tmt[:, 0, :].rearrange("p (h k) -> p h k", h=2)), [x["tmb"]], [Xb])
                    S.add("act", lambda e, Xt=Xt, pV=x["pV"]: e.activation(out=Xt[:, :, 1, :], in_=pV[:, 0:128].rearrange("p (h k) -> p h k", h=2), func=AF.Copy),
                          x["pVb"], [Xb])
                    x["X"], x["Xb"] = Xt, Xb

                def g_level_pe(x, lv):
                    nm, nmb, Xt, Xb = x["nm"], x["nmb"], x["X"], x["Xb"]
                    x["pX"], x["pXb"] = single(2)
                    for hh in range(2):
                        S.add("pe", lambda e, pX=x["pX"], hh=hh, nm=nm, Xt=Xt: e.matmul(
                            pX[:, hh * 128:(hh + 1) * 128], lhsT=nm[:, 0, hh, :], rhs=Xt[:, hh, :, :].rearrange("p a k -> p (a k)"), start=True, stop=True),
                            [nmb, Xb], x["pXb"])
                    if lv < 6:
                        x["pNM"], x["pNMb"] = single(4)
                        for hh in range(2):
                            S.add("pe", lambda e, pN=x["pNM"], hh=hh, nm=nm: e.matmul(
                                pN[:, hh * 128:(hh + 1) * 128], lhsT=nm[:, 1, hh, :], rhs=nm[:, 0, hh, :], start=True, stop=True), [nmb], x["pNMb"])
                        if lv < 5:
                            for hh in range(2):
                                S.add("pe", lambda e, pN=x["pNM"], hh=hh, nm=nm: e.matmul(
                                    pN[:, (2 + hh) * 128:(3 + hh) * 128], lhsT=nm[:, 0, hh, :], rhs=nm[:, 1, hh, :], start=True, stop=True), [nmb], x["pNMb"])

                def g_level_evac(x, lv):
                    gi = x["gi"]
                    Xt, Xb = x["X"], x["Xb"]
                    Xn, Xnb = Xtile(gi * 2 + lv + 1)
                    S.add("dve", lambda e, Xn=Xn, pX=x["pX"], Xt=Xt: e.tensor_tensor(
                        out=Xn[:, :, :, :].rearrange("p h a k -> p h (a k)"), in0=v3(pX), in1=Xt[:, :, :, :].rearrange("p h a k -> p h (a k)"), op=ALU.add),
                        x["pXb"] + [Xb], [Xnb])
                    x["X"], x["Xb"] = Xn, Xnb
                    if lv < 6:
                        nn, nnb = NMt(gi * 2 + lv + 1)
                        w = 4 if lv < 5 else 2
                        S.add("act", lambda e, nn=nn, pN=x["pNM"], w=w: e.activation(
                            out=nn[:, :, :, :].rearrange("p a h t -> p (a h) t")[:, 0:w, :], in_=v3(pN[:, 0:w * 128], w), func=AF.Copy), x["pNMb"], [nnb])
                        x["nm"], x["nmb"] = nn, nnb

                def g_state(x, c=c, csl=csl, own=own, pYbank=(pYbank if own else None), pYbb=(pYbb if own else None)):
                    gi, hp, si = x["gi"], x["hp"], x["si"]
                    Xt, Xb, tmt, tmb = x["X"], x["Xb"], x["tm"], x["tmb"]
                    ARt, ARb = x["AR"], x["ARb"]
                    wz, wzb = Wz(gi)
                    S.add("pool", lambda e, wz=wz, Xt=Xt: e.tensor_copy(out=wz[:, 0::2, :], in_=Xt[:, :, 0, :]), [Xb], [wzb])
                    wzA = wz[:, 0:2, :].rearrange("p a k -> p (a k)")
                    wzB = wz[:, 1:3, :].rearrange("p a k -> p (a k)")
                    wp, wpb = Wp(gi)
                    S.add("pool", lambda e, wp=wp, Xt=Xt: e.tensor_copy(out=wp[:, :, :], in_=Xt[:, :, 0, :]), [Xb], [wpb])
                    pAT, pATb = single(1)
                    S.add("pe", lambda e, pAT=pAT, wp=wp, tmt=tmt: e.matmul(pAT[:, 0:128], lhsT=wp[:, :, :].rearrange("p a k -> p (a k)"), rhs=tmt[:, 1, :], start=True, stop=True),
                          [wpb, tmb], pATb)
                    atb, atbb = ATbd(gi)
                    for hh in range(2):
                        S.add("dve", lambda e, atb=atb, pAT=pAT, hh=hh: e.tensor_copy(
                            out=atb[P_[hh], hh * 64:(hh + 1) * 64], in_=pAT[P_[hh], hh * 64:(hh + 1) * 64]), pATb, [atbb])
                    pG, pGb = single(1)
                    pG2, pG2b = single(1)
                    S.add("pe", lambda e, pG=pG, tmt=tmt: e.matmul(pG[:, 0:128], lhsT=tmt[:, 2, :], rhs=tmt[:, 3, :], start=True, stop=True),
                          [tmb], pGb)
                    for hh in range(2):
                        S.add("pe", lambda e, pG2=pG2, Xt=Xt, tmt=tmt, hh=hh: e.matmul(
                            pG2[:, hh * 64:(hh + 1) * 64], lhsT=tmt[:, 1, :], rhs=Xt[:, hh, 1, :], start=True, stop=True), [Xb, tmb], pG2b)
                    gs, gsb_ = Gsb(gi)
                    for hh in range(2):
                        S.add("act", lambda e, gs=gs, pG=pG, hh=hh: e.activation(
                            out=gs[P_[hh], :], in_=pG[P_[hh], hh * 64:(hh + 1) * 64], func=AF.Copy), pGb, [gsb_])
                        S.add("dve", lambda e, gs=gs, pG2=pG2, hh=hh: e.tensor_tensor(
                            out=gs[P_[hh], :], in0=pG2[P_[hh], hh * 64:(hh + 1) * 64], in1=gs[P_[hh], :], op=ALU.add), pG2b + [gsb_], [gsb_])
                    Htt, Hb_ = Ht(hp)
                    gct, gcb = gC(si)
                    if own:
                        hbz = [Hbfz(gi * 2 + hh) for hh in range(2)]
                        for hh in range(2):
                            S.add("pool", lambda e, hz=hbz[hh][0], Htt=Htt, hh=hh: e.tensor_copy(out=hz[P_[hh], :], in_=Htt[P_[hh], :]), [Hb_], [hbz[hh][1]])
                    hhl, hhlb = Hhl(gi)
                    S.add("pool", lambda e, hhl=hhl, Htt=Htt: e.tensor_copy(out=hhl[:, 0, :], in_=Htt[:]), [Hb_], [hhlb])
                    S.add("pool", lambda e, hhl=hhl, Htt=Htt: e.tensor_tensor(out=hhl[:, 1, :], in0=Htt[:], in1=hhl[:, 0, :], op=ALU.subtract), [Hb_, hhlb], [hhlb])
                    pZ, pZb = single(1)
                    S.add("pe", lambda e, pZ=pZ, atb=atb, hhl=hhl: e.matmul(pZ[:, 0:128], lhsT=atb[:], rhs=hhl[:, :, :].rearrange("p a v -> p (a v)"), start=True, stop=True),
                          [atbb, hhlb], pZb)
                    s1, s1b = s1t(gi)
                    S.add("pool", lambda e, s1=s1, Htt=Htt, gs=gs: e.tensor_tensor(out=s1[:], in0=Htt[:], in1=gs[:], op=ALU.add), [Hb_, gsb_], [s1b])
                    S.add("pool", lambda e, s1=s1, gct=gct: e.tensor_scalar(out=s1[:], in0=s1[:], scalar1=gct[:, c:c + 1], scalar2=1.0, op0=ALU.mult, op1=ALU.mult),
                          [s1b, gcb], [s1b])
                    S.add("dve", lambda e, pZ=pZ, gct=gct, s1=s1: e.scalar_tensor_tensor(
                        out=s1[:], in0=pZ[:, 0:64], scalar=gct[:, c:c + 1], in1=s1[:], op0=ALU.mult, op1=ALU.add), pZb + [gcb, s1b], [s1b])
                    S.add("dve", lambda e, Htt=Htt, pZ=pZ, gct=gct, s1=s1: e.scalar_tensor_tensor(
                        out=Htt[:], in0=pZ[:, 64:128], scalar=gct[:, c:c + 1], in1=s1[:], op0=ALU.mult, op1=ALU.add), pZb + [gcb, s1b], [Hb_])
                    if own:
                        rbk, rbkb = x["rbk"], x["rbkb"]
                        qt_, qtb = QT(gi)
                        for hh, wzX in enumerate((wzA, wzB)):
                            pQ, pQb = single(1)
                            S.add("pe", lambda e, pQ=pQ, wzX=wzX, rbk=rbk, hh=hh: e.matmul(pQ[:, 0:128], lhsT=wzX, rhs=rbk[:, 0, hh, :], start=True, stop=True),
                                  [wzb, rbkb], pQb)
                            S.add("dve", lambda e, qt_=qt_, pQ=pQ, ARt=ARt, hh=hh: e.tensor_tensor(
                                out=qt_[P_[hh], :], in0=pQ[P_[hh], 0:128], in1=ARt[P_[hh], c, 1, :], op=ALU.add), pQb + [ARb], [qtb])
                        for hh in range(2):
                            pY, pYb = pYbank, pYbb
                            ysl = slice((hp * 2 + hh) * 64, (hp * 2 + hh + 1) * 64)
                            S.add("pe", lambda e, pY=pY, ysl=ysl, hh=hh, rbk=rbk, Xt=Xt: e.matmul(
                                pY[:, ysl], lhsT=rbk[:, 0, hh, :], rhs=Xt[:, hh, 1, :], start=True, stop=False), [rbkb, Xb], pYb)
                            S.add("pe", lambda e, pY=pY, ysl=ysl, hh=hh, rbk=rbk, tmt=tmt: e.matmul(
                                pY[:, ysl], lhsT=rbk[:, 1, hh, :], rhs=tmt[:, 3, hh * 64:(hh + 1) * 64], start=False, stop=False), [rbkb, tmb], pYb)
                            S.add("pe", lambda e, pY=pY, ysl=ysl, qt_=qt_, hz=hbz[hh][0]: e.matmul(
                                pY[:, ysl], lhsT=qt_[:, :], rhs=hz[:, :], start=False, stop=True), [qtb, hbz[hh][1]], pYb)

                for pr in pairs:
                    ctxs = [mkctx(hp) for hp in pr]
                    for x in ctxs:
                        g_transposes(x)
                    for x in ctxs:
                        g_sprod_pe(x)
                    for x in ctxs:
                        g_sprod_evac(x)
                    fill()
                    if own:
                        for x in ctxs:
                            g_r_pe(x)
                        for x in ctxs:
                            g_r_evac(x)
                    for x in ctxs:
                        g_pv_pe(x)
                    for x in ctxs:
                        g_x0(x)
                    fill()
                    for lv in range(7):
                        for x in ctxs:
                            g_level_pe(x, lv)
                        for x in ctxs:
                            g_level_evac(x, lv)
                        fill()
                    for x in ctxs:
                        g_state(x)
            if not own:
                if halo_sb:
                    drain(gen_ownproj())
                g2 = gen_second(sbi)
                f2 = mkfill(g2, units=23, slots=17)
                for c in range(CPS):
                    emit_chunk_pairs(c, [PAIRS[0]], f2)
                drain(g2)
                g1 = gen_first_ab(sbi + 1) if sbi + 1 < nsb else iter(())
                f1 = mkfill(g1, units=28, slots=17)
                for c in range(CPS):
                    emit_chunk_pairs(c, [PAIRS[1]], f1)
                drain(g1)
                continue
            fb_mode[0] = 1
            g1 = iter(())
            for c in range(CPS):
                gch = sbi * CPS + c
                csl = slice(c * C, (c + 1) * C)
                pYbank, pYbb = psA[0], [psA_b[0]]
                if c == 0:
                    g2 = gen_second(sbi)
                    emit_chunk_pairs(c, [PAIRS[0]], mkfill(g2, 2))
                    drain(g2)
                    g3 = gen_ownproj()
                    emit_chunk_pairs(c, [PAIRS[1]], mkfill(g3, 2))
                    drain(g3)
                else:
                    if c == 1 and sbi + 1 < nsb:
                        g1 = gen_first(sbi + 1)
                    emit_chunk_pairs(c, [PAIRS[0]], mkfill(g1, 1))
                    emit_chunk_pairs(c, [PAIRS[1]], mkfill(g1, 1))
                afill = lambda: None
                if c == CPS - 1 and sbi + 1 < nsb:
                    drain(g1)
                    gfb = gen_first_b(sbi + 1)
                    afill = mkfill(gfb, units=13, slots=10)
                lc = (sbi - own0) * CPS + c
                g_t, g_b = gst()
                pY, pYb = pYbank, pYbb
                yq, yqb = ysqt(0)
                S.add("dve", lambda e, g_t=g_t, pY=pY: e.tensor_reduce(out=g_t[:, 0, :], in_=v3(pY[:, 0:512], 8), axis=AX.X, op=ALU.add), pYb, [g_b])
                S.add("act", lambda e, yq=yq, pY=pY: e.activation(out=yq[:], in_=pY[:, 0:512], func=AF.Square), pYb, [yqb])
                S.add("dve", lambda e, g_t=g_t, yq=yq: e.tensor_reduce(out=g_t[:, 1, :], in_=v3(yq[:, :], 8), axis=AX.X, op=ALU.add), [yqb], [g_b])
                S.add("dve", lambda e, g_t=g_t: e.tensor_scalar(out=g_t[:, 2, :], in0=g_t[:, 0, :], scalar1=1.0 / 64, scalar2=None, op0=ALU.mult), [g_b], [g_b])
                S.add("dve", lambda e, g_t=g_t: e.tensor_tensor(out=g_t[:, 3, :], in0=g_t[:, 2, :], in1=g_t[:, 2, :], op=ALU.mult), [g_b], [g_b])
                S.add("dve", lambda e, g_t=g_t: e.scalar_tensor_tensor(out=g_t[:, 4, :], in0=g_t[:, 1, :], scalar=1.0 / 64, in1=g_t[:, 3, :],
                                                                       op0=ALU.mult, op1=ALU.subtract), [g_b], [g_b])
                S.add("act", lambda e, g_t=g_t: e.activation(out=g_t[:, 5, :], in_=g_t[:, 4, :], func=AF.Ln, bias=gneps_col), [g_b, KB], [g_b])
                S.add("act", lambda e, g_t=g_t: e.activation(out=g_t[:, 5, :], in_=g_t[:, 5, :], func=AF.Exp, scale=-0.5), [g_b], [g_b])
                ynt, ynb = yn()
                S.add("dve", lambda e, yq=yq, pY=pY, g_t=g_t: e.tensor_tensor(
                    out=v3(yq[:, :], 8), in0=v3(pY[:, 0:512], 8), in1=bcl(g_t[:, 2, :], 64), op=ALU.subtract), pYb + [g_b, yqb], [yqb])
                S.add("pool", lambda e, ynt=ynt, yq=yq, g_t=g_t: e.tensor_tensor(
                    out=v3(ynt[:, :], 8), in0=v3(yq[:, :], 8), in1=bcl(g_t[:, 5, :], 64), op=ALU.mult), [yqb, g_b], [ynb])
                pt_ = psT[1][:, 0:512]
                ptb = psT_b[1]
                for hp in range(4):
                    S.add("pe", lambda e, pt_=pt_, hp=hp, ynt=ynt: e.transpose(out=pt_[:, hp * 128:(hp + 1) * 128], in_=ynt[:, hp * 128:(hp + 1) * 128], identity=ident_b),
                          [ynb, CB], ptb)
                for hp in range(4):
                    si = sbi * 4 + hp
                    t1, t1b = t1t(hp)
                    S.add("dve", lambda e, t1=t1, pt_=pt_, hp=hp: e.tensor_scalar(out=t1[:], in0=pt_[:, hp * 128:(hp + 1) * 128], scalar1=pc("gnw", hp), scalar2=pc("gnb", hp),
                                                                            op0=ALU.mult, op1=ALU.add), ptb + [PB], [t1b])
                    S.add("pool", lambda e, t1=t1, si=si, csl=csl: e.tensor_tensor(out=t1[:], in0=t1[:], in1=bonus(si)[0][:, csl], op=ALU.add), [t1b, bonus(si)[1]], [t1b])
                    S.add("pool", lambda e, t1=t1, si=si, csl=csl: e.tensor_tensor(out=zr(si)[0][:, csl], in0=t1[:], in1=sgr(si)[0][:, csl], op=ALU.mult),
                          [t1b, sgr(si)[1]], [zr(si)[1]])
                if lc == NOC - 1:
                    dump("zr0", zr(sbi * 4)[0][:], [128, SBT], [zr(sbi * 4)[1]], BF16)
                if upto < 4:
                    continue
                am_i = 0 if lc == 0 else 1
                pObank, pObb = psA[0], [psA_b[0]]
                vprev, vprevb = Vpad(lc)
                vcur, vcurb = Vpad(lc + 1)
                for qp0 in (0, 2):
                    pts = psT[0]
                    hs = []
                    for qp in (qp0, qp0 + 1):
                        si = sbi * 4 + qp
                        g = qp // 2
                        for hh in range(2):
                            hd = qp * 2 + hh
                            pS, pSb_ = single(2)
                            qz, qzb = qTz(si * 2 + hh)
                            for kk_ in range(2):
                                ks = (lc + kk_) % NKS
                                S.add("pe", lambda e, pS=pS, qz=qz, g=g, ks=ks, kk_=kk_, csl=csl: e.matmul(
                                    pS[:, kk_ * 128:(kk_ + 1) * 128], lhsT=qz[:, csl], rhs=KTatt.t[g][:, ks * C:(ks + 1) * C], start=True, stop=True),
                                    [qzb, KTatt.b[g]], pSb_)
                            j4 = (qp - qp0) * 2 + hh
                            hs.append(dict(hd=hd, qp=qp, hh=hh, g=g, si=si, pS=pS, pSb=pSb_, sm=smt(j4), a=ast(j4), p3=p32(j4), pn=pnt(j4), pt=ptt(j4),
                                           ptsl=pts[:, j4 * 256:(j4 + 1) * 256]))
                    for h in hs:
                        S.add("dve", lambda e, sm=h["sm"][0], pS=h["pS"], am_i=am_i: e.scalar_tensor_tensor(
                            out=sm[:], in0=pS[:, 0:256], scalar=0.125, in1=amask.t[0][:, am_i, :], op0=ALU.mult, op1=ALU.add), h["pSb"] + [amask.b[0]], [h["sm"][1]])
                    afill()
                    for h in hs:
                        S.add("dve", lambda e, a_t=h["a"][0], sm=h["sm"][0]: e.tensor_reduce(out=a_t[:, 0:1], in_=sm[:], axis=AX.X, op=ALU.max), [h["sm"][1]], [h["a"][1]])
                    for h in hs:
                        S.add("dve", lambda e, a_t=h["a"][0], hd=h["hd"]: e.tensor_scalar(out=a_t[:, 1:2], in0=a_t[:, 0:1], scalar1=pc("sink", hd), scalar2=-1.0, op0=ALU.max, op1=ALU.mult),
                              [h["a"][1], PB], [h["a"][1]])
                    afill()
                    for h in hs:
                        S.add("act", lambda e, pp3=h["p3"][0], sm=h["sm"][0], a_t=h["a"][0]: e.activation(out=pp3[:], in_=sm[:], func=AF.Exp, bias=a_t[:, 1:2], accum_out=a_t[:, 2:3]),
                              [h["sm"][1], h["a"][1]], [h["p3"][1], h["a"][1]])
                    for h in hs:
                        S.add("act", lambda e, a_t=h["a"][0], hd=h["hd"]: e.activation(out=a_t[:, 3:4], in_=pc("sink", hd), func=AF.Exp, bias=a_t[:, 1:2]), [h["a"][1], PB], [h["a"][1]])
                    for h in hs:
                        S.add("dve", lambda e, a_t=h["a"][0]: e.tensor_tensor(out=a_t[:, 4:5], in0=a_t[:, 2:3], in1=a_t[:, 3:4], op=ALU.add), [h["a"][1]], [h["a"][1]])
                    for h in hs:
                        S.add("dve", lambda e, a_t=h["a"][0]: e.reciprocal(out=a_t[:, 5:6], in_=a_t[:, 4:5]), [h["a"][1]], [h["a"][1]])
                    afill()
                    for h in hs:
                        S.add("dve", lambda e, pn=h["pn"][0], pp3=h["p3"][0], a_t=h["a"][0]: e.tensor_scalar(out=pn[:], in0=pp3[:], scalar1=a_t[:, 5:6], scalar2=None, op0=ALU.mult),
                              [h["p3"][1], h["a"][1]], [h["pn"][1]])
                    for h in hs:
                        for kk_ in range(2):
                            S.add("pe", lambda e, ptsl=h["ptsl"], kk_=kk_, pn=h["pn"][0]: e.transpose(out=ptsl[:, kk_ * 128:(kk_ + 1) * 128], in_=pn[:, kk_ * 128:(kk_ + 1) * 128], identity=ident_b),
                                  [h["pn"][1], CB], psT_b[0])
                    afill()
                    for h in hs:
                        S.add("act", lambda e, pt2=h["pt"][0], ptsl=h["ptsl"]: e.activation(out=pt2[:], in_=v3(ptsl), func=AF.Copy), psT_b[0], [h["pt"][1]])
                    for qp in (qp0, qp0 + 1):
                        si = sbi * 4 + qp
                        g = qp // 2
                        pO, pOb = pObank[:, qp * 128:(qp + 1) * 128], pObb
                        n_ = 0
                        for h in [h for h in hs if h["qp"] == qp]:
                            pt2, pt2b = h["pt"]
                            hh = h["hh"]
                            for kk_, (vp, vpb) in enumerate([(vprev, vprevb), (vcur, vcurb)]):
                                S.add("pe", lambda e, pO=pO, vp=vp, g=g, hh=hh, pt2=pt2, kk_=kk_, n_=n_: e.matmul(
                                    pO[:, 0:128], lhsT=vp[:, g, hh * 64:hh * 64 + 128], rhs=pt2[:, kk_, :], start=(n_ == 0), stop=(n_ == 3)),
                                    [vpb, pt2b], pOb)
                                n_ += 1
                    afill()
                    for qp in (qp0, qp0 + 1):
                        si = sbi * 4 + qp
                        pO, pOb = pObank[:, qp * 128:(qp + 1) * 128], pObb
                        S.add("dve", lambda e, si=si, pO=pO, csl=csl: e.tensor_tensor(out=zatt(si)[0][:, csl], in0=pO[:, 0:128], in1=sga(si)[0][:, csl], op=ALU.mult),
                              pOb + [sga(si)[1]], [zatt(si)[1]])
                if lc == NOC - 1:
                    dump("za0", zatt(sbi * 4)[0][:], [128, SBT], [zatt(sbi * 4)[1]], BF16)
            if not own or upto < 5:
                continue
            drain(g1)
            fb_mode[0] = 0
            if sbi + 1 >= nsb:
                gfb = iter(())
            ffb = lambda: None
            mTt, mTb = mT()
            def load_j(j):
                return [ws_load([(lambda t: t[:, :, :], wbb[:, j * 128:(j + 1) * 128].rearrange("(bh p) c -> p bh c", p=128))]),
                        ws_load([(lambda t: t[:, :, :], wcols(O_GT + j * 128))]),
                        ws_load([(lambda t: t[:, :, :], wcols(O_GT + 1024 + j * 128))])]
            for j in range(8):
                cur_w = load_j(j)
                wt, wb = cur_w[0]
                pBr, pBrb = fullbank()
                pBa, pBab = fullbank()
                for hp in range(4):
                    si = sbi * 4 + hp
                    S.add("pe", lambda e, pBr=pBr, wt=wt, hp=hp, si=si: e.matmul(pBr[:, 0:SBT], lhsT=wt[:, hp, :], rhs=zr(si)[0][:], start=(hp == 0), stop=(hp == 3)),
                          wb + [zr(si)[1]], pBrb)
                for hp in range(4):
                    si = sbi * 4 + hp
                    S.add("pe", lambda e, pBa=pBa, wt=wt, hp=hp, si=si: e.matmul(pBa[:, 0:SBT], lhsT=wt[:, 4 + hp, :], rhs=zatt(si)[0][:], start=(hp == 0), stop=(hp == 3)),
                          wb + [zatt(si)[1]], pBab)
                halves = []
                for br in range(2):
                    wt2, wb2 = cur_w[1 + br]
                    pGt, pGtb = bankx()
                    for k in range(8):
                        S.add("pe", lambda e, pGt=pGt, k=k, wt2=wt2, hTt=hTt: e.matmul(
                            pGt[:, 0:SBT], lhsT=wt2[:, k, :], rhs=hTt[:, k, :], start=(k == 0), stop=(k == 7)), wb2 + [hTb], pGtb)
                    sg_, sgb_ = sgt(br)
                    S.add("act", lambda e, sg_=sg_, pGt=pGt: e.activation(out=sg_[:], in_=pGt[:, 0:SBT], func=AF.Sigmoid), pGtb, [sgb_])
                    halves.append((sg_, sgb_))
                m1, m1b = m12(0)
                m2, m2b = m12(1)
                S.add("dve", lambda e, m1=m1, pBr=pBr, sg_=halves[0][0]: e.tensor_tensor(out=m1[:], in0=pBr[:, 0:SBT], in1=sg_[:], op=ALU.mult), pBrb + [halves[0][1]], [m1b])
                S.add("dve", lambda e, m2=m2, pBa=pBa, sg_=halves[1][0]: e.tensor_tensor(out=m2[:], in0=pBa[:, 0:SBT], in1=sg_[:], op=ALU.mult), pBab + [halves[1][1]], [m2b])
                S.add("dve", lambda e, mTt=mTt, j=j, m1=m1, m2=m2: e.tensor_tensor(out=mTt[:, j, :], in0=m1[:], in1=m2[:], op=ALU.add), [m1b, m2b], [mTb])
                ffb()
            for c in range(CPS):
                gch = sbi * CPS + c
                lc = (sbi - own0) * CPS + c
                xr, xrb = xt(gch)
                dma("sp", xr[:], xw[gch * C:(gch + 1) * C, :], [], [xrb])
                for n in range(2):
                    pa, pab = fullbank()
                    for j in range(8):
                        S.add("pe", lambda e, pa=pa, j=j, n=n, mTt=mTt, c=c: e.matmul(
                            pa[:, 0:512], lhsT=mTt[:, j, c * C:(c + 1) * C], rhs=Wout.t[0][:, j, n * 512:(n + 1) * 512], start=(j == 0), stop=(j == 7)),
                            [mTb, Wout_b[j]], pab)
                    S.add("dve", lambda e, xr=xr, pa=pa, n=n: e.tensor_tensor(out=xr[:, n * 512:(n + 1) * 512], in0=pa[:, 0:512], in1=xr[:, n * 512:(n + 1) * 512], op=ALU.add),
                          pab + [xrb], [xrb])
                ft_, fb_ = fst(gch)
                hbt, hbb = hb(gch)
                rms_rstd(xr[:], [xrb], ft_, fb_, hbt[:], hbb)
                S.add("dve", lambda e, xr=xr, ft_=ft_: e.scalar_tensor_tensor(
                    out=xr[:], in0=xr[:], scalar=ft_[:, 2:3], in1=gfin.t[0][:], op0=ALU.mult, op1=ALU.mult), [xrb, fb_, gfin.b[0]], [xrb])
                dma("sp", out_d[lc * C:(lc + 1) * C, :], xr[:], [xrb], [], is_out=True)
            drain(gfb)

        for hp in range(4):
            dump(f"H{hp}", Ht(hp)[0][:], [128, 64], [Ht(hp)[1]])

        semnames = list(Sched.ENG) + [("dma", j) for j in range(Sched.NDMA)]
        sems = {}
        for sk in semnames:
            nm = sk if isinstance(sk, str) else f"dma{sk[1]}"
            sems[sk] = es.enter_context(nc.semaphore("s_" + nm))
        nc._sbuf_left = nc.sbuf_bytes_remaining
        block = es.enter_context(nc.Block())
        S.emit(nc, block, sems)
    nc._dbg_dumps = dump_d
    nc._sched_counts = dict(S.cnt)
    nc._sched_total = S.total
    return nc


def host_consts():
    s = np.arange(128)[:, None]
    t = np.arange(128)[None, :]
    cst = np.zeros((128, 9, 128), np.float32)
    cst[:, 0] = (s == t)
    cst[:, 1] = (s < t)
    cst[:, 2] = (s < t)
    cst[:, 3] = (s > t)
    cst[:, 4] = (s > t)
    cst[:, 5] = (s <= t)
    cst[:, 6] = (s <= t)
    cst[:, 7] = ((s // 64) == (t // 64))
    cst[:, 8] = 1.0
    return cst


def attn_masks(first):
    qi = np.arange(128)[:, None]
    kj = np.arange(256)[None, :]
    dist = qi + 128 - kj
    band = (dist >= 0) & (dist < 128)
    rest = np.where(band, 0.0, -1e30).astype(np.float32)
    fm = np.where(band & (kj >= 128), 0.0, -1e30).astype(np.float32)
    am = np.stack([fm if first else rest, rest], axis=1)
    return np.ascontiguousarray(am)


def pack_params(p):
    pp = np.zeros((128, NPP_IN), np.float32)

    def put(name, vec, n):
        v = np.asarray(vec, np.float32).reshape(n, 128)
        pp[:, PPI[name]:PPI[name] + n] = v.T
    put("mu", p["mu_shift"][0], 13)
    put("w0", p["w0"][0], 4)
    put("a0", p["a0"][0], 4)
    put("kk", p["k_k"][0], 4)
    put("ka", p["k_a"][0], 4)
    put("rk", p["r_k"][0], 4)
    put("gnw", p["gn_w"][0], 4)
    put("gnb", p["gn_b"][0], 4)
    bq = np.asarray(p["b_qkv"][0], np.float32)
    put("bq", bq[0:512], 4)
    bk = bq[512:640]
    pp[:, PPI["bk"] + 0] = np.concatenate([bk[0:64], bk[0:64]])
    pp[:, PPI["bk"] + 1] = np.concatenate([bk[64:128], bk[64:128]])
    sk = np.asarray(p["sinks"][0], np.float32)
    pp[:, PPI["sink"]:PPI["sink"] + 8] = np.broadcast_to(sk[None, :], (128, 8))
    wdi = np.zeros((128, 2, 512), np.float32)
    wdi[0:64, 0] = np.asarray(p["w_decay_up"][0], np.float32)
    wdi[64:128, 1] = np.asarray(p["w_iclr_up"][0], np.float32)
    common = {
        "w_in": np.ascontiguousarray(np.asarray(p["w_in"][0], np.float32)),
        "w_br": np.ascontiguousarray(np.stack([np.asarray(p["w_branch_rwkv"][0], np.float32),
                                               np.asarray(p["w_branch_att"][0], np.float32)])),
        "w_out": np.ascontiguousarray(np.asarray(p["w_out"][0], np.float32)),
        "wdi": np.ascontiguousarray(wdi),
        "pp": pp,
        "gpre_b": np.ascontiguousarray(np.broadcast_to(np.asarray(p["g_pre"][0], np.float32)[None], (128, D))),
        "gfin_b": np.ascontiguousarray(np.broadcast_to(np.asarray(p["g_final"], np.float32)[None], (128, D))),
        "bv_b": np.ascontiguousarray(np.broadcast_to(bq[640:768][None], (128, 128))),
        "cst": host_consts(),
    }
    return common


def kernel(**inputs):
    x = np.asarray(inputs["x"], np.float32)
    common = pack_params(inputs)
    nc = build()
    in_maps = []
    for c in range(NCORES):
        b, q = c // 4, c % 4
        end = (q + 1) * OWN_TOK
        xw = np.zeros((SEQ, D), np.float32)
        xw[SEQ - end:] = x[b, :end]
        m = dict(common)
        m["xw"] = xw
        m["amask"] = attn_masks(q == 0)
        in_maps.append(m)
    res = run_bass_kernel_spmd(nc, in_maps, core_ids=list(range(NCORES)))
    out = np.zeros((2, SEQ, D), np.float32)
    for c in range(NCORES):
        b, q = c // 4, c % 4
        out[b, q * OWN_TOK:(q + 1) * OWN_TOK] = res.results[c]["out"]
    return out
```

```python
import numpy as np
import concourse.bass as bass
import concourse.mybir as mybir
from concourse.bass_utils import run_bass_kernel_spmd

F32 = mybir.dt.float32
BF16 = mybir.dt.bfloat16
AF = mybir.ActivationFunctionType
ALU = mybir.AluOpType
AX = mybir.AxisListType

D = 1024
NCORES = 8
SEQ = 8192
OWN_TOK = 2048
C = 128
SBT = 256
CPS = SBT // C
RMS_EPS = 1e-6
GN_EPS = 64e-5
IN_COLS = 5504
O_SH = 0
O_GR = 1664
O_Q = 2176
O_K = 2688
O_V = 2816
O_GA = 2944
O_GT = 3456

PPI = {}
_n = 0
for _name, _cnt in [("mu", 13), ("w0", 4), ("a0", 4), ("kk", 4), ("ka", 4), ("rk", 4),
                    ("gnw", 4), ("gnb", 4), ("bq", 4), ("bk", 2), ("sink", 8)]:
    PPI[_name] = _n
    _n += _cnt
NPP_IN = _n
for _name, _cnt in [("omu", 13), ("nw0", 4), ("omka", 4), ("na0", 4)]:
    PPI[_name] = _n
    _n += _cnt
NPP = _n


class Buf:
    __slots__ = ("name", "w", "r", "excl")

    def __init__(self, name, excl=False):
        self.name = name
        self.w = None
        self.r = []
        self.excl = excl


class Sched:
    ENG = ("pe", "act", "dve", "pool", "sp")
    NDMA = 24

    def __init__(self, same_sync=True):
        self.ops = {e: [] for e in self.ENG}
        self.cnt = {e: 0 for e in self.ENG}
        self.waited = {e: {} for e in self.ENG}
        self.same_sync = same_sync
        self.dma_val = [0] * self.NDMA
        self.dma_rr = 0
        self.dma_rr2 = 0
        self.out_tokens = []

    def add(self, eng, fn, reads=(), writes=(), dma=False, is_out=False):
        self.total = getattr(self, "total", 0) + 1
        if not dma and self.total > getattr(self, "cut", 10 ** 9):
            return None
        deps = {}

        def need(tk, hard):
            d = deps.get(tk[0])
            if d is None:
                deps[tk[0]] = [tk[1], tk[2], hard]
            else:
                d[0] = max(d[0], tk[1])
                d[2] = d[2] or hard
        for b in reads:
            if b.w is not None:
                need(b.w, True)
            if b.excl:
                for tk in b.r:
                    need(tk, False)
        for b in writes:
            if b.w is not None:
                need(b.w, True)
            for tk in b.r:
                need(tk, False)
        waits = []
        for semkey, (val, src, hard) in deps.items():
            if src == eng and not isinstance(semkey, tuple):
                if eng in ("pe", "sp"):
                    continue
                if not hard or not self.same_sync:
                    continue
            if self.waited[eng].get(semkey, 0) >= val:
                continue
            self.waited[eng][semkey] = val
            waits.append((semkey, val))
        if dma:
            half = self.NDMA // 2
            if eng == "sp":
                j = self.dma_rr
                self.dma_rr = (self.dma_rr + 1) % half
            else:
                j = half + self.dma_rr2
                self.dma_rr2 = (self.dma_rr2 + 1) % half
            semkey = ("dma", j)
            if self.dma_val[j] > 0 and self.waited[eng].get(semkey, 0) < self.dma_val[j]:
                self.waited[eng][semkey] = self.dma_val[j]
                waits.append((semkey, self.dma_val[j]))
            self.dma_val[j] += 16
            tok = (semkey, self.dma_val[j], eng)
            inc = 16
        else:
            self.cnt[eng] += 1
            tok = (eng, self.cnt[eng], eng)
            inc = 1
        for b in reads:
            b.r.append(tok)
        for b in writes:
            b.w = tok
            b.r = []
        if is_out:
            self.out_tokens.append(tok)
        self.ops[eng].append((waits, fn, tok[0], inc))
        return tok

    def emit(self, nc, block, sems):
        engmap = {"pe": block.tensor, "act": block.scalar, "dve": block.vector,
                  "pool": block.gpsimd, "sp": block.sync}
        for e in self.ENG:
            ops = self.ops[e]
            final = list(self.out_tokens) if e == "sp" else ()

            def body(eng, ops=ops, final=final):
                for waits, fn, semkey, inc in ops:
                    for sk, val in waits:
                        eng.wait_ge(sems[sk], val)
                    fn(eng).then_inc(sems[semkey], inc)
                for tk in final:
                    eng.wait_ge(sems[tk[0]], tk[1])
                if final != ():
                    for j in range(self.NDMA):
                        if self.dma_val[j] > 0:
                            eng.wait_ge(sems[("dma", j)], self.dma_val[j])
            engmap[e](body)


def build(nsb=SEQ // SBT, nown=OWN_TOK // SBT, upto=99, dumps=(), same_sync=True, cut=None):
    from contextlib import ExitStack
    nc = bass.Bass("TRN2", target_bir_lowering=False)
    WT = nsb * SBT
    OT = nown * SBT
    NOC = nown * CPS
    S = Sched(same_sync=same_sync)
    if cut is not None:
        S.cut = cut

    def din(name, shape, dt=F32):
        return nc.dram_tensor(name, list(shape), dt, kind="ExternalInput").ap()

    xw = din("xw", [WT, D])
    w_in = din("w_in", [D, IN_COLS])
    w_br = din("w_br", [2, 512, D])
    w_out = din("w_out", [D, D])
    wdi = din("wdi", [128, 2, 512])
    pp_in = din("pp", [128, NPP_IN])
    gpre_d = din("gpre_b", [128, D])
    gfin_d = din("gfin_b", [128, D])
    bv_d = din("bv_b", [128, 128])
    cst_d = din("cst", [128, 9, 128])
    am_d = din("amask", [128, 2, 256])
    out_d = nc.dram_tensor("out", [OT, D], F32, kind="ExternalOutput").ap()
    wib = nc.dram_tensor("wib_scratch", [D, IN_COLS - O_GR], BF16).ap()
    wbb = nc.dram_tensor("wbb_scratch", [2 * 512, D], BF16).ap()
    dump_d = {}

    es = ExitStack()
    with es:
        def sb(name, shape, dt=F32):
            return es.enter_context(nc.sbuf_tensor(name, list(shape), dt))

        def ps(name, shape, dt=F32):
            return es.enter_context(nc.psum_tensor(name, list(shape), dt))

        class T:
            def __init__(self, name, shape, dt=F32, n=1):
                self.t = [sb(f"{name}{i}", shape, dt) for i in range(n)]
                self.b = [Buf(f"{name}{i}") for i in range(n)]
                self.n = n

            def __call__(self, i=0):
                return self.t[i % self.n], self.b[i % self.n]

        def dma(eng, out, in_, reads, writes, is_out=False):
            return S.add(eng, lambda e: e.dma_start(out=out, in_=in_), reads, writes, dma=True, is_out=is_out)

        def dump(name, ap, shape, reads, dt=F32):
            if name not in dumps:
                return
            dd = nc.dram_tensor("dbg_" + name, list(shape), dt, kind="ExternalOutput").ap()
            dump_d[name] = dd
            dma("sp", dd, ap, reads, [], is_out=True)

        xt = T("xt", [128, D], F32, 2)
        cst_b = T("cst_b", [128, 8, 128], BF16)
        cst2 = T("cst2", [128, 1, 128])
        amask = T("amask", [128, 2, 256])
        PP = T("PP", [128, NPP])
        gpre = T("gpre", [128, D])
        gfin = T("gfin", [128, D])
        bvb = T("bvb", [128, 128])
        Wdb = T("Wdb", [128, 3, 512], BF16)
        Wsh = T("Wsh", [128, 8, 1664], BF16)
        Wout = T("Wout", [128, 8, D], BF16)

        stg = xt.t[1][:, :].rearrange("p (a b) -> p a b", a=8)
        dma("sp", stg, cst_d[:, 0:8, :], [], [xt.b[1]])
        dma("sp", cst2.t[0][:], cst_d[:, 8:9, :], [], [cst2.b[0]])
        dma("sp", PP.t[0][:, 0:NPP_IN], pp_in, [], [PP.b[0]])
        dma("sp", gpre.t[0][:], gpre_d, [], [gpre.b[0]])
        stg_w = xt.t[0][:, :].rearrange("p (a b) -> p a b", a=2)
        dma("sp", stg_w, wdi, [], [xt.b[0]])
        S.add("act", lambda e: e.activation(out=Wdb.t[0][:, 0, :], in_=stg_w[:, 0, :], func=AF.Copy), [xt.b[0]], [Wdb.b[0]])
        S.add("act", lambda e: e.activation(out=Wdb.t[0][:, 2, :], in_=stg_w[:, 1, :], func=AF.Copy), [xt.b[0]], [Wdb.b[0]])
        S.add("dve", lambda e: e.tensor_tensor(out=Wdb.t[0][:, 1, :], in0=stg_w[:, 0, :], in1=Wdb.t[0][:, 0, :], op=ALU.subtract), [xt.b[0], Wdb.b[0]], [Wdb.b[0]])
        Wsh_b = [Buf(f"Wsh_k{k}") for k in range(8)]
        for k in range(8):
            S.add("pool", lambda e, k=k: e.dma_start(out=Wsh.t[0][:, k, :], in_=w_in[k * 128:(k + 1) * 128, O_SH:O_SH + 1664]),
                  [], [Wsh_b[k]], dma=True)
        dma("sp", amask.t[0][:], am_d, [], [amask.b[0]])
        dma("sp", gfin.t[0][:], gfin_d, [], [gfin.b[0]])
        dma("sp", bvb.t[0][:], bv_d, [], [bvb.b[0]])
        S.add("dve", lambda e: e.tensor_copy(out=cst_b.t[0][:], in_=stg), [xt.b[1]], [cst_b.b[0]])
        ident_b = cst_b.t[0][:, 0, :]
        mask4 = cst_b.t[0][:, 1:5, :]
        mle2 = cst_b.t[0][:, 5:7, :]
        bones_b = cst_b.t[0][:, 7, :]
        ones_f = cst2.t[0][:, 0, :]
        CB = cst_b.b[0]
        CF = cst2.b[0]
        ppt = PP.t[0]
        PB = PP.b[0]

        def pc(name, i=0):
            j = PPI[name] + i
            return ppt[:, j:j + 1]

        S.add("dve", lambda e: e.tensor_scalar(out=ppt[:, PPI["omu"]:PPI["omu"] + 13], in0=ppt[:, PPI["mu"]:PPI["mu"] + 13],
                                               scalar1=-1.0, scalar2=1.0, op0=ALU.mult, op1=ALU.add), [PB], [PB])
        S.add("dve", lambda e: e.tensor_scalar(out=ppt[:, PPI["nw0"]:PPI["nw0"] + 4], in0=ppt[:, PPI["w0"]:PPI["w0"] + 4],
                                               scalar1=-1.0, scalar2=None, op0=ALU.mult), [PB], [PB])
        S.add("dve", lambda e: e.tensor_scalar(out=ppt[:, PPI["omka"]:PPI["omka"] + 4], in0=ppt[:, PPI["ka"]:PPI["ka"] + 4],
                                               scalar1=-1.0, scalar2=1.0, op0=ALU.mult, op1=ALU.add), [PB], [PB])
        S.add("dve", lambda e: e.tensor_scalar(out=ppt[:, PPI["na0"]:PPI["na0"] + 4], in0=ppt[:, PPI["a0"]:PPI["a0"] + 4],
                                               scalar1=-1.0, scalar2=None, op0=ALU.mult), [PB], [PB])

        psA = [ps(f"psA{i}", [128, 512]) for i in range(2)]
        psA_b = [Buf(f"psA{i}", True) for i in range(2)]
        psT = [ps(f"psT{i}", [128, 1024], BF16) for i in range(2)]
        psT_b = [[Buf(f"psT{i}_{h}", True) for h in range(2)] for i in range(2)]
        psLU = [[ps(f"psL{i}", [128, 512]), ps(f"psU{i}", [128, 512])] for i in range(2)]
        psLU_b = [[[Buf(f"psLU{i}_{lu}_{s}", True) for s in range(4)] for lu in range(2)] for i in range(2)]
        arr = [0]
        prr = [0]
        srr = [0]

        fb_mode = [0]

        def fullbank():
            if fb_mode[0]:
                return psA[1], [psA_b[1]]
            i = arr[0]
            arr[0] = (i + 1) % 2
            return psA[i], [psA_b[i]]

        def pair(ns):
            r = prr[0]
            if (r % 4) + ns > 4:
                r = (r // 4 + 1) * 4
            r %= 8
            p, s = r // 4, r % 4
            prr[0] = (r + ns) % 8
            sl = slice(s * 128, (s + ns) * 128)
            return (psLU[p][0][:, sl], psLU_b[p][0][s:s + ns], psLU[p][1][:, sl], psLU_b[p][1][s:s + ns])

        brr = [0]

        def bankx():
            i = brr[0]
            brr[0] = (i + 1) % 4
            p, lu = i // 2, i % 2
            return psLU[p][lu], list(psLU_b[p][lu])

        def single(ns):
            bk, bb = bankx()
            return bk[:, 0:ns * 128], bb

        hb = T("hb", [128, D], BF16, 1)
        hT = T("hT", [128, 8, SBT], BF16, 2)
        st0 = T("st0", [128, 4], F32, 2)
        shwa = T("shwa", [128, SBT])
        shtmp = T("shtmp", [128, SBT], F32, 2)
        ptc = [0]
        shr = T("shr", [128, SBT], F32, 2)
        shk = T("shk", [128, SBT], F32, 2)
        shv = T("shv", [128, SBT], F32, 2)
        tw = T("tw", [128, SBT])
        tw_hi = T("tw_hi", [128, SBT], BF16)
        tw_lo = T("tw_lo", [128, SBT], BF16)
        t_k2b = T("t_k2b", [128, SBT], BF16)
        t_rkb = T("t_rkb", [128, SBT], BF16)
        Hhl = T("Hhl", [128, 2, 64], BF16, 4)
        t_e1 = T("t_e1", [128, SBT])
        t_ew = T("t_ew", [128, SBT])
        t_a = T("t_a", [128, SBT])
        t_cs = T("t_cs", [128, SBT])
        t_csp = T("t_csp", [128, SBT])
        t_en = T("t_en", [128, SBT])
        t_ep = T("t_ep", [128, SBT])
        t_k2 = T("t_k2", [128, SBT])
        t_kkn = T("t_kkn", [128, SBT])
        t_ab = T("t_ab", [128, SBT])
        t_f = T("t_f", [128, SBT])
        gC = T("gC", [128, CPS], F32, 8)
        AR = T("AR", [128, CPS, 2, C], BF16, 4)
        BT = T("BT", [128, SBT], BF16, 4)
        KT = T("KT", [128, SBT], BF16, 4)
        vbf = T("vbf", [128, SBT], BF16, 4)
        bonus = T("bonus", [128, SBT], BF16, 4)
        tm = T("tm", [128, 4, 128], BF16, 4)
        PZ = T("PZ", [128, 3, SBT], BF16, 8)
        Hbfz = T("Hbfz", [128, 64], BF16, 8)
        qTz = T("qTz", [128, SBT], BF16, 8)
        NG = 4
        NMt = T("NM", [128, 2, 2, 128], BF16, 2 * NG)
        Mak = T("Mak", [128, 2, 128], BF16, NG)
        RBK = T("RBK", [128, 2, 2, 128], BF16, NG)
        PAIRS = [(0, 1), (2, 3)]
        Xtile = T("Xt", [128, 2, 2, 64], BF16, 2 * NG)
        ATbd = T("ATbd", [128, 128], BF16, NG)
        Gsb = T("Gsb", [128, 64], F32, NG)
        Ht = T("Hst", [128, 64], F32, 4)
        s1t = T("s1t", [128, 64], F32, NG)
        Wz = T("Wz", [128, 3, 64], BF16, NG)
        Wp = T("Wp", [128, 2, 64], BF16, NG)
        zlo = T("zlo", [128, 64], F32, NG)
        QT = T("QT", [128, 128], BF16, NG)
        prevcol = T("prevcol", [128, 13], F32, 2)
        kc = T("kcols", [128, 8])
        NWS = 5
        ws = T("ws", [128, 8, 128], BF16, NWS)
        ws_b2 = [Buf(f"ws_b2_{i}") for i in range(NWS)]
        wsrr = [0]
        sgr = T("sgr", [128, SBT], BF16, 4)
        sga = T("sga", [128, SBT], BF16, 4)
        NKS = 4
        KTatt = T("KTatt", [128, NKS * 128], BF16, 2)
        NV = 4
        Vpad = T("Vpad", [128, 2, 192], BF16, NV)
        ysqt = T("ysq", [128, 512], F32, 1)
        yn = T("yn", [128, 512], BF16, 1)
        gst = T("gst", [128, 6, 8], F32, 1)
        t1t = T("t1t", [128, 128], F32, 1)
        zr = T("zr", [128, SBT], BF16, 4)
        zatt = T("zatt", [128, SBT], BF16, 4)
        smt = T("smt", [128, 256], F32, 4)
        p32 = T("p32", [128, 256], F32, 4)
        pnt = T("pnt", [128, 256], BF16, 4)
        ptt = T("ptt", [128, 2, 128], BF16, 4)
        ast = T("ast", [128, 8], F32, 4)
        mT = T("mT", [128, 8, SBT], BF16, 1)
        sgt = T("sgt", [128, SBT], F32, 2)
        m12 = T("m12", [128, SBT], F32, 2)
        fst = T("fst", [128, 4], F32, 2)

        S.add("pool", lambda e: e.memset(prevcol.t[0][:], 0.0), [], [prevcol.b[0]])
        S.add("pool", lambda e: e.memset(prevcol.t[1][:], 0.0), [], [prevcol.b[1]])
        kct = kc.t[0]
        KB = kc.b[0]
        for j, val in enumerate([RMS_EPS, 1.0, -0.5, 1e-12, GN_EPS]):
            S.add("pool", lambda e, j=j, val=val: e.memset(kct[:, j:j + 1], val), [], [KB])
        eps_col = kct[:, 0:1]
        one_col = kct[:, 1:2]
        mhalf_col = kct[:, 2:3]
        tiny_col = kct[:, 3:4]
        gneps_col = kct[:, 4:5]
        for i in range(NG):
            S.add("pool", lambda e, i=i: e.memset(ATbd.t[i][:], 0.0), [], [ATbd.b[i]])
            S.add("pool", lambda e, i=i: e.memset(Wz.t[i][:], 0.0), [], [Wz.b[i]])
        for i in range(4):
            S.add("pool", lambda e, i=i: e.memset(Ht.t[i][:], 0.0), [], [Ht.b[i]])
        for i in range(NV):
            S.add("pool", lambda e, i=i: e.memset(Vpad.t[i][:], 0.0), [], [Vpad.b[i]])
        for i in range(8):
            S.add("pool", lambda e, i=i: e.memset(PZ.t[i][:], 0.0), [], [PZ.b[i]])
            S.add("pool", lambda e, i=i: e.memset(Hbfz.t[i][:], 0.0), [], [Hbfz.b[i]])
            S.add("pool", lambda e, i=i: e.memset(qTz.t[i][:], 0.0), [], [qTz.b[i]])
        for i in range(2):
            S.add("pool", lambda e, i=i: e.memset(KTatt.t[i][:], 0.0), [], [KTatt.b[i]])
        Wout_b = [Buf(f"wout{k}") for k in range(8)]
        for k in range(8):
            S.add("pool", lambda e, k=k: e.dma_start(out=Wout.t[0][:, k, :], in_=w_out[k * 128:(k + 1) * 128, :]),
                  [], [Wout_b[k]], dma=True)

        wib_b = [Buf(f"wib{k}") for k in range(8)]
        wbb_b = [Buf(f"wbb{k}") for k in range(8)]
        w_br_flat = w_br.rearrange("b r c -> (b r) c")
        for k in range(8):
            S.add("pool", lambda e, k=k: e.dma_start(out=wib[k * 128:(k + 1) * 128, :], in_=w_in[k * 128:(k + 1) * 128, O_GR:IN_COLS]),
                  [], [wib_b[k]], dma=True)
        for k in range(8):
            S.add("pool", lambda e, k=k: e.dma_start(out=wbb[k * 128:(k + 1) * 128, :], in_=w_br_flat[k * 128:(k + 1) * 128, :]),
                  [], [wbb_b[k]], dma=True)

        def bcm(ap2, n):
            a = ap2.ap
            return bass.AP(ap2.tensor, ap2.offset, [list(a[0]), [0, n], list(a[1])])

        def bcl(ap2, n):
            a = ap2.ap
            return bass.AP(ap2.tensor, ap2.offset, [list(a[0]), list(a[1]), [0, n]])

        def v3(ap, h=2):
            return ap.rearrange("p (h t) -> p h t", h=h)

        def rms_rstd(in_ap, in_bufs, stt, stb, junk_ap, junk_buf):
            S.add("act", lambda e: e.activation(out=junk_ap, in_=in_ap, func=AF.Square, accum_out=stt[:, 0:1]),
                  in_bufs, [junk_buf, stb])
            S.add("act", lambda e: e.activation(out=stt[:, 1:2], in_=stt[:, 0:1], func=AF.Ln, bias=eps_col, scale=1.0 / D),
                  [stb, KB], [stb])
            S.add("act", lambda e: e.activation(out=stt[:, 2:3], in_=stt[:, 1:2], func=AF.Exp, scale=-0.5), [stb], [stb])

        def ws_load(srcs):
            i = wsrr[0]
            wsrr[0] = (i + 1) % NWS
            t = ws.t[i]
            bufs = [ws.b[i], ws_b2[i]]
            for j, (dfn, dap) in enumerate(srcs):
                S.add("sp", lambda e, dfn=dfn, dap=dap, t=t: e.dma_start(out=dfn(t), in_=dap), wib_b + wbb_b, [bufs[j]], dma=True)
            return t, bufs[:len(srcs)]

        def wcols(c0, n=128):
            return wib[:, c0 - O_GR:c0 - O_GR + n].rearrange("(k p) c -> p k c", p=128)

        def proj_fm(hTt, hTb, wt, wbufs, ncols=SBT, col0=0):
            pa, pab = fullbank()
            for k in range(8):
                S.add("pe", lambda e, pa=pa, k=k, wt=wt, hTt=hTt: e.matmul(
                    pa[:, 0:ncols], lhsT=wt[:, k, :], rhs=hTt[:, k, col0:col0 + ncols], start=(k == 0), stop=(k == 7)),
                    wbufs + [hTb], pab)
            return pa, pab

        P_ = [slice(0, 64), slice(64, 128)]
        own0 = nsb - nown

        def stage2_header():
            swt, swb = shwa()
            twt, twb = tw()
            S.add("act", lambda e, twt=twt, swt=swt: e.activation(out=twt[0:64, :], in_=swt[0:64, :], func=AF.Exp, scale=2.0), [swb], [twb])
            S.add("act", lambda e, twt=twt: e.activation(out=twt[0:64, :], in_=twt[0:64, :], func=AF.Ln, bias=kct[0:64, 1:2]), [twb, KB], [twb])
            S.add("act", lambda e, twt=twt: e.activation(out=twt[0:64, :], in_=twt[0:64, :], func=AF.Exp, scale=-1.0), [twb], [twb])
            S.add("dve", lambda e, twt=twt: e.tensor_scalar(out=twt[0:64, :], in0=twt[0:64, :], scalar1=-2.0, scalar2=1.0, op0=ALU.mult, op1=ALU.add), [twb], [twb])
            S.add("act", lambda e, twt=twt, swt=swt: e.activation(out=twt[64:128, :], in_=swt[64:128, :], func=AF.Copy), [swb, twb], [twb])
            twh, twhb = tw_hi()
            twl, twlb = tw_lo()
            S.add("act", lambda e, twh=twh, twt=twt: e.activation(out=twh[:], in_=twt[:], func=AF.Copy), [twb], [twhb])
            S.add("dve", lambda e, twl=twl, twt=twt, twh=twh: e.tensor_tensor(out=twl[:], in0=twt[:], in1=twh[:], op=ALU.subtract), [twb, twhb], [twlb])
            return dict(twt=twt, twh=twh, twl=twl, twhb=twhb, twlb=twlb, twb=twb)
        def prep_hp(hp, sbi, own, twt=None, twh=None, twl=None, twhb=None, twlb=None, twb=None):
            si = sbi * 4 + hp
            rt, rb = shr(si)
            kt_, kb_ = shk(si)
            vt, vb = shv(si)
            pD, pDb = fullbank()
            hsl = slice(hp * 128, (hp + 1) * 128)
            S.add("pe", lambda e, pD=pD, hsl=hsl, twh=twh: e.matmul(pD[:, 0:SBT], lhsT=Wdb.t[0][:, 0, hsl], rhs=twh[:, :], start=True, stop=False),
                  [Wdb.b[0], twhb], pDb)
            S.add("pe", lambda e, pD=pD, hsl=hsl, twl=twl: e.matmul(pD[:, 0:SBT], lhsT=Wdb.t[0][:, 0, hsl], rhs=twl[:, :], start=False, stop=False),
                  [Wdb.b[0], twlb], pDb)
            S.add("pe", lambda e, pD=pD, hsl=hsl, twh=twh: e.matmul(pD[:, 0:SBT], lhsT=Wdb.t[0][:, 1, hsl], rhs=twh[:, :], start=False, stop=True),
                  [Wdb.b[0], twhb], pDb)
            e1, e1b = t_e1()
            ew, ewb = t_ew()
            at, ab_ = t_a()
            cs, csb = t_cs()
            csp, cspb = t_csp()
            en, enb = t_en()
            k2, k2b = t_k2()
            kkn, kknb = t_kkn()
            abt, abb = t_ab()
            ft, fb = t_f()
            S.add("act", lambda e, e1=e1, pD=pD, hp=hp: e.activation(out=e1[:], in_=pD[:, 0:SBT], func=AF.Exp, bias=pc("nw0", hp), scale=-1.0),
                  pDb + [PB], [e1b])
            pAa, pAb = fullbank()
            S.add("pe", lambda e, pAa=pAa, hsl=hsl, twh=twh: e.matmul(pAa[:, 0:SBT], lhsT=Wdb.t[0][:, 2, hsl], rhs=twh[:, :], start=True, stop=True),
                  [Wdb.b[0], twhb], pAb)
            S.add("act", lambda e, e1=e1: e.activation(out=e1[:], in_=e1[:], func=AF.Ln, bias=one_col), [e1b, KB], [e1b])
            S.add("act", lambda e, e1=e1, ew=ew: e.activation(out=ew[:], in_=e1[:], func=AF.Exp, bias=mhalf_col, scale=-1.0), [e1b, KB], [ewb])
            S.add("act", lambda e, at=at, pAa=pAa, hp=hp: e.activation(out=at[:], in_=pAa[:, 0:SBT], func=AF.Exp, bias=pc("na0", hp), scale=-1.0),
                  pAb + [PB], [ab_])
            yield
            S.add("act", lambda e, at=at: e.activation(out=at[:], in_=at[:], func=AF.Ln, bias=one_col), [ab_, KB], [ab_])
            S.add("act", lambda e, at=at: e.activation(out=at[:], in_=at[:], func=AF.Exp, scale=-1.0), [ab_], [ab_])
            for c in range(CPS):
                S.add("dve", lambda e, cs=cs, ew=ew, c=c: e.tensor_tensor_scan(
                    out=cs[:, c * C:(c + 1) * C], data0=ones_f, data1=ew[:, c * C:(c + 1) * C], initial=0.0,
                    op0=ALU.mult, op1=ALU.add), [ewb, CF], [csb])
            S.add("pool", lambda e, csp=csp, cs=cs, ew=ew: e.tensor_tensor(out=csp[:], in0=cs[:], in1=ew[:], op=ALU.subtract), [csb, ewb], [cspb])
            S.add("act", lambda e, en=en, cs=cs: e.activation(out=en[:], in_=cs[:], func=AF.Exp), [csb], [enb])
            S.add("act", lambda e, csp=csp: e.activation(out=csp[:], in_=csp[:], func=AF.Exp, scale=-1.0), [cspb], [cspb])
            gct, gcb = gC(si)
            S.add("act", lambda e, gct=gct, cs=cs: e.activation(
                out=gct[:, 0:CPS], in_=cs[:, :].rearrange("p (c t) -> p c t", t=C)[:, :, C - 1], func=AF.Exp, scale=-1.0), [csb], [gcb])
            yield
            k2h, k2hb = t_k2b()
            S.add("act", lambda e, k2h=k2h, kt_=kt_, hp=hp: e.activation(out=k2h[:], in_=kt_[:], func=AF.Square, scale=pc("kk", hp)), [kb_, PB], [k2hb])
            pS_, pSb = fullbank()
            S.add("pe", lambda e, pS_=pS_, k2h=k2h: e.matmul(pS_[:, 0:SBT], lhsT=bones_b, rhs=k2h[:], start=True, stop=True), [CB, k2hb], pSb)
            S.add("act", lambda e, k2=k2, pS_=pS_: e.activation(out=k2[:], in_=pS_[:, 0:SBT], func=AF.Ln, bias=tiny_col), pSb + [KB], [k2b])
            S.add("act", lambda e, k2=k2: e.activation(out=k2[:], in_=k2[:], func=AF.Exp, scale=-0.5), [k2b], [k2b])
            S.add("dve", lambda e, kkn=kkn, kt_=kt_, k2=k2, hp=hp: e.scalar_tensor_tensor(
                out=kkn[:], in0=kt_[:], scalar=pc("kk", hp), in1=k2[:], op0=ALU.mult, op1=ALU.mult), [kb_, k2b, PB], [kknb])
            yield
            ARt, ARb = AR(si)
            BTt, BTb = BT(si)
            KTt, KTb = KT(si)
            vbt, vbb = vbf(si)
            S.add("dve", lambda e, ARt=ARt, kkn=kkn, csp=csp: e.scalar_tensor_tensor(
                out=ARt[:, :, 0, :], in0=kkn[:, :].rearrange("p (c t) -> p c t", t=C), scalar=-1.0,
                in1=csp[:, :].rearrange("p (c t) -> p c t", t=C), op0=ALU.mult, op1=ALU.mult), [kknb, cspb], [ARb])
            S.add("pool", lambda e, abt=abt, kkn=kkn, at=at: e.tensor_tensor(out=abt[:], in0=kkn[:], in1=at[:], op=ALU.mult), [kknb, ab_], [abb])
            S.add("pool", lambda e, BTt=BTt, abt=abt, en=en: e.tensor_tensor(out=BTt[:], in0=abt[:], in1=en[:], op=ALU.mult), [abb, enb], [BTb])
            yield
            S.add("dve", lambda e, ft=ft, at=at, hp=hp: e.tensor_scalar(out=ft[:], in0=at[:], scalar1=pc("ka", hp), scalar2=pc("omka", hp),
                                                                    op0=ALU.mult, op1=ALU.add), [ab_, PB], [fb])
            S.add("pool", lambda e, ft=ft, kt_=kt_: e.tensor_tensor(out=ft[:], in0=kt_[:], in1=ft[:], op=ALU.mult), [kb_, fb], [fb])
            S.add("pool", lambda e, KTt=KTt, ft=ft, en=en: e.tensor_tensor(out=KTt[:], in0=ft[:], in1=en[:], op=ALU.mult), [fb, enb], [KTb])
            S.add("act", lambda e, vbt=vbt, vt=vt: e.activation(out=vbt[:], in_=vt[:], func=AF.Copy), [vb], [vbb])
            yield
            for hh in range(2):
                zt, zb = PZ(si * 2 + hh)
                S.add("pool", lambda e, zt=zt, ARt=ARt, hh=hh: e.tensor_copy(out=zt[P_[hh], 0, :].rearrange("p (c t) -> p c t", t=C), in_=ARt[P_[hh], :, 0, :]), [ARb], [zb])
                S.add("pool", lambda e, zt=zt, BTt=BTt, hh=hh: e.tensor_copy(out=zt[P_[hh], 1, :], in_=BTt[P_[hh], :]), [BTb], [zb])
                S.add("pool", lambda e, zt=zt, KTt=KTt, hh=hh: e.tensor_copy(out=zt[P_[hh], 2, :], in_=KTt[P_[hh], :]), [KTb], [zb])
            if own:
                ep, epb = t_ep()
                S.add("act", lambda e, ep=ep, cs=cs: e.activation(out=ep[:], in_=cs[:], func=AF.Exp, scale=-1.0), [csb], [epb])
                S.add("dve", lambda e, ARt=ARt, rt=rt, ep=ep: e.tensor_tensor(
                    out=ARt[:, :, 1, :], in0=rt[:, :].rearrange("p (c t) -> p c t", t=C),
                    in1=ep[:, :].rearrange("p (c t) -> p c t", t=C), op=ALU.mult), [rb, epb], [ARb])
                rkb_t, rkb_b = t_rkb()
                S.add("dve", lambda e, rkb_t=rkb_t, rt=rt, ft=ft, hp=hp: e.scalar_tensor_tensor(
                    out=rkb_t[:], in0=rt[:], scalar=pc("rk", hp), in1=ft[:], op0=ALU.mult, op1=ALU.mult), [rb, fb, PB], [rkb_b])
                pB_, pBb = fullbank()
                S.add("pe", lambda e, pB_=pB_, rkb_t=rkb_t: e.matmul(pB_[:, 0:SBT], lhsT=bones_b, rhs=rkb_t[:], start=True, stop=True), [CB, rkb_b], pBb)
                bnt, bnb = bonus(si)
                S.add("dve", lambda e, bnt=bnt, pB_=pB_, vt=vt: e.tensor_tensor(out=bnt[:], in0=pB_[:, 0:SBT], in1=vt[:], op=ALU.mult), pBb + [vb], [bnb])
            yield
        def proj_tile(sbi, ct, hTt, hTb):
            pa, pab = fullbank()
            for k in range(8):
                S.add("pe", lambda e, pa=pa, k=k, ct=ct, hTt=hTt: e.matmul(
                    pa[:, 0:SBT], lhsT=Wsh.t[0][:, k, ct * 128:(ct + 1) * 128], rhs=hTt[:, k, :],
                    start=(k == 0), stop=(k == 7)), [Wsh_b[k], hTb], pab)
            if ct == 12:
                dst, dstb = shwa()
            else:
                hp = ct % 4
                dst, dstb = (shr, shk, shv)[ct // 4](sbi * 4 + hp)
            ptc[0] += 1
            tmp, tmpb = shtmp(ptc[0])
            pcur, pcurb = prevcol(sbi)
            pnxt, pnxtb = prevcol(sbi + 1)
            S.add("act", lambda e, tmp=tmp, pa=pa, ct=ct: e.activation(
                out=tmp[:], in_=pa[:, 0:SBT], func=AF.Copy, scale=pc("omu", ct)), pab + [PB], [tmpb])
            S.add("act", lambda e, pa=pa, ct=ct, pnxt=pnxt: e.activation(
                out=pnxt[:, ct:ct + 1], in_=pa[:, SBT - 1:SBT], func=AF.Copy), pab, [pnxtb])
            S.add("dve", lambda e, dst=dst, pa=pa, tmp=tmp, ct=ct: e.scalar_tensor_tensor(
                out=dst[:, 1:SBT], in0=pa[:, 0:SBT - 1], scalar=pc("mu", ct), in1=tmp[:, 1:SBT],
                op0=ALU.mult, op1=ALU.add), pab + [tmpb, PB], [dstb])
            S.add("dve", lambda e, dst=dst, tmp=tmp, ct=ct, pcur=pcur: e.scalar_tensor_tensor(
                out=dst[:, 0:1], in0=pcur[:, ct:ct + 1], scalar=pc("mu", ct), in1=tmp[:, 0:1],
                op0=ALU.mult, op1=ALU.add), [pcurb, tmpb, PB], [dstb])

        sbst = {}

        def gen_first(sbi):
            own = sbi >= own0
            hTt, hTb = hT(sbi)
            for j in range(CPS):
                gc = sbi * CPS + j
                xtt, xtb = xt(gc)
                hbt, hbb = hb(gc)
                stt, stb = st0(gc)
                dma("sp", xtt[:], xw[gc * C:(gc + 1) * C, :], [], [xtb])
                rms_rstd(xtt[:], [xtb], stt, stb, hbt[:], hbb)
                S.add("dve", lambda e, xtt=xtt, stt=stt, hbt=hbt: e.scalar_tensor_tensor(
                    out=hbt[:], in0=xtt[:], scalar=stt[:, 2:3], in1=gpre.t[0][:], op0=ALU.mult, op1=ALU.mult),
                    [xtb, stb, gpre.b[0]], [hbb])
                yield
                pst = psT[0]
                pstb = psT_b[0]
                for k in range(8):
                    S.add("pe", lambda e, k=k, hbt=hbt, pst=pst: e.transpose(
                        out=pst[:, k * 128:(k + 1) * 128], in_=hbt[:, k * 128:(k + 1) * 128], identity=ident_b),
                        [hbb, CB], pstb)
                S.add("act", lambda e, pst=pst, hTt=hTt, j=j: e.activation(
                    out=hTt[:, :, j * C:(j + 1) * C], in_=pst[:, :].rearrange("p (k t) -> p k t", k=8), func=AF.Copy),
                    pstb, [hTb])
                yield
            proj_tile(sbi, 12, hTt, hTb)
            tw_ctx = stage2_header()
            sbst[sbi] = (tw_ctx, hTt, hTb)
            yield
            for hp in (0, 1):
                for q in range(3):
                    proj_tile(sbi, q * 4 + hp, hTt, hTb)
                    yield

        def gen_first_b(sbi):
            own = sbi >= own0
            tw_ctx, hTt, hTb = sbst[sbi]
            for hp in (0, 1):
                yield from prep_hp(hp, sbi, own, **tw_ctx)

        def gen_first_ab(sbi):
            yield from gen_first(sbi)
            yield from gen_first_b(sbi)

        def gen_second(sbi):
            own = sbi >= own0
            tw_ctx, hTt, hTb = sbst[sbi]
            for hp in (2, 3):
                for q in range(3):
                    proj_tile(sbi, q * 4 + hp, hTt, hTb)
                    yield
                yield from prep_hp(hp, sbi, own, **tw_ctx)

        def drain(g):
            for _ in g:
                pass

        def mkfill(g, n=1, units=None, slots=None):
            st = [0]

            def fill():
                if units is None:
                    k = n
                else:
                    i = st[0]
                    st[0] += 1
                    k = ((i + 1) * units) // slots - (i * units) // slots
                for _ in range(k):
                    try:
                        next(g)
                    except StopIteration:
                        return
            return fill

        drain(gen_first_ab(0))
        for sbi in range(nsb):
            own = sbi >= own0
            halo_sb = (sbi == own0 - 1)
            hTt, hTb = hT(sbi)
            def gen_ownproj(sbi=sbi, own=own, halo_sb=halo_sb, hTt=hTt, hTb=hTb):
                if own or halo_sb:
                    ncols, col0 = (SBT, 0) if own else (C, SBT - C)
                    kcol = ((sbi - own0) * CPS + 1) * C if own else 0
                    for g in range(2):
                        wt, wb = ws_load([(lambda t: t[:, :, 0:64], wcols(O_K + g * 64, 64)), (lambda t: t[:, :, 64:128], wcols(O_K + g * 64, 64))])
                        pa, pab = proj_fm(hTt, hTb, wt, wb, ncols, col0)
                        for cc in range(ncols // C):
                            ks = ((kcol // C) + cc) % NKS
                            S.add("act", lambda e, pa=pa, g=g, ks=ks, cc=cc: e.activation(
                                out=KTatt.t[g][:, ks * C:(ks + 1) * C], in_=pa[:, cc * C:(cc + 1) * C], func=AF.Identity, bias=pc("bk", g)), pab + [PB], [KTatt.b[g]])
                        yield
                    wt, wb = ws_load([(lambda t: t[:, :, :], wcols(O_V))])
                    for c in (range(CPS) if own else [CPS - 1]):
                        lc1 = (sbi - own0) * CPS + c + 1 if own else 0
                        pa, pab = fullbank()
                        for k in range(8):
                            S.add("pe", lambda e, pa=pa, k=k, wt=wt, hTt=hTt, c=c: e.matmul(
                                pa[:, 0:128], lhsT=hTt[:, k, c * C:(c + 1) * C], rhs=wt[:, k, :], start=(k == 0), stop=(k == 7)), wb + [hTb], pab)
                        vp, vpb = Vpad(lc1)
                        S.add("dve", lambda e, vp=vp, pa=pa: e.tensor_tensor(out=vp[:, :, 0:64], in0=v3(pa[:, 0:128]), in1=v3(bvb.t[0][:, :]), op=ALU.add),
                              pab + [bvb.b[0]], [vpb])
                        S.add("pool", lambda e, vp=vp: e.tensor_copy(out=vp[:, :, 128:192], in_=vp[:, :, 0:64]), [vpb], [vpb])
                        yield
                if own:
                    for ct in range(4):
                        si = sbi * 4 + ct
                        wt, wb = ws_load([(lambda t: t[:, :, :], wcols(O_GR + ct * 128))])
                        pa, pab = proj_fm(hTt, hTb, wt, wb)
                        S.add("act", lambda e, pa=pa, si=si: e.activation(out=sgr(si)[0][:], in_=pa[:, 0:SBT], func=AF.Silu), pab, [sgr(si)[1]])
                        yield
                        wt, wb = ws_load([(lambda t: t[:, :, :], wcols(O_Q + ct * 128))])
                        pa, pab = proj_fm(hTt, hTb, wt, wb)
                        for hh in range(2):
                            qz, qzb = qTz(si * 2 + hh)
                            S.add("act", lambda e, pa=pa, qz=qz, ct=ct, hh=hh: e.activation(
                                out=qz[P_[hh], :], in_=pa[P_[hh], 0:SBT], func=AF.Identity, bias=ppt[P_[hh], PPI["bq"] + ct:PPI["bq"] + ct + 1]),
                                pab + [PB], [qzb])
                        yield
                        wt, wb = ws_load([(lambda t: t[:, :, :], wcols(O_GA + ct * 128))])
                        pa, pab = proj_fm(hTt, hTb, wt, wb)
                        S.add("act", lambda e, pa=pa, si=si: e.activation(out=sga(si)[0][:], in_=pa[:, 0:SBT], func=AF.Silu), pab, [sga(si)[1]])
                        yield

                yield

            def emit_chunk_pairs(c, pairs, fill, own=own, sbi=sbi):
                gch = sbi * CPS + c
                csl = slice(c * C, (c + 1) * C)
                pYbank, pYbb = psA[0], [psA_b[0]]
                def mkctx(hp):
                    si = sbi * 4 + hp
                    gi = gch * 4 + hp
                    x = dict(hp=hp, si=si, gi=gi)
                    x["AR"], x["ARb"] = AR(si)
                    x["BT"], x["BTb"] = BT(si)
                    x["KT"], x["KTb"] = KT(si)
                    x["vb"], x["vbb"] = vbf(si)
                    x["zts"] = [PZ(si * 2 + hh) for hh in range(2)]
                    x["tm"], x["tmb"] = tm(gi)
                    return x

                def g_transposes(x, c=c, csl=csl):
                    pt_ = psT[1][:, 0:512]
                    ptb = psT_b[1]
                    srcs = [(x["AR"][:, c, 0, :], x["ARb"]), (x["BT"][:, csl], x["BTb"]), (x["KT"][:, csl], x["KTb"]), (x["vb"][:, csl], x["vbb"])]
                    for q, (sap, sbf) in enumerate(srcs):
                        S.add("pe", lambda e, pt_=pt_, q=q, sap=sap: e.transpose(out=pt_[:, q * 128:(q + 1) * 128], in_=sap, identity=ident_b),
                              [sbf, CB], ptb)
                    tmt = x["tm"]
                    S.add("act", lambda e, tmt=tmt, pt_=pt_: e.activation(out=tmt[:], in_=pt_.rearrange("p (q t) -> p q t", q=4), func=AF.Copy),
                          ptb, [x["tmb"]])

                def g_sprod_pe(x, c=c, csl=csl):
                    x["pS1"], x["pS1b"] = single(4)
                    x["pS2"], x["pS2b"] = single(2)
                    ARt, BTt = x["AR"], x["BT"]
                    for hh in range(2):
                        zt, zb = x["zts"][hh]
                        S.add("pe", lambda e, pS=x["pS1"], zt=zt, ARt=ARt, hh=hh: e.matmul(
                            pS[:, hh * 128:(hh + 1) * 128], lhsT=zt[:, 1, csl], rhs=ARt[:, c, 0, :], start=True, stop=True), [zb, x["ARb"]], x["pS1b"])
                    for hh in range(2):
                        zt, zb = x["zts"][hh]
                        S.add("pe", lambda e, pS=x["pS1"], zt=zt, BTt=BTt, hh=hh: e.matmul(
                            pS[:, (2 + hh) * 128:(3 + hh) * 128], lhsT=zt[:, 0, csl], rhs=BTt[:, csl], start=True, stop=True), [zb, x["BTb"]], x["pS1b"])
                    for hh in range(2):
                        zt, zb = x["zts"][hh]
                        S.add("pe", lambda e, pS=x["pS2"], zt=zt, ARt=ARt, hh=hh: e.matmul(
                            pS[:, hh * 128:(hh + 1) * 128], lhsT=zt[:, 2, csl], rhs=ARt[:, c, 0, :], start=True, stop=True), [zb, x["ARb"]], x["pS2b"])

                def g_sprod_evac(x):
                    gi = x["gi"]
                    nm, nmb = NMt(gi * 2)
                    mk, mkb = Mak(gi)
                    S.add("dve", lambda e, nm=nm, pS=x["pS1"]: e.tensor_tensor(out=nm[:, :, :, :].rearrange("p a h t -> p (a h) t"), in0=v3(pS, 4), in1=mask4, op=ALU.mult),
                          x["pS1b"] + [CB], [nmb])
                    S.add("dve", lambda e, mk=mk, pS=x["pS2"]: e.tensor_tensor(out=mk[:], in0=v3(pS, 2), in1=mask4[:, 0:2, :], op=ALU.mult),
                          x["pS2b"] + [CB], [mkb])
                    x["nm"], x["nmb"], x["mk"], x["mkb"] = nm, nmb, mk, mkb

                def g_r_pe(x, c=c, csl=csl):
                    x["pR"], x["pRb"] = single(4)
                    ARt = x["AR"]
                    for a_ in range(2):
                        for hh in range(2):
                            zt, zb = x["zts"][hh]
                            S.add("pe", lambda e, pR=x["pR"], zt=zt, ARt=ARt, hh=hh, a_=a_: e.matmul(
                                pR[:, (a_ * 2 + hh) * 128:(a_ * 2 + hh + 1) * 128], lhsT=zt[:, 1 + a_, csl], rhs=ARt[:, c, 1, :], start=True, stop=True),
                                [zb, x["ARb"]], x["pRb"])

                def g_r_evac(x, c=c, csl=csl):
                    rbk, rbkb = RBK(x["gi"])
                    for a_ in range(2):
                        S.add("dve", lambda e, rbk=rbk, pR=x["pR"], a_=a_: e.tensor_tensor(out=rbk[:, a_, :, :], in0=v3(pR[:, a_ * 256:(a_ + 1) * 256], 2), in1=mle2, op=ALU.mult),
                              x["pRb"] + [CB], [rbkb])
                    x["rbk"], x["rbkb"] = rbk, rbkb

                def g_pv_pe(x, c=c, csl=csl):
                    x["pV"], x["pVb"] = single(1)
                    mk, tmt = x["mk"], x["tm"]
                    for hh in range(2):
                        S.add("pe", lambda e, pV=x["pV"], hh=hh, mk=mk, tmt=tmt: e.matmul(
                            pV[:, hh * 64:(hh + 1) * 64], lhsT=mk[:, hh, :], rhs=tmt[:, 3, hh * 64:(hh + 1) * 64], start=True, stop=True), [x["mkb"], x["tmb"]], x["pVb"])

                def g_x0(x, c=c, csl=csl):
                    Xt, Xb = Xtile(x["gi"] * 2)
                    tmt = x["tm"]
                    S.add("pool", lambda e, Xt=Xt, tmt=tmt: e.tensor_copy(out=Xt[:, :, 0, :], in_=tmt[:, 0, :].rearrange("p (h k) -> p h k", h=2)), [x["tmb"]], [Xb])
                    S.add("act", lambda e, Xt=Xt, pV=x["pV"]: e.activation(out=Xt[:, :, 1, :], in_=pV[:, 0:128].rearrange("p (h k) -> p h k", h=2), func=AF.Copy),
                          x["pVb"], [Xb])
                    x["X"], x["Xb"] = Xt, Xb

                def g_level_pe(x, lv):
                    nm, nmb, Xt, Xb = x["nm"], x["nmb"], x["X"], x["Xb"]
                    x["pX"], x["pXb"] = single(2)
                    for hh in range(2):
                        S.add("pe", lambda e, pX=x["pX"], hh=hh, nm=nm, Xt=Xt: e.matmul(
                            pX[:, hh * 128:(hh + 1) * 128], lhsT=nm[:, 0, hh, :], rhs=Xt[:, hh, :, :].rearrange("p a k -> p (a k)"), start=True, stop=True),
                            [nmb, Xb], x["pXb"])
                    if lv < 6:
                        x["pNM"], x["pNMb"] = single(4)
                        for hh in range(2):
                            S.add("pe", lambda e, pN=x["pNM"], hh=hh, nm=nm: e.matmul(
                                pN[:, hh * 128:(hh + 1) * 128], lhsT=nm[:, 1, hh, :], rhs=nm[:, 0, hh, :], start=True, stop=True), [nmb], x["pNMb"])
                        if lv < 5:
                            for hh in range(2):
                                S.add("pe", lambda e, pN=x["pNM"], hh=hh, nm=nm: e.matmul(
                                    pN[:, (2 + hh) * 128:(3 + hh) * 128], lhsT=nm[:, 0, hh, :], rhs=nm[:, 1, hh, :], start=True, stop=True), [nmb], x["pNMb"])

                def g_level_evac(x, lv):
                    gi = x["gi"]
                    Xt, Xb = x["X"], x["Xb"]
                    Xn, Xnb = Xtile(gi * 2 + lv + 1)
                    S.add("dve", lambda e, Xn=Xn, pX=x["pX"], Xt=Xt: e.tensor_tensor(
                        out=Xn[:, :, :, :].rearrange("p h a k -> p h (a k)"), in0=v3(pX), in1=Xt[:, :, :, :].rearrange("p h a k -> p h (a k)"), op=ALU.add),
                        x["pXb"] + [Xb], [Xnb])
                    x["X"], x["Xb"] = Xn, Xnb
                    if lv < 6:
                        nn, nnb = NMt(gi * 2 + lv + 1)
                        w = 4 if lv < 5 else 2
                        S.add("act", lambda e, nn=nn, pN=x["pNM"], w=w: e.activation(
                            out=nn[:, :, :, :].rearrange("p a h t -> p (a h) t")[:, 0:w, :], in_=v3(pN[:, 0:w * 128], w), func=AF.Copy), x["pNMb"], [nnb])
                        x["nm"], x["nmb"] = nn, nnb

                def g_state(x, c=c, csl=csl, own=own, pYbank=(pYbank if own else None), pYbb=(pYbb if own else None)):
                    gi, hp, si = x["gi"], x["hp"], x["si"]
                    Xt, Xb, tmt, tmb = x["X"], x["Xb"], x["tm"], x["tmb"]
                    ARt, ARb = x["AR"], x["ARb"]
                    wz, wzb = Wz(gi)
                    S.add("pool", lambda e, wz=wz, Xt=Xt: e.tensor_copy(out=wz[:, 0::2, :], in_=Xt[:, :, 0, :]), [Xb], [wzb])
                    wzA = wz[:, 0:2, :].rearrange("p a k -> p (a k)")
                    wzB = wz[:, 1:3, :].rearrange("p a k -> p (a k)")
                    wp, wpb = Wp(gi)
                    S.add("pool", lambda e, wp=wp, Xt=Xt: e.tensor_copy(out=wp[:, :, :], in_=Xt[:, :, 0, :]), [Xb], [wpb])
                    pAT, pATb = single(1)
                    S.add("pe", lambda e, pAT=pAT, wp=wp, tmt=tmt: e.matmul(pAT[:, 0:128], lhsT=wp[:, :, :].rearrange("p a k -> p (a k)"), rhs=tmt[:, 1, :], start=True, stop=True),
                          [wpb, tmb], pATb)
                    atb, atbb = ATbd(gi)
                    for hh in range(2):
                        S.add("dve", lambda e, atb=atb, pAT=pAT, hh=hh: e.tensor_copy(
                            out=atb[P_[hh], hh * 64:(hh + 1) * 64], in_=pAT[P_[hh], hh * 64:(hh + 1) * 64]), pATb, [atbb])
                    pG, pGb = single(1)
                    pG2, pG2b = single(1)
                    S.add("pe", lambda e, pG=pG, tmt=tmt: e.matmul(pG[:, 0:128], lhsT=tmt[:, 2, :], rhs=tmt[:, 3, :], start=True, stop=True),
                          [tmb], pGb)
                    for hh in range(2):
                        S.add("pe", lambda e, pG2=pG2, Xt=Xt, tmt=tmt, hh=hh: e.matmul(
                            pG2[:, hh * 64:(hh + 1) * 64], lhsT=tmt[:, 1, :], rhs=Xt[:, hh, 1, :], start=True, stop=True), [Xb, tmb], pG2b)
                    gs, gsb_ = Gsb(gi)
                    for hh in range(2):
                        S.add("act", lambda e, gs=gs, pG=pG, hh=hh: e.activation(
                            out=gs[P_[hh], :], in_=pG[P_[hh], hh * 64:(hh + 1) * 64], func=AF.Copy), pGb, [gsb_])
                        S.add("dve", lambda e, gs=gs, pG2=pG2, hh=hh: e.tensor_tensor(
                            out=gs[P_[hh], :], in0=pG2[P_[hh], hh * 64:(hh + 1) * 64], in1=gs[P_[hh], :], op=ALU.add), pG2b + [gsb_], [gsb_])
                    Htt, Hb_ = Ht(hp)
                    gct, gcb = gC(si)
                    if own:
                        hbz = [Hbfz(gi * 2 + hh) for hh in range(2)]
                        for hh in range(2):
                            S.add("pool", lambda e, hz=hbz[hh][0], Htt=Htt, hh=hh: e.tensor_copy(out=hz[P_[hh], :], in_=Htt[P_[hh], :]), [Hb_], [hbz[hh][1]])
                    hhl, hhlb = Hhl(gi)
                    S.add("pool", lambda e, hhl=hhl, Htt=Htt: e.tensor_copy(out=hhl[:, 0, :], in_=Htt[:]), [Hb_], [hhlb])
                    S.add("pool", lambda e, hhl=hhl, Htt=Htt: e.tensor_tensor(out=hhl[:, 1, :], in0=Htt[:], in1=hhl[:, 0, :], op=ALU.subtract), [Hb_, hhlb], [hhlb])
                    pZ, pZb = single(1)
                    S.add("pe", lambda e, pZ=pZ, atb=atb, hhl=hhl: e.matmul(pZ[:, 0:128], lhsT=atb[:], rhs=hhl[:, :, :].rearrange("p a v -> p (a v)"), start=True, stop=True),
                          [atbb, hhlb], pZb)
                    s1, s1b = s1t(gi)
                    S.add("pool", lambda e, s1=s1, Htt=Htt, gs=gs: e.tensor_tensor(out=s1[:], in0=Htt[:], in1=gs[:], op=ALU.add), [Hb_, gsb_], [s1b])
                    S.add("pool", lambda e, s1=s1, gct=gct: e.tensor_scalar(out=s1[:], in0=s1[:], scalar1=gct[:, c:c + 1], scalar2=1.0, op0=ALU.mult, op1=ALU.mult),
                          [s1b, gcb], [s1b])
                    S.add("dve", lambda e, pZ=pZ, gct=gct, s1=s1: e.scalar_tensor_tensor(
                        out=s1[:], in0=pZ[:, 0:64], scalar=gct[:, c:c + 1], in1=s1[:], op0=ALU.mult, op1=ALU.add), pZb + [gcb, s1b], [s1b])
                    S.add("dve", lambda e, Htt=Htt, pZ=pZ, gct=gct, s1=s1: e.scalar_tensor_tensor(
                        out=Htt[:], in0=pZ[:, 64:128], scalar=gct[:, c:c + 1], in1=s1[:], op0=ALU.mult, op1=ALU.add), pZb + [gcb, s1b], [Hb_])
                    if own:
                        rbk, rbkb = x["rbk"], x["rbkb"]
                        qt_, qtb = QT(gi)
                        for hh, wzX in enumerate((wzA, wzB)):
                            pQ, pQb = single(1)
                            S.add("pe", lambda e, pQ=pQ, wzX=wzX, rbk=rbk, hh=hh: e.matmul(pQ[:, 0:128], lhsT=wzX, rhs=rbk[:, 0, hh, :], start=True, stop=True),
                                  [wzb, rbkb], pQb)
                            S.add("dve", lambda e, qt_=qt_, pQ=pQ, ARt=ARt, hh=hh: e.tensor_tensor(
                                out=qt_[P_[hh], :], in0=pQ[P_[hh], 0:128], in1=ARt[P_[hh], c, 1, :], op=ALU.add), pQb + [ARb], [qtb])
                        for hh in range(2):
                            pY, pYb = pYbank, pYbb
                            ysl = slice((hp * 2 + hh) * 64, (hp * 2 + hh + 1) * 64)
                            S.add("pe", lambda e, pY=pY, ysl=ysl, hh=hh, rbk=rbk, Xt=Xt: e.matmul(
                                pY[:, ysl], lhsT=rbk[:, 0, hh, :], rhs=Xt[:, hh, 1, :], start=True, stop=False), [rbkb, Xb], pYb)
                            S.add("pe", lambda e, pY=pY, ysl=ysl, hh=hh, rbk=rbk, tmt=tmt: e.matmul(
                                pY[:, ysl], lhsT=rbk[:, 1, hh, :], rhs=tmt[:, 3, hh * 64:(hh + 1) * 64], start=False, stop=False), [rbkb, tmb], pYb)
                            S.add("pe", lambda e, pY=pY, ysl=ysl, qt_=qt_, hz=hbz[hh][0]: e.matmul(
                                pY[:, ysl], lhsT=qt_[:, :], rhs=hz[:, :], start=False, stop=True), [qtb, hbz[hh][1]], pYb)

                for pr in pairs:
                    ctxs = [mkctx(hp) for hp in pr]
                    for x in ctxs:
                        g_transposes(x)
                    for x in ctxs:
                        g_sprod_pe(x)
                    for x in ctxs:
                        g_sprod_evac(x)
                    fill()
                    if own:
                        for x in ctxs:
                            g_r_pe(x)
                        for x in ctxs:
                            g_r_evac(x)
                    for x in ctxs:
                        g_pv_pe(x)
                    for x in ctxs:
                        g_x0(x)
                    fill()
                    for lv in range(7):
                        for x in ctxs:
                            g_level_pe(x, lv)
                        for x in ctxs:
                            g_level_evac(x, lv)
                        fill()
                    for x in ctxs:
                        g_state(x)
            if not own:
                if halo_sb:
                    drain(gen_ownproj())
                g2 = gen_second(sbi)
                f2 = mkfill(g2, units=23, slots=17)
                for c in range(CPS):
                    emit_chunk_pairs(c, [PAIRS[0]], f2)
                drain(g2)
                g1 = gen_first_ab(sbi + 1) if sbi + 1 < nsb else iter(())
                f1 = mkfill(g1, units=28, slots=17)
                for c in range(CPS):
                    emit_chunk_pairs(c, [PAIRS[1]], f1)
                drain(g1)
                continue
            fb_mode[0] = 1
            g1 = iter(())
            for c in range(CPS):
                gch = sbi * CPS + c
                csl = slice(c * C, (c + 1) * C)
                pYbank, pYbb = psA[0], [psA_b[0]]
                if c == 0:
                    g2 = gen_second(sbi)
                    emit_chunk_pairs(c, [PAIRS[0]], mkfill(g2, 2))
                    drain(g2)
                    g3 = gen_ownproj()
                    emit_chunk_pairs(c, [PAIRS[1]], mkfill(g3, 2))
                    drain(g3)
                else:
                    if c == 1 and sbi + 1 < nsb:
                        g1 = gen_first(sbi + 1)
                    emit_chunk_pairs(c, [PAIRS[0]], mkfill(g1, 1))
                    emit_chunk_pairs(c, [PAIRS[1]], mkfill(g1, 1))
                afill = lambda: None
                if c == CPS - 1 and sbi + 1 < nsb:
                    drain(g1)
                    gfb = gen_first_b(sbi + 1)
                    afill = mkfill(gfb, units=13, slots=10)
                lc = (sbi - own0) * CPS + c
                g_t, g_b = gst()
                pY, pYb = pYbank, pYbb
                yq, yqb = ysqt(0)
                S.add("dve", lambda e, g_t=g_t, pY=pY: e.tensor_reduce(out=g_t[:, 0, :], in_=v3(pY[:, 0:512], 8), axis=AX.X, op=ALU.add), pYb, [g_b])
                S.add("act", lambda e, yq=yq, pY=pY: e.activation(out=yq[:], in_=pY[:, 0:512], func=AF.Square), pYb, [yqb])
                S.add("dve", lambda e, g_t=g_t, yq=yq: e.tensor_reduce(out=g_t[:, 1, :], in_=v3(yq[:, :], 8), axis=AX.X, op=ALU.add), [yqb], [g_b])
                S.add("dve", lambda e, g_t=g_t: e.tensor_scalar(out=g_t[:, 2, :], in0=g_t[:, 0, :], scalar1=1.0 / 64, scalar2=None, op0=ALU.mult), [g_b], [g_b])
                S.add("dve", lambda e, g_t=g_t: e.tensor_tensor(out=g_t[:, 3, :], in0=g_t[:, 2, :], in1=g_t[:, 2, :], op=ALU.mult), [g_b], [g_b])
                S.add("dve", lambda e, g_t=g_t: e.scalar_tensor_tensor(out=g_t[:, 4, :], in0=g_t[:, 1, :], scalar=1.0 / 64, in1=g_t[:, 3, :],
                                                                       op0=ALU.mult, op1=ALU.subtract), [g_b], [g_b])
                S.add("act", lambda e, g_t=g_t: e.activation(out=g_t[:, 5, :], in_=g_t[:, 4, :], func=AF.Ln, bias=gneps_col), [g_b, KB], [g_b])
                S.add("act", lambda e, g_t=g_t: e.activation(out=g_t[:, 5, :], in_=g_t[:, 5, :], func=AF.Exp, scale=-0.5), [g_b], [g_b])
                ynt, ynb = yn()
                S.add("dve", lambda e, yq=yq, pY=pY, g_t=g_t: e.tensor_tensor(
                    out=v3(yq[:, :], 8), in0=v3(pY[:, 0:512], 8), in1=bcl(g_t[:, 2, :], 64), op=ALU.subtract), pYb + [g_b, yqb], [yqb])
                S.add("pool", lambda e, ynt=ynt, yq=yq, g_t=g_t: e.tensor_tensor(
                    out=v3(ynt[:, :], 8), in0=v3(yq[:, :], 8), in1=bcl(g_t[:, 5, :], 64), op=ALU.mult), [yqb, g_b], [ynb])
                pt_ = psT[1][:, 0:512]
                ptb = psT_b[1]
                for hp in range(4):
                    S.add("pe", lambda e, pt_=pt_, hp=hp, ynt=ynt: e.transpose(out=pt_[:, hp * 128:(hp + 1) * 128], in_=ynt[:, hp * 128:(hp + 1) * 128], identity=ident_b),
                          [ynb, CB], ptb)
                for hp in range(4):
                    si = sbi * 4 + hp
                    t1, t1b = t1t(hp)
                    S.add("dve", lambda e, t1=t1, pt_=pt_, hp=hp: e.tensor_scalar(out=t1[:], in0=pt_[:, hp * 128:(hp + 1) * 128], scalar1=pc("gnw", hp), scalar2=pc("gnb", hp),
                                                                            op0=ALU.mult, op1=ALU.add), ptb + [PB], [t1b])
                    S.add("pool", lambda e, t1=t1, si=si, csl=csl: e.tensor_tensor(out=t1[:], in0=t1[:], in1=bonus(si)[0][:, csl], op=ALU.add), [t1b, bonus(si)[1]], [t1b])
                    S.add("pool", lambda e, t1=t1, si=si, csl=csl: e.tensor_tensor(out=zr(si)[0][:, csl], in0=t1[:], in1=sgr(si)[0][:, csl], op=ALU.mult),
                          [t1b, sgr(si)[1]], [zr(si)[1]])
                if lc == NOC - 1:
                    dump("zr0", zr(sbi * 4)[0][:], [128, SBT], [zr(sbi * 4)[1]], BF16)
                if upto < 4:
                    continue
                am_i = 0 if lc == 0 else 1
                pObank, pObb = psA[0], [psA_b[0]]
                vprev, vprevb = Vpad(lc)
                vcur, vcurb = Vpad(lc + 1)
                for qp0 in (0, 2):
                    pts = psT[0]
                    hs = []
                    for qp in (qp0, qp0 + 1):
                        si = sbi * 4 + qp
                        g = qp // 2
                        for hh in range(2):
                            hd = qp * 2 + hh
                            pS, pSb_ = single(2)
                            qz, qzb = qTz(si * 2 + hh)
                            for kk_ in range(2):
                                ks = (lc + kk_) % NKS
                                S.add("pe", lambda e, pS=pS, qz=qz, g=g, ks=ks, kk_=kk_, csl=csl: e.matmul(
                                    pS[:, kk_ * 128:(kk_ + 1) * 128], lhsT=qz[:, csl], rhs=KTatt.t[g][:, ks * C:(ks + 1) * C], start=True, stop=True),
                                    [qzb, KTatt.b[g]], pSb_)
                            j4 = (qp - qp0) * 2 + hh
                            hs.append(dict(hd=hd, qp=qp, hh=hh, g=g, si=si, pS=pS, pSb=pSb_, sm=smt(j4), a=ast(j4), p3=p32(j4), pn=pnt(j4), pt=ptt(j4),
                                           ptsl=pts[:, j4 * 256:(j4 + 1) * 256]))
                    for h in hs:
                        S.add("dve", lambda e, sm=h["sm"][0], pS=h["pS"], am_i=am_i: e.scalar_tensor_tensor(
                            out=sm[:], in0=pS[:, 0:256], scalar=0.125, in1=amask.t[0][:, am_i, :], op0=ALU.mult, op1=ALU.add), h["pSb"] + [amask.b[0]], [h["sm"][1]])
                    afill()
                    for h in hs:
                        S.add("dve", lambda e, a_t=h["a"][0], sm=h["sm"][0]: e.tensor_reduce(out=a_t[:, 0:1], in_=sm[:], axis=AX.X, op=ALU.max), [h["sm"][1]], [h["a"][1]])
                    for h in hs:
                        S.add("dve", lambda e, a_t=h["a"][0], hd=h["hd"]: e.tensor_scalar(out=a_t[:, 1:2], in0=a_t[:, 0:1], scalar1=pc("sink", hd), scalar2=-1.0, op0=ALU.max, op1=ALU.mult),
                              [h["a"][1], PB], [h["a"][1]])
                    afill()
                    for h in hs:
                        S.add("act", lambda e, pp3=h["p3"][0], sm=h["sm"][0], a_t=h["a"][0]: e.activation(out=pp3[:], in_=sm[:], func=AF.Exp, bias=a_t[:, 1:2], accum_out=a_t[:, 2:3]),
                              [h["sm"][1], h["a"][1]], [h["p3"][1], h["a"][1]])
                    for h in hs:
                        S.add("act", lambda e, a_t=h["a"][0], hd=h["hd"]: e.activation(out=a_t[:, 3:4], in_=pc("sink", hd), func=AF.Exp, bias=a_t[:, 1:2]), [h["a"][1], PB], [h["a"][1]])
                    for h in hs:
                        S.add("dve", lambda e, a_t=h["a"][0]: e.tensor_tensor(out=a_t[:, 4:5], in0=a_t[:, 2:3], in1=a_t[:, 3:4], op=ALU.add), [h["a"][1]], [h["a"][1]])
                    for h in hs:
                        S.add("dve", lambda e, a_t=h["a"][0]: e.reciprocal(out=a_t[:, 5:6], in_=a_t[:, 4:5]), [h["a"][1]], [h["a"][1]])
                    afill()
                    for h in hs:
                        S.add("dve", lambda e, pn=h["pn"][0], pp3=h["p3"][0], a_t=h["a"][0]: e.tensor_scalar(out=pn[:], in0=pp3[:], scalar1=a_t[:, 5:6], scalar2=None, op0=ALU.mult),
                              [h["p3"][1], h["a"][1]], [h["pn"][1]])
                    for h in hs:
                        for kk_ in range(2):
                            S.add("pe", lambda e, ptsl=h["ptsl"], kk_=kk_, pn=h["pn"][0]: e.transpose(out=ptsl[:, kk_ * 128:(kk_ + 1) * 128], in_=pn[:, kk_ * 128:(kk_ + 1) * 128], identity=ident_b),
                                  [h["pn"][1], CB], psT_b[0])
                    afill()
                    for h in hs:
                        S.add("act", lambda e, pt2=h["pt"][0], ptsl=h["ptsl"]: e.activation(out=pt2[:], in_=v3(ptsl), func=AF.Copy), psT_b[0], [h["pt"][1]])
                    for qp in (qp0, qp0 + 1):
                        si = sbi * 4 + qp
                        g = qp // 2
                        pO, pOb = pObank[:, qp * 128:(qp + 1) * 128], pObb
                        n_ = 0
                        for h in [h for h in hs if h["qp"] == qp]:
                            pt2, pt2b = h["pt"]
                            hh = h["hh"]
                            for kk_, (vp, vpb) in enumerate([(vprev, vprevb), (vcur, vcurb)]):
                                S.add("pe", lambda e, pO=pO, vp=vp, g=g, hh=hh, pt2=pt2, kk_=kk_, n_=n_: e.matmul(
                                    pO[:, 0:128], lhsT=vp[:, g, hh * 64:hh * 64 + 128], rhs=pt2[:, kk_, :], start=(n_ == 0), stop=(n_ == 3)),
                                    [vpb, pt2b], pOb)
                                n_ += 1
                    afill()
                    for qp in (qp0, qp0 + 1):
                        si = sbi * 4 + qp
                        pO, pOb = pObank[:, qp * 128:(qp + 1) * 128], pObb
                        S.add("dve", lambda e, si=si, pO=pO, csl=csl: e.tensor_tensor(out=zatt(si)[0][:, csl], in0=pO[:, 0:128], in1=sga(si)[0][:, csl], op=ALU.mult),
                              pOb + [sga(si)[1]], [zatt(si)[1]])
                if lc == NOC - 1:
                    dump("za0", zatt(sbi * 4)[0][:], [128, SBT], [zatt(sbi * 4)[1]], BF16)
            if not own or upto < 5:
                continue
            drain(g1)
            fb_mode[0] = 0
            if sbi + 1 >= nsb:
                gfb = iter(())
            ffb = lambda: None
            mTt, mTb = mT()
            def load_j(j):
                return [ws_load([(lambda t: t[:, :, :], wbb[:, j * 128:(j + 1) * 128].rearrange("(bh p) c -> p bh c", p=128))]),
                        ws_load([(lambda t: t[:, :, :], wcols(O_GT + j * 128))]),
                        ws_load([(lambda t: t[:, :, :], wcols(O_GT + 1024 + j * 128))])]
            for j in range(8):
                cur_w = load_j(j)
                wt, wb = cur_w[0]
                pBr, pBrb = fullbank()
                pBa, pBab = fullbank()
                for hp in range(4):
                    si = sbi * 4 + hp
                    S.add("pe", lambda e, pBr=pBr, wt=wt, hp=hp, si=si: e.matmul(pBr[:, 0:SBT], lhsT=wt[:, hp, :], rhs=zr(si)[0][:], start=(hp == 0), stop=(hp == 3)),
                          wb + [zr(si)[1]], pBrb)
                for hp in range(4):
                    si = sbi * 4 + hp
                    S.add("pe", lambda e, pBa=pBa, wt=wt, hp=hp, si=si: e.matmul(pBa[:, 0:SBT], lhsT=wt[:, 4 + hp, :], rhs=zatt(si)[0][:], start=(hp == 0), stop=(hp == 3)),
                          wb + [zatt(si)[1]], pBab)
                halves = []
                for br in range(2):
                    wt2, wb2 = cur_w[1 + br]
                    pGt, pGtb = bankx()
                    for k in range(8):
                        S.add("pe", lambda e, pGt=pGt, k=k, wt2=wt2, hTt=hTt: e.matmul(
                            pGt[:, 0:SBT], lhsT=wt2[:, k, :], rhs=hTt[:, k, :], start=(k == 0), stop=(k == 7)), wb2 + [hTb], pGtb)
                    sg_, sgb_ = sgt(br)
                    S.add("act", lambda e, sg_=sg_, pGt=pGt: e.activation(out=sg_[:], in_=pGt[:, 0:SBT], func=AF.Sigmoid), pGtb, [sgb_])
                    halves.append((sg_, sgb_))
                m1, m1b = m12(0)
                m2, m2b = m12(1)
                S.add("dve", lambda e, m1=m1, pBr=pBr, sg_=halves[0][0]: e.tensor_tensor(out=m1[:], in0=pBr[:, 0:SBT], in1=sg_[:], op=ALU.mult), pBrb + [halves[0][1]], [m1b])
                S.add("dve", lambda e, m2=m2, pBa=pBa, sg_=halves[1][0]: e.tensor_tensor(out=m2[:], in0=pBa[:, 0:SBT], in1=sg_[:], op=ALU.mult), pBab + [halves[1][1]], [m2b])
                S.add("dve", lambda e, mTt=mTt, j=j, m1=m1, m2=m2: e.tensor_tensor(out=mTt[:, j, :], in0=m1[:], in1=m2[:], op=ALU.add), [m1b, m2b], [mTb])
                ffb()
            for c in range(CPS):
                gch = sbi * CPS + c
                lc = (sbi - own0) * CPS + c
                xr, xrb = xt(gch)
                dma("sp", xr[:], xw[gch * C:(gch + 1) * C, :], [], [xrb])
                for n in range(2):
                    pa, pab = fullbank()
                    for j in range(8):
                        S.add("pe", lambda e, pa=pa, j=j, n=n, mTt=mTt, c=c: e.matmul(
                            pa[:, 0:512], lhsT=mTt[:, j, c * C:(c + 1) * C], rhs=Wout.t[0][:, j, n * 512:(n + 1) * 512], start=(j == 0), stop=(j == 7)),
                            [mTb, Wout_b[j]], pab)
                    S.add("dve", lambda e, xr=xr, pa=pa, n=n: e.tensor_tensor(out=xr[:, n * 512:(n + 1) * 512], in0=pa[:, 0:512], in1=xr[:, n * 512:(n + 1) * 512], op=ALU.add),
                          pab + [xrb], [xrb])
                ft_, fb_ = fst(gch)
                hbt, hbb = hb(gch)
                rms_rstd(xr[:], [xrb], ft_, fb_, hbt[:], hbb)
                S.add("dve", lambda e, xr=xr, ft_=ft_: e.scalar_tensor_tensor(
                    out=xr[:], in0=xr[:], scalar=ft_[:, 2:3], in1=gfin.t[0][:], op0=ALU.mult, op1=ALU.mult), [xrb, fb_, gfin.b[0]], [xrb])
                dma("sp", out_d[lc * C:(lc + 1) * C, :], xr[:], [xrb], [], is_out=True)
            drain(gfb)

        for hp in range(4):
            dump(f"H{hp}", Ht(hp)[0][:], [128, 64], [Ht(hp)[1]])

        semnames = list(Sched.ENG) + [("dma", j) for j in range(Sched.NDMA)]
        sems = {}
        for sk in semnames:
            nm = sk if isinstance(sk, str) else f"dma{sk[1]}"
            sems[sk] = es.enter_context(nc.semaphore("s_" + nm))
        nc._sbuf_left = nc.sbuf_bytes_remaining
        block = es.enter_context(nc.Block())
        S.emit(nc, block, sems)
    nc._dbg_dumps = dump_d
    nc._sched_counts = dict(S.cnt)
    nc._sched_total = S.total
    return nc


def host_consts():
    s = np.arange(128)[:, None]
    t = np.arange(128)[None, :]
    cst = np.zeros((128, 9, 128), np.float32)
    cst[:, 0] = (s == t)
    cst[:, 1] = (s < t)
    cst[:, 2] = (s < t)
    cst[:, 3] = (s > t)
    cst[:, 4] = (s > t)
    cst[:, 5] = (s <= t)
    cst[:, 6] = (s <= t)
    cst[:, 7] = ((s // 64) == (t // 64))
    cst[:, 8] = 1.0
    return cst


def attn_masks(first):
    qi = np.arange(128)[:, None]
    kj = np.arange(256)[None, :]
    dist = qi + 128 - kj
    band = (dist >= 0) & (dist < 128)
    rest = np.where(band, 0.0, -1e30).astype(np.float32)
    fm = np.where(band & (kj >= 128), 0.0, -1e30).astype(np.float32)
    am = np.stack([fm if first else rest, rest], axis=1)
    return np.ascontiguousarray(am)


def pack_params(p):
    pp = np.zeros((128, NPP_IN), np.float32)

    def put(name, vec, n):
        v = np.asarray(vec, np.float32).reshape(n, 128)
        pp[:, PPI[name]:PPI[name] + n] = v.T
    put("mu", p["mu_shift"][0], 13)
    put("w0", p["w0"][0], 4)
    put("a0", p["a0"][0], 4)
    put("kk", p["k_k"][0], 4)
    put("ka", p["k_a"][0], 4)
    put("rk", p["r_k"][0], 4)
    put("gnw", p["gn_w"][0], 4)
    put("gnb", p["gn_b"][0], 4)
    bq = np.asarray(p["b_qkv"][0], np.float32)
    put("bq", bq[0:512], 4)
    bk = bq[512:640]
    pp[:, PPI["bk"] + 0] = np.concatenate([bk[0:64], bk[0:64]])
    pp[:, PPI["bk"] + 1] = np.concatenate([bk[64:128], bk[64:128]])
    sk = np.asarray(p["sinks"][0], np.float32)
    pp[:, PPI["sink"]:PPI["sink"] + 8] = np.broadcast_to(sk[None, :], (128, 8))
    wdi = np.zeros((128, 2, 512), np.float32)
    wdi[0:64, 0] = np.asarray(p["w_decay_up"][0], np.float32)
    wdi[64:128, 1] = np.asarray(p["w_iclr_up"][0], np.float32)
    common = {
        "w_in": np.ascontiguousarray(np.asarray(p["w_in"][0], np.float32)),
        "w_br": np.ascontiguousarray(np.stack([np.asarray(p["w_branch_rwkv"][0], np.float32),
                                               np.asarray(p["w_branch_att"][0], np.float32)])),
        "w_out": np.ascontiguousarray(np.asarray(p["w_out"][0], np.float32)),
        "wdi": np.ascontiguousarray(wdi),
        "pp": pp,
        "gpre_b": np.ascontiguousarray(np.broadcast_to(np.asarray(p["g_pre"][0], np.float32)[None], (128, D))),
        "gfin_b": np.ascontiguousarray(np.broadcast_to(np.asarray(p["g_final"], np.float32)[None], (128, D))),
        "bv_b": np.ascontiguousarray(np.broadcast_to(bq[640:768][None], (128, 128))),
        "cst": host_consts(),
    }
    return common


def kernel(**inputs):
    x = np.asarray(inputs["x"], np.float32)
    common = pack_params(inputs)
    nc = build()
    in_maps = []
    for c in range(NCORES):
        b, q = c // 4, c % 4
        end = (q + 1) * OWN_TOK
        xw = np.zeros((SEQ, D), np.float32)
        xw[SEQ - end:] = x[b, :end]
        m = dict(common)
        m["xw"] = xw
        m["amask"] = attn_masks(q == 0)
        in_maps.append(m)
    res = run_bass_kernel_spmd(nc, in_maps, core_ids=list(range(NCORES)))
    out = np.zeros((2, SEQ, D), np.float32)
    for c in range(NCORES):
        b, q = c // 4, c % 4
        out[b, q * OWN_TOK:(q + 1) * OWN_TOK] = res.results[c]["out"]
    return out
```

```python
import numpy as np
import concourse.bass as bass
import concourse.mybir as mybir
from concourse.bass_utils import run_bass_kernel_spmd

F32 = mybir.dt.float32
BF16 = mybir.dt.bfloat16
AF = mybir.ActivationFunctionType
ALU = mybir.AluOpType
AX = mybir.AxisListType

D = 1024
NCORES = 8
SEQ = 8192
OWN_TOK = 2048
C = 128
SBT = 256
CPS = SBT // C
RMS_EPS = 1e-6
GN_EPS = 64e-5
IN_COLS = 5504
O_SH = 0
O_GR = 1664
O_Q = 2176
O_K = 2688
O_V = 2816
O_GA = 2944
O_GT = 3456

PPI = {}
_n = 0
for _name, _cnt in [("mu", 13), ("w0", 4), ("a0", 4), ("kk", 4), ("ka", 4), ("rk", 4),
                    ("gnw", 4), ("gnb", 4), ("bq", 4), ("bk", 2), ("sink", 8)]:
    PPI[_name] = _n
    _n += _cnt
NPP_IN = _n
for _name, _cnt in [("omu", 13), ("nw0", 4), ("omka", 4), ("na0", 4)]:
    PPI[_name] = _n
    _n += _cnt
NPP = _n


class Buf:
    __slots__ = ("name", "w", "r", "excl")

    def __init__(self, name, excl=False):
        self.name = name
        self.w = None
        self.r = []
        self.excl = excl


class Sched:
    ENG = ("pe", "act", "dve", "pool", "sp")
    NDMA = 24

    def __init__(self, same_sync=True):
        self.ops = {e: [] for e in self.ENG}
        self.cnt = {e: 0 for e in self.ENG}
        self.waited = {e: {} for e in self.ENG}
        self.same_sync = same_sync
        self.dma_val = [0] * self.NDMA
        self.dma_rr = 0
        self.dma_rr2 = 0
        self.out_tokens = []

    def add(self, eng, fn, reads=(), writes=(), dma=False, is_out=False):
        self.total = getattr(self, "total", 0) + 1
        if not dma and self.total > getattr(self, "cut", 10 ** 9):
            return None
        deps = {}

        def need(tk, hard):
            d = deps.get(tk[0])
            if d is None:
                deps[tk[0]] = [tk[1], tk[2], hard]
            else:
                d[0] = max(d[0], tk[1])
                d[2] = d[2] or hard
        for b in reads:
            if b.w is not None:
                need(b.w, True)
            if b.excl:
                for tk in b.r:
                    need(tk, False)
        for b in writes:
            if b.w is not None:
                need(b.w, True)
            for tk in b.r:
                need(tk, False)
        waits = []
        for semkey, (val, src, hard) in deps.items():
            if src == eng and not isinstance(semkey, tuple):
                if eng in ("pe", "sp"):
                    continue
                if not hard or not self.same_sync:
                    continue
            if self.waited[eng].get(semkey, 0) >= val:
                continue
            self.waited[eng][semkey] = val
            waits.append((semkey, val))
        if dma:
            half = self.NDMA // 2
            if eng == "sp":
                j = self.dma_rr
                self.dma_rr = (self.dma_rr + 1) % half
            else:
                j = half + self.dma_rr2
                self.dma_rr2 = (self.dma_rr2 + 1) % half
            semkey = ("dma", j)
            if self.dma_val[j] > 0 and self.waited[eng].get(semkey, 0) < self.dma_val[j]:
                self.waited[eng][semkey] = self.dma_val[j]
                waits.append((semkey, self.dma_val[j]))
            self.dma_val[j] += 16
            tok = (semkey, self.dma_val[j], eng)
            inc = 16
        else:
            self.cnt[eng] += 1
            tok = (eng, self.cnt[eng], eng)
            inc = 1
        for b in reads:
            b.r.append(tok)
        for b in writes:
            b.w = tok
            b.r = []
        if is_out:
            self.out_tokens.append(tok)
        self.ops[eng].append((waits, fn, tok[0], inc))
        return tok

    def emit(self, nc, block, sems):
        engmap = {"pe": block.tensor, "act": block.scalar, "dve": block.vector,
                  "pool": block.gpsimd, "sp": block.sync}
        for e in self.ENG:
            ops = self.ops[e]
            final = list(self.out_tokens) if e == "sp" else ()

            def body(eng, ops=ops, final=final):
                for waits, fn, semkey, inc in ops:
                    for sk, val in waits:
                        eng.wait_ge(sems[sk], val)
                    fn(eng).then_inc(sems[semkey], inc)
                for tk in final:
                    eng.wait_ge(sems[tk[0]], tk[1])
                if final != ():
                    for j in range(self.NDMA):
                        if self.dma_val[j] > 0:
                            eng.wait_ge(sems[("dma", j)], self.dma_val[j])
            engmap[e](body)


def build(nsb=SEQ // SBT, nown=OWN_TOK // SBT, upto=99, dumps=(), same_sync=True, cut=None):
    from contextlib import ExitStack
    nc = bass.Bass("TRN2", target_bir_lowering=False)
    WT = nsb * SBT
    OT = nown * SBT
    NOC = nown * CPS
    S = Sched(same_sync=same_sync)
    if cut is not None:
        S.cut = cut

    def din(name, shape, dt=F32):
        return nc.dram_tensor(name, list(shape), dt, kind="ExternalInput").ap()

    xw = din("xw", [WT, D])
    w_in = din("w_in", [D, IN_COLS])
    w_br = din("w_br", [2, 512, D])
    w_out = din("w_out", [D, D])
    wdi = din("wdi", [128, 2, 512])
    pp_in = din("pp", [128, NPP_IN])
    gpre_d = din("gpre_b", [128, D])
    gfin_d = din("gfin_b", [128, D])
    bv_d = din("bv_b", [128, 128])
    cst_d = din("cst", [128, 9, 128])
    am_d = din("amask", [128, 2, 256])
    out_d = nc.dram_tensor("out", [OT, D], F32, kind="ExternalOutput").ap()
    wib = nc.dram_tensor("wib_scratch", [D, IN_COLS - O_GR], BF16).ap()
    wbb = nc.dram_tensor("wbb_scratch", [2 * 512, D], BF16).ap()
    dump_d = {}

    es = ExitStack()
    with es:
        def sb(name, shape, dt=F32):
            return es.enter_context(nc.sbuf_tensor(name, list(shape), dt))

        def ps(name, shape, dt=F32):
            return es.enter_context(nc.psum_tensor(name, list(shape), dt))

        class T:
            def __init__(self, name, shape, dt=F32, n=1):
                self.t = [sb(f"{name}{i}", shape, dt) for i in range(n)]
                self.b = [Buf(f"{name}{i}") for i in range(n)]
                self.n = n

            def __call__(self, i=0):
                return self.t[i % self.n], self.b[i % self.n]

        def dma(eng, out, in_, reads, writes, is_out=False):
            return S.add(eng, lambda e: e.dma_start(out=out, in_=in_), reads, writes, dma=True, is_out=is_out)

        def dump(name, ap, shape, reads, dt=F32):
            if name not in dumps:
                return
            dd = nc.dram_tensor("dbg_" + name, list(shape), dt, kind="ExternalOutput").ap()
            dump_d[name] = dd
            dma("sp", dd, ap, reads, [], is_out=True)

        xt = T("xt", [128, D], F32, 2)
        cst_b = T("cst_b", [128, 8, 128], BF16)
        cst2 = T("cst2", [128, 1, 128])
        amask = T("amask", [128, 2, 256])
        PP = T("PP", [128, NPP])
        gpre = T("gpre", [128, D])
        gfin = T("gfin", [128, D])
        bvb = T("bvb", [128, 128])
        Wdb = T("Wdb", [128, 3, 512], BF16)
        Wsh = T("Wsh", [128, 8, 1664], BF16)
        Wout = T("Wout", [128, 8, D], BF16)

        stg = xt.t[1][:, :].rearrange("p (a b) -> p a b", a=8)
        dma("sp", stg, cst_d[:, 0:8, :], [], [xt.b[1]])
        dma("sp", cst2.t[0][:], cst_d[:, 8:9, :], [], [cst2.b[0]])
        dma("sp", PP.t[0][:, 0:NPP_IN], pp_in, [], [PP.b[0]])
        dma("sp", gpre.t[0][:], gpre_d, [], [gpre.b[0]])
        stg_w = xt.t[0][:, :].rearrange("p (a b) -> p a b", a=2)
        dma("sp", stg_w, wdi, [], [xt.b[0]])
        S.add("act", lambda e: e.activation(out=Wdb.t[0][:, 0, :], in_=stg_w[:, 0, :], func=AF.Copy), [xt.b[0]], [Wdb.b[0]])
        S.add("act", lambda e: e.activation(out=Wdb.t[0][:, 2, :], in_=stg_w[:, 1, :], func=AF.Copy), [xt.b[0]], [Wdb.b[0]])
        S.add("dve", lambda e: e.tensor_tensor(out=Wdb.t[0][:, 1, :], in0=stg_w[:, 0, :], in1=Wdb.t[0][:, 0, :], op=ALU.subtract), [xt.b[0], Wdb.b[0]], [Wdb.b[0]])
        Wsh_b = [Buf(f"Wsh_k{k}") for k in range(8)]
        for k in range(8):
            S.add("pool", lambda e, k=k: e.dma_start(out=Wsh.t[0][:, k, :], in_=w_in[k * 128:(k + 1) * 128, O_SH:O_SH + 1664]),
                  [], [Wsh_b[k]], dma=True)
        dma("sp", amask.t[0][:], am_d, [], [amask.b[0]])
        dma("sp", gfin.t[0][:], gfin_d, [], [gfin.b[0]])
        dma("sp", bvb.t[0][:], bv_d, [], [bvb.b[0]])
        S.add("dve", lambda e: e.tensor_copy(out=cst_b.t[0][:], in_=stg), [xt.b[1]], [cst_b.b[0]])
        ident_b = cst_b.t[0][:, 0, :]
        mask4 = cst_b.t[0][:, 1:5, :]
        mle2 = cst_b.t[0][:, 5:7, :]
        bones_b = cst_b.t[0][:, 7, :]
        ones_f = cst2.t[0][:, 0, :]
        CB = cst_b.b[0]
        CF = cst2.b[0]
        ppt = PP.t[0]
        PB = PP.b[0]

        def pc(name, i=0):
            j = PPI[name] + i
            return ppt[:, j:j + 1]

        S.add("dve", lambda e: e.tensor_scalar(out=ppt[:, PPI["omu"]:PPI["omu"] + 13], in0=ppt[:, PPI["mu"]:PPI["mu"] + 13],
                                               scalar1=-1.0, scalar2=1.0, op0=ALU.mult, op1=ALU.add), [PB], [PB])
        S.add("dve", lambda e: e.tensor_scalar(out=ppt[:, PPI["nw0"]:PPI["nw0"] + 4], in0=ppt[:, PPI["w0"]:PPI["w0"] + 4],
                                               scalar1=-1.0, scalar2=None, op0=ALU.mult), [PB], [PB])
        S.add("dve", lambda e: e.tensor_scalar(out=ppt[:, PPI["omka"]:PPI["omka"] + 4], in0=ppt[:, PPI["ka"]:PPI["ka"] + 4],
                                               scalar1=-1.0, scalar2=1.0, op0=ALU.mult, op1=ALU.add), [PB], [PB])
        S.add("dve", lambda e: e.tensor_scalar(out=ppt[:, PPI["na0"]:PPI["na0"] + 4], in0=ppt[:, PPI["a0"]:PPI["a0"] + 4],
                                               scalar1=-1.0, scalar2=None, op0=ALU.mult), [PB], [PB])

        psA = [ps(f"psA{i}", [128, 512]) for i in range(2)]
        psA_b = [Buf(f"psA{i}", True) for i in range(2)]
        psT = [ps(f"psT{i}", [128, 1024], BF16) for i in range(2)]
        psT_b = [[Buf(f"psT{i}_{h}", True) for h in range(2)] for i in range(2)]
        psLU = [[ps(f"psL{i}", [128, 512]), ps(f"psU{i}", [128, 512])] for i in range(2)]
        psLU_b = [[[Buf(f"psLU{i}_{lu}_{s}", True) for s in range(4)] for lu in range(2)] for i in range(2)]
        arr = [0]
        prr = [0]
        srr = [0]

        fb_mode = [0]

        def fullbank():
            if fb_mode[0]:
                return psA[1], [psA_b[1]]
            i = arr[0]
            arr[0] = (i + 1) % 2
            return psA[i], [psA_b[i]]

        def pair(ns):
            r = prr[0]
            if (r % 4) + ns > 4:
                r = (r // 4 + 1) * 4
            r %= 8
            p, s = r // 4, r % 4
            prr[0] = (r + ns) % 8
            sl = slice(s * 128, (s + ns) * 128)
            return (psLU[p][0][:, sl], psLU_b[p][0][s:s + ns], psLU[p][1][:, sl], psLU_b[p][1][s:s + ns])

        brr = [0]

        def bankx():
            i = brr[0]
            brr[0] = (i + 1) % 4
            p, lu = i // 2, i % 2
            return psLU[p][lu], list(psLU_b[p][lu])

        def single(ns):
            bk, bb = bankx()
            return bk[:, 0:ns * 128], bb

        hb = T("hb", [128, D], BF16, 1)
        hT = T("hT", [128, 8, SBT], BF16, 2)
        st0 = T("st0", [128, 4], F32, 2)
        shwa = T("shwa", [128, SBT])
        shtmp = T("shtmp", [128, SBT], F32, 2)
        ptc = [0]
        shr = T("shr", [128, SBT], F32, 2)
        shk = T("shk", [128, SBT], F32, 2)
        shv = T("shv", [128, SBT], F32, 2)
        tw = T("tw", [128, SBT])
        tw_hi = T("tw_hi", [128, SBT], BF16)
        tw_lo = T("tw_lo", [128, SBT], BF16)
        t_k2b = T("t_k2b", [128, SBT], BF16)
        t_rkb = T("t_rkb", [128, SBT], BF16)
        Hhl = T("Hhl", [128, 2, 64], BF16, 4)
        t_e1 = T("t_e1", [128, SBT])
        t_ew = T("t_ew", [128, SBT])
        t_a = T("t_a", [128, SBT])
        t_cs = T("t_cs", [128, SBT])
        t_csp = T("t_csp", [128, SBT])
        t_en = T("t_en", [128, SBT])
        t_ep = T("t_ep", [128, SBT])
        t_k2 = T("t_k2", [128, SBT])
        t_kkn = T("t_kkn", [128, SBT])
        t_ab = T("t_ab", [128, SBT])
        t_f = T("t_f", [128, SBT])
        gC = T("gC", [128, CPS], F32, 8)
        AR = T("AR", [128, CPS, 2, C], BF16, 4)
        BT = T("BT", [128, SBT], BF16, 4)
        KT = T("KT", [128, SBT], BF16, 4)
        vbf = T("vbf", [128, SBT], BF16, 4)
        bonus = T("bonus", [128, SBT], BF16, 4)
        tm = T("tm", [128, 4, 128], BF16, 4)
        PZ = T("PZ", [128, 3, SBT], BF16, 8)
        Hbfz = T("Hbfz", [128, 64], BF16, 8)
        qTz = T("qTz", [128, SBT], BF16, 8)
        NG = 4
        NMt = T("NM", [128, 2, 2, 128], BF16, 2 * NG)
        Mak = T("Mak", [128, 2, 128], BF16, NG)
        RBK = T("RBK", [128, 2, 2, 128], BF16, NG)
        PAIRS = [(0, 1), (2, 3)]
        Xtile = T("Xt", [128, 2, 2, 64], BF16, 2 * NG)
        ATbd = T("ATbd", [128, 128], BF16, NG)
        Gsb = T("Gsb", [128, 64], F32, NG)
        Ht = T("Hst", [128, 64], F32, 4)
        s1t = T("s1t", [128, 64], F32, NG)
        Wz = T("Wz", [128, 3, 64], BF16, NG)
        Wp = T("Wp", [128, 2, 64], BF16, NG)
        zlo = T("zlo", [128, 64], F32, NG)
        QT = T("QT", [128, 128], BF16, NG)
        prevcol = T("prevcol", [128, 13], F32, 2)
        kc = T("kcols", [128, 8])
        NWS = 5
        ws = T("ws", [128, 8, 128], BF16, NWS)
        ws_b2 = [Buf(f"ws_b2_{i}") for i in range(NWS)]
        wsrr = [0]
        sgr = T("sgr", [128, SBT], BF16, 4)
        sga = T("sga", [128, SBT], BF16, 4)
        NKS = 4
        KTatt = T("KTatt", [128, NKS * 128], BF16, 2)
        NV = 4
        Vpad = T("Vpad", [128, 2, 192], BF16, NV)
        ysqt = T("ysq", [128, 512], F32, 1)
        yn = T("yn", [128, 512], BF16, 1)
        gst = T("gst", [128, 6, 8], F32, 1)
        t1t = T("t1t", [128, 128], F32, 1)
        zr = T("zr", [128, SBT], BF16, 4)
        zatt = T("zatt", [128, SBT], BF16, 4)
        smt = T("smt", [128, 256], F32, 4)
        p32 = T("p32", [128, 256], F32, 4)
        pnt = T("pnt", [128, 256], BF16, 4)
        ptt = T("ptt", [128, 2, 128], BF16, 4)
        ast = T("ast", [128, 8], F32, 4)
        mT = T("mT", [128, 8, SBT], BF16, 1)
        sgt = T("sgt", [128, SBT], F32, 2)
        m12 = T("m12", [128, SBT], F32, 2)
        fst = T("fst", [128, 4], F32, 2)

        S.add("pool", lambda e: e.memset(prevcol.t[0][:], 0.0), [], [prevcol.b[0]])
        S.add("pool", lambda e: e.memset(prevcol.t[1][:], 0.0), [], [prevcol.b[1]])
        kct = kc.t[0]
        KB = kc.b[0]
        for j, val in enumerate([RMS_EPS, 1.0, -0.5, 1e-12, GN_EPS]):
            S.add("pool", lambda e, j=j, val=val: e.memset(kct[:, j:j + 1], val), [], [KB])
        eps_col = kct[:, 0:1]
        one_col = kct[:, 1:2]
        mhalf_col = kct[:, 2:3]
        tiny_col = kct[:, 3:4]
        gneps_col = kct[:, 4:5]
        for i in range(NG):
            S.add("pool", lambda e, i=i: e.memset(ATbd.t[i][:], 0.0), [], [ATbd.b[i]])
            S.add("pool", lambda e, i=i: e.memset(Wz.t[i][:], 0.0), [], [Wz.b[i]])
        for i in range(4):
            S.add("pool", lambda e, i=i: e.memset(Ht.t[i][:], 0.0), [], [Ht.b[i]])
        for i in range(NV):
            S.add("pool", lambda e, i=i: e.memset(Vpad.t[i][:], 0.0), [], [Vpad.b[i]])
        for i in range(8):
            S.add("pool", lambda e, i=i: e.memset(PZ.t[i][:], 0.0), [], [PZ.b[i]])
            S.add("pool", lambda e, i=i: e.memset(Hbfz.t[i][:], 0.0), [], [Hbfz.b[i]])
            S.add("pool", lambda e, i=i: e.memset(qTz.t[i][:], 0.0), [], [qTz.b[i]])
        for i in range(2):
            S.add("pool", lambda e, i=i: e.memset(KTatt.t[i][:], 0.0), [], [KTatt.b[i]])
        Wout_b = [Buf(f"wout{k}") for k in range(8)]
        for k in range(8):
            S.add("pool", lambda e, k=k: e.dma_start(out=Wout.t[0][:, k, :], in_=w_out[k * 128:(k + 1) * 128, :]),
                  [], [Wout_b[k]], dma=True)

        wib_b = [Buf(f"wib{k}") for k in range(8)]
        wbb_b = [Buf(f"wbb{k}") for k in range(8)]
        w_br_flat = w_br.rearrange("b r c -> (b r) c")
        for k in range(8):
            S.add("pool", lambda e, k=k: e.dma_start(out=wib[k * 128:(k + 1) * 128, :], in_=w_in[k * 128:(k + 1) * 128, O_GR:IN_COLS]),
                  [], [wib_b[k]], dma=True)
        for k in range(8):
            S.add("pool", lambda e, k=k: e.dma_start(out=wbb[k * 128:(k + 1) * 128, :], in_=w_br_flat[k * 128:(k + 1) * 128, :]),
                  [], [wbb_b[k]], dma=True)

        def bcm(ap2, n):
            a = ap2.ap
            return bass.AP(ap2.tensor, ap2.offset, [list(a[0]), [0, n], list(a[1])])

        def bcl(ap2, n):
            a = ap2.ap
            return bass.AP(ap2.tensor, ap2.offset, [list(a[0]), list(a[1]), [0, n]])

        def v3(ap, h=2):
            return ap.rearrange("p (h t) -> p h t", h=h)

        def rms_rstd(in_ap, in_bufs, stt, stb, junk_ap, junk_buf):
            S.add("act", lambda e: e.activation(out=junk_ap, in_=in_ap, func=AF.Square, accum_out=stt[:, 0:1]),
                  in_bufs, [junk_buf, stb])
            S.add("act", lambda e: e.activation(out=stt[:, 1:2], in_=stt[:, 0:1], func=AF.Ln, bias=eps_col, scale=1.0 / D),
                  [stb, KB], [stb])
            S.add("act", lambda e: e.activation(out=stt[:, 2:3], in_=stt[:, 1:2], func=AF.Exp, scale=-0.5), [stb], [stb])

        def ws_load(srcs):
            i = wsrr[0]
            wsrr[0] = (i + 1) % NWS
            t = ws.t[i]
            bufs = [ws.b[i], ws_b2[i]]
            for j, (dfn, dap) in enumerate(srcs):
                S.add("sp", lambda e, dfn=dfn, dap=dap, t=t: e.dma_start(out=dfn(t), in_=dap), wib_b + wbb_b, [bufs[j]], dma=True)
            return t, bufs[:len(srcs)]

        def wcols(c0, n=128):
            return wib[:, c0 - O_GR:c0 - O_GR + n].rearrange("(k p) c -> p k c", p=128)

        def proj_fm(hTt, hTb, wt, wbufs, ncols=SBT, col0=0):
            pa, pab = fullbank()
            for k in range(8):
                S.add("pe", lambda e, pa=pa, k=k, wt=wt, hTt=hTt: e.matmul(
                    pa[:, 0:ncols], lhsT=wt[:, k, :], rhs=hTt[:, k, col0:col0 + ncols], start=(k == 0), stop=(k == 7)),
                    wbufs + [hTb], pab)
            return pa, pab

        P_ = [slice(0, 64), slice(64, 128)]
        own0 = nsb - nown

        def stage2_header():
            swt, swb = shwa()
            twt, twb = tw()
            S.add("act", lambda e, twt=twt, swt=swt: e.activation(out=twt[0:64, :], in_=swt[0:64, :], func=AF.Exp, scale=2.0), [swb], [twb])
            S.add("act", lambda e, twt=twt: e.activation(out=twt[0:64, :], in_=twt[0:64, :], func=AF.Ln, bias=kct[0:64, 1:2]), [twb, KB], [twb])
            S.add("act", lambda e, twt=twt: e.activation(out=twt[0:64, :], in_=twt[0:64, :], func=AF.Exp, scale=-1.0), [twb], [twb])
            S.add("dve", lambda e, twt=twt: e.tensor_scalar(out=twt[0:64, :], in0=twt[0:64, :], scalar1=-2.0, scalar2=1.0, op0=ALU.mult, op1=ALU.add), [twb], [twb])
            S.add("act", lambda e, twt=twt, swt=swt: e.activation(out=twt[64:128, :], in_=swt[64:128, :], func=AF.Copy), [swb, twb], [twb])
            twh, twhb = tw_hi()
            twl, twlb = tw_lo()
            S.add("act", lambda e, twh=twh, twt=twt: e.activation(out=twh[:], in_=twt[:], func=AF.Copy), [twb], [twhb])
            S.add("dve", lambda e, twl=twl, twt=twt, twh=twh: e.tensor_tensor(out=twl[:], in0=twt[:], in1=twh[:], op=ALU.subtract), [twb, twhb], [twlb])
            return dict(twt=twt, twh=twh, twl=twl, twhb=twhb, twlb=twlb, twb=twb)
        def prep_hp(hp, sbi, own, twt=None, twh=None, twl=None, twhb=None, twlb=None, twb=None):
            si = sbi * 4 + hp
            rt, rb = shr(si)
            kt_, kb_ = shk(si)
            vt, vb = shv(si)
            pD, pDb = fullbank()
            hsl = slice(hp * 128, (hp + 1) * 128)
            S.add("pe", lambda e, pD=pD, hsl=hsl, twh=twh: e.matmul(pD[:, 0:SBT], lhsT=Wdb.t[0][:, 0, hsl], rhs=twh[:, :], start=True, stop=False),
                  [Wdb.b[0], twhb], pDb)
            S.add("pe", lambda e, pD=pD, hsl=hsl, twl=twl: e.matmul(pD[:, 0:SBT], lhsT=Wdb.t[0][:, 0, hsl], rhs=twl[:, :], start=False, stop=False),
                  [Wdb.b[0], twlb], pDb)
            S.add("pe", lambda e, pD=pD, hsl=hsl, twh=twh: e.matmul(pD[:, 0:SBT], lhsT=Wdb.t[0][:, 1, hsl], rhs=twh[:, :], start=False, stop=True),
                  [Wdb.b[0], twhb], pDb)
            e1, e1b = t_e1()
            ew, ewb = t_ew()
            at, ab_ = t_a()
            cs, csb = t_cs()
            csp, cspb = t_csp()
            en, enb = t_en()
            k2, k2b = t_k2()
            kkn, kknb = t_kkn()
            abt, abb = t_ab()
            ft, fb = t_f()
            S.add("act", lambda e, e1=e1, pD=pD, hp=hp: e.activation(out=e1[:], in_=pD[:, 0:SBT], func=AF.Exp, bias=pc("nw0", hp), scale=-1.0),
                  pDb + [PB], [e1b])
            pAa, pAb = fullbank()
            S.add("pe", lambda e, pAa=pAa, hsl=hsl, twh=twh: e.matmul(pAa[:, 0:SBT], lhsT=Wdb.t[0][:, 2, hsl], rhs=twh[:, :], start=True, stop=True),
                  [Wdb.b[0], twhb], pAb)
            S.add("act", lambda e, e1=e1: e.activation(out=e1[:], in_=e1[:], func=AF.Ln, bias=one_col), [e1b, KB], [e1b])
            S.add("act", lambda e, e1=e1, ew=ew: e.activation(out=ew[:], in_=e1[:], func=AF.Exp, bias=mhalf_col, scale=-1.0), [e1b, KB], [ewb])
            S.add("act", lambda e, at=at, pAa=pAa, hp=hp: e.activation(out=at[:], in_=pAa[:, 0:SBT], func=AF.Exp, bias=pc("na0", hp), scale=-1.0),
                  pAb + [PB], [ab_])
            yield
            S.add("act", lambda e, at=at: e.activation(out=at[:], in_=at[:], func=AF.Ln, bias=one_col), [ab_, KB], [ab_])
            S.add("act", lambda e, at=at: e.activation(out=at[:], in_=at[:], func=AF.Exp, scale=-1.0), [ab_], [ab_])
            for c in range(CPS):
                S.add("dve", lambda e, cs=cs, ew=ew, c=c: e.tensor_tensor_scan(
                    out=cs[:, c * C:(c + 1) * C], data0=ones_f, data1=ew[:, c * C:(c + 1) * C], initial=0.0,
                    op0=ALU.mult, op1=ALU.add), [ewb, CF], [csb])
            S.add("pool", lambda e, csp=csp, cs=cs, ew=ew: e.tensor_tensor(out=csp[:], in0=cs[:], in1=ew[:], op=ALU.subtract), [csb, ewb], [cspb])
            S.add("act", lambda e, en=en, cs=cs: e.activation(out=en[:], in_=cs[:], func=AF.Exp), [csb], [enb])
            S.add("act", lambda e, csp=csp: e.activation(out=csp[:], in_=csp[:], func=AF.Exp, scale=-1.0), [cspb], [cspb])
            gct, gcb = gC(si)
            S.add("act", lambda e, gct=gct, cs=cs: e.activation(
                out=gct[:, 0:CPS], in_=cs[:, :].rearrange("p (c t) -> p c t", t=C)[:, :, C - 1], func=AF.Exp, scale=-1.0), [csb], [gcb])
            yield
            k2h, k2hb = t_k2b()
            S.add("act", lambda e, k2h=k2h, kt_=kt_, hp=hp: e.activation(out=k2h[:], in_=kt_[:], func=AF.Square, scale=pc("kk", hp)), [kb_, PB], [k2hb])
            pS_, pSb = fullbank()
            S.add("pe", lambda e, pS_=pS_, k2h=k2h: e.matmul(pS_[:, 0:SBT], lhsT=bones_b, rhs=k2h[:], start=True, stop=True), [CB, k2hb], pSb)
            S.add("act", lambda e, k2=k2, pS_=pS_: e.activation(out=k2[:], in_=pS_[:, 0:SBT], func=AF.Ln, bias=tiny_col), pSb + [KB], [k2b])
            S.add("act", lambda e, k2=k2: e.activation(out=k2[:], in_=k2[:], func=AF.Exp, scale=-0.5), [k2b], [k2b])
            S.add("dve", lambda e, kkn=kkn, kt_=kt_, k2=k2, hp=hp: e.scalar_tensor_tensor(
                out=kkn[:], in0=kt_[:], scalar=pc("kk", hp), in1=k2[:], op0=ALU.mult, op1=ALU.mult), [kb_, k2b, PB], [kknb])
            yield
            ARt, ARb = AR(si)
            BTt, BTb = BT(si)
            KTt, KTb = KT(si)
            vbt, vbb = vbf(si)
            S.add("dve", lambda e, ARt=ARt, kkn=kkn, csp=csp: e.scalar_tensor_tensor(
                out=ARt[:, :, 0, :], in0=kkn[:, :].rearrange("p (c t) -> p c t", t=C), scalar=-1.0,
                in1=csp[:, :].rearrange("p (c t) -> p c t", t=C), op0=ALU.mult, op1=ALU.mult), [kknb, cspb], [ARb])
            S.add("pool", lambda e, abt=abt, kkn=kkn, at=at: e.tensor_tensor(out=abt[:], in0=kkn[:], in1=at[:], op=ALU.mult), [kknb, ab_], [abb])
            S.add("pool", lambda e, BTt=BTt, abt=abt, en=en: e.tensor_tensor(out=BTt[:], in0=abt[:], in1=en[:], op=ALU.mult), [abb, enb], [BTb])
            yield
            S.add("dve", lambda e, ft=ft, at=at, hp=hp: e.tensor_scalar(out=ft[:], in0=at[:], scalar1=pc("ka", hp), scalar2=pc("omka", hp),
                                                                    op0=ALU.mult, op1=ALU.add), [ab_, PB], [fb])
            S.add("pool", lambda e, ft=ft, kt_=kt_: e.tensor_tensor(out=ft[:], in0=kt_[:], in1=ft[:], op=ALU.mult), [kb_, fb], [fb])
            S.add("pool", lambda e, KTt=KTt, ft=ft, en=en: e.tensor_tensor(out=KTt[:], in0=ft[:], in1=en[:], op=ALU.mult), [fb, enb], [KTb])
            S.add("act", lambda e, vbt=vbt, vt=vt: e.activation(out=vbt[:], in_=vt[:], func=AF.Copy), [vb], [vbb])
            yield
            for hh in range(2):
                zt, zb = PZ(si * 2 + hh)
                S.add("pool", lambda e, zt=zt, ARt=ARt, hh=hh: e.tensor_copy(out=zt[P_[hh], 0, :].rearrange("p (c t) -> p c t", t=C), in_=ARt[P_[hh], :, 0, :]), [ARb], [zb])
                S.add("pool", lambda e, zt=zt, BTt=BTt, hh=hh: e.tensor_copy(out=zt[P_[hh], 1, :], in_=BTt[P_[hh], :]), [BTb], [zb])
                S.add("pool", lambda e, zt=zt, KTt=KTt, hh=hh: e.tensor_copy(out=zt[P_[hh], 2, :], in_=KTt[P_[hh], :]), [KTb], [zb])
            if own:
                ep, epb = t_ep()
                S.add("act", lambda e, ep=ep, cs=cs: e.activation(out=ep[:], in_=cs[:], func=AF.Exp, scale=-1.0), [csb], [epb])
                S.add("dve", lambda e, ARt=ARt, rt=rt, ep=ep: e.tensor_tensor(
                    out=ARt[:, :, 1, :], in0=rt[:, :].rearrange("p (c t) -> p c t", t=C),
                    in1=ep[:, :].rearrange("p (c t) -> p c t", t=C), op=ALU.mult), [rb, epb], [ARb])
                rkb_t, rkb_b = t_rkb()
                S.add("dve", lambda e, rkb_t=rkb_t, rt=rt, ft=ft, hp=hp: e.scalar_tensor_tensor(
                    out=rkb_t[:], in0=rt[:], scalar=pc("rk", hp), in1=ft[:], op0=ALU.mult, op1=ALU.mult), [rb, fb, PB], [rkb_b])
                pB_, pBb = fullbank()
                S.add("pe", lambda e, pB_=pB_, rkb_t=rkb_t: e.matmul(pB_[:, 0:SBT], lhsT=bones_b, rhs=rkb_t[:], start=True, stop=True), [CB, rkb_b], pBb)
                bnt, bnb = bonus(si)
                S.add("dve", lambda e, bnt=bnt, pB_=pB_, vt=vt: e.tensor_tensor(out=bnt[:], in0=pB_[:, 0:SBT], in1=vt[:], op=ALU.mult), pBb + [vb], [bnb])
            yield
        def proj_tile(sbi, ct, hTt, hTb):
            pa, pab = fullbank()
            for k in range(8):
                S.add("pe", lambda e, pa=pa, k=k, ct=ct, hTt=hTt: e.matmul(
                    pa[:, 0:SBT], lhsT=Wsh.t[0][:, k, ct * 128:(ct + 1) * 128], rhs=hTt[:, k, :],
                    start=(k == 0), stop=(k == 7)), [Wsh_b[k], hTb], pab)
            if ct == 12:
                dst, dstb = shwa()
            else:
                hp = ct % 4
                dst, dstb = (shr, shk, shv)[ct // 4](sbi * 4 + hp)
            ptc[0] += 1
            tmp, tmpb = shtmp(ptc[0])
            pcur, pcurb = prevcol(sbi)
            pnxt, pnxtb = prevcol(sbi + 1)
            S.add("act", lambda e, tmp=tmp, pa=pa, ct=ct: e.activation(
                out=tmp[:], in_=pa[:, 0:SBT], func=AF.Copy, scale=pc("omu", ct)), pab + [PB], [tmpb])
            S.add("act", lambda e, pa=pa, ct=ct, pnxt=pnxt: e.activation(
                out=pnxt[:, ct:ct + 1], in_=pa[:, SBT - 1:SBT], func=AF.Copy), pab, [pnxtb])
            S.add("dve", lambda e, dst=dst, pa=pa, tmp=tmp, ct=ct: e.scalar_tensor_tensor(
                out=dst[:, 1:SBT], in0=pa[:, 0:SBT - 1], scalar=pc("mu", ct), in1=tmp[:, 1:SBT],
                op0=ALU.mult, op1=ALU.add), pab + [tmpb, PB], [dstb])
            S.add("dve", lambda e, dst=dst, tmp=tmp, ct=ct, pcur=pcur: e.scalar_tensor_tensor(
                out=dst[:, 0:1], in0=pcur[:, ct:ct + 1], scalar=pc("mu", ct), in1=tmp[:, 0:1],
                op0=ALU.mult, op1=ALU.add), [pcurb, tmpb, PB], [dstb])

        sbst = {}

        def gen_first(sbi):
            own = sbi >= own0
            hTt, hTb = hT(sbi)
            for j in range(CPS):
                gc = sbi * CPS + j
                xtt, xtb = xt(gc)
                hbt, hbb = hb(gc)
                stt, stb = st0(gc)
                dma("sp", xtt[:], xw[gc * C:(gc + 1) * C, :], [], [xtb])
                rms_rstd(xtt[:], [xtb], stt, stb, hbt[:], hbb)
                S.add("dve", lambda e, xtt=xtt, stt=stt, hbt=hbt: e.scalar_tensor_tensor(
                    out=hbt[:], in0=xtt[:], scalar=stt[:, 2:3], in1=gpre.t[0][:], op0=ALU.mult, op1=ALU.mult),
                    [xtb, stb, gpre.b[0]], [hbb])
                yield
                pst = psT[0]
                pstb = psT_b[0]
                for k in range(8):
                    S.add("pe", lambda e, k=k, hbt=hbt, pst=pst: e.transpose(
                        out=pst[:, k * 128:(k + 1) * 128], in_=hbt[:, k * 128:(k + 1) * 128], identity=ident_b),
                        [hbb, CB], pstb)
                S.add("act", lambda e, pst=pst, hTt=hTt, j=j: e.activation(
                    out=hTt[:, :, j * C:(j + 1) * C], in_=pst[:, :].rearrange("p (k t) -> p k t", k=8), func=AF.Copy),
                    pstb, [hTb])
                yield
            proj_tile(sbi, 12, hTt, hTb)
            tw_ctx = stage2_header()
            sbst[sbi] = (tw_ctx, hTt, hTb)
            yield
            for hp in (0, 1):
                for q in range(3):
                    proj_tile(sbi, q * 4 + hp, hTt, hTb)
                    yield

        def gen_first_b(sbi):
            own = sbi >= own0
            tw_ctx, hTt, hTb = sbst[sbi]
            for hp in (0, 1):
                yield from prep_hp(hp, sbi, own, **tw_ctx)

        def gen_first_ab(sbi):
            yield from gen_first(sbi)
            yield from gen_first_b(sbi)

        def gen_second(sbi):
            own = sbi >= own0
            tw_ctx, hTt, hTb = sbst[sbi]
            for hp in (2, 3):
                for q in range(3):
                    proj_tile(sbi, q * 4 + hp, hTt, hTb)
                    yield
                yield from prep_hp(hp, sbi, own, **tw_ctx)

        def drain(g):
            for _ in g:
                pass

        def mkfill(g, n=1, units=None, slots=None):
            st = [0]

            def fill():
                if units is None:
                    k = n
                else:
                    i = st[0]
                    st[0] += 1
                    k = ((i + 1) * units) // slots - (i * units) // slots
                for _ in range(k):
                    try:
                        next(g)
                    except StopIteration:
                        return
            return fill

        drain(gen_first_ab(0))
        for sbi in range(nsb):
            own = sbi >= own0
            halo_sb = (sbi == own0 - 1)
            hTt, hTb = hT(sbi)
            def gen_ownproj(sbi=sbi, own=own, halo_sb=halo_sb, hTt=hTt, hTb=hTb):
                if own or halo_sb:
                    ncols, col0 = (SBT, 0) if own else (C, SBT - C)
                    kcol = ((sbi - own0) * CPS + 1) * C if own else 0
                    for g in range(2):
                        wt, wb = ws_load([(lambda t: t[:, :, 0:64], wcols(O_K + g * 64, 64)), (lambda t: t[:, :, 64:128], wcols(O_K + g * 64, 64))])
                        pa, pab = proj_fm(hTt, hTb, wt, wb, ncols, col0)
                        for cc in range(ncols // C):
                            ks = ((kcol // C) + cc) % NKS
                            S.add("act", lambda e, pa=pa, g=g, ks=ks, cc=cc: e.activation(
                                out=KTatt.t[g][:, ks * C:(ks + 1) * C], in_=pa[:, cc * C:(cc + 1) * C], func=AF.Identity, bias=pc("bk", g)), pab + [PB], [KTatt.b[g]])
                        yield
                    wt, wb = ws_load([(lambda t: t[:, :, :], wcols(O_V))])
                    for c in (range(CPS) if own else [CPS - 1]):
                        lc1 = (sbi - own0) * CPS + c + 1 if own else 0
                        pa, pab = fullbank()
                        for k in range(8):
                            S.add("pe", lambda e, pa=pa, k=k, wt=wt, hTt=hTt, c=c: e.matmul(
                                pa[:, 0:128], lhsT=hTt[:, k, c * C:(c + 1) * C], rhs=wt[:, k, :], start=(k == 0), stop=(k == 7)), wb + [hTb], pab)
                        vp, vpb = Vpad(lc1)
                        S.add("dve", lambda e, vp=vp, pa=pa: e.tensor_tensor(out=vp[:, :, 0:64], in0=v3(pa[:, 0:128]), in1=v3(bvb.t[0][:, :]), op=ALU.add),
                              pab + [bvb.b[0]], [vpb])
                        S.add("pool", lambda e, vp=vp: e.tensor_copy(out=vp[:, :, 128:192], in_=vp[:, :, 0:64]), [vpb], [vpb])
                        yield
                if own:
                    for ct in range(4):
                        si = sbi * 4 + ct
                        wt, wb = ws_load([(lambda t: t[:, :, :], wcols(O_GR + ct * 128))])
                        pa, pab = proj_fm(hTt, hTb, wt, wb)
                        S.add("act", lambda e, pa=pa, si=si: e.activation(out=sgr(si)[0][:], in_=pa[:, 0:SBT], func=AF.Silu), pab, [sgr(si)[1]])
                        yield
                        wt, wb = ws_load([(lambda t: t[:, :, :], wcols(O_Q + ct * 128))])
                        pa, pab = proj_fm(hTt, hTb, wt, wb)
                        for hh in range(2):
                            qz, qzb = qTz(si * 2 + hh)
                            S.add("act", lambda e, pa=pa, qz=qz, ct=ct, hh=hh: e.activation(
                                out=qz[P_[hh], :], in_=pa[P_[hh], 0:SBT], func=AF.Identity, bias=ppt[P_[hh], PPI["bq"] + ct:PPI["bq"] + ct + 1]),
                                pab + [PB], [qzb])
                        yield
                        wt, wb = ws_load([(lambda t: t[:, :, :], wcols(O_GA + ct * 128))])
                        pa, pab = proj_fm(hTt, hTb, wt, wb)
                        S.add("act", lambda e, pa=pa, si=si: e.activation(out=sga(si)[0][:], in_=pa[:, 0:SBT], func=AF.Silu), pab, [sga(si)[1]])
                        yield

                yield

            def emit_chunk_pairs(c, pairs, fill, own=own, sbi=sbi):
                gch = sbi * CPS + c
                csl = slice(c * C, (c + 1) * C)
                pYbank, pYbb = psA[0], [psA_b[0]]
                def mkctx(hp):
                    si = sbi * 4 + hp
                    gi = gch * 4 + hp
                    x = dict(hp=hp, si=si, gi=gi)
                    x["AR"], x["ARb"] = AR(si)
                    x["BT"], x["BTb"] = BT(si)
                    x["KT"], x["KTb"] = KT(si)
                    x["vb"], x["vbb"] = vbf(si)
                    x["zts"] = [PZ(si * 2 + hh) for hh in range(2)]
                    x["tm"], x["tmb"] = tm(gi)
                    return x

                def g_transposes(x, c=c, csl=csl):
                    pt_ = psT[1][:, 0:512]
                    ptb = psT_b[1]
                    srcs = [(x["AR"][:, c, 0, :], x["ARb"]), (x["BT"][:, csl], x["BTb"]), (x["KT"][:, csl], x["KTb"]), (x["vb"][:, csl], x["vbb"])]
                    for q, (sap, sbf) in enumerate(srcs):
                        S.add("pe", lambda e, pt_=pt_, q=q, sap=sap: e.transpose(out=pt_[:, q * 128:(q + 1) * 128], in_=sap, identity=ident_b),
                              [sbf, CB], ptb)
                    tmt = x["tm"]
                    S.add("act", lambda e, tmt=tmt, pt_=pt_: e.activation(out=tmt[:], in_=pt_.rearrange("p (q t) -> p q t", q=4), func=AF.Copy),
                          ptb, [x["tmb"]])

                def g_sprod_pe(x, c=c, csl=csl):
                    x["pS1"], x["pS1b"] = single(4)
                    x["pS2"], x["pS2b"] = single(2)
                    ARt, BTt = x["AR"], x["BT"]
                    for hh in range(2):
                        zt, zb = x["zts"][hh]
                        S.add("pe", lambda e, pS=x["pS1"], zt=zt, ARt=ARt, hh=hh: e.matmul(
                            pS[:, hh * 128:(hh + 1) * 128], lhsT=zt[:, 1, csl], rhs=ARt[:, c, 0, :], start=True, stop=True), [zb, x["ARb"]], x["pS1b"])
                    for hh in range(2):
                        zt, zb = x["zts"][hh]
                        S.add("pe", lambda e, pS=x["pS1"], zt=zt, BTt=BTt, hh=hh: e.matmul(
                            pS[:, (2 + hh) * 128:(3 + hh) * 128], lhsT=zt[:, 0, csl], rhs=BTt[:, csl], start=True, stop=True), [zb, x["BTb"]], x["pS1b"])
                    for hh in range(2):
                        zt, zb = x["zts"][hh]
                        S.add("pe", lambda e, pS=x["pS2"], zt=zt, ARt=ARt, hh=hh: e.matmul(
                            pS[:, hh * 128:(hh + 1) * 128], lhsT=zt[:, 2, csl], rhs=ARt[:, c, 0, :], start=True, stop=True), [zb, x["ARb"]], x["pS2b"])

                def g_sprod_evac(x):
                    gi = x["gi"]
                    nm, nmb = NMt(gi * 2)
                    mk, mkb = Mak(gi)
                    S.add("dve", lambda e, nm=nm, pS=x["pS1"]: e.tensor_tensor(out=nm[:, :, :, :].rearrange("p a h t -> p (a h) t"), in0=v3(pS, 4), in1=mask4, op=ALU.mult),
                          x["pS1b"] + [CB], [nmb])
                    S.add("dve", lambda e, mk=mk, pS=x["pS2"]: e.tensor_tensor(out=mk[:], in0=v3(pS, 2), in1=mask4[:, 0:2, :], op=ALU.mult),
                          x["pS2b"] + [CB], [mkb])
                    x["nm"], x["nmb"], x["mk"], x["mkb"] = nm, nmb, mk, mkb

                def g_r_pe(x, c=c, csl=csl):
                    x["pR"], x["pRb"] = single(4)
                    ARt = x["AR"]
                    for a_ in range(2):
                        for hh in range(2):
                            zt, zb = x["zts"][hh]
                            S.add("pe", lambda e, pR=x["pR"], zt=zt, ARt=ARt, hh=hh, a_=a_: e.matmul(
                                pR[:, (a_ * 2 + hh) * 128:(a_ * 2 + hh + 1) * 128], lhsT=zt[:, 1 + a_, csl], rhs=ARt[:, c, 1, :], start=True, stop=True),
                                [zb, x["ARb"]], x["pRb"])

                def g_r_evac(x, c=c, csl=csl):
                    rbk, rbkb = RBK(x["gi"])
                    for a_ in range(2):
                        S.add("dve", lambda e, rbk=rbk, pR=x["pR"], a_=a_: e.tensor_tensor(out=rbk[:, a_, :, :], in0=v3(pR[:, a_ * 256:(a_ + 1) * 256], 2), in1=mle2, op=ALU.mult),
                              x["pRb"] + [CB], [rbkb])
                    x["rbk"], x["rbkb"] = rbk, rbkb

                def g_pv_pe(x, c=c, csl=csl):
                    x["pV"], x["pVb"] = single(1)
                    mk, tmt = x["mk"], x["tm"]
                    for hh in range(2):
                        S.add("pe", lambda e, pV=x["pV"], hh=hh, mk=mk, tmt=tmt: e.matmul(
                            pV[:, hh * 64:(hh + 1) * 64], lhsT=mk[:, hh, :], rhs=tmt[:, 3, hh * 64:(hh + 1) * 64], start=True, stop=True), [x["mkb"], x["tmb"]], x["pVb"])

                def g_x0(x, c=c, csl=csl):
                    Xt, Xb = Xtile(x["gi"] * 2)
                    tmt = x["tm"]
                    S.add("pool", lambda e, Xt=Xt, tmt=tmt: e.tensor_copy(out=Xt[:, :, 0, :], in_=tmt[:, 0, :].rearrange("p (h k) -> p h k", h=2)), [x["tmb"]], [Xb])
                    S.add("act", lambda e, Xt=Xt, pV=x["pV"]: e.activation(out=Xt[:, :, 1, :], in_=pV[:, 0:128].rearrange("p (h k) -> p h k", h=2), func=AF.Copy),
                          x["pVb"], [Xb])
                    x["X"], x["Xb"] = Xt, Xb

                def g_level_pe(x, lv):
                    nm, nmb, Xt, Xb = x["nm"], x["nmb"], x["X"], x["Xb"]
                    x["pX"], x["pXb"] = single(2)
                    for hh in range(2):
                        S.add("pe", lambda e, pX=x["pX"], hh=hh, nm=nm, Xt=Xt: e.matmul(
                            pX[:, hh * 128:(hh + 1) * 128], lhsT=nm[:, 0, hh, :], rhs=Xt[:, hh, :, :].rearrange("p a k -> p (a k)"), start=True, stop=True),
                            [nmb, Xb], x["pXb"])
                    if lv < 6:
                        x["pNM"], x["pNMb"] = single(4)
                        for hh in range(2):
                            S.add("pe", lambda e, pN=x["pNM"], hh=hh, nm=nm: e.matmul(
                                pN[:, hh * 128:(hh + 1) * 128], lhsT=nm[:, 1, hh, :], rhs=nm[:, 0, hh, :], start=True, stop=True), [nmb], x["pNMb"])
                        if lv < 5:
                            for hh in range(2):
                                S.add("pe", lambda e, pN=x["pNM"], hh=hh, nm=nm: e.matmul(
                                    pN[:, (2 + hh) * 128:(3 + hh) * 128], lhsT=nm[:, 0, hh, :], rhs=nm[:, 1, hh, :], start=True, stop=True), [nmb], x["pNMb"])

                def g_level_evac(x, lv):
                    gi = x["gi"]
                    Xt, Xb = x["X"], x["Xb"]
                    Xn, Xnb = Xtile(gi * 2 + lv + 1)
                    S.add("dve", lambda e, Xn=Xn, pX=x["pX"], Xt=Xt: e.tensor_tensor(
                        out=Xn[:, :, :, :].rearrange("p h a k -> p h (a k)"), in0=v3(pX), in1=Xt[:, :, :, :].rearrange("p h a k -> p h (a k)"), op=ALU.add),
                        x["pXb"] + [Xb], [Xnb])
                    x["X"], x["Xb"] = Xn, Xnb
                    if lv < 6:
                        nn, nnb = NMt(gi * 2 + lv + 1)
                        w = 4 if lv < 5 else 2
                        S.add("act", lambda e, nn=nn, pN=x["pNM"], w=w: e.activation(
                            out=nn[:, :, :, :].rearrange("p a h t -> p (a h) t")[:, 0:w, :], in_=v3(pN[:, 0:w * 128], w), func=AF.Copy), x["pNMb"], [nnb])
                        x["nm"], x["nmb"] = nn, nnb

                def g_state(x, c=c, csl=csl, own=own, pYbank=(pYbank if own else None), pYbb=(pYbb if own else None)):
                    gi, hp, si = x["gi"], x["hp"], x["si"]
                    Xt, Xb, tmt, tmb = x["X"], x["Xb"], x["tm"], x["tmb"]
                    ARt, ARb = x["AR"], x["ARb"]
                    wz, wzb = Wz(gi)
                    S.add("pool", lambda e, wz=wz, Xt=Xt: e.tensor_copy(out=wz[:, 0::2, :], in_=Xt[:, :, 0, :]), [Xb], [wzb])
                    wzA = wz[:, 0:2, :].rearrange("p a k -> p (a k)")
                    wzB = wz[:, 1:3, :].rearrange("p a k -> p (a k)")
                    wp, wpb = Wp(gi)
                    S.add("pool", lambda e, wp=wp, Xt=Xt: e.tensor_copy(out=wp[:, :, :], in_=Xt[:, :, 0, :]), [Xb], [wpb])
                    pAT, pATb = single(1)
                    S.add("pe", lambda e, pAT=pAT, wp=wp, tmt=tmt: e.matmul(pAT[:, 0:128], lhsT=wp[:, :, :].rearrange("p a k -> p (a k)"), rhs=tmt[:, 1, :], start=True, stop=True),
                          [wpb, tmb], pATb)
                    atb, atbb = ATbd(gi)
                    for hh in range(2):
                        S.add("dve", lambda e, atb=atb, pAT=pAT, hh=hh: e.tensor_copy(
                            out=atb[P_[hh], hh * 64:(hh + 1) * 64], in_=pAT[P_[hh], hh * 64:(hh + 1) * 64]), pATb, [atbb])
                    pG, pGb = single(1)
                    pG2, pG2b = single(1)
                    S.add("pe", lambda e, pG=pG, tmt=tmt: e.matmul(pG[:, 0:128], lhsT=tmt[:, 2, :], rhs=tmt[:, 3, :], start=True, stop=True),
                          [tmb], pGb)
                    for hh in range(2):
                        S.add("pe", lambda e, pG2=pG2, Xt=Xt, tmt=tmt, hh=hh: e.matmul(
                            pG2[:, hh * 64:(hh + 1) * 64], lhsT=tmt[:, 1, :], rhs=Xt[:, hh, 1, :], start=True, stop=True), [Xb, tmb], pG2b)
                    gs, gsb_ = Gsb(gi)
                    for hh in range(2):
                        S.add("act", lambda e, gs=gs, pG=pG, hh=hh: e.activation(
                            out=gs[P_[hh], :], in_=pG[P_[hh], hh * 64:(hh + 1) * 64], func=AF.Copy), pGb, [gsb_])
                        S.add("dve", lambda e, gs=gs, pG2=pG2, hh=hh: e.tensor_tensor(
                            out=gs[P_[hh], :], in0=pG2[P_[hh], hh * 64:(hh + 1) * 64], in1=gs[P_[hh], :], op=ALU.add), pG2b + [gsb_], [gsb_])
                    Htt, Hb_ = Ht(hp)
                    gct, gcb = gC(si)
                    if own:
                        hbz = [Hbfz(gi * 2 + hh) for hh in range(2)]
                        for hh in range(2):
                            S.add("pool", lambda e, hz=hbz[hh][0], Htt=Htt, hh=hh: e.tensor_copy(out=hz[P_[hh], :], in_=Htt[P_[hh], :]), [Hb_], [hbz[hh][1]])
                    hhl, hhlb = Hhl(gi)
                    S.add("pool", lambda e, hhl=hhl, Htt=Htt: e.tensor_copy(out=hhl[:, 0, :], in_=Htt[:]), [Hb_], [hhlb])
                    S.add("pool", lambda e, hhl=hhl, Htt=Htt: e.tensor_tensor(out=hhl[:, 1, :], in0=Htt[:], in1=hhl[:, 0, :], op=ALU.subtract), [Hb_, hhlb], [hhlb])
                    pZ, pZb = single(1)
                    S.add("pe", lambda e, pZ=pZ, atb=atb, hhl=hhl: e.matmul(pZ[:, 0:128], lhsT=atb[:], rhs=hhl[:, :, :].rearrange("p a v -> p (a v)"), start=True, stop=True),
                          [atbb, hhlb], pZb)
                    s1, s1b = s1t(gi)
                    S.add("pool", lambda e, s1=s1, Htt=Htt, gs=gs: e.tensor_tensor(out=s1[:], in0=Htt[:], in1=gs[:], op=ALU.add), [Hb_, gsb_], [s1b])
                    S.add("pool", lambda e, s1=s1, gct=gct: e.tensor_scalar(out=s1[:], in0=s1[:], scalar1=gct[:, c:c + 1], scalar2=1.0, op0=ALU.mult, op1=ALU.mult),
                          [s1b, gcb], [s1b])
                    S.add("dve", lambda e, pZ=pZ, gct=gct, s1=s1: e.scalar_tensor_tensor(
                        out=s1[:], in0=pZ[:, 0:64], scalar=gct[:, c:c + 1], in1=s1[:], op0=ALU.mult, op1=ALU.add), pZb + [gcb, s1b], [s1b])
                    S.add("dve", lambda e, Htt=Htt, pZ=pZ, gct=gct, s1=s1: e.scalar_tensor_tensor(
                        out=Htt[:], in0=pZ[:, 64:128], scalar=gct[:, c:c + 1], in1=s1[:], op0=ALU.mult, op1=ALU.add), pZb + [gcb, s1b], [Hb_])
                    if own:
                        rbk, rbkb = x["rbk"], x["rbkb"]
                        qt_, qtb = QT(gi)
                        for hh, wzX in enumerate((wzA, wzB)):
                            pQ, pQb = single(1)
                            S.add("pe", lambda e, pQ=pQ, wzX=wzX, rbk=rbk, hh=hh: e.matmul(pQ[:, 0:128], lhsT=wzX, rhs=rbk[:, 0, hh, :], start=True, stop=True),
                                  [wzb, rbkb], pQb)
                            S.add("dve", lambda e, qt_=qt_, pQ=pQ, ARt=ARt, hh=hh: e.tensor_tensor(
                                out=qt_[P_[hh], :], in0=pQ[P_[hh], 0:128], in1=ARt[P_[hh], c, 1, :], op=ALU.add), pQb + [ARb], [qtb])
                        for hh in range(2):
                            pY, pYb = pYbank, pYbb
                            ysl = slice((hp * 2 + hh) * 64, (hp * 2 + hh + 1) * 64)
                            S.add("pe", lambda e, pY=pY, ysl=ysl, hh=hh, rbk=rbk, Xt=Xt: e.matmul(
                                pY[:, ysl], lhsT=rbk[:, 0, hh, :], rhs=Xt[:, hh, 1, :], start=True, stop=False), [rbkb, Xb], pYb)
                            S.add("pe", lambda e, pY=pY, ysl=ysl, hh=hh, rbk=rbk, tmt=tmt: e.matmul(
                                pY[:, ysl], lhsT=rbk[:, 1, hh, :], rhs=tmt[:, 3, hh * 64:(hh + 1) * 64], start=False, stop=False), [rbkb, tmb], pYb)
                            S.add("pe", lambda e, pY=pY, ysl=ysl, qt_=qt_, hz=hbz[hh][0]: e.matmul(
                                pY[:, ysl], lhsT=qt_[:, :], rhs=hz[:, :], start=False, stop=True), [qtb, hbz[hh][1]], pYb)

                for pr in pairs:
                    ctxs = [mkctx(hp) for hp in pr]
                    for x in ctxs:
                        g_transposes(x)
                    for x in ctxs:
                        g_sprod_pe(x)
                    for x in ctxs:
                        g_sprod_evac(x)
                    fill()
                    if own:
                        for x in ctxs:
                            g_r_pe(x)
                        for x in ctxs:
                            g_r_evac(x)
                    for x in ctxs:
                        g_pv_pe(x)
                    for x in ctxs:
                        g_x0(x)
                    fill()
                    for lv in range(7):
                        for x in ctxs:
                            g_level_pe(x, lv)
                        for x in ctxs:
                            g_level_evac(x, lv)
                        fill()
                    for x in ctxs:
                        g_state(x)
            if not own:
                if halo_sb:
                    drain(gen_ownproj())
                g2 = gen_second(sbi)
                f2 = mkfill(g2, units=23, slots=17)
                for c in range(CPS):
                    emit_chunk_pairs(c, [PAIRS[0]], f2)
                drain(g2)
                g1 = gen_first_ab(sbi + 1) if sbi + 1 < nsb else iter(())
                f1 = mkfill(g1, units=28, slots=17)
                for c in range(CPS):
                    emit_chunk_pairs(c, [PAIRS[1]], f1)
                drain(g1)
                continue
            fb_mode[0] = 1
            g1 = iter(())
            for c in range(CPS):
                gch = sbi * CPS + c
                csl = slice(c * C, (c + 1) * C)
                pYbank, pYbb = psA[0], [psA_b[0]]
                if c == 0:
                    g2 = gen_second(sbi)
                    emit_chunk_pairs(c, [PAIRS[0]], mkfill(g2, 2))
                    drain(g2)
                    g3 = gen_ownproj()
                    emit_chunk_pairs(c, [PAIRS[1]], mkfill(g3, 2))
                    drain(g3)
                else:
                    if c == 1 and sbi + 1 < nsb:
                        g1 = gen_first(sbi + 1)
                    emit_chunk_pairs(c, [PAIRS[0]], mkfill(g1, 1))
                    emit_chunk_pairs(c, [PAIRS[1]], mkfill(g1, 1))
                afill = lambda: None
                if c == CPS - 1 and sbi + 1 < nsb:
                    drain(g1)
                    gfb = gen_first_b(sbi + 1)
                    afill = mkfill(gfb, units=13, slots=10)
                lc = (sbi - own0) * CPS + c
                g_t, g_b = gst()
                pY, pYb = pYbank, pYbb
                yq, yqb = ysqt(0)
                S.add("dve", lambda e, g_t=g_t, pY=pY: e.tensor_reduce(out=g_t[:, 0, :], in_=v3(pY[:, 0:512], 8), axis=AX.X, op=ALU.add), pYb, [g_b])
                S.add("act", lambda e, yq=yq, pY=pY: e.activation(out=yq[:], in_=pY[:, 0:512], func=AF.Square), pYb, [yqb])
                S.add("dve", lambda e, g_t=g_t, yq=yq: e.tensor_reduce(out=g_t[:, 1, :], in_=v3(yq[:, :], 8), axis=AX.X, op=ALU.add), [yqb], [g_b])
                S.add("dve", lambda e, g_t=g_t: e.tensor_scalar(out=g_t[:, 2, :], in0=g_t[:, 0, :], scalar1=1.0 / 64, scalar2=None, op0=ALU.mult), [g_b], [g_b])
                S.add("dve", lambda e, g_t=g_t: e.tensor_tensor(out=g_t[:, 3, :], in0=g_t[:, 2, :], in1=g_t[:, 2, :], op=ALU.mult), [g_b], [g_b])
                S.add("dve", lambda e, g_t=g_t: e.scalar_tensor_tensor(out=g_t[:, 4, :], in0=g_t[:, 1, :], scalar=1.0 / 64, in1=g_t[:, 3, :],
                                                                       op0=ALU.mult, op1=ALU.subtract), [g_b], [g_b])
                S.add("act", lambda e, g_t=g_t: e.activation(out=g_t[:, 5, :], in_=g_t[:, 4, :], func=AF.Ln, bias=gneps_col), [g_b, KB], [g_b])
                S.add("act", lambda e, g_t=g_t: e.activation(out=g_t[:, 5, :], in_=g_t[:, 5, :], func=AF.Exp, scale=-0.5), [g_b], [g_b])
                ynt, ynb = yn()
                S.add("dve", lambda e, yq=yq, pY=pY, g_t=g_t: e.tensor_tensor(
                    out=v3(yq[:, :], 8), in0=v3(pY[:, 0:512], 8), in1=bcl(g_t[:, 2, :], 64), op=ALU.subtract), pYb + [g_b, yqb], [yqb])
                S.add("pool", lambda e, ynt=ynt, yq=yq, g_t=g_t: e.tensor_tensor(
                    out=v3(ynt[:, :], 8), in0=v3(yq[:, :], 8), in1=bcl(g_t[:, 5, :], 64), op=ALU.mult), [yqb, g_b], [ynb])
                pt_ = psT[1][:, 0:512]
                ptb = psT_b[1]
                for hp in range(4):
                    S.add("pe", lambda e, pt_=pt_, hp=hp, ynt=ynt: e.transpose(out=pt_[:, hp * 128:(hp + 1) * 128], in_=ynt[:, hp * 128:(hp + 1) * 128], identity=ident_b),
                          [ynb, CB], ptb)
                for hp in range(4):
                    si = sbi * 4 + hp
                    t1, t1b = t1t(hp)
                    S.add("dve", lambda e, t1=t1, pt_=pt_, hp=hp: e.tensor_scalar(out=t1[:], in0=pt_[:, hp * 128:(hp + 1) * 128], scalar1=pc("gnw", hp), scalar2=pc("gnb", hp),
                                                                            op0=ALU.mult, op1=ALU.add), ptb + [PB], [t1b])
                    S.add("pool", lambda e, t1=t1, si=si, csl=csl: e.tensor_tensor(out=t1[:], in0=t1[:], in1=bonus(si)[0][:, csl], op=ALU.add), [t1b, bonus(si)[1]], [t1b])
                    S.add("pool", lambda e, t1=t1, si=si, csl=csl: e.tensor_tensor(out=zr(si)[0][:, csl], in0=t1[:], in1=sgr(si)[0][:, csl], op=ALU.mult),
                          [t1b, sgr(si)[1]], [zr(si)[1]])
                if lc == NOC - 1:
                    dump("zr0", zr(sbi * 4)[0][:], [128, SBT], [zr(sbi * 4)[1]], BF16)
                if upto < 4:
                    continue
                am_i = 0 if lc == 0 else 1
                pObank, pObb = psA[0], [psA_b[0]]
                vprev, vprevb = Vpad(lc)
                vcur, vcurb = Vpad(lc + 1)
                for qp0 in (0, 2):
                    pts = psT[0]
                    hs = []
                    for qp in (qp0, qp0 + 1):
                        si = sbi * 4 + qp
                        g = qp // 2
                        for hh in range(2):
                            hd = qp * 2 + hh
                            pS, pSb_ = single(2)
                            qz, qzb = qTz(si * 2 + hh)
                            for kk_ in range(2):
                                ks = (lc + kk_) % NKS
                                S.add("pe", lambda e, pS=pS, qz=qz, g=g, ks=ks, kk_=kk_, csl=csl: e.matmul(
                                    pS[:, kk_ * 128:(kk_ + 1) * 128], lhsT=qz[:, csl], rhs=KTatt.t[g][:, ks * C:(ks + 1) * C], start=True, stop=True),
                                    [qzb, KTatt.b[g]], pSb_)
                            j4 = (qp - qp0) * 2 + hh
                            hs.append(dict(hd=hd, qp=qp, hh=hh, g=g, si=si, pS=pS, pSb=pSb_, sm=smt(j4), a=ast(j4), p3=p32(j4), pn=pnt(j4), pt=ptt(j4),
                                           ptsl=pts[:, j4 * 256:(j4 + 1) * 256]))
                    for h in hs:
                        S.add("dve", lambda e, sm=h["sm"][0], pS=h["pS"], am_i=am_i: e.scalar_tensor_tensor(
                            out=sm[:], in0=pS[:, 0:256], scalar=0.125, in1=amask.t[0][:, am_i, :], op0=ALU.mult, op1=ALU.add), h["pSb"] + [amask.b[0]], [h["sm"][1]])
                    afill()
                    for h in hs:
                        S.add("dve", lambda e, a_t=h["a"][0], sm=h["sm"][0]: e.tensor_reduce(out=a_t[:, 0:1], in_=sm[:], axis=AX.X, op=ALU.max), [h["sm"][1]], [h["a"][1]])
                    for h in hs:
                        S.add("dve", lambda e, a_t=h["a"][0], hd=h["hd"]: e.tensor_scalar(out=a_t[:, 1:2], in0=a_t[:, 0:1], scalar1=pc("sink", hd), scalar2=-1.0, op0=ALU.max, op1=ALU.mult),
                              [h["a"][1], PB], [h["a"][1]])
                    afill()
                    for h in hs:
                        S.add("act", lambda e, pp3=h["p3"][0], sm=h["sm"][0], a_t=h["a"][0]: e.activation(out=pp3[:], in_=sm[:], func=AF.Exp, bias=a_t[:, 1:2], accum_out=a_t[:, 2:3]),
                              [h["sm"][1], h["a"][1]], [h["p3"][1], h["a"][1]])
                    for h in hs:
                        S.add("act", lambda e, a_t=h["a"][0], hd=h["hd"]: e.activation(out=a_t[:, 3:4], in_=pc("sink", hd), func=AF.Exp, bias=a_t[:, 1:2]), [h["a"][1], PB], [h["a"][1]])
                    for h in hs:
                        S.add("dve", lambda e, a_t=h["a"][0]: e.tensor_tensor(out=a_t[:, 4:5], in0=a_t[:, 2:3], in1=a_t[:, 3:4], op=ALU.add), [h["a"][1]], [h["a"][1]])
                    for h in hs:
                        S.add("dve", lambda e, a_t=h["a"][0]: e.reciprocal(out=a_t[:, 5:6], in_=a_t[:, 4:5]), [h["a"][1]], [h["a"][1]])
                    afill()
                    for h in hs:
                        S.add("dve", lambda e, pn=h["pn"][0], pp3=h["p3"][0], a_t=h["a"][0]: e.tensor_scalar(out=pn[:], in0=pp3[:], scalar1=a_t[:, 5:6], scalar2=None, op0=ALU.mult),
                              [h["p3"][1], h["a"][1]], [h["pn"][1]])
                    for h in hs:
                        for kk_ in range(2):
                            S.add("pe", lambda e, ptsl=h["ptsl"], kk_=kk_, pn=h["pn"][0]: e.transpose(out=ptsl[:, kk_ * 128:(kk_ + 1) * 128], in_=pn[:, kk_ * 128:(kk_ + 1) * 128], identity=ident_b),
                                  [h["pn"][1], CB], psT_b[0])
                    afill()
                    for h in hs:
                        S.add("act", lambda e, pt2=h["pt"][0], ptsl=h["ptsl"]: e.activation(out=pt2[:], in_=v3(ptsl), func=AF.Copy), psT_b[0], [h["pt"][1]])
                    for qp in (qp0, qp0 + 1):
                        si = sbi * 4 + qp
                        g = qp // 2
                        pO, pOb = pObank[:, qp * 128:(qp + 1) * 128], pObb
                        n_ = 0
                        for h in [h for h in hs if h["qp"] == qp]:
                            pt2, pt2b = h["pt"]
                            hh = h["hh"]
                            for kk_, (vp, vpb) in enumerate([(vprev, vprevb), (vcur, vcurb)]):
                                S.add("pe", lambda e, pO=pO, vp=vp, g=g, hh=hh, pt2=pt2, kk_=kk_, n_=n_: e.matmul(
                                    pO[:, 0:128], lhsT=vp[:, g, hh * 64:hh * 64 + 128], rhs=pt2[:, kk_, :], start=(n_ == 0), stop=(n_ == 3)),
                                    [vpb, pt2b], pOb)
                                n_ += 1
                    afill()
                    for qp in (qp0, qp0 + 1):
                        si = sbi * 4 + qp
                        pO, pOb = pObank[:, qp * 128:(qp + 1) * 128], pObb
                        S.add("dve", lambda e, si=si, pO=pO, csl=csl: e.tensor_tensor(out=zatt(si)[0][:, csl], in0=pO[:, 0:128], in1=sga(si)[0][:, csl], op=ALU.mult),
                              pOb + [sga(si)[1]], [zatt(si)[1]])
                if lc == NOC - 1:
                    dump("za0", zatt(sbi * 4)[0][:], [128, SBT], [zatt(sbi * 4)[1]], BF16)
            if not own or upto < 5:
                continue
            drain(g1)
            fb_mode[0] = 0
            if sbi + 1 >= nsb:
                gfb = iter(())
            ffb = lambda: None
            for c in range(CPS):
                gch = sbi * CPS + c
                xr, xrb = xt(gch)
                dma("sp", xr[:], xw[gch * C:(gch + 1) * C, :], [], [xrb])
            mTt, mTb = mT()
            def load_j(j):
                return [ws_load([(lambda t: t[:, :, :], wbb[:, j * 128:(j + 1) * 128].rearrange("(bh p) c -> p bh c", p=128))]),
                        ws_load([(lambda t: t[:, :, :], wcols(O_GT + j * 128))]),
                        ws_load([(lambda t: t[:, :, :], wcols(O_GT + 1024 + j * 128))])]
            for j in range(8):
                cur_w = load_j(j)
                wt, wb = cur_w[0]
                pBr, pBrb = fullbank()
                pBa, pBab = fullbank()
                for hp in range(4):
                    si = sbi * 4 + hp
                    S.add("pe", lambda e, pBr=pBr, wt=wt, hp=hp, si=si: e.matmul(pBr[:, 0:SBT], lhsT=wt[:, hp, :], rhs=zr(si)[0][:], start=(hp == 0), stop=(hp == 3)),
                          wb + [zr(si)[1]], pBrb)
                for hp in range(4):
                    si = sbi * 4 + hp
                    S.add("pe", lambda e, pBa=pBa, wt=wt, hp=hp, si=si: e.matmul(pBa[:, 0:SBT], lhsT=wt[:, 4 + hp, :], rhs=zatt(si)[0][:], start=(hp == 0), stop=(hp == 3)),
                          wb + [zatt(si)[1]], pBab)
                halves = []
                for br in range(2):
                    wt2, wb2 = cur_w[1 + br]
                    pGt, pGtb = bankx()
                    for k in range(8):
                        S.add("pe", lambda e, pGt=pGt, k=k, wt2=wt2, hTt=hTt: e.matmul(
                            pGt[:, 0:SBT], lhsT=wt2[:, k, :], rhs=hTt[:, k, :], start=(k == 0), stop=(k == 7)), wb2 + [hTb], pGtb)
                    sg_, sgb_ = sgt(br)
                    S.add("act", lambda e, sg_=sg_, pGt=pGt: e.activation(out=sg_[:], in_=pGt[:, 0:SBT], func=AF.Sigmoid), pGtb, [sgb_])
                    halves.append((sg_, sgb_))
                m1, m1b = m12(0)
                m2, m2b = m12(1)
                S.add("dve", lambda e, m1=m1, pBr=pBr, sg_=halves[0][0]: e.tensor_tensor(out=m1[:], in0=pBr[:, 0:SBT], in1=sg_[:], op=ALU.mult), pBrb + [halves[0][1]], [m1b])
                S.add("dve", lambda e, m2=m2, pBa=pBa, sg_=halves[1][0]: e.tensor_tensor(out=m2[:], in0=pBa[:, 0:SBT], in1=sg_[:], op=ALU.mult), pBab + [halves[1][1]], [m2b])
                S.add("dve", lambda e, mTt=mTt, j=j, m1=m1, m2=m2: e.tensor_tensor(out=mTt[:, j, :], in0=m1[:], in1=m2[:], op=ALU.add), [m1b, m2b], [mTb])
                ffb()
            for c in range(CPS):
                gch = sbi * CPS + c
                lc = (sbi - own0) * CPS + c
                xr, xrb = xt(gch)
                for n in range(2):
                    pa, pab = fullbank()
                    for j in range(8):
                        S.add("pe", lambda e, pa=pa, j=j, n=n, mTt=mTt, c=c: e.matmul(
                            pa[:, 0:512], lhsT=mTt[:, j, c * C:(c + 1) * C], rhs=Wout.t[0][:, j, n * 512:(n + 1) * 512], start=(j == 0), stop=(j == 7)),
                            [mTb, Wout_b[j]], pab)
                    S.add("dve", lambda e, xr=xr, pa=pa, n=n: e.tensor_tensor(out=xr[:, n * 512:(n + 1) * 512], in0=pa[:, 0:512], in1=xr[:, n * 512:(n + 1) * 512], op=ALU.add),
                          pab + [xrb], [xrb])
                ft_, fb_ = fst(gch)
                hbt, hbb = hb(gch)
                rms_rstd(xr[:], [xrb], ft_, fb_, hbt[:], hbb)
                S.add("dve", lambda e, xr=xr, ft_=ft_: e.scalar_tensor_tensor(
                    out=xr[:], in0=xr[:], scalar=ft_[:, 2:3], in1=gfin.t[0][:], op0=ALU.mult, op1=ALU.mult), [xrb, fb_, gfin.b[0]], [xrb])
                dma("sp", out_d[lc * C:(lc + 1) * C, :], xr[:], [xrb], [], is_out=True)
            drain(gfb)

        for hp in range(4):
            dump(f"H{hp}", Ht(hp)[0][:], [128, 64], [Ht(hp)[1]])

        semnames = list(Sched.ENG) + [("dma", j) for j in range(Sched.NDMA)]
        sems = {}
        for sk in semnames:
            nm = sk if isinstance(sk, str) else f"dma{sk[1]}"
            sems[sk] = es.enter_context(nc.semaphore("s_" + nm))
        nc._sbuf_left = nc.sbuf_bytes_remaining
        block = es.enter_context(nc.Block())
        S.emit(nc, block, sems)
    nc._dbg_dumps = dump_d
    nc._sched_counts = dict(S.cnt)
    nc._sched_total = S.total
    return nc


def host_consts():
    s = np.arange(128)[:, None]
    t = np.arange(128)[None, :]
    cst = np.zeros((128, 9, 128), np.float32)
    cst[:, 0] = (s == t)
    cst[:, 1] = (s < t)
    cst[:, 2] = (s < t)
    cst[:, 3] = (s > t)
    cst[:, 4] = (s > t)
    cst[:, 5] = (s <= t)
    cst[:, 6] = (s <= t)
    cst[:, 7] = ((s // 64) == (t // 64))
    cst[:, 8] = 1.0
    return cst


def attn_masks(first):
    qi = np.arange(128)[:, None]
    kj = np.arange(256)[None, :]
    dist = qi + 128 - kj
    band = (dist >= 0) & (dist < 128)
    rest = np.where(band, 0.0, -1e30).astype(np.float32)
    fm = np.where(band & (kj >= 128), 0.0, -1e30).astype(np.float32)
    am = np.stack([fm if first else rest, rest], axis=1)
    return np.ascontiguousarray(am)


def pack_params(p):
    pp = np.zeros((128, NPP_IN), np.float32)

    def put(name, vec, n):
        v = np.asarray(vec, np.float32).reshape(n, 128)
        pp[:, PPI[name]:PPI[name] + n] = v.T
    put("mu", p["mu_shift"][0], 13)
    put("w0", p["w0"][0], 4)
    put("a0", p["a0"][0], 4)
    put("kk", p["k_k"][0], 4)
    put("ka", p["k_a"][0], 4)
    put("rk", p["r_k"][0], 4)
    put("gnw", p["gn_w"][0], 4)
    put("gnb", p["gn_b"][0], 4)
    bq = np.asarray(p["b_qkv"][0], np.float32)
    put("bq", bq[0:512], 4)
    bk = bq[512:640]
    pp[:, PPI["bk"] + 0] = np.concatenate([bk[0:64], bk[0:64]])
    pp[:, PPI["bk"] + 1] = np.concatenate([bk[64:128], bk[64:128]])
    sk = np.asarray(p["sinks"][0], np.float32)
    pp[:, PPI["sink"]:PPI["sink"] + 8] = np.broadcast_to(sk[None, :], (128, 8))
    wdi = np.zeros((128, 2, 512), np.float32)
    wdi[0:64, 0] = np.asarray(p["w_decay_up"][0], np.float32)
    wdi[64:128, 1] = np.asarray(p["w_iclr_up"][0], np.float32)
    common = {
        "w_in": np.ascontiguousarray(np.asarray(p["w_in"][0], np.float32)),
        "w_br": np.ascontiguousarray(np.stack([np.asarray(p["w_branch_rwkv"][0], np.float32),
                                               np.asarray(p["w_branch_att"][0], np.float32)])),
        "w_out": np.ascontiguousarray(np.asarray(p["w_out"][0], np.float32)),
        "wdi": np.ascontiguousarray(wdi),
        "pp": pp,
        "gpre_b": np.ascontiguousarray(np.broadcast_to(np.asarray(p["g_pre"][0], np.float32)[None], (128, D))),
        "gfin_b": np.ascontiguousarray(np.broadcast_to(np.asarray(p["g_final"], np.float32)[None], (128, D))),
        "bv_b": np.ascontiguousarray(np.broadcast_to(bq[640:768][None], (128, 128))),
        "cst": host_consts(),
    }
    return common


def kernel(**inputs):
    x = np.asarray(inputs["x"], np.float32)
    common = pack_params(inputs)
    nc = build()
    in_maps = []
    for c in range(NCORES):
        b, q = c // 4, c % 4
        end = (q + 1) * OWN_TOK
        xw = np.zeros((SEQ, D), np.float32)
        xw[SEQ - end:] = x[b, :end]
        m = dict(common)
        m["xw"] = xw
        m["amask"] = attn_masks(q == 0)
        in_maps.append(m)
    res = run_bass_kernel_spmd(nc, in_maps, core_ids=list(range(NCORES)))
    out = np.zeros((2, SEQ, D), np.float32)
    for c in range(NCORES):
        b, q = c // 4, c % 4
        out[b, q * OWN_TOK:(q + 1) * OWN_TOK] = res.results[c]["out"]
    return out
```
